# Optimizing a Trainium2 kernel written in Bass

```python
import jax
import jax.numpy as jnp
from jax import lax
import numpy as np

D_MODEL = 1024
BATCH = 16
SEQ = 2048
DEPTH = 4

GRID_W = 64
CTX_LEN = 256
N_MIXERS = 3
N_HG = (DEPTH + 2) // 3
N_RW = (DEPTH + 1) // 3
N_MLA = DEPTH // 3
NORM_EPS = 1e-6
D_FF = 2816
FFN_CONV_W = 3
HG_HEADS = 8
HG_DK = D_MODEL // HG_HEADS
HG_DV = D_MODEL // HG_HEADS
HG_CHUNK = 64
RW_HEAD = 64
RW_HEADS = D_MODEL // RW_HEAD
RW_DECAY_LORA = 64
RW_AAA_LORA = 64
RW_GATE_LORA = 160
RW_LN_EPS = 64e-5
MLA_HEADS = 16
MLA_NOPE = 64
MLA_ROPE = 32
MLA_V = 64
MLA_Q_LORA = 256
MLA_KV_LORA = 256
MLA_SCALE = (MLA_NOPE + MLA_ROPE) ** -0.5
ROPE_BASE = 10000.0
Q_BLOCK = 128

kernel_name = 'hybrid_hgrn2_rwkv7_mla_prefix_trunk'


def rmsnorm(x, g):
    xf = x.astype(jnp.float32)
    y = xf * lax.rsqrt(jnp.mean(xf * xf, axis=-1, keepdims=True) + NORM_EPS)
    return (y * g.astype(jnp.float32)).astype(x.dtype)


def modulate(h, shift, scale):
    return h * (1.0 + scale) + shift


def axial_rope(length, dim):
    n_rows = length // GRID_W
    row = jnp.repeat(jnp.arange(n_rows, dtype=jnp.float32), GRID_W)
    col = jnp.tile(jnp.arange(GRID_W, dtype=jnp.float32), n_rows)
    nq = dim // 4
    inv_freq = ROPE_BASE ** (-jnp.arange(nq, dtype=jnp.float32) / nq)
    ang_r = row[:, None] * inv_freq
    ang_c = col[:, None] * inv_freq
    ang = jnp.concatenate([ang_r, ang_r, ang_c, ang_c], axis=-1)
    return jnp.cos(ang), jnp.sin(ang)


def apply_rope(x, cos, sin):
    nq = x.shape[-1] // 4
    xr = x.reshape(x.shape[:-1] + (2, 2, nq))
    rot = jnp.stack([-xr[..., 1, :], xr[..., 0, :]], axis=-2).reshape(x.shape)
    return x * cos.astype(x.dtype) + rot * sin.astype(x.dtype)


def conv_ffn(h, w_in, w_conv, b_conv, w_out):
    a, v = jnp.split(h @ w_in, 2, axis=-1)
    a = lax.conv_general_dilated(a, w_conv[:, None, :], window_strides=(1,), padding=((FFN_CONV_W // 2, FFN_CONV_W // 2),), dimension_numbers=('NWC', 'WIO', 'NWC'), feature_group_count=D_FF) + b_conv
    return (jax.nn.silu(a) * v) @ w_out


def chunk_gla(q, k, v, log_f, s0):
    B, H, L, _ = q.shape
    n = L // HG_CHUNK
    mid = HG_CHUNK // 2
    causal = jnp.tril(jnp.ones((HG_CHUNK, HG_CHUNK), dtype=bool))

    def chunks(t):
        return jnp.moveaxis(t.reshape(B, H, n, HG_CHUNK, t.shape[-1]), 2, 0)

    def step(S, inp):
        qc, kc, vc, gc = inp
        b = jnp.cumsum(gc, axis=-2)
        b_mid = b[..., mid:mid + 1, :]
        b_end = b[..., -1:, :]
        att = jnp.einsum('bhtk,bhsk->bhts', qc * jnp.exp(b - b_mid), kc * jnp.exp(b_mid - b))
        att = jnp.where(causal, att, 0.0)
        o = jnp.einsum('bhts,bhsv->bhtv', att, vc) + jnp.einsum('bhtk,bhkv->bhtv', qc * jnp.exp(b), S)
        S = jnp.exp(b_end)[..., 0, :, None] * S + jnp.einsum('bhsk,bhsv->bhkv', kc * jnp.exp(b_end - b), vc)
        return S, o

    S, o = lax.scan(step, s0, (chunks(q), chunks(k), chunks(v), chunks(log_f)))
    return jnp.moveaxis(o, 0, 2).reshape(B, H, L, v.shape[-1]), S


def hgrn2_mixer(hc, hl, w_in, lb_fwd, lb_bwd, g_norm, w_o, need_ctx):
    f32 = jnp.float32

    def heads(t):
        b, l, _ = t.shape
        return t.reshape(b, l, HG_HEADS, -1).transpose(0, 2, 1, 3).astype(f32)

    def project(h):
        q, i, g, z_f, z_b = jnp.split(h @ w_in, 5, axis=-1)
        return heads(jax.nn.silu(q)), heads(i), g, heads(z_f), heads(z_b)

    def gates(z, lb):
        lb = lb.reshape(HG_HEADS, 1, HG_DK).astype(f32)
        return jnp.log(lb + (1.0 - lb) * jax.nn.sigmoid(z)), (1.0 - lb) * jax.nn.sigmoid(-z)

    def flip(t):
        return jnp.flip(t, axis=2)

    def readout(o, g):
        b, h, l, dv = o.shape
        o = rmsnorm(jnp.moveaxis(o, 1, 2), g_norm).reshape(b, l, h * dv)
        return (o.astype(g.dtype) * jax.nn.silu(g)) @ w_o

    q_c, i_c, g_c, zf_c, zb_c = project(hc)
    q_l, i_l, g_l, zf_l, zb_l = project(hl)
    s0 = jnp.zeros((hl.shape[0], HG_HEADS, HG_DK, HG_DV), f32)
    lf_c, kf_c = gates(zf_c, lb_fwd)
    lf_l, kf_l = gates(zf_l, lb_fwd)
    of_c, s_fwd = chunk_gla(q_c, kf_c, i_c, lf_c, s0)
    of_l, _ = chunk_gla(q_l, kf_l, i_l, lf_l, s_fwd)
    lbk_c, kb_c = gates(zb_c, lb_bwd)
    lbk_l, kb_l = gates(zb_l, lb_bwd)
    ob_c, s_bwd = chunk_gla(flip(q_c), flip(kb_c), flip(i_c), flip(lbk_c), s0)
    ob_l, _ = chunk_gla(flip(q_l), flip(kb_l), flip(i_l), flip(lbk_l), s_bwd)
    y_l = readout(of_l + flip(ob_l), g_l)
    y_c = readout(of_c + flip(ob_c), g_c) if need_ctx else None
    return y_c, y_l


def rwkv7_scan(r, w, k, v, kk, a, s0):
    def step(S, inp):
        rt, wt, kt, vt, kkt, at = inp
        sa = jnp.einsum('bhvk,bhk->bhv', S, -kkt)
        S = S * wt[:, :, None, :] + sa[..., None] * (kkt * at)[:, :, None, :] + vt[..., None] * kt[:, :, None, :]
        return S, jnp.einsum('bhvk,bhk->bhv', S, rt)

    xs = tuple(jnp.moveaxis(t, 1, 0) for t in (r, w, k, v, kk, a))
    S, y = lax.scan(step, s0, xs)
    return jnp.moveaxis(y, 0, 1), S


def rwkv7_mixer(hc, hl, mu, w_rkv, w0, w1, w2, a0, a1, a2, g1, g2, k_k, k_a, r_k, ln_w, ln_b, w_o, need_ctx):
    f32 = jnp.float32

    def heads(t):
        return t.reshape(t.shape[:-1] + (RW_HEADS, RW_HEAD)).astype(f32)

    k_k_h = heads(k_k)
    k_a_h = heads(k_a)

    def project(h):
        hp = jnp.pad(h, ((0, 0), (1, 1), (0, 0)))
        dx = 0.5 * (hp[:, :-2] + hp[:, 2:]) - h
        x_r, x_w, x_k, x_v, x_a, x_g = [h + dx * mu[j] for j in range(6)]
        r = heads(x_r @ w_rkv[0])
        k = heads(x_k @ w_rkv[1])
        v = heads(x_v @ w_rkv[2])
        g = jax.nn.sigmoid(x_g @ g1) @ g2
        kk = k * k_k_h
        kk = kk / jnp.maximum(jnp.sqrt(jnp.sum(kk * kk, axis=-1, keepdims=True)), 1e-12)
        per_dir = []
        for d in range(2):
            w_log = -jax.nn.softplus(-(w0[d] + jnp.tanh(x_w @ w1[d]) @ w2[d])) - 0.5
            decay = jnp.exp(-jnp.exp(heads(w_log)))
            a = heads(jax.nn.sigmoid(a0[d] + (x_a @ a1[d]) @ a2[d]))
            per_dir.append((decay, k * (1.0 + (a - 1.0) * k_a_h), a))
        return r, v, kk, g, per_dir

    def flip(t):
        return jnp.flip(t, axis=1)

    def readout(y, r, v, k_sum, g):
        b, l = y.shape[:2]
        mean = jnp.mean(y, axis=-1, keepdims=True)
        var = jnp.mean(jnp.square(y - mean), axis=-1, keepdims=True)
        yn = ((y - mean) * lax.rsqrt(var + RW_LN_EPS)).reshape(b, l, D_MODEL) * ln_w.astype(f32) + ln_b.astype(f32)
        bonus = (jnp.sum(r * k_sum * r_k.astype(f32), axis=-1, keepdims=True) * v).reshape(b, l, D_MODEL)
        return ((yn + bonus).astype(g.dtype) * g) @ w_o

    r_c, v_c, kk_c, g_c, (fw_c, bw_c) = project(hc)
    r_l, v_l, kk_l, g_l, (fw_l, bw_l) = project(hl)
    s0 = jnp.zeros((hl.shape[0], RW_HEADS, RW_HEAD, RW_HEAD), f32)
    yf_c, s_fwd = rwkv7_scan(r_c, fw_c[0], fw_c[1], v_c, kk_c, fw_c[2], s0)
    yf_l, _ = rwkv7_scan(r_l, fw_l[0], fw_l[1], v_l, kk_l, fw_l[2], s_fwd)
    yb_c, s_bwd = rwkv7_scan(flip(r_c), flip(bw_c[0]), flip(bw_c[1]), flip(v_c), flip(kk_c), flip(bw_c[2]), s0)
    yb_l, _ = rwkv7_scan(flip(r_l), flip(bw_l[0]), flip(bw_l[1]), flip(v_l), flip(kk_l), flip(bw_l[2]), s_bwd)
    y_l = readout(yf_l + flip(yb_l), r_l, v_l, fw_l[1] + bw_l[1], g_l)
    y_c = readout(yf_c + flip(yb_c), r_c, v_c, fw_c[1] + bw_c[1], g_c) if need_ctx else None
    return y_c, y_l


def softmax_attend(q, k, v):
    s = jnp.einsum('bhqd,bhkd->bhqk', q, k).astype(jnp.float32) * MLA_SCALE
    p = jax.nn.softmax(s, axis=-1)
    return jnp.einsum('bhqk,bhkd->bhqd', p.astype(v.dtype), v)


def attention_blocked(q, k, v):
    B, H, L, dq = q.shape
    nb = L // Q_BLOCK
    qb = jnp.moveaxis(q.reshape(B, H, nb, Q_BLOCK, dq), 2, 0)
    ob = lax.map(lambda qi: softmax_attend(qi, k, v), qb)
    return jnp.moveaxis(ob, 0, 2).reshape(B, H, L, v.shape[-1])


def mla_mixer(hc, hl, w_dqkv, q_norm, kv_norm, w_uq, w_ukv, w_o, need_ctx):
    def project(h, rope):
        B, L, _ = h.shape
        cq, ckv, kr = jnp.split(h @ w_dqkv, [MLA_Q_LORA, MLA_Q_LORA + MLA_KV_LORA], axis=-1)
        q = (rmsnorm(cq, q_norm) @ w_uq).reshape(B, L, MLA_HEADS, MLA_NOPE + MLA_ROPE).transpose(0, 2, 1, 3)
        kv = (rmsnorm(ckv, kv_norm) @ w_ukv).reshape(B, L, MLA_HEADS, MLA_NOPE + MLA_V).transpose(0, 2, 1, 3)
        q_nope, q_rope = jnp.split(q, [MLA_NOPE], axis=-1)
        k_nope, v = jnp.split(kv, [MLA_NOPE], axis=-1)
        k_rope = kr[:, None]
        if rope is not None:
            cos, sin = rope
            q_rope = apply_rope(q_rope, cos, sin)
            k_rope = apply_rope(k_rope, cos, sin)
        q = jnp.concatenate([q_nope, q_rope], axis=-1)
        k = jnp.concatenate([k_nope, jnp.broadcast_to(k_rope, (B, MLA_HEADS, L, MLA_ROPE))], axis=-1)
        return q, k, v

    def out(o):
        B, H, L, V = o.shape
        return jnp.moveaxis(o, 1, 2).reshape(B, L, H * V) @ w_o

    q_c, k_c, v_c = project(hc, None)
    q_l, k_l, v_l = project(hl, axial_rope(hl.shape[1], MLA_ROPE))
    y_l = out(attention_blocked(q_l, jnp.concatenate([k_c, k_l], axis=2), jnp.concatenate([v_c, v_l], axis=2)))
    y_c = out(softmax_attend(q_c, k_c, v_c)) if need_ctx else None
    return y_c, y_l


def setup_inputs(seed: int = 0) -> dict:
    key = jax.random.key(seed)
    ks = iter(jax.random.split(key, 48))
    f32 = jnp.float32
    D = D_MODEL
    HK = HG_HEADS * HG_DK

    def nrm(shape, scale):
        return scale * jax.random.normal(next(ks), shape, f32)

    def gain(shape):
        return 1.0 + nrm(shape, 0.02)

    return {
        'x': nrm((BATCH, SEQ, D), 1.0),
        'c': nrm((BATCH, D), 1.0),
        'ctx': nrm((BATCH, CTX_LEN, D), 1.0),
        'c_ctx': nrm((D,), 1.0),
        'w_mod': nrm((DEPTH, D, 6 * D), 0.5 * D ** -0.5),
        'b_mod': nrm((DEPTH, 6 * D), 0.02),
        'norm1': gain((DEPTH, D)),
        'norm2': gain((DEPTH, D)),
        'ffn_w_in': nrm((DEPTH, D, 2 * D_FF), D ** -0.5),
        'ffn_conv': nrm((DEPTH, FFN_CONV_W, D_FF), FFN_CONV_W ** -0.5),
        'ffn_conv_b': nrm((DEPTH, D_FF), 0.02),
        'ffn_w_out': nrm((DEPTH, D_FF, D), D_FF ** -0.5),
        'hg_w_in': nrm((N_HG, D, 5 * HK), D ** -0.5),
        'hg_lb': nrm((2, N_HG, HK), 1.0),
        'hg_norm': gain((N_HG, HG_DV)),
        'hg_w_o': nrm((N_HG, HK, D), HK ** -0.5),
        'rw_mu': jax.random.uniform(next(ks), (N_RW, 6, D), f32),
        'rw_w_rkv': nrm((N_RW, 3, D, D), D ** -0.5),
        'rw_w0': jax.random.uniform(next(ks), (N_RW, 2, D), f32, -6.0, -1.0),
        'rw_w1': nrm((N_RW, 2, D, RW_DECAY_LORA), D ** -0.5),
        'rw_w2': nrm((N_RW, 2, RW_DECAY_LORA, D), 0.1 * RW_DECAY_LORA ** -0.5),
        'rw_a0': nrm((N_RW, 2, D), 0.1),
        'rw_a1': nrm((N_RW, 2, D, RW_AAA_LORA), D ** -0.5),
        'rw_a2': nrm((N_RW, 2, RW_AAA_LORA, D), 0.1 * RW_AAA_LORA ** -0.5),
        'rw_g1': nrm((N_RW, D, RW_GATE_LORA), D ** -0.5),
        'rw_g2': nrm((N_RW, RW_GATE_LORA, D), RW_GATE_LORA ** -0.5),
        'rw_k_k': 0.85 + nrm((N_RW, D), 0.02),
        'rw_k_a': gain((N_RW, D)),
        'rw_r_k': nrm((N_RW, RW_HEADS, RW_HEAD), 0.1),
        'rw_ln_w': gain((N_RW, D)),
        'rw_ln_b': nrm((N_RW, D), 0.02),
        'rw_w_o': nrm((N_RW, D, D), D ** -0.5),
        'mla_w_dqkv': nrm((N_MLA, D, MLA_Q_LORA + MLA_KV_LORA + MLA_ROPE), D ** -0.5),
        'mla_q_norm': gain((N_MLA, MLA_Q_LORA)),
        'mla_kv_norm': gain((N_MLA, MLA_KV_LORA)),
        'mla_w_uq': nrm((N_MLA, MLA_Q_LORA, MLA_HEADS * (MLA_NOPE + MLA_ROPE)), MLA_Q_LORA ** -0.5),
        'mla_w_ukv': nrm((N_MLA, MLA_KV_LORA, MLA_HEADS * (MLA_NOPE + MLA_V)), MLA_KV_LORA ** -0.5),
        'mla_w_o': nrm((N_MLA, MLA_HEADS * MLA_V, D), (MLA_HEADS * MLA_V) ** -0.5),
        'norm_f': gain((D,)),
    }


def reference(x, c, ctx, c_ctx, w_mod, b_mod, norm1, norm2, ffn_w_in, ffn_conv, ffn_conv_b, ffn_w_out, hg_w_in, hg_lb, hg_norm, hg_w_o, rw_mu, rw_w_rkv, rw_w0, rw_w1, rw_w2, rw_a0, rw_a1, rw_a2, rw_g1, rw_g2, rw_k_k, rw_k_a, rw_r_k, rw_ln_w, rw_ln_b, rw_w_o, mla_w_dqkv, mla_q_norm, mla_kv_norm, mla_w_uq, mla_w_ukv, mla_w_o, norm_f):
    lb_p = jnp.cumsum(jax.nn.softmax(hg_lb.astype(jnp.float32), axis=1), axis=1)
    lower_bounds = lb_p - lb_p[:, :1]
    silu_c = jax.nn.silu(c)
    silu_cc = jax.nn.silu(c_ctx)
    x_l, x_c = x, ctx
    for layer in range(DEPTH):
        last = layer == DEPTH - 1
        kind, j = layer % N_MIXERS, layer // N_MIXERS
        mod_l = jnp.split((silu_c @ w_mod[layer] + b_mod[layer])[:, None, :], 6, axis=-1)
        mod_c = jnp.split(silu_cc @ w_mod[layer] + b_mod[layer], 6, axis=-1)
        h_l = modulate(rmsnorm(x_l, norm1[layer]), mod_l[0], mod_l[1])
        h_c = modulate(rmsnorm(x_c, norm1[layer]), mod_c[0], mod_c[1])
        if kind == 0:
            y_c, y_l = hgrn2_mixer(h_c, h_l, hg_w_in[j], lower_bounds[0, j], lower_bounds[1, j], hg_norm[j], hg_w_o[j], not last)
        elif kind == 1:
            y_c, y_l = rwkv7_mixer(h_c, h_l, rw_mu[j], rw_w_rkv[j], rw_w0[j], rw_w1[j], rw_w2[j], rw_a0[j], rw_a1[j], rw_a2[j], rw_g1[j], rw_g2[j], rw_k_k[j], rw_k_a[j], rw_r_k[j], rw_ln_w[j], rw_ln_b[j], rw_w_o[j], not last)
        else:
            y_c, y_l = mla_mixer(h_c, h_l, mla_w_dqkv[j], mla_q_norm[j], mla_kv_norm[j], mla_w_uq[j], mla_w_ukv[j], mla_w_o[j], not last)
        x_l = x_l + mod_l[2] * y_l
        x_l = x_l + mod_l[5] * conv_ffn(modulate(rmsnorm(x_l, norm2[layer]), mod_l[3], mod_l[4]), ffn_w_in[layer], ffn_conv[layer], ffn_conv_b[layer], ffn_w_out[layer])
        if not last:
            x_c = x_c + mod_c[2] * y_c
            x_c = x_c + mod_c[5] * conv_ffn(modulate(rmsnorm(x_c, norm2[layer]), mod_c[3], mod_c[4]), ffn_w_in[layer], ffn_conv[layer], ffn_conv_b[layer], ffn_w_out[layer])
    return rmsnorm(x_l, norm_f)
```

```python
from contextlib import ExitStack
import numpy as np
import concourse.bass as bass
import concourse.mybir as mybir
from concourse.bass_utils import run_bass_kernel_spmd

F32 = mybir.dt.float32
BF16 = mybir.dt.bfloat16
AF = mybir.ActivationFunctionType
ALU = mybir.AluOpType

NCORES = 8
NB = 2
TC = 256
TL = 2048
T = TC + TL
TT = NB * T
D = 1024
DEPTH = 4
DFF = 2816
NFC = DFF // 128
BLK = 256
NBLK = T // BLK
EPS = 1e-6
CH = 64
NCH = T // CH
RW_LN_EPS = 64e-5
MLA_SCALE = 96 ** -0.5


class Buf:
    __slots__ = ("name", "w", "r")

    def __init__(self, name=""):
        self.name = name
        self.w = None
        self.r = []


class _Eng:
    def __init__(self, S, name, eng):
        self.S = S
        self.name = name
        self.eng = eng
        self.sem = None
        self.count = 0
        self.seen = {}
        self.nsem = 0
        self.ninst = 0

    def new_sem(self):
        self.sem = self.S.nc.alloc_semaphore(f"e_{self.name}_{self.nsem}")
        self.nsem += 1
        self.count = 0

    def wait(self, ev):
        sem, val = ev
        k = id(sem)
        if self.seen.get(k, 0) >= val:
            return
        self.eng.wait_ge(sem, val)
        self.seen[k] = val


class Sched:
    EPOCH = 30000

    def __init__(self, nc, ndma_sems=48):
        self.nc = nc
        self.E = {}
        for name, eng in (("pe", nc.tensor), ("dve", nc.vector), ("act", nc.scalar),
                          ("pool", nc.gpsimd), ("sp", nc.sync)):
            e = _Eng(self, name, eng)
            e.new_sem()
            self.E[name] = e
        self.dsems = [[nc.alloc_semaphore(f"d{i}"), 0] for i in range(ndma_sems)]
        self.dnext = 0
        self._keep = []

    @staticmethod
    def _deps(reads, writes):
        deps = []
        for b in reads:
            if b.w is not None:
                deps.append(b.w)
        for b in writes:
            if b.w is not None:
                deps.append(b.w)
            deps.extend(b.r)
        return deps

    @staticmethod
    def _mark(ev, reads, writes):
        for b in writes:
            b.w = ev
            b.r = []
        for b in reads:
            if b not in writes:
                b.r.append(ev)
                if len(b.r) > 32:
                    b.r = b.r[-32:]

    def op(self, ename, fn, reads=(), writes=()):
        e = self.E[ename]
        for ev in self._deps(reads, writes):
            e.wait(ev)
        if e.count >= self.EPOCH:
            self._keep.append(e.sem)
            e.new_sem()
        inst = fn()
        e.count += 1
        e.ninst += 1
        inst.then_inc(e.sem, 1)
        ev = (e.sem, e.count)
        self._mark(ev, reads, writes)
        return ev

    def dma(self, qname, out, in_, reads=(), writes=(), **kw):
        q = self.E[qname]
        for ev in self._deps(reads, writes):
            q.wait(ev)
        slot = self.dsems[self.dnext % len(self.dsems)]
        self.dnext += 1
        if slot[1] >= self.EPOCH:
            self._keep.append(slot[0])
            slot[0] = self.nc.alloc_semaphore(f"dx{self.dnext}")
            slot[1] = 0
        if slot[1] > 0:
            q.wait((slot[0], slot[1]))
        q.eng.dma_start(out=out, in_=in_, **kw).then_inc(slot[0], 16)
        q.ninst += 1
        slot[1] += 16
        ev = (slot[0], slot[1])
        self._mark(ev, reads, writes)
        return ev

    def barrier(self):
        evs = [(e.sem, e.count) for e in self.E.values() if e.count > 0]
        evs += [(s[0], s[1]) for s in self.dsems if s[1] > 0]
        for e in self.E.values():
            for ev in evs:
                if ev[0] is e.sem:
                    continue
                e.wait(ev)


class PVec:
    def __init__(self):
        self.cols = []
        self.off = {}
        self.n = 0

    def add(self, name, vec):
        vec = np.asarray(vec, dtype=np.float32).reshape(-1)
        assert vec.size % 128 == 0
        nch = vec.size // 128
        self.off[name] = (self.n, nch)
        self.cols.append(np.ascontiguousarray(vec.reshape(nch, 128).T))
        self.n += nch

    def array(self):
        return np.ascontiguousarray(np.concatenate(self.cols, axis=1))


def pvec_layout(inputs):
    pv = PVec()
    for l in range(DEPTH):
        pv.add(f"b_mod{l}", inputs["b_mod"][l])
        pv.add(f"norm1_{l}", inputs["norm1"][l])
        pv.add(f"norm2_{l}", inputs["norm2"][l])
        for k in range(3):
            pv.add(f"conv{l}_{k}", inputs["ffn_conv"][l, k])
        pv.add(f"convb{l}", inputs["ffn_conv_b"][l])
    pv.add("norm_f", inputs["norm_f"])
    for d in range(2):
        for j in range(2):
            pv.add(f"hg_lb{d}_{j}", inputs["hg_lb"][d, j])
    for j in range(2):
        pv.add(f"hg_norm{j}", inputs["hg_norm"][j])
    for k in range(6):
        pv.add(f"rw_mu{k}", inputs["rw_mu"][0, k])
    for d in range(2):
        pv.add(f"rw_w0_{d}", inputs["rw_w0"][0, d])
        pv.add(f"rw_a0_{d}", inputs["rw_a0"][0, d])
    for nm in ("rw_k_k", "rw_k_a", "rw_r_k", "rw_ln_w", "rw_ln_b"):
        pv.add(nm, inputs[nm][0])
    pv.add("mla_q_norm", inputs["mla_q_norm"][0])
    pv.add("mla_kv_norm", inputs["mla_kv_norm"][0])
    return pv


def make_consts():
    c = {}
    c["ident"] = np.eye(128, dtype=np.float32)
    c["ones"] = np.ones((128, 128), dtype=np.float32)
    bo = np.zeros((128, 128), dtype=np.float32)
    bo[:64, :64] = 1.0
    bo[64:, 64:] = 1.0
    c["blk64"] = bo
    i = np.arange(64)[:, None]
    t = np.arange(64)[None, :]
    su = (i < t).astype(np.float32)
    iu = (i <= t).astype(np.float32)
    sl = (i > t).astype(np.float32)
    il = (i >= t).astype(np.float32)
    c["masks"] = np.concatenate([np.concatenate([su, iu, sl, il], axis=1)] * 2, axis=0)
    m = np.ones((128, T), dtype=np.float32)
    m[:, ::CH] = 0.0
    c["scanmask"] = m
    nq = 8
    inv_freq = (10000.0 ** (-np.arange(nq, dtype=np.float32) / nq)).astype(np.float32)
    pos = np.arange(TL)
    row = (pos // 64).astype(np.float32)
    col = (pos % 64).astype(np.float32)
    ang_r = row[:, None] * inv_freq
    ang_c = col[:, None] * inv_freq
    ang = np.concatenate([ang_r, ang_r, ang_c, ang_c], axis=-1).astype(np.float32)
    cos = np.ones((32, T), dtype=np.float32)
    sin = np.zeros((32, T), dtype=np.float32)
    cos[:, TC:] = np.cos(ang).T
    sin[:, TC:] = np.sin(ang).T
    c["rope_cos"] = cos
    c["rope_sin"] = sin
    return c


WEIGHT_NAMES = ["w_mod", "ffn_w_in", "ffn_w_out", "hg_w_in", "hg_w_o", "rw_w_rkv", "rw_w1", "rw_w2",
                "rw_a1", "rw_a2", "rw_g1", "rw_g2", "rw_w_o", "mla_w_dqkv", "mla_w_uq", "mla_w_ukv", "mla_w_o"]


class Stage:
    def __init__(self, P, name):
        self.P = P
        self.name = name
        self.es = ExitStack()
        P.nstage += 1
        self.k = 0

    def sb(self, name, shape, dt=F32):
        self.k += 1
        h = self.es.enter_context(self.P.nc.sbuf_tensor(f"{self.name}{self.P.nstage}_{name}_{self.k}", list(shape), dt))
        return h.ap()

    def close(self):
        self.P.S.barrier()
        self.es.close()


class Prog:
    def __init__(self, wshapes, pv_off, npv, dbg=(), xin_name=None):
        nc = bass.Bass("TRN2", target_bir_lowering=False)
        self.nc = nc
        self.dbg = set(dbg)
        self.pv_off = pv_off
        self.nstage = 0
        di = lambda n, s: nc.dram_tensor(n, list(s), F32, kind="ExternalInput").ap()
        self.x = di("x", [NB, TL, D])
        self.ctx = di("ctx", [NB, TC, D])
        self.cvec = di("cvec", [3, D])
        self.pvec_d = di("pvec", [128, npv])
        self.cd = {n: di("c_" + n, s) for n, s in (("ident", [128, 128]), ("ones", [128, 128]), ("blk64", [128, 128]),
                                                    ("masks", [128, 256]), ("scanmask", [128, T]),
                                                    ("rope_cos", [32, T]), ("rope_sin", [32, T]))}
        self.W = {n: di(n, wshapes[n]) for n in WEIGHT_NAMES}
        self.out = nc.dram_tensor("out", [NB, TL, D], F32, kind="ExternalOutput").ap()
        self.scratch = {}
        self.S = Sched(nc)
        S = self.S
        self.PS = [nc.alloc_psum_tensor(f"psb{i}", [128, 512], F32).ap() for i in range(8)]
        self.BPS = [Buf(f"ps{i}") for i in range(8)]
        g = lambda n, s, dt=F32: nc.alloc_sbuf_tensor("g_" + n, list(s), dt).ap()
        self.ident = g("ident", [128, 128])
        self.identb = g("identb", [128, 128], BF16)
        self.onesf = g("onesf", [128, 128])
        self.onesb = g("onesb", [128, 128], BF16)
        self.blk64 = g("blk64", [128, 128])
        self.masks = g("masks", [128, 256])
        self.pvec = g("pvec", [128, npv])
        self.MOD = g("MOD", [128, DEPTH, 48, 3])
        self.MA = g("MA", [128, DEPTH, 2, 8, 3])
        self.epsD = g("epsD", [128, 1])
        self.BC = Buf("consts")
        self.BMOD = Buf("mod")
        S.op("dve", lambda: nc.vector.memset(self.epsD, EPS), [], [self.BC])
        S.dma("sp", self.ident, self.cd["ident"], writes=[self.BC])
        b1, b2, b3, b4, b5, b6 = [Buf() for _ in range(6)]
        S.dma("sp", self.onesf, self.cd["ones"], writes=[b1])
        S.dma("sp", self.blk64, self.cd["blk64"], writes=[b2])
        S.dma("sp", self.masks, self.cd["masks"], writes=[b3])
        S.dma("sp", self.pvec, self.pvec_d, writes=[b4])
        S.dma("pool", self.identb, self.cd["ident"], writes=[b5])
        S.dma("pool", self.onesb, self.cd["ones"], writes=[b6])
        S.barrier()

    def scr(self, name, shape, dt=F32):
        if name not in self.scratch:
            kind = "ExternalOutput" if name in self.dbg else "Internal"
            self.scratch[name] = self.nc.dram_tensor("s_" + name, list(shape), dt, kind=kind).ap()
        return self.scratch[name]

    def pv(self, name, c=None):
        off, nch = self.pv_off[name]
        if c is None:
            return self.pvec[:, off:off + nch]
        return self.pvec[:, off + c:off + c + 1]

    def load_w(self, dst, src, bufs_cols, q="pool"):
        S = self.S
        n = dst.shape[2]
        v = src.rearrange("(kc p) n -> p kc n", p=128)
        bufs = []
        for n0 in range(0, n, 512):
            n1 = min(n, n0 + 512)
            b = Buf()
            S.dma(q, dst[:, :, n0:n1], v[:, :, n0:n1], writes=[b])
            bufs.append(b)
        return bufs

    def prologue_transpose(self, xT):
        nc, S = self.nc, self.S
        st = Stage(self, "pt")
        tin = [st.sb(f"tin{i}", [128, D]) for i in range(2)]
        tout = [st.sb(f"tout{i}", [128, 8, 128]) for i in range(2)]
        Bin = [Buf(), Buf()]
        Bout = [Buf(), Buf()]
        xTv = xT.rearrange("(c p) t -> p c t", p=128)
        tiles = []
        for b in range(NB):
            for k in range(T // 128):
                tiles.append((b, k))

        def src(b, k):
            t0 = k * 128
            if t0 < TC:
                return self.ctx[b, t0:t0 + 128, :]
            return self.x[b, t0 - TC:t0 - TC + 128, :]

        S.dma("sp", tin[0], src(*tiles[0]), writes=[Bin[0]])
        for n, (b, k) in enumerate(tiles):
            i = n % 2
            if n + 1 < len(tiles):
                S.dma("sp", tin[1 - i], src(*tiles[n + 1]), writes=[Bin[1 - i]])
            for hf in range(2):
                pb = 2 * (n % 2) + hf
                for c4 in range(4):
                    c = hf * 4 + c4
                    S.op("pe", lambda: nc.tensor.transpose(out=self.PS[pb][:, c4 * 128:(c4 + 1) * 128], in_=tin[i][:, c * 128:(c + 1) * 128], identity=self.ident),
                         [Bin[i]], [self.BPS[pb]])
                eng = "dve" if hf == 0 else "act"
                if hf == 0:
                    S.op("dve", lambda: nc.vector.tensor_copy(out=tout[i][:, 0:4, :], in_=self.PS[pb][:].rearrange("p (c t) -> p c t", c=4)), [self.BPS[pb]], [Bout[i]])
                else:
                    S.op("act", lambda: nc.scalar.copy(out=tout[i][:, 4:8, :], in_=self.PS[pb][:].rearrange("p (c t) -> p c t", c=4)), [self.BPS[pb]], [Bout[i]])
            col = b * T + k * 128
            S.dma("pool", xTv[:, :, col:col + 128], tout[i], reads=[Bout[i]])
        st.close()

    def prologue_mod(self):
        nc, S = self.nc, self.S
        st = Stage(self, "pm")
        cv = st.sb("cv", [3, D])
        sc = st.sb("sc", [3, D])
        scT = st.sb("scT", [128, 8, 3])
        Bcv, Bsc, BscT = Buf(), Buf(), Buf()
        S.dma("sp", cv, self.cvec, writes=[Bcv])
        S.op("act", lambda: nc.scalar.activation(out=sc, in_=cv, func=AF.Silu), [Bcv], [Bsc])
        for kc in range(8):
            S.op("pe", lambda: nc.tensor.transpose(out=self.PS[0][:, kc * 4:kc * 4 + 3], in_=sc[0:3, kc * 128:(kc + 1) * 128], identity=self.ident[0:3, 0:3]),
                 [Bsc], [self.BPS[0]])
        S.op("dve", lambda: nc.vector.tensor_copy(out=scT, in_=self.PS[0][:, 0:32].rearrange("p (k f) -> p k f", f=4)[:, :, 0:3]), [self.BPS[0]], [BscT])
        wt = [st.sb(f"wt{i}", [128, 8, 512]) for i in range(2)]
        Bwt = [Buf(), Buf()]
        groups = [(l, g) for l in range(DEPTH) for g in range(12)]

        def wsrc(l, g):
            return self.W["w_mod"][l].rearrange("(kc p) n -> p kc n", p=128)[:, :, g * 512:(g + 1) * 512]

        S.dma("sp", wt[0], wsrc(*groups[0]), writes=[Bwt[0]])
        for n, (l, g) in enumerate(groups):
            i = n % 2
            if n + 1 < len(groups):
                S.dma("sp", wt[1 - i], wsrc(*groups[n + 1]), writes=[Bwt[1 - i]])
            pb = 1 + (n % 2)
            for oc in range(4):
                for kc in range(8):
                    S.op("pe", lambda: nc.tensor.matmul(self.PS[pb][:, oc * 4:oc * 4 + 3], lhsT=wt[i][:, kc, oc * 128:(oc + 1) * 128], rhs=scT[:, kc, :], start=(kc == 0), stop=(kc == 7)),
                         [Bwt[i], BscT], [self.BPS[pb]])
            boff, _ = self.pv_off[f"b_mod{l}"]
            bias = self.pvec[:, boff + g * 4:boff + g * 4 + 4].unsqueeze(2).to_broadcast([128, 4, 3])
            S.op("dve", lambda: nc.vector.tensor_tensor(out=self.MOD[:, l, g * 4:(g + 1) * 4, :], in0=self.PS[pb][:, 0:16].rearrange("p (o f) -> p o f", f=4)[:, :, 0:3], in1=bias, op=ALU.add),
                 [self.BPS[pb]], [self.BMOD])
        for l in range(DEPTH):
            for w in range(2):
                sc_idx = 8 if w == 0 else 32
                nrm = self.pv(f"norm{w + 1}_{l}").unsqueeze(2).to_broadcast([128, 8, 3])
                S.op("dve", lambda: nc.vector.scalar_tensor_tensor(out=self.MA[:, l, w, :, :], in0=self.MOD[:, l, sc_idx:sc_idx + 8, :], scalar=1.0, in1=nrm, op0=ALU.add, op1=ALU.mult),
                     [self.BMOD], [self.BMOD])
        st.close()

    def norm_tiles(self, st, n=BLK + 2):
        return dict(sq=st.sb("nsq", [128, 8, n], BF16), tmp=st.sb("ntmp", [128, 8, n]), r0=st.sb("nr0", [128, n]), r1=st.sb("nr1", [128, n]),
                    B=[Buf() for _ in range(4)])

    def norm_block(self, nt, xs, Bxs, n, A, Bsh, hb, Bhb, bank):
        nc, S = self.nc, self.S
        sq, tmp, r0, r1 = nt["sq"], nt["tmp"], nt["r0"], nt["r1"]
        Bsq, Btmp, Br0, Br1 = nt["B"]
        S.op("act", lambda: nc.scalar.activation(out=sq[:, :, :n], in_=xs, func=AF.Square), [Bxs], [Bsq])
        ps = self.PS[bank]
        for c in range(8):
            S.op("pe", lambda: nc.tensor.matmul(ps[:, :n], lhsT=self.onesb, rhs=sq[:, c, :n], start=(c == 0), stop=(c == 7)), [Bsq], [self.BPS[bank]])
        S.op("act", lambda: nc.scalar.activation(out=r0[:, :n], in_=ps[:, :n], func=AF.Sqrt, scale=1.0 / D, bias=self.epsD), [self.BPS[bank]], [Br0])
        S.op("dve", lambda: nc.vector.reciprocal(out=r1[:, :n], in_=r0[:, :n]), [Br0], [Br1])
        S.op("dve", lambda: nc.vector.tensor_tensor(out=tmp[:, :, :n], in0=xs, in1=r1[:, :n].unsqueeze(1).to_broadcast([128, 8, n]), op=ALU.mult), [Bxs, Br1], [Btmp])
        for c in range(8):
            S.op("act", lambda: nc.scalar.activation(out=hb[:, c, :n], in_=tmp[:, c, :n], func=AF.Identity, scale=A[:, c:c + 1], bias=(Bsh[:, c:c + 1] if Bsh is not None else 0.0)),
                 [Btmp, self.BMOD], [Bhb])

    def mod_ab(self, l, w, j):
        A = self.MA[:, l, w, :, j]
        sh = self.MOD[:, l, (0 if w == 0 else 24):(8 if w == 0 else 32), j]
        gt = self.MOD[:, l, (16 if w == 0 else 40):(24 if w == 0 else 48), j]
        return A, sh, gt

    @staticmethod
    def blocks(skip_ctx=False):
        out = []
        for b in range(NB):
            for k in range(NBLK):
                if skip_ctx and k == 0:
                    continue
                out.append((b, k))
        return out

    @staticmethod
    def blk_range(k):
        seq0, seq1 = (0, TC) if k == 0 else (TC, T)
        t0 = k * BLK
        lo = max(t0 - 1, seq0)
        hi = min(t0 + BLK + 1, seq1)
        return t0, lo, hi, (t0 == seq0), (t0 + BLK == seq1)

    def ffn_stage(self, l, xin, xout, skip_ctx):
        nc, S = self.nc, self.S
        st = Stage(self, "ffn")
        Win = st.sb("win", [128, 8, 2 * DFF], BF16)
        Wout = st.sb("wout", [128, NFC, D], BF16)
        BWin = self.load_w(Win, self.W["ffn_w_in"][l], None)
        BWout = []
        osrc = self.W["ffn_w_out"][l].rearrange("(fc p) n -> p fc n", p=128)
        for f0 in range(0, NFC, 2):
            b = Buf()
            S.dma("pool", Wout[:, f0:f0 + 2, :], osrc[:, f0:f0 + 2, :], writes=[b])
            BWout.append(b)
        NH = BLK + 2
        xs = [st.sb(f"xs{i}", [128, 8, NH]) for i in range(2)]
        hb = [st.sb(f"hb{i}", [128, 8, NH], BF16) for i in range(2)]
        gt_ = [st.sb(f"g{i}", [128, NFC, BLK], BF16) for i in range(2)]
        cv = [st.sb(f"cv{i}", [128, BLK]) for i in range(2)]
        sl = [st.sb(f"sl{i}", [128, BLK]) for i in range(2)]
        Bxs, Bhb, Bg, Bcv, Bsl = [[Buf(), Buf()] for _ in range(5)]
        nt = self.norm_tiles(st)
        for i in range(2):
            S.op("dve", lambda: nc.vector.memset(xs[i], 0.0), [], [Bxs[i]])
        xiv = xin.rearrange("(c p) t -> p c t", p=128)
        xov = xout.rearrange("(c p) t -> p c t", p=128)
        blocks = self.blocks(skip_ctx)

        def load(n):
            b, k = blocks[n]
            t0, lo, hi, _, _ = self.blk_range(k)
            S.dma("sp", xs[n % 2][:, :, lo - (t0 - 1):hi - (t0 - 1)], xiv[:, :, b * T + lo:b * T + hi], writes=[Bxs[n % 2]])

        load(0)
        for n, (b, k) in enumerate(blocks):
            i = n % 2
            if n + 1 < len(blocks):
                load(n + 1)
            t0, lo, hi, first, last = self.blk_range(k)
            j = 2 if k == 0 else b
            A, sh, gate = self.mod_ab(l, 1, j)
            self.norm_block(nt, xs[i], Bxs[i], NH, A, sh, hb[i], Bhb[i], 6)
            for fc in range(NFC):
                q = fc % 2
                pa, pvv = self.PS[q], self.PS[2 + q]
                ga = BWin[(fc * 128) // 512]
                gv = BWin[(DFF + fc * 128) // 512]
                for kc in range(8):
                    S.op("pe", lambda: nc.tensor.matmul(pa[:, :NH], lhsT=Win[:, kc, fc * 128:(fc + 1) * 128], rhs=hb[i][:, kc, :], start=(kc == 0), stop=(kc == 7)),
                         [ga, Bhb[i]], [self.BPS[q]])
                for kc in range(8):
                    S.op("pe", lambda: nc.tensor.matmul(pvv[:, :BLK], lhsT=Win[:, kc, DFF + fc * 128:DFF + (fc + 1) * 128], rhs=hb[i][:, kc, 1:1 + BLK], start=(kc == 0), stop=(kc == 7)),
                         [gv, Bhb[i]], [self.BPS[2 + q]])
                w0, w1, w2, cb = self.pv(f"conv{l}_0", fc), self.pv(f"conv{l}_1", fc), self.pv(f"conv{l}_2", fc), self.pv(f"convb{l}", fc)
                S.op("act", lambda: nc.scalar.activation(out=cv[q], in_=pa[:, 1:1 + BLK], func=AF.Identity, scale=w1, bias=cb), [self.BPS[q]], [Bcv[q]])
                c0 = 1 if first else 0
                S.op("dve", lambda: nc.vector.scalar_tensor_tensor(out=cv[q][:, c0:BLK], in0=pa[:, c0:BLK], scalar=w0, in1=cv[q][:, c0:BLK], op0=ALU.mult, op1=ALU.add),
                     [self.BPS[q], Bcv[q]], [Bcv[q]])
                c1 = BLK - 1 if last else BLK
                S.op("dve", lambda: nc.vector.scalar_tensor_tensor(out=cv[q][:, 0:c1], in0=pa[:, 2:2 + c1], scalar=w2, in1=cv[q][:, 0:c1], op0=ALU.mult, op1=ALU.add),
                     [self.BPS[q], Bcv[q]], [Bcv[q]])
                S.op("act", lambda: nc.scalar.activation(out=sl[q], in_=cv[q], func=AF.Silu), [Bcv[q]], [Bsl[q]])
                S.op("dve", lambda: nc.vector.tensor_tensor(out=gt_[i][:, fc, :], in0=sl[q], in1=pvv[:, :BLK], op=ALU.mult), [Bsl[q], self.BPS[2 + q]], [Bg[i]])
            for oc in range(8):
                q = 4 + oc % 2
                po = self.PS[q]
                for fc in range(NFC):
                    S.op("pe", lambda: nc.tensor.matmul(po[:, :BLK], lhsT=Wout[:, fc, oc * 128:(oc + 1) * 128], rhs=gt_[i][:, fc, :], start=(fc == 0), stop=(fc == NFC - 1)),
                         [BWout[fc // 2], Bg[i]], [self.BPS[q]])
                S.op("dve", lambda: nc.vector.scalar_tensor_tensor(out=xs[i][:, oc, 1:1 + BLK], in0=po[:, :BLK], scalar=gate[:, oc:oc + 1], in1=xs[i][:, oc, 1:1 + BLK], op0=ALU.mult, op1=ALU.add),
                     [self.BPS[q], Bxs[i], self.BMOD], [Bxs[i]])
            S.dma("pool", xov[:, :, b * T + t0:b * T + t0 + BLK], xs[i][:, :, 1:1 + BLK], reads=[Bxs[i]])
        st.close()

    def final_stage(self, xin):
        nc, S = self.nc, self.S
        st = Stage(self, "fin")
        xs = [st.sb(f"xs{i}", [128, 8, BLK]) for i in range(2)]
        hb = [st.sb(f"hb{i}", [128, 8, BLK]) for i in range(2)]
        ot = [st.sb(f"ot{i}", [128, D]) for i in range(2)]
        Bxs, Bhb, Bot = [[Buf(), Buf()] for _ in range(3)]
        nt = self.norm_tiles(st, BLK)
        xiv = xin.rearrange("(c p) t -> p c t", p=128)
        blocks = self.blocks(True)
        A = self.pv("norm_f")

        def load(n):
            b, k = blocks[n]
            S.dma("sp", xs[n % 2], xiv[:, :, b * T + k * BLK:b * T + (k + 1) * BLK], writes=[Bxs[n % 2]])

        load(0)
        nt_i = 0
        for n, (b, k) in enumerate(blocks):
            i = n % 2
            if n + 1 < len(blocks):
                load(n + 1)
            self.norm_block(nt, xs[i], Bxs[i], BLK, A, None, hb[i], Bhb[i], 6)
            for tt in range(2):
                o = nt_i % 2
                nt_i += 1
                for hf in range(2):
                    pb = 2 * o + hf
                    for c4 in range(4):
                        c = hf * 4 + c4
                        S.op("pe", lambda: nc.tensor.transpose(out=self.PS[pb][:, c4 * 128:(c4 + 1) * 128], in_=hb[i][:, c, tt * 128:(tt + 1) * 128], identity=self.ident),
                             [Bhb[i]], [self.BPS[pb]])
                    if hf == 0:
                        S.op("dve", lambda: nc.vector.tensor_copy(out=ot[o][:, 0:512], in_=self.PS[pb]), [self.BPS[pb]], [Bot[o]])
                    else:
                        S.op("act", lambda: nc.scalar.copy(out=ot[o][:, 512:1024], in_=self.PS[pb]), [self.BPS[pb]], [Bot[o]])
                tl = k * BLK - TC + tt * 128
                S.dma("pool", self.out[b, tl:tl + 128, :], ot[o], reads=[Bot[o]])
        st.close()


def build_program(wshapes, pv_off, npv, plan=None, dbg=()):
    P = Prog(wshapes, pv_off, npv, dbg=dbg)
    xa = P.scr("xA", [D, TT])
    xb = P.scr("xB", [D, TT])
    if plan is None:
        plan = ["tr", "mod"]
        for l in range(DEPTH):
            plan += [f"mix{l}", f"ffn{l}"]
        plan += ["final"]
    cur, nxt = xa, xb
    for step in plan:
        if step == "tr":
            P.prologue_transpose(cur)
        elif step == "mod":
            P.prologue_mod()
        elif step.startswith("mix"):
            l = int(step[3:])
            P.mixer(l, cur, nxt)
            cur, nxt = nxt, cur
        elif step.startswith("ffn"):
            l = int(step[3:])
            P.ffn_stage(l, cur, nxt, skip_ctx=(l == DEPTH - 1))
            cur, nxt = nxt, cur
        elif step == "final":
            P.final_stage(cur)
    P.S.barrier()
    return P


def prep_inputs(inputs, cores=range(NCORES)):
    pv = pvec_layout(inputs)
    pva = pv.array()
    consts = make_consts()
    shared = {"pvec": pva}
    for k, v in consts.items():
        shared["c_" + k] = v
    for n in WEIGHT_NAMES:
        shared[n] = np.ascontiguousarray(inputs[n], dtype=np.float32)
    in_maps = []
    for c in cores:
        m = dict(shared)
        m["x"] = np.ascontiguousarray(inputs["x"][NB * c:NB * (c + 1)], dtype=np.float32)
        m["ctx"] = np.ascontiguousarray(inputs["ctx"][NB * c:NB * (c + 1)], dtype=np.float32)
        m["cvec"] = np.ascontiguousarray(np.concatenate([inputs["c"][NB * c:NB * (c + 1)], inputs["c_ctx"][None, :]], axis=0), dtype=np.float32)
        in_maps.append(m)
    wshapes = {n: list(inputs[n].shape) for n in WEIGHT_NAMES}
    return in_maps, wshapes, pv.off, pva.shape[1]


def kernel(**inputs):
    inputs = {k: np.asarray(v) for k, v in inputs.items()}
    in_maps, wshapes, pv_off, npv = prep_inputs(inputs)
    P = build_program(wshapes, pv_off, npv)
    res = run_bass_kernel_spmd(P.nc, in_maps, core_ids=list(range(NCORES)))
    out = np.concatenate([np.asarray(r["out"]) for r in res.results], axis=0)
    return out.astype(np.float32)


def _inproj_stage(self, l, xin, Wd, N, dst_fm, tm_specs, f32_h=False):
    nc, S = self.nc, self.S
    st = Stage(self, "ip")
    Wt = st.sb("w", [128, 8, N], BF16)
    BW = self.load_w(Wt, Wd, None)
    xs = [st.sb(f"xs{i}", [128, 8, BLK]) for i in range(2)]
    hb = [st.sb(f"hb{i}", [128, 8, BLK], BF16) for i in range(2)]
    sg = [st.sb(f"sg{i}", [128, 8, BLK]) for i in range(2)]
    tmw = max([nc_ for (_, nc_, _) in tm_specs], default=0)
    tms = [st.sb(f"tm{i}", [128, max(tmw, 1)], BF16) for i in range(2)]
    Bxs, Bhb, Bsg, Btm = [[Buf(), Buf()] for _ in range(4)]
    nt = self.norm_tiles(st, BLK)
    xiv = xin.rearrange("(c p) t -> p c t", p=128)
    dv = dst_fm.rearrange("(c p) t -> p c t", p=128)
    blocks = self.blocks(False)

    def load(n):
        b, k = blocks[n]
        S.dma("sp", xs[n % 2], xiv[:, :, b * T + k * BLK:b * T + (k + 1) * BLK], writes=[Bxs[n % 2]])

    load(0)
    sgi = 0
    tmi = 0
    pbank = 0
    for n, (b, k) in enumerate(blocks):
        i = n % 2
        if n + 1 < len(blocks):
            load(n + 1)
        j = 2 if k == 0 else b
        A, sh, _ = self.mod_ab(l, 0, j)
        self.norm_block(nt, xs[i], Bxs[i], BLK, A, sh, hb[i], Bhb[i], 6)
        col = b * T + k * BLK
        for og in range(N // 1024):
            s_ = sgi % 2
            sgi += 1
            for o8 in range(8):
                oc = og * 8 + o8
                pb = pbank % 4
                pbank += 1
                for kc in range(8):
                    S.op("pe", lambda: nc.tensor.matmul(self.PS[pb][:, :BLK], lhsT=Wt[:, kc, oc * 128:(oc + 1) * 128], rhs=hb[i][:, kc, :], start=(kc == 0), stop=(kc == 7)),
                         [BW[(oc * 128) // 512], Bhb[i]], [self.BPS[pb]])
                if o8 % 2 == 0:
                    S.op("act", lambda: nc.scalar.copy(out=sg[s_][:, o8, :], in_=self.PS[pb][:, :BLK]), [self.BPS[pb]], [Bsg[s_]])
                else:
                    S.op("dve", lambda: nc.vector.tensor_copy(out=sg[s_][:, o8, :], in_=self.PS[pb][:, :BLK]), [self.BPS[pb]], [Bsg[s_]])
            S.dma("pool", dv[:, og * 8:(og + 1) * 8, col:col + BLK], sg[s_], reads=[Bsg[s_]])
        for (c0, ncols, dst_tm) in tm_specs:
            for tt in range(BLK // 128):
                s_ = tmi % 2
                tmi += 1
                for n0 in range(0, ncols, 512):
                    pb = 4 + (pbank % 2)
                    pbank += 1
                    for kc in range(8):
                        S.op("pe", lambda: nc.tensor.matmul(self.PS[pb][:, :512], lhsT=hb[i][:, kc, tt * 128:(tt + 1) * 128], rhs=Wt[:, kc, c0 + n0:c0 + n0 + 512], start=(kc == 0), stop=(kc == 7)),
                             [BW[(c0 + n0) // 512], Bhb[i]], [self.BPS[pb]])
                    S.op("act", lambda: nc.scalar.copy(out=tms[s_][:, n0:n0 + 512], in_=self.PS[pb][:, :512]), [self.BPS[pb]], [Btm[s_]])
                S.dma("pool", dst_tm[col + tt * 128:col + (tt + 1) * 128, :], tms[s_][:, :ncols], reads=[Btm[s_]])
    st.close()


def _outproj_stage(self, l, og, Wd, xin, xout, skip_ctx):
    nc, S = self.nc, self.S
    st = Stage(self, "op")
    Wt = st.sb("w", [128, 8, D], BF16)
    BW = self.load_w(Wt, Wd, None)
    xs = [st.sb(f"xs{i}", [128, 8, BLK]) for i in range(2)]
    ob = [st.sb(f"ob{i}", [128, 8, BLK], BF16) for i in range(2)]
    Bxs, Bob = [[Buf(), Buf()] for _ in range(2)]
    xiv = xin.rearrange("(c p) t -> p c t", p=128)
    xov = xout.rearrange("(c p) t -> p c t", p=128)
    ogv = og.rearrange("(c p) t -> p c t", p=128)
    blocks = self.blocks(skip_ctx)

    def load(n):
        b, k = blocks[n]
        col = b * T + k * BLK
        S.dma("sp", xs[n % 2], xiv[:, :, col:col + BLK], writes=[Bxs[n % 2]])
        S.dma("sp", ob[n % 2], ogv[:, :, col:col + BLK], writes=[Bob[n % 2]])

    load(0)
    for n, (b, k) in enumerate(blocks):
        i = n % 2
        if n + 1 < len(blocks):
            load(n + 1)
        j = 2 if k == 0 else b
        _, _, gate = self.mod_ab(l, 0, j)
        for oc in range(8):
            pb = oc % 4
            for kc in range(8):
                S.op("pe", lambda: nc.tensor.matmul(self.PS[pb][:, :BLK], lhsT=Wt[:, kc, oc * 128:(oc + 1) * 128], rhs=ob[i][:, kc, :], start=(kc == 0), stop=(kc == 7)),
                     [BW[(oc * 128) // 512], Bob[i]], [self.BPS[pb]])
            S.op("dve", lambda: nc.vector.scalar_tensor_tensor(out=xs[i][:, oc, :], in0=self.PS[pb][:, :BLK], scalar=gate[:, oc:oc + 1], in1=xs[i][:, oc, :], op0=ALU.mult, op1=ALU.add),
                 [self.BPS[pb], Bxs[i], self.BMOD], [Bxs[i]])
        col = b * T + k * BLK
        S.dma("pool", xov[:, :, col:col + BLK], xs[i], reads=[Bxs[i]])
    st.close()


def _hgrn2_scan(self, jh, Pfm, Itm, og):
    nc, S = self.nc, self.S
    st = Stage(self, "hs")
    A_ = nc.vector
    LB = st.sb("LB", [128, 2, 8])
    OML = st.sb("OML", [128, 2, 8])
    e0 = st.sb("e0", [128, 8]); e1 = st.sb("e1", [128, 8]); rr = st.sb("rr", [128, 8]); p0 = st.sb("p0", [128, 8]); p1 = st.sb("p1", [128, 8])
    BL = Buf()
    for d in range(2):
        S.op("act", lambda: nc.scalar.activation(out=e0, in_=self.pv(f"hg_lb{d}_0"), func=AF.Exp), [], [BL])
        S.op("act", lambda: nc.scalar.activation(out=e1, in_=self.pv(f"hg_lb{d}_1"), func=AF.Exp), [BL], [BL])
        S.op("dve", lambda: A_.tensor_tensor(out=rr, in0=e0, in1=e1, op=ALU.add), [BL], [BL])
        S.op("dve", lambda: A_.reciprocal(out=rr, in_=rr), [BL], [BL])
        S.op("dve", lambda: A_.tensor_tensor(out=p0, in0=e0, in1=rr, op=ALU.mult), [BL], [BL])
        S.op("dve", lambda: A_.tensor_tensor(out=p1, in0=e1, in1=rr, op=ALU.mult), [BL], [BL])
        if jh == 1:
            S.op("dve", lambda: A_.tensor_tensor(out=p1, in0=p0, in1=p1, op=ALU.add), [BL], [BL])
        else:
            S.op("dve", lambda: A_.tensor_copy(out=p1, in_=p0), [BL], [BL])
        S.op("dve", lambda: A_.tensor_tensor(out=LB[:, d, :], in0=p1, in1=p0, op=ALU.subtract), [BL], [BL])
        S.op("dve", lambda: A_.tensor_scalar(out=OML[:, d, :], in0=LB[:, d, :], scalar1=-1.0, scalar2=1.0, op0=ALU.mult, op1=ALU.add), [BL], [BL])
    smask = st.sb("smask", [128, T])
    Bsm = Buf()
    S.dma("sp", smask, self.cd["scanmask"], writes=[Bsm])
    f32t = lambda n: st.sb(n, [128, T])
    qs = f32t("qs"); graw = f32t("graw"); kk = f32t("kk"); ep = f32t("ep"); en = f32t("en")
    z = [f32t("z0"), f32t("z1")]; bb = [f32t("b0"), f32t("b1")]; of = [f32t("of0"), f32t("of1")]
    qt = [st.sb(f"qt{d}", [128, T], BF16) for d in range(2)]
    kh = [st.sb(f"kh{d}", [128, T], BF16) for d in range(2)]
    sqb = st.sb("sqb", [128, T], BF16)
    ogb = st.sb("ogb", [128, T], BF16)
    Vt = st.sb("Vt", [64, NCH, 128], BF16)
    emid = [st.sb(f"emid{d}", [128, NCH]) for d in range(2)]
    eend = [st.sb(f"eend{d}", [128, NCH]) for d in range(2)]
    eem = [st.sb(f"eem{d}", [128, NCH]) for d in range(2)]
    Sst = [st.sb(f"S{d}", [128, 128]) for d in range(2)]
    Sm = [st.sb(f"Sm{d}", [128, 128], BF16) for d in range(2)]
    tmpS = [st.sb(f"tS{d}", [128, 128]) for d in range(2)]
    khT = [st.sb(f"khT{d}", [64, 128], BF16) for d in range(2)]
    att = [st.sb(f"att{d}", [64, 64], BF16) for d in range(2)]
    Bqs, Bgr, Bkk, Bep, Ben, Bsq, Bog, BVt = [Buf() for _ in range(8)]
    Bz, Bbb, Bof, Bqt, Bkh, Bes, BS, BSm, BtS, BkT, Batt = [[Buf(), Buf()] for _ in range(11)]
    PSb = [self.PS[i].bitcast(BF16) for i in range(8)]
    for d in range(2):
        S.op("dve", lambda: A_.memset(att[d], 0.0), [], [Batt[d]])
    cf = list(range(NCH))
    cb = list(range(TC // CH - 1, -1, -1)) + list(range(NCH - 1, TC // CH - 1, -1))
    order = [cf, cb]
    for b in range(NB):
        for h in range(8):
            rows = slice(h * 128, (h + 1) * 128)
            cols = slice(b * T, (b + 1) * T)
            S.dma("sp", qs, Pfm[0 * D + h * 128:0 * D + (h + 1) * 128, cols], writes=[Bqs])
            S.dma("sp", z[0], Pfm[3 * D + h * 128:3 * D + (h + 1) * 128, cols], writes=[Bz[0]])
            S.dma("sp", z[1], Pfm[4 * D + h * 128:4 * D + (h + 1) * 128, cols], writes=[Bz[1]])
            S.dma("sp", graw, Pfm[2 * D + h * 128:2 * D + (h + 1) * 128, cols], writes=[Bgr])
            S.dma("sp", Vt, Itm[cols, rows].rearrange("(c s) v -> s c v", s=CH), writes=[BVt])
            S.op("act", lambda: nc.scalar.activation(out=qs, in_=qs, func=AF.Silu), [Bqs], [Bqs])
            for d in range(2):
                m_idx = 32 if d == 0 else 31
                zt = z[d]
                S.op("act", lambda: nc.scalar.activation(out=zt, in_=zt, func=AF.Sigmoid), [Bz[d]], [Bz[d]])
                S.op("dve", lambda: A_.tensor_scalar(out=zt, in0=zt, scalar1=OML[:, d, h:h + 1], scalar2=LB[:, d, h:h + 1], op0=ALU.mult, op1=ALU.add), [Bz[d], BL], [Bz[d]])
                S.op("dve", lambda: A_.tensor_scalar(out=kk, in0=zt, scalar1=-1.0, scalar2=1.0, op0=ALU.mult, op1=ALU.add), [Bz[d]], [Bkk])
                S.op("act", lambda: nc.scalar.activation(out=zt, in_=zt, func=AF.Ln), [Bz[d]], [Bz[d]])
                S.op("dve", lambda: A_.tensor_tensor_scan(out=bb[d], data0=smask, data1=zt, initial=0.0, op0=ALU.mult, op1=ALU.add), [Bsm, Bz[d]], [Bbb[d]])
                b3 = bb[d].rearrange("p (c s) -> p c s", s=CH)
                if d == 1:
                    S.op("dve", lambda: A_.tensor_tensor(out=zt, in0=zt, in1=bb[d], op=ALU.subtract), [Bz[d], Bbb[d]], [Bz[d]])
                    S.op("dve", lambda: A_.tensor_tensor(out=ep.rearrange("p (c s) -> p c s", s=CH), in0=zt.rearrange("p (c s) -> p c s", s=CH),
                                                          in1=b3[:, :, CH - 1:CH].to_broadcast([128, NCH, CH]), op=ALU.add), [Bz[d], Bbb[d]], [Bep])
                    S.op("dve", lambda: A_.tensor_copy(out=bb[d], in_=ep), [Bep], [Bbb[d]])
                e_idx = CH - 1 if d == 0 else 0
                S.op("act", lambda: nc.scalar.activation(out=emid[d], in_=b3[:, :, m_idx], func=AF.Exp), [Bbb[d]], [Bes[d]])
                S.op("act", lambda: nc.scalar.activation(out=eend[d], in_=b3[:, :, e_idx], func=AF.Exp), [Bbb[d]], [Bes[d]])
                S.op("dve", lambda: A_.tensor_tensor(out=eem[d], in0=b3[:, :, e_idx], in1=b3[:, :, m_idx], op=ALU.subtract), [Bbb[d]], [Bes[d]])
                S.op("act", lambda: nc.scalar.activation(out=eem[d], in_=eem[d], func=AF.Exp), [Bes[d]], [Bes[d]])
                S.op("dve", lambda: A_.tensor_tensor(out=ep.rearrange("p (c s) -> p c s", s=CH), in0=b3, in1=b3[:, :, m_idx:m_idx + 1].to_broadcast([128, NCH, CH]), op=ALU.subtract),
                     [Bbb[d]], [Bep])
                S.op("act", lambda: nc.scalar.activation(out=en, in_=ep, func=AF.Exp, scale=-1.0), [Bep], [Ben])
                S.op("act", lambda: nc.scalar.activation(out=ep, in_=ep, func=AF.Exp), [Bep], [Bep])
                S.op("dve", lambda: A_.tensor_tensor(out=qt[d], in0=qs, in1=ep, op=ALU.mult), [Bqs, Bep], [Bqt[d]])
                S.op("dve", lambda: A_.tensor_tensor(out=kh[d], in0=kk, in1=en, op=ALU.mult), [Bkk, Ben], [Bkh[d]])
                S.op("dve", lambda: A_.memset(Sst[d], 0.0), [], [BS[d]])
                S.op("dve", lambda: A_.memset(Sm[d], 0.0), [], [BSm[d]])
            for step in range(NCH):
                for d in range(2):
                    c = order[d][step]
                    cs = slice(c * CH, (c + 1) * CH)
                    pb = d * 4
                    mk = (self.masks[0:64, 64:128] if d == 0 else self.masks[0:64, 192:256]).bitcast(mybir.dt.uint32)
                    S.op("pe", lambda: nc.tensor.transpose(out=PSb[pb][0:64, 0:128], in_=kh[d][:, cs], identity=self.identb), [Bkh[d]], [self.BPS[pb]])
                    S.op("act", lambda: nc.scalar.copy(out=khT[d], in_=PSb[pb][0:64, 0:128]), [self.BPS[pb]], [BkT[d]])
                    S.op("pe", lambda: nc.tensor.matmul(self.PS[pb + 1][0:64, 0:64], lhsT=kh[d][:, cs], rhs=qt[d][:, cs], start=True, stop=True), [Bkh[d], Bqt[d]], [self.BPS[pb + 1]])
                    S.op("dve", lambda: A_.copy_predicated(out=att[d], mask=mk, data=self.PS[pb + 1][0:64, 0:64]), [self.BPS[pb + 1]], [Batt[d]])
                    S.op("pe", lambda: nc.tensor.matmul(self.PS[pb + 2][:, 0:64], lhsT=Vt[:, c, :], rhs=att[d], start=True, stop=False), [BVt, Batt[d]], [self.BPS[pb + 2]])
                    S.op("pe", lambda: nc.tensor.matmul(self.PS[pb + 2][:, 0:64], lhsT=Sm[d], rhs=qt[d][:, cs], start=False, stop=True), [BSm[d], Bqt[d]], [self.BPS[pb + 2]])
                    S.op("act", lambda: nc.scalar.copy(out=of[d][:, cs], in_=self.PS[pb + 2][:, 0:64]), [self.BPS[pb + 2]], [Bof[d]])
                    S.op("pe", lambda: nc.tensor.matmul(self.PS[pb + 3][:, 0:128], lhsT=khT[d], rhs=Vt[:, c, :], start=True, stop=True), [BkT[d], BVt], [self.BPS[pb + 3]])
                    S.op("act", lambda: nc.scalar.activation(out=tmpS[d], in_=self.PS[pb + 3][:, 0:128], func=AF.Identity, scale=eem[d][:, c:c + 1]), [self.BPS[pb + 3], Bes[d]], [BtS[d]])
                    S.op("dve", lambda: A_.scalar_tensor_tensor(out=Sst[d], in0=Sst[d], scalar=eend[d][:, c:c + 1], in1=tmpS[d], op0=ALU.mult, op1=ALU.add), [BS[d], BtS[d], Bes[d]], [BS[d]])
                    if step + 1 < NCH:
                        cn = order[d][step + 1]
                        S.op("dve", lambda: A_.tensor_scalar(out=Sm[d], in0=Sst[d], scalar1=emid[d][:, cn:cn + 1], scalar2=None, op0=ALU.mult), [BS[d], Bes[d]], [BSm[d]])
            S.op("dve", lambda: A_.tensor_tensor(out=of[0], in0=of[0], in1=of[1], op=ALU.add), [Bof[0], Bof[1]], [Bof[0]])
            S.op("act", lambda: nc.scalar.activation(out=sqb, in_=of[0], func=AF.Square), [Bof[0]], [Bsq])
            for pc in range(6):
                sl_ = slice(pc * 384, (pc + 1) * 384)
                pb = pc % 2
                S.op("pe", lambda: nc.tensor.matmul(self.PS[pb][:, 0:384], lhsT=self.onesb, rhs=sqb[:, sl_], start=True, stop=True), [Bsq], [self.BPS[pb]])
                S.op("act", lambda: nc.scalar.activation(out=ep[:, sl_], in_=self.PS[pb][:, 0:384], func=AF.Sqrt, scale=1.0 / 128, bias=self.epsD), [self.BPS[pb]], [Bep])
            S.op("dve", lambda: A_.reciprocal(out=ep, in_=ep), [Bep], [Bep])
            S.op("dve", lambda: A_.tensor_tensor(out=of[0], in0=of[0], in1=ep, op=ALU.mult), [Bof[0], Bep], [Bof[0]])
            S.op("act", lambda: nc.scalar.activation(out=graw, in_=graw, func=AF.Silu), [Bgr], [Bgr])
            S.op("dve", lambda: A_.scalar_tensor_tensor(out=ogb, in0=of[0], scalar=self.pv(f"hg_norm{jh}", 0), in1=graw, op0=ALU.mult, op1=ALU.mult), [Bof[0], Bgr], [Bog])
            S.dma("pool", og[rows, cols], ogb, reads=[Bog])
    st.close()


def _mixer(self, l, cur, nxt):
    kind, j = l % 3, l // 3
    last = (l == DEPTH - 1)
    og = self.scr("og", [D, TT], BF16)
    if kind == 0:
        Pfm = self.scr("hgP", [5 * D, TT])
        Itm = self.scr("hgI", [TT, D], BF16)
        self.inproj_stage(l, cur, self.W["hg_w_in"][j], 5 * D, Pfm, [(D, D, Itm)])
        self.hgrn2_scan(j, Pfm, Itm, og)
        self.outproj_stage(l, og, self.W["hg_w_o"][j], cur, nxt, last)
    elif kind == 1:
        self.rwkv_mixer(l, cur, og)
        self.outproj_stage(l, og, self.W["rw_w_o"][j], cur, nxt, last)
    else:
        self.mla_mixer(l, cur, og)
        self.outproj_stage(l, og, self.W["mla_w_o"][j], cur, nxt, last)


Prog.inproj_stage = _inproj_stage
Prog.outproj_stage = _outproj_stage
Prog.hgrn2_scan = _hgrn2_scan
Prog.mixer = _mixer


def _mla_mixer(self, l, xin, og):
    nc, S = self.nc, self.S
    A_ = nc.vector
    NH = 16
    QN = self.scr("mlaQN", [64, NH, TT], BF16)
    QR = self.scr("mlaQR", [32, NH, TT], BF16)
    KN = self.scr("mlaKN", [64, NH, TT], BF16)
    KR = self.scr("mlaKR", [32, TT], BF16)
    VT = self.scr("mlaVT", [TT, D], BF16)
    st = Stage(self, "m1")
    Wd = st.sb("wd", [128, 8, 544], BF16)
    Wq = st.sb("wq", [128, 2, 1536], BF16)
    Wk = st.sb("wk", [128, 2, 2048], BF16)
    Wdr = st.sb("wdr", [128, 8, 32], BF16)
    Wqr = st.sb("wqr", [128, 2, NH, 32], BF16)
    BWd, BWq, BWk, BWr = Buf(), Buf(), Buf(), Buf()
    S.dma("pool", Wd, self.W["mla_w_dqkv"][0].rearrange("(kc p) n -> p kc n", p=128), writes=[BWd])
    wqv = self.W["mla_w_uq"][0].rearrange("(kc p) n -> p kc n", p=128)
    for i3 in range(3):
        S.dma("pool", Wq[:, :, i3 * 512:(i3 + 1) * 512], wqv[:, :, i3 * 512:(i3 + 1) * 512], writes=[BWq])
    wkv = self.W["mla_w_ukv"][0].rearrange("(kc p) n -> p kc n", p=128)
    for i4 in range(4):
        S.dma("pool", Wk[:, :, i4 * 512:(i4 + 1) * 512], wkv[:, :, i4 * 512:(i4 + 1) * 512], writes=[BWk])
    Wq4 = Wq.rearrange("p k (h c) -> p k h c", c=96)
    for seg in range(2):
        for half in range(2):
            sgn = -1.0 if half == 0 else 1.0
            so = 64 + seg * 16 + (1 - half) * 8
            do = seg * 16 + half * 8
            S.op("act", lambda: nc.scalar.activation(out=Wqr[:, :, :, do:do + 8], in_=Wq4[:, :, :, so:so + 8], func=AF.Copy, scale=sgn), [BWq], [BWr])
            so2 = 512 + seg * 16 + (1 - half) * 8
            S.op("act", lambda: nc.scalar.activation(out=Wdr[:, :, do:do + 8], in_=Wd[:, :, so2:so2 + 8], func=AF.Copy, scale=sgn), [BWd], [BWr])
    cos = st.sb("cos", [32, T]); sin = st.sb("sin", [32, T])
    Bcs = Buf()
    S.dma("sp", cos, self.cd["rope_cos"], writes=[Bcs])
    S.dma("sp", sin, self.cd["rope_sin"], writes=[Bcs])
    xs = [st.sb(f"xs{i}", [128, 8, BLK]) for i in range(2)]
    hb = [st.sb(f"hb{i}", [128, 8, BLK], BF16) for i in range(2)]
    Bxs, Bhb = [[Buf(), Buf()] for _ in range(2)]
    nt = self.norm_tiles(st, BLK)
    cs_ = st.sb("cs", [128, 4, BLK]); csq = st.sb("csq", [128, 4, BLK], BF16); cn = st.sb("cn", [128, 4, BLK], BF16)
    rr0 = st.sb("rr0", [128, 2, BLK]); rr1 = st.sb("rr1", [128, 2, BLK]); ctmp = st.sb("ctmp", [128, 4, BLK])
    Bcs_, Bcsq, Bcn, Brr, Bct = [Buf() for _ in range(5)]
    qn_s = [st.sb(f"qns{i}", [64, NH, BLK], BF16) for i in range(2)]
    kn_s = [st.sb(f"kns{i}", [64, NH, BLK], BF16) for i in range(2)]
    qr_s = [st.sb(f"qrs{i}", [32, NH, BLK], BF16) for i in range(2)]
    kr_s = [st.sb(f"krs{i}", [32, BLK], BF16) for i in range(2)]
    vt_s = [st.sb(f"vts{i}", [128, D], BF16) for i in range(2)]
    t1 = st.sb("t1", [32, 2, BLK]); t2 = st.sb("t2", [32, 2, BLK])
    Bt1, Bt2 = Buf(), Buf()
    Bqn, Bkn, Bqr, Bkr, Bvt = [[Buf(), Buf()] for _ in range(5)]
    xiv = xin.rearrange("(c p) t -> p c t", p=128)
    blocks = self.blocks(False)

    def load(n):
        b, k = blocks[n]
        S.dma("sp", xs[n % 2], xiv[:, :, b * T + k * BLK:b * T + (k + 1) * BLK], writes=[Bxs[n % 2]])

    load(0)
    vti = 0
    for n, (b, k) in enumerate(blocks):
        i = n % 2
        if n + 1 < len(blocks):
            load(n + 1)
        j = 2 if k == 0 else b
        A, sh, _ = self.mod_ab(l, 0, j)
        self.norm_block(nt, xs[i], Bxs[i], BLK, A, sh, hb[i], Bhb[i], 6)
        col = b * T + k * BLK
        tcol = slice(k * BLK, (k + 1) * BLK)
        for c4 in range(4):
            pb = c4 // 2
            for kc in range(8):
                S.op("pe", lambda: nc.tensor.matmul(self.PS[pb][:, (c4 % 2) * BLK:(c4 % 2 + 1) * BLK], lhsT=Wd[:, kc, c4 * 128:(c4 + 1) * 128], rhs=hb[i][:, kc, :], start=(kc == 0), stop=(kc == 7)),
                     [BWd, Bhb[i]], [self.BPS[pb]])
        for kc in range(8):
            S.op("pe", lambda: nc.tensor.matmul(self.PS[2][0:32, 0:BLK], lhsT=Wd[:, kc, 512:544], rhs=hb[i][:, kc, :], start=(kc == 0), stop=(kc == 7)), [BWd, Bhb[i]], [self.BPS[2]])
        for kc in range(8):
            S.op("pe", lambda: nc.tensor.matmul(self.PS[2][0:32, BLK:2 * BLK], lhsT=Wdr[:, kc, :], rhs=hb[i][:, kc, :], start=(kc == 0), stop=(kc == 7)), [BWr, Bhb[i]], [self.BPS[2]])
        for pb in range(2):
            S.op("act", lambda: nc.scalar.copy(out=cs_[:, 2 * pb:2 * pb + 2, :], in_=self.PS[pb].rearrange("p (c t) -> p c t", c=2)), [self.BPS[pb]], [Bcs_])
            S.op("act", lambda: nc.scalar.activation(out=csq[:, 2 * pb:2 * pb + 2, :], in_=self.PS[pb].rearrange("p (c t) -> p c t", c=2), func=AF.Square), [self.BPS[pb]], [Bcsq])
        S.op("dve", lambda: A_.tensor_tensor(out=t1[:, 0, :], in0=self.PS[2][0:32, 0:BLK], in1=cos[:, tcol], op=ALU.mult), [self.BPS[2], Bcs], [Bt1])
        S.op("dve", lambda: A_.tensor_tensor(out=t2[:, 0, :], in0=self.PS[2][0:32, BLK:2 * BLK], in1=sin[:, tcol], op=ALU.mult), [self.BPS[2], Bcs], [Bt2])
        S.op("dve", lambda: A_.tensor_tensor(out=kr_s[i], in0=t1[:, 0, :], in1=t2[:, 0, :], op=ALU.add), [Bt1, Bt2], [Bkr[i]])
        S.dma("pool", KR[:, col:col + BLK], kr_s[i], reads=[Bkr[i]])
        for w in range(2):
            for c in range(2):
                S.op("pe", lambda: nc.tensor.matmul(self.PS[3][:, w * BLK:(w + 1) * BLK], lhsT=self.onesb, rhs=csq[:, 2 * w + c, :], start=(c == 0), stop=(c == 1)), [Bcsq], [self.BPS[3]])
        S.op("act", lambda: nc.scalar.activation(out=rr0, in_=self.PS[3].rearrange("p (w t) -> p w t", w=2), func=AF.Sqrt, scale=1.0 / 256, bias=self.epsD), [self.BPS[3]], [Brr])
        S.op("dve", lambda: A_.reciprocal(out=rr1, in_=rr0), [Brr], [Brr])
        S.op("dve", lambda: A_.tensor_tensor(out=ctmp.rearrange("p (w c) t -> p w c t", w=2), in0=cs_.rearrange("p (w c) t -> p w c t", w=2),
                                              in1=rr1.unsqueeze(2).to_broadcast([128, 2, 2, BLK]), op=ALU.mult), [Bcs_, Brr], [Bct])
        for c4 in range(4):
            gname = "mla_q_norm" if c4 < 2 else "mla_kv_norm"
            S.op("act", lambda: nc.scalar.activation(out=cn[:, c4, :], in_=ctmp[:, c4, :], func=AF.Identity, scale=self.pv(gname, c4 % 2)), [Bct], [Bcn])
        for hp in range(8):
            for which in range(2):
                pb = 4 + (2 * hp + which) % 2
                Wt_, coff, hw, ci = (Wq, 0, 96, 0) if which == 0 else (Wk, 0, 128, 2)
                for hh in range(2):
                    h = 2 * hp + hh
                    for kc in range(2):
                        S.op("pe", lambda: nc.tensor.matmul(self.PS[pb][0:64, hh * BLK:(hh + 1) * BLK], lhsT=Wt_[:, kc, h * hw:h * hw + 64], rhs=cn[:, ci + kc, :], start=(kc == 0), stop=(kc == 1)),
                             [BWq if which == 0 else BWk, Bcn], [self.BPS[pb]])
                dst = qn_s[i] if which == 0 else kn_s[i]
                Bd = Bqn[i] if which == 0 else Bkn[i]
                if which == 0:
                    S.op("act", lambda: nc.scalar.copy(out=dst[:, 2 * hp:2 * hp + 2, :], in_=self.PS[pb][0:64, :].rearrange("p (h t) -> p h t", h=2)), [self.BPS[pb]], [Bd])
                else:
                    S.op("dve", lambda: A_.tensor_copy(out=dst[:, 2 * hp:2 * hp + 2, :], in_=self.PS[pb][0:64, :].rearrange("p (h t) -> p h t", h=2)), [self.BPS[pb]], [Bd])
            for hh in range(2):
                h = 2 * hp + hh
                for kc in range(2):
                    S.op("pe", lambda: nc.tensor.matmul(self.PS[6][0:32, hh * BLK:(hh + 1) * BLK], lhsT=Wq[:, kc, h * 96 + 64:h * 96 + 96], rhs=cn[:, kc, :], start=(kc == 0), stop=(kc == 1)), [BWq, Bcn], [self.BPS[6]])
                for kc in range(2):
                    S.op("pe", lambda: nc.tensor.matmul(self.PS[7][0:32, hh * BLK:(hh + 1) * BLK], lhsT=Wqr[:, kc, h, :], rhs=cn[:, kc, :], start=(kc == 0), stop=(kc == 1)), [BWr, Bcn], [self.BPS[7]])
            cosb = cos[:, tcol].unsqueeze(1).to_broadcast([32, 2, BLK])
            sinb = sin[:, tcol].unsqueeze(1).to_broadcast([32, 2, BLK])
            S.op("dve", lambda: A_.tensor_tensor(out=t1, in0=self.PS[6][0:32, :].rearrange("p (h t) -> p h t", h=2), in1=cosb, op=ALU.mult), [self.BPS[6], Bcs], [Bt1])
            S.op("dve", lambda: A_.tensor_tensor(out=t2, in0=self.PS[7][0:32, :].rearrange("p (h t) -> p h t", h=2), in1=sinb, op=ALU.mult), [self.BPS[7], Bcs], [Bt2])
            S.op("dve", lambda: A_.tensor_tensor(out=qr_s[i][:, 2 * hp:2 * hp + 2, :], in0=t1, in1=t2, op=ALU.add), [Bt1, Bt2], [Bqr[i]])
        S.dma("pool", QN[:, :, col:col + BLK], qn_s[i], reads=[Bqn[i]])
        S.dma("pool", KN[:, :, col:col + BLK], kn_s[i], reads=[Bkn[i]])
        S.dma("pool", QR[:, :, col:col + BLK], qr_s[i], reads=[Bqr[i]])
        Wkv = Wk.rearrange("p k (h c) -> p k h c", c=128)
        for tt in range(BLK // 128):
            vi = vti % 2
            vti += 1
            for hf in range(2):
                pb = 4 + hf
                for kc in range(2):
                    S.op("pe", lambda: nc.tensor.matmul(self.PS[pb][:, 0:512], lhsT=cn[:, 2 + kc, tt * 128:(tt + 1) * 128], rhs=Wkv[:, kc, hf * 8:(hf + 1) * 8, 64:128], start=(kc == 0), stop=(kc == 1)),
                         [BWk, Bcn], [self.BPS[pb]])
                S.op("act", lambda: nc.scalar.copy(out=vt_s[vi][:, hf * 512:(hf + 1) * 512], in_=self.PS[pb][:, 0:512]), [self.BPS[pb]], [Bvt[vi]])
            S.dma("pool", VT[col + tt * 128:col + (tt + 1) * 128, :], vt_s[vi], reads=[Bvt[vi]])
    st.close()
    st = Stage(self, "m2")
    NKT = T // 128
    Vall = st.sb("Vall", [128, NKT, D], BF16)
    KRs = st.sb("KRs", [32, T], BF16)
    KNh = [st.sb(f"KNh{i}", [64, T], BF16) for i in range(2)]
    QNh = [st.sb(f"QNh{i}", [64, T], BF16) for i in range(2)]
    QRh = [st.sb(f"QRh{i}", [32, T], BF16) for i in range(2)]
    VX = [st.sb(f"VX{i}", [128, NKT, 65], BF16) for i in range(2)]
    PT = [st.sb(f"PT{i}", [128, 512], BF16) for i in range(3)]
    rd = st.sb("rd", [65, 512]); rb = [st.sb(f"rb{i}", [64, 512]) for i in range(2)]
    ob = [st.sb(f"ob{i}", [64, 512], BF16) for i in range(2)]
    BVa, BKR, Brd = Buf(), Buf(), Buf()
    BKN, BQN, BQR, BVX, Brb, Bob = [[Buf(), Buf()] for _ in range(6)]
    BPT = [Buf() for _ in range(3)]
    for i in range(2):
        S.op("pool", lambda: nc.gpsimd.memset(VX[i], 1.0), [], [BVX[i]])
    qblocks = [(0, TC, 2)] + [(TC + qb * 512, 512, NKT) for qb in range(4)]
    pti = 0
    hn = 0
    for b in range(NB):
        c0 = b * T
        S.dma("sp", Vall, VT[c0:c0 + T, :].rearrange("(kt p) v -> p kt v", p=128), writes=[BVa])
        S.dma("sp", KRs, KR[:, c0:c0 + T], writes=[BKR])
        for h in range(NH):
            i = hn % 2
            hn += 1
            S.dma("sp", KNh[i], KN[:, h, c0:c0 + T], writes=[BKN[i]])
            S.dma("sp", QNh[i], QN[:, h, c0:c0 + T], writes=[BQN[i]])
            S.dma("sp", QRh[i], QR[:, h, c0:c0 + T], writes=[BQR[i]])
            S.op("pool", lambda: nc.gpsimd.tensor_copy(out=VX[i][:, :, 0:64], in_=Vall[:, :, h * 64:(h + 1) * 64]), [BVa], [BVX[i]])
            for qi, (q0, nq, nkt) in enumerate(qblocks):
                po = 4 + (qi % 2)
                for kt in range(nkt):
                    ps = kt % 4
                    p3 = pti % 3
                    pti += 1
                    ks = slice(kt * 128, (kt + 1) * 128)
                    S.op("pe", lambda: nc.tensor.matmul(self.PS[ps][:, 0:nq], lhsT=KNh[i][:, ks], rhs=QNh[i][:, q0:q0 + nq], start=True, stop=False), [BKN[i], BQN[i]], [self.BPS[ps]])
                    S.op("pe", lambda: nc.tensor.matmul(self.PS[ps][:, 0:nq], lhsT=KRs[:, ks], rhs=QRh[i][:, q0:q0 + nq], start=False, stop=True), [BKR, BQR[i]], [self.BPS[ps]])
                    S.op("act", lambda: nc.scalar.activation(out=PT[p3][:, 0:nq], in_=self.PS[ps][:, 0:nq], func=AF.Exp, scale=MLA_SCALE), [self.BPS[ps]], [BPT[p3]])
                    S.op("pe", lambda: nc.tensor.matmul(self.PS[po][0:65, 0:nq], lhsT=VX[i][:, kt, :], rhs=PT[p3][:, 0:nq], start=(kt == 0), stop=(kt == nkt - 1)), [BVX[i], BPT[p3]], [self.BPS[po]])
                r2 = qi % 2
                S.op("dve", lambda: A_.reciprocal(out=rd[64:65, 0:nq], in_=self.PS[po][64:65, 0:nq]), [self.BPS[po]], [Brd])
                S.op("pe", lambda: nc.tensor.matmul(self.PS[6 + r2][0:64, 0:nq], lhsT=self.onesf[64:65, 0:64], rhs=rd[64:65, 0:nq], start=True, stop=True), [Brd], [self.BPS[6 + r2]])
                S.op("act", lambda: nc.scalar.copy(out=rb[r2][:, 0:nq], in_=self.PS[6 + r2][0:64, 0:nq]), [self.BPS[6 + r2]], [Brb[r2]])
                S.op("dve", lambda: A_.tensor_tensor(out=ob[r2][:, 0:nq], in0=self.PS[po][0:64, 0:nq], in1=rb[r2][:, 0:nq], op=ALU.mult), [self.BPS[po], Brb[r2]], [Bob[r2]])
                S.dma("pool", og[h * 64:(h + 1) * 64, c0 + q0:c0 + q0 + nq], ob[r2][:, 0:nq], reads=[Bob[r2]])
    st.close()


Prog.mla_mixer = _mla_mixer


RW_ARR = ["r", "kt0", "kt1", "be0", "be1", "kap", "lw0", "lw1", "v", "g"]


def _rwkv_proj(self, l, xin, RWP, Vtm):
    nc, S = self.nc, self.S
    A_ = nc.vector
    st = Stage(self, "r1")
    Wrkv = st.sb("wrkv", [128, 8, 3 * D], BF16)
    BWrkv = []
    for i3 in range(3):
        v_ = self.W["rw_w_rkv"][0, i3].rearrange("(kc p) n -> p kc n", p=128)
        for hf in range(2):
            bb_ = Buf()
            S.dma("pool", Wrkv[:, :, i3 * D + hf * 512:i3 * D + (hf + 1) * 512], v_[:, :, hf * 512:(hf + 1) * 512], writes=[bb_])
            BWrkv.append(bb_)
    W1 = st.sb("w1", [128, 8, 2, 64], BF16); A1 = st.sb("a1", [128, 8, 2, 64], BF16); G1 = st.sb("g1", [128, 8, 160], BF16)
    W2 = st.sb("w2", [64, 2, D], BF16); A2 = st.sb("a2", [64, 2, D], BF16); G2a = st.sb("g2a", [128, D], BF16); G2b = st.sb("g2b", [32, D], BF16)
    Bsw = Buf()
    for d in range(2):
        S.dma("pool", W1[:, :, d, :], self.W["rw_w1"][0, d].rearrange("(kc p) n -> p kc n", p=128), writes=[Bsw])
        S.dma("pool", A1[:, :, d, :], self.W["rw_a1"][0, d].rearrange("(kc p) n -> p kc n", p=128), writes=[Bsw])
        S.dma("pool", W2[:, d, :], self.W["rw_w2"][0, d], writes=[Bsw])
        S.dma("pool", A2[:, d, :], self.W["rw_a2"][0, d], writes=[Bsw])
    S.dma("pool", G1, self.W["rw_g1"][0].rearrange("(kc p) n -> p kc n", p=128), writes=[Bsw])
    S.dma("pool", G2a, self.W["rw_g2"][0, 0:128, :], writes=[Bsw])
    S.dma("pool", G2b, self.W["rw_g2"][0, 128:160, :], writes=[Bsw])
    NH_ = BLK + 2
    xs = [st.sb(f"xs{i}", [128, 8, NH_]) for i in range(2)]
    hf_ = st.sb("hf", [128, 8, NH_])
    dx = st.sb("dx", [128, 8, BLK])
    xj = [st.sb(f"xj{j}", [128, 8, BLK], BF16) for j in range(6)]
    Bxs = [Buf(), Buf()]
    Bhf, Bdx = Buf(), Buf()
    Bxj = [Buf() for _ in range(6)]
    nt = self.norm_tiles(st)
    lt = st.sb("lt", [64, 5, BLK], BF16)
    gh = st.sb("gh", [128, BLK], BF16)
    Blt = Buf()
    stg = [st.sb(f"stg{i}", [128, 10, BLK]) for i in range(2)]
    Bstg = [Buf(), Buf()]
    tmp = [st.sb(f"tmp{i}", [128, BLK]) for i in range(6)]
    Btmp = [Buf() for _ in range(6)]
    sqb = st.sb("sqb", [128, BLK], BF16)
    Bsqb = Buf()
    vts = [st.sb(f"vts{i}", [128, D], BF16) for i in range(2)]
    Bvts = [Buf(), Buf()]
    for i in range(2):
        S.op("dve", lambda: A_.memset(xs[i], 0.0), [], [Bxs[i]])
    xiv = xin.rearrange("(c p) t -> p c t", p=128)
    blocks = self.blocks(False)
    blk64b = st.sb("blk64b", [128, 128], BF16)
    Bb64 = Buf()
    S.op("dve", lambda: A_.tensor_copy(out=blk64b, in_=self.blk64), [], [Bb64])

    def load(n):
        b, k = blocks[n]
        t0, lo, hi, _, _ = self.blk_range(k)
        S.dma("sp", xs[n % 2][:, :, lo - (t0 - 1):hi - (t0 - 1)], xiv[:, :, b * T + lo:b * T + hi], writes=[Bxs[n % 2]])

    load(0)
    si = 0
    vi_ = 0
    pbk = 0
    for n, (b, k) in enumerate(blocks):
        i = n % 2
        if n + 1 < len(blocks):
            load(n + 1)
        t0, lo, hi, first, last = self.blk_range(k)
        j = 2 if k == 0 else b
        A, sh, _ = self.mod_ab(l, 0, j)
        self.norm_block(nt, xs[i], Bxs[i], NH_, A, sh, hf_, Bhf, 6)
        if first:
            S.op("dve", lambda: A_.memset(hf_[:, :, 0:1], 0.0), [], [Bhf])
        if last:
            S.op("dve", lambda: A_.memset(hf_[:, :, NH_ - 1:NH_], 0.0), [], [Bhf])
        S.op("dve", lambda: A_.tensor_tensor(out=dx, in0=hf_[:, :, 0:BLK], in1=hf_[:, :, 2:2 + BLK], op=ALU.add), [Bhf], [Bdx])
        S.op("dve", lambda: A_.scalar_tensor_tensor(out=dx, in0=dx, scalar=0.5, in1=hf_[:, :, 1:1 + BLK], op0=ALU.mult, op1=ALU.subtract), [Bhf, Bdx], [Bdx])
        for jj in range(6):
            for c in range(8):
                S.op("dve", lambda: A_.scalar_tensor_tensor(out=xj[jj][:, c, :], in0=dx[:, c, :], scalar=self.pv(f"rw_mu{jj}", c), in1=hf_[:, c, 1:1 + BLK], op0=ALU.mult, op1=ALU.add),
                     [Bdx, Bhf], [Bxj[jj]])
        for d in range(2):
            for kc in range(8):
                S.op("pe", lambda: nc.tensor.matmul(self.PS[5][0:64, d * BLK:(d + 1) * BLK], lhsT=W1[:, kc, d, :], rhs=xj[1][:, kc, :], start=(kc == 0), stop=(kc == 7)), [Bsw, Bxj[1]], [self.BPS[5]])
        S.op("act", lambda: nc.scalar.activation(out=lt[:, 0:2, :], in_=self.PS[5][0:64, :].rearrange("p (d t) -> p d t", d=2), func=AF.Tanh), [self.BPS[5]], [Blt])
        for d in range(2):
            for kc in range(8):
                S.op("pe", lambda: nc.tensor.matmul(self.PS[5][0:64, d * BLK:(d + 1) * BLK], lhsT=A1[:, kc, d, :], rhs=xj[4][:, kc, :], start=(kc == 0), stop=(kc == 7)), [Bsw, Bxj[4]], [self.BPS[5]])
        S.op("act", lambda: nc.scalar.copy(out=lt[:, 2:4, :], in_=self.PS[5][0:64, :].rearrange("p (d t) -> p d t", d=2)), [self.BPS[5]], [Blt])
        for kc in range(8):
            S.op("pe", lambda: nc.tensor.matmul(self.PS[5][:, 0:BLK], lhsT=G1[:, kc, 0:128], rhs=xj[5][:, kc, :], start=(kc == 0), stop=(kc == 7)), [Bsw, Bxj[5]], [self.BPS[5]])
        for kc in range(8):
            S.op("pe", lambda: nc.tensor.matmul(self.PS[5][0:32, BLK:2 * BLK], lhsT=G1[:, kc, 128:160], rhs=xj[5][:, kc, :], start=(kc == 0), stop=(kc == 7)), [Bsw, Bxj[5]], [self.BPS[5]])
        S.op("act", lambda: nc.scalar.activation(out=gh, in_=self.PS[5][:, 0:BLK], func=AF.Sigmoid), [self.BPS[5]], [Blt])
        S.op("act", lambda: nc.scalar.activation(out=lt[0:32, 4, :], in_=self.PS[5][0:32, BLK:2 * BLK], func=AF.Sigmoid), [self.BPS[5]], [Blt])
        col = b * T + t0
        for c in range(8):
            s_ = si % 2
            si += 1
            sg_ = stg[s_]
            Bs = Bstg[s_]
            cs = slice(c * 128, (c + 1) * 128)

            def bank():
                nonlocal pbk
                pbk += 1
                return pbk % 5

            prk = []
            for which, xsrc in ((0, 0), (1, 2), (2, 3)):
                pb = bank()
                for kc in range(8):
                    S.op("pe", lambda: nc.tensor.matmul(self.PS[pb][:, 0:BLK], lhsT=Wrkv[:, kc, which * D + c * 128:which * D + (c + 1) * 128], rhs=xj[xsrc][:, kc, :], start=(kc == 0), stop=(kc == 7)),
                         [BWrkv[which * 2 + (c // 4)], Bxj[xsrc]], [self.BPS[pb]])
                prk.append(pb)
            S.op("act", lambda: nc.scalar.copy(out=sg_[:, 0, :], in_=self.PS[prk[0]][:, 0:BLK]), [self.BPS[prk[0]]], [Bs])
            S.op("act", lambda: nc.scalar.copy(out=sg_[:, 8, :], in_=self.PS[prk[2]][:, 0:BLK]), [self.BPS[prk[2]]], [Bs])
            kraw = tmp[0]
            S.op("act", lambda: nc.scalar.copy(out=kraw, in_=self.PS[prk[1]][:, 0:BLK]), [self.BPS[prk[1]]], [Btmp[0]])
            S.op("dve", lambda: A_.tensor_scalar(out=tmp[1], in0=kraw, scalar1=self.pv("rw_k_k", c), scalar2=None, op0=ALU.mult), [Btmp[0]], [Btmp[1]])
            S.op("act", lambda: nc.scalar.activation(out=sqb, in_=tmp[1], func=AF.Square), [Btmp[1]], [Bsqb])
            pb = bank()
            S.op("pe", lambda: nc.tensor.matmul(self.PS[pb][:, 0:BLK], lhsT=blk64b, rhs=sqb, start=True, stop=True), [Bsqb, Bb64], [self.BPS[pb]])
            S.op("act", lambda: nc.scalar.activation(out=tmp[2], in_=self.PS[pb][:, 0:BLK], func=AF.Sqrt), [self.BPS[pb]], [Btmp[2]])
            S.op("dve", lambda: A_.tensor_scalar(out=tmp[2], in0=tmp[2], scalar1=1e-12, scalar2=None, op0=ALU.max), [Btmp[2]], [Btmp[2]])
            S.op("dve", lambda: A_.reciprocal(out=tmp[2], in_=tmp[2]), [Btmp[2]], [Btmp[2]])
            S.op("dve", lambda: A_.tensor_tensor(out=sg_[:, 5, :], in0=tmp[1], in1=tmp[2], op=ALU.mult), [Btmp[1], Btmp[2]], [Bs])
            pb = bank()
            S.op("pe", lambda: nc.tensor.matmul(self.PS[pb][:, 0:BLK], lhsT=G2a[:, cs], rhs=gh, start=True, stop=False), [Bsw, Blt], [self.BPS[pb]])
            S.op("pe", lambda: nc.tensor.matmul(self.PS[pb][:, 0:BLK], lhsT=G2b[:, cs], rhs=lt[0:32, 4, :], start=False, stop=True), [Bsw, Blt], [self.BPS[pb]])
            S.op("act", lambda: nc.scalar.copy(out=sg_[:, 9, :], in_=self.PS[pb][:, 0:BLK]), [self.BPS[pb]], [Bs])
            for d in range(2):
                pb = bank()
                S.op("pe", lambda: nc.tensor.matmul(self.PS[pb][:, 0:BLK], lhsT=W2[:, d, cs], rhs=lt[:, d, :], start=True, stop=True), [Bsw, Blt], [self.BPS[pb]])
                S.op("act", lambda: nc.scalar.activation(out=tmp[3], in_=self.PS[pb][:, 0:BLK], func=AF.Sigmoid, bias=self.pv(f"rw_w0_{d}", c)), [self.BPS[pb]], [Btmp[3]])
                S.op("dve", lambda: A_.tensor_scalar(out=sg_[:, 6 + d, :], in0=tmp[3], scalar1=-float(np.exp(-0.5)), scalar2=None, op0=ALU.mult), [Btmp[3]], [Bs])
                pb = bank()
                S.op("pe", lambda: nc.tensor.matmul(self.PS[pb][:, 0:BLK], lhsT=A2[:, d, cs], rhs=lt[:, 2 + d, :], start=True, stop=True), [Bsw, Blt], [self.BPS[pb]])
                S.op("act", lambda: nc.scalar.activation(out=tmp[4], in_=self.PS[pb][:, 0:BLK], func=AF.Sigmoid, bias=self.pv(f"rw_a0_{d}", c)), [self.BPS[pb]], [Btmp[4]])
                S.op("dve", lambda: A_.tensor_tensor(out=sg_[:, 3 + d, :], in0=tmp[4], in1=sg_[:, 5, :], op=ALU.mult), [Btmp[4], Bs], [Bs])
                S.op("dve", lambda: A_.tensor_scalar(out=tmp[5], in0=tmp[4], scalar1=-1.0, scalar2=None, op0=ALU.add), [Btmp[4]], [Btmp[5]])
                S.op("dve", lambda: A_.tensor_scalar(out=tmp[5], in0=tmp[5], scalar1=self.pv("rw_k_a", c), scalar2=1.0, op0=ALU.mult, op1=ALU.add), [Btmp[5]], [Btmp[5]])
                S.op("dve", lambda: A_.tensor_tensor(out=sg_[:, 1 + d, :], in0=tmp[5], in1=kraw, op=ALU.mult), [Btmp[5], Btmp[0]], [Bs])
            S.dma("pool", RWP[:, c * 128:(c + 1) * 128, col:col + BLK].rearrange("a p t -> p a t"), sg_, reads=[Bs])
        for tt in range(BLK // 128):
            vi = vi_ % 2
            vi_ += 1
            for hfv in range(2):
                pb = 4 - hfv
                for kc in range(8):
                    S.op("pe", lambda: nc.tensor.matmul(self.PS[pb][:, 0:512], lhsT=xj[3][:, kc, tt * 128:(tt + 1) * 128], rhs=Wrkv[:, kc, 2 * D + hfv * 512:2 * D + (hfv + 1) * 512], start=(kc == 0), stop=(kc == 7)),
                         [BWrkv[4 + hfv], Bxj[3]], [self.BPS[pb]])
                S.op("act", lambda: nc.scalar.copy(out=vts[vi][:, hfv * 512:(hfv + 1) * 512], in_=self.PS[pb][:, 0:512]), [self.BPS[pb]], [Bvts[vi]])
            S.dma("pool", Vtm[col + tt * 128:col + (tt + 1) * 128, :], vts[vi], reads=[Bvts[vi]])
    st.close()


def _rwkv_mixer(self, l, xin, og):
    RWP = self.scr("rwP", [10, D, TT])
    Vtm = self.scr("rwV", [TT, D], BF16)
    self.rwkv_proj(l, xin, RWP, Vtm)
    if getattr(self, "rw_stop", 0) == 1:
        return
    self.rwkv_scan(RWP, Vtm, og)


Prog.rwkv_proj = _rwkv_proj
Prog.rwkv_mixer = _rwkv_mixer


def _rwkv_scan(self, RWP, Vtm, og):
    nc, S = self.nc, self.S
    A_ = nc.vector
    U32 = mybir.dt.uint32
    RWD = self.scr("rwD", [NB, 8, 2, 2, 128, NCH * 128], BF16)
    RWS = self.scr("rwS", [NB, 8, 2, 128, 3 * NCH])
    skipA = getattr(self, "rw_skipA", False)
    st = Stage(self, "r2a")
    smask = st.sb("smask", [128, T])
    Bsm = Buf()
    S.dma("sp", smask, self.cd["scanmask"], writes=[Bsm])
    lw = st.sb("lw", [128, T]); kap = st.sb("kap", [128, T]); rr = st.sb("r", [128, T]); kt = st.sb("kt", [128, T]); be = st.sb("be", [128, T])
    cw = st.sb("cw", [128, T]); cm = st.sb("cm", [128, T]); en = st.sb("en", [128, T]); ex = st.sb("ex", [128, T])
    ABt = [st.sb(f"AB{i}", [128, NCH, 2, CH], BF16) for i in range(2)]
    KBt_ = [st.sb(f"KB{i}", [128, NCH, 2, CH], BF16) for i in range(2)]
    SC = [st.sb(f"SC{i}", [128, 3, NCH]) for i in range(2)]
    Blw, Bkap, Br, Bkt, Bbe, Bcw, Bcm, Ben, Bex = [Buf() for _ in range(9)]
    BAB, BKB, BSC = [[Buf(), Buf()] for _ in range(3)]
    it = 0
    v3 = lambda t_: t_.rearrange("p (c s) -> p c s", s=CH)
    for b in range(0 if skipA else NB):
        cols = slice(b * T, (b + 1) * T)
        for p in range(8):
            rows = slice(p * 128, (p + 1) * 128)
            for d in range(2):
                i = it % 2
                it += 1
                S.dma("sp", lw, RWP[6 + d, rows, cols], writes=[Blw])
                S.dma("sp", kap, RWP[5, rows, cols], writes=[Bkap])
                S.dma("sp", rr, RWP[0, rows, cols], writes=[Br])
                S.dma("sp", kt, RWP[1 + d, rows, cols], writes=[Bkt])
                S.dma("sp", be, RWP[3 + d, rows, cols], writes=[Bbe])
                S.op("dve", lambda: A_.tensor_tensor_scan(out=cw, data0=smask, data1=lw, initial=0.0, op0=ALU.mult, op1=ALU.add), [Bsm, Blw], [Bcw])
                if d == 1:
                    S.op("dve", lambda: A_.tensor_tensor(out=cm, in0=lw, in1=cw, op=ALU.subtract), [Blw, Bcw], [Bcm])
                    S.op("dve", lambda: A_.tensor_tensor(out=v3(en), in0=v3(cm), in1=v3(cw)[:, :, CH - 1:CH].to_broadcast([128, NCH, CH]), op=ALU.add), [Bcm, Bcw], [Ben])
                    S.op("dve", lambda: A_.tensor_copy(out=cw, in_=en), [Ben], [Bcw])
                m_idx = 32 if d == 0 else 31
                e_idx = CH - 1 if d == 0 else 0
                c3 = v3(cw)
                S.op("act", lambda: nc.scalar.activation(out=SC[i][:, 0, :], in_=c3[:, :, m_idx], func=AF.Exp), [Bcw], [BSC[i]])
                S.op("act", lambda: nc.scalar.activation(out=SC[i][:, 1, :], in_=c3[:, :, e_idx], func=AF.Exp), [Bcw], [BSC[i]])
                S.op("dve", lambda: A_.tensor_tensor(out=SC[i][:, 2, :], in0=c3[:, :, e_idx], in1=c3[:, :, m_idx], op=ALU.subtract), [Bcw], [BSC[i]])
                S.op("act", lambda: nc.scalar.activation(out=SC[i][:, 2, :], in_=SC[i][:, 2, :], func=AF.Exp), [BSC[i]], [BSC[i]])
                S.dma("pool", RWS[b, p, d], SC[i].rearrange("p a c -> p (a c)"), reads=[BSC[i]])
                S.op("dve", lambda: A_.tensor_tensor(out=v3(cm), in0=c3, in1=c3[:, :, m_idx:m_idx + 1].to_broadcast([128, NCH, CH]), op=ALU.subtract), [Bcw], [Bcm])
                S.op("act", lambda: nc.scalar.activation(out=en, in_=cm, func=AF.Exp, scale=-1.0), [Bcm], [Ben])
                S.op("dve", lambda: A_.tensor_tensor(out=ex, in0=cm, in1=lw, op=ALU.subtract), [Bcm, Blw], [Bex])
                S.op("act", lambda: nc.scalar.activation(out=ex, in_=ex, func=AF.Exp), [Bex], [Bex])
                S.op("act", lambda: nc.scalar.activation(out=cm, in_=cm, func=AF.Exp), [Bcm], [Bcm])
                S.op("dve", lambda: A_.tensor_tensor(out=ABt[i][:, :, 0, :], in0=v3(kap), in1=v3(ex), op=ALU.mult), [Bkap, Bex], [BAB[i]])
                S.op("dve", lambda: A_.tensor_tensor(out=ABt[i][:, :, 1, :], in0=v3(rr), in1=v3(cm), op=ALU.mult), [Br, Bcm], [BAB[i]])
                S.op("dve", lambda: A_.tensor_tensor(out=KBt_[i][:, :, 0, :], in0=v3(kt), in1=v3(en), op=ALU.mult), [Bkt, Ben], [BKB[i]])
                S.op("dve", lambda: A_.tensor_tensor(out=KBt_[i][:, :, 1, :], in0=v3(be), in1=v3(en), op=ALU.mult), [Bbe, Ben], [BKB[i]])
                S.dma("pool", RWD[b, p, d, 0], ABt[i].rearrange("p c a s -> p (c a s)"), reads=[BAB[i]])
                S.dma("pool", RWD[b, p, d, 1], KBt_[i].rearrange("p c a s -> p (c a s)"), reads=[BKB[i]])
    st.close()
    if getattr(self, "rw_stop", 0) == 2:
        return
    st = Stage(self, "r2b")
    epsLN = st.sb("epsLN", [128, 1])
    Bgl = Buf()
    S.op("dve", lambda: A_.memset(epsLN, RW_LN_EPS), [], [Bgl])
    AB = [st.sb(f"AB{d}", [128, NCH, 128], BF16) for d in range(2)]
    KB = [st.sb(f"KB{d}", [128, NCH, 128], BF16) for d in range(2)]
    SCs = [st.sb(f"SC{d}", [128, 3, NCH]) for d in range(2)]
    Vst = st.sb("Vst", [64, NCH, 128], BF16)
    BABl, BKBl, BSCl = [[Buf(), Buf()] for _ in range(3)]
    BVst = Buf()
    chains = [(hd, d) for d in range(2) for hd in range(2)]
    VU, GGb, AN0, ANp, Xp, Wf, KBtr = {}, {}, {}, {}, {}, {}, {}
    BVU, BGG, BAN0, BANp, BXp, BWf, BKBtr, BST, BS0, BtS, By = [dict() for _ in range(11)]
    for ch in chains:
        nm = f"{ch[0]}{ch[1]}"
        VU[ch] = st.sb("VU" + nm, [128, NCH, CH], BF16)
        GGb[ch] = st.sb("GG" + nm, [128, 128], BF16)
        AN0[ch] = st.sb("AN0" + nm, [128, 128])
        ANp[ch] = [st.sb(f"ANp{q}" + nm, [128, 128]) for q in range(2)]
        Xp[ch] = [st.sb(f"X{q}" + nm, [128, CH]) for q in range(2)]
        Wf[ch] = st.sb("Wf" + nm, [128, CH])
        KBtr[ch] = st.sb("KBt" + nm, [128, CH], BF16)
        BVU[ch], BGG[ch], BAN0[ch], BWf[ch], BKBtr[ch], BST[ch], BS0[ch], BtS[ch], By[ch] = [Buf() for _ in range(9)]
        BANp[ch] = [Buf(), Buf()]
        BXp[ch] = [Buf(), Buf()]
        S.op("dve", lambda: A_.memset(GGb[ch], 0.0), [], [BGG[ch]])
        S.op("dve", lambda: A_.memset(AN0[ch], 0.0), [], [BAN0[ch]])
    ST = [st.sb(f"ST{d}", [128, CH]) for d in range(2)]
    S0m = [st.sb(f"S0m{d}", [128, CH], BF16) for d in range(2)]
    tS = [st.sb(f"tS{d}", [128, CH]) for d in range(2)]
    yacc = [st.sb(f"yacc{d}", [128, T]) for d in range(2)]
    rl = st.sb("rl", [128, T]); k0 = st.sb("k0", [128, T]); k1 = st.sb("k1", [128, T]); vf = st.sb("vf", [128, T]); gg = st.sb("gg", [128, T])
    t0_ = st.sb("t0", [128, T]); t1_ = st.sb("t1", [128, T])
    ogb = st.sb("ogb", [128, T], BF16)
    Brl, Bk0, Bk1, Bvf, Bgg, Bt0, Bt1, Bogb = [Buf() for _ in range(8)]
    R = {}
    BR = {}
    for ci, ch in enumerate(chains):
        b0, b1 = self.PS[2 * ci], self.PS[2 * ci + 1]
        R[ch] = dict(GA=b0[:, 0:128], LV=b0[:, 192:320], Wp=b0[:, 384:448],
                     XL=b1[:, 320:384], Up=b1[:, 448:512], Nn=b1[:, 128:192],
                     Yp=b1[:, 0:64], Sd=b1[:, 64:128], TR=b1.bitcast(BF16)[:, 512:576])
        u0, u1, u2, u3 = Buf(), Buf(), Buf(), Buf()
        ykp = [u2] if ch[0] == 0 else [u3]
        BR[ch] = dict(GAlo=[u0], GAup=[u1], GA=[u0, u1], LV=[u1], Wp=[u1], XL=[u3], Up=[u3], Nn=[u3], Yp=ykp, Sd=ykp, TR=[u2, u3], ALL=[u0, u1, u2, u3])
    up, lo = slice(64, 128), slice(0, 64)
    mU = lambda ap: ap.bitcast(U32)
    cf = list(range(NCH))
    cbk = list(range(TC // CH - 1, -1, -1)) + list(range(NCH - 1, TC // CH - 1, -1))
    order = [cf, cbk]
    dbgn = getattr(self, "rw_dbg", None)
    for b in range(NB):
        cols = slice(b * T, (b + 1) * T)
        for p in range(8):
            if dbgn is not None and (b * 8 + p) >= dbgn[0]:
                continue
            rows = slice(p * 128, (p + 1) * 128)
            for d in range(2):
                S.dma("sp", AB[d], RWD[b, p, d, 0].rearrange("k (c x) -> k c x", x=128), writes=[BABl[d]])
                S.dma("sp", KB[d], RWD[b, p, d, 1].rearrange("k (c x) -> k c x", x=128), writes=[BKBl[d]])
                S.dma("sp", SCs[d], RWS[b, p, d].rearrange("k (a c) -> k a c", a=3), writes=[BSCl[d]])
            S.dma("sp", Vst, Vtm[cols, rows].rearrange("(c s) v -> s c v", s=CH), writes=[BVst])
            S.dma("sp", rl, RWP[0, rows, cols], writes=[Brl])
            S.dma("sp", k0, RWP[1, rows, cols], writes=[Bk0])
            S.dma("sp", k1, RWP[2, rows, cols], writes=[Bk1])
            S.dma("sp", vf, RWP[8, rows, cols], writes=[Bvf])
            S.dma("sp", gg, RWP[9, rows, cols], writes=[Bgg])
            for ch in chains:
                hd, d = ch
                kp = slice(hd * 64, hd * 64 + 64)
                S.op("pool", lambda: nc.gpsimd.tensor_copy(out=VU[ch][lo, :, :], in_=Vst[:, :, hd * 64:(hd + 1) * 64]), [BVst], [BVU[ch]])
                S.op("dve", lambda: A_.memset(ST[d][kp, :], 0.0), [], [BST[ch]])
                S.op("dve", lambda: A_.memset(S0m[d][kp, :], 0.0), [], [BS0[ch]])
            for step in range(NCH if dbgn is None else dbgn[1]):
                for ch in chains:
                    hd, d = ch
                    kp = slice(hd * 64, hd * 64 + 64)
                    c = order[d][step]
                    cs = slice(c * CH, (c + 1) * CH)
                    r_, br_ = R[ch], BR[ch]
                    M4 = self.masks[:, 0:128] if d == 0 else self.masks[:, 128:256]
                    mA = self.masks[up, 0:64] if d == 0 else self.masks[up, 128:192]
                    mN = self.masks[up, 128:192] if d == 0 else self.masks[up, 0:64]
                    cut = getattr(self, "rw_cut", 99)
                    if cut < 1:
                        continue
                    S.op("pe", lambda: nc.tensor.transpose(out=r_["TR"], in_=KB[d][kp, c, :], identity=self.identb[kp, kp]), [BKBl[d]], br_["TR"])
                    S.op("act", lambda: nc.scalar.copy(out=KBtr[ch], in_=r_["TR"]), br_["TR"], [BKBtr[ch]])
                    if cut < 2:
                        continue
                    S.op("pe", lambda: nc.tensor.matmul(r_["GA"][lo, :], lhsT=KB[d][kp, c, 0:64], rhs=AB[d][kp, c, :], start=True, stop=True), [BKBl[d], BABl[d]], br_["GAlo"])
                    S.op("pe", lambda: nc.tensor.matmul(r_["GA"][up, :], lhsT=KB[d][kp, c, 64:128], rhs=AB[d][kp, c, :], start=True, stop=True), [BKBl[d], BABl[d]], br_["GAup"])
                    S.op("pe", lambda: nc.tensor.matmul(r_["Nn"][up, :], lhsT=AB[d][kp, c, 0:64], rhs=KB[d][kp, c, 64:128], start=True, stop=True), [BKBl[d], BABl[d]], br_["Nn"])
                    if cut < 3:
                        continue
                    S.op("dve", lambda: A_.copy_predicated(out=GGb[ch], mask=mU(M4), data=r_["GA"]), br_["GA"], [BGG[ch]])
                    S.op("dve", lambda: A_.copy_predicated(out=AN0[ch][up, 0:64], mask=mU(mA), data=r_["GA"][up, 0:64]), br_["GAup"], [BAN0[ch]])
                    S.op("dve", lambda: A_.copy_predicated(out=AN0[ch][up, 64:128], mask=mU(mN), data=r_["Nn"][up, :]), br_["Nn"], [BAN0[ch]])
                    S.op("dve", lambda: A_.tensor_tensor(out=Xp[ch][0][up, :], in0=self.ident[up, up], in1=AN0[ch][up, 0:64], op=ALU.subtract), [BAN0[ch]], [BXp[ch][0]])
                    if cut < 4:
                        continue
                    cur, Bcur = AN0[ch], BAN0[ch]
                    xq = 0
                    for lv in range(1, 6):
                        nx, Bnx = ANp[ch][lv % 2], BANp[ch][lv % 2]
                        if lv < 5:
                            S.op("pe", lambda: nc.tensor.matmul(r_["LV"][up, 0:64], lhsT=cur[up, 64:128], rhs=cur[up, 0:64], start=True, stop=True), [Bcur], br_["LV"])
                        S.op("pe", lambda: nc.tensor.matmul(r_["LV"][up, 64:128], lhsT=cur[up, 0:64], rhs=cur[up, 64:128], start=True, stop=True), [Bcur], br_["LV"])
                        if lv < 5:
                            S.op("act", lambda: nc.scalar.copy(out=nx[up, :], in_=r_["LV"][up, :]), br_["LV"], [Bnx])
                        else:
                            S.op("act", lambda: nc.scalar.copy(out=nx[up, 64:128], in_=r_["LV"][up, 64:128]), br_["LV"], [Bnx])
                        S.op("pe", lambda: nc.tensor.matmul(r_["XL"][up, :], lhsT=nx[up, 64:128], rhs=Xp[ch][xq][up, :], start=True, stop=True), [Bnx, BXp[ch][xq]], br_["XL"])
                        S.op("dve", lambda: A_.tensor_tensor(out=Xp[ch][1 - xq][up, :], in0=r_["XL"][up, :], in1=Xp[ch][xq][up, :], op=ALU.add), br_["XL"] + [BXp[ch][xq]], [BXp[ch][1 - xq]])
                        xq = 1 - xq
                        cur, Bcur = nx, Bnx
                    if cut < 5:
                        continue
                    S.op("pe", lambda: nc.tensor.matmul(r_["Wp"][up, :], lhsT=AB[d][kp, c, 0:64], rhs=S0m[d][kp, :], start=True, stop=False), [BABl[d], BS0[ch]], br_["Wp"])
                    S.op("pe", lambda: nc.tensor.matmul(r_["Wp"][up, :], lhsT=GGb[ch][lo, 0:64], rhs=VU[ch][lo, c, :], start=False, stop=True), [BGG[ch], BVU[ch]], br_["Wp"])
                    S.op("act", lambda: nc.scalar.copy(out=Wf[ch][up, :], in_=r_["Wp"][up, :]), br_["Wp"], [BWf[ch]])
                    if cut < 6:
                        continue
                    S.op("pe", lambda: nc.tensor.matmul(r_["Up"][up, :], lhsT=Xp[ch][xq][up, :], rhs=Wf[ch][up, :], start=True, stop=True), [BXp[ch][xq], BWf[ch]], br_["Up"])
                    S.op("act", lambda: nc.scalar.activation(out=VU[ch][up, c, :], in_=r_["Up"][up, :], func=AF.Copy, scale=-1.0), br_["Up"], [BVU[ch]])
                    if cut < 7:
                        continue
                    S.op("pe", lambda: nc.tensor.matmul(r_["Yp"][kp, :], lhsT=S0m[d][kp, :], rhs=AB[d][kp, c, 64:128], start=True, stop=False), [BS0[ch], BABl[d]], br_["Yp"])
                    S.op("pe", lambda: nc.tensor.matmul(r_["Yp"][kp, :], lhsT=VU[ch][:, c, :], rhs=GGb[ch][:, 64:128], start=False, stop=True), [BVU[ch], BGG[ch]], br_["Yp"])
                    S.op("act", lambda: nc.scalar.copy(out=yacc[d][kp, cs], in_=r_["Yp"][kp, :]), br_["Yp"], [By[ch]])
                    if cut < 8:
                        continue
                    S.op("pe", lambda: nc.tensor.matmul(r_["Sd"][kp, :], lhsT=KBtr[ch], rhs=VU[ch][:, c, :], start=True, stop=True), [BKBtr[ch], BVU[ch]], br_["Sd"])
                    S.op("act", lambda: nc.scalar.activation(out=tS[d][kp, :], in_=r_["Sd"][kp, :], func=AF.Identity, scale=SCs[d][kp, 2, c:c + 1]), br_["Sd"] + [BSCl[d]], [BtS[ch]])
                    S.op("dve", lambda: A_.scalar_tensor_tensor(out=ST[d][kp, :], in0=ST[d][kp, :], scalar=SCs[d][kp, 1, c:c + 1], in1=tS[d][kp, :], op0=ALU.mult, op1=ALU.add), [BST[ch], BtS[ch], BSCl[d]], [BST[ch]])
                    if step + 1 < NCH:
                        cn = order[d][step + 1]
                        S.op("dve", lambda: A_.tensor_scalar(out=S0m[d][kp, :], in0=ST[d][kp, :], scalar1=SCs[d][kp, 0, cn:cn + 1], scalar2=None, op0=ALU.mult), [BST[ch], BSCl[d]], [BS0[ch]])
            RB = {0: [BR[chains[0]]["ALL"][0], BR[chains[0]]["ALL"][1]], 1: [BR[chains[0]]["ALL"][2], BR[chains[0]]["ALL"][3]]}
            By_all = [By[ch] for ch in chains]
            Byy = Buf()
            S.op("dve", lambda: A_.tensor_tensor(out=yacc[0], in0=yacc[0], in1=yacc[1], op=ALU.add), By_all, [Byy])
            NP_ = 6
            W_ = T // NP_
            for pc in range(NP_):
                sl_ = slice(pc * W_, (pc + 1) * W_)
                pb = pc % 2
                S.op("pe", lambda: nc.tensor.matmul(self.PS[pb][:, 0:W_], lhsT=self.blk64, rhs=yacc[0][:, sl_], start=True, stop=True), [Byy], RB[pb])
                S.op("dve", lambda: A_.scalar_tensor_tensor(out=t0_[:, sl_], in0=self.PS[pb][:, 0:W_], scalar=-1.0 / 64, in1=yacc[0][:, sl_], op0=ALU.mult, op1=ALU.add), RB[pb] + [Byy], [Bt0])
            S.op("act", lambda: nc.scalar.activation(out=t1_, in_=t0_, func=AF.Square), [Bt0], [Bt1])
            for pc in range(NP_):
                sl_ = slice(pc * W_, (pc + 1) * W_)
                pb = pc % 2
                S.op("pe", lambda: nc.tensor.matmul(self.PS[pb][:, 0:W_], lhsT=self.blk64, rhs=t1_[:, sl_], start=True, stop=True), [Bt1], RB[pb])
                S.op("act", lambda: nc.scalar.activation(out=yacc[1][:, sl_], in_=self.PS[pb][:, 0:W_], func=AF.Sqrt, scale=1.0 / 64, bias=epsLN), RB[pb] + [Bgl], [Byy])
            S.op("dve", lambda: A_.reciprocal(out=yacc[1], in_=yacc[1]), [Byy], [Byy])
            S.op("dve", lambda: A_.tensor_tensor(out=t0_, in0=t0_, in1=yacc[1], op=ALU.mult), [Bt0, Byy], [Bt0])
            S.op("act", lambda: nc.scalar.activation(out=t0_, in_=t0_, func=AF.Identity, scale=self.pv("rw_ln_w", p), bias=self.pv("rw_ln_b", p)), [Bt0], [Bt0])
            S.op("dve", lambda: A_.tensor_tensor(out=k0, in0=k0, in1=k1, op=ALU.add), [Bk0, Bk1], [Bk0])
            S.op("dve", lambda: A_.scalar_tensor_tensor(out=t1_, in0=rl, scalar=self.pv("rw_r_k", p), in1=k0, op0=ALU.mult, op1=ALU.mult), [Brl, Bk0, Bt1], [Bt1])
            for pc in range(NP_):
                sl_ = slice(pc * W_, (pc + 1) * W_)
                pb = pc % 2
                S.op("pe", lambda: nc.tensor.matmul(self.PS[pb][:, 0:W_], lhsT=self.blk64, rhs=t1_[:, sl_], start=True, stop=True), [Bt1], RB[pb])
                S.op("dve", lambda: A_.tensor_tensor(out=yacc[1][:, sl_], in0=self.PS[pb][:, 0:W_], in1=vf[:, sl_], op=ALU.mult), RB[pb] + [Bvf, Byy], [Byy])
            S.op("dve", lambda: A_.tensor_tensor(out=t0_, in0=t0_, in1=yacc[1], op=ALU.add), [Bt0, Byy], [Bt0])
            S.op("dve", lambda: A_.tensor_tensor(out=ogb, in0=t0_, in1=gg, op=ALU.mult), [Bt0, Bgg], [Bogb])
            S.dma("pool", og[rows, cols], ogb, reads=[Bogb])
            for ch in chains:
                By[ch].r.append(Byy.w)
    st.close()


Prog.rwkv_scan = _rwkv_scan
```

```python
from contextlib import ExitStack
import numpy as np
import concourse.bass as bass
import concourse.mybir as mybir
from concourse.bass_utils import run_bass_kernel_spmd

F32 = mybir.dt.float32
BF16 = mybir.dt.bfloat16
AF = mybir.ActivationFunctionType
ALU = mybir.AluOpType

NCORES = 8
NB = 2
TC = 256
TL = 2048
T = TC + TL
TT = NB * T
D = 1024
DEPTH = 4
DFF = 2816
NFC = DFF // 128
BLK = 256
NBLK = T // BLK
EPS = 1e-6
CH = 64
NCH = T // CH
RW_LN_EPS = 64e-5
MLA_SCALE = 96 ** -0.5


class Buf:
    __slots__ = ("name", "w", "r")

    def __init__(self, name=""):
        self.name = name
        self.w = None
        self.r = []


class _Eng:
    def __init__(self, S, name, eng):
        self.S = S
        self.name = name
        self.eng = eng
        self.sem = None
        self.count = 0
        self.seen = {}
        self.nsem = 0
        self.ninst = 0
        self.own = set()

    def new_sem(self):
        self.sem = self.S.nc.alloc_semaphore(f"e_{self.name}_{self.nsem}")
        self.own.add(id(self.sem))
        self.nsem += 1
        self.count = 0

    def wait(self, ev):
        sem, val = ev
        k = id(sem)
        if self.name == "pe" and k in self.own and not self.S.pe_selfwait:
            return
        if self.seen.get(k, 0) >= val:
            return
        self.eng.wait_ge(sem, val)
        self.seen[k] = val


class Sched:
    EPOCH = 30000

    def __init__(self, nc, ndma_sems=48):
        self.nc = nc
        self.E = {}
        for name, eng in (("pe", nc.tensor), ("dve", nc.vector), ("act", nc.scalar),
                          ("pool", nc.gpsimd), ("sp", nc.sync)):
            e = _Eng(self, name, eng)
            e.new_sem()
            self.E[name] = e
        self.dsems = [[nc.alloc_semaphore(f"d{i}"), 0] for i in range(ndma_sems)]
        self.dnext = 0
        self._keep = []
        self.pe_selfwait = False
        self.pe_drain = 0
        self.last_pemode = None

    @staticmethod
    def _deps(reads, writes):
        deps = []
        for b in reads:
            if b.w is not None:
                deps.append(b.w)
        for b in writes:
            if b.w is not None:
                deps.append(b.w)
            deps.extend(b.r)
        return deps

    @staticmethod
    def _mark(ev, reads, writes):
        for b in writes:
            b.w = ev
            b.r = []
        for b in reads:
            if b not in writes:
                b.r.append(ev)
                if len(b.r) > 32:
                    b.r = b.r[-32:]

    def op(self, ename, fn, reads=(), writes=(), pemode=None):
        e = self.E[ename]
        for ev in self._deps(reads, writes):
            e.wait(ev)
        drain = False
        if ename == "pe":
            drain = self.pe_drain == 1 or (self.pe_drain == 2 and pemode != self.last_pemode)
            self.last_pemode = pemode
        if drain and e.count > 0:
            k = id(e.sem)
            if e.seen.get(k, 0) < e.count:
                e.eng.wait_ge(e.sem, e.count)
                e.seen[k] = e.count
        if e.count >= self.EPOCH:
            self._keep.append(e.sem)
            e.new_sem()
        inst = fn()
        e.count += 1
        e.ninst += 1
        inst.then_inc(e.sem, 1)
        ev = (e.sem, e.count)
        self._mark(ev, reads, writes)
        return ev

    def dma(self, qname, out, in_, reads=(), writes=(), **kw):
        q = self.E[qname]
        for ev in self._deps(reads, writes):
            q.wait(ev)
        slot = self.dsems[self.dnext % len(self.dsems)]
        self.dnext += 1
        if slot[1] >= self.EPOCH:
            self._keep.append(slot[0])
            slot[0] = self.nc.alloc_semaphore(f"dx{self.dnext}")
            slot[1] = 0
        if slot[1] > 0:
            q.wait((slot[0], slot[1]))
        q.eng.dma_start(out=out, in_=in_, **kw).then_inc(slot[0], 16)
        q.ninst += 1
        slot[1] += 16
        ev = (slot[0], slot[1])
        self._mark(ev, reads, writes)
        return ev

    def barrier(self):
        evs = [(e.sem, e.count) for e in self.E.values() if e.count > 0]
        evs += [(s[0], s[1]) for s in self.dsems if s[1] > 0]
        for e in self.E.values():
            for ev in evs:
                if ev[0] is e.sem:
                    continue
                e.wait(ev)


class PVec:
    def __init__(self):
        self.cols = []
        self.off = {}
        self.n = 0

    def add(self, name, vec):
        vec = np.asarray(vec, dtype=np.float32).reshape(-1)
        assert vec.size % 128 == 0
        nch = vec.size // 128
        self.off[name] = (self.n, nch)
        self.cols.append(np.ascontiguousarray(vec.reshape(nch, 128).T))
        self.n += nch

    def array(self):
        return np.ascontiguousarray(np.concatenate(self.cols, axis=1))


def pvec_layout(inputs):
    pv = PVec()
    for l in range(DEPTH):
        pv.add(f"b_mod{l}", inputs["b_mod"][l])
        pv.add(f"norm1_{l}", inputs["norm1"][l])
        pv.add(f"norm2_{l}", inputs["norm2"][l])
        for k in range(3):
            pv.add(f"conv{l}_{k}", inputs["ffn_conv"][l, k])
        pv.add(f"convb{l}", inputs["ffn_conv_b"][l])
    pv.add("norm_f", inputs["norm_f"])
    for d in range(2):
        for j in range(2):
            pv.add(f"hg_lb{d}_{j}", inputs["hg_lb"][d, j])
    for j in range(2):
        pv.add(f"hg_norm{j}", inputs["hg_norm"][j])
    for k in range(6):
        pv.add(f"rw_mu{k}", inputs["rw_mu"][0, k])
    for d in range(2):
        pv.add(f"rw_w0_{d}", inputs["rw_w0"][0, d])
        pv.add(f"rw_a0_{d}", inputs["rw_a0"][0, d])
    for nm in ("rw_k_k", "rw_k_a", "rw_r_k", "rw_ln_w", "rw_ln_b"):
        pv.add(nm, inputs[nm][0])
    pv.add("mla_q_norm", inputs["mla_q_norm"][0])
    pv.add("mla_kv_norm", inputs["mla_kv_norm"][0])
    return pv


def make_consts():
    c = {}
    c["ident"] = np.eye(128, dtype=np.float32)
    c["ones"] = np.ones((128, 128), dtype=np.float32)
    bo = np.zeros((128, 128), dtype=np.float32)
    bo[:64, :64] = 1.0
    bo[64:, 64:] = 1.0
    c["blk64"] = bo
    i = np.arange(64)[:, None]
    t = np.arange(64)[None, :]
    su = (i < t).astype(np.float32)
    iu = (i <= t).astype(np.float32)
    sl = (i > t).astype(np.float32)
    il = (i >= t).astype(np.float32)
    c["masks"] = np.concatenate([np.concatenate([su, iu, sl, il], axis=1)] * 2, axis=0)
    m = np.ones((128, T), dtype=np.float32)
    m[:, ::CH] = 0.0
    c["scanmask"] = m
    nq = 8
    inv_freq = (10000.0 ** (-np.arange(nq, dtype=np.float32) / nq)).astype(np.float32)
    pos = np.arange(TL)
    row = (pos // 64).astype(np.float32)
    col = (pos % 64).astype(np.float32)
    ang_r = row[:, None] * inv_freq
    ang_c = col[:, None] * inv_freq
    ang = np.concatenate([ang_r, ang_r, ang_c, ang_c], axis=-1).astype(np.float32)
    cos = np.ones((32, T), dtype=np.float32)
    sin = np.zeros((32, T), dtype=np.float32)
    cos[:, TC:] = np.cos(ang).T
    sin[:, TC:] = np.sin(ang).T
    c["rope_cos"] = cos
    c["rope_sin"] = sin
    return c


WEIGHT_NAMES = ["w_mod", "ffn_w_in", "ffn_w_out", "hg_w_in", "hg_w_o", "rw_w_rkv", "rw_w1", "rw_w2",
                "rw_a1", "rw_a2", "rw_g1", "rw_g2", "rw_w_o", "mla_w_dqkv", "mla_w_uq", "mla_w_ukv", "mla_w_o"]


class Stage:
    def __init__(self, P, name):
        self.P = P
        self.name = name
        self.es = ExitStack()
        P.nstage += 1
        self.k = 0

    def sb(self, name, shape, dt=F32):
        self.k += 1
        h = self.es.enter_context(self.P.nc.sbuf_tensor(f"{self.name}{self.P.nstage}_{name}_{self.k}", list(shape), dt))
        return h.ap()

    def close(self):
        self.P.S.barrier()
        self.es.close()


class Prog:
    def __init__(self, wshapes, pv_off, npv, dbg=(), xin_name=None):
        nc = bass.Bass("TRN2", target_bir_lowering=False)
        self.nc = nc
        self.dbg = set(dbg)
        self.pv_off = pv_off
        self.nstage = 0
        di = lambda n, s: nc.dram_tensor(n, list(s), F32, kind="ExternalInput").ap()
        self.x = di("x", [NB, TL, D])
        self.ctx = di("ctx", [NB, TC, D])
        self.cvec = di("cvec", [3, D])
        self.pvec_d = di("pvec", [128, npv])
        self.cd = {n: di("c_" + n, s) for n, s in (("ident", [128, 128]), ("ones", [128, 128]), ("blk64", [128, 128]),
                                                    ("masks", [128, 256]), ("scanmask", [128, T]),
                                                    ("rope_cos", [32, T]), ("rope_sin", [32, T]))}
        self.W = {n: di(n, wshapes[n]) for n in WEIGHT_NAMES}
        self.out = nc.dram_tensor("out", [NB, TL, D], F32, kind="ExternalOutput").ap()
        self.scratch = {}
        self.S = Sched(nc)
        S = self.S
        self.PS = [nc.alloc_psum_tensor(f"psb{i}", [128, 512], F32).ap() for i in range(8)]
        self.BPS = [Buf(f"ps{i}") for i in range(8)]
        g = lambda n, s, dt=F32: nc.alloc_sbuf_tensor("g_" + n, list(s), dt).ap()
        self.ident = g("ident", [128, 128])
        self.identb = g("identb", [128, 128], BF16)
        self.onesf = g("onesf", [128, 128])
        self.onesb = g("onesb", [128, 128], BF16)
        self.blk64 = g("blk64", [128, 128])
        self.masks = g("masks", [128, 256])
        self.pvec = g("pvec", [128, npv])
        self.MOD = g("MOD", [128, DEPTH, 48, 3])
        self.MA = g("MA", [128, DEPTH, 2, 8, 3])
        self.epsD = g("epsD", [128, 1])
        self.BC = Buf("consts")
        self.BMOD = Buf("mod")
        S.op("dve", lambda: nc.vector.memset(self.epsD, EPS), [], [self.BC])
        S.dma("sp", self.ident, self.cd["ident"], writes=[self.BC])
        b1, b2, b3, b4, b5, b6 = [Buf() for _ in range(6)]
        S.dma("sp", self.onesf, self.cd["ones"], writes=[b1])
        S.dma("sp", self.blk64, self.cd["blk64"], writes=[b2])
        S.dma("sp", self.masks, self.cd["masks"], writes=[b3])
        S.dma("sp", self.pvec, self.pvec_d, writes=[b4])
        S.dma("pool", self.identb, self.cd["ident"], writes=[b5])
        S.dma("pool", self.onesb, self.cd["ones"], writes=[b6])
        S.barrier()

    def scr(self, name, shape, dt=F32):
        if name not in self.scratch:
            kind = "ExternalOutput" if name in self.dbg else "Internal"
            self.scratch[name] = self.nc.dram_tensor("s_" + name, list(shape), dt, kind=kind).ap()
        return self.scratch[name]

    def pv(self, name, c=None):
        off, nch = self.pv_off[name]
        if c is None:
            return self.pvec[:, off:off + nch]
        return self.pvec[:, off + c:off + c + 1]

    def load_w(self, dst, src, bufs_cols, q="pool"):
        S = self.S
        n = dst.shape[2]
        v = src.rearrange("(kc p) n -> p kc n", p=128)
        bufs = []
        for n0 in range(0, n, 512):
            n1 = min(n, n0 + 512)
            b = Buf()
            S.dma(q, dst[:, :, n0:n1], v[:, :, n0:n1], writes=[b])
            bufs.append(b)
        return bufs

    def prologue_transpose(self, xT):
        nc, S = self.nc, self.S
        st = Stage(self, "pt")
        tin = [st.sb(f"tin{i}", [128, D]) for i in range(2)]
        tout = [st.sb(f"tout{i}", [128, 8, 128]) for i in range(2)]
        Bin = [Buf(), Buf()]
        Bout = [Buf(), Buf()]
        xTv = xT.rearrange("(c p) t -> p c t", p=128)
        tiles = []
        for b in range(NB):
            for k in range(T // 128):
                tiles.append((b, k))

        def src(b, k):
            t0 = k * 128
            if t0 < TC:
                return self.ctx[b, t0:t0 + 128, :]
            return self.x[b, t0 - TC:t0 - TC + 128, :]

        S.dma("sp", tin[0], src(*tiles[0]), writes=[Bin[0]])
        for n, (b, k) in enumerate(tiles):
            i = n % 2
            if n + 1 < len(tiles):
                S.dma("sp", tin[1 - i], src(*tiles[n + 1]), writes=[Bin[1 - i]])
            for hf in range(2):
                pb = 2 * (n % 2) + hf
                for c4 in range(4):
                    c = hf * 4 + c4
                    S.op("pe", lambda: nc.tensor.transpose(out=self.PS[pb][:, c4 * 128:(c4 + 1) * 128], in_=tin[i][:, c * 128:(c + 1) * 128], identity=self.ident),
                         [Bin[i]], [self.BPS[pb]])
                eng = "dve" if hf == 0 else "act"
                if hf == 0:
                    S.op("dve", lambda: nc.vector.tensor_copy(out=tout[i][:, 0:4, :], in_=self.PS[pb][:].rearrange("p (c t) -> p c t", c=4)), [self.BPS[pb]], [Bout[i]])
                else:
                    S.op("act", lambda: nc.scalar.copy(out=tout[i][:, 4:8, :], in_=self.PS[pb][:].rearrange("p (c t) -> p c t", c=4)), [self.BPS[pb]], [Bout[i]])
            col = b * T + k * 128
            S.dma("pool", xTv[:, :, col:col + 128], tout[i], reads=[Bout[i]])
        st.close()

    def prologue_mod(self):
        nc, S = self.nc, self.S
        st = Stage(self, "pm")
        cv = st.sb("cv", [3, D])
        sc = st.sb("sc", [3, D])
        scT = st.sb("scT", [128, 8, 3])
        Bcv, Bsc, BscT = Buf(), Buf(), Buf()
        S.dma("sp", cv, self.cvec, writes=[Bcv])
        S.op("act", lambda: nc.scalar.activation(out=sc, in_=cv, func=AF.Silu), [Bcv], [Bsc])
        for kc in range(8):
            S.op("pe", lambda: nc.tensor.transpose(out=self.PS[0][:, kc * 4:kc * 4 + 3], in_=sc[0:3, kc * 128:(kc + 1) * 128], identity=self.ident[0:3, 0:3]),
                 [Bsc], [self.BPS[0]])
        S.op("dve", lambda: nc.vector.tensor_copy(out=scT, in_=self.PS[0][:, 0:32].rearrange("p (k f) -> p k f", f=4)[:, :, 0:3]), [self.BPS[0]], [BscT])
        wt = [st.sb(f"wt{i}", [128, 8, 512]) for i in range(2)]
        Bwt = [Buf(), Buf()]
        groups = [(l, g) for l in range(DEPTH) for g in range(12)]

        def wsrc(l, g):
            return self.W["w_mod"][l].rearrange("(kc p) n -> p kc n", p=128)[:, :, g * 512:(g + 1) * 512]

        S.dma("sp", wt[0], wsrc(*groups[0]), writes=[Bwt[0]])
        for n, (l, g) in enumerate(groups):
            i = n % 2
            if n + 1 < len(groups):
                S.dma("sp", wt[1 - i], wsrc(*groups[n + 1]), writes=[Bwt[1 - i]])
            pb = 1 + (n % 2)
            for oc in range(4):
                for kc in range(8):
                    S.op("pe", lambda: nc.tensor.matmul(self.PS[pb][:, oc * 4:oc * 4 + 3], lhsT=wt[i][:, kc, oc * 128:(oc + 1) * 128], rhs=scT[:, kc, :], start=(kc == 0), stop=(kc == 7)),
                         [Bwt[i], BscT], [self.BPS[pb]])
            boff, _ = self.pv_off[f"b_mod{l}"]
            bias = self.pvec[:, boff + g * 4:boff + g * 4 + 4].unsqueeze(2).to_broadcast([128, 4, 3])
            S.op("dve", lambda: nc.vector.tensor_tensor(out=self.MOD[:, l, g * 4:(g + 1) * 4, :], in0=self.PS[pb][:, 0:16].rearrange("p (o f) -> p o f", f=4)[:, :, 0:3], in1=bias, op=ALU.add),
                 [self.BPS[pb]], [self.BMOD])
        for l in range(DEPTH):
            for w in range(2):
                sc_idx = 8 if w == 0 else 32
                nrm = self.pv(f"norm{w + 1}_{l}").unsqueeze(2).to_broadcast([128, 8, 3])
                S.op("dve", lambda: nc.vector.scalar_tensor_tensor(out=self.MA[:, l, w, :, :], in0=self.MOD[:, l, sc_idx:sc_idx + 8, :], scalar=1.0, in1=nrm, op0=ALU.add, op1=ALU.mult),
                     [self.BMOD], [self.BMOD])
        st.close()

    def norm_tiles(self, st, n=BLK + 2):
        return dict(sq=st.sb("nsq", [128, 8, n], BF16), tmp=st.sb("ntmp", [128, 8, n]), r0=st.sb("nr0", [128, n]), r1=st.sb("nr1", [128, n]),
                    B=[Buf() for _ in range(4)])

    def norm_block(self, nt, xs, Bxs, n, A, Bsh, hb, Bhb, bank):
        nc, S = self.nc, self.S
        sq, tmp, r0, r1 = nt["sq"], nt["tmp"], nt["r0"], nt["r1"]
        Bsq, Btmp, Br0, Br1 = nt["B"]
        S.op("act", lambda: nc.scalar.activation(out=sq[:, :, :n], in_=xs, func=AF.Square), [Bxs], [Bsq])
        ps = self.PS[bank]
        for c in range(8):
            S.op("pe", lambda: nc.tensor.matmul(ps[:, :n], lhsT=self.onesb, rhs=sq[:, c, :n], start=(c == 0), stop=(c == 7)), [Bsq], [self.BPS[bank]])
        S.op("act", lambda: nc.scalar.activation(out=r0[:, :n], in_=ps[:, :n], func=AF.Sqrt, scale=1.0 / D, bias=self.epsD), [self.BPS[bank]], [Br0])
        S.op("dve", lambda: nc.vector.reciprocal(out=r1[:, :n], in_=r0[:, :n]), [Br0], [Br1])
        S.op("dve", lambda: nc.vector.tensor_tensor(out=tmp[:, :, :n], in0=xs, in1=r1[:, :n].unsqueeze(1).to_broadcast([128, 8, n]), op=ALU.mult), [Bxs, Br1], [Btmp])
        for c in range(8):
            S.op("act", lambda: nc.scalar.activation(out=hb[:, c, :n], in_=tmp[:, c, :n], func=AF.Identity, scale=A[:, c:c + 1], bias=(Bsh[:, c:c + 1] if Bsh is not None else 0.0)),
                 [Btmp, self.BMOD], [Bhb])

    def mod_ab(self, l, w, j):
        A = self.MA[:, l, w, :, j]
        sh = self.MOD[:, l, (0 if w == 0 else 24):(8 if w == 0 else 32), j]
        gt = self.MOD[:, l, (16 if w == 0 else 40):(24 if w == 0 else 48), j]
        return A, sh, gt

    @staticmethod
    def blocks(skip_ctx=False):
        out = []
        for b in range(NB):
            for k in range(NBLK):
                if skip_ctx and k == 0:
                    continue
                out.append((b, k))
        return out

    @staticmethod
    def blk_range(k):
        seq0, seq1 = (0, TC) if k == 0 else (TC, T)
        t0 = k * BLK
        lo = max(t0 - 1, seq0)
        hi = min(t0 + BLK + 1, seq1)
        return t0, lo, hi, (t0 == seq0), (t0 + BLK == seq1)

    def ffn_stage(self, l, xin, xout, skip_ctx):
        nc, S = self.nc, self.S
        st = Stage(self, "ffn")
        Win = st.sb("win", [128, 8, 2 * DFF], BF16)
        Wout = st.sb("wout", [128, NFC, D], BF16)
        BWin = self.load_w(Win, self.W["ffn_w_in"][l], None)
        BWout = []
        osrc = self.W["ffn_w_out"][l].rearrange("(fc p) n -> p fc n", p=128)
        for f0 in range(0, NFC, 2):
            b = Buf()
            S.dma("pool", Wout[:, f0:f0 + 2, :], osrc[:, f0:f0 + 2, :], writes=[b])
            BWout.append(b)
        NH = BLK + 2
        xs = [st.sb(f"xs{i}", [128, 8, NH]) for i in range(2)]
        hb = [st.sb(f"hb{i}", [128, 8, NH], BF16) for i in range(2)]
        gt_ = [st.sb(f"g{i}", [128, NFC, BLK], BF16) for i in range(2)]
        cv = [st.sb(f"cv{i}", [128, BLK]) for i in range(2)]
        sl = [st.sb(f"sl{i}", [128, BLK]) for i in range(2)]
        Bxs, Bhb, Bg, Bcv, Bsl = [[Buf(), Buf()] for _ in range(5)]
        nt = self.norm_tiles(st)
        for i in range(2):
            S.op("dve", lambda: nc.vector.memset(xs[i], 0.0), [], [Bxs[i]])
        xiv = xin.rearrange("(c p) t -> p c t", p=128)
        xov = xout.rearrange("(c p) t -> p c t", p=128)
        blocks = self.blocks(skip_ctx)

        def load(n):
            b, k = blocks[n]
            t0, lo, hi, _, _ = self.blk_range(k)
            S.dma("sp", xs[n % 2][:, :, lo - (t0 - 1):hi - (t0 - 1)], xiv[:, :, b * T + lo:b * T + hi], writes=[Bxs[n % 2]])

        load(0)
        for n, (b, k) in enumerate(blocks):
            i = n % 2
            if n + 1 < len(blocks):
                load(n + 1)
            t0, lo, hi, first, last = self.blk_range(k)
            j = 2 if k == 0 else b
            A, sh, gate = self.mod_ab(l, 1, j)
            self.norm_block(nt, xs[i], Bxs[i], NH, A, sh, hb[i], Bhb[i], 6)
            for fc in range(NFC):
                q = fc % 2
                pa, pvv = self.PS[q], self.PS[2 + q]
                ga = BWin[(fc * 128) // 512]
                gv = BWin[(DFF + fc * 128) // 512]
                for kc in range(8):
                    S.op("pe", lambda: nc.tensor.matmul(pa[:, :NH], lhsT=Win[:, kc, fc * 128:(fc + 1) * 128], rhs=hb[i][:, kc, :], start=(kc == 0), stop=(kc == 7)),
                         [ga, Bhb[i]], [self.BPS[q]])
                for kc in range(8):
                    S.op("pe", lambda: nc.tensor.matmul(pvv[:, :BLK], lhsT=Win[:, kc, DFF + fc * 128:DFF + (fc + 1) * 128], rhs=hb[i][:, kc, 1:1 + BLK], start=(kc == 0), stop=(kc == 7)),
                         [gv, Bhb[i]], [self.BPS[2 + q]])
                w0, w1, w2, cb = self.pv(f"conv{l}_0", fc), self.pv(f"conv{l}_1", fc), self.pv(f"conv{l}_2", fc), self.pv(f"convb{l}", fc)
                S.op("act", lambda: nc.scalar.activation(out=cv[q], in_=pa[:, 1:1 + BLK], func=AF.Identity, scale=w1, bias=cb), [self.BPS[q]], [Bcv[q]])
                c0 = 1 if first else 0
                S.op("dve", lambda: nc.vector.scalar_tensor_tensor(out=cv[q][:, c0:BLK], in0=pa[:, c0:BLK], scalar=w0, in1=cv[q][:, c0:BLK], op0=ALU.mult, op1=ALU.add),
                     [self.BPS[q], Bcv[q]], [Bcv[q]])
                c1 = BLK - 1 if last else BLK
                S.op("dve", lambda: nc.vector.scalar_tensor_tensor(out=cv[q][:, 0:c1], in0=pa[:, 2:2 + c1], scalar=w2, in1=cv[q][:, 0:c1], op0=ALU.mult, op1=ALU.add),
                     [self.BPS[q], Bcv[q]], [Bcv[q]])
                S.op("act", lambda: nc.scalar.activation(out=sl[q], in_=cv[q], func=AF.Silu), [Bcv[q]], [Bsl[q]])
                S.op("dve", lambda: nc.vector.tensor_tensor(out=gt_[i][:, fc, :], in0=sl[q], in1=pvv[:, :BLK], op=ALU.mult), [Bsl[q], self.BPS[2 + q]], [Bg[i]])
            for oc in range(8):
                q = 4 + oc % 2
                po = self.PS[q]
                for fc in range(NFC):
                    S.op("pe", lambda: nc.tensor.matmul(po[:, :BLK], lhsT=Wout[:, fc, oc * 128:(oc + 1) * 128], rhs=gt_[i][:, fc, :], start=(fc == 0), stop=(fc == NFC - 1)),
                         [BWout[fc // 2], Bg[i]], [self.BPS[q]])
                S.op("dve", lambda: nc.vector.scalar_tensor_tensor(out=xs[i][:, oc, 1:1 + BLK], in0=po[:, :BLK], scalar=gate[:, oc:oc + 1], in1=xs[i][:, oc, 1:1 + BLK], op0=ALU.mult, op1=ALU.add),
                     [self.BPS[q], Bxs[i], self.BMOD], [Bxs[i]])
            S.dma("pool", xov[:, :, b * T + t0:b * T + t0 + BLK], xs[i][:, :, 1:1 + BLK], reads=[Bxs[i]])
        st.close()

    def final_stage(self, xin):
        nc, S = self.nc, self.S
        st = Stage(self, "fin")
        xs = [st.sb(f"xs{i}", [128, 8, BLK]) for i in range(2)]
        hb = [st.sb(f"hb{i}", [128, 8, BLK]) for i in range(2)]
        ot = [st.sb(f"ot{i}", [128, D]) for i in range(2)]
        Bxs, Bhb, Bot = [[Buf(), Buf()] for _ in range(3)]
        nt = self.norm_tiles(st, BLK)
        xiv = xin.rearrange("(c p) t -> p c t", p=128)
        blocks = self.blocks(True)
        A = self.pv("norm_f")

        def load(n):
            b, k = blocks[n]
            S.dma("sp", xs[n % 2], xiv[:, :, b * T + k * BLK:b * T + (k + 1) * BLK], writes=[Bxs[n % 2]])

        load(0)
        nt_i = 0
        for n, (b, k) in enumerate(blocks):
            i = n % 2
            if n + 1 < len(blocks):
                load(n + 1)
            self.norm_block(nt, xs[i], Bxs[i], BLK, A, None, hb[i], Bhb[i], 6)
            for tt in range(2):
                o = nt_i % 2
                nt_i += 1
                for hf in range(2):
                    pb = 2 * o + hf
                    for c4 in range(4):
                        c = hf * 4 + c4
                        S.op("pe", lambda: nc.tensor.transpose(out=self.PS[pb][:, c4 * 128:(c4 + 1) * 128], in_=hb[i][:, c, tt * 128:(tt + 1) * 128], identity=self.ident),
                             [Bhb[i]], [self.BPS[pb]])
                    if hf == 0:
                        S.op("dve", lambda: nc.vector.tensor_copy(out=ot[o][:, 0:512], in_=self.PS[pb]), [self.BPS[pb]], [Bot[o]])
                    else:
                        S.op("act", lambda: nc.scalar.copy(out=ot[o][:, 512:1024], in_=self.PS[pb]), [self.BPS[pb]], [Bot[o]])
                tl = k * BLK - TC + tt * 128
                S.dma("pool", self.out[b, tl:tl + 128, :], ot[o], reads=[Bot[o]])
        st.close()


def build_program(wshapes, pv_off, npv, plan=None, dbg=()):
    P = Prog(wshapes, pv_off, npv, dbg=dbg)
    xa = P.scr("xA", [D, TT])
    xb = P.scr("xB", [D, TT])
    if plan is None:
        plan = ["tr", "mod"]
        for l in range(DEPTH):
            plan += [f"mix{l}", f"ffn{l}"]
        plan += ["final"]
    cur, nxt = xa, xb
    for step in plan:
        if step == "tr":
            P.prologue_transpose(cur)
        elif step == "mod":
            P.prologue_mod()
        elif step.startswith("mix"):
            l = int(step[3:])
            P.mixer(l, cur, nxt)
            cur, nxt = nxt, cur
        elif step.startswith("ffn"):
            l = int(step[3:])
            P.ffn_stage(l, cur, nxt, skip_ctx=(l == DEPTH - 1))
            cur, nxt = nxt, cur
        elif step == "final":
            P.final_stage(cur)
    P.S.barrier()
    return P


def prep_inputs(inputs, cores=range(NCORES)):
    pv = pvec_layout(inputs)
    pva = pv.array()
    consts = make_consts()
    shared = {"pvec": pva}
    for k, v in consts.items():
        shared["c_" + k] = v
    for n in WEIGHT_NAMES:
        shared[n] = np.ascontiguousarray(inputs[n], dtype=np.float32)
    in_maps = []
    for c in cores:
        m = dict(shared)
        m["x"] = np.ascontiguousarray(inputs["x"][NB * c:NB * (c + 1)], dtype=np.float32)
        m["ctx"] = np.ascontiguousarray(inputs["ctx"][NB * c:NB * (c + 1)], dtype=np.float32)
        m["cvec"] = np.ascontiguousarray(np.concatenate([inputs["c"][NB * c:NB * (c + 1)], inputs["c_ctx"][None, :]], axis=0), dtype=np.float32)
        in_maps.append(m)
    wshapes = {n: list(inputs[n].shape) for n in WEIGHT_NAMES}
    return in_maps, wshapes, pv.off, pva.shape[1]


def kernel(**inputs):
    inputs = {k: np.asarray(v) for k, v in inputs.items()}
    in_maps, wshapes, pv_off, npv = prep_inputs(inputs)
    P = build_program(wshapes, pv_off, npv)
    res = run_bass_kernel_spmd(P.nc, in_maps, core_ids=list(range(NCORES)))
    out = np.concatenate([np.asarray(r["out"]) for r in res.results], axis=0)
    return out.astype(np.float32)


def _inproj_stage(self, l, xin, Wd, N, dst_fm, tm_specs, f32_h=False):
    nc, S = self.nc, self.S
    st = Stage(self, "ip")
    Wt = st.sb("w", [128, 8, N], BF16)
    BW = self.load_w(Wt, Wd, None)
    xs = [st.sb(f"xs{i}", [128, 8, BLK]) for i in range(2)]
    hb = [st.sb(f"hb{i}", [128, 8, BLK], BF16) for i in range(2)]
    sg = [st.sb(f"sg{i}", [128, 8, BLK]) for i in range(2)]
    tmw = max([nc_ for (_, nc_, _) in tm_specs], default=0)
    tms = [st.sb(f"tm{i}", [128, max(tmw, 1)], BF16) for i in range(2)]
    Bxs, Bhb, Bsg, Btm = [[Buf(), Buf()] for _ in range(4)]
    nt = self.norm_tiles(st, BLK)
    xiv = xin.rearrange("(c p) t -> p c t", p=128)
    dv = dst_fm.rearrange("(c p) t -> p c t", p=128)
    blocks = self.blocks(False)

    def load(n):
        b, k = blocks[n]
        S.dma("sp", xs[n % 2], xiv[:, :, b * T + k * BLK:b * T + (k + 1) * BLK], writes=[Bxs[n % 2]])

    load(0)
    sgi = 0
    tmi = 0
    pbank = 0
    for n, (b, k) in enumerate(blocks):
        i = n % 2
        if n + 1 < len(blocks):
            load(n + 1)
        j = 2 if k == 0 else b
        A, sh, _ = self.mod_ab(l, 0, j)
        self.norm_block(nt, xs[i], Bxs[i], BLK, A, sh, hb[i], Bhb[i], 6)
        col = b * T + k * BLK
        for og in range(N // 1024):
            s_ = sgi % 2
            sgi += 1
            for o8 in range(8):
                oc = og * 8 + o8
                pb = pbank % 4
                pbank += 1
                for kc in range(8):
                    S.op("pe", lambda: nc.tensor.matmul(self.PS[pb][:, :BLK], lhsT=Wt[:, kc, oc * 128:(oc + 1) * 128], rhs=hb[i][:, kc, :], start=(kc == 0), stop=(kc == 7)),
                         [BW[(oc * 128) // 512], Bhb[i]], [self.BPS[pb]])
                if o8 % 2 == 0:
                    S.op("act", lambda: nc.scalar.copy(out=sg[s_][:, o8, :], in_=self.PS[pb][:, :BLK]), [self.BPS[pb]], [Bsg[s_]])
                else:
                    S.op("dve", lambda: nc.vector.tensor_copy(out=sg[s_][:, o8, :], in_=self.PS[pb][:, :BLK]), [self.BPS[pb]], [Bsg[s_]])
            S.dma("pool", dv[:, og * 8:(og + 1) * 8, col:col + BLK], sg[s_], reads=[Bsg[s_]])
        for (c0, ncols, dst_tm) in tm_specs:
            for tt in range(BLK // 128):
                s_ = tmi % 2
                tmi += 1
                for n0 in range(0, ncols, 512):
                    pb = 4 + (pbank % 2)
                    pbank += 1
                    for kc in range(8):
                        S.op("pe", lambda: nc.tensor.matmul(self.PS[pb][:, :512], lhsT=hb[i][:, kc, tt * 128:(tt + 1) * 128], rhs=Wt[:, kc, c0 + n0:c0 + n0 + 512], start=(kc == 0), stop=(kc == 7)),
                             [BW[(c0 + n0) // 512], Bhb[i]], [self.BPS[pb]])
                    S.op("act", lambda: nc.scalar.copy(out=tms[s_][:, n0:n0 + 512], in_=self.PS[pb][:, :512]), [self.BPS[pb]], [Btm[s_]])
                S.dma("pool", dst_tm[col + tt * 128:col + (tt + 1) * 128, :], tms[s_][:, :ncols], reads=[Btm[s_]])
    st.close()


def _outproj_stage(self, l, og, Wd, xin, xout, skip_ctx):
    nc, S = self.nc, self.S
    st = Stage(self, "op")
    Wt = st.sb("w", [128, 8, D], BF16)
    BW = self.load_w(Wt, Wd, None)
    xs = [st.sb(f"xs{i}", [128, 8, BLK]) for i in range(2)]
    ob = [st.sb(f"ob{i}", [128, 8, BLK], BF16) for i in range(2)]
    Bxs, Bob = [[Buf(), Buf()] for _ in range(2)]
    xiv = xin.rearrange("(c p) t -> p c t", p=128)
    xov = xout.rearrange("(c p) t -> p c t", p=128)
    ogv = og.rearrange("(c p) t -> p c t", p=128)
    blocks = self.blocks(skip_ctx)

    def load(n):
        b, k = blocks[n]
        col = b * T + k * BLK
        S.dma("sp", xs[n % 2], xiv[:, :, col:col + BLK], writes=[Bxs[n % 2]])
        S.dma("sp", ob[n % 2], ogv[:, :, col:col + BLK], writes=[Bob[n % 2]])

    load(0)
    for n, (b, k) in enumerate(blocks):
        i = n % 2
        if n + 1 < len(blocks):
            load(n + 1)
        j = 2 if k == 0 else b
        _, _, gate = self.mod_ab(l, 0, j)
        for oc in range(8):
            pb = oc % 4
            for kc in range(8):
                S.op("pe", lambda: nc.tensor.matmul(self.PS[pb][:, :BLK], lhsT=Wt[:, kc, oc * 128:(oc + 1) * 128], rhs=ob[i][:, kc, :], start=(kc == 0), stop=(kc == 7)),
                     [BW[(oc * 128) // 512], Bob[i]], [self.BPS[pb]])
            S.op("dve", lambda: nc.vector.scalar_tensor_tensor(out=xs[i][:, oc, :], in0=self.PS[pb][:, :BLK], scalar=gate[:, oc:oc + 1], in1=xs[i][:, oc, :], op0=ALU.mult, op1=ALU.add),
                 [self.BPS[pb], Bxs[i], self.BMOD], [Bxs[i]])
        col = b * T + k * BLK
        S.dma("pool", xov[:, :, col:col + BLK], xs[i], reads=[Bxs[i]])
    st.close()


def _hgrn2_scan(self, jh, Pfm, Itm, og):
    nc, S = self.nc, self.S
    st = Stage(self, "hs")
    A_ = nc.vector
    LB = st.sb("LB", [128, 2, 8])
    OML = st.sb("OML", [128, 2, 8])
    e0 = st.sb("e0", [128, 8]); e1 = st.sb("e1", [128, 8]); rr = st.sb("rr", [128, 8]); p0 = st.sb("p0", [128, 8]); p1 = st.sb("p1", [128, 8])
    BL = Buf()
    for d in range(2):
        S.op("act", lambda: nc.scalar.activation(out=e0, in_=self.pv(f"hg_lb{d}_0"), func=AF.Exp), [], [BL])
        S.op("act", lambda: nc.scalar.activation(out=e1, in_=self.pv(f"hg_lb{d}_1"), func=AF.Exp), [BL], [BL])
        S.op("dve", lambda: A_.tensor_tensor(out=rr, in0=e0, in1=e1, op=ALU.add), [BL], [BL])
        S.op("dve", lambda: A_.reciprocal(out=rr, in_=rr), [BL], [BL])
        S.op("dve", lambda: A_.tensor_tensor(out=p0, in0=e0, in1=rr, op=ALU.mult), [BL], [BL])
        S.op("dve", lambda: A_.tensor_tensor(out=p1, in0=e1, in1=rr, op=ALU.mult), [BL], [BL])
        if jh == 1:
            S.op("dve", lambda: A_.tensor_tensor(out=p1, in0=p0, in1=p1, op=ALU.add), [BL], [BL])
        else:
            S.op("dve", lambda: A_.tensor_copy(out=p1, in_=p0), [BL], [BL])
        S.op("dve", lambda: A_.tensor_tensor(out=LB[:, d, :], in0=p1, in1=p0, op=ALU.subtract), [BL], [BL])
        S.op("dve", lambda: A_.tensor_scalar(out=OML[:, d, :], in0=LB[:, d, :], scalar1=-1.0, scalar2=1.0, op0=ALU.mult, op1=ALU.add), [BL], [BL])
    smask = st.sb("smask", [128, T])
    Bsm = Buf()
    S.dma("sp", smask, self.cd["scanmask"], writes=[Bsm])
    f32t = lambda n: st.sb(n, [128, T])
    qs = f32t("qs"); graw = f32t("graw"); kk = f32t("kk"); ep = f32t("ep"); en = f32t("en")
    z = [f32t("z0"), f32t("z1")]; bb = [f32t("b0"), f32t("b1")]; of = [f32t("of0"), f32t("of1")]
    qt = [st.sb(f"qt{d}", [128, T], BF16) for d in range(2)]
    kh = [st.sb(f"kh{d}", [128, T], BF16) for d in range(2)]
    sqb = st.sb("sqb", [128, T], BF16)
    ogb = st.sb("ogb", [128, T], BF16)
    Vt = st.sb("Vt", [64, NCH, 128], BF16)
    emid = [st.sb(f"emid{d}", [128, NCH]) for d in range(2)]
    eend = [st.sb(f"eend{d}", [128, NCH]) for d in range(2)]
    eem = [st.sb(f"eem{d}", [128, NCH]) for d in range(2)]
    Sst = [st.sb(f"S{d}", [128, 128]) for d in range(2)]
    Sm = [st.sb(f"Sm{d}", [128, 128], BF16) for d in range(2)]
    tmpS = [st.sb(f"tS{d}", [128, 128]) for d in range(2)]
    khT = [st.sb(f"khT{d}", [64, 128], BF16) for d in range(2)]
    att = [st.sb(f"att{d}", [64, 64], BF16) for d in range(2)]
    Bqs, Bgr, Bkk, Bep, Ben, Bsq, Bog, BVt = [Buf() for _ in range(8)]
    Bz, Bbb, Bof, Bqt, Bkh, Bes, BS, BSm, BtS, BkT, Batt = [[Buf(), Buf()] for _ in range(11)]
    PSb = [self.PS[i].bitcast(BF16) for i in range(8)]
    for d in range(2):
        S.op("dve", lambda: A_.memset(att[d], 0.0), [], [Batt[d]])
    cf = list(range(NCH))
    cb = list(range(TC // CH - 1, -1, -1)) + list(range(NCH - 1, TC // CH - 1, -1))
    order = [cf, cb]
    for b in range(NB):
        for h in range(8):
            rows = slice(h * 128, (h + 1) * 128)
            cols = slice(b * T, (b + 1) * T)
            S.dma("sp", qs, Pfm[0 * D + h * 128:0 * D + (h + 1) * 128, cols], writes=[Bqs])
            S.dma("sp", z[0], Pfm[3 * D + h * 128:3 * D + (h + 1) * 128, cols], writes=[Bz[0]])
            S.dma("sp", z[1], Pfm[4 * D + h * 128:4 * D + (h + 1) * 128, cols], writes=[Bz[1]])
            S.dma("sp", graw, Pfm[2 * D + h * 128:2 * D + (h + 1) * 128, cols], writes=[Bgr])
            S.dma("sp", Vt, Itm[cols, rows].rearrange("(c s) v -> s c v", s=CH), writes=[BVt])
            S.op("act", lambda: nc.scalar.activation(out=qs, in_=qs, func=AF.Silu), [Bqs], [Bqs])
            for d in range(2):
                m_idx = 32 if d == 0 else 31
                zt = z[d]
                S.op("act", lambda: nc.scalar.activation(out=zt, in_=zt, func=AF.Sigmoid), [Bz[d]], [Bz[d]])
                S.op("dve", lambda: A_.tensor_scalar(out=zt, in0=zt, scalar1=OML[:, d, h:h + 1], scalar2=LB[:, d, h:h + 1], op0=ALU.mult, op1=ALU.add), [Bz[d], BL], [Bz[d]])
                S.op("dve", lambda: A_.tensor_scalar(out=kk, in0=zt, scalar1=-1.0, scalar2=1.0, op0=ALU.mult, op1=ALU.add), [Bz[d]], [Bkk])
                S.op("act", lambda: nc.scalar.activation(out=zt, in_=zt, func=AF.Ln), [Bz[d]], [Bz[d]])
                S.op("dve", lambda: A_.tensor_tensor_scan(out=bb[d], data0=smask, data1=zt, initial=0.0, op0=ALU.mult, op1=ALU.add), [Bsm, Bz[d]], [Bbb[d]])
                b3 = bb[d].rearrange("p (c s) -> p c s", s=CH)
                if d == 1:
                    S.op("dve", lambda: A_.tensor_tensor(out=zt, in0=zt, in1=bb[d], op=ALU.subtract), [Bz[d], Bbb[d]], [Bz[d]])
                    S.op("dve", lambda: A_.tensor_tensor(out=ep.rearrange("p (c s) -> p c s", s=CH), in0=zt.rearrange("p (c s) -> p c s", s=CH),
                                                          in1=b3[:, :, CH - 1:CH].to_broadcast([128, NCH, CH]), op=ALU.add), [Bz[d], Bbb[d]], [Bep])
                    S.op("dve", lambda: A_.tensor_copy(out=bb[d], in_=ep), [Bep], [Bbb[d]])
                e_idx = CH - 1 if d == 0 else 0
                S.op("act", lambda: nc.scalar.activation(out=emid[d], in_=b3[:, :, m_idx], func=AF.Exp), [Bbb[d]], [Bes[d]])
                S.op("act", lambda: nc.scalar.activation(out=eend[d], in_=b3[:, :, e_idx], func=AF.Exp), [Bbb[d]], [Bes[d]])
                S.op("dve", lambda: A_.tensor_tensor(out=eem[d], in0=b3[:, :, e_idx], in1=b3[:, :, m_idx], op=ALU.subtract), [Bbb[d]], [Bes[d]])
                S.op("act", lambda: nc.scalar.activation(out=eem[d], in_=eem[d], func=AF.Exp), [Bes[d]], [Bes[d]])
                S.op("dve", lambda: A_.tensor_tensor(out=ep.rearrange("p (c s) -> p c s", s=CH), in0=b3, in1=b3[:, :, m_idx:m_idx + 1].to_broadcast([128, NCH, CH]), op=ALU.subtract),
                     [Bbb[d]], [Bep])
                S.op("act", lambda: nc.scalar.activation(out=en, in_=ep, func=AF.Exp, scale=-1.0), [Bep], [Ben])
                S.op("act", lambda: nc.scalar.activation(out=ep, in_=ep, func=AF.Exp), [Bep], [Bep])
                S.op("dve", lambda: A_.tensor_tensor(out=qt[d], in0=qs, in1=ep, op=ALU.mult), [Bqs, Bep], [Bqt[d]])
                S.op("dve", lambda: A_.tensor_tensor(out=kh[d], in0=kk, in1=en, op=ALU.mult), [Bkk, Ben], [Bkh[d]])
                S.op("dve", lambda: A_.memset(Sst[d], 0.0), [], [BS[d]])
                S.op("dve", lambda: A_.memset(Sm[d], 0.0), [], [BSm[d]])
            def hstep(d, step):
                c = order[d][step]
                cs = slice(c * CH, (c + 1) * CH)
                pb = d * 4
                mk = (self.masks[0:64, 64:128] if d == 0 else self.masks[0:64, 192:256]).bitcast(mybir.dt.uint32)
                S.op("pe", lambda: nc.tensor.transpose(out=PSb[pb][0:64, 0:128], in_=kh[d][:, cs], identity=self.identb), [Bkh[d]], [self.BPS[pb]])
                S.op("pe", lambda: nc.tensor.matmul(self.PS[pb + 1][0:64, 0:64], lhsT=kh[d][:, cs], rhs=qt[d][:, cs], start=True, stop=True), [Bkh[d], Bqt[d]], [self.BPS[pb + 1]])
                yield
                S.op("act", lambda: nc.scalar.copy(out=khT[d], in_=PSb[pb][0:64, 0:128]), [self.BPS[pb]], [BkT[d]])
                S.op("dve", lambda: A_.copy_predicated(out=att[d], mask=mk, data=self.PS[pb + 1][0:64, 0:64]), [self.BPS[pb + 1]], [Batt[d]])
                S.op("pe", lambda: nc.tensor.matmul(self.PS[pb + 2][:, 0:64], lhsT=Vt[:, c, :], rhs=att[d], start=True, stop=False), [BVt, Batt[d]], [self.BPS[pb + 2]])
                S.op("pe", lambda: nc.tensor.matmul(self.PS[pb + 2][:, 0:64], lhsT=Sm[d], rhs=qt[d][:, cs], start=False, stop=True), [BSm[d], Bqt[d]], [self.BPS[pb + 2]])
                S.op("pe", lambda: nc.tensor.matmul(self.PS[pb + 3][:, 0:128], lhsT=khT[d], rhs=Vt[:, c, :], start=True, stop=True), [BkT[d], BVt], [self.BPS[pb + 3]])
                yield
                S.op("act", lambda: nc.scalar.activation(out=tmpS[d], in_=self.PS[pb + 3][:, 0:128], func=AF.Identity, scale=eem[d][:, c:c + 1]), [self.BPS[pb + 3], Bes[d]], [BtS[d]])
                S.op("dve", lambda: A_.scalar_tensor_tensor(out=Sst[d], in0=Sst[d], scalar=eend[d][:, c:c + 1], in1=tmpS[d], op0=ALU.mult, op1=ALU.add), [BS[d], BtS[d], Bes[d]], [BS[d]])
                S.op("act", lambda: nc.scalar.copy(out=of[d][:, cs], in_=self.PS[pb + 2][:, 0:64]), [self.BPS[pb + 2]], [Bof[d]])
                if step + 1 < NCH:
                    cn = order[d][step + 1]
                    S.op("dve", lambda: A_.tensor_scalar(out=Sm[d], in0=Sst[d], scalar1=emid[d][:, cn:cn + 1], scalar2=None, op0=ALU.mult), [BS[d], Bes[d]], [BSm[d]])

            for step in range(NCH):
                gens = [hstep(d, step) for d in range(2)]
                while gens:
                    for g_ in list(gens):
                        try:
                            next(g_)
                        except StopIteration:
                            gens.remove(g_)
            S.op("dve", lambda: A_.tensor_tensor(out=of[0], in0=of[0], in1=of[1], op=ALU.add), [Bof[0], Bof[1]], [Bof[0]])
            S.op("act", lambda: nc.scalar.activation(out=sqb, in_=of[0], func=AF.Square), [Bof[0]], [Bsq])
            for pc in range(6):
                sl_ = slice(pc * 384, (pc + 1) * 384)
                pb = pc % 2
                S.op("pe", lambda: nc.tensor.matmul(self.PS[pb][:, 0:384], lhsT=self.onesb, rhs=sqb[:, sl_], start=True, stop=True), [Bsq], [self.BPS[pb]])
                S.op("act", lambda: nc.scalar.activation(out=ep[:, sl_], in_=self.PS[pb][:, 0:384], func=AF.Sqrt, scale=1.0 / 128, bias=self.epsD), [self.BPS[pb]], [Bep])
            S.op("dve", lambda: A_.reciprocal(out=ep, in_=ep), [Bep], [Bep])
            S.op("dve", lambda: A_.tensor_tensor(out=of[0], in0=of[0], in1=ep, op=ALU.mult), [Bof[0], Bep], [Bof[0]])
            S.op("act", lambda: nc.scalar.activation(out=graw, in_=graw, func=AF.Silu), [Bgr], [Bgr])
            S.op("dve", lambda: A_.scalar_tensor_tensor(out=ogb, in0=of[0], scalar=self.pv(f"hg_norm{jh}", 0), in1=graw, op0=ALU.mult, op1=ALU.mult), [Bof[0], Bgr], [Bog])
            S.dma("pool", og[rows, cols], ogb, reads=[Bog])
    st.close()


def _mixer(self, l, cur, nxt):
    kind, j = l % 3, l // 3
    last = (l == DEPTH - 1)
    og = self.scr("og", [D, TT], BF16)
    if kind == 0:
        Pfm = self.scr("hgP", [5 * D, TT])
        Itm = self.scr("hgI", [TT, D], BF16)
        self.inproj_stage(l, cur, self.W["hg_w_in"][j], 5 * D, Pfm, [(D, D, Itm)])
        self.hgrn2_scan(j, Pfm, Itm, og)
        self.outproj_stage(l, og, self.W["hg_w_o"][j], cur, nxt, last)
    elif kind == 1:
        self.rwkv_mixer(l, cur, og)
        self.outproj_stage(l, og, self.W["rw_w_o"][j], cur, nxt, last)
    else:
        self.mla_mixer(l, cur, og)
        self.outproj_stage(l, og, self.W["mla_w_o"][j], cur, nxt, last)


Prog.inproj_stage = _inproj_stage
Prog.outproj_stage = _outproj_stage
Prog.hgrn2_scan = _hgrn2_scan
Prog.mixer = _mixer


def _mla_mixer(self, l, xin, og):
    nc, S = self.nc, self.S
    A_ = nc.vector
    NH = 16
    QN = self.scr("mlaQN", [64, NH, TT], BF16)
    QR = self.scr("mlaQR", [32, NH, TT], BF16)
    KN = self.scr("mlaKN", [64, NH, TT], BF16)
    KR = self.scr("mlaKR", [32, TT], BF16)
    VT = self.scr("mlaVT", [TT, D], BF16)
    st = Stage(self, "m1")
    Wd = st.sb("wd", [128, 8, 544], BF16)
    Wq = st.sb("wq", [128, 2, 1536], BF16)
    Wk = st.sb("wk", [128, 2, 2048], BF16)
    Wdr = st.sb("wdr", [128, 8, 32], BF16)
    Wqr = st.sb("wqr", [128, 2, NH, 32], BF16)
    BWd, BWq, BWk, BWr = Buf(), Buf(), Buf(), Buf()
    S.dma("pool", Wd, self.W["mla_w_dqkv"][0].rearrange("(kc p) n -> p kc n", p=128), writes=[BWd])
    wqv = self.W["mla_w_uq"][0].rearrange("(kc p) n -> p kc n", p=128)
    for i3 in range(3):
        S.dma("pool", Wq[:, :, i3 * 512:(i3 + 1) * 512], wqv[:, :, i3 * 512:(i3 + 1) * 512], writes=[BWq])
    wkv = self.W["mla_w_ukv"][0].rearrange("(kc p) n -> p kc n", p=128)
    for i4 in range(4):
        S.dma("pool", Wk[:, :, i4 * 512:(i4 + 1) * 512], wkv[:, :, i4 * 512:(i4 + 1) * 512], writes=[BWk])
    Wq4 = Wq.rearrange("p k (h c) -> p k h c", c=96)
    for seg in range(2):
        for half in range(2):
            sgn = -1.0 if half == 0 else 1.0
            so = 64 + seg * 16 + (1 - half) * 8
            do = seg * 16 + half * 8
            S.op("act", lambda: nc.scalar.activation(out=Wqr[:, :, :, do:do + 8], in_=Wq4[:, :, :, so:so + 8], func=AF.Copy, scale=sgn), [BWq], [BWr])
            so2 = 512 + seg * 16 + (1 - half) * 8
            S.op("act", lambda: nc.scalar.activation(out=Wdr[:, :, do:do + 8], in_=Wd[:, :, so2:so2 + 8], func=AF.Copy, scale=sgn), [BWd], [BWr])
    cos = st.sb("cos", [32, T]); sin = st.sb("sin", [32, T])
    Bcs = Buf()
    S.dma("sp", cos, self.cd["rope_cos"], writes=[Bcs])
    S.dma("sp", sin, self.cd["rope_sin"], writes=[Bcs])
    xs = [st.sb(f"xs{i}", [128, 8, BLK]) for i in range(2)]
    hb = [st.sb(f"hb{i}", [128, 8, BLK], BF16) for i in range(2)]
    Bxs, Bhb = [[Buf(), Buf()] for _ in range(2)]
    nt = self.norm_tiles(st, BLK)
    cs_ = st.sb("cs", [128, 4, BLK]); csq = st.sb("csq", [128, 4, BLK], BF16); cn = st.sb("cn", [128, 4, BLK], BF16)
    rr0 = st.sb("rr0", [128, 2, BLK]); rr1 = st.sb("rr1", [128, 2, BLK]); ctmp = st.sb("ctmp", [128, 4, BLK])
    Bcs_, Bcsq, Bcn, Brr, Bct = [Buf() for _ in range(5)]
    qn_s = [st.sb(f"qns{i}", [64, NH, BLK], BF16) for i in range(2)]
    kn_s = [st.sb(f"kns{i}", [64, NH, BLK], BF16) for i in range(2)]
    qr_s = [st.sb(f"qrs{i}", [32, NH, BLK], BF16) for i in range(2)]
    kr_s = [st.sb(f"krs{i}", [32, BLK], BF16) for i in range(2)]
    vt_s = [st.sb(f"vts{i}", [128, D], BF16) for i in range(2)]
    t1 = st.sb("t1", [32, 2, BLK]); t2 = st.sb("t2", [32, 2, BLK])
    Bt1, Bt2 = Buf(), Buf()
    Bqn, Bkn, Bqr, Bkr, Bvt = [[Buf(), Buf()] for _ in range(5)]
    xiv = xin.rearrange("(c p) t -> p c t", p=128)
    blocks = self.blocks(False)

    def load(n):
        b, k = blocks[n]
        S.dma("sp", xs[n % 2], xiv[:, :, b * T + k * BLK:b * T + (k + 1) * BLK], writes=[Bxs[n % 2]])

    load(0)
    vti = 0
    for n, (b, k) in enumerate(blocks):
        i = n % 2
        if n + 1 < len(blocks):
            load(n + 1)
        j = 2 if k == 0 else b
        A, sh, _ = self.mod_ab(l, 0, j)
        self.norm_block(nt, xs[i], Bxs[i], BLK, A, sh, hb[i], Bhb[i], 6)
        col = b * T + k * BLK
        tcol = slice(k * BLK, (k + 1) * BLK)
        for c4 in range(4):
            pb = c4 // 2
            for kc in range(8):
                S.op("pe", lambda: nc.tensor.matmul(self.PS[pb][:, (c4 % 2) * BLK:(c4 % 2 + 1) * BLK], lhsT=Wd[:, kc, c4 * 128:(c4 + 1) * 128], rhs=hb[i][:, kc, :], start=(kc == 0), stop=(kc == 7)),
                     [BWd, Bhb[i]], [self.BPS[pb]])
        for kc in range(8):
            S.op("pe", lambda: nc.tensor.matmul(self.PS[2][0:32, 0:BLK], lhsT=Wd[:, kc, 512:544], rhs=hb[i][:, kc, :], start=(kc == 0), stop=(kc == 7)), [BWd, Bhb[i]], [self.BPS[2]])
        for kc in range(8):
            S.op("pe", lambda: nc.tensor.matmul(self.PS[2][0:32, BLK:2 * BLK], lhsT=Wdr[:, kc, :], rhs=hb[i][:, kc, :], start=(kc == 0), stop=(kc == 7)), [BWr, Bhb[i]], [self.BPS[2]])
        for pb in range(2):
            S.op("act", lambda: nc.scalar.copy(out=cs_[:, 2 * pb:2 * pb + 2, :], in_=self.PS[pb].rearrange("p (c t) -> p c t", c=2)), [self.BPS[pb]], [Bcs_])
            S.op("act", lambda: nc.scalar.activation(out=csq[:, 2 * pb:2 * pb + 2, :], in_=self.PS[pb].rearrange("p (c t) -> p c t", c=2), func=AF.Square), [self.BPS[pb]], [Bcsq])
        S.op("dve", lambda: A_.tensor_tensor(out=t1[:, 0, :], in0=self.PS[2][0:32, 0:BLK], in1=cos[:, tcol], op=ALU.mult), [self.BPS[2], Bcs], [Bt1])
        S.op("dve", lambda: A_.tensor_tensor(out=t2[:, 0, :], in0=self.PS[2][0:32, BLK:2 * BLK], in1=sin[:, tcol], op=ALU.mult), [self.BPS[2], Bcs], [Bt2])
        S.op("dve", lambda: A_.tensor_tensor(out=kr_s[i], in0=t1[:, 0, :], in1=t2[:, 0, :], op=ALU.add), [Bt1, Bt2], [Bkr[i]])
        S.dma("pool", KR[:, col:col + BLK], kr_s[i], reads=[Bkr[i]])
        for w in range(2):
            for c in range(2):
                S.op("pe", lambda: nc.tensor.matmul(self.PS[3][:, w * BLK:(w + 1) * BLK], lhsT=self.onesb, rhs=csq[:, 2 * w + c, :], start=(c == 0), stop=(c == 1)), [Bcsq], [self.BPS[3]])
        S.op("act", lambda: nc.scalar.activation(out=rr0, in_=self.PS[3].rearrange("p (w t) -> p w t", w=2), func=AF.Sqrt, scale=1.0 / 256, bias=self.epsD), [self.BPS[3]], [Brr])
        S.op("dve", lambda: A_.reciprocal(out=rr1, in_=rr0), [Brr], [Brr])
        S.op("dve", lambda: A_.tensor_tensor(out=ctmp.rearrange("p (w c) t -> p w c t", w=2), in0=cs_.rearrange("p (w c) t -> p w c t", w=2),
                                              in1=rr1.unsqueeze(2).to_broadcast([128, 2, 2, BLK]), op=ALU.mult), [Bcs_, Brr], [Bct])
        for c4 in range(4):
            gname = "mla_q_norm" if c4 < 2 else "mla_kv_norm"
            S.op("act", lambda: nc.scalar.activation(out=cn[:, c4, :], in_=ctmp[:, c4, :], func=AF.Identity, scale=self.pv(gname, c4 % 2)), [Bct], [Bcn])
        for hp in range(8):
            for which in range(2):
                pb = 4 + (2 * hp + which) % 2
                Wt_, coff, hw, ci = (Wq, 0, 96, 0) if which == 0 else (Wk, 0, 128, 2)
                for hh in range(2):
                    h = 2 * hp + hh
                    for kc in range(2):
                        S.op("pe", lambda: nc.tensor.matmul(self.PS[pb][0:64, hh * BLK:(hh + 1) * BLK], lhsT=Wt_[:, kc, h * hw:h * hw + 64], rhs=cn[:, ci + kc, :], start=(kc == 0), stop=(kc == 1)),
                             [BWq if which == 0 else BWk, Bcn], [self.BPS[pb]])
                dst = qn_s[i] if which == 0 else kn_s[i]
                Bd = Bqn[i] if which == 0 else Bkn[i]
                if which == 0:
                    S.op("act", lambda: nc.scalar.copy(out=dst[:, 2 * hp:2 * hp + 2, :], in_=self.PS[pb][0:64, :].rearrange("p (h t) -> p h t", h=2)), [self.BPS[pb]], [Bd])
                else:
                    S.op("dve", lambda: A_.tensor_copy(out=dst[:, 2 * hp:2 * hp + 2, :], in_=self.PS[pb][0:64, :].rearrange("p (h t) -> p h t", h=2)), [self.BPS[pb]], [Bd])
            for hh in range(2):
                h = 2 * hp + hh
                for kc in range(2):
                    S.op("pe", lambda: nc.tensor.matmul(self.PS[6][0:32, hh * BLK:(hh + 1) * BLK], lhsT=Wq[:, kc, h * 96 + 64:h * 96 + 96], rhs=cn[:, kc, :], start=(kc == 0), stop=(kc == 1)), [BWq, Bcn], [self.BPS[6]])
                for kc in range(2):
                    S.op("pe", lambda: nc.tensor.matmul(self.PS[7][0:32, hh * BLK:(hh + 1) * BLK], lhsT=Wqr[:, kc, h, :], rhs=cn[:, kc, :], start=(kc == 0), stop=(kc == 1)), [BWr, Bcn], [self.BPS[7]])
            cosb = cos[:, tcol].unsqueeze(1).to_broadcast([32, 2, BLK])
            sinb = sin[:, tcol].unsqueeze(1).to_broadcast([32, 2, BLK])
            S.op("dve", lambda: A_.tensor_tensor(out=t1, in0=self.PS[6][0:32, :].rearrange("p (h t) -> p h t", h=2), in1=cosb, op=ALU.mult), [self.BPS[6], Bcs], [Bt1])
            S.op("dve", lambda: A_.tensor_tensor(out=t2, in0=self.PS[7][0:32, :].rearrange("p (h t) -> p h t", h=2), in1=sinb, op=ALU.mult), [self.BPS[7], Bcs], [Bt2])
            S.op("dve", lambda: A_.tensor_tensor(out=qr_s[i][:, 2 * hp:2 * hp + 2, :], in0=t1, in1=t2, op=ALU.add), [Bt1, Bt2], [Bqr[i]])
        S.dma("pool", QN[:, :, col:col + BLK], qn_s[i], reads=[Bqn[i]])
        S.dma("pool", KN[:, :, col:col + BLK], kn_s[i], reads=[Bkn[i]])
        S.dma("pool", QR[:, :, col:col + BLK], qr_s[i], reads=[Bqr[i]])
        Wkv = Wk.rearrange("p k (h c) -> p k h c", c=128)
        for tt in range(BLK // 128):
            vi = vti % 2
            vti += 1
            for hf in range(2):
                pb = 4 + hf
                for kc in range(2):
                    S.op("pe", lambda: nc.tensor.matmul(self.PS[pb][:, 0:512], lhsT=cn[:, 2 + kc, tt * 128:(tt + 1) * 128], rhs=Wkv[:, kc, hf * 8:(hf + 1) * 8, 64:128], start=(kc == 0), stop=(kc == 1)),
                         [BWk, Bcn], [self.BPS[pb]])
                S.op("act", lambda: nc.scalar.copy(out=vt_s[vi][:, hf * 512:(hf + 1) * 512], in_=self.PS[pb][:, 0:512]), [self.BPS[pb]], [Bvt[vi]])
            S.dma("pool", VT[col + tt * 128:col + (tt + 1) * 128, :], vt_s[vi], reads=[Bvt[vi]])
    st.close()
    st = Stage(self, "m2")
    NKT = T // 128
    Vall = st.sb("Vall", [128, NKT, D], BF16)
    KRs = st.sb("KRs", [32, T], BF16)
    KNh = [st.sb(f"KNh{i}", [64, T], BF16) for i in range(2)]
    QNh = [st.sb(f"QNh{i}", [64, T], BF16) for i in range(2)]
    QRh = [st.sb(f"QRh{i}", [32, T], BF16) for i in range(2)]
    VX = [st.sb(f"VX{i}", [128, NKT, 65], BF16) for i in range(2)]
    PT = [st.sb(f"PT{i}", [128, 512], BF16) for i in range(3)]
    rd = st.sb("rd", [65, 512]); rb = [st.sb(f"rb{i}", [64, 512]) for i in range(2)]
    ob = [st.sb(f"ob{i}", [64, 512], BF16) for i in range(2)]
    BVa, BKR, Brd = Buf(), Buf(), Buf()
    BKN, BQN, BQR, BVX, Brb, Bob = [[Buf(), Buf()] for _ in range(6)]
    BPT = [Buf() for _ in range(3)]
    for i in range(2):
        S.op("pool", lambda: nc.gpsimd.memset(VX[i], 1.0), [], [BVX[i]])
    qblocks = [(0, TC, 2)] + [(TC + qb * 512, 512, NKT) for qb in range(4)]
    pti = 0
    hn = 0
    for b in range(NB):
        c0 = b * T
        S.dma("sp", Vall, VT[c0:c0 + T, :].rearrange("(kt p) v -> p kt v", p=128), writes=[BVa])
        S.dma("sp", KRs, KR[:, c0:c0 + T], writes=[BKR])
        for h in range(NH):
            i = hn % 2
            hn += 1
            S.dma("sp", KNh[i], KN[:, h, c0:c0 + T], writes=[BKN[i]])
            S.dma("sp", QNh[i], QN[:, h, c0:c0 + T], writes=[BQN[i]])
            S.dma("sp", QRh[i], QR[:, h, c0:c0 + T], writes=[BQR[i]])
            S.op("pool", lambda: nc.gpsimd.tensor_copy(out=VX[i][:, :, 0:64], in_=Vall[:, :, h * 64:(h + 1) * 64]), [BVa], [BVX[i]])
            for qi, (q0, nq, nkt) in enumerate(qblocks):
                po = 4 + (qi % 2)

                def score(kt):
                    ps = kt % 4
                    ks = slice(kt * 128, (kt + 1) * 128)
                    S.op("pe", lambda: nc.tensor.matmul(self.PS[ps][:, 0:nq], lhsT=KNh[i][:, ks], rhs=QNh[i][:, q0:q0 + nq], start=True, stop=False), [BKN[i], BQN[i]], [self.BPS[ps]])
                    S.op("pe", lambda: nc.tensor.matmul(self.PS[ps][:, 0:nq], lhsT=KRs[:, ks], rhs=QRh[i][:, q0:q0 + nq], start=False, stop=True), [BKR, BQR[i]], [self.BPS[ps]])

                score(0)
                if nkt > 1:
                    score(1)
                for kt in range(nkt):
                    ps = kt % 4
                    p3 = pti % 3
                    pti += 1
                    if kt + 2 < nkt:
                        score(kt + 2)
                    S.op("act", lambda: nc.scalar.activation(out=PT[p3][:, 0:nq], in_=self.PS[ps][:, 0:nq], func=AF.Exp, scale=MLA_SCALE), [self.BPS[ps]], [BPT[p3]])
                    S.op("pe", lambda: nc.tensor.matmul(self.PS[po][0:65, 0:nq], lhsT=VX[i][:, kt, :], rhs=PT[p3][:, 0:nq], start=(kt == 0), stop=(kt == nkt - 1)), [BVX[i], BPT[p3]], [self.BPS[po]])
                r2 = qi % 2
                S.op("dve", lambda: A_.reciprocal(out=rd[64:65, 0:nq], in_=self.PS[po][64:65, 0:nq]), [self.BPS[po]], [Brd])
                S.op("pe", lambda: nc.tensor.matmul(self.PS[6 + r2][0:64, 0:nq], lhsT=self.onesf[64:65, 0:64], rhs=rd[64:65, 0:nq], start=True, stop=True), [Brd], [self.BPS[6 + r2]])
                S.op("act", lambda: nc.scalar.copy(out=rb[r2][:, 0:nq], in_=self.PS[6 + r2][0:64, 0:nq]), [self.BPS[6 + r2]], [Brb[r2]])
                S.op("dve", lambda: A_.tensor_tensor(out=ob[r2][:, 0:nq], in0=self.PS[po][0:64, 0:nq], in1=rb[r2][:, 0:nq], op=ALU.mult), [self.BPS[po], Brb[r2]], [Bob[r2]])
                S.dma("pool", og[h * 64:(h + 1) * 64, c0 + q0:c0 + q0 + nq], ob[r2][:, 0:nq], reads=[Bob[r2]])
    st.close()


Prog.mla_mixer = _mla_mixer


RW_ARR = ["r", "kt0", "kt1", "be0", "be1", "kap", "lw0", "lw1", "v", "g"]


def _rwkv_proj(self, l, xin, RWP, Vtm):
    nc, S = self.nc, self.S
    A_ = nc.vector
    st = Stage(self, "r1")
    Wrkv = st.sb("wrkv", [128, 8, 3 * D], BF16)
    BWrkv = []
    for i3 in range(3):
        v_ = self.W["rw_w_rkv"][0, i3].rearrange("(kc p) n -> p kc n", p=128)
        for hf in range(2):
            bb_ = Buf()
            S.dma("pool", Wrkv[:, :, i3 * D + hf * 512:i3 * D + (hf + 1) * 512], v_[:, :, hf * 512:(hf + 1) * 512], writes=[bb_])
            BWrkv.append(bb_)
    W1 = st.sb("w1", [128, 8, 2, 64], BF16); A1 = st.sb("a1", [128, 8, 2, 64], BF16); G1 = st.sb("g1", [128, 8, 160], BF16)
    W2 = st.sb("w2", [64, 2, D], BF16); A2 = st.sb("a2", [64, 2, D], BF16); G2a = st.sb("g2a", [128, D], BF16); G2b = st.sb("g2b", [32, D], BF16)
    Bsw = Buf()
    for d in range(2):
        S.dma("pool", W1[:, :, d, :], self.W["rw_w1"][0, d].rearrange("(kc p) n -> p kc n", p=128), writes=[Bsw])
        S.dma("pool", A1[:, :, d, :], self.W["rw_a1"][0, d].rearrange("(kc p) n -> p kc n", p=128), writes=[Bsw])
        S.dma("pool", W2[:, d, :], self.W["rw_w2"][0, d], writes=[Bsw])
        S.dma("pool", A2[:, d, :], self.W["rw_a2"][0, d], writes=[Bsw])
    S.dma("pool", G1, self.W["rw_g1"][0].rearrange("(kc p) n -> p kc n", p=128), writes=[Bsw])
    S.dma("pool", G2a, self.W["rw_g2"][0, 0:128, :], writes=[Bsw])
    S.dma("pool", G2b, self.W["rw_g2"][0, 128:160, :], writes=[Bsw])
    NH_ = BLK + 2
    xs = [st.sb(f"xs{i}", [128, 8, NH_]) for i in range(2)]
    hf_ = st.sb("hf", [128, 8, NH_])
    dx = st.sb("dx", [128, 8, BLK])
    xj = [st.sb(f"xj{j}", [128, 8, BLK], BF16) for j in range(6)]
    Bxs = [Buf(), Buf()]
    Bhf, Bdx = Buf(), Buf()
    Bxj = [Buf() for _ in range(6)]
    nt = self.norm_tiles(st)
    lt = st.sb("lt", [64, 5, BLK], BF16)
    gh = st.sb("gh", [128, BLK], BF16)
    Blt = Buf()
    stg = [st.sb(f"stg{i}", [128, 10, BLK]) for i in range(2)]
    Bstg = [Buf(), Buf()]
    tmp = [st.sb(f"tmp{i}", [128, BLK]) for i in range(6)]
    Btmp = [Buf() for _ in range(6)]
    sqb = st.sb("sqb", [128, BLK], BF16)
    Bsqb = Buf()
    vts = [st.sb(f"vts{i}", [128, D], BF16) for i in range(2)]
    Bvts = [Buf(), Buf()]
    for i in range(2):
        S.op("dve", lambda: A_.memset(xs[i], 0.0), [], [Bxs[i]])
    xiv = xin.rearrange("(c p) t -> p c t", p=128)
    blocks = self.blocks(False)
    blk64b = st.sb("blk64b", [128, 128], BF16)
    Bb64 = Buf()
    S.op("dve", lambda: A_.tensor_copy(out=blk64b, in_=self.blk64), [], [Bb64])

    def load(n):
        b, k = blocks[n]
        t0, lo, hi, _, _ = self.blk_range(k)
        S.dma("sp", xs[n % 2][:, :, lo - (t0 - 1):hi - (t0 - 1)], xiv[:, :, b * T + lo:b * T + hi], writes=[Bxs[n % 2]])

    load(0)
    si = 0
    vi_ = 0
    pbk = 0
    for n, (b, k) in enumerate(blocks):
        i = n % 2
        if n + 1 < len(blocks):
            load(n + 1)
        t0, lo, hi, first, last = self.blk_range(k)
        j = 2 if k == 0 else b
        A, sh, _ = self.mod_ab(l, 0, j)
        self.norm_block(nt, xs[i], Bxs[i], NH_, A, sh, hf_, Bhf, 6)
        if first:
            S.op("dve", lambda: A_.memset(hf_[:, :, 0:1], 0.0), [], [Bhf])
        if last:
            S.op("dve", lambda: A_.memset(hf_[:, :, NH_ - 1:NH_], 0.0), [], [Bhf])
        S.op("dve", lambda: A_.tensor_tensor(out=dx, in0=hf_[:, :, 0:BLK], in1=hf_[:, :, 2:2 + BLK], op=ALU.add), [Bhf], [Bdx])
        S.op("dve", lambda: A_.scalar_tensor_tensor(out=dx, in0=dx, scalar=0.5, in1=hf_[:, :, 1:1 + BLK], op0=ALU.mult, op1=ALU.subtract), [Bhf, Bdx], [Bdx])
        for jj in range(6):
            for c in range(8):
                S.op("dve", lambda: A_.scalar_tensor_tensor(out=xj[jj][:, c, :], in0=dx[:, c, :], scalar=self.pv(f"rw_mu{jj}", c), in1=hf_[:, c, 1:1 + BLK], op0=ALU.mult, op1=ALU.add),
                     [Bdx, Bhf], [Bxj[jj]])
        for d in range(2):
            for kc in range(8):
                S.op("pe", lambda: nc.tensor.matmul(self.PS[5][0:64, d * BLK:(d + 1) * BLK], lhsT=W1[:, kc, d, :], rhs=xj[1][:, kc, :], start=(kc == 0), stop=(kc == 7)), [Bsw, Bxj[1]], [self.BPS[5]])
        S.op("act", lambda: nc.scalar.activation(out=lt[:, 0:2, :], in_=self.PS[5][0:64, :].rearrange("p (d t) -> p d t", d=2), func=AF.Tanh), [self.BPS[5]], [Blt])
        for d in range(2):
            for kc in range(8):
                S.op("pe", lambda: nc.tensor.matmul(self.PS[5][0:64, d * BLK:(d + 1) * BLK], lhsT=A1[:, kc, d, :], rhs=xj[4][:, kc, :], start=(kc == 0), stop=(kc == 7)), [Bsw, Bxj[4]], [self.BPS[5]])
        S.op("act", lambda: nc.scalar.copy(out=lt[:, 2:4, :], in_=self.PS[5][0:64, :].rearrange("p (d t) -> p d t", d=2)), [self.BPS[5]], [Blt])
        for kc in range(8):
            S.op("pe", lambda: nc.tensor.matmul(self.PS[5][:, 0:BLK], lhsT=G1[:, kc, 0:128], rhs=xj[5][:, kc, :], start=(kc == 0), stop=(kc == 7)), [Bsw, Bxj[5]], [self.BPS[5]])
        for kc in range(8):
            S.op("pe", lambda: nc.tensor.matmul(self.PS[5][0:32, BLK:2 * BLK], lhsT=G1[:, kc, 128:160], rhs=xj[5][:, kc, :], start=(kc == 0), stop=(kc == 7)), [Bsw, Bxj[5]], [self.BPS[5]])
        S.op("act", lambda: nc.scalar.activation(out=gh, in_=self.PS[5][:, 0:BLK], func=AF.Sigmoid), [self.BPS[5]], [Blt])
        S.op("act", lambda: nc.scalar.activation(out=lt[0:32, 4, :], in_=self.PS[5][0:32, BLK:2 * BLK], func=AF.Sigmoid), [self.BPS[5]], [Blt])
        col = b * T + t0
        for c in range(8):
            s_ = si % 2
            si += 1
            sg_ = stg[s_]
            Bs = Bstg[s_]
            cs = slice(c * 128, (c + 1) * 128)

            def bank():
                nonlocal pbk
                pbk += 1
                return pbk % 5

            prk = []
            for which, xsrc in ((0, 0), (1, 2), (2, 3)):
                pb = bank()
                for kc in range(8):
                    S.op("pe", lambda: nc.tensor.matmul(self.PS[pb][:, 0:BLK], lhsT=Wrkv[:, kc, which * D + c * 128:which * D + (c + 1) * 128], rhs=xj[xsrc][:, kc, :], start=(kc == 0), stop=(kc == 7)),
                         [BWrkv[which * 2 + (c // 4)], Bxj[xsrc]], [self.BPS[pb]])
                prk.append(pb)
            S.op("act", lambda: nc.scalar.copy(out=sg_[:, 0, :], in_=self.PS[prk[0]][:, 0:BLK]), [self.BPS[prk[0]]], [Bs])
            S.op("act", lambda: nc.scalar.copy(out=sg_[:, 8, :], in_=self.PS[prk[2]][:, 0:BLK]), [self.BPS[prk[2]]], [Bs])
            kraw = tmp[0]
            S.op("act", lambda: nc.scalar.copy(out=kraw, in_=self.PS[prk[1]][:, 0:BLK]), [self.BPS[prk[1]]], [Btmp[0]])
            S.op("dve", lambda: A_.tensor_scalar(out=tmp[1], in0=kraw, scalar1=self.pv("rw_k_k", c), scalar2=None, op0=ALU.mult), [Btmp[0]], [Btmp[1]])
            S.op("act", lambda: nc.scalar.activation(out=sqb, in_=tmp[1], func=AF.Square), [Btmp[1]], [Bsqb])
            pb = bank()
            S.op("pe", lambda: nc.tensor.matmul(self.PS[pb][:, 0:BLK], lhsT=blk64b, rhs=sqb, start=True, stop=True), [Bsqb, Bb64], [self.BPS[pb]])
            S.op("act", lambda: nc.scalar.activation(out=tmp[2], in_=self.PS[pb][:, 0:BLK], func=AF.Sqrt), [self.BPS[pb]], [Btmp[2]])
            S.op("dve", lambda: A_.tensor_scalar(out=tmp[2], in0=tmp[2], scalar1=1e-12, scalar2=None, op0=ALU.max), [Btmp[2]], [Btmp[2]])
            S.op("dve", lambda: A_.reciprocal(out=tmp[2], in_=tmp[2]), [Btmp[2]], [Btmp[2]])
            S.op("dve", lambda: A_.tensor_tensor(out=sg_[:, 5, :], in0=tmp[1], in1=tmp[2], op=ALU.mult), [Btmp[1], Btmp[2]], [Bs])
            pb = bank()
            S.op("pe", lambda: nc.tensor.matmul(self.PS[pb][:, 0:BLK], lhsT=G2a[:, cs], rhs=gh, start=True, stop=False), [Bsw, Blt], [self.BPS[pb]])
            S.op("pe", lambda: nc.tensor.matmul(self.PS[pb][:, 0:BLK], lhsT=G2b[:, cs], rhs=lt[0:32, 4, :], start=False, stop=True), [Bsw, Blt], [self.BPS[pb]])
            S.op("act", lambda: nc.scalar.copy(out=sg_[:, 9, :], in_=self.PS[pb][:, 0:BLK]), [self.BPS[pb]], [Bs])
            for d in range(2):
                pb = bank()
                S.op("pe", lambda: nc.tensor.matmul(self.PS[pb][:, 0:BLK], lhsT=W2[:, d, cs], rhs=lt[:, d, :], start=True, stop=True), [Bsw, Blt], [self.BPS[pb]])
                S.op("act", lambda: nc.scalar.activation(out=tmp[3], in_=self.PS[pb][:, 0:BLK], func=AF.Sigmoid, bias=self.pv(f"rw_w0_{d}", c)), [self.BPS[pb]], [Btmp[3]])
                S.op("dve", lambda: A_.tensor_scalar(out=sg_[:, 6 + d, :], in0=tmp[3], scalar1=-float(np.exp(-0.5)), scalar2=None, op0=ALU.mult), [Btmp[3]], [Bs])
                pb = bank()
                S.op("pe", lambda: nc.tensor.matmul(self.PS[pb][:, 0:BLK], lhsT=A2[:, d, cs], rhs=lt[:, 2 + d, :], start=True, stop=True), [Bsw, Blt], [self.BPS[pb]])
                S.op("act", lambda: nc.scalar.activation(out=tmp[4], in_=self.PS[pb][:, 0:BLK], func=AF.Sigmoid, bias=self.pv(f"rw_a0_{d}", c)), [self.BPS[pb]], [Btmp[4]])
                S.op("dve", lambda: A_.tensor_tensor(out=sg_[:, 3 + d, :], in0=tmp[4], in1=sg_[:, 5, :], op=ALU.mult), [Btmp[4], Bs], [Bs])
                S.op("dve", lambda: A_.tensor_scalar(out=tmp[5], in0=tmp[4], scalar1=-1.0, scalar2=None, op0=ALU.add), [Btmp[4]], [Btmp[5]])
                S.op("dve", lambda: A_.tensor_scalar(out=tmp[5], in0=tmp[5], scalar1=self.pv("rw_k_a", c), scalar2=1.0, op0=ALU.mult, op1=ALU.add), [Btmp[5]], [Btmp[5]])
                S.op("dve", lambda: A_.tensor_tensor(out=sg_[:, 1 + d, :], in0=tmp[5], in1=kraw, op=ALU.mult), [Btmp[5], Btmp[0]], [Bs])
            S.dma("pool", RWP[:, c * 128:(c + 1) * 128, col:col + BLK].rearrange("a p t -> p a t"), sg_, reads=[Bs])
        for tt in range(BLK // 128):
            vi = vi_ % 2
            vi_ += 1
            for hfv in range(2):
                pb = 4 - hfv
                for kc in range(8):
                    S.op("pe", lambda: nc.tensor.matmul(self.PS[pb][:, 0:512], lhsT=xj[3][:, kc, tt * 128:(tt + 1) * 128], rhs=Wrkv[:, kc, 2 * D + hfv * 512:2 * D + (hfv + 1) * 512], start=(kc == 0), stop=(kc == 7)),
                         [BWrkv[4 + hfv], Bxj[3]], [self.BPS[pb]])
                S.op("act", lambda: nc.scalar.copy(out=vts[vi][:, hfv * 512:(hfv + 1) * 512], in_=self.PS[pb][:, 0:512]), [self.BPS[pb]], [Bvts[vi]])
            S.dma("pool", Vtm[col + tt * 128:col + (tt + 1) * 128, :], vts[vi], reads=[Bvts[vi]])
    st.close()


def _rwkv_mixer(self, l, xin, og):
    RWP = self.scr("rwP", [10, D, TT])
    Vtm = self.scr("rwV", [TT, D], BF16)
    self.rwkv_proj(l, xin, RWP, Vtm)
    if getattr(self, "rw_stop", 0) == 1:
        return
    self.rwkv_scan(RWP, Vtm, og)


Prog.rwkv_proj = _rwkv_proj
Prog.rwkv_mixer = _rwkv_mixer


def _rwkv_scan(self, RWP, Vtm, og):
    nc, S = self.nc, self.S
    A_ = nc.vector
    U32 = mybir.dt.uint32
    RWD = self.scr("rwD", [NB, 8, 2, 2, 128, NCH * 128], BF16)
    RWS = self.scr("rwS", [NB, 8, 2, 128, 3 * NCH])
    skipA = getattr(self, "rw_skipA", False)
    st = Stage(self, "r2a")
    smask = st.sb("smask", [128, T])
    Bsm = Buf()
    S.dma("sp", smask, self.cd["scanmask"], writes=[Bsm])
    lw = st.sb("lw", [128, T]); kap = st.sb("kap", [128, T]); rr = st.sb("r", [128, T]); kt = st.sb("kt", [128, T]); be = st.sb("be", [128, T])
    cw = st.sb("cw", [128, T]); cm = st.sb("cm", [128, T]); en = st.sb("en", [128, T]); ex = st.sb("ex", [128, T])
    ABt = [st.sb(f"AB{i}", [128, NCH, 2, CH], BF16) for i in range(2)]
    KBt_ = [st.sb(f"KB{i}", [128, NCH, 2, CH], BF16) for i in range(2)]
    SC = [st.sb(f"SC{i}", [128, 3, NCH]) for i in range(2)]
    Blw, Bkap, Br, Bkt, Bbe, Bcw, Bcm, Ben, Bex = [Buf() for _ in range(9)]
    BAB, BKB, BSC = [[Buf(), Buf()] for _ in range(3)]
    it = 0
    v3 = lambda t_: t_.rearrange("p (c s) -> p c s", s=CH)
    for b in range(0 if skipA else NB):
        cols = slice(b * T, (b + 1) * T)
        for p in range(8):
            rows = slice(p * 128, (p + 1) * 128)
            for d in range(2):
                i = it % 2
                it += 1
                S.dma("sp", lw, RWP[6 + d, rows, cols], writes=[Blw])
                S.dma("sp", kap, RWP[5, rows, cols], writes=[Bkap])
                S.dma("sp", rr, RWP[0, rows, cols], writes=[Br])
                S.dma("sp", kt, RWP[1 + d, rows, cols], writes=[Bkt])
                S.dma("sp", be, RWP[3 + d, rows, cols], writes=[Bbe])
                S.op("dve", lambda: A_.tensor_tensor_scan(out=cw, data0=smask, data1=lw, initial=0.0, op0=ALU.mult, op1=ALU.add), [Bsm, Blw], [Bcw])
                if d == 1:
                    S.op("dve", lambda: A_.tensor_tensor(out=cm, in0=lw, in1=cw, op=ALU.subtract), [Blw, Bcw], [Bcm])
                    S.op("dve", lambda: A_.tensor_tensor(out=v3(en), in0=v3(cm), in1=v3(cw)[:, :, CH - 1:CH].to_broadcast([128, NCH, CH]), op=ALU.add), [Bcm, Bcw], [Ben])
                    S.op("dve", lambda: A_.tensor_copy(out=cw, in_=en), [Ben], [Bcw])
                m_idx = 32 if d == 0 else 31
                e_idx = CH - 1 if d == 0 else 0
                c3 = v3(cw)
                S.op("act", lambda: nc.scalar.activation(out=SC[i][:, 0, :], in_=c3[:, :, m_idx], func=AF.Exp), [Bcw], [BSC[i]])
                S.op("act", lambda: nc.scalar.activation(out=SC[i][:, 1, :], in_=c3[:, :, e_idx], func=AF.Exp), [Bcw], [BSC[i]])
                S.op("dve", lambda: A_.tensor_tensor(out=SC[i][:, 2, :], in0=c3[:, :, e_idx], in1=c3[:, :, m_idx], op=ALU.subtract), [Bcw], [BSC[i]])
                S.op("act", lambda: nc.scalar.activation(out=SC[i][:, 2, :], in_=SC[i][:, 2, :], func=AF.Exp), [BSC[i]], [BSC[i]])
                S.dma("pool", RWS[b, p, d], SC[i].rearrange("p a c -> p (a c)"), reads=[BSC[i]])
                S.op("dve", lambda: A_.tensor_tensor(out=v3(cm), in0=c3, in1=c3[:, :, m_idx:m_idx + 1].to_broadcast([128, NCH, CH]), op=ALU.subtract), [Bcw], [Bcm])
                S.op("act", lambda: nc.scalar.activation(out=en, in_=cm, func=AF.Exp, scale=-1.0), [Bcm], [Ben])
                S.op("dve", lambda: A_.tensor_tensor(out=ex, in0=cm, in1=lw, op=ALU.subtract), [Bcm, Blw], [Bex])
                S.op("act", lambda: nc.scalar.activation(out=ex, in_=ex, func=AF.Exp), [Bex], [Bex])
                S.op("act", lambda: nc.scalar.activation(out=cm, in_=cm, func=AF.Exp), [Bcm], [Bcm])
                S.op("dve", lambda: A_.tensor_tensor(out=ABt[i][:, :, 0, :], in0=v3(kap), in1=v3(ex), op=ALU.mult), [Bkap, Bex], [BAB[i]])
                S.op("dve", lambda: A_.tensor_tensor(out=ABt[i][:, :, 1, :], in0=v3(rr), in1=v3(cm), op=ALU.mult), [Br, Bcm], [BAB[i]])
                S.op("dve", lambda: A_.tensor_tensor(out=KBt_[i][:, :, 0, :], in0=v3(kt), in1=v3(en), op=ALU.mult), [Bkt, Ben], [BKB[i]])
                S.op("dve", lambda: A_.tensor_tensor(out=KBt_[i][:, :, 1, :], in0=v3(be), in1=v3(en), op=ALU.mult), [Bbe, Ben], [BKB[i]])
                S.dma("pool", RWD[b, p, d, 0], ABt[i].rearrange("p c a s -> p (c a s)"), reads=[BAB[i]])
                S.dma("pool", RWD[b, p, d, 1], KBt_[i].rearrange("p c a s -> p (c a s)"), reads=[BKB[i]])
    st.close()
    if getattr(self, "rw_stop", 0) == 2:
        return
    st = Stage(self, "r2b")
    S.pe_selfwait = getattr(self, "rw_selfwait", False)
    S.pe_drain = getattr(self, "rw_drain", 2)
    epsLN = st.sb("epsLN", [128, 1])
    Bgl = Buf()
    S.op("dve", lambda: A_.memset(epsLN, RW_LN_EPS), [], [Bgl])
    AB = [st.sb(f"AB{d}", [128, NCH, 128], BF16) for d in range(2)]
    KB = [st.sb(f"KB{d}", [128, NCH, 128], BF16) for d in range(2)]
    SCs = [st.sb(f"SC{d}", [128, 3, NCH]) for d in range(2)]
    Vst = st.sb("Vst", [64, NCH, 128], BF16)
    BABl, BKBl, BSCl = [[Buf(), Buf()] for _ in range(3)]
    BVst = Buf()
    chains = [(hd, d) for hd in range(2) for d in range(2)]
    VU, GGb, AN0, ANp, Xp, Wf, KBtr = {}, {}, {}, {}, {}, {}, {}
    BVU, BGG, BAN0, BANp, BXp, BWf, BKBtr, BST, BS0, BtS, By = [dict() for _ in range(11)]
    for ch in chains:
        nm = f"{ch[0]}{ch[1]}"
        VU[ch] = st.sb("VU" + nm, [128, NCH, CH], BF16)
        GGb[ch] = st.sb("GG" + nm, [128, 128], BF16)
        AN0[ch] = st.sb("AN0" + nm, [128, 128])
        ANp[ch] = [st.sb(f"ANp{q}" + nm, [128, 128]) for q in range(2)]
        Xp[ch] = [st.sb(f"X{q}" + nm, [128, CH]) for q in range(2)]
        Wf[ch] = st.sb("Wf" + nm, [128, CH])
        KBtr[ch] = st.sb("KBt" + nm, [128, CH], BF16)
        BVU[ch], BGG[ch], BAN0[ch], BWf[ch], BKBtr[ch], BST[ch], BS0[ch], BtS[ch], By[ch] = [Buf() for _ in range(9)]
        BANp[ch] = [Buf(), Buf()]
        BXp[ch] = [Buf(), Buf()]
        S.op("dve", lambda: A_.memset(GGb[ch], 0.0), [], [BGG[ch]])
        S.op("dve", lambda: A_.memset(AN0[ch], 0.0), [], [BAN0[ch]])
    ST = [st.sb(f"ST{d}", [128, CH]) for d in range(2)]
    S0m = [st.sb(f"S0m{d}", [128, CH], BF16) for d in range(2)]
    tS = [st.sb(f"tS{d}", [128, CH]) for d in range(2)]
    yacc = [st.sb(f"yacc{d}", [128, T]) for d in range(2)]
    rl = st.sb("rl", [128, T]); k0 = st.sb("k0", [128, T]); k1 = st.sb("k1", [128, T]); vf = st.sb("vf", [128, T]); gg = st.sb("gg", [128, T])
    t0_ = st.sb("t0", [128, T]); t1_ = st.sb("t1", [128, T])
    ogb = st.sb("ogb", [128, T], BF16)
    Brl, Bk0, Bk1, Bvf, Bgg, Bt0, Bt1, Bogb = [Buf() for _ in range(8)]
    R = {}
    BR = {}
    for ci, ch in enumerate(chains):
        b0, b1 = self.PS[2 * ci], self.PS[2 * ci + 1]
        R[ch] = dict(GA=b0[:, 0:128], LV=b0[:, 192:320], Wp=b0[:, 384:448],
                     XL=b1[:, 320:384], Up=b1[:, 448:512], Nn=b1[:, 128:192],
                     Yp=b1[:, 0:64], Sd=b1[:, 64:128], TR=b1.bitcast(BF16)[:, 512:576])
        u0, u1, u2, u3 = Buf(), Buf(), Buf(), Buf()
        ykp = [u2] if ch[0] == 0 else [u3]
        BR[ch] = dict(GAlo=[u0], GAup=[u1], GA=[u0, u1], LV=[u1], Wp=[u1], XL=[u3], Up=[u3], Nn=[u3], Yp=ykp, Sd=ykp, TR=[u2, u3], ALL=[u0, u1, u2, u3])
    up, lo = slice(64, 128), slice(0, 64)
    mU = lambda ap: ap.bitcast(U32)
    cf = list(range(NCH))
    cbk = list(range(TC // CH - 1, -1, -1)) + list(range(NCH - 1, TC // CH - 1, -1))
    order = [cf, cbk]
    dbgn = getattr(self, "rw_dbg", None)
    for b in range(NB):
        cols = slice(b * T, (b + 1) * T)
        for p in range(8):
            if dbgn is not None and (b * 8 + p) >= dbgn[0]:
                continue
            rows = slice(p * 128, (p + 1) * 128)
            for d in range(2):
                S.dma("sp", AB[d], RWD[b, p, d, 0].rearrange("k (c x) -> k c x", x=128), writes=[BABl[d]])
                S.dma("sp", KB[d], RWD[b, p, d, 1].rearrange("k (c x) -> k c x", x=128), writes=[BKBl[d]])
                S.dma("sp", SCs[d], RWS[b, p, d].rearrange("k (a c) -> k a c", a=3), writes=[BSCl[d]])
            S.dma("sp", Vst, Vtm[cols, rows].rearrange("(c s) v -> s c v", s=CH), writes=[BVst])
            S.dma("sp", rl, RWP[0, rows, cols], writes=[Brl])
            S.dma("sp", k0, RWP[1, rows, cols], writes=[Bk0])
            S.dma("sp", k1, RWP[2, rows, cols], writes=[Bk1])
            S.dma("sp", vf, RWP[8, rows, cols], writes=[Bvf])
            S.dma("sp", gg, RWP[9, rows, cols], writes=[Bgg])
            for ch in chains:
                hd, d = ch
                kp = slice(hd * 64, hd * 64 + 64)
                S.op("pool", lambda: nc.gpsimd.tensor_copy(out=VU[ch][lo, :, :], in_=Vst[:, :, hd * 64:(hd + 1) * 64]), [BVst], [BVU[ch]])
                S.op("dve", lambda: A_.memset(ST[d][kp, :], 0.0), [], [BST[ch]])
                S.op("dve", lambda: A_.memset(S0m[d][kp, :], 0.0), [], [BS0[ch]])
            def chain_step(ch, step):
                hd, d = ch
                kp = slice(hd * 64, hd * 64 + 64)
                c = order[d][step]
                cs = slice(c * CH, (c + 1) * CH)
                r_, br_ = R[ch], BR[ch]
                M4 = self.masks[:, 0:128] if d == 0 else self.masks[:, 128:256]
                mA = self.masks[up, 0:64] if d == 0 else self.masks[up, 128:192]
                mN = self.masks[up, 128:192] if d == 0 else self.masks[up, 0:64]
                S.op("pe", lambda: nc.tensor.transpose(out=r_["TR"], in_=KB[d][kp, c, :], identity=self.identb[kp, kp]), [BKBl[d]], br_["TR"], pemode=("T", hd))
                S.op("act", lambda: nc.scalar.copy(out=KBtr[ch], in_=r_["TR"]), br_["TR"], [BKBtr[ch]])
                S.op("pe", lambda: nc.tensor.matmul(r_["GA"][lo, :], lhsT=KB[d][kp, c, 0:64], rhs=AB[d][kp, c, :], start=True, stop=True), [BKBl[d], BABl[d]], br_["GAlo"], pemode=("g", hd))
                S.op("pe", lambda: nc.tensor.matmul(r_["GA"][up, :], lhsT=KB[d][kp, c, 64:128], rhs=AB[d][kp, c, :], start=True, stop=True), [BKBl[d], BABl[d]], br_["GAup"], pemode=("g", hd))
                S.op("pe", lambda: nc.tensor.matmul(r_["Nn"][up, :], lhsT=AB[d][kp, c, 0:64], rhs=KB[d][kp, c, 64:128], start=True, stop=True), [BKBl[d], BABl[d]], br_["Nn"], pemode=("g", hd))
                yield
                S.op("dve", lambda: A_.copy_predicated(out=GGb[ch], mask=mU(M4), data=r_["GA"]), br_["GA"], [BGG[ch]])
                S.op("dve", lambda: A_.copy_predicated(out=AN0[ch][up, 0:64], mask=mU(mA), data=r_["GA"][up, 0:64]), br_["GAup"], [BAN0[ch]])
                S.op("dve", lambda: A_.copy_predicated(out=AN0[ch][up, 64:128], mask=mU(mN), data=r_["Nn"][up, :]), br_["Nn"], [BAN0[ch]])
                S.op("dve", lambda: A_.tensor_tensor(out=Xp[ch][0][up, :], in0=self.ident[up, up], in1=AN0[ch][up, 0:64], op=ALU.subtract), [BAN0[ch]], [BXp[ch][0]])
                yield
                cur, Bcur = AN0[ch], BAN0[ch]
                xq = 0
                for lv in range(1, 6):
                    nx, Bnx = ANp[ch][lv % 2], BANp[ch][lv % 2]
                    if lv < 5:
                        S.op("pe", lambda: nc.tensor.matmul(r_["LV"][up, 0:64], lhsT=cur[up, 64:128], rhs=cur[up, 0:64], start=True, stop=True), [Bcur], br_["LV"], pemode=("f",))
                    S.op("pe", lambda: nc.tensor.matmul(r_["LV"][up, 64:128], lhsT=cur[up, 0:64], rhs=cur[up, 64:128], start=True, stop=True), [Bcur], br_["LV"], pemode=("f",))
                    yield
                    if lv < 5:
                        S.op("act", lambda: nc.scalar.copy(out=nx[up, :], in_=r_["LV"][up, :]), br_["LV"], [Bnx])
                    else:
                        S.op("act", lambda: nc.scalar.copy(out=nx[up, 64:128], in_=r_["LV"][up, 64:128]), br_["LV"], [Bnx])
                    yield
                    S.op("pe", lambda: nc.tensor.matmul(r_["XL"][up, :], lhsT=nx[up, 64:128], rhs=Xp[ch][xq][up, :], start=True, stop=True), [Bnx, BXp[ch][xq]], br_["XL"], pemode=("f",))
                    yield
                    S.op("dve", lambda: A_.tensor_tensor(out=Xp[ch][1 - xq][up, :], in0=r_["XL"][up, :], in1=Xp[ch][xq][up, :], op=ALU.add), br_["XL"] + [BXp[ch][xq]], [BXp[ch][1 - xq]])
                    xq = 1 - xq
                    cur, Bcur = nx, Bnx
                yield
                S.op("pe", lambda: nc.tensor.matmul(r_["Wp"][up, :], lhsT=AB[d][kp, c, 0:64], rhs=S0m[d][kp, :], start=True, stop=False), [BABl[d], BS0[ch]], br_["Wp"], pemode=("g", hd))
                S.op("pe", lambda: nc.tensor.matmul(r_["Wp"][up, :], lhsT=GGb[ch][lo, 0:64], rhs=VU[ch][lo, c, :], start=False, stop=True), [BGG[ch], BVU[ch]], br_["Wp"], pemode=("w2",))
                yield
                S.op("act", lambda: nc.scalar.copy(out=Wf[ch][up, :], in_=r_["Wp"][up, :]), br_["Wp"], [BWf[ch]])
                yield
                S.op("pe", lambda: nc.tensor.matmul(r_["Up"][up, :], lhsT=Xp[ch][xq][up, :], rhs=Wf[ch][up, :], start=True, stop=True), [BXp[ch][xq], BWf[ch]], br_["Up"], pemode=("f",))
                yield
                S.op("act", lambda: nc.scalar.activation(out=VU[ch][up, c, :], in_=r_["Up"][up, :], func=AF.Copy, scale=-1.0), br_["Up"], [BVU[ch]])
                yield
                S.op("pe", lambda: nc.tensor.matmul(r_["Yp"][kp, :], lhsT=S0m[d][kp, :], rhs=AB[d][kp, c, 64:128], start=True, stop=False), [BS0[ch], BABl[d]], br_["Yp"], pemode=("g", hd))
                S.op("pe", lambda: nc.tensor.matmul(r_["Yp"][kp, :], lhsT=VU[ch][:, c, :], rhs=GGb[ch][:, 64:128], start=False, stop=True), [BVU[ch], BGG[ch]], br_["Yp"], pemode=("full",))
                yield
                S.op("act", lambda: nc.scalar.copy(out=yacc[d][kp, cs], in_=r_["Yp"][kp, :]), br_["Yp"], [By[ch]])
                S.op("pe", lambda: nc.tensor.matmul(r_["Sd"][kp, :], lhsT=KBtr[ch], rhs=VU[ch][:, c, :], start=True, stop=True), [BKBtr[ch], BVU[ch]], br_["Sd"], pemode=("full",))
                yield
                S.op("act", lambda: nc.scalar.activation(out=tS[d][kp, :], in_=r_["Sd"][kp, :], func=AF.Identity, scale=SCs[d][kp, 2, c:c + 1]), br_["Sd"] + [BSCl[d]], [BtS[ch]])
                S.op("dve", lambda: A_.scalar_tensor_tensor(out=ST[d][kp, :], in0=ST[d][kp, :], scalar=SCs[d][kp, 1, c:c + 1], in1=tS[d][kp, :], op0=ALU.mult, op1=ALU.add), [BST[ch], BtS[ch], BSCl[d]], [BST[ch]])
                if step + 1 < NCH:
                    cn = order[d][step + 1]
                    S.op("dve", lambda: A_.tensor_scalar(out=S0m[d][kp, :], in0=ST[d][kp, :], scalar1=SCs[d][kp, 0, cn:cn + 1], scalar2=None, op0=ALU.mult), [BST[ch], BSCl[d]], [BS0[ch]])

            for step in range(NCH if dbgn is None else dbgn[1]):
                gens = [chain_step(ch, step) for ch in chains]
                if getattr(self, "rw_order", "phase") == "chain":
                    for g_ in gens:
                        for _ in g_:
                            pass
                    gens = []
                while gens:
                    for g_ in list(gens):
                        try:
                            next(g_)
                        except StopIteration:
                            gens.remove(g_)
            RB = {0: [BR[chains[0]]["ALL"][0], BR[chains[0]]["ALL"][1]], 1: [BR[chains[0]]["ALL"][2], BR[chains[0]]["ALL"][3]]}
            By_all = [By[ch] for ch in chains]
            Byy = Buf()
            S.op("dve", lambda: A_.tensor_tensor(out=yacc[0], in0=yacc[0], in1=yacc[1], op=ALU.add), By_all, [Byy])
            NP_ = 6
            W_ = T // NP_
            for pc in range(NP_):
                sl_ = slice(pc * W_, (pc + 1) * W_)
                pb = pc % 2
                S.op("pe", lambda: nc.tensor.matmul(self.PS[pb][:, 0:W_], lhsT=self.blk64, rhs=yacc[0][:, sl_], start=True, stop=True), [Byy], RB[pb])
                S.op("dve", lambda: A_.scalar_tensor_tensor(out=t0_[:, sl_], in0=self.PS[pb][:, 0:W_], scalar=-1.0 / 64, in1=yacc[0][:, sl_], op0=ALU.mult, op1=ALU.add), RB[pb] + [Byy], [Bt0])
            S.op("act", lambda: nc.scalar.activation(out=t1_, in_=t0_, func=AF.Square), [Bt0], [Bt1])
            for pc in range(NP_):
                sl_ = slice(pc * W_, (pc + 1) * W_)
                pb = pc % 2
                S.op("pe", lambda: nc.tensor.matmul(self.PS[pb][:, 0:W_], lhsT=self.blk64, rhs=t1_[:, sl_], start=True, stop=True), [Bt1], RB[pb])
                S.op("act", lambda: nc.scalar.activation(out=yacc[1][:, sl_], in_=self.PS[pb][:, 0:W_], func=AF.Sqrt, scale=1.0 / 64, bias=epsLN), RB[pb] + [Bgl], [Byy])
            S.op("dve", lambda: A_.reciprocal(out=yacc[1], in_=yacc[1]), [Byy], [Byy])
            S.op("dve", lambda: A_.tensor_tensor(out=t0_, in0=t0_, in1=yacc[1], op=ALU.mult), [Bt0, Byy], [Bt0])
            S.op("act", lambda: nc.scalar.activation(out=t0_, in_=t0_, func=AF.Identity, scale=self.pv("rw_ln_w", p), bias=self.pv("rw_ln_b", p)), [Bt0], [Bt0])
            S.op("dve", lambda: A_.tensor_tensor(out=k0, in0=k0, in1=k1, op=ALU.add), [Bk0, Bk1], [Bk0])
            S.op("dve", lambda: A_.scalar_tensor_tensor(out=t1_, in0=rl, scalar=self.pv("rw_r_k", p), in1=k0, op0=ALU.mult, op1=ALU.mult), [Brl, Bk0, Bt1], [Bt1])
            for pc in range(NP_):
                sl_ = slice(pc * W_, (pc + 1) * W_)
                pb = pc % 2
                S.op("pe", lambda: nc.tensor.matmul(self.PS[pb][:, 0:W_], lhsT=self.blk64, rhs=t1_[:, sl_], start=True, stop=True), [Bt1], RB[pb])
                S.op("dve", lambda: A_.tensor_tensor(out=yacc[1][:, sl_], in0=self.PS[pb][:, 0:W_], in1=vf[:, sl_], op=ALU.mult), RB[pb] + [Bvf, Byy], [Byy])
            S.op("dve", lambda: A_.tensor_tensor(out=t0_, in0=t0_, in1=yacc[1], op=ALU.add), [Bt0, Byy], [Bt0])
            S.op("dve", lambda: A_.tensor_tensor(out=ogb, in0=t0_, in1=gg, op=ALU.mult), [Bt0, Bgg], [Bogb])
            S.dma("pool", og[rows, cols], ogb, reads=[Bogb])
            for ch in chains:
                By[ch].r.append(Byy.w)
    st.close()
    S.pe_selfwait = False
    S.pe_drain = 0


Prog.rwkv_scan = _rwkv_scan
```

```python
from contextlib import ExitStack
import numpy as np
import concourse.bass as bass
import concourse.mybir as mybir
from concourse.bass_utils import run_bass_kernel_spmd

F32 = mybir.dt.float32
BF16 = mybir.dt.bfloat16
AF = mybir.ActivationFunctionType
ALU = mybir.AluOpType

NCORES = 8
NB = 2
TC = 256
TL = 2048
T = TC + TL
TT = NB * T
D = 1024
DEPTH = 4
DFF = 2816
NFC = DFF // 128
BLK = 256
NBLK = T // BLK
EPS = 1e-6
CH = 64
NCH = T // CH
RW_LN_EPS = 64e-5
MLA_SCALE = 96 ** -0.5


class Buf:
    __slots__ = ("name", "w", "r")

    def __init__(self, name=""):
        self.name = name
        self.w = None
        self.r = []


class _Eng:
    def __init__(self, S, name, eng):
        self.S = S
        self.name = name
        self.eng = eng
        self.sem = None
        self.count = 0
        self.seen = {}
        self.nsem = 0
        self.ninst = 0
        self.own = set()

    def new_sem(self):
        self.sem = self.S.nc.alloc_semaphore(f"e_{self.name}_{self.nsem}")
        self.own.add(id(self.sem))
        self.nsem += 1
        self.count = 0

    def wait(self, ev):
        sem, val = ev
        k = id(sem)
        if self.name == "pe" and k in self.own and not self.S.pe_selfwait:
            return
        if self.seen.get(k, 0) >= val:
            return
        self.eng.wait_ge(sem, val)
        self.seen[k] = val


class Sched:
    EPOCH = 30000

    def __init__(self, nc, ndma_sems=48):
        self.nc = nc
        self.E = {}
        for name, eng in (("pe", nc.tensor), ("dve", nc.vector), ("act", nc.scalar),
                          ("pool", nc.gpsimd), ("sp", nc.sync)):
            e = _Eng(self, name, eng)
            e.new_sem()
            self.E[name] = e
        self.dsems = [[nc.alloc_semaphore(f"d{i}"), 0] for i in range(ndma_sems)]
        self.dnext = 0
        self._keep = []
        self.pe_selfwait = False
        self.pe_drain = 0
        self.last_pemode = None

    @staticmethod
    def _deps(reads, writes):
        deps = []
        for b in reads:
            if b.w is not None:
                deps.append(b.w)
        for b in writes:
            if b.w is not None:
                deps.append(b.w)
            deps.extend(b.r)
        return deps

    @staticmethod
    def _mark(ev, reads, writes):
        for b in writes:
            b.w = ev
            b.r = []
        for b in reads:
            if b not in writes:
                b.r.append(ev)
                if len(b.r) > 32:
                    b.r = b.r[-32:]

    def op(self, ename, fn, reads=(), writes=(), pemode=None):
        e = self.E[ename]
        for ev in self._deps(reads, writes):
            e.wait(ev)
        drain = False
        if ename == "pe":
            drain = self.pe_drain == 1 or (self.pe_drain == 2 and pemode != self.last_pemode)
            self.last_pemode = pemode
        if drain and e.count > 0:
            k = id(e.sem)
            if e.seen.get(k, 0) < e.count:
                e.eng.wait_ge(e.sem, e.count)
                e.seen[k] = e.count
        if e.count >= self.EPOCH:
            self._keep.append(e.sem)
            e.new_sem()
        inst = fn()
        e.count += 1
        e.ninst += 1
        inst.then_inc(e.sem, 1)
        ev = (e.sem, e.count)
        self._mark(ev, reads, writes)
        return ev

    def dma(self, qname, out, in_, reads=(), writes=(), **kw):
        q = self.E[qname]
        for ev in self._deps(reads, writes):
            q.wait(ev)
        slot = self.dsems[self.dnext % len(self.dsems)]
        self.dnext += 1
        if slot[1] >= self.EPOCH:
            self._keep.append(slot[0])
            slot[0] = self.nc.alloc_semaphore(f"dx{self.dnext}")
            slot[1] = 0
        if slot[1] > 0:
            q.wait((slot[0], slot[1]))
        q.eng.dma_start(out=out, in_=in_, **kw).then_inc(slot[0], 16)
        q.ninst += 1
        slot[1] += 16
        ev = (slot[0], slot[1])
        self._mark(ev, reads, writes)
        return ev

    def barrier(self):
        evs = [(e.sem, e.count) for e in self.E.values() if e.count > 0]
        evs += [(s[0], s[1]) for s in self.dsems if s[1] > 0]
        for e in self.E.values():
            for ev in evs:
                if ev[0] is e.sem:
                    continue
                e.wait(ev)


class PVec:
    def __init__(self):
        self.cols = []
        self.off = {}
        self.n = 0

    def add(self, name, vec):
        vec = np.asarray(vec, dtype=np.float32).reshape(-1)
        assert vec.size % 128 == 0
        nch = vec.size // 128
        self.off[name] = (self.n, nch)
        self.cols.append(np.ascontiguousarray(vec.reshape(nch, 128).T))
        self.n += nch

    def array(self):
        return np.ascontiguousarray(np.concatenate(self.cols, axis=1))


def pvec_layout(inputs):
    pv = PVec()
    for l in range(DEPTH):
        pv.add(f"b_mod{l}", inputs["b_mod"][l])
        pv.add(f"norm1_{l}", inputs["norm1"][l])
        pv.add(f"norm2_{l}", inputs["norm2"][l])
        for k in range(3):
            pv.add(f"conv{l}_{k}", inputs["ffn_conv"][l, k])
        pv.add(f"convb{l}", inputs["ffn_conv_b"][l])
    pv.add("norm_f", inputs["norm_f"])
    for d in range(2):
        for j in range(2):
            pv.add(f"hg_lb{d}_{j}", inputs["hg_lb"][d, j])
    for j in range(2):
        pv.add(f"hg_norm{j}", inputs["hg_norm"][j])
    for k in range(6):
        pv.add(f"rw_mu{k}", inputs["rw_mu"][0, k])
    for d in range(2):
        pv.add(f"rw_w0_{d}", inputs["rw_w0"][0, d])
        pv.add(f"rw_a0_{d}", inputs["rw_a0"][0, d])
    for nm in ("rw_k_k", "rw_k_a", "rw_r_k", "rw_ln_w", "rw_ln_b"):
        pv.add(nm, inputs[nm][0])
    pv.add("mla_q_norm", inputs["mla_q_norm"][0])
    pv.add("mla_kv_norm", inputs["mla_kv_norm"][0])
    return pv


def make_consts():
    c = {}
    c["ident"] = np.eye(128, dtype=np.float32)
    c["ones"] = np.ones((128, 128), dtype=np.float32)
    bo = np.zeros((128, 128), dtype=np.float32)
    bo[:64, :64] = 1.0
    bo[64:, 64:] = 1.0
    c["blk64"] = bo
    i = np.arange(64)[:, None]
    t = np.arange(64)[None, :]
    su = (i < t).astype(np.float32)
    iu = (i <= t).astype(np.float32)
    sl = (i > t).astype(np.float32)
    il = (i >= t).astype(np.float32)
    c["masks"] = np.concatenate([np.concatenate([su, iu, sl, il], axis=1)] * 2, axis=0)
    m = np.ones((128, T), dtype=np.float32)
    m[:, ::CH] = 0.0
    c["scanmask"] = m
    nq = 8
    inv_freq = (10000.0 ** (-np.arange(nq, dtype=np.float32) / nq)).astype(np.float32)
    pos = np.arange(TL)
    row = (pos // 64).astype(np.float32)
    col = (pos % 64).astype(np.float32)
    ang_r = row[:, None] * inv_freq
    ang_c = col[:, None] * inv_freq
    ang = np.concatenate([ang_r, ang_r, ang_c, ang_c], axis=-1).astype(np.float32)
    cos = np.ones((32, T), dtype=np.float32)
    sin = np.zeros((32, T), dtype=np.float32)
    cos[:, TC:] = np.cos(ang).T
    sin[:, TC:] = np.sin(ang).T
    c["rope_cos"] = cos
    c["rope_sin"] = sin
    return c


WEIGHT_NAMES = ["w_mod", "ffn_w_in", "ffn_w_out", "hg_w_in", "hg_w_o", "rw_w_rkv", "rw_w1", "rw_w2",
                "rw_a1", "rw_a2", "rw_g1", "rw_g2", "rw_w_o", "mla_w_dqkv", "mla_w_uq", "mla_w_ukv", "mla_w_o"]


class Stage:
    def __init__(self, P, name):
        self.P = P
        self.name = name
        self.es = ExitStack()
        P.nstage += 1
        self.k = 0

    def sb(self, name, shape, dt=F32):
        self.k += 1
        h = self.es.enter_context(self.P.nc.sbuf_tensor(f"{self.name}{self.P.nstage}_{name}_{self.k}", list(shape), dt))
        return h.ap()

    def close(self):
        self.P.S.barrier()
        self.es.close()


class Prog:
    def __init__(self, wshapes, pv_off, npv, dbg=(), xin_name=None):
        nc = bass.Bass("TRN2", target_bir_lowering=False)
        self.nc = nc
        self.dbg = set(dbg)
        self.pv_off = pv_off
        self.nstage = 0
        di = lambda n, s: nc.dram_tensor(n, list(s), F32, kind="ExternalInput").ap()
        self.x = di("x", [NB, TL, D])
        self.ctx = di("ctx", [NB, TC, D])
        self.cvec = di("cvec", [3, D])
        self.pvec_d = di("pvec", [128, npv])
        self.cd = {n: di("c_" + n, s) for n, s in (("ident", [128, 128]), ("ones", [128, 128]), ("blk64", [128, 128]),
                                                    ("masks", [128, 256]), ("scanmask", [128, T]),
                                                    ("rope_cos", [32, T]), ("rope_sin", [32, T]))}
        self.W = {n: di(n, wshapes[n]) for n in WEIGHT_NAMES}
        self.out = nc.dram_tensor("out", [NB, TL, D], F32, kind="ExternalOutput").ap()
        self.scratch = {}
        self.S = Sched(nc)
        S = self.S
        self.PS = [nc.alloc_psum_tensor(f"psb{i}", [128, 512], F32).ap() for i in range(8)]
        self.BPS = [Buf(f"ps{i}") for i in range(8)]
        g = lambda n, s, dt=F32: nc.alloc_sbuf_tensor("g_" + n, list(s), dt).ap()
        self.ident = g("ident", [128, 128])
        self.identb = g("identb", [128, 128], BF16)
        self.onesf = g("onesf", [128, 128])
        self.onesb = g("onesb", [128, 128], BF16)
        self.blk64 = g("blk64", [128, 128])
        self.masks = g("masks", [128, 256])
        self.pvec = g("pvec", [128, npv])
        self.MOD = g("MOD", [128, DEPTH, 48, 3])
        self.MA = g("MA", [128, DEPTH, 2, 8, 3])
        self.epsD = g("epsD", [128, 1])
        self.BC = Buf("consts")
        self.BMOD = Buf("mod")
        S.op("dve", lambda: nc.vector.memset(self.epsD, EPS), [], [self.BC])
        S.dma("sp", self.ident, self.cd["ident"], writes=[self.BC])
        b1, b2, b3, b4, b5, b6 = [Buf() for _ in range(6)]
        S.dma("sp", self.onesf, self.cd["ones"], writes=[b1])
        S.dma("sp", self.blk64, self.cd["blk64"], writes=[b2])
        S.dma("sp", self.masks, self.cd["masks"], writes=[b3])
        S.dma("sp", self.pvec, self.pvec_d, writes=[b4])
        S.dma("pool", self.identb, self.cd["ident"], writes=[b5])
        S.dma("pool", self.onesb, self.cd["ones"], writes=[b6])
        S.barrier()

    def scr(self, name, shape, dt=F32):
        if name not in self.scratch:
            kind = "ExternalOutput" if name in self.dbg else "Internal"
            self.scratch[name] = self.nc.dram_tensor("s_" + name, list(shape), dt, kind=kind).ap()
        return self.scratch[name]

    def pv(self, name, c=None):
        off, nch = self.pv_off[name]
        if c is None:
            return self.pvec[:, off:off + nch]
        return self.pvec[:, off + c:off + c + 1]

    def load_w(self, dst, src, bufs_cols, q="pool"):
        S = self.S
        n = dst.shape[2]
        v = src.rearrange("(kc p) n -> p kc n", p=128)
        bufs = []
        for n0 in range(0, n, 512):
            n1 = min(n, n0 + 512)
            b = Buf()
            S.dma(q, dst[:, :, n0:n1], v[:, :, n0:n1], writes=[b])
            bufs.append(b)
        return bufs

    def prologue_transpose(self, xT):
        nc, S = self.nc, self.S
        st = Stage(self, "pt")
        tin = [st.sb(f"tin{i}", [128, D]) for i in range(2)]
        tout = [st.sb(f"tout{i}", [128, 8, 128]) for i in range(2)]
        Bin = [Buf(), Buf()]
        Bout = [Buf(), Buf()]
        xTv = xT.rearrange("(c p) t -> p c t", p=128)
        tiles = []
        for b in range(NB):
            for k in range(T // 128):
                tiles.append((b, k))

        def src(b, k):
            t0 = k * 128
            if t0 < TC:
                return self.ctx[b, t0:t0 + 128, :]
            return self.x[b, t0 - TC:t0 - TC + 128, :]

        S.dma("sp", tin[0], src(*tiles[0]), writes=[Bin[0]])
        for n, (b, k) in enumerate(tiles):
            i = n % 2
            if n + 1 < len(tiles):
                S.dma("sp", tin[1 - i], src(*tiles[n + 1]), writes=[Bin[1 - i]])
            for hf in range(2):
                pb = 2 * (n % 2) + hf
                for c4 in range(4):
                    c = hf * 4 + c4
                    S.op("pe", lambda: nc.tensor.transpose(out=self.PS[pb][:, c4 * 128:(c4 + 1) * 128], in_=tin[i][:, c * 128:(c + 1) * 128], identity=self.ident),
                         [Bin[i]], [self.BPS[pb]])
                eng = "dve" if hf == 0 else "act"
                if hf == 0:
                    S.op("dve", lambda: nc.vector.tensor_copy(out=tout[i][:, 0:4, :], in_=self.PS[pb][:].rearrange("p (c t) -> p c t", c=4)), [self.BPS[pb]], [Bout[i]])
                else:
                    S.op("act", lambda: nc.scalar.copy(out=tout[i][:, 4:8, :], in_=self.PS[pb][:].rearrange("p (c t) -> p c t", c=4)), [self.BPS[pb]], [Bout[i]])
            col = b * T + k * 128
            S.dma("pool", xTv[:, :, col:col + 128], tout[i], reads=[Bout[i]])
        st.close()

    def prologue_mod(self):
        nc, S = self.nc, self.S
        st = Stage(self, "pm")
        cv = st.sb("cv", [3, D])
        sc = st.sb("sc", [3, D])
        scT = st.sb("scT", [128, 8, 3])
        Bcv, Bsc, BscT = Buf(), Buf(), Buf()
        S.dma("sp", cv, self.cvec, writes=[Bcv])
        S.op("act", lambda: nc.scalar.activation(out=sc, in_=cv, func=AF.Silu), [Bcv], [Bsc])
        for kc in range(8):
            S.op("pe", lambda: nc.tensor.transpose(out=self.PS[0][:, kc * 4:kc * 4 + 3], in_=sc[0:3, kc * 128:(kc + 1) * 128], identity=self.ident[0:3, 0:3]),
                 [Bsc], [self.BPS[0]])
        S.op("dve", lambda: nc.vector.tensor_copy(out=scT, in_=self.PS[0][:, 0:32].rearrange("p (k f) -> p k f", f=4)[:, :, 0:3]), [self.BPS[0]], [BscT])
        NWB = 4
        wt = [st.sb(f"wt{i}", [128, 8, 512]) for i in range(NWB)]
        Bwt = [Buf() for _ in range(NWB)]
        groups = [(l, g) for l in range(DEPTH) for g in range(12)]

        def wsrc(l, g):
            return self.W["w_mod"][l].rearrange("(kc p) n -> p kc n", p=128)[:, :, g * 512:(g + 1) * 512]

        def wload(n):
            S.dma("sp" if n % 2 == 0 else "act", wt[n % NWB], wsrc(*groups[n]), writes=[Bwt[n % NWB]])

        for n in range(NWB - 1):
            wload(n)
        for n, (l, g) in enumerate(groups):
            i = n % NWB
            if n + NWB - 1 < len(groups):
                wload(n + NWB - 1)
            pb = 1 + (n % 2)
            for oc in range(4):
                for kc in range(8):
                    S.op("pe", lambda: nc.tensor.matmul(self.PS[pb][:, oc * 4:oc * 4 + 3], lhsT=wt[i][:, kc, oc * 128:(oc + 1) * 128], rhs=scT[:, kc, :], start=(kc == 0), stop=(kc == 7)),
                         [Bwt[i], BscT], [self.BPS[pb]])
            boff, _ = self.pv_off[f"b_mod{l}"]
            bias = self.pvec[:, boff + g * 4:boff + g * 4 + 4].unsqueeze(2).to_broadcast([128, 4, 3])
            S.op("dve", lambda: nc.vector.tensor_tensor(out=self.MOD[:, l, g * 4:(g + 1) * 4, :], in0=self.PS[pb][:, 0:16].rearrange("p (o f) -> p o f", f=4)[:, :, 0:3], in1=bias, op=ALU.add),
                 [self.BPS[pb]], [self.BMOD])
        for l in range(DEPTH):
            for w in range(2):
                sc_idx = 8 if w == 0 else 32
                nrm = self.pv(f"norm{w + 1}_{l}").unsqueeze(2).to_broadcast([128, 8, 3])
                S.op("dve", lambda: nc.vector.scalar_tensor_tensor(out=self.MA[:, l, w, :, :], in0=self.MOD[:, l, sc_idx:sc_idx + 8, :], scalar=1.0, in1=nrm, op0=ALU.add, op1=ALU.mult),
                     [self.BMOD], [self.BMOD])
        st.close()

    def norm_tiles(self, st, n=BLK + 2):
        return dict(sq=st.sb("nsq", [128, 8, n], BF16), tmp=st.sb("ntmp", [128, 8, n]), r0=st.sb("nr0", [128, n]), r1=st.sb("nr1", [128, n]),
                    B=[Buf() for _ in range(4)])

    def norm_block(self, nt, xs, Bxs, n, A, Bsh, hb, Bhb, bank):
        nc, S = self.nc, self.S
        sq, tmp, r0, r1 = nt["sq"], nt["tmp"], nt["r0"], nt["r1"]
        Bsq, Btmp, Br0, Br1 = nt["B"]
        S.op("act", lambda: nc.scalar.activation(out=sq[:, :, :n], in_=xs, func=AF.Square), [Bxs], [Bsq])
        ps = self.PS[bank]
        for c in range(8):
            S.op("pe", lambda: nc.tensor.matmul(ps[:, :n], lhsT=self.onesb, rhs=sq[:, c, :n], start=(c == 0), stop=(c == 7)), [Bsq], [self.BPS[bank]])
        S.op("act", lambda: nc.scalar.activation(out=r0[:, :n], in_=ps[:, :n], func=AF.Sqrt, scale=1.0 / D, bias=self.epsD), [self.BPS[bank]], [Br0])
        S.op("dve", lambda: nc.vector.reciprocal(out=r1[:, :n], in_=r0[:, :n]), [Br0], [Br1])
        S.op("dve", lambda: nc.vector.tensor_tensor(out=tmp[:, :, :n], in0=xs, in1=r1[:, :n].unsqueeze(1).to_broadcast([128, 8, n]), op=ALU.mult), [Bxs, Br1], [Btmp])
        for c in range(8):
            S.op("act", lambda: nc.scalar.activation(out=hb[:, c, :n], in_=tmp[:, c, :n], func=AF.Identity, scale=A[:, c:c + 1], bias=(Bsh[:, c:c + 1] if Bsh is not None else 0.0)),
                 [Btmp, self.BMOD], [Bhb])

    def mod_ab(self, l, w, j):
        A = self.MA[:, l, w, :, j]
        sh = self.MOD[:, l, (0 if w == 0 else 24):(8 if w == 0 else 32), j]
        gt = self.MOD[:, l, (16 if w == 0 else 40):(24 if w == 0 else 48), j]
        return A, sh, gt

    @staticmethod
    def blocks(skip_ctx=False):
        out = []
        for b in range(NB):
            for k in range(NBLK):
                if skip_ctx and k == 0:
                    continue
                out.append((b, k))
        return out

    @staticmethod
    def blk_range(k):
        seq0, seq1 = (0, TC) if k == 0 else (TC, T)
        t0 = k * BLK
        lo = max(t0 - 1, seq0)
        hi = min(t0 + BLK + 1, seq1)
        return t0, lo, hi, (t0 == seq0), (t0 + BLK == seq1)

    def ffn_stage(self, l, xin, xout, skip_ctx):
        nc, S = self.nc, self.S
        st = Stage(self, "ffn")
        Win = st.sb("win", [128, 8, 2 * DFF], BF16)
        Wout = st.sb("wout", [128, NFC, D], BF16)
        BWin = self.load_w(Win, self.W["ffn_w_in"][l], None)
        BWout = []
        osrc = self.W["ffn_w_out"][l].rearrange("(fc p) n -> p fc n", p=128)
        for f0 in range(0, NFC, 2):
            b = Buf()
            S.dma("pool", Wout[:, f0:f0 + 2, :], osrc[:, f0:f0 + 2, :], writes=[b])
            BWout.append(b)
        NH = BLK + 2
        xs = [st.sb(f"xs{i}", [128, 8, NH]) for i in range(2)]
        hb = [st.sb(f"hb{i}", [128, 8, NH], BF16) for i in range(2)]
        gt_ = [st.sb(f"g{i}", [128, NFC, BLK], BF16) for i in range(2)]
        cv = [st.sb(f"cv{i}", [128, BLK]) for i in range(2)]
        sl = [st.sb(f"sl{i}", [128, BLK]) for i in range(2)]
        Bxs, Bhb, Bg, Bcv, Bsl = [[Buf(), Buf()] for _ in range(5)]
        nt = self.norm_tiles(st)
        for i in range(2):
            S.op("dve", lambda: nc.vector.memset(xs[i], 0.0), [], [Bxs[i]])
        xiv = xin.rearrange("(c p) t -> p c t", p=128)
        xov = xout.rearrange("(c p) t -> p c t", p=128)
        blocks = self.blocks(skip_ctx)

        def load(n):
            b, k = blocks[n]
            t0, lo, hi, _, _ = self.blk_range(k)
            S.dma("sp", xs[n % 2][:, :, lo - (t0 - 1):hi - (t0 - 1)], xiv[:, :, b * T + lo:b * T + hi], writes=[Bxs[n % 2]])

        load(0)
        for n, (b, k) in enumerate(blocks):
            i = n % 2
            if n + 1 < len(blocks):
                load(n + 1)
            t0, lo, hi, first, last = self.blk_range(k)
            j = 2 if k == 0 else b
            A, sh, gate = self.mod_ab(l, 1, j)
            self.norm_block(nt, xs[i], Bxs[i], NH, A, sh, hb[i], Bhb[i], 6)
            for fc in range(NFC):
                q = fc % 2
                pa, pvv = self.PS[q], self.PS[2 + q]
                ga = BWin[(fc * 128) // 512]
                gv = BWin[(DFF + fc * 128) // 512]
                for kc in range(8):
                    S.op("pe", lambda: nc.tensor.matmul(pa[:, :NH], lhsT=Win[:, kc, fc * 128:(fc + 1) * 128], rhs=hb[i][:, kc, :], start=(kc == 0), stop=(kc == 7)),
                         [ga, Bhb[i]], [self.BPS[q]])
                for kc in range(8):
                    S.op("pe", lambda: nc.tensor.matmul(pvv[:, :BLK], lhsT=Win[:, kc, DFF + fc * 128:DFF + (fc + 1) * 128], rhs=hb[i][:, kc, 1:1 + BLK], start=(kc == 0), stop=(kc == 7)),
                         [gv, Bhb[i]], [self.BPS[2 + q]])
                w0, w1, w2, cb = self.pv(f"conv{l}_0", fc), self.pv(f"conv{l}_1", fc), self.pv(f"conv{l}_2", fc), self.pv(f"convb{l}", fc)
                S.op("act", lambda: nc.scalar.activation(out=cv[q], in_=pa[:, 1:1 + BLK], func=AF.Identity, scale=w1, bias=cb), [self.BPS[q]], [Bcv[q]])
                c0 = 1 if first else 0
                S.op("dve", lambda: nc.vector.scalar_tensor_tensor(out=cv[q][:, c0:BLK], in0=pa[:, c0:BLK], scalar=w0, in1=cv[q][:, c0:BLK], op0=ALU.mult, op1=ALU.add),
                     [self.BPS[q], Bcv[q]], [Bcv[q]])
                c1 = BLK - 1 if last else BLK
                S.op("dve", lambda: nc.vector.scalar_tensor_tensor(out=cv[q][:, 0:c1], in0=pa[:, 2:2 + c1], scalar=w2, in1=cv[q][:, 0:c1], op0=ALU.mult, op1=ALU.add),
                     [self.BPS[q], Bcv[q]], [Bcv[q]])
                S.op("act", lambda: nc.scalar.activation(out=sl[q], in_=cv[q], func=AF.Silu), [Bcv[q]], [Bsl[q]])
                S.op("dve", lambda: nc.vector.tensor_tensor(out=gt_[i][:, fc, :], in0=sl[q], in1=pvv[:, :BLK], op=ALU.mult), [Bsl[q], self.BPS[2 + q]], [Bg[i]])
            for oc in range(8):
                q = 4 + oc % 2
                po = self.PS[q]
                for fc in range(NFC):
                    S.op("pe", lambda: nc.tensor.matmul(po[:, :BLK], lhsT=Wout[:, fc, oc * 128:(oc + 1) * 128], rhs=gt_[i][:, fc, :], start=(fc == 0), stop=(fc == NFC - 1)),
                         [BWout[fc // 2], Bg[i]], [self.BPS[q]])
                S.op("dve", lambda: nc.vector.scalar_tensor_tensor(out=xs[i][:, oc, 1:1 + BLK], in0=po[:, :BLK], scalar=gate[:, oc:oc + 1], in1=xs[i][:, oc, 1:1 + BLK], op0=ALU.mult, op1=ALU.add),
                     [self.BPS[q], Bxs[i], self.BMOD], [Bxs[i]])
            S.dma("pool", xov[:, :, b * T + t0:b * T + t0 + BLK], xs[i][:, :, 1:1 + BLK], reads=[Bxs[i]])
        st.close()

    def final_stage(self, xin):
        nc, S = self.nc, self.S
        st = Stage(self, "fin")
        xs = [st.sb(f"xs{i}", [128, 8, BLK]) for i in range(2)]
        hb = [st.sb(f"hb{i}", [128, 8, BLK]) for i in range(2)]
        ot = [st.sb(f"ot{i}", [128, D]) for i in range(2)]
        Bxs, Bhb, Bot = [[Buf(), Buf()] for _ in range(3)]
        nt = self.norm_tiles(st, BLK)
        xiv = xin.rearrange("(c p) t -> p c t", p=128)
        blocks = self.blocks(True)
        A = self.pv("norm_f")

        def load(n):
            b, k = blocks[n]
            S.dma("sp", xs[n % 2], xiv[:, :, b * T + k * BLK:b * T + (k + 1) * BLK], writes=[Bxs[n % 2]])

        load(0)
        nt_i = 0
        for n, (b, k) in enumerate(blocks):
            i = n % 2
            if n + 1 < len(blocks):
                load(n + 1)
            self.norm_block(nt, xs[i], Bxs[i], BLK, A, None, hb[i], Bhb[i], 6)
            for tt in range(2):
                o = nt_i % 2
                nt_i += 1
                for hf in range(2):
                    pb = 2 * o + hf
                    for c4 in range(4):
                        c = hf * 4 + c4
                        S.op("pe", lambda: nc.tensor.transpose(out=self.PS[pb][:, c4 * 128:(c4 + 1) * 128], in_=hb[i][:, c, tt * 128:(tt + 1) * 128], identity=self.ident),
                             [Bhb[i]], [self.BPS[pb]])
                    if hf == 0:
                        S.op("dve", lambda: nc.vector.tensor_copy(out=ot[o][:, 0:512], in_=self.PS[pb]), [self.BPS[pb]], [Bot[o]])
                    else:
                        S.op("act", lambda: nc.scalar.copy(out=ot[o][:, 512:1024], in_=self.PS[pb]), [self.BPS[pb]], [Bot[o]])
                tl = k * BLK - TC + tt * 128
                S.dma("pool", self.out[b, tl:tl + 128, :], ot[o], reads=[Bot[o]])
        st.close()


def build_program(wshapes, pv_off, npv, plan=None, dbg=()):
    P = Prog(wshapes, pv_off, npv, dbg=dbg)
    xa = P.scr("xA", [D, TT])
    xb = P.scr("xB", [D, TT])
    if plan is None:
        plan = ["tr", "mod"]
        for l in range(DEPTH):
            plan += [f"mix{l}", f"ffn{l}"]
        plan += ["final"]
    cur, nxt = xa, xb
    for step in plan:
        if step == "tr":
            P.prologue_transpose(cur)
        elif step == "mod":
            P.prologue_mod()
        elif step.startswith("mix"):
            l = int(step[3:])
            P.mixer(l, cur, nxt)
            cur, nxt = nxt, cur
        elif step.startswith("ffn"):
            l = int(step[3:])
            P.ffn_stage(l, cur, nxt, skip_ctx=(l == DEPTH - 1))
            cur, nxt = nxt, cur
        elif step == "final":
            P.final_stage(cur)
    P.S.barrier()
    return P


def prep_inputs(inputs, cores=range(NCORES)):
    pv = pvec_layout(inputs)
    pva = pv.array()
    consts = make_consts()
    shared = {"pvec": pva}
    for k, v in consts.items():
        shared["c_" + k] = v
    for n in WEIGHT_NAMES:
        shared[n] = np.ascontiguousarray(inputs[n], dtype=np.float32)
    in_maps = []
    for c in cores:
        m = dict(shared)
        m["x"] = np.ascontiguousarray(inputs["x"][NB * c:NB * (c + 1)], dtype=np.float32)
        m["ctx"] = np.ascontiguousarray(inputs["ctx"][NB * c:NB * (c + 1)], dtype=np.float32)
        m["cvec"] = np.ascontiguousarray(np.concatenate([inputs["c"][NB * c:NB * (c + 1)], inputs["c_ctx"][None, :]], axis=0), dtype=np.float32)
        in_maps.append(m)
    wshapes = {n: list(inputs[n].shape) for n in WEIGHT_NAMES}
    return in_maps, wshapes, pv.off, pva.shape[1]


def kernel(**inputs):
    inputs = {k: np.asarray(v) for k, v in inputs.items()}
    in_maps, wshapes, pv_off, npv = prep_inputs(inputs)
    P = build_program(wshapes, pv_off, npv)
    res = run_bass_kernel_spmd(P.nc, in_maps, core_ids=list(range(NCORES)))
    out = np.concatenate([np.asarray(r["out"]) for r in res.results], axis=0)
    return out.astype(np.float32)


def _inproj_stage(self, l, xin, Wd, N, dst_fm, tm_specs, f32_h=False):
    nc, S = self.nc, self.S
    st = Stage(self, "ip")
    Wt = st.sb("w", [128, 8, N], BF16)
    BW = self.load_w(Wt, Wd, None)
    xs = [st.sb(f"xs{i}", [128, 8, BLK]) for i in range(2)]
    hb = [st.sb(f"hb{i}", [128, 8, BLK], BF16) for i in range(2)]
    sg = [st.sb(f"sg{i}", [128, 8, BLK]) for i in range(2)]
    tmw = max([nc_ for (_, nc_, _) in tm_specs], default=0)
    tms = [st.sb(f"tm{i}", [128, max(tmw, 1)], BF16) for i in range(2)]
    Bxs, Bhb, Bsg, Btm = [[Buf(), Buf()] for _ in range(4)]
    nt = self.norm_tiles(st, BLK)
    xiv = xin.rearrange("(c p) t -> p c t", p=128)
    dv = dst_fm.rearrange("(c p) t -> p c t", p=128)
    blocks = self.blocks(False)

    def load(n):
        b, k = blocks[n]
        S.dma("sp", xs[n % 2], xiv[:, :, b * T + k * BLK:b * T + (k + 1) * BLK], writes=[Bxs[n % 2]])

    load(0)
    sgi = 0
    tmi = 0
    pbank = 0
    for n, (b, k) in enumerate(blocks):
        i = n % 2
        if n + 1 < len(blocks):
            load(n + 1)
        j = 2 if k == 0 else b
        A, sh, _ = self.mod_ab(l, 0, j)
        self.norm_block(nt, xs[i], Bxs[i], BLK, A, sh, hb[i], Bhb[i], 6)
        col = b * T + k * BLK
        for og in range(N // 1024):
            s_ = sgi % 2
            sgi += 1
            for o8 in range(8):
                oc = og * 8 + o8
                pb = pbank % 4
                pbank += 1
                for kc in range(8):
                    S.op("pe", lambda: nc.tensor.matmul(self.PS[pb][:, :BLK], lhsT=Wt[:, kc, oc * 128:(oc + 1) * 128], rhs=hb[i][:, kc, :], start=(kc == 0), stop=(kc == 7)),
                         [BW[(oc * 128) // 512], Bhb[i]], [self.BPS[pb]])
                if o8 % 2 == 0:
                    S.op("act", lambda: nc.scalar.copy(out=sg[s_][:, o8, :], in_=self.PS[pb][:, :BLK]), [self.BPS[pb]], [Bsg[s_]])
                else:
                    S.op("dve", lambda: nc.vector.tensor_copy(out=sg[s_][:, o8, :], in_=self.PS[pb][:, :BLK]), [self.BPS[pb]], [Bsg[s_]])
            S.dma("pool", dv[:, og * 8:(og + 1) * 8, col:col + BLK], sg[s_], reads=[Bsg[s_]])
        for (c0, ncols, dst_tm) in tm_specs:
            for tt in range(BLK // 128):
                s_ = tmi % 2
                tmi += 1
                for n0 in range(0, ncols, 512):
                    pb = 4 + (pbank % 2)
                    pbank += 1
                    for kc in range(8):
                        S.op("pe", lambda: nc.tensor.matmul(self.PS[pb][:, :512], lhsT=hb[i][:, kc, tt * 128:(tt + 1) * 128], rhs=Wt[:, kc, c0 + n0:c0 + n0 + 512], start=(kc == 0), stop=(kc == 7)),
                             [BW[(c0 + n0) // 512], Bhb[i]], [self.BPS[pb]])
                    S.op("act", lambda: nc.scalar.copy(out=tms[s_][:, n0:n0 + 512], in_=self.PS[pb][:, :512]), [self.BPS[pb]], [Btm[s_]])
                S.dma("pool", dst_tm[col + tt * 128:col + (tt + 1) * 128, :], tms[s_][:, :ncols], reads=[Btm[s_]])
    st.close()


def _outproj_stage(self, l, og, Wd, xin, xout, skip_ctx):
    nc, S = self.nc, self.S
    st = Stage(self, "op")
    Wt = st.sb("w", [128, 8, D], BF16)
    BW = self.load_w(Wt, Wd, None)
    xs = [st.sb(f"xs{i}", [128, 8, BLK]) for i in range(2)]
    ob = [st.sb(f"ob{i}", [128, 8, BLK], BF16) for i in range(2)]
    Bxs, Bob = [[Buf(), Buf()] for _ in range(2)]
    xiv = xin.rearrange("(c p) t -> p c t", p=128)
    xov = xout.rearrange("(c p) t -> p c t", p=128)
    ogv = og.rearrange("(c p) t -> p c t", p=128)
    blocks = self.blocks(skip_ctx)

    def load(n):
        b, k = blocks[n]
        col = b * T + k * BLK
        S.dma("sp", xs[n % 2], xiv[:, :, col:col + BLK], writes=[Bxs[n % 2]])
        S.dma("sp", ob[n % 2], ogv[:, :, col:col + BLK], writes=[Bob[n % 2]])

    load(0)
    for n, (b, k) in enumerate(blocks):
        i = n % 2
        if n + 1 < len(blocks):
            load(n + 1)
        j = 2 if k == 0 else b
        _, _, gate = self.mod_ab(l, 0, j)
        for oc in range(8):
            pb = oc % 4
            for kc in range(8):
                S.op("pe", lambda: nc.tensor.matmul(self.PS[pb][:, :BLK], lhsT=Wt[:, kc, oc * 128:(oc + 1) * 128], rhs=ob[i][:, kc, :], start=(kc == 0), stop=(kc == 7)),
                     [BW[(oc * 128) // 512], Bob[i]], [self.BPS[pb]])
            S.op("dve", lambda: nc.vector.scalar_tensor_tensor(out=xs[i][:, oc, :], in0=self.PS[pb][:, :BLK], scalar=gate[:, oc:oc + 1], in1=xs[i][:, oc, :], op0=ALU.mult, op1=ALU.add),
                 [self.BPS[pb], Bxs[i], self.BMOD], [Bxs[i]])
        col = b * T + k * BLK
        S.dma("pool", xov[:, :, col:col + BLK], xs[i], reads=[Bxs[i]])
    st.close()


def _hgrn2_scan(self, jh, Pfm, Itm, og):
    nc, S = self.nc, self.S
    st = Stage(self, "hs")
    A_ = nc.vector
    LB = st.sb("LB", [128, 2, 8])
    OML = st.sb("OML", [128, 2, 8])
    e0 = st.sb("e0", [128, 8]); e1 = st.sb("e1", [128, 8]); rr = st.sb("rr", [128, 8]); p0 = st.sb("p0", [128, 8]); p1 = st.sb("p1", [128, 8])
    BL = Buf()
    for d in range(2):
        S.op("act", lambda: nc.scalar.activation(out=e0, in_=self.pv(f"hg_lb{d}_0"), func=AF.Exp), [], [BL])
        S.op("act", lambda: nc.scalar.activation(out=e1, in_=self.pv(f"hg_lb{d}_1"), func=AF.Exp), [BL], [BL])
        S.op("dve", lambda: A_.tensor_tensor(out=rr, in0=e0, in1=e1, op=ALU.add), [BL], [BL])
        S.op("dve", lambda: A_.reciprocal(out=rr, in_=rr), [BL], [BL])
        S.op("dve", lambda: A_.tensor_tensor(out=p0, in0=e0, in1=rr, op=ALU.mult), [BL], [BL])
        S.op("dve", lambda: A_.tensor_tensor(out=p1, in0=e1, in1=rr, op=ALU.mult), [BL], [BL])
        if jh == 1:
            S.op("dve", lambda: A_.tensor_tensor(out=p1, in0=p0, in1=p1, op=ALU.add), [BL], [BL])
        else:
            S.op("dve", lambda: A_.tensor_copy(out=p1, in_=p0), [BL], [BL])
        S.op("dve", lambda: A_.tensor_tensor(out=LB[:, d, :], in0=p1, in1=p0, op=ALU.subtract), [BL], [BL])
        S.op("dve", lambda: A_.tensor_scalar(out=OML[:, d, :], in0=LB[:, d, :], scalar1=-1.0, scalar2=1.0, op0=ALU.mult, op1=ALU.add), [BL], [BL])
    smask = st.sb("smask", [128, T])
    Bsm = Buf()
    S.dma("sp", smask, self.cd["scanmask"], writes=[Bsm])
    f32t = lambda n: st.sb(n, [128, T])
    qs = f32t("qs"); graw = f32t("graw"); kk = f32t("kk"); ep = f32t("ep"); en = f32t("en")
    z = [f32t("z0"), f32t("z1")]; bb = [f32t("b0"), f32t("b1")]; of = [f32t("of0"), f32t("of1")]
    qt = [st.sb(f"qt{d}", [128, T], BF16) for d in range(2)]
    kh = [st.sb(f"kh{d}", [128, T], BF16) for d in range(2)]
    sqb = st.sb("sqb", [128, T], BF16)
    ogb = st.sb("ogb", [128, T], BF16)
    Vt = st.sb("Vt", [64, NCH, 128], BF16)
    emid = [st.sb(f"emid{d}", [128, NCH]) for d in range(2)]
    eend = [st.sb(f"eend{d}", [128, NCH]) for d in range(2)]
    eem = [st.sb(f"eem{d}", [128, NCH]) for d in range(2)]
    Sst = [st.sb(f"S{d}", [128, 128]) for d in range(2)]
    Sm = [st.sb(f"Sm{d}", [128, 128], BF16) for d in range(2)]
    tmpS = [st.sb(f"tS{d}", [128, 128]) for d in range(2)]
    khT = [st.sb(f"khT{d}", [64, 128], BF16) for d in range(2)]
    att = [st.sb(f"att{d}", [64, 64], BF16) for d in range(2)]
    Bqs, Bgr, Bkk, Bep, Ben, Bsq, Bog, BVt = [Buf() for _ in range(8)]
    Bz, Bbb, Bof, Bqt, Bkh, Bes, BS, BSm, BtS, BkT, Batt = [[Buf(), Buf()] for _ in range(11)]
    PSb = [self.PS[i].bitcast(BF16) for i in range(8)]
    for d in range(2):
        S.op("dve", lambda: A_.memset(att[d], 0.0), [], [Batt[d]])
    cf = list(range(NCH))
    cb = list(range(TC // CH - 1, -1, -1)) + list(range(NCH - 1, TC // CH - 1, -1))
    order = [cf, cb]
    for b in range(NB):
        for h in range(8):
            rows = slice(h * 128, (h + 1) * 128)
            cols = slice(b * T, (b + 1) * T)
            S.dma("sp", qs, Pfm[0 * D + h * 128:0 * D + (h + 1) * 128, cols], writes=[Bqs])
            S.dma("sp", z[0], Pfm[3 * D + h * 128:3 * D + (h + 1) * 128, cols], writes=[Bz[0]])
            S.dma("sp", z[1], Pfm[4 * D + h * 128:4 * D + (h + 1) * 128, cols], writes=[Bz[1]])
            S.dma("sp", graw, Pfm[2 * D + h * 128:2 * D + (h + 1) * 128, cols], writes=[Bgr])
            S.dma("sp", Vt, Itm[cols, rows].rearrange("(c s) v -> s c v", s=CH), writes=[BVt])
            S.op("act", lambda: nc.scalar.activation(out=qs, in_=qs, func=AF.Silu), [Bqs], [Bqs])
            for d in range(2):
                m_idx = 32 if d == 0 else 31
                zt = z[d]
                S.op("act", lambda: nc.scalar.activation(out=zt, in_=zt, func=AF.Sigmoid), [Bz[d]], [Bz[d]])
                S.op("dve", lambda: A_.tensor_scalar(out=zt, in0=zt, scalar1=OML[:, d, h:h + 1], scalar2=LB[:, d, h:h + 1], op0=ALU.mult, op1=ALU.add), [Bz[d], BL], [Bz[d]])
                S.op("dve", lambda: A_.tensor_scalar(out=kk, in0=zt, scalar1=-1.0, scalar2=1.0, op0=ALU.mult, op1=ALU.add), [Bz[d]], [Bkk])
                S.op("act", lambda: nc.scalar.activation(out=zt, in_=zt, func=AF.Ln), [Bz[d]], [Bz[d]])
                S.op("dve", lambda: A_.tensor_tensor_scan(out=bb[d], data0=smask, data1=zt, initial=0.0, op0=ALU.mult, op1=ALU.add), [Bsm, Bz[d]], [Bbb[d]])
                b3 = bb[d].rearrange("p (c s) -> p c s", s=CH)
                if d == 1:
                    S.op("dve", lambda: A_.tensor_tensor(out=zt, in0=zt, in1=bb[d], op=ALU.subtract), [Bz[d], Bbb[d]], [Bz[d]])
                    S.op("dve", lambda: A_.tensor_tensor(out=ep.rearrange("p (c s) -> p c s", s=CH), in0=zt.rearrange("p (c s) -> p c s", s=CH),
                                                          in1=b3[:, :, CH - 1:CH].to_broadcast([128, NCH, CH]), op=ALU.add), [Bz[d], Bbb[d]], [Bep])
                    S.op("dve", lambda: A_.tensor_copy(out=bb[d], in_=ep), [Bep], [Bbb[d]])
                e_idx = CH - 1 if d == 0 else 0
                S.op("act", lambda: nc.scalar.activation(out=emid[d], in_=b3[:, :, m_idx], func=AF.Exp), [Bbb[d]], [Bes[d]])
                S.op("act", lambda: nc.scalar.activation(out=eend[d], in_=b3[:, :, e_idx], func=AF.Exp), [Bbb[d]], [Bes[d]])
                S.op("dve", lambda: A_.tensor_tensor(out=eem[d], in0=b3[:, :, e_idx], in1=b3[:, :, m_idx], op=ALU.subtract), [Bbb[d]], [Bes[d]])
                S.op("act", lambda: nc.scalar.activation(out=eem[d], in_=eem[d], func=AF.Exp), [Bes[d]], [Bes[d]])
                S.op("dve", lambda: A_.tensor_tensor(out=ep.rearrange("p (c s) -> p c s", s=CH), in0=b3, in1=b3[:, :, m_idx:m_idx + 1].to_broadcast([128, NCH, CH]), op=ALU.subtract),
                     [Bbb[d]], [Bep])
                S.op("act", lambda: nc.scalar.activation(out=en, in_=ep, func=AF.Exp, scale=-1.0), [Bep], [Ben])
                S.op("act", lambda: nc.scalar.activation(out=ep, in_=ep, func=AF.Exp), [Bep], [Bep])
                S.op("dve", lambda: A_.tensor_tensor(out=qt[d], in0=qs, in1=ep, op=ALU.mult), [Bqs, Bep], [Bqt[d]])
                S.op("dve", lambda: A_.tensor_tensor(out=kh[d], in0=kk, in1=en, op=ALU.mult), [Bkk, Ben], [Bkh[d]])
                S.op("dve", lambda: A_.memset(Sst[d], 0.0), [], [BS[d]])
                S.op("dve", lambda: A_.memset(Sm[d], 0.0), [], [BSm[d]])
            def hstep(d, step):
                c = order[d][step]
                cs = slice(c * CH, (c + 1) * CH)
                pb = d * 4
                mk = (self.masks[0:64, 64:128] if d == 0 else self.masks[0:64, 192:256]).bitcast(mybir.dt.uint32)
                S.op("pe", lambda: nc.tensor.transpose(out=PSb[pb][0:64, 0:128], in_=kh[d][:, cs], identity=self.identb), [Bkh[d]], [self.BPS[pb]])
                S.op("pe", lambda: nc.tensor.matmul(self.PS[pb + 1][0:64, 0:64], lhsT=kh[d][:, cs], rhs=qt[d][:, cs], start=True, stop=True), [Bkh[d], Bqt[d]], [self.BPS[pb + 1]])
                yield
                S.op("act", lambda: nc.scalar.copy(out=khT[d], in_=PSb[pb][0:64, 0:128]), [self.BPS[pb]], [BkT[d]])
                S.op("dve", lambda: A_.copy_predicated(out=att[d], mask=mk, data=self.PS[pb + 1][0:64, 0:64]), [self.BPS[pb + 1]], [Batt[d]])
                S.op("pe", lambda: nc.tensor.matmul(self.PS[pb + 2][:, 0:64], lhsT=Vt[:, c, :], rhs=att[d], start=True, stop=False), [BVt, Batt[d]], [self.BPS[pb + 2]])
                S.op("pe", lambda: nc.tensor.matmul(self.PS[pb + 2][:, 0:64], lhsT=Sm[d], rhs=qt[d][:, cs], start=False, stop=True), [BSm[d], Bqt[d]], [self.BPS[pb + 2]])
                S.op("pe", lambda: nc.tensor.matmul(self.PS[pb + 3][:, 0:128], lhsT=khT[d], rhs=Vt[:, c, :], start=True, stop=True), [BkT[d], BVt], [self.BPS[pb + 3]])
                yield
                S.op("act", lambda: nc.scalar.activation(out=tmpS[d], in_=self.PS[pb + 3][:, 0:128], func=AF.Identity, scale=eem[d][:, c:c + 1]), [self.BPS[pb + 3], Bes[d]], [BtS[d]])
                S.op("dve", lambda: A_.scalar_tensor_tensor(out=Sst[d], in0=Sst[d], scalar=eend[d][:, c:c + 1], in1=tmpS[d], op0=ALU.mult, op1=ALU.add), [BS[d], BtS[d], Bes[d]], [BS[d]])
                S.op("act", lambda: nc.scalar.copy(out=of[d][:, cs], in_=self.PS[pb + 2][:, 0:64]), [self.BPS[pb + 2]], [Bof[d]])
                if step + 1 < NCH:
                    cn = order[d][step + 1]
                    S.op("dve", lambda: A_.tensor_scalar(out=Sm[d], in0=Sst[d], scalar1=emid[d][:, cn:cn + 1], scalar2=None, op0=ALU.mult), [BS[d], Bes[d]], [BSm[d]])

            for step in range(NCH):
                gens = [hstep(d, step) for d in range(2)]
                while gens:
                    for g_ in list(gens):
                        try:
                            next(g_)
                        except StopIteration:
                            gens.remove(g_)
            S.op("dve", lambda: A_.tensor_tensor(out=of[0], in0=of[0], in1=of[1], op=ALU.add), [Bof[0], Bof[1]], [Bof[0]])
            S.op("act", lambda: nc.scalar.activation(out=sqb, in_=of[0], func=AF.Square), [Bof[0]], [Bsq])
            for pc in range(6):
                sl_ = slice(pc * 384, (pc + 1) * 384)
                pb = pc % 2
                S.op("pe", lambda: nc.tensor.matmul(self.PS[pb][:, 0:384], lhsT=self.onesb, rhs=sqb[:, sl_], start=True, stop=True), [Bsq], [self.BPS[pb]])
                S.op("act", lambda: nc.scalar.activation(out=ep[:, sl_], in_=self.PS[pb][:, 0:384], func=AF.Sqrt, scale=1.0 / 128, bias=self.epsD), [self.BPS[pb]], [Bep])
            S.op("dve", lambda: A_.reciprocal(out=ep, in_=ep), [Bep], [Bep])
            S.op("dve", lambda: A_.tensor_tensor(out=of[0], in0=of[0], in1=ep, op=ALU.mult), [Bof[0], Bep], [Bof[0]])
            S.op("act", lambda: nc.scalar.activation(out=graw, in_=graw, func=AF.Silu), [Bgr], [Bgr])
            S.op("dve", lambda: A_.scalar_tensor_tensor(out=ogb, in0=of[0], scalar=self.pv(f"hg_norm{jh}", 0), in1=graw, op0=ALU.mult, op1=ALU.mult), [Bof[0], Bgr], [Bog])
            S.dma("pool", og[rows, cols], ogb, reads=[Bog])
    st.close()


def _mixer(self, l, cur, nxt):
    kind, j = l % 3, l // 3
    last = (l == DEPTH - 1)
    og = self.scr("og", [D, TT], BF16)
    if kind == 0:
        Pfm = self.scr("hgP", [5 * D, TT])
        Itm = self.scr("hgI", [TT, D], BF16)
        self.inproj_stage(l, cur, self.W["hg_w_in"][j], 5 * D, Pfm, [(D, D, Itm)])
        self.hgrn2_scan(j, Pfm, Itm, og)
        self.outproj_stage(l, og, self.W["hg_w_o"][j], cur, nxt, last)
    elif kind == 1:
        self.rwkv_mixer(l, cur, og)
        self.outproj_stage(l, og, self.W["rw_w_o"][j], cur, nxt, last)
    else:
        self.mla_mixer(l, cur, og)
        self.outproj_stage(l, og, self.W["mla_w_o"][j], cur, nxt, last)


Prog.inproj_stage = _inproj_stage
Prog.outproj_stage = _outproj_stage
Prog.hgrn2_scan = _hgrn2_scan
Prog.mixer = _mixer


def _mla_mixer(self, l, xin, og):
    nc, S = self.nc, self.S
    A_ = nc.vector
    NH = 16
    QN = self.scr("mlaQN", [64, NH, TT], BF16)
    QR = self.scr("mlaQR", [32, NH, TT], BF16)
    KN = self.scr("mlaKN", [64, NH, TT], BF16)
    KR = self.scr("mlaKR", [32, TT], BF16)
    VT = self.scr("mlaVT", [TT, D], BF16)
    st = Stage(self, "m1")
    Wd = st.sb("wd", [128, 8, 544], BF16)
    Wq = st.sb("wq", [128, 2, 1536], BF16)
    Wk = st.sb("wk", [128, 2, 2048], BF16)
    Wdr = st.sb("wdr", [128, 8, 32], BF16)
    Wqr = st.sb("wqr", [128, 2, NH, 32], BF16)
    BWd, BWq, BWk, BWr = Buf(), Buf(), Buf(), Buf()
    S.dma("pool", Wd, self.W["mla_w_dqkv"][0].rearrange("(kc p) n -> p kc n", p=128), writes=[BWd])
    wqv = self.W["mla_w_uq"][0].rearrange("(kc p) n -> p kc n", p=128)
    for i3 in range(3):
        S.dma("pool", Wq[:, :, i3 * 512:(i3 + 1) * 512], wqv[:, :, i3 * 512:(i3 + 1) * 512], writes=[BWq])
    wkv = self.W["mla_w_ukv"][0].rearrange("(kc p) n -> p kc n", p=128)
    for i4 in range(4):
        S.dma("pool", Wk[:, :, i4 * 512:(i4 + 1) * 512], wkv[:, :, i4 * 512:(i4 + 1) * 512], writes=[BWk])
    Wq4 = Wq.rearrange("p k (h c) -> p k h c", c=96)
    for seg in range(2):
        for half in range(2):
            sgn = -1.0 if half == 0 else 1.0
            so = 64 + seg * 16 + (1 - half) * 8
            do = seg * 16 + half * 8
            S.op("act", lambda: nc.scalar.activation(out=Wqr[:, :, :, do:do + 8], in_=Wq4[:, :, :, so:so + 8], func=AF.Copy, scale=sgn), [BWq], [BWr])
            so2 = 512 + seg * 16 + (1 - half) * 8
            S.op("act", lambda: nc.scalar.activation(out=Wdr[:, :, do:do + 8], in_=Wd[:, :, so2:so2 + 8], func=AF.Copy, scale=sgn), [BWd], [BWr])
    cos = st.sb("cos", [32, T]); sin = st.sb("sin", [32, T])
    Bcs = Buf()
    S.dma("sp", cos, self.cd["rope_cos"], writes=[Bcs])
    S.dma("sp", sin, self.cd["rope_sin"], writes=[Bcs])
    xs = [st.sb(f"xs{i}", [128, 8, BLK]) for i in range(2)]
    hb = [st.sb(f"hb{i}", [128, 8, BLK], BF16) for i in range(2)]
    Bxs, Bhb = [[Buf(), Buf()] for _ in range(2)]
    nt = self.norm_tiles(st, BLK)
    cs_ = st.sb("cs", [128, 4, BLK]); csq = st.sb("csq", [128, 4, BLK], BF16); cn = st.sb("cn", [128, 4, BLK], BF16)
    rr0 = st.sb("rr0", [128, 2, BLK]); rr1 = st.sb("rr1", [128, 2, BLK]); ctmp = st.sb("ctmp", [128, 4, BLK])
    Bcs_, Bcsq, Bcn, Brr, Bct = [Buf() for _ in range(5)]
    qn_s = [st.sb(f"qns{i}", [64, NH, BLK], BF16) for i in range(2)]
    kn_s = [st.sb(f"kns{i}", [64, NH, BLK], BF16) for i in range(2)]
    qr_s = [st.sb(f"qrs{i}", [32, NH, BLK], BF16) for i in range(2)]
    kr_s = [st.sb(f"krs{i}", [32, BLK], BF16) for i in range(2)]
    vt_s = [st.sb(f"vts{i}", [128, D], BF16) for i in range(2)]
    t1 = st.sb("t1", [32, 2, BLK]); t2 = st.sb("t2", [32, 2, BLK])
    Bt1, Bt2 = Buf(), Buf()
    Bqn, Bkn, Bqr, Bkr, Bvt = [[Buf(), Buf()] for _ in range(5)]
    xiv = xin.rearrange("(c p) t -> p c t", p=128)
    blocks = self.blocks(False)

    def load(n):
        b, k = blocks[n]
        S.dma("sp", xs[n % 2], xiv[:, :, b * T + k * BLK:b * T + (k + 1) * BLK], writes=[Bxs[n % 2]])

    load(0)
    vti = 0
    for n, (b, k) in enumerate(blocks):
        i = n % 2
        if n + 1 < len(blocks):
            load(n + 1)
        j = 2 if k == 0 else b
        A, sh, _ = self.mod_ab(l, 0, j)
        self.norm_block(nt, xs[i], Bxs[i], BLK, A, sh, hb[i], Bhb[i], 6)
        col = b * T + k * BLK
        tcol = slice(k * BLK, (k + 1) * BLK)
        for c4 in range(4):
            pb = c4 // 2
            for kc in range(8):
                S.op("pe", lambda: nc.tensor.matmul(self.PS[pb][:, (c4 % 2) * BLK:(c4 % 2 + 1) * BLK], lhsT=Wd[:, kc, c4 * 128:(c4 + 1) * 128], rhs=hb[i][:, kc, :], start=(kc == 0), stop=(kc == 7)),
                     [BWd, Bhb[i]], [self.BPS[pb]])
        for kc in range(8):
            S.op("pe", lambda: nc.tensor.matmul(self.PS[2][0:32, 0:BLK], lhsT=Wd[:, kc, 512:544], rhs=hb[i][:, kc, :], start=(kc == 0), stop=(kc == 7)), [BWd, Bhb[i]], [self.BPS[2]])
        for kc in range(8):
            S.op("pe", lambda: nc.tensor.matmul(self.PS[2][0:32, BLK:2 * BLK], lhsT=Wdr[:, kc, :], rhs=hb[i][:, kc, :], start=(kc == 0), stop=(kc == 7)), [BWr, Bhb[i]], [self.BPS[2]])
        for pb in range(2):
            S.op("act", lambda: nc.scalar.copy(out=cs_[:, 2 * pb:2 * pb + 2, :], in_=self.PS[pb].rearrange("p (c t) -> p c t", c=2)), [self.BPS[pb]], [Bcs_])
            S.op("act", lambda: nc.scalar.activation(out=csq[:, 2 * pb:2 * pb + 2, :], in_=self.PS[pb].rearrange("p (c t) -> p c t", c=2), func=AF.Square), [self.BPS[pb]], [Bcsq])
        S.op("dve", lambda: A_.tensor_tensor(out=t1[:, 0, :], in0=self.PS[2][0:32, 0:BLK], in1=cos[:, tcol], op=ALU.mult), [self.BPS[2], Bcs], [Bt1])
        S.op("dve", lambda: A_.tensor_tensor(out=t2[:, 0, :], in0=self.PS[2][0:32, BLK:2 * BLK], in1=sin[:, tcol], op=ALU.mult), [self.BPS[2], Bcs], [Bt2])
        S.op("dve", lambda: A_.tensor_tensor(out=kr_s[i], in0=t1[:, 0, :], in1=t2[:, 0, :], op=ALU.add), [Bt1, Bt2], [Bkr[i]])
        S.dma("pool", KR[:, col:col + BLK], kr_s[i], reads=[Bkr[i]])
        for w in range(2):
            for c in range(2):
                S.op("pe", lambda: nc.tensor.matmul(self.PS[3][:, w * BLK:(w + 1) * BLK], lhsT=self.onesb, rhs=csq[:, 2 * w + c, :], start=(c == 0), stop=(c == 1)), [Bcsq], [self.BPS[3]])
        S.op("act", lambda: nc.scalar.activation(out=rr0, in_=self.PS[3].rearrange("p (w t) -> p w t", w=2), func=AF.Sqrt, scale=1.0 / 256, bias=self.epsD), [self.BPS[3]], [Brr])
        S.op("dve", lambda: A_.reciprocal(out=rr1, in_=rr0), [Brr], [Brr])
        S.op("dve", lambda: A_.tensor_tensor(out=ctmp.rearrange("p (w c) t -> p w c t", w=2), in0=cs_.rearrange("p (w c) t -> p w c t", w=2),
                                              in1=rr1.unsqueeze(2).to_broadcast([128, 2, 2, BLK]), op=ALU.mult), [Bcs_, Brr], [Bct])
        for c4 in range(4):
            gname = "mla_q_norm" if c4 < 2 else "mla_kv_norm"
            S.op("act", lambda: nc.scalar.activation(out=cn[:, c4, :], in_=ctmp[:, c4, :], func=AF.Identity, scale=self.pv(gname, c4 % 2)), [Bct], [Bcn])
        for hp in range(8):
            for which in range(2):
                pb = 4 + (2 * hp + which) % 2
                Wt_, coff, hw, ci = (Wq, 0, 96, 0) if which == 0 else (Wk, 0, 128, 2)
                for hh in range(2):
                    h = 2 * hp + hh
                    for kc in range(2):
                        S.op("pe", lambda: nc.tensor.matmul(self.PS[pb][0:64, hh * BLK:(hh + 1) * BLK], lhsT=Wt_[:, kc, h * hw:h * hw + 64], rhs=cn[:, ci + kc, :], start=(kc == 0), stop=(kc == 1)),
                             [BWq if which == 0 else BWk, Bcn], [self.BPS[pb]])
                dst = qn_s[i] if which == 0 else kn_s[i]
                Bd = Bqn[i] if which == 0 else Bkn[i]
                if which == 0:
                    S.op("act", lambda: nc.scalar.copy(out=dst[:, 2 * hp:2 * hp + 2, :], in_=self.PS[pb][0:64, :].rearrange("p (h t) -> p h t", h=2)), [self.BPS[pb]], [Bd])
                else:
                    S.op("dve", lambda: A_.tensor_copy(out=dst[:, 2 * hp:2 * hp + 2, :], in_=self.PS[pb][0:64, :].rearrange("p (h t) -> p h t", h=2)), [self.BPS[pb]], [Bd])
            for hh in range(2):
                h = 2 * hp + hh
                for kc in range(2):
                    S.op("pe", lambda: nc.tensor.matmul(self.PS[6][0:32, hh * BLK:(hh + 1) * BLK], lhsT=Wq[:, kc, h * 96 + 64:h * 96 + 96], rhs=cn[:, kc, :], start=(kc == 0), stop=(kc == 1)), [BWq, Bcn], [self.BPS[6]])
                for kc in range(2):
                    S.op("pe", lambda: nc.tensor.matmul(self.PS[7][0:32, hh * BLK:(hh + 1) * BLK], lhsT=Wqr[:, kc, h, :], rhs=cn[:, kc, :], start=(kc == 0), stop=(kc == 1)), [BWr, Bcn], [self.BPS[7]])
            cosb = cos[:, tcol].unsqueeze(1).to_broadcast([32, 2, BLK])
            sinb = sin[:, tcol].unsqueeze(1).to_broadcast([32, 2, BLK])
            S.op("dve", lambda: A_.tensor_tensor(out=t1, in0=self.PS[6][0:32, :].rearrange("p (h t) -> p h t", h=2), in1=cosb, op=ALU.mult), [self.BPS[6], Bcs], [Bt1])
            S.op("dve", lambda: A_.tensor_tensor(out=t2, in0=self.PS[7][0:32, :].rearrange("p (h t) -> p h t", h=2), in1=sinb, op=ALU.mult), [self.BPS[7], Bcs], [Bt2])
            S.op("dve", lambda: A_.tensor_tensor(out=qr_s[i][:, 2 * hp:2 * hp + 2, :], in0=t1, in1=t2, op=ALU.add), [Bt1, Bt2], [Bqr[i]])
        S.dma("pool", QN[:, :, col:col + BLK], qn_s[i], reads=[Bqn[i]])
        S.dma("pool", KN[:, :, col:col + BLK], kn_s[i], reads=[Bkn[i]])
        S.dma("pool", QR[:, :, col:col + BLK], qr_s[i], reads=[Bqr[i]])
        Wkv = Wk.rearrange("p k (h c) -> p k h c", c=128)
        for tt in range(BLK // 128):
            vi = vti % 2
            vti += 1
            for hf in range(2):
                pb = 4 + hf
                for kc in range(2):
                    S.op("pe", lambda: nc.tensor.matmul(self.PS[pb][:, 0:512], lhsT=cn[:, 2 + kc, tt * 128:(tt + 1) * 128], rhs=Wkv[:, kc, hf * 8:(hf + 1) * 8, 64:128], start=(kc == 0), stop=(kc == 1)),
                         [BWk, Bcn], [self.BPS[pb]])
                S.op("act", lambda: nc.scalar.copy(out=vt_s[vi][:, hf * 512:(hf + 1) * 512], in_=self.PS[pb][:, 0:512]), [self.BPS[pb]], [Bvt[vi]])
            S.dma("pool", VT[col + tt * 128:col + (tt + 1) * 128, :], vt_s[vi], reads=[Bvt[vi]])
    st.close()
    st = Stage(self, "m2")
    NKT = T // 128
    Vall = st.sb("Vall", [128, NKT, D], BF16)
    KRs = st.sb("KRs", [32, T], BF16)
    KNh = [st.sb(f"KNh{i}", [64, T], BF16) for i in range(2)]
    QNh = [st.sb(f"QNh{i}", [64, T], BF16) for i in range(2)]
    QRh = [st.sb(f"QRh{i}", [32, T], BF16) for i in range(2)]
    VX = [st.sb(f"VX{i}", [128, NKT, 65], BF16) for i in range(2)]
    PT = [st.sb(f"PT{i}", [128, 512], BF16) for i in range(3)]
    rd = st.sb("rd", [65, 512]); rb = [st.sb(f"rb{i}", [64, 512]) for i in range(2)]
    ob = [st.sb(f"ob{i}", [64, 512], BF16) for i in range(2)]
    BVa, BKR, Brd = Buf(), Buf(), Buf()
    BKN, BQN, BQR, BVX, Brb, Bob = [[Buf(), Buf()] for _ in range(6)]
    BPT = [Buf() for _ in range(3)]
    for i in range(2):
        S.op("pool", lambda: nc.gpsimd.memset(VX[i], 1.0), [], [BVX[i]])
    qblocks = [(0, TC, 2)] + [(TC + qb * 512, 512, NKT) for qb in range(4)]
    pti = 0
    hn = 0
    for b in range(NB):
        c0 = b * T
        S.dma("sp", Vall, VT[c0:c0 + T, :].rearrange("(kt p) v -> p kt v", p=128), writes=[BVa])
        S.dma("sp", KRs, KR[:, c0:c0 + T], writes=[BKR])
        for h in range(NH):
            i = hn % 2
            hn += 1
            S.dma("sp", KNh[i], KN[:, h, c0:c0 + T], writes=[BKN[i]])
            S.dma("sp", QNh[i], QN[:, h, c0:c0 + T], writes=[BQN[i]])
            S.dma("sp", QRh[i], QR[:, h, c0:c0 + T], writes=[BQR[i]])
            S.op("pool", lambda: nc.gpsimd.tensor_copy(out=VX[i][:, :, 0:64], in_=Vall[:, :, h * 64:(h + 1) * 64]), [BVa], [BVX[i]])
            for qi, (q0, nq, nkt) in enumerate(qblocks):
                po = 4 + (qi % 2)

                def score(kt):
                    ps = kt % 4
                    ks = slice(kt * 128, (kt + 1) * 128)
                    S.op("pe", lambda: nc.tensor.matmul(self.PS[ps][:, 0:nq], lhsT=KNh[i][:, ks], rhs=QNh[i][:, q0:q0 + nq], start=True, stop=False), [BKN[i], BQN[i]], [self.BPS[ps]])
                    S.op("pe", lambda: nc.tensor.matmul(self.PS[ps][:, 0:nq], lhsT=KRs[:, ks], rhs=QRh[i][:, q0:q0 + nq], start=False, stop=True), [BKR, BQR[i]], [self.BPS[ps]])

                score(0)
                if nkt > 1:
                    score(1)
                for kt in range(nkt):
                    ps = kt % 4
                    p3 = pti % 3
                    pti += 1
                    if kt + 2 < nkt:
                        score(kt + 2)
                    S.op("act", lambda: nc.scalar.activation(out=PT[p3][:, 0:nq], in_=self.PS[ps][:, 0:nq], func=AF.Exp, scale=MLA_SCALE), [self.BPS[ps]], [BPT[p3]])
                    S.op("pe", lambda: nc.tensor.matmul(self.PS[po][0:65, 0:nq], lhsT=VX[i][:, kt, :], rhs=PT[p3][:, 0:nq], start=(kt == 0), stop=(kt == nkt - 1)), [BVX[i], BPT[p3]], [self.BPS[po]])
                r2 = qi % 2
                S.op("dve", lambda: A_.reciprocal(out=rd[64:65, 0:nq], in_=self.PS[po][64:65, 0:nq]), [self.BPS[po]], [Brd])
                S.op("pe", lambda: nc.tensor.matmul(self.PS[6 + r2][0:64, 0:nq], lhsT=self.onesf[64:65, 0:64], rhs=rd[64:65, 0:nq], start=True, stop=True), [Brd], [self.BPS[6 + r2]])
                S.op("act", lambda: nc.scalar.copy(out=rb[r2][:, 0:nq], in_=self.PS[6 + r2][0:64, 0:nq]), [self.BPS[6 + r2]], [Brb[r2]])
                S.op("dve", lambda: A_.tensor_tensor(out=ob[r2][:, 0:nq], in0=self.PS[po][0:64, 0:nq], in1=rb[r2][:, 0:nq], op=ALU.mult), [self.BPS[po], Brb[r2]], [Bob[r2]])
                S.dma("pool", og[h * 64:(h + 1) * 64, c0 + q0:c0 + q0 + nq], ob[r2][:, 0:nq], reads=[Bob[r2]])
    st.close()


Prog.mla_mixer = _mla_mixer


RW_ARR = ["r", "kt0", "kt1", "be0", "be1", "kap", "lw0", "lw1", "v", "g"]


def _rwkv_proj(self, l, xin, RWP, Vtm):
    nc, S = self.nc, self.S
    A_ = nc.vector
    st = Stage(self, "r1")
    Wrkv = st.sb("wrkv", [128, 8, 3 * D], BF16)
    BWrkv = []
    for i3 in range(3):
        v_ = self.W["rw_w_rkv"][0, i3].rearrange("(kc p) n -> p kc n", p=128)
        for hf in range(2):
            bb_ = Buf()
            S.dma("pool", Wrkv[:, :, i3 * D + hf * 512:i3 * D + (hf + 1) * 512], v_[:, :, hf * 512:(hf + 1) * 512], writes=[bb_])
            BWrkv.append(bb_)
    W1 = st.sb("w1", [128, 8, 2, 64], BF16); A1 = st.sb("a1", [128, 8, 2, 64], BF16); G1 = st.sb("g1", [128, 8, 160], BF16)
    W2 = st.sb("w2", [64, 2, D], BF16); A2 = st.sb("a2", [64, 2, D], BF16); G2a = st.sb("g2a", [128, D], BF16); G2b = st.sb("g2b", [32, D], BF16)
    Bsw = Buf()
    for d in range(2):
        S.dma("pool", W1[:, :, d, :], self.W["rw_w1"][0, d].rearrange("(kc p) n -> p kc n", p=128), writes=[Bsw])
        S.dma("pool", A1[:, :, d, :], self.W["rw_a1"][0, d].rearrange("(kc p) n -> p kc n", p=128), writes=[Bsw])
        S.dma("pool", W2[:, d, :], self.W["rw_w2"][0, d], writes=[Bsw])
        S.dma("pool", A2[:, d, :], self.W["rw_a2"][0, d], writes=[Bsw])
    S.dma("pool", G1, self.W["rw_g1"][0].rearrange("(kc p) n -> p kc n", p=128), writes=[Bsw])
    S.dma("pool", G2a, self.W["rw_g2"][0, 0:128, :], writes=[Bsw])
    S.dma("pool", G2b, self.W["rw_g2"][0, 128:160, :], writes=[Bsw])
    NH_ = BLK + 2
    xs = [st.sb(f"xs{i}", [128, 8, NH_]) for i in range(2)]
    hf_ = st.sb("hf", [128, 8, NH_])
    dx = st.sb("dx", [128, 8, BLK])
    xj = [st.sb(f"xj{j}", [128, 8, BLK], BF16) for j in range(6)]
    Bxs = [Buf(), Buf()]
    Bhf, Bdx = Buf(), Buf()
    Bxj = [Buf() for _ in range(6)]
    nt = self.norm_tiles(st)
    lt = st.sb("lt", [64, 5, BLK], BF16)
    gh = st.sb("gh", [128, BLK], BF16)
    Blt = Buf()
    stg = [st.sb(f"stg{i}", [128, 10, BLK]) for i in range(2)]
    Bstg = [Buf(), Buf()]
    tmp = [st.sb(f"tmp{i}", [128, BLK]) for i in range(6)]
    Btmp = [Buf() for _ in range(6)]
    sqb = st.sb("sqb", [128, BLK], BF16)
    Bsqb = Buf()
    vts = [st.sb(f"vts{i}", [128, D], BF16) for i in range(2)]
    Bvts = [Buf(), Buf()]
    for i in range(2):
        S.op("dve", lambda: A_.memset(xs[i], 0.0), [], [Bxs[i]])
    xiv = xin.rearrange("(c p) t -> p c t", p=128)
    blocks = self.blocks(False)
    blk64b = st.sb("blk64b", [128, 128], BF16)
    Bb64 = Buf()
    S.op("dve", lambda: A_.tensor_copy(out=blk64b, in_=self.blk64), [], [Bb64])

    def load(n):
        b, k = blocks[n]
        t0, lo, hi, _, _ = self.blk_range(k)
        S.dma("sp", xs[n % 2][:, :, lo - (t0 - 1):hi - (t0 - 1)], xiv[:, :, b * T + lo:b * T + hi], writes=[Bxs[n % 2]])

    load(0)
    si = 0
    vi_ = 0
    pbk = 0
    for n, (b, k) in enumerate(blocks):
        i = n % 2
        if n + 1 < len(blocks):
            load(n + 1)
        t0, lo, hi, first, last = self.blk_range(k)
        j = 2 if k == 0 else b
        A, sh, _ = self.mod_ab(l, 0, j)
        self.norm_block(nt, xs[i], Bxs[i], NH_, A, sh, hf_, Bhf, 6)
        if first:
            S.op("dve", lambda: A_.memset(hf_[:, :, 0:1], 0.0), [], [Bhf])
        if last:
            S.op("dve", lambda: A_.memset(hf_[:, :, NH_ - 1:NH_], 0.0), [], [Bhf])
        S.op("dve", lambda: A_.tensor_tensor(out=dx, in0=hf_[:, :, 0:BLK], in1=hf_[:, :, 2:2 + BLK], op=ALU.add), [Bhf], [Bdx])
        S.op("dve", lambda: A_.scalar_tensor_tensor(out=dx, in0=dx, scalar=0.5, in1=hf_[:, :, 1:1 + BLK], op0=ALU.mult, op1=ALU.subtract), [Bhf, Bdx], [Bdx])
        for jj in range(6):
            for c in range(8):
                S.op("dve", lambda: A_.scalar_tensor_tensor(out=xj[jj][:, c, :], in0=dx[:, c, :], scalar=self.pv(f"rw_mu{jj}", c), in1=hf_[:, c, 1:1 + BLK], op0=ALU.mult, op1=ALU.add),
                     [Bdx, Bhf], [Bxj[jj]])
        for d in range(2):
            for kc in range(8):
                S.op("pe", lambda: nc.tensor.matmul(self.PS[5][0:64, d * BLK:(d + 1) * BLK], lhsT=W1[:, kc, d, :], rhs=xj[1][:, kc, :], start=(kc == 0), stop=(kc == 7)), [Bsw, Bxj[1]], [self.BPS[5]])
        S.op("act", lambda: nc.scalar.activation(out=lt[:, 0:2, :], in_=self.PS[5][0:64, :].rearrange("p (d t) -> p d t", d=2), func=AF.Tanh), [self.BPS[5]], [Blt])
        for d in range(2):
            for kc in range(8):
                S.op("pe", lambda: nc.tensor.matmul(self.PS[5][0:64, d * BLK:(d + 1) * BLK], lhsT=A1[:, kc, d, :], rhs=xj[4][:, kc, :], start=(kc == 0), stop=(kc == 7)), [Bsw, Bxj[4]], [self.BPS[5]])
        S.op("act", lambda: nc.scalar.copy(out=lt[:, 2:4, :], in_=self.PS[5][0:64, :].rearrange("p (d t) -> p d t", d=2)), [self.BPS[5]], [Blt])
        for kc in range(8):
            S.op("pe", lambda: nc.tensor.matmul(self.PS[5][:, 0:BLK], lhsT=G1[:, kc, 0:128], rhs=xj[5][:, kc, :], start=(kc == 0), stop=(kc == 7)), [Bsw, Bxj[5]], [self.BPS[5]])
        for kc in range(8):
            S.op("pe", lambda: nc.tensor.matmul(self.PS[5][0:32, BLK:2 * BLK], lhsT=G1[:, kc, 128:160], rhs=xj[5][:, kc, :], start=(kc == 0), stop=(kc == 7)), [Bsw, Bxj[5]], [self.BPS[5]])
        S.op("act", lambda: nc.scalar.activation(out=gh, in_=self.PS[5][:, 0:BLK], func=AF.Sigmoid), [self.BPS[5]], [Blt])
        S.op("act", lambda: nc.scalar.activation(out=lt[0:32, 4, :], in_=self.PS[5][0:32, BLK:2 * BLK], func=AF.Sigmoid), [self.BPS[5]], [Blt])
        col = b * T + t0
        for c in range(8):
            s_ = si % 2
            si += 1
            sg_ = stg[s_]
            Bs = Bstg[s_]
            cs = slice(c * 128, (c + 1) * 128)

            def bank():
                nonlocal pbk
                pbk += 1
                return pbk % 5

            prk = []
            for which, xsrc in ((0, 0), (1, 2), (2, 3)):
                pb = bank()
                for kc in range(8):
                    S.op("pe", lambda: nc.tensor.matmul(self.PS[pb][:, 0:BLK], lhsT=Wrkv[:, kc, which * D + c * 128:which * D + (c + 1) * 128], rhs=xj[xsrc][:, kc, :], start=(kc == 0), stop=(kc == 7)),
                         [BWrkv[which * 2 + (c // 4)], Bxj[xsrc]], [self.BPS[pb]])
                prk.append(pb)
            S.op("act", lambda: nc.scalar.copy(out=sg_[:, 0, :], in_=self.PS[prk[0]][:, 0:BLK]), [self.BPS[prk[0]]], [Bs])
            S.op("act", lambda: nc.scalar.copy(out=sg_[:, 8, :], in_=self.PS[prk[2]][:, 0:BLK]), [self.BPS[prk[2]]], [Bs])
            kraw = tmp[0]
            S.op("act", lambda: nc.scalar.copy(out=kraw, in_=self.PS[prk[1]][:, 0:BLK]), [self.BPS[prk[1]]], [Btmp[0]])
            S.op("dve", lambda: A_.tensor_scalar(out=tmp[1], in0=kraw, scalar1=self.pv("rw_k_k", c), scalar2=None, op0=ALU.mult), [Btmp[0]], [Btmp[1]])
            S.op("act", lambda: nc.scalar.activation(out=sqb, in_=tmp[1], func=AF.Square), [Btmp[1]], [Bsqb])
            pb = bank()
            S.op("pe", lambda: nc.tensor.matmul(self.PS[pb][:, 0:BLK], lhsT=blk64b, rhs=sqb, start=True, stop=True), [Bsqb, Bb64], [self.BPS[pb]])
            S.op("act", lambda: nc.scalar.activation(out=tmp[2], in_=self.PS[pb][:, 0:BLK], func=AF.Sqrt), [self.BPS[pb]], [Btmp[2]])
            S.op("dve", lambda: A_.tensor_scalar(out=tmp[2], in0=tmp[2], scalar1=1e-12, scalar2=None, op0=ALU.max), [Btmp[2]], [Btmp[2]])
            S.op("dve", lambda: A_.reciprocal(out=tmp[2], in_=tmp[2]), [Btmp[2]], [Btmp[2]])
            S.op("dve", lambda: A_.tensor_tensor(out=sg_[:, 5, :], in0=tmp[1], in1=tmp[2], op=ALU.mult), [Btmp[1], Btmp[2]], [Bs])
            pb = bank()
            S.op("pe", lambda: nc.tensor.matmul(self.PS[pb][:, 0:BLK], lhsT=G2a[:, cs], rhs=gh, start=True, stop=False), [Bsw, Blt], [self.BPS[pb]])
            S.op("pe", lambda: nc.tensor.matmul(self.PS[pb][:, 0:BLK], lhsT=G2b[:, cs], rhs=lt[0:32, 4, :], start=False, stop=True), [Bsw, Blt], [self.BPS[pb]])
            S.op("act", lambda: nc.scalar.copy(out=sg_[:, 9, :], in_=self.PS[pb][:, 0:BLK]), [self.BPS[pb]], [Bs])
            for d in range(2):
                pb = bank()
                S.op("pe", lambda: nc.tensor.matmul(self.PS[pb][:, 0:BLK], lhsT=W2[:, d, cs], rhs=lt[:, d, :], start=True, stop=True), [Bsw, Blt], [self.BPS[pb]])
                S.op("act", lambda: nc.scalar.activation(out=tmp[3], in_=self.PS[pb][:, 0:BLK], func=AF.Sigmoid, bias=self.pv(f"rw_w0_{d}", c)), [self.BPS[pb]], [Btmp[3]])
                S.op("dve", lambda: A_.tensor_scalar(out=sg_[:, 6 + d, :], in0=tmp[3], scalar1=-float(np.exp(-0.5)), scalar2=None, op0=ALU.mult), [Btmp[3]], [Bs])
                pb = bank()
                S.op("pe", lambda: nc.tensor.matmul(self.PS[pb][:, 0:BLK], lhsT=A2[:, d, cs], rhs=lt[:, 2 + d, :], start=True, stop=True), [Bsw, Blt], [self.BPS[pb]])
                S.op("act", lambda: nc.scalar.activation(out=tmp[4], in_=self.PS[pb][:, 0:BLK], func=AF.Sigmoid, bias=self.pv(f"rw_a0_{d}", c)), [self.BPS[pb]], [Btmp[4]])
                S.op("dve", lambda: A_.tensor_tensor(out=sg_[:, 3 + d, :], in0=tmp[4], in1=sg_[:, 5, :], op=ALU.mult), [Btmp[4], Bs], [Bs])
                S.op("dve", lambda: A_.tensor_scalar(out=tmp[5], in0=tmp[4], scalar1=-1.0, scalar2=None, op0=ALU.add), [Btmp[4]], [Btmp[5]])
                S.op("dve", lambda: A_.tensor_scalar(out=tmp[5], in0=tmp[5], scalar1=self.pv("rw_k_a", c), scalar2=1.0, op0=ALU.mult, op1=ALU.add), [Btmp[5]], [Btmp[5]])
                S.op("dve", lambda: A_.tensor_tensor(out=sg_[:, 1 + d, :], in0=tmp[5], in1=kraw, op=ALU.mult), [Btmp[5], Btmp[0]], [Bs])
            S.dma("pool", RWP[:, c * 128:(c + 1) * 128, col:col + BLK].rearrange("a p t -> p a t"), sg_, reads=[Bs])
        for tt in range(BLK // 128):
            vi = vi_ % 2
            vi_ += 1
            for hfv in range(2):
                pb = 4 - hfv
                for kc in range(8):
                    S.op("pe", lambda: nc.tensor.matmul(self.PS[pb][:, 0:512], lhsT=xj[3][:, kc, tt * 128:(tt + 1) * 128], rhs=Wrkv[:, kc, 2 * D + hfv * 512:2 * D + (hfv + 1) * 512], start=(kc == 0), stop=(kc == 7)),
                         [BWrkv[4 + hfv], Bxj[3]], [self.BPS[pb]])
                S.op("act", lambda: nc.scalar.copy(out=vts[vi][:, hfv * 512:(hfv + 1) * 512], in_=self.PS[pb][:, 0:512]), [self.BPS[pb]], [Bvts[vi]])
            S.dma("pool", Vtm[col + tt * 128:col + (tt + 1) * 128, :], vts[vi], reads=[Bvts[vi]])
    st.close()


def _rwkv_mixer(self, l, xin, og):
    RWP = self.scr("rwP", [10, D, TT])
    Vtm = self.scr("rwV", [TT, D], BF16)
    self.rwkv_proj(l, xin, RWP, Vtm)
    if getattr(self, "rw_stop", 0) == 1:
        return
    self.rwkv_scan(RWP, Vtm, og)


Prog.rwkv_proj = _rwkv_proj
Prog.rwkv_mixer = _rwkv_mixer


def _rwkv_scan(self, RWP, Vtm, og):
    nc, S = self.nc, self.S
    A_ = nc.vector
    U32 = mybir.dt.uint32
    RWD = self.scr("rwD", [NB, 8, 2, 2, 128, NCH * 128], BF16)
    RWS = self.scr("rwS", [NB, 8, 2, 128, 3 * NCH])
    skipA = getattr(self, "rw_skipA", False)
    st = Stage(self, "r2a")
    smask = st.sb("smask", [128, T])
    Bsm = Buf()
    S.dma("sp", smask, self.cd["scanmask"], writes=[Bsm])
    lw = st.sb("lw", [128, T]); kap = st.sb("kap", [128, T]); rr = st.sb("r", [128, T]); kt = st.sb("kt", [128, T]); be = st.sb("be", [128, T])
    cw = st.sb("cw", [128, T]); cm = st.sb("cm", [128, T]); en = st.sb("en", [128, T]); ex = st.sb("ex", [128, T])
    ABt = [st.sb(f"AB{i}", [128, NCH, 2, CH], BF16) for i in range(2)]
    KBt_ = [st.sb(f"KB{i}", [128, NCH, 2, CH], BF16) for i in range(2)]
    SC = [st.sb(f"SC{i}", [128, 3, NCH]) for i in range(2)]
    Blw, Bkap, Br, Bkt, Bbe, Bcw, Bcm, Ben, Bex = [Buf() for _ in range(9)]
    BAB, BKB, BSC = [[Buf(), Buf()] for _ in range(3)]
    it = 0
    v3 = lambda t_: t_.rearrange("p (c s) -> p c s", s=CH)
    for b in range(0 if skipA else NB):
        cols = slice(b * T, (b + 1) * T)
        for p in range(8):
            rows = slice(p * 128, (p + 1) * 128)
            for d in range(2):
                i = it % 2
                it += 1
                S.dma("sp", lw, RWP[6 + d, rows, cols], writes=[Blw])
                S.dma("sp", kap, RWP[5, rows, cols], writes=[Bkap])
                S.dma("sp", rr, RWP[0, rows, cols], writes=[Br])
                S.dma("sp", kt, RWP[1 + d, rows, cols], writes=[Bkt])
                S.dma("sp", be, RWP[3 + d, rows, cols], writes=[Bbe])
                S.op("dve", lambda: A_.tensor_tensor_scan(out=cw, data0=smask, data1=lw, initial=0.0, op0=ALU.mult, op1=ALU.add), [Bsm, Blw], [Bcw])
                if d == 1:
                    S.op("dve", lambda: A_.tensor_tensor(out=cm, in0=lw, in1=cw, op=ALU.subtract), [Blw, Bcw], [Bcm])
                    S.op("dve", lambda: A_.tensor_tensor(out=v3(en), in0=v3(cm), in1=v3(cw)[:, :, CH - 1:CH].to_broadcast([128, NCH, CH]), op=ALU.add), [Bcm, Bcw], [Ben])
                    S.op("dve", lambda: A_.tensor_copy(out=cw, in_=en), [Ben], [Bcw])
                m_idx = 32 if d == 0 else 31
                e_idx = CH - 1 if d == 0 else 0
                c3 = v3(cw)
                S.op("act", lambda: nc.scalar.activation(out=SC[i][:, 0, :], in_=c3[:, :, m_idx], func=AF.Exp), [Bcw], [BSC[i]])
                S.op("act", lambda: nc.scalar.activation(out=SC[i][:, 1, :], in_=c3[:, :, e_idx], func=AF.Exp), [Bcw], [BSC[i]])
                S.op("dve", lambda: A_.tensor_tensor(out=SC[i][:, 2, :], in0=c3[:, :, e_idx], in1=c3[:, :, m_idx], op=ALU.subtract), [Bcw], [BSC[i]])
                S.op("act", lambda: nc.scalar.activation(out=SC[i][:, 2, :], in_=SC[i][:, 2, :], func=AF.Exp), [BSC[i]], [BSC[i]])
                S.dma("pool", RWS[b, p, d], SC[i].rearrange("p a c -> p (a c)"), reads=[BSC[i]])
                S.op("dve", lambda: A_.tensor_tensor(out=v3(cm), in0=c3, in1=c3[:, :, m_idx:m_idx + 1].to_broadcast([128, NCH, CH]), op=ALU.subtract), [Bcw], [Bcm])
                S.op("act", lambda: nc.scalar.activation(out=en, in_=cm, func=AF.Exp, scale=-1.0), [Bcm], [Ben])
                S.op("dve", lambda: A_.tensor_tensor(out=ex, in0=cm, in1=lw, op=ALU.subtract), [Bcm, Blw], [Bex])
                S.op("act", lambda: nc.scalar.activation(out=ex, in_=ex, func=AF.Exp), [Bex], [Bex])
                S.op("act", lambda: nc.scalar.activation(out=cm, in_=cm, func=AF.Exp), [Bcm], [Bcm])
                S.op("dve", lambda: A_.tensor_tensor(out=ABt[i][:, :, 0, :], in0=v3(kap), in1=v3(ex), op=ALU.mult), [Bkap, Bex], [BAB[i]])
                S.op("dve", lambda: A_.tensor_tensor(out=ABt[i][:, :, 1, :], in0=v3(rr), in1=v3(cm), op=ALU.mult), [Br, Bcm], [BAB[i]])
                S.op("dve", lambda: A_.tensor_tensor(out=KBt_[i][:, :, 0, :], in0=v3(kt), in1=v3(en), op=ALU.mult), [Bkt, Ben], [BKB[i]])
                S.op("dve", lambda: A_.tensor_tensor(out=KBt_[i][:, :, 1, :], in0=v3(be), in1=v3(en), op=ALU.mult), [Bbe, Ben], [BKB[i]])
                S.dma("pool", RWD[b, p, d, 0], ABt[i].rearrange("p c a s -> p (c a s)"), reads=[BAB[i]])
                S.dma("pool", RWD[b, p, d, 1], KBt_[i].rearrange("p c a s -> p (c a s)"), reads=[BKB[i]])
    st.close()
    if getattr(self, "rw_stop", 0) == 2:
        return
    st = Stage(self, "r2b")
    S.pe_selfwait = getattr(self, "rw_selfwait", False)
    S.pe_drain = getattr(self, "rw_drain", 2)
    epsLN = st.sb("epsLN", [128, 1])
    Bgl = Buf()
    S.op("dve", lambda: A_.memset(epsLN, RW_LN_EPS), [], [Bgl])
    AB = [st.sb(f"AB{d}", [128, NCH, 128], BF16) for d in range(2)]
    KB = [st.sb(f"KB{d}", [128, NCH, 128], BF16) for d in range(2)]
    SCs = [st.sb(f"SC{d}", [128, 3, NCH]) for d in range(2)]
    Vst = st.sb("Vst", [64, NCH, 128], BF16)
    BABl, BKBl, BSCl = [[Buf(), Buf()] for _ in range(3)]
    BVst = Buf()
    chains = [(hd, d) for hd in range(2) for d in range(2)]
    VU, GGb, AN0, ANp, Xp, Wf, KBtr = {}, {}, {}, {}, {}, {}, {}
    BVU, BGG, BAN0, BANp, BXp, BWf, BKBtr, BST, BS0, BtS, By = [dict() for _ in range(11)]
    for ch in chains:
        nm = f"{ch[0]}{ch[1]}"
        VU[ch] = st.sb("VU" + nm, [128, NCH, CH], BF16)
        GGb[ch] = st.sb("GG" + nm, [128, 128], BF16)
        AN0[ch] = st.sb("AN0" + nm, [128, 128])
        ANp[ch] = [st.sb(f"ANp{q}" + nm, [128, 128]) for q in range(2)]
        Xp[ch] = [st.sb(f"X{q}" + nm, [128, CH]) for q in range(2)]
        Wf[ch] = st.sb("Wf" + nm, [128, CH])
        KBtr[ch] = st.sb("KBt" + nm, [128, CH], BF16)
        BVU[ch], BGG[ch], BAN0[ch], BWf[ch], BKBtr[ch], BST[ch], BS0[ch], BtS[ch], By[ch] = [Buf() for _ in range(9)]
        BANp[ch] = [Buf(), Buf()]
        BXp[ch] = [Buf(), Buf()]
        S.op("dve", lambda: A_.memset(GGb[ch], 0.0), [], [BGG[ch]])
        S.op("dve", lambda: A_.memset(AN0[ch], 0.0), [], [BAN0[ch]])
    ST = [st.sb(f"ST{d}", [128, CH]) for d in range(2)]
    S0m = [st.sb(f"S0m{d}", [128, CH], BF16) for d in range(2)]
    tS = [st.sb(f"tS{d}", [128, CH]) for d in range(2)]
    yacc = [st.sb(f"yacc{d}", [128, T]) for d in range(2)]
    rl = st.sb("rl", [128, T]); k0 = st.sb("k0", [128, T]); k1 = st.sb("k1", [128, T]); vf = st.sb("vf", [128, T]); gg = st.sb("gg", [128, T])
    t0_ = st.sb("t0", [128, T]); t1_ = st.sb("t1", [128, T])
    ogb = st.sb("ogb", [128, T], BF16)
    Brl, Bk0, Bk1, Bvf, Bgg, Bt0, Bt1, Bogb = [Buf() for _ in range(8)]
    R = {}
    BR = {}
    for ci, ch in enumerate(chains):
        b0, b1 = self.PS[2 * ci], self.PS[2 * ci + 1]
        R[ch] = dict(GA=b0[:, 0:128], LV=b0[:, 192:320], Wp=b0[:, 384:448],
                     XL=b1[:, 320:384], Up=b1[:, 448:512], Nn=b1[:, 128:192],
                     Yp=b1[:, 0:64], Sd=b1[:, 64:128], TR=b1.bitcast(BF16)[:, 512:576])
        u0, u1, u2, u3 = Buf(), Buf(), Buf(), Buf()
        ykp = [u2] if ch[0] == 0 else [u3]
        BR[ch] = dict(GAlo=[u0], GAup=[u1], GA=[u0, u1], LV=[u1], Wp=[u1], XL=[u3], Up=[u3], Nn=[u3], Yp=ykp, Sd=ykp, TR=[u2, u3], ALL=[u0, u1, u2, u3])
    up, lo = slice(64, 128), slice(0, 64)
    mU = lambda ap: ap.bitcast(U32)
    cf = list(range(NCH))
    cbk = list(range(TC // CH - 1, -1, -1)) + list(range(NCH - 1, TC // CH - 1, -1))
    order = [cf, cbk]
    dbgn = getattr(self, "rw_dbg", None)
    for b in range(NB):
        cols = slice(b * T, (b + 1) * T)
        for p in range(8):
            if dbgn is not None and (b * 8 + p) >= dbgn[0]:
                continue
            rows = slice(p * 128, (p + 1) * 128)
            for d in range(2):
                S.dma("sp", AB[d], RWD[b, p, d, 0].rearrange("k (c x) -> k c x", x=128), writes=[BABl[d]])
                S.dma("sp", KB[d], RWD[b, p, d, 1].rearrange("k (c x) -> k c x", x=128), writes=[BKBl[d]])
                S.dma("sp", SCs[d], RWS[b, p, d].rearrange("k (a c) -> k a c", a=3), writes=[BSCl[d]])
            S.dma("sp", Vst, Vtm[cols, rows].rearrange("(c s) v -> s c v", s=CH), writes=[BVst])
            S.dma("sp", rl, RWP[0, rows, cols], writes=[Brl])
            S.dma("sp", k0, RWP[1, rows, cols], writes=[Bk0])
            S.dma("sp", k1, RWP[2, rows, cols], writes=[Bk1])
            S.dma("sp", vf, RWP[8, rows, cols], writes=[Bvf])
            S.dma("sp", gg, RWP[9, rows, cols], writes=[Bgg])
            for ch in chains:
                hd, d = ch
                kp = slice(hd * 64, hd * 64 + 64)
                S.op("pool", lambda: nc.gpsimd.tensor_copy(out=VU[ch][lo, :, :], in_=Vst[:, :, hd * 64:(hd + 1) * 64]), [BVst], [BVU[ch]])
                S.op("dve", lambda: A_.memset(ST[d][kp, :], 0.0), [], [BST[ch]])
                S.op("dve", lambda: A_.memset(S0m[d][kp, :], 0.0), [], [BS0[ch]])
            def chain_step(ch, step):
                hd, d = ch
                kp = slice(hd * 64, hd * 64 + 64)
                c = order[d][step]
                cs = slice(c * CH, (c + 1) * CH)
                r_, br_ = R[ch], BR[ch]
                M4 = self.masks[:, 0:128] if d == 0 else self.masks[:, 128:256]
                mA = self.masks[up, 0:64] if d == 0 else self.masks[up, 128:192]
                mN = self.masks[up, 128:192] if d == 0 else self.masks[up, 0:64]
                S.op("pe", lambda: nc.tensor.transpose(out=r_["TR"], in_=KB[d][kp, c, :], identity=self.identb[kp, kp]), [BKBl[d]], br_["TR"], pemode=("T", hd))
                S.op("act", lambda: nc.scalar.copy(out=KBtr[ch], in_=r_["TR"]), br_["TR"], [BKBtr[ch]])
                S.op("pe", lambda: nc.tensor.matmul(r_["GA"][lo, :], lhsT=KB[d][kp, c, 0:64], rhs=AB[d][kp, c, :], start=True, stop=True), [BKBl[d], BABl[d]], br_["GAlo"], pemode=("g", hd))
                S.op("pe", lambda: nc.tensor.matmul(r_["GA"][up, :], lhsT=KB[d][kp, c, 64:128], rhs=AB[d][kp, c, :], start=True, stop=True), [BKBl[d], BABl[d]], br_["GAup"], pemode=("g", hd))
                S.op("pe", lambda: nc.tensor.matmul(r_["Nn"][up, :], lhsT=AB[d][kp, c, 0:64], rhs=KB[d][kp, c, 64:128], start=True, stop=True), [BKBl[d], BABl[d]], br_["Nn"], pemode=("g", hd))
                yield
                S.op("dve", lambda: A_.copy_predicated(out=GGb[ch], mask=mU(M4), data=r_["GA"]), br_["GA"], [BGG[ch]])
                S.op("dve", lambda: A_.copy_predicated(out=AN0[ch][up, 0:64], mask=mU(mA), data=r_["GA"][up, 0:64]), br_["GAup"], [BAN0[ch]])
                S.op("dve", lambda: A_.copy_predicated(out=AN0[ch][up, 64:128], mask=mU(mN), data=r_["Nn"][up, :]), br_["Nn"], [BAN0[ch]])
                S.op("dve", lambda: A_.tensor_tensor(out=Xp[ch][0][up, :], in0=self.ident[up, up], in1=AN0[ch][up, 0:64], op=ALU.subtract), [BAN0[ch]], [BXp[ch][0]])
                yield
                cur, Bcur = AN0[ch], BAN0[ch]
                xq = 0
                for lv in range(1, 6):
                    nx, Bnx = ANp[ch][lv % 2], BANp[ch][lv % 2]
                    if lv < 5:
                        S.op("pe", lambda: nc.tensor.matmul(r_["LV"][up, 0:64], lhsT=cur[up, 64:128], rhs=cur[up, 0:64], start=True, stop=True), [Bcur], br_["LV"], pemode=("f",))
                    S.op("pe", lambda: nc.tensor.matmul(r_["LV"][up, 64:128], lhsT=cur[up, 0:64], rhs=cur[up, 64:128], start=True, stop=True), [Bcur], br_["LV"], pemode=("f",))
                    yield
                    if lv < 5:
                        S.op("act", lambda: nc.scalar.copy(out=nx[up, :], in_=r_["LV"][up, :]), br_["LV"], [Bnx])
                    else:
                        S.op("act", lambda: nc.scalar.copy(out=nx[up, 64:128], in_=r_["LV"][up, 64:128]), br_["LV"], [Bnx])
                    yield
                    S.op("pe", lambda: nc.tensor.matmul(r_["XL"][up, :], lhsT=nx[up, 64:128], rhs=Xp[ch][xq][up, :], start=True, stop=True), [Bnx, BXp[ch][xq]], br_["XL"], pemode=("f",))
                    yield
                    S.op("dve", lambda: A_.tensor_tensor(out=Xp[ch][1 - xq][up, :], in0=r_["XL"][up, :], in1=Xp[ch][xq][up, :], op=ALU.add), br_["XL"] + [BXp[ch][xq]], [BXp[ch][1 - xq]])
                    xq = 1 - xq
                    cur, Bcur = nx, Bnx
                yield
                S.op("pe", lambda: nc.tensor.matmul(r_["Wp"][up, :], lhsT=AB[d][kp, c, 0:64], rhs=S0m[d][kp, :], start=True, stop=False), [BABl[d], BS0[ch]], br_["Wp"], pemode=("g", hd))
                S.op("pe", lambda: nc.tensor.matmul(r_["Wp"][up, :], lhsT=GGb[ch][lo, 0:64], rhs=VU[ch][lo, c, :], start=False, stop=True), [BGG[ch], BVU[ch]], br_["Wp"], pemode=("w2",))
                yield
                S.op("act", lambda: nc.scalar.copy(out=Wf[ch][up, :], in_=r_["Wp"][up, :]), br_["Wp"], [BWf[ch]])
                yield
                S.op("pe", lambda: nc.tensor.matmul(r_["Up"][up, :], lhsT=Xp[ch][xq][up, :], rhs=Wf[ch][up, :], start=True, stop=True), [BXp[ch][xq], BWf[ch]], br_["Up"], pemode=("f",))
                yield
                S.op("act", lambda: nc.scalar.activation(out=VU[ch][up, c, :], in_=r_["Up"][up, :], func=AF.Copy, scale=-1.0), br_["Up"], [BVU[ch]])
                yield
                S.op("pe", lambda: nc.tensor.matmul(r_["Yp"][kp, :], lhsT=S0m[d][kp, :], rhs=AB[d][kp, c, 64:128], start=True, stop=False), [BS0[ch], BABl[d]], br_["Yp"], pemode=("g", hd))
                S.op("pe", lambda: nc.tensor.matmul(r_["Yp"][kp, :], lhsT=VU[ch][:, c, :], rhs=GGb[ch][:, 64:128], start=False, stop=True), [BVU[ch], BGG[ch]], br_["Yp"], pemode=("full",))
                yield
                S.op("act", lambda: nc.scalar.copy(out=yacc[d][kp, cs], in_=r_["Yp"][kp, :]), br_["Yp"], [By[ch]])
                S.op("pe", lambda: nc.tensor.matmul(r_["Sd"][kp, :], lhsT=KBtr[ch], rhs=VU[ch][:, c, :], start=True, stop=True), [BKBtr[ch], BVU[ch]], br_["Sd"], pemode=("full",))
                yield
                S.op("act", lambda: nc.scalar.activation(out=tS[d][kp, :], in_=r_["Sd"][kp, :], func=AF.Identity, scale=SCs[d][kp, 2, c:c + 1]), br_["Sd"] + [BSCl[d]], [BtS[ch]])
                S.op("dve", lambda: A_.scalar_tensor_tensor(out=ST[d][kp, :], in0=ST[d][kp, :], scalar=SCs[d][kp, 1, c:c + 1], in1=tS[d][kp, :], op0=ALU.mult, op1=ALU.add), [BST[ch], BtS[ch], BSCl[d]], [BST[ch]])
                if step + 1 < NCH:
                    cn = order[d][step + 1]
                    S.op("dve", lambda: A_.tensor_scalar(out=S0m[d][kp, :], in0=ST[d][kp, :], scalar1=SCs[d][kp, 0, cn:cn + 1], scalar2=None, op0=ALU.mult), [BST[ch], BSCl[d]], [BS0[ch]])

            for step in range(NCH if dbgn is None else dbgn[1]):
                gens = [chain_step(ch, step) for ch in chains]
                if getattr(self, "rw_order", "phase") == "chain":
                    for g_ in gens:
                        for _ in g_:
                            pass
                    gens = []
                while gens:
                    for g_ in list(gens):
                        try:
                            next(g_)
                        except StopIteration:
                            gens.remove(g_)
            RB = {0: [BR[chains[0]]["ALL"][0], BR[chains[0]]["ALL"][1]], 1: [BR[chains[0]]["ALL"][2], BR[chains[0]]["ALL"][3]]}
            By_all = [By[ch] for ch in chains]
            Byy = Buf()
            S.op("dve", lambda: A_.tensor_tensor(out=yacc[0], in0=yacc[0], in1=yacc[1], op=ALU.add), By_all, [Byy])
            NP_ = 6
            W_ = T // NP_
            for pc in range(NP_):
                sl_ = slice(pc * W_, (pc + 1) * W_)
                pb = pc % 2
                S.op("pe", lambda: nc.tensor.matmul(self.PS[pb][:, 0:W_], lhsT=self.blk64, rhs=yacc[0][:, sl_], start=True, stop=True), [Byy], RB[pb])
                S.op("dve", lambda: A_.scalar_tensor_tensor(out=t0_[:, sl_], in0=self.PS[pb][:, 0:W_], scalar=-1.0 / 64, in1=yacc[0][:, sl_], op0=ALU.mult, op1=ALU.add), RB[pb] + [Byy], [Bt0])
            S.op("act", lambda: nc.scalar.activation(out=t1_, in_=t0_, func=AF.Square), [Bt0], [Bt1])
            for pc in range(NP_):
                sl_ = slice(pc * W_, (pc + 1) * W_)
                pb = pc % 2
                S.op("pe", lambda: nc.tensor.matmul(self.PS[pb][:, 0:W_], lhsT=self.blk64, rhs=t1_[:, sl_], start=True, stop=True), [Bt1], RB[pb])
                S.op("act", lambda: nc.scalar.activation(out=yacc[1][:, sl_], in_=self.PS[pb][:, 0:W_], func=AF.Sqrt, scale=1.0 / 64, bias=epsLN), RB[pb] + [Bgl], [Byy])
            S.op("dve", lambda: A_.reciprocal(out=yacc[1], in_=yacc[1]), [Byy], [Byy])
            S.op("dve", lambda: A_.tensor_tensor(out=t0_, in0=t0_, in1=yacc[1], op=ALU.mult), [Bt0, Byy], [Bt0])
            S.op("act", lambda: nc.scalar.activation(out=t0_, in_=t0_, func=AF.Identity, scale=self.pv("rw_ln_w", p), bias=self.pv("rw_ln_b", p)), [Bt0], [Bt0])
            S.op("dve", lambda: A_.tensor_tensor(out=k0, in0=k0, in1=k1, op=ALU.add), [Bk0, Bk1], [Bk0])
            S.op("dve", lambda: A_.scalar_tensor_tensor(out=t1_, in0=rl, scalar=self.pv("rw_r_k", p), in1=k0, op0=ALU.mult, op1=ALU.mult), [Brl, Bk0, Bt1], [Bt1])
            for pc in range(NP_):
                sl_ = slice(pc * W_, (pc + 1) * W_)
                pb = pc % 2
                S.op("pe", lambda: nc.tensor.matmul(self.PS[pb][:, 0:W_], lhsT=self.blk64, rhs=t1_[:, sl_], start=True, stop=True), [Bt1], RB[pb])
                S.op("dve", lambda: A_.tensor_tensor(out=yacc[1][:, sl_], in0=self.PS[pb][:, 0:W_], in1=vf[:, sl_], op=ALU.mult), RB[pb] + [Bvf, Byy], [Byy])
            S.op("dve", lambda: A_.tensor_tensor(out=t0_, in0=t0_, in1=yacc[1], op=ALU.add), [Bt0, Byy], [Bt0])
            S.op("dve", lambda: A_.tensor_tensor(out=ogb, in0=t0_, in1=gg, op=ALU.mult), [Bt0, Bgg], [Bogb])
            S.dma("pool", og[rows, cols], ogb, reads=[Bogb])
            for ch in chains:
                By[ch].r.append(Byy.w)
    st.close()
    S.pe_selfwait = False
    S.pe_drain = 0


Prog.rwkv_scan = _rwkv_scan
```

```python
from contextlib import ExitStack
import numpy as np
import concourse.bass as bass
import concourse.mybir as mybir
from concourse.bass_utils import run_bass_kernel_spmd

F32 = mybir.dt.float32
BF16 = mybir.dt.bfloat16
AF = mybir.ActivationFunctionType
ALU = mybir.AluOpType

NCORES = 8
NB = 2
TC = 256
TL = 2048
T = TC + TL
TT = NB * T
D = 1024
DEPTH = 4
DFF = 2816
NFC = DFF // 128
BLK = 256
NBLK = T // BLK
EPS = 1e-6
CH = 64
NCH = T // CH
RW_LN_EPS = 64e-5
MLA_SCALE = 96 ** -0.5


class Buf:
    __slots__ = ("name", "w", "r")

    def __init__(self, name=""):
        self.name = name
        self.w = None
        self.r = []


class _Eng:
    def __init__(self, S, name, eng):
        self.S = S
        self.name = name
        self.eng = eng
        self.sem = None
        self.count = 0
        self.seen = {}
        self.nsem = 0
        self.ninst = 0
        self.own = set()

    def new_sem(self):
        self.sem = self.S.nc.alloc_semaphore(f"e_{self.name}_{self.nsem}")
        self.own.add(id(self.sem))
        self.nsem += 1
        self.count = 0

    def wait(self, ev):
        sem, val = ev
        k = id(sem)
        if self.name == "pe" and k in self.own and not self.S.pe_selfwait:
            return
        if self.seen.get(k, 0) >= val:
            return
        self.eng.wait_ge(sem, val)
        self.seen[k] = val


class Sched:
    EPOCH = 30000

    def __init__(self, nc, ndma_sems=48):
        self.nc = nc
        self.E = {}
        for name, eng in (("pe", nc.tensor), ("dve", nc.vector), ("act", nc.scalar),
                          ("pool", nc.gpsimd), ("sp", nc.sync)):
            e = _Eng(self, name, eng)
            e.new_sem()
            self.E[name] = e
        self.dsems = [[nc.alloc_semaphore(f"d{i}"), 0] for i in range(ndma_sems)]
        self.dnext = 0
        self._keep = []
        self.pe_selfwait = False
        self.pe_drain = 0
        self.last_pemode = None

    @staticmethod
    def _deps(reads, writes):
        deps = []
        for b in reads:
            if b.w is not None:
                deps.append(b.w)
        for b in writes:
            if b.w is not None:
                deps.append(b.w)
            deps.extend(b.r)
        return deps

    @staticmethod
    def _mark(ev, reads, writes):
        for b in writes:
            b.w = ev
            b.r = []
        for b in reads:
            if b not in writes:
                b.r.append(ev)
                if len(b.r) > 32:
                    b.r = b.r[-32:]

    def op(self, ename, fn, reads=(), writes=(), pemode=None):
        e = self.E[ename]
        for ev in self._deps(reads, writes):
            e.wait(ev)
        drain = False
        if ename == "pe":
            drain = self.pe_drain == 1 or (self.pe_drain == 2 and pemode != self.last_pemode)
            self.last_pemode = pemode
        if drain and e.count > 0:
            k = id(e.sem)
            if e.seen.get(k, 0) < e.count:
                e.eng.wait_ge(e.sem, e.count)
                e.seen[k] = e.count
        if e.count >= self.EPOCH:
            self._keep.append(e.sem)
            e.new_sem()
        inst = fn()
        e.count += 1
        e.ninst += 1
        inst.then_inc(e.sem, 1)
        ev = (e.sem, e.count)
        self._mark(ev, reads, writes)
        return ev

    def dma(self, qname, out, in_, reads=(), writes=(), **kw):
        q = self.E[qname]
        for ev in self._deps(reads, writes):
            q.wait(ev)
        slot = self.dsems[self.dnext % len(self.dsems)]
        self.dnext += 1
        if slot[1] >= self.EPOCH:
            self._keep.append(slot[0])
            slot[0] = self.nc.alloc_semaphore(f"dx{self.dnext}")
            slot[1] = 0
        if slot[1] > 0:
            q.wait((slot[0], slot[1]))
        q.eng.dma_start(out=out, in_=in_, **kw).then_inc(slot[0], 16)
        q.ninst += 1
        slot[1] += 16
        ev = (slot[0], slot[1])
        self._mark(ev, reads, writes)
        return ev

    def barrier(self):
        evs = [(e.sem, e.count) for e in self.E.values() if e.count > 0]
        evs += [(s[0], s[1]) for s in self.dsems if s[1] > 0]
        for e in self.E.values():
            for ev in evs:
                if ev[0] is e.sem:
                    continue
                e.wait(ev)


class PVec:
    def __init__(self):
        self.cols = []
        self.off = {}
        self.n = 0

    def add(self, name, vec):
        vec = np.asarray(vec, dtype=np.float32).reshape(-1)
        assert vec.size % 128 == 0
        nch = vec.size // 128
        self.off[name] = (self.n, nch)
        self.cols.append(np.ascontiguousarray(vec.reshape(nch, 128).T))
        self.n += nch

    def array(self):
        return np.ascontiguousarray(np.concatenate(self.cols, axis=1))


def pvec_layout(inputs):
    pv = PVec()
    for l in range(DEPTH):
        pv.add(f"b_mod{l}", inputs["b_mod"][l])
        pv.add(f"norm1_{l}", inputs["norm1"][l])
        pv.add(f"norm2_{l}", inputs["norm2"][l])
        for k in range(3):
            pv.add(f"conv{l}_{k}", inputs["ffn_conv"][l, k])
        pv.add(f"convb{l}", inputs["ffn_conv_b"][l])
    pv.add("norm_f", inputs["norm_f"])
    for d in range(2):
        for j in range(2):
            pv.add(f"hg_lb{d}_{j}", inputs["hg_lb"][d, j])
    for j in range(2):
        pv.add(f"hg_norm{j}", inputs["hg_norm"][j])
    for k in range(6):
        pv.add(f"rw_mu{k}", inputs["rw_mu"][0, k])
    for d in range(2):
        pv.add(f"rw_w0_{d}", inputs["rw_w0"][0, d])
        pv.add(f"rw_a0_{d}", inputs["rw_a0"][0, d])
    for nm in ("rw_k_k", "rw_k_a", "rw_r_k", "rw_ln_w", "rw_ln_b"):
        pv.add(nm, inputs[nm][0])
    pv.add("mla_q_norm", inputs["mla_q_norm"][0])
    pv.add("mla_kv_norm", inputs["mla_kv_norm"][0])
    return pv


def make_consts():
    c = {}
    c["ident"] = np.eye(128, dtype=np.float32)
    c["ones"] = np.ones((128, 128), dtype=np.float32)
    bo = np.zeros((128, 128), dtype=np.float32)
    bo[:64, :64] = 1.0
    bo[64:, 64:] = 1.0
    c["blk64"] = bo
    i = np.arange(64)[:, None]
    t = np.arange(64)[None, :]
    su = (i < t).astype(np.float32)
    iu = (i <= t).astype(np.float32)
    sl = (i > t).astype(np.float32)
    il = (i >= t).astype(np.float32)
    c["masks"] = np.concatenate([np.concatenate([su, iu, sl, il], axis=1)] * 2, axis=0)
    m = np.ones((128, T), dtype=np.float32)
    m[:, ::CH] = 0.0
    c["scanmask"] = m
    nq = 8
    inv_freq = (10000.0 ** (-np.arange(nq, dtype=np.float32) / nq)).astype(np.float32)
    pos = np.arange(TL)
    row = (pos // 64).astype(np.float32)
    col = (pos % 64).astype(np.float32)
    ang_r = row[:, None] * inv_freq
    ang_c = col[:, None] * inv_freq
    ang = np.concatenate([ang_r, ang_r, ang_c, ang_c], axis=-1).astype(np.float32)
    cos = np.ones((32, T), dtype=np.float32)
    sin = np.zeros((32, T), dtype=np.float32)
    cos[:, TC:] = np.cos(ang).T
    sin[:, TC:] = np.sin(ang).T
    c["rope_cos"] = cos
    c["rope_sin"] = sin
    return c


WEIGHT_NAMES = ["w_mod", "ffn_w_in", "ffn_w_out", "hg_w_in", "hg_w_o", "rw_w_rkv", "rw_w1", "rw_w2",
                "rw_a1", "rw_a2", "rw_g1", "rw_g2", "rw_w_o", "mla_w_dqkv", "mla_w_uq", "mla_w_ukv", "mla_w_o"]


class Stage:
    def __init__(self, P, name):
        self.P = P
        self.name = name
        self.es = ExitStack()
        P.nstage += 1
        self.k = 0

    def sb(self, name, shape, dt=F32):
        self.k += 1
        h = self.es.enter_context(self.P.nc.sbuf_tensor(f"{self.name}{self.P.nstage}_{name}_{self.k}", list(shape), dt))
        return h.ap()

    def close(self):
        self.P.S.barrier()
        self.es.close()


class Prog:
    def __init__(self, wshapes, pv_off, npv, dbg=(), xin_name=None):
        nc = bass.Bass("TRN2", target_bir_lowering=False)
        self.nc = nc
        self.dbg = set(dbg)
        self.pv_off = pv_off
        self.nstage = 0
        di = lambda n, s: nc.dram_tensor(n, list(s), F32, kind="ExternalInput").ap()
        self.x = di("x", [NB, TL, D])
        self.ctx = di("ctx", [NB, TC, D])
        self.cvec = di("cvec", [3, D])
        self.pvec_d = di("pvec", [128, npv])
        self.cd = {n: di("c_" + n, s) for n, s in (("ident", [128, 128]), ("ones", [128, 128]), ("blk64", [128, 128]),
                                                    ("masks", [128, 256]), ("scanmask", [128, T]),
                                                    ("rope_cos", [32, T]), ("rope_sin", [32, T]))}
        self.W = {n: di(n, wshapes[n]) for n in WEIGHT_NAMES}
        self.out = nc.dram_tensor("out", [NB, TL, D], F32, kind="ExternalOutput").ap()
        self.scratch = {}
        self.S = Sched(nc)
        S = self.S
        self.PS = [nc.alloc_psum_tensor(f"psb{i}", [128, 512], F32).ap() for i in range(8)]
        self.BPS = [Buf(f"ps{i}") for i in range(8)]
        g = lambda n, s, dt=F32: nc.alloc_sbuf_tensor("g_" + n, list(s), dt).ap()
        self.ident = g("ident", [128, 128])
        self.identb = g("identb", [128, 128], BF16)
        self.onesf = g("onesf", [128, 128])
        self.onesb = g("onesb", [128, 128], BF16)
        self.blk64 = g("blk64", [128, 128])
        self.masks = g("masks", [128, 256])
        self.pvec = g("pvec", [128, npv])
        self.MOD = g("MOD", [128, DEPTH, 48, 3])
        self.MA = g("MA", [128, DEPTH, 2, 8, 3])
        self.epsD = g("epsD", [128, 1])
        self.BC = Buf("consts")
        self.BMOD = Buf("mod")
        S.op("dve", lambda: nc.vector.memset(self.epsD, EPS), [], [self.BC])
        S.dma("sp", self.ident, self.cd["ident"], writes=[self.BC])
        b1, b2, b3, b4, b5, b6 = [Buf() for _ in range(6)]
        S.dma("sp", self.onesf, self.cd["ones"], writes=[b1])
        S.dma("sp", self.blk64, self.cd["blk64"], writes=[b2])
        S.dma("sp", self.masks, self.cd["masks"], writes=[b3])
        S.dma("sp", self.pvec, self.pvec_d, writes=[b4])
        S.dma("pool", self.identb, self.cd["ident"], writes=[b5])
        S.dma("pool", self.onesb, self.cd["ones"], writes=[b6])
        S.barrier()

    def scr(self, name, shape, dt=F32):
        if name not in self.scratch:
            kind = "ExternalOutput" if name in self.dbg else "Internal"
            self.scratch[name] = self.nc.dram_tensor("s_" + name, list(shape), dt, kind=kind).ap()
        return self.scratch[name]

    def pv(self, name, c=None):
        off, nch = self.pv_off[name]
        if c is None:
            return self.pvec[:, off:off + nch]
        return self.pvec[:, off + c:off + c + 1]

    def load_w(self, dst, src, bufs_cols, q="pool"):
        S = self.S
        n = dst.shape[2]
        v = src.rearrange("(kc p) n -> p kc n", p=128)
        bufs = []
        for n0 in range(0, n, 512):
            n1 = min(n, n0 + 512)
            b = Buf()
            S.dma(q, dst[:, :, n0:n1], v[:, :, n0:n1], writes=[b])
            bufs.append(b)
        return bufs

    def prologue_transpose(self, xT):
        nc, S = self.nc, self.S
        st = Stage(self, "pt")
        tin = [st.sb(f"tin{i}", [128, D]) for i in range(2)]
        tout = [st.sb(f"tout{i}", [128, 8, 128]) for i in range(2)]
        Bin = [Buf(), Buf()]
        Bout = [Buf(), Buf()]
        xTv = xT.rearrange("(c p) t -> p c t", p=128)
        tiles = []
        for b in range(NB):
            for k in range(T // 128):
                tiles.append((b, k))

        def src(b, k):
            t0 = k * 128
            if t0 < TC:
                return self.ctx[b, t0:t0 + 128, :]
            return self.x[b, t0 - TC:t0 - TC + 128, :]

        S.dma("sp", tin[0], src(*tiles[0]), writes=[Bin[0]])
        for n, (b, k) in enumerate(tiles):
            i = n % 2
            if n + 1 < len(tiles):
                S.dma("sp", tin[1 - i], src(*tiles[n + 1]), writes=[Bin[1 - i]])
            for hf in range(2):
                pb = 2 * (n % 2) + hf
                for c4 in range(4):
                    c = hf * 4 + c4
                    S.op("pe", lambda: nc.tensor.transpose(out=self.PS[pb][:, c4 * 128:(c4 + 1) * 128], in_=tin[i][:, c * 128:(c + 1) * 128], identity=self.ident),
                         [Bin[i]], [self.BPS[pb]])
                eng = "dve" if hf == 0 else "act"
                if hf == 0:
                    S.op("dve", lambda: nc.vector.tensor_copy(out=tout[i][:, 0:4, :], in_=self.PS[pb][:].rearrange("p (c t) -> p c t", c=4)), [self.BPS[pb]], [Bout[i]])
                else:
                    S.op("act", lambda: nc.scalar.copy(out=tout[i][:, 4:8, :], in_=self.PS[pb][:].rearrange("p (c t) -> p c t", c=4)), [self.BPS[pb]], [Bout[i]])
            col = b * T + k * 128
            S.dma("pool", xTv[:, :, col:col + 128], tout[i], reads=[Bout[i]])
        st.close()

    def prologue_mod(self):
        nc, S = self.nc, self.S
        st = Stage(self, "pm")
        cv = st.sb("cv", [3, D])
        sc = st.sb("sc", [3, D])
        scT = st.sb("scT", [128, 8, 3])
        Bcv, Bsc, BscT = Buf(), Buf(), Buf()
        S.dma("sp", cv, self.cvec, writes=[Bcv])
        S.op("act", lambda: nc.scalar.activation(out=sc, in_=cv, func=AF.Silu), [Bcv], [Bsc])
        for kc in range(8):
            S.op("pe", lambda: nc.tensor.transpose(out=self.PS[0][:, kc * 4:kc * 4 + 3], in_=sc[0:3, kc * 128:(kc + 1) * 128], identity=self.ident[0:3, 0:3]),
                 [Bsc], [self.BPS[0]])
        S.op("dve", lambda: nc.vector.tensor_copy(out=scT, in_=self.PS[0][:, 0:32].rearrange("p (k f) -> p k f", f=4)[:, :, 0:3]), [self.BPS[0]], [BscT])
        NWB = 4
        wt = [st.sb(f"wt{i}", [128, 8, 512]) for i in range(NWB)]
        Bwt = [Buf() for _ in range(NWB)]
        groups = [(l, g) for l in range(DEPTH) for g in range(12)]

        def wsrc(l, g):
            return self.W["w_mod"][l].rearrange("(kc p) n -> p kc n", p=128)[:, :, g * 512:(g + 1) * 512]

        def wload(n):
            S.dma("sp" if n % 2 == 0 else "act", wt[n % NWB], wsrc(*groups[n]), writes=[Bwt[n % NWB]])

        for n in range(NWB - 1):
            wload(n)
        for n, (l, g) in enumerate(groups):
            i = n % NWB
            if n + NWB - 1 < len(groups):
                wload(n + NWB - 1)
            pb = 1 + (n % 2)
            for oc in range(4):
                for kc in range(8):
                    S.op("pe", lambda: nc.tensor.matmul(self.PS[pb][:, oc * 4:oc * 4 + 3], lhsT=wt[i][:, kc, oc * 128:(oc + 1) * 128], rhs=scT[:, kc, :], start=(kc == 0), stop=(kc == 7)),
                         [Bwt[i], BscT], [self.BPS[pb]])
            boff, _ = self.pv_off[f"b_mod{l}"]
            bias = self.pvec[:, boff + g * 4:boff + g * 4 + 4].unsqueeze(2).to_broadcast([128, 4, 3])
            S.op("dve", lambda: nc.vector.tensor_tensor(out=self.MOD[:, l, g * 4:(g + 1) * 4, :], in0=self.PS[pb][:, 0:16].rearrange("p (o f) -> p o f", f=4)[:, :, 0:3], in1=bias, op=ALU.add),
                 [self.BPS[pb]], [self.BMOD])
        for l in range(DEPTH):
            for w in range(2):
                sc_idx = 8 if w == 0 else 32
                nrm = self.pv(f"norm{w + 1}_{l}").unsqueeze(2).to_broadcast([128, 8, 3])
                S.op("dve", lambda: nc.vector.scalar_tensor_tensor(out=self.MA[:, l, w, :, :], in0=self.MOD[:, l, sc_idx:sc_idx + 8, :], scalar=1.0, in1=nrm, op0=ALU.add, op1=ALU.mult),
                     [self.BMOD], [self.BMOD])
        st.close()

    def norm_tiles(self, st, n=BLK + 2):
        return dict(sq=st.sb("nsq", [128, 8, n], BF16), tmp=st.sb("ntmp", [128, 8, n]), r0=st.sb("nr0", [128, n]), r1=st.sb("nr1", [128, n]),
                    B=[Buf() for _ in range(4)])

    def norm_block(self, nt, xs, Bxs, n, A, Bsh, hb, Bhb, bank):
        nc, S = self.nc, self.S
        sq, tmp, r0, r1 = nt["sq"], nt["tmp"], nt["r0"], nt["r1"]
        Bsq, Btmp, Br0, Br1 = nt["B"]
        S.op("act", lambda: nc.scalar.activation(out=sq[:, :, :n], in_=xs, func=AF.Square), [Bxs], [Bsq])
        ps = self.PS[bank]
        for c in range(8):
            S.op("pe", lambda: nc.tensor.matmul(ps[:, :n], lhsT=self.onesb, rhs=sq[:, c, :n], start=(c == 0), stop=(c == 7)), [Bsq], [self.BPS[bank]])
        S.op("act", lambda: nc.scalar.activation(out=r0[:, :n], in_=ps[:, :n], func=AF.Sqrt, scale=1.0 / D, bias=self.epsD), [self.BPS[bank]], [Br0])
        S.op("dve", lambda: nc.vector.reciprocal(out=r1[:, :n], in_=r0[:, :n]), [Br0], [Br1])
        S.op("dve", lambda: nc.vector.tensor_tensor(out=tmp[:, :, :n], in0=xs, in1=r1[:, :n].unsqueeze(1).to_broadcast([128, 8, n]), op=ALU.mult), [Bxs, Br1], [Btmp])
        for c in range(8):
            S.op("act", lambda: nc.scalar.activation(out=hb[:, c, :n], in_=tmp[:, c, :n], func=AF.Identity, scale=A[:, c:c + 1], bias=(Bsh[:, c:c + 1] if Bsh is not None else 0.0)),
                 [Btmp, self.BMOD], [Bhb])

    def mod_ab(self, l, w, j):
        A = self.MA[:, l, w, :, j]
        sh = self.MOD[:, l, (0 if w == 0 else 24):(8 if w == 0 else 32), j]
        gt = self.MOD[:, l, (16 if w == 0 else 40):(24 if w == 0 else 48), j]
        return A, sh, gt

    @staticmethod
    def blocks(skip_ctx=False):
        out = []
        for b in range(NB):
            for k in range(NBLK):
                if skip_ctx and k == 0:
                    continue
                out.append((b, k))
        return out

    @staticmethod
    def blk_range(k):
        seq0, seq1 = (0, TC) if k == 0 else (TC, T)
        t0 = k * BLK
        lo = max(t0 - 1, seq0)
        hi = min(t0 + BLK + 1, seq1)
        return t0, lo, hi, (t0 == seq0), (t0 + BLK == seq1)

    def ffn_stage(self, l, xin, xout, skip_ctx):
        nc, S = self.nc, self.S
        st = Stage(self, "ffn")
        Win = st.sb("win", [128, 8, 2 * DFF], BF16)
        Wout = st.sb("wout", [128, NFC, D], BF16)
        BWin = self.load_w(Win, self.W["ffn_w_in"][l], None)
        BWout = []
        osrc = self.W["ffn_w_out"][l].rearrange("(fc p) n -> p fc n", p=128)
        for f0 in range(0, NFC, 2):
            b = Buf()
            S.dma("pool", Wout[:, f0:f0 + 2, :], osrc[:, f0:f0 + 2, :], writes=[b])
            BWout.append(b)
        NH = BLK + 2
        xs = [st.sb(f"xs{i}", [128, 8, NH]) for i in range(2)]
        hb = [st.sb(f"hb{i}", [128, 8, NH], BF16) for i in range(2)]
        gt_ = [st.sb(f"g{i}", [128, NFC, BLK], BF16) for i in range(2)]
        cv = [st.sb(f"cv{i}", [128, BLK]) for i in range(2)]
        sl = [st.sb(f"sl{i}", [128, BLK]) for i in range(2)]
        Bxs, Bhb, Bg, Bcv, Bsl = [[Buf(), Buf()] for _ in range(5)]
        nt = self.norm_tiles(st)
        for i in range(2):
            S.op("dve", lambda: nc.vector.memset(xs[i], 0.0), [], [Bxs[i]])
        xiv = xin.rearrange("(c p) t -> p c t", p=128)
        xov = xout.rearrange("(c p) t -> p c t", p=128)
        blocks = self.blocks(skip_ctx)

        def load(n):
            b, k = blocks[n]
            t0, lo, hi, _, _ = self.blk_range(k)
            S.dma("sp", xs[n % 2][:, :, lo - (t0 - 1):hi - (t0 - 1)], xiv[:, :, b * T + lo:b * T + hi], writes=[Bxs[n % 2]])

        load(0)
        for n, (b, k) in enumerate(blocks):
            i = n % 2
            if n + 1 < len(blocks):
                load(n + 1)
            t0, lo, hi, first, last = self.blk_range(k)
            j = 2 if k == 0 else b
            A, sh, gate = self.mod_ab(l, 1, j)
            self.norm_block(nt, xs[i], Bxs[i], NH, A, sh, hb[i], Bhb[i], 6)
            for fc in range(NFC):
                q = fc % 2
                pa, pvv = self.PS[q], self.PS[2 + q]
                ga = BWin[(fc * 128) // 512]
                gv = BWin[(DFF + fc * 128) // 512]
                for kc in range(8):
                    S.op("pe", lambda: nc.tensor.matmul(pa[:, :NH], lhsT=Win[:, kc, fc * 128:(fc + 1) * 128], rhs=hb[i][:, kc, :], start=(kc == 0), stop=(kc == 7)),
                         [ga, Bhb[i]], [self.BPS[q]])
                for kc in range(8):
                    S.op("pe", lambda: nc.tensor.matmul(pvv[:, :BLK], lhsT=Win[:, kc, DFF + fc * 128:DFF + (fc + 1) * 128], rhs=hb[i][:, kc, 1:1 + BLK], start=(kc == 0), stop=(kc == 7)),
                         [gv, Bhb[i]], [self.BPS[2 + q]])
                w0, w1, w2, cb = self.pv(f"conv{l}_0", fc), self.pv(f"conv{l}_1", fc), self.pv(f"conv{l}_2", fc), self.pv(f"convb{l}", fc)
                S.op("act", lambda: nc.scalar.activation(out=cv[q], in_=pa[:, 1:1 + BLK], func=AF.Identity, scale=w1, bias=cb), [self.BPS[q]], [Bcv[q]])
                c0 = 1 if first else 0
                S.op("dve", lambda: nc.vector.scalar_tensor_tensor(out=cv[q][:, c0:BLK], in0=pa[:, c0:BLK], scalar=w0, in1=cv[q][:, c0:BLK], op0=ALU.mult, op1=ALU.add),
                     [self.BPS[q], Bcv[q]], [Bcv[q]])
                c1 = BLK - 1 if last else BLK
                S.op("dve", lambda: nc.vector.scalar_tensor_tensor(out=cv[q][:, 0:c1], in0=pa[:, 2:2 + c1], scalar=w2, in1=cv[q][:, 0:c1], op0=ALU.mult, op1=ALU.add),
                     [self.BPS[q], Bcv[q]], [Bcv[q]])
                S.op("act", lambda: nc.scalar.activation(out=sl[q], in_=cv[q], func=AF.Silu), [Bcv[q]], [Bsl[q]])
                S.op("dve", lambda: nc.vector.tensor_tensor(out=gt_[i][:, fc, :], in0=sl[q], in1=pvv[:, :BLK], op=ALU.mult), [Bsl[q], self.BPS[2 + q]], [Bg[i]])
            for oc in range(8):
                q = 4 + oc % 2
                po = self.PS[q]
                for fc in range(NFC):
                    S.op("pe", lambda: nc.tensor.matmul(po[:, :BLK], lhsT=Wout[:, fc, oc * 128:(oc + 1) * 128], rhs=gt_[i][:, fc, :], start=(fc == 0), stop=(fc == NFC - 1)),
                         [BWout[fc // 2], Bg[i]], [self.BPS[q]])
                S.op("dve", lambda: nc.vector.scalar_tensor_tensor(out=xs[i][:, oc, 1:1 + BLK], in0=po[:, :BLK], scalar=gate[:, oc:oc + 1], in1=xs[i][:, oc, 1:1 + BLK], op0=ALU.mult, op1=ALU.add),
                     [self.BPS[q], Bxs[i], self.BMOD], [Bxs[i]])
            S.dma("pool", xov[:, :, b * T + t0:b * T + t0 + BLK], xs[i][:, :, 1:1 + BLK], reads=[Bxs[i]])
        st.close()

    def final_stage(self, xin):
        nc, S = self.nc, self.S
        st = Stage(self, "fin")
        xs = [st.sb(f"xs{i}", [128, 8, BLK]) for i in range(2)]
        hb = [st.sb(f"hb{i}", [128, 8, BLK]) for i in range(2)]
        ot = [st.sb(f"ot{i}", [128, D]) for i in range(2)]
        Bxs, Bhb, Bot = [[Buf(), Buf()] for _ in range(3)]
        nt = self.norm_tiles(st, BLK)
        xiv = xin.rearrange("(c p) t -> p c t", p=128)
        blocks = self.blocks(True)
        A = self.pv("norm_f")

        def load(n):
            b, k = blocks[n]
            S.dma("sp", xs[n % 2], xiv[:, :, b * T + k * BLK:b * T + (k + 1) * BLK], writes=[Bxs[n % 2]])

        load(0)
        nt_i = 0
        for n, (b, k) in enumerate(blocks):
            i = n % 2
            if n + 1 < len(blocks):
                load(n + 1)
            self.norm_block(nt, xs[i], Bxs[i], BLK, A, None, hb[i], Bhb[i], 6)
            for tt in range(2):
                o = nt_i % 2
                nt_i += 1
                for hf in range(2):
                    pb = 2 * o + hf
                    for c4 in range(4):
                        c = hf * 4 + c4
                        S.op("pe", lambda: nc.tensor.transpose(out=self.PS[pb][:, c4 * 128:(c4 + 1) * 128], in_=hb[i][:, c, tt * 128:(tt + 1) * 128], identity=self.ident),
                             [Bhb[i]], [self.BPS[pb]])
                    if hf == 0:
                        S.op("dve", lambda: nc.vector.tensor_copy(out=ot[o][:, 0:512], in_=self.PS[pb]), [self.BPS[pb]], [Bot[o]])
                    else:
                        S.op("act", lambda: nc.scalar.copy(out=ot[o][:, 512:1024], in_=self.PS[pb]), [self.BPS[pb]], [Bot[o]])
                tl = k * BLK - TC + tt * 128
                S.dma("pool", self.out[b, tl:tl + 128, :], ot[o], reads=[Bot[o]])
        st.close()


def build_program(wshapes, pv_off, npv, plan=None, dbg=()):
    P = Prog(wshapes, pv_off, npv, dbg=dbg)
    xa = P.scr("xA", [D, TT])
    xb = P.scr("xB", [D, TT])
    if plan is None:
        plan = ["tr", "mod"]
        for l in range(DEPTH):
            plan += [f"mix{l}", f"ffn{l}"]
        plan += ["final"]
    cur, nxt = xa, xb
    for step in plan:
        if step == "tr":
            P.prologue_transpose(cur)
        elif step == "mod":
            P.prologue_mod()
        elif step.startswith("mix"):
            l = int(step[3:])
            P.mixer(l, cur, nxt)
            cur, nxt = nxt, cur
        elif step.startswith("ffn"):
            l = int(step[3:])
            P.ffn_stage(l, cur, nxt, skip_ctx=(l == DEPTH - 1))
            cur, nxt = nxt, cur
        elif step == "final":
            P.final_stage(cur)
    P.S.barrier()
    return P


def prep_inputs(inputs, cores=range(NCORES)):
    pv = pvec_layout(inputs)
    pva = pv.array()
    consts = make_consts()
    shared = {"pvec": pva}
    for k, v in consts.items():
        shared["c_" + k] = v
    for n in WEIGHT_NAMES:
        shared[n] = np.ascontiguousarray(inputs[n], dtype=np.float32)
    in_maps = []
    for c in cores:
        m = dict(shared)
        m["x"] = np.ascontiguousarray(inputs["x"][NB * c:NB * (c + 1)], dtype=np.float32)
        m["ctx"] = np.ascontiguousarray(inputs["ctx"][NB * c:NB * (c + 1)], dtype=np.float32)
        m["cvec"] = np.ascontiguousarray(np.concatenate([inputs["c"][NB * c:NB * (c + 1)], inputs["c_ctx"][None, :]], axis=0), dtype=np.float32)
        in_maps.append(m)
    wshapes = {n: list(inputs[n].shape) for n in WEIGHT_NAMES}
    return in_maps, wshapes, pv.off, pva.shape[1]


def kernel(**inputs):
    inputs = {k: np.asarray(v) for k, v in inputs.items()}
    in_maps, wshapes, pv_off, npv = prep_inputs(inputs)
    P = build_program(wshapes, pv_off, npv)
    res = run_bass_kernel_spmd(P.nc, in_maps, core_ids=list(range(NCORES)))
    out = np.concatenate([np.asarray(r["out"]) for r in res.results], axis=0)
    return out.astype(np.float32)


def _inproj_stage(self, l, xin, Wd, N, dst_fm, tm_specs, f32_h=False):
    nc, S = self.nc, self.S
    st = Stage(self, "ip")
    Wt = st.sb("w", [128, 8, N], BF16)
    BW = self.load_w(Wt, Wd, None)
    xs = [st.sb(f"xs{i}", [128, 8, BLK]) for i in range(2)]
    hb = [st.sb(f"hb{i}", [128, 8, BLK], BF16) for i in range(2)]
    sg = [st.sb(f"sg{i}", [128, 8, BLK]) for i in range(2)]
    tmw = max([nc_ for (_, nc_, _) in tm_specs], default=0)
    tms = [st.sb(f"tm{i}", [128, max(tmw, 1)], BF16) for i in range(2)]
    Bxs, Bhb, Bsg, Btm = [[Buf(), Buf()] for _ in range(4)]
    nt = self.norm_tiles(st, BLK)
    xiv = xin.rearrange("(c p) t -> p c t", p=128)
    dv = dst_fm.rearrange("(c p) t -> p c t", p=128)
    blocks = self.blocks(False)

    def load(n):
        b, k = blocks[n]
        S.dma("sp", xs[n % 2], xiv[:, :, b * T + k * BLK:b * T + (k + 1) * BLK], writes=[Bxs[n % 2]])

    load(0)
    sgi = 0
    tmi = 0
    pbank = 0
    for n, (b, k) in enumerate(blocks):
        i = n % 2
        if n + 1 < len(blocks):
            load(n + 1)
        j = 2 if k == 0 else b
        A, sh, _ = self.mod_ab(l, 0, j)
        self.norm_block(nt, xs[i], Bxs[i], BLK, A, sh, hb[i], Bhb[i], 6)
        col = b * T + k * BLK
        for og in range(N // 1024):
            s_ = sgi % 2
            sgi += 1
            for o8 in range(8):
                oc = og * 8 + o8
                pb = pbank % 4
                pbank += 1
                for kc in range(8):
                    S.op("pe", lambda: nc.tensor.matmul(self.PS[pb][:, :BLK], lhsT=Wt[:, kc, oc * 128:(oc + 1) * 128], rhs=hb[i][:, kc, :], start=(kc == 0), stop=(kc == 7)),
                         [BW[(oc * 128) // 512], Bhb[i]], [self.BPS[pb]])
                if o8 % 2 == 0:
                    S.op("act", lambda: nc.scalar.copy(out=sg[s_][:, o8, :], in_=self.PS[pb][:, :BLK]), [self.BPS[pb]], [Bsg[s_]])
                else:
                    S.op("dve", lambda: nc.vector.tensor_copy(out=sg[s_][:, o8, :], in_=self.PS[pb][:, :BLK]), [self.BPS[pb]], [Bsg[s_]])
            S.dma("pool", dv[:, og * 8:(og + 1) * 8, col:col + BLK], sg[s_], reads=[Bsg[s_]])
        for (c0, ncols, dst_tm) in tm_specs:
            for tt in range(BLK // 128):
                s_ = tmi % 2
                tmi += 1
                for n0 in range(0, ncols, 512):
                    pb = 4 + (pbank % 2)
                    pbank += 1
                    for kc in range(8):
                        S.op("pe", lambda: nc.tensor.matmul(self.PS[pb][:, :512], lhsT=hb[i][:, kc, tt * 128:(tt + 1) * 128], rhs=Wt[:, kc, c0 + n0:c0 + n0 + 512], start=(kc == 0), stop=(kc == 7)),
                             [BW[(c0 + n0) // 512], Bhb[i]], [self.BPS[pb]])
                    S.op("act", lambda: nc.scalar.copy(out=tms[s_][:, n0:n0 + 512], in_=self.PS[pb][:, :512]), [self.BPS[pb]], [Btm[s_]])
                S.dma("pool", dst_tm[col + tt * 128:col + (tt + 1) * 128, :], tms[s_][:, :ncols], reads=[Btm[s_]])
    st.close()


def _outproj_stage(self, l, og, Wd, xin, xout, skip_ctx):
    nc, S = self.nc, self.S
    st = Stage(self, "op")
    Wt = st.sb("w", [128, 8, D], BF16)
    BW = self.load_w(Wt, Wd, None)
    xs = [st.sb(f"xs{i}", [128, 8, BLK]) for i in range(2)]
    ob = [st.sb(f"ob{i}", [128, 8, BLK], BF16) for i in range(2)]
    Bxs, Bob = [[Buf(), Buf()] for _ in range(2)]
    xiv = xin.rearrange("(c p) t -> p c t", p=128)
    xov = xout.rearrange("(c p) t -> p c t", p=128)
    ogv = og.rearrange("(c p) t -> p c t", p=128)
    blocks = self.blocks(skip_ctx)

    def load(n):
        b, k = blocks[n]
        col = b * T + k * BLK
        S.dma("sp", xs[n % 2], xiv[:, :, col:col + BLK], writes=[Bxs[n % 2]])
        S.dma("sp", ob[n % 2], ogv[:, :, col:col + BLK], writes=[Bob[n % 2]])

    load(0)
    for n, (b, k) in enumerate(blocks):
        i = n % 2
        if n + 1 < len(blocks):
            load(n + 1)
        j = 2 if k == 0 else b
        _, _, gate = self.mod_ab(l, 0, j)
        for oc in range(8):
            pb = oc % 4
            for kc in range(8):
                S.op("pe", lambda: nc.tensor.matmul(self.PS[pb][:, :BLK], lhsT=Wt[:, kc, oc * 128:(oc + 1) * 128], rhs=ob[i][:, kc, :], start=(kc == 0), stop=(kc == 7)),
                     [BW[(oc * 128) // 512], Bob[i]], [self.BPS[pb]])
            S.op("dve", lambda: nc.vector.scalar_tensor_tensor(out=xs[i][:, oc, :], in0=self.PS[pb][:, :BLK], scalar=gate[:, oc:oc + 1], in1=xs[i][:, oc, :], op0=ALU.mult, op1=ALU.add),
                 [self.BPS[pb], Bxs[i], self.BMOD], [Bxs[i]])
        col = b * T + k * BLK
        S.dma("pool", xov[:, :, col:col + BLK], xs[i], reads=[Bxs[i]])
    st.close()


def _hgrn2_scan(self, jh, Pfm, Itm, og):
    nc, S = self.nc, self.S
    st = Stage(self, "hs")
    A_ = nc.vector
    LB = st.sb("LB", [128, 2, 8])
    OML = st.sb("OML", [128, 2, 8])
    e0 = st.sb("e0", [128, 8]); e1 = st.sb("e1", [128, 8]); rr = st.sb("rr", [128, 8]); p0 = st.sb("p0", [128, 8]); p1 = st.sb("p1", [128, 8])
    BL = Buf()
    for d in range(2):
        S.op("act", lambda: nc.scalar.activation(out=e0, in_=self.pv(f"hg_lb{d}_0"), func=AF.Exp), [], [BL])
        S.op("act", lambda: nc.scalar.activation(out=e1, in_=self.pv(f"hg_lb{d}_1"), func=AF.Exp), [BL], [BL])
        S.op("dve", lambda: A_.tensor_tensor(out=rr, in0=e0, in1=e1, op=ALU.add), [BL], [BL])
        S.op("dve", lambda: A_.reciprocal(out=rr, in_=rr), [BL], [BL])
        S.op("dve", lambda: A_.tensor_tensor(out=p0, in0=e0, in1=rr, op=ALU.mult), [BL], [BL])
        S.op("dve", lambda: A_.tensor_tensor(out=p1, in0=e1, in1=rr, op=ALU.mult), [BL], [BL])
        if jh == 1:
            S.op("dve", lambda: A_.tensor_tensor(out=p1, in0=p0, in1=p1, op=ALU.add), [BL], [BL])
        else:
            S.op("dve", lambda: A_.tensor_copy(out=p1, in_=p0), [BL], [BL])
        S.op("dve", lambda: A_.tensor_tensor(out=LB[:, d, :], in0=p1, in1=p0, op=ALU.subtract), [BL], [BL])
        S.op("dve", lambda: A_.tensor_scalar(out=OML[:, d, :], in0=LB[:, d, :], scalar1=-1.0, scalar2=1.0, op0=ALU.mult, op1=ALU.add), [BL], [BL])
    smask = st.sb("smask", [128, T])
    Bsm = Buf()
    S.dma("sp", smask, self.cd["scanmask"], writes=[Bsm])
    f32t = lambda n: st.sb(n, [128, T])
    qs = f32t("qs"); graw = f32t("graw"); kk = f32t("kk"); ep = f32t("ep"); en = f32t("en")
    z = [f32t("z0"), f32t("z1")]; bb = [f32t("b0"), f32t("b1")]; of = [f32t("of0"), f32t("of1")]
    qt = [st.sb(f"qt{d}", [128, T], BF16) for d in range(2)]
    kh = [st.sb(f"kh{d}", [128, T], BF16) for d in range(2)]
    sqb = st.sb("sqb", [128, T], BF16)
    ogb = st.sb("ogb", [128, T], BF16)
    Vt = st.sb("Vt", [64, NCH, 128], BF16)
    emid = [st.sb(f"emid{d}", [128, NCH]) for d in range(2)]
    eend = [st.sb(f"eend{d}", [128, NCH]) for d in range(2)]
    eem = [st.sb(f"eem{d}", [128, NCH]) for d in range(2)]
    Sst = [st.sb(f"S{d}", [128, 128]) for d in range(2)]
    Sm = [st.sb(f"Sm{d}", [128, 128], BF16) for d in range(2)]
    tmpS = [st.sb(f"tS{d}", [128, 128]) for d in range(2)]
    khT = [st.sb(f"khT{d}", [64, 128], BF16) for d in range(2)]
    att = [st.sb(f"att{d}", [64, 64], BF16) for d in range(2)]
    Bqs, Bgr, Bkk, Bep, Ben, Bsq, Bog, BVt = [Buf() for _ in range(8)]
    Bz, Bbb, Bof, Bqt, Bkh, Bes, BS, BSm, BtS, BkT, Batt = [[Buf(), Buf()] for _ in range(11)]
    PSb = [self.PS[i].bitcast(BF16) for i in range(8)]
    for d in range(2):
        S.op("dve", lambda: A_.memset(att[d], 0.0), [], [Batt[d]])
    cf = list(range(NCH))
    cb = list(range(TC // CH - 1, -1, -1)) + list(range(NCH - 1, TC // CH - 1, -1))
    order = [cf, cb]
    for b in range(NB):
        for h in range(8):
            rows = slice(h * 128, (h + 1) * 128)
            cols = slice(b * T, (b + 1) * T)
            S.dma("sp", qs, Pfm[0 * D + h * 128:0 * D + (h + 1) * 128, cols], writes=[Bqs])
            S.dma("sp", z[0], Pfm[3 * D + h * 128:3 * D + (h + 1) * 128, cols], writes=[Bz[0]])
            S.dma("sp", z[1], Pfm[4 * D + h * 128:4 * D + (h + 1) * 128, cols], writes=[Bz[1]])
            S.dma("sp", graw, Pfm[2 * D + h * 128:2 * D + (h + 1) * 128, cols], writes=[Bgr])
            S.dma("sp", Vt, Itm[cols, rows].rearrange("(c s) v -> s c v", s=CH), writes=[BVt])
            S.op("act", lambda: nc.scalar.activation(out=qs, in_=qs, func=AF.Silu), [Bqs], [Bqs])
            for d in range(2):
                m_idx = 32 if d == 0 else 31
                zt = z[d]
                S.op("act", lambda: nc.scalar.activation(out=zt, in_=zt, func=AF.Sigmoid), [Bz[d]], [Bz[d]])
                S.op("dve", lambda: A_.tensor_scalar(out=zt, in0=zt, scalar1=OML[:, d, h:h + 1], scalar2=LB[:, d, h:h + 1], op0=ALU.mult, op1=ALU.add), [Bz[d], BL], [Bz[d]])
                S.op("dve", lambda: A_.tensor_scalar(out=kk, in0=zt, scalar1=-1.0, scalar2=1.0, op0=ALU.mult, op1=ALU.add), [Bz[d]], [Bkk])
                S.op("act", lambda: nc.scalar.activation(out=zt, in_=zt, func=AF.Ln), [Bz[d]], [Bz[d]])
                S.op("dve", lambda: A_.tensor_tensor_scan(out=bb[d], data0=smask, data1=zt, initial=0.0, op0=ALU.mult, op1=ALU.add), [Bsm, Bz[d]], [Bbb[d]])
                b3 = bb[d].rearrange("p (c s) -> p c s", s=CH)
                if d == 1:
                    S.op("dve", lambda: A_.tensor_tensor(out=zt, in0=zt, in1=bb[d], op=ALU.subtract), [Bz[d], Bbb[d]], [Bz[d]])
                    S.op("dve", lambda: A_.tensor_tensor(out=ep.rearrange("p (c s) -> p c s", s=CH), in0=zt.rearrange("p (c s) -> p c s", s=CH),
                                                          in1=b3[:, :, CH - 1:CH].to_broadcast([128, NCH, CH]), op=ALU.add), [Bz[d], Bbb[d]], [Bep])
                    S.op("dve", lambda: A_.tensor_copy(out=bb[d], in_=ep), [Bep], [Bbb[d]])
                e_idx = CH - 1 if d == 0 else 0
                S.op("act", lambda: nc.scalar.activation(out=emid[d], in_=b3[:, :, m_idx], func=AF.Exp), [Bbb[d]], [Bes[d]])
                S.op("act", lambda: nc.scalar.activation(out=eend[d], in_=b3[:, :, e_idx], func=AF.Exp), [Bbb[d]], [Bes[d]])
                S.op("dve", lambda: A_.tensor_tensor(out=eem[d], in0=b3[:, :, e_idx], in1=b3[:, :, m_idx], op=ALU.subtract), [Bbb[d]], [Bes[d]])
                S.op("act", lambda: nc.scalar.activation(out=eem[d], in_=eem[d], func=AF.Exp), [Bes[d]], [Bes[d]])
                S.op("dve", lambda: A_.tensor_tensor(out=ep.rearrange("p (c s) -> p c s", s=CH), in0=b3, in1=b3[:, :, m_idx:m_idx + 1].to_broadcast([128, NCH, CH]), op=ALU.subtract),
                     [Bbb[d]], [Bep])
                S.op("act", lambda: nc.scalar.activation(out=en, in_=ep, func=AF.Exp, scale=-1.0), [Bep], [Ben])
                S.op("act", lambda: nc.scalar.activation(out=ep, in_=ep, func=AF.Exp), [Bep], [Bep])
                S.op("dve", lambda: A_.tensor_tensor(out=qt[d], in0=qs, in1=ep, op=ALU.mult), [Bqs, Bep], [Bqt[d]])
                S.op("dve", lambda: A_.tensor_tensor(out=kh[d], in0=kk, in1=en, op=ALU.mult), [Bkk, Ben], [Bkh[d]])
                S.op("dve", lambda: A_.memset(Sst[d], 0.0), [], [BS[d]])
                S.op("dve", lambda: A_.memset(Sm[d], 0.0), [], [BSm[d]])
            def hstep(d, step):
                c = order[d][step]
                cs = slice(c * CH, (c + 1) * CH)
                pb = d * 4
                mk = (self.masks[0:64, 64:128] if d == 0 else self.masks[0:64, 192:256]).bitcast(mybir.dt.uint32)
                S.op("pe", lambda: nc.tensor.transpose(out=PSb[pb][0:64, 0:128], in_=kh[d][:, cs], identity=self.identb), [Bkh[d]], [self.BPS[pb]])
                S.op("pe", lambda: nc.tensor.matmul(self.PS[pb + 1][0:64, 0:64], lhsT=kh[d][:, cs], rhs=qt[d][:, cs], start=True, stop=True), [Bkh[d], Bqt[d]], [self.BPS[pb + 1]])
                yield
                S.op("act", lambda: nc.scalar.copy(out=khT[d], in_=PSb[pb][0:64, 0:128]), [self.BPS[pb]], [BkT[d]])
                S.op("dve", lambda: A_.copy_predicated(out=att[d], mask=mk, data=self.PS[pb + 1][0:64, 0:64]), [self.BPS[pb + 1]], [Batt[d]])
                S.op("pe", lambda: nc.tensor.matmul(self.PS[pb + 2][:, 0:64], lhsT=Vt[:, c, :], rhs=att[d], start=True, stop=False), [BVt, Batt[d]], [self.BPS[pb + 2]])
                S.op("pe", lambda: nc.tensor.matmul(self.PS[pb + 2][:, 0:64], lhsT=Sm[d], rhs=qt[d][:, cs], start=False, stop=True), [BSm[d], Bqt[d]], [self.BPS[pb + 2]])
                S.op("pe", lambda: nc.tensor.matmul(self.PS[pb + 3][:, 0:128], lhsT=khT[d], rhs=Vt[:, c, :], start=True, stop=True), [BkT[d], BVt], [self.BPS[pb + 3]])
                yield
                S.op("act", lambda: nc.scalar.activation(out=tmpS[d], in_=self.PS[pb + 3][:, 0:128], func=AF.Identity, scale=eem[d][:, c:c + 1]), [self.BPS[pb + 3], Bes[d]], [BtS[d]])
                S.op("dve", lambda: A_.scalar_tensor_tensor(out=Sst[d], in0=Sst[d], scalar=eend[d][:, c:c + 1], in1=tmpS[d], op0=ALU.mult, op1=ALU.add), [BS[d], BtS[d], Bes[d]], [BS[d]])
                S.op("act", lambda: nc.scalar.copy(out=of[d][:, cs], in_=self.PS[pb + 2][:, 0:64]), [self.BPS[pb + 2]], [Bof[d]])
                if step + 1 < NCH:
                    cn = order[d][step + 1]
                    S.op("dve", lambda: A_.tensor_scalar(out=Sm[d], in0=Sst[d], scalar1=emid[d][:, cn:cn + 1], scalar2=None, op0=ALU.mult), [BS[d], Bes[d]], [BSm[d]])

            for step in range(NCH):
                gens = [hstep(d, step) for d in range(2)]
                while gens:
                    for g_ in list(gens):
                        try:
                            next(g_)
                        except StopIteration:
                            gens.remove(g_)
            S.op("dve", lambda: A_.tensor_tensor(out=of[0], in0=of[0], in1=of[1], op=ALU.add), [Bof[0], Bof[1]], [Bof[0]])
            S.op("act", lambda: nc.scalar.activation(out=sqb, in_=of[0], func=AF.Square), [Bof[0]], [Bsq])
            for pc in range(6):
                sl_ = slice(pc * 384, (pc + 1) * 384)
                pb = pc % 2
                S.op("pe", lambda: nc.tensor.matmul(self.PS[pb][:, 0:384], lhsT=self.onesb, rhs=sqb[:, sl_], start=True, stop=True), [Bsq], [self.BPS[pb]])
                S.op("act", lambda: nc.scalar.activation(out=ep[:, sl_], in_=self.PS[pb][:, 0:384], func=AF.Sqrt, scale=1.0 / 128, bias=self.epsD), [self.BPS[pb]], [Bep])
            S.op("dve", lambda: A_.reciprocal(out=ep, in_=ep), [Bep], [Bep])
            S.op("dve", lambda: A_.tensor_tensor(out=of[0], in0=of[0], in1=ep, op=ALU.mult), [Bof[0], Bep], [Bof[0]])
            S.op("act", lambda: nc.scalar.activation(out=graw, in_=graw, func=AF.Silu), [Bgr], [Bgr])
            S.op("dve", lambda: A_.scalar_tensor_tensor(out=ogb, in0=of[0], scalar=self.pv(f"hg_norm{jh}", 0), in1=graw, op0=ALU.mult, op1=ALU.mult), [Bof[0], Bgr], [Bog])
            S.dma("pool", og[rows, cols], ogb, reads=[Bog])
    st.close()


def _mixer(self, l, cur, nxt):
    kind, j = l % 3, l // 3
    last = (l == DEPTH - 1)
    og = self.scr("og", [D, TT], BF16)
    if kind == 0:
        Pfm = self.scr("hgP", [5 * D, TT])
        Itm = self.scr("hgI", [TT, D], BF16)
        self.inproj_stage(l, cur, self.W["hg_w_in"][j], 5 * D, Pfm, [(D, D, Itm)])
        self.hgrn2_scan(j, Pfm, Itm, og)
        self.outproj_stage(l, og, self.W["hg_w_o"][j], cur, nxt, last)
    elif kind == 1:
        self.rwkv_mixer(l, cur, og)
        self.outproj_stage(l, og, self.W["rw_w_o"][j], cur, nxt, last)
    else:
        self.mla_mixer(l, cur, og)
        self.outproj_stage(l, og, self.W["mla_w_o"][j], cur, nxt, last)


Prog.inproj_stage = _inproj_stage
Prog.outproj_stage = _outproj_stage
Prog.hgrn2_scan = _hgrn2_scan
Prog.mixer = _mixer


def _mla_mixer(self, l, xin, og):
    nc, S = self.nc, self.S
    A_ = nc.vector
    NH = 16
    QN = self.scr("mlaQN", [64, NH, TT], BF16)
    QR = self.scr("mlaQR", [32, NH, TT], BF16)
    KN = self.scr("mlaKN", [64, NH, TT], BF16)
    KR = self.scr("mlaKR", [32, TT], BF16)
    VT = self.scr("mlaVT", [TT, D], BF16)
    st = Stage(self, "m1")
    Wd = st.sb("wd", [128, 8, 544], BF16)
    Wq = st.sb("wq", [128, 2, 1536], BF16)
    Wk = st.sb("wk", [128, 2, 2048], BF16)
    Wdr = st.sb("wdr", [128, 8, 32], BF16)
    Wqr = st.sb("wqr", [128, 2, NH, 32], BF16)
    BWd, BWq, BWk, BWr = Buf(), Buf(), Buf(), Buf()
    S.dma("pool", Wd, self.W["mla_w_dqkv"][0].rearrange("(kc p) n -> p kc n", p=128), writes=[BWd])
    wqv = self.W["mla_w_uq"][0].rearrange("(kc p) n -> p kc n", p=128)
    for i3 in range(3):
        S.dma("pool", Wq[:, :, i3 * 512:(i3 + 1) * 512], wqv[:, :, i3 * 512:(i3 + 1) * 512], writes=[BWq])
    wkv = self.W["mla_w_ukv"][0].rearrange("(kc p) n -> p kc n", p=128)
    for i4 in range(4):
        S.dma("pool", Wk[:, :, i4 * 512:(i4 + 1) * 512], wkv[:, :, i4 * 512:(i4 + 1) * 512], writes=[BWk])
    Wq4 = Wq.rearrange("p k (h c) -> p k h c", c=96)
    for seg in range(2):
        for half in range(2):
            sgn = -1.0 if half == 0 else 1.0
            so = 64 + seg * 16 + (1 - half) * 8
            do = seg * 16 + half * 8
            S.op("act", lambda: nc.scalar.activation(out=Wqr[:, :, :, do:do + 8], in_=Wq4[:, :, :, so:so + 8], func=AF.Copy, scale=sgn), [BWq], [BWr])
            so2 = 512 + seg * 16 + (1 - half) * 8
            S.op("act", lambda: nc.scalar.activation(out=Wdr[:, :, do:do + 8], in_=Wd[:, :, so2:so2 + 8], func=AF.Copy, scale=sgn), [BWd], [BWr])
    cos = st.sb("cos", [32, T]); sin = st.sb("sin", [32, T])
    Bcs = Buf()
    S.dma("sp", cos, self.cd["rope_cos"], writes=[Bcs])
    S.dma("sp", sin, self.cd["rope_sin"], writes=[Bcs])
    xs = [st.sb(f"xs{i}", [128, 8, BLK]) for i in range(2)]
    hb = [st.sb(f"hb{i}", [128, 8, BLK], BF16) for i in range(2)]
    Bxs, Bhb = [[Buf(), Buf()] for _ in range(2)]
    nt = self.norm_tiles(st, BLK)
    cs_ = st.sb("cs", [128, 4, BLK]); csq = st.sb("csq", [128, 4, BLK], BF16); cn = st.sb("cn", [128, 4, BLK], BF16)
    rr0 = st.sb("rr0", [128, 2, BLK]); rr1 = st.sb("rr1", [128, 2, BLK]); ctmp = st.sb("ctmp", [128, 4, BLK])
    Bcs_, Bcsq, Bcn, Brr, Bct = [Buf() for _ in range(5)]
    qn_s = [st.sb(f"qns{i}", [64, NH, BLK], BF16) for i in range(2)]
    kn_s = [st.sb(f"kns{i}", [64, NH, BLK], BF16) for i in range(2)]
    qr_s = [st.sb(f"qrs{i}", [32, NH, BLK], BF16) for i in range(2)]
    kr_s = [st.sb(f"krs{i}", [32, BLK], BF16) for i in range(2)]
    vt_s = [st.sb(f"vts{i}", [128, D], BF16) for i in range(2)]
    t1 = st.sb("t1", [32, 2, BLK]); t2 = st.sb("t2", [32, 2, BLK])
    Bt1, Bt2 = Buf(), Buf()
    Bqn, Bkn, Bqr, Bkr, Bvt = [[Buf(), Buf()] for _ in range(5)]
    xiv = xin.rearrange("(c p) t -> p c t", p=128)
    blocks = self.blocks(False)

    def load(n):
        b, k = blocks[n]
        S.dma("sp", xs[n % 2], xiv[:, :, b * T + k * BLK:b * T + (k + 1) * BLK], writes=[Bxs[n % 2]])

    load(0)
    vti = 0
    for n, (b, k) in enumerate(blocks):
        i = n % 2
        if n + 1 < len(blocks):
            load(n + 1)
        j = 2 if k == 0 else b
        A, sh, _ = self.mod_ab(l, 0, j)
        self.norm_block(nt, xs[i], Bxs[i], BLK, A, sh, hb[i], Bhb[i], 6)
        col = b * T + k * BLK
        tcol = slice(k * BLK, (k + 1) * BLK)
        for c4 in range(4):
            pb = c4 // 2
            for kc in range(8):
                S.op("pe", lambda: nc.tensor.matmul(self.PS[pb][:, (c4 % 2) * BLK:(c4 % 2 + 1) * BLK], lhsT=Wd[:, kc, c4 * 128:(c4 + 1) * 128], rhs=hb[i][:, kc, :], start=(kc == 0), stop=(kc == 7)),
                     [BWd, Bhb[i]], [self.BPS[pb]])
        for kc in range(8):
            S.op("pe", lambda: nc.tensor.matmul(self.PS[2][0:32, 0:BLK], lhsT=Wd[:, kc, 512:544], rhs=hb[i][:, kc, :], start=(kc == 0), stop=(kc == 7)), [BWd, Bhb[i]], [self.BPS[2]])
        for kc in range(8):
            S.op("pe", lambda: nc.tensor.matmul(self.PS[2][0:32, BLK:2 * BLK], lhsT=Wdr[:, kc, :], rhs=hb[i][:, kc, :], start=(kc == 0), stop=(kc == 7)), [BWr, Bhb[i]], [self.BPS[2]])
        for pb in range(2):
            S.op("act", lambda: nc.scalar.copy(out=cs_[:, 2 * pb:2 * pb + 2, :], in_=self.PS[pb].rearrange("p (c t) -> p c t", c=2)), [self.BPS[pb]], [Bcs_])
            S.op("act", lambda: nc.scalar.activation(out=csq[:, 2 * pb:2 * pb + 2, :], in_=self.PS[pb].rearrange("p (c t) -> p c t", c=2), func=AF.Square), [self.BPS[pb]], [Bcsq])
        S.op("dve", lambda: A_.tensor_tensor(out=t1[:, 0, :], in0=self.PS[2][0:32, 0:BLK], in1=cos[:, tcol], op=ALU.mult), [self.BPS[2], Bcs], [Bt1])
        S.op("dve", lambda: A_.tensor_tensor(out=t2[:, 0, :], in0=self.PS[2][0:32, BLK:2 * BLK], in1=sin[:, tcol], op=ALU.mult), [self.BPS[2], Bcs], [Bt2])
        S.op("dve", lambda: A_.tensor_tensor(out=kr_s[i], in0=t1[:, 0, :], in1=t2[:, 0, :], op=ALU.add), [Bt1, Bt2], [Bkr[i]])
        S.dma("pool", KR[:, col:col + BLK], kr_s[i], reads=[Bkr[i]])
        for w in range(2):
            for c in range(2):
                S.op("pe", lambda: nc.tensor.matmul(self.PS[3][:, w * BLK:(w + 1) * BLK], lhsT=self.onesb, rhs=csq[:, 2 * w + c, :], start=(c == 0), stop=(c == 1)), [Bcsq], [self.BPS[3]])
        S.op("act", lambda: nc.scalar.activation(out=rr0, in_=self.PS[3].rearrange("p (w t) -> p w t", w=2), func=AF.Sqrt, scale=1.0 / 256, bias=self.epsD), [self.BPS[3]], [Brr])
        S.op("dve", lambda: A_.reciprocal(out=rr1, in_=rr0), [Brr], [Brr])
        S.op("dve", lambda: A_.tensor_tensor(out=ctmp.rearrange("p (w c) t -> p w c t", w=2), in0=cs_.rearrange("p (w c) t -> p w c t", w=2),
                                              in1=rr1.unsqueeze(2).to_broadcast([128, 2, 2, BLK]), op=ALU.mult), [Bcs_, Brr], [Bct])
        for c4 in range(4):
            gname = "mla_q_norm" if c4 < 2 else "mla_kv_norm"
            S.op("act", lambda: nc.scalar.activation(out=cn[:, c4, :], in_=ctmp[:, c4, :], func=AF.Identity, scale=self.pv(gname, c4 % 2)), [Bct], [Bcn])
        for hp in range(8):
            for which in range(2):
                pb = 4 + (2 * hp + which) % 2
                Wt_, coff, hw, ci = (Wq, 0, 96, 0) if which == 0 else (Wk, 0, 128, 2)
                for hh in range(2):
                    h = 2 * hp + hh
                    for kc in range(2):
                        S.op("pe", lambda: nc.tensor.matmul(self.PS[pb][0:64, hh * BLK:(hh + 1) * BLK], lhsT=Wt_[:, kc, h * hw:h * hw + 64], rhs=cn[:, ci + kc, :], start=(kc == 0), stop=(kc == 1)),
                             [BWq if which == 0 else BWk, Bcn], [self.BPS[pb]])
                dst = qn_s[i] if which == 0 else kn_s[i]
                Bd = Bqn[i] if which == 0 else Bkn[i]
                if which == 0:
                    S.op("act", lambda: nc.scalar.copy(out=dst[:, 2 * hp:2 * hp + 2, :], in_=self.PS[pb][0:64, :].rearrange("p (h t) -> p h t", h=2)), [self.BPS[pb]], [Bd])
                else:
                    S.op("dve", lambda: A_.tensor_copy(out=dst[:, 2 * hp:2 * hp + 2, :], in_=self.PS[pb][0:64, :].rearrange("p (h t) -> p h t", h=2)), [self.BPS[pb]], [Bd])
            for hh in range(2):
                h = 2 * hp + hh
                for kc in range(2):
                    S.op("pe", lambda: nc.tensor.matmul(self.PS[6][0:32, hh * BLK:(hh + 1) * BLK], lhsT=Wq[:, kc, h * 96 + 64:h * 96 + 96], rhs=cn[:, kc, :], start=(kc == 0), stop=(kc == 1)), [BWq, Bcn], [self.BPS[6]])
                for kc in range(2):
                    S.op("pe", lambda: nc.tensor.matmul(self.PS[7][0:32, hh * BLK:(hh + 1) * BLK], lhsT=Wqr[:, kc, h, :], rhs=cn[:, kc, :], start=(kc == 0), stop=(kc == 1)), [BWr, Bcn], [self.BPS[7]])
            cosb = cos[:, tcol].unsqueeze(1).to_broadcast([32, 2, BLK])
            sinb = sin[:, tcol].unsqueeze(1).to_broadcast([32, 2, BLK])
            S.op("dve", lambda: A_.tensor_tensor(out=t1, in0=self.PS[6][0:32, :].rearrange("p (h t) -> p h t", h=2), in1=cosb, op=ALU.mult), [self.BPS[6], Bcs], [Bt1])
            S.op("dve", lambda: A_.tensor_tensor(out=t2, in0=self.PS[7][0:32, :].rearrange("p (h t) -> p h t", h=2), in1=sinb, op=ALU.mult), [self.BPS[7], Bcs], [Bt2])
            S.op("dve", lambda: A_.tensor_tensor(out=qr_s[i][:, 2 * hp:2 * hp + 2, :], in0=t1, in1=t2, op=ALU.add), [Bt1, Bt2], [Bqr[i]])
        S.dma("pool", QN[:, :, col:col + BLK], qn_s[i], reads=[Bqn[i]])
        S.dma("pool", KN[:, :, col:col + BLK], kn_s[i], reads=[Bkn[i]])
        S.dma("pool", QR[:, :, col:col + BLK], qr_s[i], reads=[Bqr[i]])
        Wkv = Wk.rearrange("p k (h c) -> p k h c", c=128)
        for tt in range(BLK // 128):
            vi = vti % 2
            vti += 1
            for hf in range(2):
                pb = 4 + hf
                for kc in range(2):
                    S.op("pe", lambda: nc.tensor.matmul(self.PS[pb][:, 0:512], lhsT=cn[:, 2 + kc, tt * 128:(tt + 1) * 128], rhs=Wkv[:, kc, hf * 8:(hf + 1) * 8, 64:128], start=(kc == 0), stop=(kc == 1)),
                         [BWk, Bcn], [self.BPS[pb]])
                S.op("act", lambda: nc.scalar.copy(out=vt_s[vi][:, hf * 512:(hf + 1) * 512], in_=self.PS[pb][:, 0:512]), [self.BPS[pb]], [Bvt[vi]])
            S.dma("pool", VT[col + tt * 128:col + (tt + 1) * 128, :], vt_s[vi], reads=[Bvt[vi]])
    st.close()
    st = Stage(self, "m2")
    NKT = T // 128
    Vall = st.sb("Vall", [128, NKT, D], BF16)
    KRs = st.sb("KRs", [32, T], BF16)
    KNh = [st.sb(f"KNh{i}", [64, T], BF16) for i in range(2)]
    QNh = [st.sb(f"QNh{i}", [64, T], BF16) for i in range(2)]
    QRh = [st.sb(f"QRh{i}", [32, T], BF16) for i in range(2)]
    VX = [st.sb(f"VX{i}", [128, NKT, 65], BF16) for i in range(2)]
    PT = [st.sb(f"PT{i}", [128, 512], BF16) for i in range(3)]
    rd = st.sb("rd", [65, 512]); rb = [st.sb(f"rb{i}", [64, 512]) for i in range(2)]
    ob = [st.sb(f"ob{i}", [64, 512], BF16) for i in range(2)]
    BVa, BKR, Brd = Buf(), Buf(), Buf()
    BKN, BQN, BQR, BVX, Brb, Bob = [[Buf(), Buf()] for _ in range(6)]
    BPT = [Buf() for _ in range(3)]
    for i in range(2):
        S.op("pool", lambda: nc.gpsimd.memset(VX[i], 1.0), [], [BVX[i]])
    qblocks = [(0, TC, 2)] + [(TC + qb * 512, 512, NKT) for qb in range(4)]
    pti = 0
    hn = 0
    for b in range(NB):
        c0 = b * T
        S.dma("sp", Vall, VT[c0:c0 + T, :].rearrange("(kt p) v -> p kt v", p=128), writes=[BVa])
        S.dma("sp", KRs, KR[:, c0:c0 + T], writes=[BKR])
        for h in range(NH):
            i = hn % 2
            hn += 1
            S.dma("sp", KNh[i], KN[:, h, c0:c0 + T], writes=[BKN[i]])
            S.dma("sp", QNh[i], QN[:, h, c0:c0 + T], writes=[BQN[i]])
            S.dma("sp", QRh[i], QR[:, h, c0:c0 + T], writes=[BQR[i]])
            S.op("pool", lambda: nc.gpsimd.tensor_copy(out=VX[i][:, :, 0:64], in_=Vall[:, :, h * 64:(h + 1) * 64]), [BVa], [BVX[i]])
            for qi, (q0, nq, nkt) in enumerate(qblocks):
                po = 4 + (qi % 2)

                def score(kt):
                    ps = kt % 4
                    ks = slice(kt * 128, (kt + 1) * 128)
                    S.op("pe", lambda: nc.tensor.matmul(self.PS[ps][:, 0:nq], lhsT=KNh[i][:, ks], rhs=QNh[i][:, q0:q0 + nq], start=True, stop=False), [BKN[i], BQN[i]], [self.BPS[ps]])
                    S.op("pe", lambda: nc.tensor.matmul(self.PS[ps][:, 0:nq], lhsT=KRs[:, ks], rhs=QRh[i][:, q0:q0 + nq], start=False, stop=True), [BKR, BQR[i]], [self.BPS[ps]])

                score(0)
                if nkt > 1:
                    score(1)
                for kt in range(nkt):
                    ps = kt % 4
                    p3 = pti % 3
                    pti += 1
                    if kt + 2 < nkt:
                        score(kt + 2)
                    S.op("act", lambda: nc.scalar.activation(out=PT[p3][:, 0:nq], in_=self.PS[ps][:, 0:nq], func=AF.Exp, scale=MLA_SCALE), [self.BPS[ps]], [BPT[p3]])
                    S.op("pe", lambda: nc.tensor.matmul(self.PS[po][0:65, 0:nq], lhsT=VX[i][:, kt, :], rhs=PT[p3][:, 0:nq], start=(kt == 0), stop=(kt == nkt - 1)), [BVX[i], BPT[p3]], [self.BPS[po]])
                r2 = qi % 2
                S.op("dve", lambda: A_.reciprocal(out=rd[64:65, 0:nq], in_=self.PS[po][64:65, 0:nq]), [self.BPS[po]], [Brd])
                S.op("pe", lambda: nc.tensor.matmul(self.PS[6 + r2][0:64, 0:nq], lhsT=self.onesf[64:65, 0:64], rhs=rd[64:65, 0:nq], start=True, stop=True), [Brd], [self.BPS[6 + r2]])
                S.op("act", lambda: nc.scalar.copy(out=rb[r2][:, 0:nq], in_=self.PS[6 + r2][0:64, 0:nq]), [self.BPS[6 + r2]], [Brb[r2]])
                S.op("dve", lambda: A_.tensor_tensor(out=ob[r2][:, 0:nq], in0=self.PS[po][0:64, 0:nq], in1=rb[r2][:, 0:nq], op=ALU.mult), [self.BPS[po], Brb[r2]], [Bob[r2]])
                S.dma("pool", og[h * 64:(h + 1) * 64, c0 + q0:c0 + q0 + nq], ob[r2][:, 0:nq], reads=[Bob[r2]])
    st.close()


Prog.mla_mixer = _mla_mixer


RW_ARR = ["r", "kt0", "kt1", "be0", "be1", "kap", "lw0", "lw1", "v", "g"]


def _rwkv_proj(self, l, xin, RWP, Vtm):
    nc, S = self.nc, self.S
    A_ = nc.vector
    st = Stage(self, "r1")
    Wrkv = st.sb("wrkv", [128, 8, 3 * D], BF16)
    BWrkv = []
    for i3 in range(3):
        v_ = self.W["rw_w_rkv"][0, i3].rearrange("(kc p) n -> p kc n", p=128)
        for hf in range(2):
            bb_ = Buf()
            S.dma("pool", Wrkv[:, :, i3 * D + hf * 512:i3 * D + (hf + 1) * 512], v_[:, :, hf * 512:(hf + 1) * 512], writes=[bb_])
            BWrkv.append(bb_)
    W1 = st.sb("w1", [128, 8, 2, 64], BF16); A1 = st.sb("a1", [128, 8, 2, 64], BF16); G1 = st.sb("g1", [128, 8, 160], BF16)
    W2 = st.sb("w2", [64, 2, D], BF16); A2 = st.sb("a2", [64, 2, D], BF16); G2a = st.sb("g2a", [128, D], BF16); G2b = st.sb("g2b", [32, D], BF16)
    Bsw = Buf()
    for d in range(2):
        S.dma("pool", W1[:, :, d, :], self.W["rw_w1"][0, d].rearrange("(kc p) n -> p kc n", p=128), writes=[Bsw])
        S.dma("pool", A1[:, :, d, :], self.W["rw_a1"][0, d].rearrange("(kc p) n -> p kc n", p=128), writes=[Bsw])
        S.dma("pool", W2[:, d, :], self.W["rw_w2"][0, d], writes=[Bsw])
        S.dma("pool", A2[:, d, :], self.W["rw_a2"][0, d], writes=[Bsw])
    S.dma("pool", G1, self.W["rw_g1"][0].rearrange("(kc p) n -> p kc n", p=128), writes=[Bsw])
    S.dma("pool", G2a, self.W["rw_g2"][0, 0:128, :], writes=[Bsw])
    S.dma("pool", G2b, self.W["rw_g2"][0, 128:160, :], writes=[Bsw])
    NH_ = BLK + 2
    xs = [st.sb(f"xs{i}", [128, 8, NH_]) for i in range(2)]
    hf_ = st.sb("hf", [128, 8, NH_])
    dx = st.sb("dx", [128, 8, BLK])
    xj = [st.sb(f"xj{j}", [128, 8, BLK], BF16) for j in range(6)]
    Bxs = [Buf(), Buf()]
    Bhf, Bdx = Buf(), Buf()
    Bxj = [Buf() for _ in range(6)]
    nt = self.norm_tiles(st)
    lt = st.sb("lt", [64, 5, BLK], BF16)
    gh = st.sb("gh", [128, BLK], BF16)
    Blt = Buf()
    stg = [st.sb(f"stg{i}", [128, 10, BLK]) for i in range(2)]
    Bstg = [Buf(), Buf()]
    tmp = [st.sb(f"tmp{i}", [128, BLK]) for i in range(6)]
    Btmp = [Buf() for _ in range(6)]
    sqb = st.sb("sqb", [128, BLK], BF16)
    Bsqb = Buf()
    vts = [st.sb(f"vts{i}", [128, D], BF16) for i in range(2)]
    Bvts = [Buf(), Buf()]
    for i in range(2):
        S.op("dve", lambda: A_.memset(xs[i], 0.0), [], [Bxs[i]])
    xiv = xin.rearrange("(c p) t -> p c t", p=128)
    blocks = self.blocks(False)
    blk64b = st.sb("blk64b", [128, 128], BF16)
    Bb64 = Buf()
    S.op("dve", lambda: A_.tensor_copy(out=blk64b, in_=self.blk64), [], [Bb64])

    def load(n):
        b, k = blocks[n]
        t0, lo, hi, _, _ = self.blk_range(k)
        S.dma("sp", xs[n % 2][:, :, lo - (t0 - 1):hi - (t0 - 1)], xiv[:, :, b * T + lo:b * T + hi], writes=[Bxs[n % 2]])

    load(0)
    si = 0
    vi_ = 0
    pbk = 0
    for n, (b, k) in enumerate(blocks):
        i = n % 2
        if n + 1 < len(blocks):
            load(n + 1)
        t0, lo, hi, first, last = self.blk_range(k)
        j = 2 if k == 0 else b
        A, sh, _ = self.mod_ab(l, 0, j)
        self.norm_block(nt, xs[i], Bxs[i], NH_, A, sh, hf_, Bhf, 6)
        if first:
            S.op("dve", lambda: A_.memset(hf_[:, :, 0:1], 0.0), [], [Bhf])
        if last:
            S.op("dve", lambda: A_.memset(hf_[:, :, NH_ - 1:NH_], 0.0), [], [Bhf])
        S.op("dve", lambda: A_.tensor_tensor(out=dx, in0=hf_[:, :, 0:BLK], in1=hf_[:, :, 2:2 + BLK], op=ALU.add), [Bhf], [Bdx])
        S.op("dve", lambda: A_.scalar_tensor_tensor(out=dx, in0=dx, scalar=0.5, in1=hf_[:, :, 1:1 + BLK], op0=ALU.mult, op1=ALU.subtract), [Bhf, Bdx], [Bdx])
        for jj in range(6):
            for c in range(8):
                S.op("dve", lambda: A_.scalar_tensor_tensor(out=xj[jj][:, c, :], in0=dx[:, c, :], scalar=self.pv(f"rw_mu{jj}", c), in1=hf_[:, c, 1:1 + BLK], op0=ALU.mult, op1=ALU.add),
                     [Bdx, Bhf], [Bxj[jj]])
        for d in range(2):
            for kc in range(8):
                S.op("pe", lambda: nc.tensor.matmul(self.PS[5][0:64, d * BLK:(d + 1) * BLK], lhsT=W1[:, kc, d, :], rhs=xj[1][:, kc, :], start=(kc == 0), stop=(kc == 7)), [Bsw, Bxj[1]], [self.BPS[5]])
        S.op("act", lambda: nc.scalar.activation(out=lt[:, 0:2, :], in_=self.PS[5][0:64, :].rearrange("p (d t) -> p d t", d=2), func=AF.Tanh), [self.BPS[5]], [Blt])
        for d in range(2):
            for kc in range(8):
                S.op("pe", lambda: nc.tensor.matmul(self.PS[5][0:64, d * BLK:(d + 1) * BLK], lhsT=A1[:, kc, d, :], rhs=xj[4][:, kc, :], start=(kc == 0), stop=(kc == 7)), [Bsw, Bxj[4]], [self.BPS[5]])
        S.op("act", lambda: nc.scalar.copy(out=lt[:, 2:4, :], in_=self.PS[5][0:64, :].rearrange("p (d t) -> p d t", d=2)), [self.BPS[5]], [Blt])
        for kc in range(8):
            S.op("pe", lambda: nc.tensor.matmul(self.PS[5][:, 0:BLK], lhsT=G1[:, kc, 0:128], rhs=xj[5][:, kc, :], start=(kc == 0), stop=(kc == 7)), [Bsw, Bxj[5]], [self.BPS[5]])
        for kc in range(8):
            S.op("pe", lambda: nc.tensor.matmul(self.PS[5][0:32, BLK:2 * BLK], lhsT=G1[:, kc, 128:160], rhs=xj[5][:, kc, :], start=(kc == 0), stop=(kc == 7)), [Bsw, Bxj[5]], [self.BPS[5]])
        S.op("act", lambda: nc.scalar.activation(out=gh, in_=self.PS[5][:, 0:BLK], func=AF.Sigmoid), [self.BPS[5]], [Blt])
        S.op("act", lambda: nc.scalar.activation(out=lt[0:32, 4, :], in_=self.PS[5][0:32, BLK:2 * BLK], func=AF.Sigmoid), [self.BPS[5]], [Blt])
        col = b * T + t0
        for c in range(8):
            s_ = si % 2
            si += 1
            sg_ = stg[s_]
            Bs = Bstg[s_]
            cs = slice(c * 128, (c + 1) * 128)

            def bank():
                nonlocal pbk
                pbk += 1
                return pbk % 5

            prk = []
            for which, xsrc in ((0, 0), (1, 2), (2, 3)):
                pb = bank()
                for kc in range(8):
                    S.op("pe", lambda: nc.tensor.matmul(self.PS[pb][:, 0:BLK], lhsT=Wrkv[:, kc, which * D + c * 128:which * D + (c + 1) * 128], rhs=xj[xsrc][:, kc, :], start=(kc == 0), stop=(kc == 7)),
                         [BWrkv[which * 2 + (c // 4)], Bxj[xsrc]], [self.BPS[pb]])
                prk.append(pb)
            S.op("act", lambda: nc.scalar.copy(out=sg_[:, 0, :], in_=self.PS[prk[0]][:, 0:BLK]), [self.BPS[prk[0]]], [Bs])
            S.op("act", lambda: nc.scalar.copy(out=sg_[:, 8, :], in_=self.PS[prk[2]][:, 0:BLK]), [self.BPS[prk[2]]], [Bs])
            kraw = tmp[0]
            S.op("act", lambda: nc.scalar.copy(out=kraw, in_=self.PS[prk[1]][:, 0:BLK]), [self.BPS[prk[1]]], [Btmp[0]])
            S.op("dve", lambda: A_.tensor_scalar(out=tmp[1], in0=kraw, scalar1=self.pv("rw_k_k", c), scalar2=None, op0=ALU.mult), [Btmp[0]], [Btmp[1]])
            S.op("act", lambda: nc.scalar.activation(out=sqb, in_=tmp[1], func=AF.Square), [Btmp[1]], [Bsqb])
            pb = bank()
            S.op("pe", lambda: nc.tensor.matmul(self.PS[pb][:, 0:BLK], lhsT=blk64b, rhs=sqb, start=True, stop=True), [Bsqb, Bb64], [self.BPS[pb]])
            S.op("act", lambda: nc.scalar.activation(out=tmp[2], in_=self.PS[pb][:, 0:BLK], func=AF.Sqrt), [self.BPS[pb]], [Btmp[2]])
            S.op("dve", lambda: A_.tensor_scalar(out=tmp[2], in0=tmp[2], scalar1=1e-12, scalar2=None, op0=ALU.max), [Btmp[2]], [Btmp[2]])
            S.op("dve", lambda: A_.reciprocal(out=tmp[2], in_=tmp[2]), [Btmp[2]], [Btmp[2]])
            S.op("dve", lambda: A_.tensor_tensor(out=sg_[:, 5, :], in0=tmp[1], in1=tmp[2], op=ALU.mult), [Btmp[1], Btmp[2]], [Bs])
            pb = bank()
            S.op("pe", lambda: nc.tensor.matmul(self.PS[pb][:, 0:BLK], lhsT=G2a[:, cs], rhs=gh, start=True, stop=False), [Bsw, Blt], [self.BPS[pb]])
            S.op("pe", lambda: nc.tensor.matmul(self.PS[pb][:, 0:BLK], lhsT=G2b[:, cs], rhs=lt[0:32, 4, :], start=False, stop=True), [Bsw, Blt], [self.BPS[pb]])
            S.op("act", lambda: nc.scalar.copy(out=sg_[:, 9, :], in_=self.PS[pb][:, 0:BLK]), [self.BPS[pb]], [Bs])
            for d in range(2):
                pb = bank()
                S.op("pe", lambda: nc.tensor.matmul(self.PS[pb][:, 0:BLK], lhsT=W2[:, d, cs], rhs=lt[:, d, :], start=True, stop=True), [Bsw, Blt], [self.BPS[pb]])
                S.op("act", lambda: nc.scalar.activation(out=tmp[3], in_=self.PS[pb][:, 0:BLK], func=AF.Sigmoid, bias=self.pv(f"rw_w0_{d}", c)), [self.BPS[pb]], [Btmp[3]])
                S.op("dve", lambda: A_.tensor_scalar(out=sg_[:, 6 + d, :], in0=tmp[3], scalar1=-float(np.exp(-0.5)), scalar2=None, op0=ALU.mult), [Btmp[3]], [Bs])
                pb = bank()
                S.op("pe", lambda: nc.tensor.matmul(self.PS[pb][:, 0:BLK], lhsT=A2[:, d, cs], rhs=lt[:, 2 + d, :], start=True, stop=True), [Bsw, Blt], [self.BPS[pb]])
                S.op("act", lambda: nc.scalar.activation(out=tmp[4], in_=self.PS[pb][:, 0:BLK], func=AF.Sigmoid, bias=self.pv(f"rw_a0_{d}", c)), [self.BPS[pb]], [Btmp[4]])
                S.op("dve", lambda: A_.tensor_tensor(out=sg_[:, 3 + d, :], in0=tmp[4], in1=sg_[:, 5, :], op=ALU.mult), [Btmp[4], Bs], [Bs])
                S.op("dve", lambda: A_.tensor_scalar(out=tmp[5], in0=tmp[4], scalar1=-1.0, scalar2=None, op0=ALU.add), [Btmp[4]], [Btmp[5]])
                S.op("dve", lambda: A_.tensor_scalar(out=tmp[5], in0=tmp[5], scalar1=self.pv("rw_k_a", c), scalar2=1.0, op0=ALU.mult, op1=ALU.add), [Btmp[5]], [Btmp[5]])
                S.op("dve", lambda: A_.tensor_tensor(out=sg_[:, 1 + d, :], in0=tmp[5], in1=kraw, op=ALU.mult), [Btmp[5], Btmp[0]], [Bs])
            S.dma("pool", RWP[:, c * 128:(c + 1) * 128, col:col + BLK].rearrange("a p t -> p a t"), sg_, reads=[Bs])
        for tt in range(BLK // 128):
            vi = vi_ % 2
            vi_ += 1
            for hfv in range(2):
                pb = 4 - hfv
                for kc in range(8):
                    S.op("pe", lambda: nc.tensor.matmul(self.PS[pb][:, 0:512], lhsT=xj[3][:, kc, tt * 128:(tt + 1) * 128], rhs=Wrkv[:, kc, 2 * D + hfv * 512:2 * D + (hfv + 1) * 512], start=(kc == 0), stop=(kc == 7)),
                         [BWrkv[4 + hfv], Bxj[3]], [self.BPS[pb]])
                S.op("act", lambda: nc.scalar.copy(out=vts[vi][:, hfv * 512:(hfv + 1) * 512], in_=self.PS[pb][:, 0:512]), [self.BPS[pb]], [Bvts[vi]])
            S.dma("pool", Vtm[col + tt * 128:col + (tt + 1) * 128, :], vts[vi], reads=[Bvts[vi]])
    st.close()


def _rwkv_mixer(self, l, xin, og):
    RWP = self.scr("rwP", [10, D, TT])
    Vtm = self.scr("rwV", [TT, D], BF16)
    self.rwkv_proj(l, xin, RWP, Vtm)
    if getattr(self, "rw_stop", 0) == 1:
        return
    self.rwkv_scan(RWP, Vtm, og)


Prog.rwkv_proj = _rwkv_proj
Prog.rwkv_mixer = _rwkv_mixer


def _rwkv_scan(self, RWP, Vtm, og):
    nc, S = self.nc, self.S
    A_ = nc.vector
    U32 = mybir.dt.uint32
    RWD = self.scr("rwD", [NB, 8, 2, 2, 128, NCH * 128], BF16)
    RWS = self.scr("rwS", [NB, 8, 2, 128, 3 * NCH])
    skipA = getattr(self, "rw_skipA", False)
    st = Stage(self, "r2a")
    smask = st.sb("smask", [128, T])
    Bsm = Buf()
    S.dma("sp", smask, self.cd["scanmask"], writes=[Bsm])
    lw = st.sb("lw", [128, T]); kap = st.sb("kap", [128, T]); rr = st.sb("r", [128, T]); kt = st.sb("kt", [128, T]); be = st.sb("be", [128, T])
    cw = st.sb("cw", [128, T]); cm = st.sb("cm", [128, T]); en = st.sb("en", [128, T]); ex = st.sb("ex", [128, T])
    ABt = [st.sb(f"AB{i}", [128, NCH, 2, CH], BF16) for i in range(2)]
    KBt_ = [st.sb(f"KB{i}", [128, NCH, 2, CH], BF16) for i in range(2)]
    SC = [st.sb(f"SC{i}", [128, 3, NCH]) for i in range(2)]
    Blw, Bkap, Br, Bkt, Bbe, Bcw, Bcm, Ben, Bex = [Buf() for _ in range(9)]
    BAB, BKB, BSC = [[Buf(), Buf()] for _ in range(3)]
    it = 0
    v3 = lambda t_: t_.rearrange("p (c s) -> p c s", s=CH)
    for b in range(0 if skipA else NB):
        cols = slice(b * T, (b + 1) * T)
        for p in range(8):
            rows = slice(p * 128, (p + 1) * 128)
            for d in range(2):
                i = it % 2
                it += 1
                S.dma("sp", lw, RWP[6 + d, rows, cols], writes=[Blw])
                S.dma("sp", kap, RWP[5, rows, cols], writes=[Bkap])
                S.dma("sp", rr, RWP[0, rows, cols], writes=[Br])
                S.dma("sp", kt, RWP[1 + d, rows, cols], writes=[Bkt])
                S.dma("sp", be, RWP[3 + d, rows, cols], writes=[Bbe])
                S.op("dve", lambda: A_.tensor_tensor_scan(out=cw, data0=smask, data1=lw, initial=0.0, op0=ALU.mult, op1=ALU.add), [Bsm, Blw], [Bcw])
                if d == 1:
                    S.op("dve", lambda: A_.tensor_tensor(out=cm, in0=lw, in1=cw, op=ALU.subtract), [Blw, Bcw], [Bcm])
                    S.op("dve", lambda: A_.tensor_tensor(out=v3(en), in0=v3(cm), in1=v3(cw)[:, :, CH - 1:CH].to_broadcast([128, NCH, CH]), op=ALU.add), [Bcm, Bcw], [Ben])
                    S.op("dve", lambda: A_.tensor_copy(out=cw, in_=en), [Ben], [Bcw])
                m_idx = 32 if d == 0 else 31
                e_idx = CH - 1 if d == 0 else 0
                c3 = v3(cw)
                S.op("act", lambda: nc.scalar.activation(out=SC[i][:, 0, :], in_=c3[:, :, m_idx], func=AF.Exp), [Bcw], [BSC[i]])
                S.op("act", lambda: nc.scalar.activation(out=SC[i][:, 1, :], in_=c3[:, :, e_idx], func=AF.Exp), [Bcw], [BSC[i]])
                S.op("dve", lambda: A_.tensor_tensor(out=SC[i][:, 2, :], in0=c3[:, :, e_idx], in1=c3[:, :, m_idx], op=ALU.subtract), [Bcw], [BSC[i]])
                S.op("act", lambda: nc.scalar.activation(out=SC[i][:, 2, :], in_=SC[i][:, 2, :], func=AF.Exp), [BSC[i]], [BSC[i]])
                S.dma("pool", RWS[b, p, d], SC[i].rearrange("p a c -> p (a c)"), reads=[BSC[i]])
                S.op("dve", lambda: A_.tensor_tensor(out=v3(cm), in0=c3, in1=c3[:, :, m_idx:m_idx + 1].to_broadcast([128, NCH, CH]), op=ALU.subtract), [Bcw], [Bcm])
                S.op("act", lambda: nc.scalar.activation(out=en, in_=cm, func=AF.Exp, scale=-1.0), [Bcm], [Ben])
                S.op("dve", lambda: A_.tensor_tensor(out=ex, in0=cm, in1=lw, op=ALU.subtract), [Bcm, Blw], [Bex])
                S.op("act", lambda: nc.scalar.activation(out=ex, in_=ex, func=AF.Exp), [Bex], [Bex])
                S.op("act", lambda: nc.scalar.activation(out=cm, in_=cm, func=AF.Exp), [Bcm], [Bcm])
                S.op("dve", lambda: A_.tensor_tensor(out=ABt[i][:, :, 0, :], in0=v3(kap), in1=v3(ex), op=ALU.mult), [Bkap, Bex], [BAB[i]])
                S.op("dve", lambda: A_.tensor_tensor(out=ABt[i][:, :, 1, :], in0=v3(rr), in1=v3(cm), op=ALU.mult), [Br, Bcm], [BAB[i]])
                S.op("dve", lambda: A_.tensor_tensor(out=KBt_[i][:, :, 0, :], in0=v3(kt), in1=v3(en), op=ALU.mult), [Bkt, Ben], [BKB[i]])
                S.op("dve", lambda: A_.tensor_tensor(out=KBt_[i][:, :, 1, :], in0=v3(be), in1=v3(en), op=ALU.mult), [Bbe, Ben], [BKB[i]])
                S.dma("pool", RWD[b, p, d, 0], ABt[i].rearrange("p c a s -> p (c a s)"), reads=[BAB[i]])
                S.dma("pool", RWD[b, p, d, 1], KBt_[i].rearrange("p c a s -> p (c a s)"), reads=[BKB[i]])
    st.close()
    if getattr(self, "rw_stop", 0) == 2:
        return
    st = Stage(self, "r2b")
    S.pe_selfwait = getattr(self, "rw_selfwait", False)
    S.pe_drain = getattr(self, "rw_drain", 2)
    epsLN = st.sb("epsLN", [128, 1])
    Bgl = Buf()
    S.op("dve", lambda: A_.memset(epsLN, RW_LN_EPS), [], [Bgl])
    AB = [st.sb(f"AB{d}", [128, NCH, 128], BF16) for d in range(2)]
    KB = [st.sb(f"KB{d}", [128, NCH, 128], BF16) for d in range(2)]
    SCs = [st.sb(f"SC{d}", [128, 3, NCH]) for d in range(2)]
    Vst = st.sb("Vst", [64, NCH, 128], BF16)
    BABl, BKBl, BSCl = [[Buf(), Buf()] for _ in range(3)]
    BVst = Buf()
    chains = [(hd, d) for hd in range(2) for d in range(2)]
    IDT = BF16 if getattr(self, "rw_inv_bf16", True) else F32
    VU, GGb, AN0, ANp, Xp, Wf, KBtr = {}, {}, {}, {}, {}, {}, {}
    BVU, BGG, BAN0, BANp, BXp, BWf, BKBtr, BST, BS0, BtS, By = [dict() for _ in range(11)]
    for ch in chains:
        nm = f"{ch[0]}{ch[1]}"
        VU[ch] = st.sb("VU" + nm, [128, NCH, CH], BF16)
        GGb[ch] = st.sb("GG" + nm, [128, 128], BF16)
        AN0[ch] = st.sb("AN0" + nm, [128, 128], IDT)
        ANp[ch] = [st.sb(f"ANp{q}" + nm, [128, 128], IDT) for q in range(2)]
        Xp[ch] = [st.sb(f"X{q}" + nm, [128, CH], IDT) for q in range(2)]
        Wf[ch] = st.sb("Wf" + nm, [128, CH], IDT)
        KBtr[ch] = st.sb("KBt" + nm, [128, CH], BF16)
        BVU[ch], BGG[ch], BAN0[ch], BWf[ch], BKBtr[ch], BST[ch], BS0[ch], BtS[ch], By[ch] = [Buf() for _ in range(9)]
        BANp[ch] = [Buf(), Buf()]
        BXp[ch] = [Buf(), Buf()]
        S.op("dve", lambda: A_.memset(GGb[ch], 0.0), [], [BGG[ch]])
        S.op("dve", lambda: A_.memset(AN0[ch], 0.0), [], [BAN0[ch]])
    ST = [st.sb(f"ST{d}", [128, CH]) for d in range(2)]
    S0m = [st.sb(f"S0m{d}", [128, CH], BF16) for d in range(2)]
    tS = [st.sb(f"tS{d}", [128, CH]) for d in range(2)]
    yacc = [st.sb(f"yacc{d}", [128, T]) for d in range(2)]
    rl = st.sb("rl", [128, T]); k0 = st.sb("k0", [128, T]); k1 = st.sb("k1", [128, T]); vf = st.sb("vf", [128, T]); gg = st.sb("gg", [128, T])
    t0_ = st.sb("t0", [128, T]); t1_ = st.sb("t1", [128, T])
    ogb = st.sb("ogb", [128, T], BF16)
    Brl, Bk0, Bk1, Bvf, Bgg, Bt0, Bt1, Bogb = [Buf() for _ in range(8)]
    MERGE = getattr(self, "rw_merge", True)
    if MERGE:
        mKB = [st.sb(f"mKBt{d}", [128, 2, CH], BF16) for d in range(2)]
        mGG = [st.sb(f"mGG{d}", [128, 2, 128], BF16) for d in range(2)]
        mAN0 = [st.sb(f"mAN0{d}", [128, 2, 128], IDT) for d in range(2)]
        mANp = [[st.sb(f"mANp{q}{d}", [128, 2, 128], IDT) for q in range(2)] for d in range(2)]
        mXp = [[st.sb(f"mX{q}{d}", [128, 2, CH], IDT) for q in range(2)] for d in range(2)]
        mWf = [st.sb(f"mWf{d}", [128, 2, CH], IDT) for d in range(2)]
        mVU = [st.sb(f"mVU{d}", [128, NCH, 2, CH], BF16) for d in range(2)]
        M4x2 = [st.sb(f"M4x2{d}", [128, 2, 128]) for d in range(2)]
        mAx2 = [st.sb(f"mAx2{d}", [128, 2, CH]) for d in range(2)]
        mNx2 = [st.sb(f"mNx2{d}", [128, 2, CH]) for d in range(2)]
        I2 = st.sb("I2", [128, 2, CH])
        Bmk = Buf()
        mBKB, mBGG, mBAN0, mBWf, mBVU, mBST, mBS0, mBtS, mBy = [[Buf(), Buf()] for _ in range(9)]
        mBANp = [[Buf(), Buf()], [Buf(), Buf()]]
        mBXp = [[Buf(), Buf()], [Buf(), Buf()]]
        mUB = [[Buf() for _ in range(4)] for d in range(2)]
        for d in range(2):
            S.op("dve", lambda: A_.memset(mGG[d], 0.0), [], [mBGG[d]])
            S.op("dve", lambda: A_.memset(mAN0[d], 0.0), [], [mBAN0[d]])
            for hd in range(2):
                S.op("dve", lambda: A_.tensor_copy(out=M4x2[d][:, hd, :], in_=(self.masks[:, 0:128] if d == 0 else self.masks[:, 128:256])), [], [Bmk])
                S.op("dve", lambda: A_.tensor_copy(out=mAx2[d][:, hd, :], in_=(self.masks[:, 0:64] if d == 0 else self.masks[:, 128:192])), [], [Bmk])
                S.op("dve", lambda: A_.tensor_copy(out=mNx2[d][:, hd, :], in_=(self.masks[:, 128:192] if d == 0 else self.masks[:, 0:64])), [], [Bmk])
        for hd in range(2):
            S.op("dve", lambda: A_.tensor_copy(out=I2[64:128, hd, :], in_=self.ident[64:128, 64:128]), [], [Bmk])
    R = {}
    BR = {}
    for ci, ch in enumerate(chains):
        b0, b1 = self.PS[2 * ci], self.PS[2 * ci + 1]
        R[ch] = dict(GA=b0[:, 0:128], LV=b0[:, 192:320], Wp=b0[:, 384:448],
                     XL=b1[:, 320:384], Up=b1[:, 448:512], Nn=b1[:, 128:192],
                     Yp=b1[:, 0:64], Sd=b1[:, 64:128], TR=b1.bitcast(BF16)[:, 512:576])
        u0, u1, u2, u3 = Buf(), Buf(), Buf(), Buf()
        ykp = [u2] if ch[0] == 0 else [u3]
        BR[ch] = dict(GAlo=[u0], GAup=[u1], GA=[u0, u1], LV=[u1], Wp=[u1], XL=[u3], Up=[u3], Nn=[u3], Yp=ykp, Sd=ykp, TR=[u2, u3], ALL=[u0, u1, u2, u3])
    up, lo = slice(64, 128), slice(0, 64)
    mU = lambda ap: ap.bitcast(U32)
    cf = list(range(NCH))
    cbk = list(range(TC // CH - 1, -1, -1)) + list(range(NCH - 1, TC // CH - 1, -1))
    order = [cf, cbk]
    dbgn = getattr(self, "rw_dbg", None)
    for b in range(NB):
        cols = slice(b * T, (b + 1) * T)
        for p in range(8):
            if dbgn is not None and (b * 8 + p) >= dbgn[0]:
                continue
            rows = slice(p * 128, (p + 1) * 128)
            for d in range(2):
                S.dma("sp", AB[d], RWD[b, p, d, 0].rearrange("k (c x) -> k c x", x=128), writes=[BABl[d]])
                S.dma("sp", KB[d], RWD[b, p, d, 1].rearrange("k (c x) -> k c x", x=128), writes=[BKBl[d]])
                S.dma("sp", SCs[d], RWS[b, p, d].rearrange("k (a c) -> k a c", a=3), writes=[BSCl[d]])
            S.dma("sp", Vst, Vtm[cols, rows].rearrange("(c s) v -> s c v", s=CH), writes=[BVst])
            S.dma("sp", rl, RWP[0, rows, cols], writes=[Brl])
            S.dma("sp", k0, RWP[1, rows, cols], writes=[Bk0])
            S.dma("sp", k1, RWP[2, rows, cols], writes=[Bk1])
            S.dma("sp", vf, RWP[8, rows, cols], writes=[Bvf])
            S.dma("sp", gg, RWP[9, rows, cols], writes=[Bgg])
            if MERGE:
                for d in range(2):
                    S.op("pool", lambda: nc.gpsimd.tensor_copy(out=mVU[d][lo, :, :, :], in_=Vst.rearrange("s c (h v) -> s c h v", h=2)), [BVst], [mBVU[d]])
                    S.op("dve", lambda: A_.memset(ST[d], 0.0), [], [mBST[d]])
                    S.op("dve", lambda: A_.memset(S0m[d], 0.0), [], [mBS0[d]])

                def dstep(d, step):
                    c = order[d][step]
                    cs = slice(c * CH, (c + 1) * CH)
                    bA, bB, bC, bD = [self.PS[4 * d + q] for q in range(4)]
                    uA, uB, uC, uD = mUB[d]
                    h2 = lambda ap: ap.rearrange("p (h x) -> p h x", h=2)
                    GA = h2(bA[:, 0:256]); LV = h2(bB[:, 0:256]); Wp = h2(bB[:, 256:384])
                    XL = h2(bC[:, 0:128]); Up_ = h2(bC[:, 128:256]); Nn = h2(bC[:, 256:384])
                    Yp = bD[:, 0:64]; Sd = bD[:, 64:128]; TR = h2(bD.bitcast(BF16)[:, 512:640])
                    KP = [slice(0, 64), slice(64, 128)]
                    for hd in range(2):
                        kp = KP[hd]
                        S.op("pe", lambda: nc.tensor.transpose(out=TR[:, hd, :], in_=KB[d][kp, c, :], identity=self.identb[kp, kp]), [BKBl[d]], [uD], pemode=("T", hd))
                        S.op("pe", lambda: nc.tensor.matmul(GA[lo, hd, :], lhsT=KB[d][kp, c, 0:64], rhs=AB[d][kp, c, :], start=True, stop=True), [BKBl[d], BABl[d]], [uA], pemode=("g", hd))
                        S.op("pe", lambda: nc.tensor.matmul(GA[up, hd, :], lhsT=KB[d][kp, c, 64:128], rhs=AB[d][kp, c, :], start=True, stop=True), [BKBl[d], BABl[d]], [uA], pemode=("g", hd))
                        S.op("pe", lambda: nc.tensor.matmul(Nn[up, hd, :], lhsT=AB[d][kp, c, 0:64], rhs=KB[d][kp, c, 64:128], start=True, stop=True), [BKBl[d], BABl[d]], [uC], pemode=("g", hd))
                    yield
                    S.op("act", lambda: nc.scalar.copy(out=mKB[d], in_=TR), [uD], [mBKB[d]])
                    S.op("dve", lambda: A_.copy_predicated(out=mGG[d], mask=mU(M4x2[d][:]), data=GA), [uA, Bmk], [mBGG[d]])
                    S.op("dve", lambda: A_.copy_predicated(out=mAN0[d][up, :, 0:64], mask=mU(mAx2[d][up, :, :]), data=GA[up, :, 0:64]), [uA, Bmk], [mBAN0[d]])
                    S.op("dve", lambda: A_.copy_predicated(out=mAN0[d][up, :, 64:128], mask=mU(mNx2[d][up, :, :]), data=Nn[up, :, :]), [uC, Bmk], [mBAN0[d]])
                    S.op("dve", lambda: A_.tensor_tensor(out=mXp[d][0][up, :, :], in0=I2[up, :, :], in1=mAN0[d][up, :, 0:64], op=ALU.subtract), [mBAN0[d], Bmk], [mBXp[d][0]])
                    yield
                    cur, Bcur = mAN0[d], mBAN0[d]
                    xq = 0
                    for lv in range(1, 7):
                        nx, Bnx = mANp[d][lv % 2], mBANp[d][lv % 2]
                        for hd in range(2):
                            if lv <= 5:
                                if lv < 5:
                                    S.op("pe", lambda: nc.tensor.matmul(LV[up, hd, 0:64], lhsT=cur[up, hd, 64:128], rhs=cur[up, hd, 0:64], start=True, stop=True), [Bcur], [uB], pemode=("f",))
                                S.op("pe", lambda: nc.tensor.matmul(LV[up, hd, 64:128], lhsT=cur[up, hd, 0:64], rhs=cur[up, hd, 64:128], start=True, stop=True), [Bcur], [uB], pemode=("f",))
                            if lv >= 2:
                                S.op("pe", lambda: nc.tensor.matmul(XL[up, hd, :], lhsT=cur[up, hd, 64:128], rhs=mXp[d][xq][up, hd, :], start=True, stop=True), [Bcur, mBXp[d][xq]], [uC], pemode=("f",))
                        yield
                        if lv <= 5:
                            if lv < 5:
                                S.op("act", lambda: nc.scalar.copy(out=nx[up, :, :], in_=LV[up, :, :]), [uB], [Bnx])
                            else:
                                S.op("act", lambda: nc.scalar.copy(out=nx[up, :, 64:128], in_=LV[up, :, 64:128]), [uB], [Bnx])
                        if lv >= 2:
                            S.op("dve", lambda: A_.tensor_tensor(out=mXp[d][1 - xq][up, :, :], in0=XL[up, :, :], in1=mXp[d][xq][up, :, :], op=ALU.add), [uC, mBXp[d][xq]], [mBXp[d][1 - xq]])
                            xq = 1 - xq
                        if lv <= 5:
                            cur, Bcur = nx, Bnx
                        yield
                    for hd in range(2):
                        kp = KP[hd]
                        S.op("pe", lambda: nc.tensor.matmul(Wp[up, hd, :], lhsT=AB[d][kp, c, 0:64], rhs=S0m[d][kp, :], start=True, stop=False), [BABl[d], mBS0[d]], [uB], pemode=("g", hd))
                        S.op("pe", lambda: nc.tensor.matmul(Wp[up, hd, :], lhsT=mGG[d][lo, hd, 0:64], rhs=mVU[d][lo, c, hd, :], start=False, stop=True), [mBGG[d], mBVU[d]], [uB], pemode=("w2",))
                    yield
                    S.op("act", lambda: nc.scalar.copy(out=mWf[d][up, :, :], in_=Wp[up, :, :]), [uB], [mBWf[d]])
                    yield
                    for hd in range(2):
                        S.op("pe", lambda: nc.tensor.matmul(Up_[up, hd, :], lhsT=mXp[d][xq][up, hd, :], rhs=mWf[d][up, hd, :], start=True, stop=True), [mBXp[d][xq], mBWf[d]], [uC], pemode=("f",))
                    yield
                    S.op("act", lambda: nc.scalar.activation(out=mVU[d][up, c, :, :], in_=Up_[up, :, :], func=AF.Copy, scale=-1.0), [uC], [mBVU[d]])
                    yield
                    for hd in range(2):
                        kp = KP[hd]
                        S.op("pe", lambda: nc.tensor.matmul(Yp[kp, :], lhsT=S0m[d][kp, :], rhs=AB[d][kp, c, 64:128], start=True, stop=False), [mBS0[d], BABl[d]], [uD], pemode=("g", hd))
                        S.op("pe", lambda: nc.tensor.matmul(Yp[kp, :], lhsT=mVU[d][:, c, hd, :], rhs=mGG[d][:, hd, 64:128], start=False, stop=True), [mBVU[d], mBGG[d]], [uD], pemode=("full",))
                    for hd in range(2):
                        kp = KP[hd]
                        S.op("pe", lambda: nc.tensor.matmul(Sd[kp, :], lhsT=mKB[d][:, hd, :], rhs=mVU[d][:, c, hd, :], start=True, stop=True), [mBKB[d], mBVU[d]], [uD], pemode=("full",))
                    yield
                    S.op("act", lambda: nc.scalar.copy(out=yacc[d][:, cs], in_=Yp), [uD], [mBy[d]])
                    S.op("act", lambda: nc.scalar.activation(out=tS[d], in_=Sd, func=AF.Identity, scale=SCs[d][:, 2, c:c + 1]), [uD, BSCl[d]], [mBtS[d]])
                    S.op("dve", lambda: A_.scalar_tensor_tensor(out=ST[d], in0=ST[d], scalar=SCs[d][:, 1, c:c + 1], in1=tS[d], op0=ALU.mult, op1=ALU.add), [mBST[d], mBtS[d], BSCl[d]], [mBST[d]])
                    if step + 1 < NCH:
                        cn = order[d][step + 1]
                        S.op("dve", lambda: A_.tensor_scalar(out=S0m[d], in0=ST[d], scalar1=SCs[d][:, 0, cn:cn + 1], scalar2=None, op0=ALU.mult), [mBST[d], BSCl[d]], [mBS0[d]])

                for step in range(NCH if dbgn is None else dbgn[1]):
                    gens = [dstep(d, step) for d in range(2)]
                    while gens:
                        for g_ in list(gens):
                            try:
                                next(g_)
                            except StopIteration:
                                gens.remove(g_)
            else:
                for ch in chains:
                    hd, d = ch
                    kp = slice(hd * 64, hd * 64 + 64)
                    S.op("pool", lambda: nc.gpsimd.tensor_copy(out=VU[ch][lo, :, :], in_=Vst[:, :, hd * 64:(hd + 1) * 64]), [BVst], [BVU[ch]])
                    S.op("dve", lambda: A_.memset(ST[d][kp, :], 0.0), [], [BST[ch]])
                    S.op("dve", lambda: A_.memset(S0m[d][kp, :], 0.0), [], [BS0[ch]])
                def chain_step(ch, step):
                    hd, d = ch
                    kp = slice(hd * 64, hd * 64 + 64)
                    c = order[d][step]
                    cs = slice(c * CH, (c + 1) * CH)
                    r_, br_ = R[ch], BR[ch]
                    M4 = self.masks[:, 0:128] if d == 0 else self.masks[:, 128:256]
                    mA = self.masks[up, 0:64] if d == 0 else self.masks[up, 128:192]
                    mN = self.masks[up, 128:192] if d == 0 else self.masks[up, 0:64]
                    S.op("pe", lambda: nc.tensor.transpose(out=r_["TR"], in_=KB[d][kp, c, :], identity=self.identb[kp, kp]), [BKBl[d]], br_["TR"], pemode=("T", hd))
                    S.op("act", lambda: nc.scalar.copy(out=KBtr[ch], in_=r_["TR"]), br_["TR"], [BKBtr[ch]])
                    S.op("pe", lambda: nc.tensor.matmul(r_["GA"][lo, :], lhsT=KB[d][kp, c, 0:64], rhs=AB[d][kp, c, :], start=True, stop=True), [BKBl[d], BABl[d]], br_["GAlo"], pemode=("g", hd))
                    S.op("pe", lambda: nc.tensor.matmul(r_["GA"][up, :], lhsT=KB[d][kp, c, 64:128], rhs=AB[d][kp, c, :], start=True, stop=True), [BKBl[d], BABl[d]], br_["GAup"], pemode=("g", hd))
                    S.op("pe", lambda: nc.tensor.matmul(r_["Nn"][up, :], lhsT=AB[d][kp, c, 0:64], rhs=KB[d][kp, c, 64:128], start=True, stop=True), [BKBl[d], BABl[d]], br_["Nn"], pemode=("g", hd))
                    yield
                    S.op("dve", lambda: A_.copy_predicated(out=GGb[ch], mask=mU(M4), data=r_["GA"]), br_["GA"], [BGG[ch]])
                    S.op("dve", lambda: A_.copy_predicated(out=AN0[ch][up, 0:64], mask=mU(mA), data=r_["GA"][up, 0:64]), br_["GAup"], [BAN0[ch]])
                    S.op("dve", lambda: A_.copy_predicated(out=AN0[ch][up, 64:128], mask=mU(mN), data=r_["Nn"][up, :]), br_["Nn"], [BAN0[ch]])
                    S.op("dve", lambda: A_.tensor_tensor(out=Xp[ch][0][up, :], in0=self.ident[up, up], in1=AN0[ch][up, 0:64], op=ALU.subtract), [BAN0[ch]], [BXp[ch][0]])
                    yield
                    cur, Bcur = AN0[ch], BAN0[ch]
                    xq = 0
                    for lv in range(1, 7):
                        nx, Bnx = ANp[ch][lv % 2], BANp[ch][lv % 2]
                        if lv <= 5:
                            if lv < 5:
                                S.op("pe", lambda: nc.tensor.matmul(r_["LV"][up, 0:64], lhsT=cur[up, 64:128], rhs=cur[up, 0:64], start=True, stop=True), [Bcur], br_["LV"], pemode=("f",))
                            S.op("pe", lambda: nc.tensor.matmul(r_["LV"][up, 64:128], lhsT=cur[up, 0:64], rhs=cur[up, 64:128], start=True, stop=True), [Bcur], br_["LV"], pemode=("f",))
                        if lv >= 2:
                            S.op("pe", lambda: nc.tensor.matmul(r_["XL"][up, :], lhsT=cur[up, 64:128], rhs=Xp[ch][xq][up, :], start=True, stop=True), [Bcur, BXp[ch][xq]], br_["XL"], pemode=("f",))
                        yield
                        if lv <= 5:
                            if lv < 5:
                                S.op("act", lambda: nc.scalar.copy(out=nx[up, :], in_=r_["LV"][up, :]), br_["LV"], [Bnx])
                            else:
                                S.op("act", lambda: nc.scalar.copy(out=nx[up, 64:128], in_=r_["LV"][up, 64:128]), br_["LV"], [Bnx])
                        if lv >= 2:
                            S.op("dve", lambda: A_.tensor_tensor(out=Xp[ch][1 - xq][up, :], in0=r_["XL"][up, :], in1=Xp[ch][xq][up, :], op=ALU.add), br_["XL"] + [BXp[ch][xq]], [BXp[ch][1 - xq]])
                            xq = 1 - xq
                        if lv <= 5:
                            cur, Bcur = nx, Bnx
                        if lv < 6:
                            yield
                    yield
                    S.op("pe", lambda: nc.tensor.matmul(r_["Wp"][up, :], lhsT=AB[d][kp, c, 0:64], rhs=S0m[d][kp, :], start=True, stop=False), [BABl[d], BS0[ch]], br_["Wp"], pemode=("g", hd))
                    S.op("pe", lambda: nc.tensor.matmul(r_["Wp"][up, :], lhsT=GGb[ch][lo, 0:64], rhs=VU[ch][lo, c, :], start=False, stop=True), [BGG[ch], BVU[ch]], br_["Wp"], pemode=("w2",))
                    yield
                    S.op("act", lambda: nc.scalar.copy(out=Wf[ch][up, :], in_=r_["Wp"][up, :]), br_["Wp"], [BWf[ch]])
                    yield
                    S.op("pe", lambda: nc.tensor.matmul(r_["Up"][up, :], lhsT=Xp[ch][xq][up, :], rhs=Wf[ch][up, :], start=True, stop=True), [BXp[ch][xq], BWf[ch]], br_["Up"], pemode=("f",))
                    yield
                    S.op("act", lambda: nc.scalar.activation(out=VU[ch][up, c, :], in_=r_["Up"][up, :], func=AF.Copy, scale=-1.0), br_["Up"], [BVU[ch]])
                    yield
                    S.op("pe", lambda: nc.tensor.matmul(r_["Yp"][kp, :], lhsT=S0m[d][kp, :], rhs=AB[d][kp, c, 64:128], start=True, stop=False), [BS0[ch], BABl[d]], br_["Yp"], pemode=("g", hd))
                    S.op("pe", lambda: nc.tensor.matmul(r_["Yp"][kp, :], lhsT=VU[ch][:, c, :], rhs=GGb[ch][:, 64:128], start=False, stop=True), [BVU[ch], BGG[ch]], br_["Yp"], pemode=("full",))
                    yield
                    S.op("act", lambda: nc.scalar.copy(out=yacc[d][kp, cs], in_=r_["Yp"][kp, :]), br_["Yp"], [By[ch]])
                    S.op("pe", lambda: nc.tensor.matmul(r_["Sd"][kp, :], lhsT=KBtr[ch], rhs=VU[ch][:, c, :], start=True, stop=True), [BKBtr[ch], BVU[ch]], br_["Sd"], pemode=("full",))
                    yield
                    S.op("act", lambda: nc.scalar.activation(out=tS[d][kp, :], in_=r_["Sd"][kp, :], func=AF.Identity, scale=SCs[d][kp, 2, c:c + 1]), br_["Sd"] + [BSCl[d]], [BtS[ch]])
                    S.op("dve", lambda: A_.scalar_tensor_tensor(out=ST[d][kp, :], in0=ST[d][kp, :], scalar=SCs[d][kp, 1, c:c + 1], in1=tS[d][kp, :], op0=ALU.mult, op1=ALU.add), [BST[ch], BtS[ch], BSCl[d]], [BST[ch]])
                    if step + 1 < NCH:
                        cn = order[d][step + 1]
                        S.op("dve", lambda: A_.tensor_scalar(out=S0m[d][kp, :], in0=ST[d][kp, :], scalar1=SCs[d][kp, 0, cn:cn + 1], scalar2=None, op0=ALU.mult), [BST[ch], BSCl[d]], [BS0[ch]])

                for step in range(NCH if dbgn is None else dbgn[1]):
                    gens = [chain_step(ch, step) for ch in chains]
                    if getattr(self, "rw_order", "phase") == "chain":
                        for g_ in gens:
                            for _ in g_:
                                pass
                        gens = []
                    while gens:
                        for g_ in list(gens):
                            try:
                                next(g_)
                            except StopIteration:
                                gens.remove(g_)
            if MERGE:
                RB = {0: [mUB[0][0]], 1: [mUB[0][1]]}
                By_all = [mBy[0], mBy[1]]
            else:
                RB = {0: [BR[chains[0]]["ALL"][0], BR[chains[0]]["ALL"][1]], 1: [BR[chains[0]]["ALL"][2], BR[chains[0]]["ALL"][3]]}
                By_all = [By[ch] for ch in chains]
            Byy = Buf()
            S.op("dve", lambda: A_.tensor_tensor(out=yacc[0], in0=yacc[0], in1=yacc[1], op=ALU.add), By_all, [Byy])
            NP_ = 6
            W_ = T // NP_
            for pc in range(NP_):
                sl_ = slice(pc * W_, (pc + 1) * W_)
                pb = pc % 2
                S.op("pe", lambda: nc.tensor.matmul(self.PS[pb][:, 0:W_], lhsT=self.blk64, rhs=yacc[0][:, sl_], start=True, stop=True), [Byy], RB[pb])
                S.op("dve", lambda: A_.scalar_tensor_tensor(out=t0_[:, sl_], in0=self.PS[pb][:, 0:W_], scalar=-1.0 / 64, in1=yacc[0][:, sl_], op0=ALU.mult, op1=ALU.add), RB[pb] + [Byy], [Bt0])
            S.op("act", lambda: nc.scalar.activation(out=t1_, in_=t0_, func=AF.Square), [Bt0], [Bt1])
            for pc in range(NP_):
                sl_ = slice(pc * W_, (pc + 1) * W_)
                pb = pc % 2
                S.op("pe", lambda: nc.tensor.matmul(self.PS[pb][:, 0:W_], lhsT=self.blk64, rhs=t1_[:, sl_], start=True, stop=True), [Bt1], RB[pb])
                S.op("act", lambda: nc.scalar.activation(out=yacc[1][:, sl_], in_=self.PS[pb][:, 0:W_], func=AF.Sqrt, scale=1.0 / 64, bias=epsLN), RB[pb] + [Bgl], [Byy])
            S.op("dve", lambda: A_.reciprocal(out=yacc[1], in_=yacc[1]), [Byy], [Byy])
            S.op("dve", lambda: A_.tensor_tensor(out=t0_, in0=t0_, in1=yacc[1], op=ALU.mult), [Bt0, Byy], [Bt0])
            S.op("act", lambda: nc.scalar.activation(out=t0_, in_=t0_, func=AF.Identity, scale=self.pv("rw_ln_w", p), bias=self.pv("rw_ln_b", p)), [Bt0], [Bt0])
            S.op("dve", lambda: A_.tensor_tensor(out=k0, in0=k0, in1=k1, op=ALU.add), [Bk0, Bk1], [Bk0])
            S.op("dve", lambda: A_.scalar_tensor_tensor(out=t1_, in0=rl, scalar=self.pv("rw_r_k", p), in1=k0, op0=ALU.mult, op1=ALU.mult), [Brl, Bk0, Bt1], [Bt1])
            for pc in range(NP_):
                sl_ = slice(pc * W_, (pc + 1) * W_)
                pb = pc % 2
                S.op("pe", lambda: nc.tensor.matmul(self.PS[pb][:, 0:W_], lhsT=self.blk64, rhs=t1_[:, sl_], start=True, stop=True), [Bt1], RB[pb])
                S.op("dve", lambda: A_.tensor_tensor(out=yacc[1][:, sl_], in0=self.PS[pb][:, 0:W_], in1=vf[:, sl_], op=ALU.mult), RB[pb] + [Bvf, Byy], [Byy])
            S.op("dve", lambda: A_.tensor_tensor(out=t0_, in0=t0_, in1=yacc[1], op=ALU.add), [Bt0, Byy], [Bt0])
            S.op("dve", lambda: A_.tensor_tensor(out=ogb, in0=t0_, in1=gg, op=ALU.mult), [Bt0, Bgg], [Bogb])
            S.dma("pool", og[rows, cols], ogb, reads=[Bogb])
            for b_ in By_all:
                b_.r.append(Byy.w)
    st.close()
    S.pe_selfwait = False
    S.pe_drain = 0


Prog.rwkv_scan = _rwkv_scan
```

```python
from contextlib import ExitStack
import numpy as np
import concourse.bass as bass
import concourse.mybir as mybir
from concourse.bass_utils import run_bass_kernel_spmd

F32 = mybir.dt.float32
BF16 = mybir.dt.bfloat16
AF = mybir.ActivationFunctionType
ALU = mybir.AluOpType

NCORES = 8
NB = 2
TC = 256
TL = 2048
T = TC + TL
TT = NB * T
D = 1024
DEPTH = 4
DFF = 2816
NFC = DFF // 128
BLK = 256
NBLK = T // BLK
EPS = 1e-6
CH = 64
NCH = T // CH
RW_LN_EPS = 64e-5
MLA_SCALE = 96 ** -0.5


class Buf:
    __slots__ = ("name", "w", "r")

    def __init__(self, name=""):
        self.name = name
        self.w = None
        self.r = []


class _Eng:
    def __init__(self, S, name, eng):
        self.S = S
        self.name = name
        self.eng = eng
        self.sem = None
        self.count = 0
        self.seen = {}
        self.nsem = 0
        self.ninst = 0
        self.own = set()

    def new_sem(self):
        self.sem = self.S.nc.alloc_semaphore(f"e_{self.name}_{self.nsem}")
        self.own.add(id(self.sem))
        self.nsem += 1
        self.count = 0

    def wait(self, ev):
        sem, val = ev
        k = id(sem)
        if self.name == "pe" and k in self.own and not self.S.pe_selfwait:
            return
        if self.seen.get(k, 0) >= val:
            return
        self.eng.wait_ge(sem, val)
        self.seen[k] = val


class Sched:
    EPOCH = 30000

    def __init__(self, nc, ndma_sems=48):
        self.nc = nc
        self.E = {}
        for name, eng in (("pe", nc.tensor), ("dve", nc.vector), ("act", nc.scalar),
                          ("pool", nc.gpsimd), ("sp", nc.sync)):
            e = _Eng(self, name, eng)
            e.new_sem()
            self.E[name] = e
        self.dsems = [[nc.alloc_semaphore(f"d{i}"), 0] for i in range(ndma_sems)]
        self.dnext = 0
        self._keep = []
        self.pe_selfwait = False
        self.pe_drain = 0
        self.last_pemode = None

    @staticmethod
    def _deps(reads, writes):
        deps = []
        for b in reads:
            if b.w is not None:
                deps.append(b.w)
        for b in writes:
            if b.w is not None:
                deps.append(b.w)
            deps.extend(b.r)
        return deps

    @staticmethod
    def _mark(ev, reads, writes):
        for b in writes:
            b.w = ev
            b.r = []
        for b in reads:
            if b not in writes:
                b.r.append(ev)
                if len(b.r) > 32:
                    b.r = b.r[-32:]

    def op(self, ename, fn, reads=(), writes=(), pemode=None):
        e = self.E[ename]
        for ev in self._deps(reads, writes):
            e.wait(ev)
        drain = False
        if ename == "pe":
            drain = self.pe_drain == 1 or (self.pe_drain == 2 and pemode != self.last_pemode)
            self.last_pemode = pemode
        if drain and e.count > 0:
            k = id(e.sem)
            if e.seen.get(k, 0) < e.count:
                e.eng.wait_ge(e.sem, e.count)
                e.seen[k] = e.count
        if e.count >= self.EPOCH:
            self._keep.append(e.sem)
            e.new_sem()
        inst = fn()
        e.count += 1
        e.ninst += 1
        inst.then_inc(e.sem, 1)
        ev = (e.sem, e.count)
        self._mark(ev, reads, writes)
        return ev

    def dma(self, qname, out, in_, reads=(), writes=(), **kw):
        q = self.E[qname]
        for ev in self._deps(reads, writes):
            q.wait(ev)
        slot = self.dsems[self.dnext % len(self.dsems)]
        self.dnext += 1
        if slot[1] >= self.EPOCH:
            self._keep.append(slot[0])
            slot[0] = self.nc.alloc_semaphore(f"dx{self.dnext}")
            slot[1] = 0
        if slot[1] > 0:
            q.wait((slot[0], slot[1]))
        q.eng.dma_start(out=out, in_=in_, **kw).then_inc(slot[0], 16)
        q.ninst += 1
        slot[1] += 16
        ev = (slot[0], slot[1])
        self._mark(ev, reads, writes)
        return ev

    def barrier(self):
        evs = [(e.sem, e.count) for e in self.E.values() if e.count > 0]
        evs += [(s[0], s[1]) for s in self.dsems if s[1] > 0]
        for e in self.E.values():
            for ev in evs:
                if ev[0] is e.sem:
                    continue
                e.wait(ev)


class PVec:
    def __init__(self):
        self.cols = []
        self.off = {}
        self.n = 0

    def add(self, name, vec):
        vec = np.asarray(vec, dtype=np.float32).reshape(-1)
        assert vec.size % 128 == 0
        nch = vec.size // 128
        self.off[name] = (self.n, nch)
        self.cols.append(np.ascontiguousarray(vec.reshape(nch, 128).T))
        self.n += nch

    def array(self):
        return np.ascontiguousarray(np.concatenate(self.cols, axis=1))


def pvec_layout(inputs):
    pv = PVec()
    for l in range(DEPTH):
        pv.add(f"b_mod{l}", inputs["b_mod"][l])
        pv.add(f"norm1_{l}", inputs["norm1"][l])
        pv.add(f"norm2_{l}", inputs["norm2"][l])
        for k in range(3):
            pv.add(f"conv{l}_{k}", inputs["ffn_conv"][l, k])
        pv.add(f"convb{l}", inputs["ffn_conv_b"][l])
    pv.add("norm_f", inputs["norm_f"])
    for d in range(2):
        for j in range(2):
            pv.add(f"hg_lb{d}_{j}", inputs["hg_lb"][d, j])
    for j in range(2):
        pv.add(f"hg_norm{j}", inputs["hg_norm"][j])
    for k in range(6):
        pv.add(f"rw_mu{k}", inputs["rw_mu"][0, k])
    for d in range(2):
        pv.add(f"rw_w0_{d}", inputs["rw_w0"][0, d])
        pv.add(f"rw_a0_{d}", inputs["rw_a0"][0, d])
    for nm in ("rw_k_k", "rw_k_a", "rw_r_k", "rw_ln_w", "rw_ln_b"):
        pv.add(nm, inputs[nm][0])
    pv.add("mla_q_norm", inputs["mla_q_norm"][0])
    pv.add("mla_kv_norm", inputs["mla_kv_norm"][0])
    return pv


def make_consts():
    c = {}
    c["ident"] = np.eye(128, dtype=np.float32)
    c["ones"] = np.ones((128, 128), dtype=np.float32)
    bo = np.zeros((128, 128), dtype=np.float32)
    bo[:64, :64] = 1.0
    bo[64:, 64:] = 1.0
    c["blk64"] = bo
    i = np.arange(64)[:, None]
    t = np.arange(64)[None, :]
    su = (i < t).astype(np.float32)
    iu = (i <= t).astype(np.float32)
    sl = (i > t).astype(np.float32)
    il = (i >= t).astype(np.float32)
    c["masks"] = np.concatenate([np.concatenate([su, iu, sl, il], axis=1)] * 2, axis=0)
    m = np.ones((128, T), dtype=np.float32)
    m[:, ::CH] = 0.0
    c["scanmask"] = m
    nq = 8
    inv_freq = (10000.0 ** (-np.arange(nq, dtype=np.float32) / nq)).astype(np.float32)
    pos = np.arange(TL)
    row = (pos // 64).astype(np.float32)
    col = (pos % 64).astype(np.float32)
    ang_r = row[:, None] * inv_freq
    ang_c = col[:, None] * inv_freq
    ang = np.concatenate([ang_r, ang_r, ang_c, ang_c], axis=-1).astype(np.float32)
    cos = np.ones((32, T), dtype=np.float32)
    sin = np.zeros((32, T), dtype=np.float32)
    cos[:, TC:] = np.cos(ang).T
    sin[:, TC:] = np.sin(ang).T
    c["rope_cos"] = cos
    c["rope_sin"] = sin
    return c


WEIGHT_NAMES = ["w_mod", "ffn_w_in", "ffn_w_out", "hg_w_in", "hg_w_o", "rw_w_rkv", "rw_w1", "rw_w2",
                "rw_a1", "rw_a2", "rw_g1", "rw_g2", "rw_w_o", "mla_w_dqkv", "mla_w_uq", "mla_w_ukv", "mla_w_o"]


class Stage:
    def __init__(self, P, name):
        self.P = P
        self.name = name
        self.es = ExitStack()
        P.nstage += 1
        self.k = 0

    def sb(self, name, shape, dt=F32):
        self.k += 1
        h = self.es.enter_context(self.P.nc.sbuf_tensor(f"{self.name}{self.P.nstage}_{name}_{self.k}", list(shape), dt))
        return h.ap()

    def close(self):
        self.P.S.barrier()
        self.es.close()


class Prog:
    def __init__(self, wshapes, pv_off, npv, dbg=(), xin_name=None):
        nc = bass.Bass("TRN2", target_bir_lowering=False)
        self.nc = nc
        self.dbg = set(dbg)
        self.pv_off = pv_off
        self.nstage = 0
        di = lambda n, s: nc.dram_tensor(n, list(s), F32, kind="ExternalInput").ap()
        self.x = di("x", [NB, TL, D])
        self.ctx = di("ctx", [NB, TC, D])
        self.cvec = di("cvec", [3, D])
        self.pvec_d = di("pvec", [128, npv])
        self.cd = {n: di("c_" + n, s) for n, s in (("ident", [128, 128]), ("ones", [128, 128]), ("blk64", [128, 128]),
                                                    ("masks", [128, 256]), ("scanmask", [128, T]),
                                                    ("rope_cos", [32, T]), ("rope_sin", [32, T]))}
        self.W = {n: di(n, wshapes[n]) for n in WEIGHT_NAMES}
        self.out = nc.dram_tensor("out", [NB, TL, D], F32, kind="ExternalOutput").ap()
        self.scratch = {}
        self.S = Sched(nc)
        S = self.S
        self.PS = [nc.alloc_psum_tensor(f"psb{i}", [128, 512], F32).ap() for i in range(8)]
        self.BPS = [Buf(f"ps{i}") for i in range(8)]
        g = lambda n, s, dt=F32: nc.alloc_sbuf_tensor("g_" + n, list(s), dt).ap()
        self.ident = g("ident", [128, 128])
        self.identb = g("identb", [128, 128], BF16)
        self.onesf = g("onesf", [128, 128])
        self.onesb = g("onesb", [128, 128], BF16)
        self.blk64 = g("blk64", [128, 128])
        self.masks = g("masks", [128, 256])
        self.pvec = g("pvec", [128, npv])
        self.MOD = g("MOD", [128, DEPTH, 48, 3])
        self.MA = g("MA", [128, DEPTH, 2, 8, 3])
        self.epsD = g("epsD", [128, 1])
        self.BC = Buf("consts")
        self.BMOD = Buf("mod")
        S.op("dve", lambda: nc.vector.memset(self.epsD, EPS), [], [self.BC])
        S.dma("sp", self.ident, self.cd["ident"], writes=[self.BC])
        b1, b2, b3, b4, b5, b6 = [Buf() for _ in range(6)]
        S.dma("sp", self.onesf, self.cd["ones"], writes=[b1])
        S.dma("sp", self.blk64, self.cd["blk64"], writes=[b2])
        S.dma("sp", self.masks, self.cd["masks"], writes=[b3])
        S.dma("sp", self.pvec, self.pvec_d, writes=[b4])
        S.dma("pool", self.identb, self.cd["ident"], writes=[b5])
        S.dma("pool", self.onesb, self.cd["ones"], writes=[b6])
        S.barrier()

    def scr(self, name, shape, dt=F32):
        if name not in self.scratch:
            kind = "ExternalOutput" if name in self.dbg else "Internal"
            self.scratch[name] = self.nc.dram_tensor("s_" + name, list(shape), dt, kind=kind).ap()
        return self.scratch[name]

    def pv(self, name, c=None):
        off, nch = self.pv_off[name]
        if c is None:
            return self.pvec[:, off:off + nch]
        return self.pvec[:, off + c:off + c + 1]

    def load_w(self, dst, src, bufs_cols, q="pool"):
        S = self.S
        n = dst.shape[2]
        v = src.rearrange("(kc p) n -> p kc n", p=128)
        bufs = []
        for n0 in range(0, n, 512):
            n1 = min(n, n0 + 512)
            b = Buf()
            S.dma(q, dst[:, :, n0:n1], v[:, :, n0:n1], writes=[b])
            bufs.append(b)
        return bufs

    def prologue_transpose(self, xT):
        nc, S = self.nc, self.S
        st = Stage(self, "pt")
        tin = [st.sb(f"tin{i}", [128, D]) for i in range(2)]
        tout = [st.sb(f"tout{i}", [128, 8, 128]) for i in range(2)]
        Bin = [Buf(), Buf()]
        Bout = [Buf(), Buf()]
        xTv = xT.rearrange("(c p) t -> p c t", p=128)
        tiles = []
        for b in range(NB):
            for k in range(T // 128):
                tiles.append((b, k))

        def src(b, k):
            t0 = k * 128
            if t0 < TC:
                return self.ctx[b, t0:t0 + 128, :]
            return self.x[b, t0 - TC:t0 - TC + 128, :]

        S.dma("sp", tin[0], src(*tiles[0]), writes=[Bin[0]])
        for n, (b, k) in enumerate(tiles):
            i = n % 2
            if n + 1 < len(tiles):
                S.dma("sp", tin[1 - i], src(*tiles[n + 1]), writes=[Bin[1 - i]])
            for hf in range(2):
                pb = 2 * (n % 2) + hf
                for c4 in range(4):
                    c = hf * 4 + c4
                    S.op("pe", lambda: nc.tensor.transpose(out=self.PS[pb][:, c4 * 128:(c4 + 1) * 128], in_=tin[i][:, c * 128:(c + 1) * 128], identity=self.ident),
                         [Bin[i]], [self.BPS[pb]])
                eng = "dve" if hf == 0 else "act"
                if hf == 0:
                    S.op("dve", lambda: nc.vector.tensor_copy(out=tout[i][:, 0:4, :], in_=self.PS[pb][:].rearrange("p (c t) -> p c t", c=4)), [self.BPS[pb]], [Bout[i]])
                else:
                    S.op("act", lambda: nc.scalar.copy(out=tout[i][:, 4:8, :], in_=self.PS[pb][:].rearrange("p (c t) -> p c t", c=4)), [self.BPS[pb]], [Bout[i]])
            col = b * T + k * 128
            S.dma("pool", xTv[:, :, col:col + 128], tout[i], reads=[Bout[i]])
        st.close()

    def prologue_mod(self):
        nc, S = self.nc, self.S
        st = Stage(self, "pm")
        cv = st.sb("cv", [3, D])
        sc = st.sb("sc", [3, D])
        scT = st.sb("scT", [128, 8, 3])
        Bcv, Bsc, BscT = Buf(), Buf(), Buf()
        S.dma("sp", cv, self.cvec, writes=[Bcv])
        S.op("act", lambda: nc.scalar.activation(out=sc, in_=cv, func=AF.Silu), [Bcv], [Bsc])
        for kc in range(8):
            S.op("pe", lambda: nc.tensor.transpose(out=self.PS[0][:, kc * 4:kc * 4 + 3], in_=sc[0:3, kc * 128:(kc + 1) * 128], identity=self.ident[0:3, 0:3]),
                 [Bsc], [self.BPS[0]])
        S.op("dve", lambda: nc.vector.tensor_copy(out=scT, in_=self.PS[0][:, 0:32].rearrange("p (k f) -> p k f", f=4)[:, :, 0:3]), [self.BPS[0]], [BscT])
        NWB = 4
        wt = [st.sb(f"wt{i}", [128, 8, 512]) for i in range(NWB)]
        Bwt = [Buf() for _ in range(NWB)]
        groups = [(l, g) for l in range(DEPTH) for g in range(12)]

        def wsrc(l, g):
            return self.W["w_mod"][l].rearrange("(kc p) n -> p kc n", p=128)[:, :, g * 512:(g + 1) * 512]

        def wload(n):
            S.dma("sp" if n % 2 == 0 else "act", wt[n % NWB], wsrc(*groups[n]), writes=[Bwt[n % NWB]])

        for n in range(NWB - 1):
            wload(n)
        for n, (l, g) in enumerate(groups):
            i = n % NWB
            if n + NWB - 1 < len(groups):
                wload(n + NWB - 1)
            pb = 1 + (n % 2)
            for oc in range(4):
                for kc in range(8):
                    S.op("pe", lambda: nc.tensor.matmul(self.PS[pb][:, oc * 4:oc * 4 + 3], lhsT=wt[i][:, kc, oc * 128:(oc + 1) * 128], rhs=scT[:, kc, :], start=(kc == 0), stop=(kc == 7)),
                         [Bwt[i], BscT], [self.BPS[pb]])
            boff, _ = self.pv_off[f"b_mod{l}"]
            bias = self.pvec[:, boff + g * 4:boff + g * 4 + 4].unsqueeze(2).to_broadcast([128, 4, 3])
            S.op("dve", lambda: nc.vector.tensor_tensor(out=self.MOD[:, l, g * 4:(g + 1) * 4, :], in0=self.PS[pb][:, 0:16].rearrange("p (o f) -> p o f", f=4)[:, :, 0:3], in1=bias, op=ALU.add),
                 [self.BPS[pb]], [self.BMOD])
        for l in range(DEPTH):
            for w in range(2):
                sc_idx = 8 if w == 0 else 32
                nrm = self.pv(f"norm{w + 1}_{l}").unsqueeze(2).to_broadcast([128, 8, 3])
                S.op("dve", lambda: nc.vector.scalar_tensor_tensor(out=self.MA[:, l, w, :, :], in0=self.MOD[:, l, sc_idx:sc_idx + 8, :], scalar=1.0, in1=nrm, op0=ALU.add, op1=ALU.mult),
                     [self.BMOD], [self.BMOD])
        st.close()

    def norm_tiles(self, st, n=BLK + 2):
        return dict(sq=st.sb("nsq", [128, 8, n], BF16), tmp=st.sb("ntmp", [128, 8, n]), r0=st.sb("nr0", [128, n]), r1=st.sb("nr1", [128, n]),
                    B=[Buf() for _ in range(4)])

    def norm_block(self, nt, xs, Bxs, n, A, Bsh, hb, Bhb, bank):
        nc, S = self.nc, self.S
        sq, tmp, r0, r1 = nt["sq"], nt["tmp"], nt["r0"], nt["r1"]
        Bsq, Btmp, Br0, Br1 = nt["B"]
        S.op("act", lambda: nc.scalar.activation(out=sq[:, :, :n], in_=xs, func=AF.Square), [Bxs], [Bsq])
        ps = self.PS[bank]
        for c in range(8):
            S.op("pe", lambda: nc.tensor.matmul(ps[:, :n], lhsT=self.onesb, rhs=sq[:, c, :n], start=(c == 0), stop=(c == 7)), [Bsq], [self.BPS[bank]])
        S.op("act", lambda: nc.scalar.activation(out=r0[:, :n], in_=ps[:, :n], func=AF.Sqrt, scale=1.0 / D, bias=self.epsD), [self.BPS[bank]], [Br0])
        S.op("dve", lambda: nc.vector.reciprocal(out=r1[:, :n], in_=r0[:, :n]), [Br0], [Br1])
        S.op("dve", lambda: nc.vector.tensor_tensor(out=tmp[:, :, :n], in0=xs, in1=r1[:, :n].unsqueeze(1).to_broadcast([128, 8, n]), op=ALU.mult), [Bxs, Br1], [Btmp])
        for c in range(8):
            S.op("act", lambda: nc.scalar.activation(out=hb[:, c, :n], in_=tmp[:, c, :n], func=AF.Identity, scale=A[:, c:c + 1], bias=(Bsh[:, c:c + 1] if Bsh is not None else 0.0)),
                 [Btmp, self.BMOD], [Bhb])

    def mod_ab(self, l, w, j):
        A = self.MA[:, l, w, :, j]
        sh = self.MOD[:, l, (0 if w == 0 else 24):(8 if w == 0 else 32), j]
        gt = self.MOD[:, l, (16 if w == 0 else 40):(24 if w == 0 else 48), j]
        return A, sh, gt

    @staticmethod
    def blocks(skip_ctx=False):
        out = []
        for b in range(NB):
            for k in range(NBLK):
                if skip_ctx and k == 0:
                    continue
                out.append((b, k))
        return out

    @staticmethod
    def blk_range(k):
        seq0, seq1 = (0, TC) if k == 0 else (TC, T)
        t0 = k * BLK
        lo = max(t0 - 1, seq0)
        hi = min(t0 + BLK + 1, seq1)
        return t0, lo, hi, (t0 == seq0), (t0 + BLK == seq1)

    def ffn_stage(self, l, xin, xout, skip_ctx):
        nc, S = self.nc, self.S
        st = Stage(self, "ffn")
        Win = st.sb("win", [128, 8, 2 * DFF], BF16)
        Wout = st.sb("wout", [128, NFC, D], BF16)
        BWin = self.load_w(Win, self.W["ffn_w_in"][l], None)
        BWout = []
        osrc = self.W["ffn_w_out"][l].rearrange("(fc p) n -> p fc n", p=128)
        for f0 in range(0, NFC, 2):
            b = Buf()
            S.dma("pool", Wout[:, f0:f0 + 2, :], osrc[:, f0:f0 + 2, :], writes=[b])
            BWout.append(b)
        NH = BLK + 2
        xs = [st.sb(f"xs{i}", [128, 8, NH]) for i in range(2)]
        hb = [st.sb(f"hb{i}", [128, 8, NH], BF16) for i in range(2)]
        gt_ = [st.sb(f"g{i}", [128, NFC, BLK], BF16) for i in range(2)]
        cv = [st.sb(f"cv{i}", [128, BLK]) for i in range(2)]
        sl = [st.sb(f"sl{i}", [128, BLK]) for i in range(2)]
        Bxs, Bhb, Bg, Bcv, Bsl = [[Buf(), Buf()] for _ in range(5)]
        nt = self.norm_tiles(st)
        for i in range(2):
            S.op("dve", lambda: nc.vector.memset(xs[i], 0.0), [], [Bxs[i]])
        xiv = xin.rearrange("(c p) t -> p c t", p=128)
        xov = xout.rearrange("(c p) t -> p c t", p=128)
        blocks = self.blocks(skip_ctx)

        def load(n):
            b, k = blocks[n]
            t0, lo, hi, _, _ = self.blk_range(k)
            S.dma("sp", xs[n % 2][:, :, lo - (t0 - 1):hi - (t0 - 1)], xiv[:, :, b * T + lo:b * T + hi], writes=[Bxs[n % 2]])

        load(0)
        for n, (b, k) in enumerate(blocks):
            i = n % 2
            if n + 1 < len(blocks):
                load(n + 1)
            t0, lo, hi, first, last = self.blk_range(k)
            j = 2 if k == 0 else b
            A, sh, gate = self.mod_ab(l, 1, j)
            self.norm_block(nt, xs[i], Bxs[i], NH, A, sh, hb[i], Bhb[i], 6)
            for fc in range(NFC):
                q = fc % 2
                pa, pvv = self.PS[q], self.PS[2 + q]
                ga = BWin[(fc * 128) // 512]
                gv = BWin[(DFF + fc * 128) // 512]
                for kc in range(8):
                    S.op("pe", lambda: nc.tensor.matmul(pa[:, :NH], lhsT=Win[:, kc, fc * 128:(fc + 1) * 128], rhs=hb[i][:, kc, :], start=(kc == 0), stop=(kc == 7)),
                         [ga, Bhb[i]], [self.BPS[q]])
                for kc in range(8):
                    S.op("pe", lambda: nc.tensor.matmul(pvv[:, :BLK], lhsT=Win[:, kc, DFF + fc * 128:DFF + (fc + 1) * 128], rhs=hb[i][:, kc, 1:1 + BLK], start=(kc == 0), stop=(kc == 7)),
                         [gv, Bhb[i]], [self.BPS[2 + q]])
                w0, w1, w2, cb = self.pv(f"conv{l}_0", fc), self.pv(f"conv{l}_1", fc), self.pv(f"conv{l}_2", fc), self.pv(f"convb{l}", fc)
                S.op("act", lambda: nc.scalar.activation(out=cv[q], in_=pa[:, 1:1 + BLK], func=AF.Identity, scale=w1, bias=cb), [self.BPS[q]], [Bcv[q]])
                c0 = 1 if first else 0
                S.op("dve", lambda: nc.vector.scalar_tensor_tensor(out=cv[q][:, c0:BLK], in0=pa[:, c0:BLK], scalar=w0, in1=cv[q][:, c0:BLK], op0=ALU.mult, op1=ALU.add),
                     [self.BPS[q], Bcv[q]], [Bcv[q]])
                c1 = BLK - 1 if last else BLK
                S.op("dve", lambda: nc.vector.scalar_tensor_tensor(out=cv[q][:, 0:c1], in0=pa[:, 2:2 + c1], scalar=w2, in1=cv[q][:, 0:c1], op0=ALU.mult, op1=ALU.add),
                     [self.BPS[q], Bcv[q]], [Bcv[q]])
                S.op("act", lambda: nc.scalar.activation(out=sl[q], in_=cv[q], func=AF.Silu), [Bcv[q]], [Bsl[q]])
                S.op("dve", lambda: nc.vector.tensor_tensor(out=gt_[i][:, fc, :], in0=sl[q], in1=pvv[:, :BLK], op=ALU.mult), [Bsl[q], self.BPS[2 + q]], [Bg[i]])
            for oc in range(8):
                q = 4 + oc % 2
                po = self.PS[q]
                for fc in range(NFC):
                    S.op("pe", lambda: nc.tensor.matmul(po[:, :BLK], lhsT=Wout[:, fc, oc * 128:(oc + 1) * 128], rhs=gt_[i][:, fc, :], start=(fc == 0), stop=(fc == NFC - 1)),
                         [BWout[fc // 2], Bg[i]], [self.BPS[q]])
                S.op("dve", lambda: nc.vector.scalar_tensor_tensor(out=xs[i][:, oc, 1:1 + BLK], in0=po[:, :BLK], scalar=gate[:, oc:oc + 1], in1=xs[i][:, oc, 1:1 + BLK], op0=ALU.mult, op1=ALU.add),
                     [self.BPS[q], Bxs[i], self.BMOD], [Bxs[i]])
            S.dma("pool", xov[:, :, b * T + t0:b * T + t0 + BLK], xs[i][:, :, 1:1 + BLK], reads=[Bxs[i]])
        st.close()

    def final_stage(self, xin):
        nc, S = self.nc, self.S
        st = Stage(self, "fin")
        xs = [st.sb(f"xs{i}", [128, 8, BLK]) for i in range(2)]
        hb = [st.sb(f"hb{i}", [128, 8, BLK]) for i in range(2)]
        ot = [st.sb(f"ot{i}", [128, D]) for i in range(2)]
        Bxs, Bhb, Bot = [[Buf(), Buf()] for _ in range(3)]
        nt = self.norm_tiles(st, BLK)
        xiv = xin.rearrange("(c p) t -> p c t", p=128)
        blocks = self.blocks(True)
        A = self.pv("norm_f")

        def load(n):
            b, k = blocks[n]
            S.dma("sp", xs[n % 2], xiv[:, :, b * T + k * BLK:b * T + (k + 1) * BLK], writes=[Bxs[n % 2]])

        load(0)
        nt_i = 0
        for n, (b, k) in enumerate(blocks):
            i = n % 2
            if n + 1 < len(blocks):
                load(n + 1)
            self.norm_block(nt, xs[i], Bxs[i], BLK, A, None, hb[i], Bhb[i], 6)
            for tt in range(2):
                o = nt_i % 2
                nt_i += 1
                for hf in range(2):
                    pb = 2 * o + hf
                    for c4 in range(4):
                        c = hf * 4 + c4
                        S.op("pe", lambda: nc.tensor.transpose(out=self.PS[pb][:, c4 * 128:(c4 + 1) * 128], in_=hb[i][:, c, tt * 128:(tt + 1) * 128], identity=self.ident),
                             [Bhb[i]], [self.BPS[pb]])
                    if hf == 0:
                        S.op("dve", lambda: nc.vector.tensor_copy(out=ot[o][:, 0:512], in_=self.PS[pb]), [self.BPS[pb]], [Bot[o]])
                    else:
                        S.op("act", lambda: nc.scalar.copy(out=ot[o][:, 512:1024], in_=self.PS[pb]), [self.BPS[pb]], [Bot[o]])
                tl = k * BLK - TC + tt * 128
                S.dma("pool", self.out[b, tl:tl + 128, :], ot[o], reads=[Bot[o]])
        st.close()


def build_program(wshapes, pv_off, npv, plan=None, dbg=()):
    P = Prog(wshapes, pv_off, npv, dbg=dbg)
    xa = P.scr("xA", [D, TT])
    xb = P.scr("xB", [D, TT])
    if plan is None:
        plan = ["tr", "mod"]
        for l in range(DEPTH):
            plan += [f"mix{l}", f"ffn{l}"]
        plan += ["final"]
    cur, nxt = xa, xb
    for step in plan:
        if step == "tr":
            P.prologue_transpose(cur)
        elif step == "mod":
            P.prologue_mod()
        elif step.startswith("mix"):
            l = int(step[3:])
            P.mixer(l, cur, nxt)
            cur, nxt = nxt, cur
        elif step.startswith("ffn"):
            l = int(step[3:])
            P.ffn_stage(l, cur, nxt, skip_ctx=(l == DEPTH - 1))
            cur, nxt = nxt, cur
        elif step == "final":
            P.final_stage(cur)
    P.S.barrier()
    return P


def prep_inputs(inputs, cores=range(NCORES)):
    pv = pvec_layout(inputs)
    pva = pv.array()
    consts = make_consts()
    shared = {"pvec": pva}
    for k, v in consts.items():
        shared["c_" + k] = v
    for n in WEIGHT_NAMES:
        shared[n] = np.ascontiguousarray(inputs[n], dtype=np.float32)
    in_maps = []
    for c in cores:
        m = dict(shared)
        m["x"] = np.ascontiguousarray(inputs["x"][NB * c:NB * (c + 1)], dtype=np.float32)
        m["ctx"] = np.ascontiguousarray(inputs["ctx"][NB * c:NB * (c + 1)], dtype=np.float32)
        m["cvec"] = np.ascontiguousarray(np.concatenate([inputs["c"][NB * c:NB * (c + 1)], inputs["c_ctx"][None, :]], axis=0), dtype=np.float32)
        in_maps.append(m)
    wshapes = {n: list(inputs[n].shape) for n in WEIGHT_NAMES}
    return in_maps, wshapes, pv.off, pva.shape[1]


def kernel(**inputs):
    inputs = {k: np.asarray(v) for k, v in inputs.items()}
    in_maps, wshapes, pv_off, npv = prep_inputs(inputs)
    P = build_program(wshapes, pv_off, npv)
    res = run_bass_kernel_spmd(P.nc, in_maps, core_ids=list(range(NCORES)))
    out = np.concatenate([np.asarray(r["out"]) for r in res.results], axis=0)
    return out.astype(np.float32)


def _inproj_stage(self, l, xin, Wd, N, dst_fm, tm_specs, f32_h=False):
    nc, S = self.nc, self.S
    st = Stage(self, "ip")
    Wt = st.sb("w", [128, 8, N], BF16)
    BW = self.load_w(Wt, Wd, None)
    xs = [st.sb(f"xs{i}", [128, 8, BLK]) for i in range(2)]
    hb = [st.sb(f"hb{i}", [128, 8, BLK], BF16) for i in range(2)]
    sg = [st.sb(f"sg{i}", [128, 8, BLK]) for i in range(2)]
    tmw = max([nc_ for (_, nc_, _) in tm_specs], default=0)
    tms = [st.sb(f"tm{i}", [128, max(tmw, 1)], BF16) for i in range(2)]
    Bxs, Bhb, Bsg, Btm = [[Buf(), Buf()] for _ in range(4)]
    nt = self.norm_tiles(st, BLK)
    xiv = xin.rearrange("(c p) t -> p c t", p=128)
    dv = dst_fm.rearrange("(c p) t -> p c t", p=128)
    blocks = self.blocks(False)

    def load(n):
        b, k = blocks[n]
        S.dma("sp", xs[n % 2], xiv[:, :, b * T + k * BLK:b * T + (k + 1) * BLK], writes=[Bxs[n % 2]])

    load(0)
    sgi = 0
    tmi = 0
    pbank = 0
    for n, (b, k) in enumerate(blocks):
        i = n % 2
        if n + 1 < len(blocks):
            load(n + 1)
        j = 2 if k == 0 else b
        A, sh, _ = self.mod_ab(l, 0, j)
        self.norm_block(nt, xs[i], Bxs[i], BLK, A, sh, hb[i], Bhb[i], 6)
        col = b * T + k * BLK
        for og in range(N // 1024):
            s_ = sgi % 2
            sgi += 1
            for o8 in range(8):
                oc = og * 8 + o8
                pb = pbank % 4
                pbank += 1
                for kc in range(8):
                    S.op("pe", lambda: nc.tensor.matmul(self.PS[pb][:, :BLK], lhsT=Wt[:, kc, oc * 128:(oc + 1) * 128], rhs=hb[i][:, kc, :], start=(kc == 0), stop=(kc == 7)),
                         [BW[(oc * 128) // 512], Bhb[i]], [self.BPS[pb]])
                if o8 % 2 == 0:
                    S.op("act", lambda: nc.scalar.copy(out=sg[s_][:, o8, :], in_=self.PS[pb][:, :BLK]), [self.BPS[pb]], [Bsg[s_]])
                else:
                    S.op("dve", lambda: nc.vector.tensor_copy(out=sg[s_][:, o8, :], in_=self.PS[pb][:, :BLK]), [self.BPS[pb]], [Bsg[s_]])
            S.dma("pool", dv[:, og * 8:(og + 1) * 8, col:col + BLK], sg[s_], reads=[Bsg[s_]])
        for (c0, ncols, dst_tm) in tm_specs:
            for tt in range(BLK // 128):
                s_ = tmi % 2
                tmi += 1
                for n0 in range(0, ncols, 512):
                    pb = 4 + (pbank % 2)
                    pbank += 1
                    for kc in range(8):
                        S.op("pe", lambda: nc.tensor.matmul(self.PS[pb][:, :512], lhsT=hb[i][:, kc, tt * 128:(tt + 1) * 128], rhs=Wt[:, kc, c0 + n0:c0 + n0 + 512], start=(kc == 0), stop=(kc == 7)),
                             [BW[(c0 + n0) // 512], Bhb[i]], [self.BPS[pb]])
                    S.op("act", lambda: nc.scalar.copy(out=tms[s_][:, n0:n0 + 512], in_=self.PS[pb][:, :512]), [self.BPS[pb]], [Btm[s_]])
                S.dma("pool", dst_tm[col + tt * 128:col + (tt + 1) * 128, :], tms[s_][:, :ncols], reads=[Btm[s_]])
    st.close()


def _outproj_stage(self, l, og, Wd, xin, xout, skip_ctx):
    nc, S = self.nc, self.S
    st = Stage(self, "op")
    Wt = st.sb("w", [128, 8, D], BF16)
    BW = self.load_w(Wt, Wd, None)
    xs = [st.sb(f"xs{i}", [128, 8, BLK]) for i in range(2)]
    ob = [st.sb(f"ob{i}", [128, 8, BLK], BF16) for i in range(2)]
    Bxs, Bob = [[Buf(), Buf()] for _ in range(2)]
    xiv = xin.rearrange("(c p) t -> p c t", p=128)
    xov = xout.rearrange("(c p) t -> p c t", p=128)
    ogv = og.rearrange("(c p) t -> p c t", p=128)
    blocks = self.blocks(skip_ctx)

    def load(n):
        b, k = blocks[n]
        col = b * T + k * BLK
        S.dma("sp", xs[n % 2], xiv[:, :, col:col + BLK], writes=[Bxs[n % 2]])
        S.dma("sp", ob[n % 2], ogv[:, :, col:col + BLK], writes=[Bob[n % 2]])

    load(0)
    for n, (b, k) in enumerate(blocks):
        i = n % 2
        if n + 1 < len(blocks):
            load(n + 1)
        j = 2 if k == 0 else b
        _, _, gate = self.mod_ab(l, 0, j)
        for oc in range(8):
            pb = oc % 4
            for kc in range(8):
                S.op("pe", lambda: nc.tensor.matmul(self.PS[pb][:, :BLK], lhsT=Wt[:, kc, oc * 128:(oc + 1) * 128], rhs=ob[i][:, kc, :], start=(kc == 0), stop=(kc == 7)),
                     [BW[(oc * 128) // 512], Bob[i]], [self.BPS[pb]])
            S.op("dve", lambda: nc.vector.scalar_tensor_tensor(out=xs[i][:, oc, :], in0=self.PS[pb][:, :BLK], scalar=gate[:, oc:oc + 1], in1=xs[i][:, oc, :], op0=ALU.mult, op1=ALU.add),
                 [self.BPS[pb], Bxs[i], self.BMOD], [Bxs[i]])
        col = b * T + k * BLK
        S.dma("pool", xov[:, :, col:col + BLK], xs[i], reads=[Bxs[i]])
    st.close()


def _hgrn2_scan(self, jh, Pfm, Itm, og):
    nc, S = self.nc, self.S
    st = Stage(self, "hs")
    A_ = nc.vector
    LB = st.sb("LB", [128, 2, 8])
    OML = st.sb("OML", [128, 2, 8])
    e0 = st.sb("e0", [128, 8]); e1 = st.sb("e1", [128, 8]); rr = st.sb("rr", [128, 8]); p0 = st.sb("p0", [128, 8]); p1 = st.sb("p1", [128, 8])
    BL = Buf()
    for d in range(2):
        S.op("act", lambda: nc.scalar.activation(out=e0, in_=self.pv(f"hg_lb{d}_0"), func=AF.Exp), [], [BL])
        S.op("act", lambda: nc.scalar.activation(out=e1, in_=self.pv(f"hg_lb{d}_1"), func=AF.Exp), [BL], [BL])
        S.op("dve", lambda: A_.tensor_tensor(out=rr, in0=e0, in1=e1, op=ALU.add), [BL], [BL])
        S.op("dve", lambda: A_.reciprocal(out=rr, in_=rr), [BL], [BL])
        S.op("dve", lambda: A_.tensor_tensor(out=p0, in0=e0, in1=rr, op=ALU.mult), [BL], [BL])
        S.op("dve", lambda: A_.tensor_tensor(out=p1, in0=e1, in1=rr, op=ALU.mult), [BL], [BL])
        if jh == 1:
            S.op("dve", lambda: A_.tensor_tensor(out=p1, in0=p0, in1=p1, op=ALU.add), [BL], [BL])
        else:
            S.op("dve", lambda: A_.tensor_copy(out=p1, in_=p0), [BL], [BL])
        S.op("dve", lambda: A_.tensor_tensor(out=LB[:, d, :], in0=p1, in1=p0, op=ALU.subtract), [BL], [BL])
        S.op("dve", lambda: A_.tensor_scalar(out=OML[:, d, :], in0=LB[:, d, :], scalar1=-1.0, scalar2=1.0, op0=ALU.mult, op1=ALU.add), [BL], [BL])
    smask = st.sb("smask", [128, T])
    Bsm = Buf()
    S.dma("sp", smask, self.cd["scanmask"], writes=[Bsm])
    f32t = lambda n: st.sb(n, [128, T])
    qs = f32t("qs"); graw = f32t("graw"); kk = f32t("kk"); ep = f32t("ep"); en = f32t("en")
    z = [f32t("z0"), f32t("z1")]; bb = [f32t("b0"), f32t("b1")]; of = [f32t("of0"), f32t("of1")]
    qt = [st.sb(f"qt{d}", [128, T], BF16) for d in range(2)]
    kh = [st.sb(f"kh{d}", [128, T], BF16) for d in range(2)]
    sqb = st.sb("sqb", [128, T], BF16)
    ogb = st.sb("ogb", [128, T], BF16)
    Vt = st.sb("Vt", [64, NCH, 128], BF16)
    emid = [st.sb(f"emid{d}", [128, NCH]) for d in range(2)]
    eend = [st.sb(f"eend{d}", [128, NCH]) for d in range(2)]
    eem = [st.sb(f"eem{d}", [128, NCH]) for d in range(2)]
    Sst = [st.sb(f"S{d}", [128, 128]) for d in range(2)]
    Sm = [st.sb(f"Sm{d}", [128, 128], BF16) for d in range(2)]
    tmpS = [st.sb(f"tS{d}", [128, 128]) for d in range(2)]
    khT = [st.sb(f"khT{d}", [64, 128], BF16) for d in range(2)]
    att = [st.sb(f"att{d}", [64, 64], BF16) for d in range(2)]
    Bqs, Bgr, Bkk, Bep, Ben, Bsq, Bog, BVt = [Buf() for _ in range(8)]
    Bz, Bbb, Bof, Bqt, Bkh, Bes, BS, BSm, BtS, BkT, Batt = [[Buf(), Buf()] for _ in range(11)]
    PSb = [self.PS[i].bitcast(BF16) for i in range(8)]
    for d in range(2):
        S.op("dve", lambda: A_.memset(att[d], 0.0), [], [Batt[d]])
    cf = list(range(NCH))
    cb = list(range(TC // CH - 1, -1, -1)) + list(range(NCH - 1, TC // CH - 1, -1))
    order = [cf, cb]
    for b in range(NB):
        for h in range(8):
            rows = slice(h * 128, (h + 1) * 128)
            cols = slice(b * T, (b + 1) * T)
            S.dma("sp", qs, Pfm[0 * D + h * 128:0 * D + (h + 1) * 128, cols], writes=[Bqs])
            S.dma("sp", z[0], Pfm[3 * D + h * 128:3 * D + (h + 1) * 128, cols], writes=[Bz[0]])
            S.dma("sp", z[1], Pfm[4 * D + h * 128:4 * D + (h + 1) * 128, cols], writes=[Bz[1]])
            S.dma("sp", graw, Pfm[2 * D + h * 128:2 * D + (h + 1) * 128, cols], writes=[Bgr])
            S.dma("sp", Vt, Itm[cols, rows].rearrange("(c s) v -> s c v", s=CH), writes=[BVt])
            S.op("act", lambda: nc.scalar.activation(out=qs, in_=qs, func=AF.Silu), [Bqs], [Bqs])
            for d in range(2):
                m_idx = 32 if d == 0 else 31
                zt = z[d]
                S.op("act", lambda: nc.scalar.activation(out=zt, in_=zt, func=AF.Sigmoid), [Bz[d]], [Bz[d]])
                S.op("dve", lambda: A_.tensor_scalar(out=zt, in0=zt, scalar1=OML[:, d, h:h + 1], scalar2=LB[:, d, h:h + 1], op0=ALU.mult, op1=ALU.add), [Bz[d], BL], [Bz[d]])
                S.op("dve", lambda: A_.tensor_scalar(out=kk, in0=zt, scalar1=-1.0, scalar2=1.0, op0=ALU.mult, op1=ALU.add), [Bz[d]], [Bkk])
                S.op("act", lambda: nc.scalar.activation(out=zt, in_=zt, func=AF.Ln), [Bz[d]], [Bz[d]])
                S.op("dve", lambda: A_.tensor_tensor_scan(out=bb[d], data0=smask, data1=zt, initial=0.0, op0=ALU.mult, op1=ALU.add), [Bsm, Bz[d]], [Bbb[d]])
                b3 = bb[d].rearrange("p (c s) -> p c s", s=CH)
                if d == 1:
                    S.op("dve", lambda: A_.tensor_tensor(out=zt, in0=zt, in1=bb[d], op=ALU.subtract), [Bz[d], Bbb[d]], [Bz[d]])
                    S.op("dve", lambda: A_.tensor_tensor(out=ep.rearrange("p (c s) -> p c s", s=CH), in0=zt.rearrange("p (c s) -> p c s", s=CH),
                                                          in1=b3[:, :, CH - 1:CH].to_broadcast([128, NCH, CH]), op=ALU.add), [Bz[d], Bbb[d]], [Bep])
                    S.op("dve", lambda: A_.tensor_copy(out=bb[d], in_=ep), [Bep], [Bbb[d]])
                e_idx = CH - 1 if d == 0 else 0
                S.op("act", lambda: nc.scalar.activation(out=emid[d], in_=b3[:, :, m_idx], func=AF.Exp), [Bbb[d]], [Bes[d]])
                S.op("act", lambda: nc.scalar.activation(out=eend[d], in_=b3[:, :, e_idx], func=AF.Exp), [Bbb[d]], [Bes[d]])
                S.op("dve", lambda: A_.tensor_tensor(out=eem[d], in0=b3[:, :, e_idx], in1=b3[:, :, m_idx], op=ALU.subtract), [Bbb[d]], [Bes[d]])
                S.op("act", lambda: nc.scalar.activation(out=eem[d], in_=eem[d], func=AF.Exp), [Bes[d]], [Bes[d]])
                S.op("dve", lambda: A_.tensor_tensor(out=ep.rearrange("p (c s) -> p c s", s=CH), in0=b3, in1=b3[:, :, m_idx:m_idx + 1].to_broadcast([128, NCH, CH]), op=ALU.subtract),
                     [Bbb[d]], [Bep])
                S.op("act", lambda: nc.scalar.activation(out=en, in_=ep, func=AF.Exp, scale=-1.0), [Bep], [Ben])
                S.op("act", lambda: nc.scalar.activation(out=ep, in_=ep, func=AF.Exp), [Bep], [Bep])
                S.op("dve", lambda: A_.tensor_tensor(out=qt[d], in0=qs, in1=ep, op=ALU.mult), [Bqs, Bep], [Bqt[d]])
                S.op("dve", lambda: A_.tensor_tensor(out=kh[d], in0=kk, in1=en, op=ALU.mult), [Bkk, Ben], [Bkh[d]])
                S.op("dve", lambda: A_.memset(Sst[d], 0.0), [], [BS[d]])
                S.op("dve", lambda: A_.memset(Sm[d], 0.0), [], [BSm[d]])
            def hstep(d, step):
                c = order[d][step]
                cs = slice(c * CH, (c + 1) * CH)
                pb = d * 4
                mk = (self.masks[0:64, 64:128] if d == 0 else self.masks[0:64, 192:256]).bitcast(mybir.dt.uint32)
                S.op("pe", lambda: nc.tensor.transpose(out=PSb[pb][0:64, 0:128], in_=kh[d][:, cs], identity=self.identb), [Bkh[d]], [self.BPS[pb]])
                S.op("pe", lambda: nc.tensor.matmul(self.PS[pb + 1][0:64, 0:64], lhsT=kh[d][:, cs], rhs=qt[d][:, cs], start=True, stop=True), [Bkh[d], Bqt[d]], [self.BPS[pb + 1]])
                yield
                S.op("act", lambda: nc.scalar.copy(out=khT[d], in_=PSb[pb][0:64, 0:128]), [self.BPS[pb]], [BkT[d]])
                S.op("dve", lambda: A_.copy_predicated(out=att[d], mask=mk, data=self.PS[pb + 1][0:64, 0:64]), [self.BPS[pb + 1]], [Batt[d]])
                S.op("pe", lambda: nc.tensor.matmul(self.PS[pb + 2][:, 0:64], lhsT=Vt[:, c, :], rhs=att[d], start=True, stop=False), [BVt, Batt[d]], [self.BPS[pb + 2]])
                S.op("pe", lambda: nc.tensor.matmul(self.PS[pb + 2][:, 0:64], lhsT=Sm[d], rhs=qt[d][:, cs], start=False, stop=True), [BSm[d], Bqt[d]], [self.BPS[pb + 2]])
                S.op("pe", lambda: nc.tensor.matmul(self.PS[pb + 3][:, 0:128], lhsT=khT[d], rhs=Vt[:, c, :], start=True, stop=True), [BkT[d], BVt], [self.BPS[pb + 3]])
                yield
                S.op("act", lambda: nc.scalar.activation(out=tmpS[d], in_=self.PS[pb + 3][:, 0:128], func=AF.Identity, scale=eem[d][:, c:c + 1]), [self.BPS[pb + 3], Bes[d]], [BtS[d]])
                S.op("dve", lambda: A_.scalar_tensor_tensor(out=Sst[d], in0=Sst[d], scalar=eend[d][:, c:c + 1], in1=tmpS[d], op0=ALU.mult, op1=ALU.add), [BS[d], BtS[d], Bes[d]], [BS[d]])
                S.op("act", lambda: nc.scalar.copy(out=of[d][:, cs], in_=self.PS[pb + 2][:, 0:64]), [self.BPS[pb + 2]], [Bof[d]])
                if step + 1 < NCH:
                    cn = order[d][step + 1]
                    S.op("dve", lambda: A_.tensor_scalar(out=Sm[d], in0=Sst[d], scalar1=emid[d][:, cn:cn + 1], scalar2=None, op0=ALU.mult), [BS[d], Bes[d]], [BSm[d]])

            for step in range(NCH):
                gens = [hstep(d, step) for d in range(2)]
                while gens:
                    for g_ in list(gens):
                        try:
                            next(g_)
                        except StopIteration:
                            gens.remove(g_)
            S.op("dve", lambda: A_.tensor_tensor(out=of[0], in0=of[0], in1=of[1], op=ALU.add), [Bof[0], Bof[1]], [Bof[0]])
            S.op("act", lambda: nc.scalar.activation(out=sqb, in_=of[0], func=AF.Square), [Bof[0]], [Bsq])
            for pc in range(6):
                sl_ = slice(pc * 384, (pc + 1) * 384)
                pb = pc % 2
                S.op("pe", lambda: nc.tensor.matmul(self.PS[pb][:, 0:384], lhsT=self.onesb, rhs=sqb[:, sl_], start=True, stop=True), [Bsq], [self.BPS[pb]])
                S.op("act", lambda: nc.scalar.activation(out=ep[:, sl_], in_=self.PS[pb][:, 0:384], func=AF.Sqrt, scale=1.0 / 128, bias=self.epsD), [self.BPS[pb]], [Bep])
            S.op("dve", lambda: A_.reciprocal(out=ep, in_=ep), [Bep], [Bep])
            S.op("dve", lambda: A_.tensor_tensor(out=of[0], in0=of[0], in1=ep, op=ALU.mult), [Bof[0], Bep], [Bof[0]])
            S.op("act", lambda: nc.scalar.activation(out=graw, in_=graw, func=AF.Silu), [Bgr], [Bgr])
            S.op("dve", lambda: A_.scalar_tensor_tensor(out=ogb, in0=of[0], scalar=self.pv(f"hg_norm{jh}", 0), in1=graw, op0=ALU.mult, op1=ALU.mult), [Bof[0], Bgr], [Bog])
            S.dma("pool", og[rows, cols], ogb, reads=[Bog])
    st.close()


def _mixer(self, l, cur, nxt):
    kind, j = l % 3, l // 3
    last = (l == DEPTH - 1)
    og = self.scr("og", [D, TT], BF16)
    if kind == 0:
        Pfm = self.scr("hgP", [5 * D, TT])
        Itm = self.scr("hgI", [TT, D], BF16)
        self.inproj_stage(l, cur, self.W["hg_w_in"][j], 5 * D, Pfm, [(D, D, Itm)])
        self.hgrn2_scan(j, Pfm, Itm, og)
        self.outproj_stage(l, og, self.W["hg_w_o"][j], cur, nxt, last)
    elif kind == 1:
        self.rwkv_mixer(l, cur, og)
        self.outproj_stage(l, og, self.W["rw_w_o"][j], cur, nxt, last)
    else:
        self.mla_mixer(l, cur, og)
        self.outproj_stage(l, og, self.W["mla_w_o"][j], cur, nxt, last)


Prog.inproj_stage = _inproj_stage
Prog.outproj_stage = _outproj_stage
Prog.hgrn2_scan = _hgrn2_scan
Prog.mixer = _mixer


def _mla_mixer(self, l, xin, og):
    nc, S = self.nc, self.S
    A_ = nc.vector
    NH = 16
    QN = self.scr("mlaQN", [96, NH, TT], BF16)
    KN = self.scr("mlaKN", [96, NH, TT], BF16)
    VT = self.scr("mlaVT", [TT, D], BF16)
    st = Stage(self, "m1")
    Wd = st.sb("wd", [128, 8, 544], BF16)
    Wq = st.sb("wq", [128, 2, 1536], BF16)
    Wk = st.sb("wk", [128, 2, 2048], BF16)
    Wdr = st.sb("wdr", [128, 8, 32], BF16)
    Wqr = st.sb("wqr", [128, 2, NH, 32], BF16)
    BWd, BWq, BWk, BWr = Buf(), Buf(), Buf(), Buf()
    S.dma("pool", Wd, self.W["mla_w_dqkv"][0].rearrange("(kc p) n -> p kc n", p=128), writes=[BWd])
    wqv = self.W["mla_w_uq"][0].rearrange("(kc p) n -> p kc n", p=128)
    for i3 in range(3):
        S.dma("pool", Wq[:, :, i3 * 512:(i3 + 1) * 512], wqv[:, :, i3 * 512:(i3 + 1) * 512], writes=[BWq])
    wkv = self.W["mla_w_ukv"][0].rearrange("(kc p) n -> p kc n", p=128)
    for i4 in range(4):
        S.dma("pool", Wk[:, :, i4 * 512:(i4 + 1) * 512], wkv[:, :, i4 * 512:(i4 + 1) * 512], writes=[BWk])
    Wq4 = Wq.rearrange("p k (h c) -> p k h c", c=96)
    for seg in range(2):
        for half in range(2):
            sgn = -1.0 if half == 0 else 1.0
            so = 64 + seg * 16 + (1 - half) * 8
            do = seg * 16 + half * 8
            S.op("act", lambda: nc.scalar.activation(out=Wqr[:, :, :, do:do + 8], in_=Wq4[:, :, :, so:so + 8], func=AF.Copy, scale=sgn), [BWq], [BWr])
            so2 = 512 + seg * 16 + (1 - half) * 8
            S.op("act", lambda: nc.scalar.activation(out=Wdr[:, :, do:do + 8], in_=Wd[:, :, so2:so2 + 8], func=AF.Copy, scale=sgn), [BWd], [BWr])
    cos = st.sb("cos", [96, T]); sin = st.sb("sin", [96, T])
    Bcs = Buf()
    RP = slice(64, 96)
    S.dma("sp", cos[RP, :], self.cd["rope_cos"], writes=[Bcs])
    S.dma("sp", sin[RP, :], self.cd["rope_sin"], writes=[Bcs])
    xs = [st.sb(f"xs{i}", [128, 8, BLK]) for i in range(2)]
    hb = [st.sb(f"hb{i}", [128, 8, BLK], BF16) for i in range(2)]
    Bxs, Bhb = [[Buf(), Buf()] for _ in range(2)]
    nt = self.norm_tiles(st, BLK)
    cs_ = st.sb("cs", [128, 4, BLK]); csq = st.sb("csq", [128, 4, BLK], BF16); cn = st.sb("cn", [128, 4, BLK], BF16)
    rr0 = st.sb("rr0", [128, 2, BLK]); rr1 = st.sb("rr1", [128, 2, BLK]); ctmp = st.sb("ctmp", [128, 4, BLK])
    Bcs_, Bcsq, Bcn, Brr, Bct = [Buf() for _ in range(5)]
    qn_s = [st.sb(f"qns{i}", [96, NH, BLK], BF16) for i in range(2)]
    kn_s = [st.sb(f"kns{i}", [96, NH, BLK], BF16) for i in range(2)]
    vt_s = [st.sb(f"vts{i}", [128, D], BF16) for i in range(2)]
    t1 = st.sb("t1", [96, 2, BLK]); t2 = st.sb("t2", [96, 2, BLK])
    Bt1, Bt2 = Buf(), Buf()
    Bqn, Bkn, Bqr, Bkr, Bvt = [[Buf(), Buf()] for _ in range(5)]
    xiv = xin.rearrange("(c p) t -> p c t", p=128)
    blocks = self.blocks(False)

    def load(n):
        b, k = blocks[n]
        S.dma("sp", xs[n % 2], xiv[:, :, b * T + k * BLK:b * T + (k + 1) * BLK], writes=[Bxs[n % 2]])

    load(0)
    vti = 0
    for n, (b, k) in enumerate(blocks):
        i = n % 2
        if n + 1 < len(blocks):
            load(n + 1)
        j = 2 if k == 0 else b
        A, sh, _ = self.mod_ab(l, 0, j)
        self.norm_block(nt, xs[i], Bxs[i], BLK, A, sh, hb[i], Bhb[i], 6)
        col = b * T + k * BLK
        tcol = slice(k * BLK, (k + 1) * BLK)
        for c4 in range(4):
            pb = c4 // 2
            for kc in range(8):
                S.op("pe", lambda: nc.tensor.matmul(self.PS[pb][:, (c4 % 2) * BLK:(c4 % 2 + 1) * BLK], lhsT=Wd[:, kc, c4 * 128:(c4 + 1) * 128], rhs=hb[i][:, kc, :], start=(kc == 0), stop=(kc == 7)),
                     [BWd, Bhb[i]], [self.BPS[pb]])
        for kc in range(8):
            S.op("pe", lambda: nc.tensor.matmul(self.PS[2][RP, 0:BLK], lhsT=Wd[:, kc, 512:544], rhs=hb[i][:, kc, :], start=(kc == 0), stop=(kc == 7)), [BWd, Bhb[i]], [self.BPS[2]])
        for kc in range(8):
            S.op("pe", lambda: nc.tensor.matmul(self.PS[2][RP, BLK:2 * BLK], lhsT=Wdr[:, kc, :], rhs=hb[i][:, kc, :], start=(kc == 0), stop=(kc == 7)), [BWr, Bhb[i]], [self.BPS[2]])
        for pb in range(2):
            S.op("act", lambda: nc.scalar.copy(out=cs_[:, 2 * pb:2 * pb + 2, :], in_=self.PS[pb].rearrange("p (c t) -> p c t", c=2)), [self.BPS[pb]], [Bcs_])
            S.op("act", lambda: nc.scalar.activation(out=csq[:, 2 * pb:2 * pb + 2, :], in_=self.PS[pb].rearrange("p (c t) -> p c t", c=2), func=AF.Square), [self.BPS[pb]], [Bcsq])
        S.op("dve", lambda: A_.tensor_tensor(out=t1[RP, 0, :], in0=self.PS[2][RP, 0:BLK], in1=cos[RP, tcol], op=ALU.mult), [self.BPS[2], Bcs], [Bt1])
        S.op("dve", lambda: A_.tensor_tensor(out=t2[RP, 0, :], in0=self.PS[2][RP, BLK:2 * BLK], in1=sin[RP, tcol], op=ALU.mult), [self.BPS[2], Bcs], [Bt2])
        S.op("dve", lambda: A_.tensor_tensor(out=kn_s[i][RP, :, :], in0=t1[RP, 0:1, :].to_broadcast([32, NH, BLK]), in1=t2[RP, 0:1, :].to_broadcast([32, NH, BLK]), op=ALU.add), [Bt1, Bt2], [Bkn[i]])
        for w in range(2):
            for c in range(2):
                S.op("pe", lambda: nc.tensor.matmul(self.PS[3][:, w * BLK:(w + 1) * BLK], lhsT=self.onesb, rhs=csq[:, 2 * w + c, :], start=(c == 0), stop=(c == 1)), [Bcsq], [self.BPS[3]])
        S.op("act", lambda: nc.scalar.activation(out=rr0, in_=self.PS[3].rearrange("p (w t) -> p w t", w=2), func=AF.Sqrt, scale=1.0 / 256, bias=self.epsD), [self.BPS[3]], [Brr])
        S.op("dve", lambda: A_.reciprocal(out=rr1, in_=rr0), [Brr], [Brr])
        S.op("dve", lambda: A_.tensor_tensor(out=ctmp.rearrange("p (w c) t -> p w c t", w=2), in0=cs_.rearrange("p (w c) t -> p w c t", w=2),
                                              in1=rr1.unsqueeze(2).to_broadcast([128, 2, 2, BLK]), op=ALU.mult), [Bcs_, Brr], [Bct])
        for c4 in range(4):
            gname = "mla_q_norm" if c4 < 2 else "mla_kv_norm"
            S.op("act", lambda: nc.scalar.activation(out=cn[:, c4, :], in_=ctmp[:, c4, :], func=AF.Identity, scale=self.pv(gname, c4 % 2)), [Bct], [Bcn])
        for hp in range(8):
            for which in range(2):
                pb = 4 + (2 * hp + which) % 2
                Wt_, coff, hw, ci = (Wq, 0, 96, 0) if which == 0 else (Wk, 0, 128, 2)
                for hh in range(2):
                    h = 2 * hp + hh
                    for kc in range(2):
                        S.op("pe", lambda: nc.tensor.matmul(self.PS[pb][0:64, hh * BLK:(hh + 1) * BLK], lhsT=Wt_[:, kc, h * hw:h * hw + 64], rhs=cn[:, ci + kc, :], start=(kc == 0), stop=(kc == 1)),
                             [BWq if which == 0 else BWk, Bcn], [self.BPS[pb]])
                dst = qn_s[i] if which == 0 else kn_s[i]
                Bd = Bqn[i] if which == 0 else Bkn[i]
                if which == 0:
                    S.op("act", lambda: nc.scalar.copy(out=dst[0:64, 2 * hp:2 * hp + 2, :], in_=self.PS[pb][0:64, :].rearrange("p (h t) -> p h t", h=2)), [self.BPS[pb]], [Bd])
                else:
                    S.op("dve", lambda: A_.tensor_copy(out=dst[0:64, 2 * hp:2 * hp + 2, :], in_=self.PS[pb][0:64, :].rearrange("p (h t) -> p h t", h=2)), [self.BPS[pb]], [Bd])
            for hh in range(2):
                h = 2 * hp + hh
                for kc in range(2):
                    S.op("pe", lambda: nc.tensor.matmul(self.PS[6][RP, hh * BLK:(hh + 1) * BLK], lhsT=Wq[:, kc, h * 96 + 64:h * 96 + 96], rhs=cn[:, kc, :], start=(kc == 0), stop=(kc == 1)), [BWq, Bcn], [self.BPS[6]])
                for kc in range(2):
                    S.op("pe", lambda: nc.tensor.matmul(self.PS[7][RP, hh * BLK:(hh + 1) * BLK], lhsT=Wqr[:, kc, h, :], rhs=cn[:, kc, :], start=(kc == 0), stop=(kc == 1)), [BWr, Bcn], [self.BPS[7]])
            cosb = cos[RP, tcol].unsqueeze(1).to_broadcast([32, 2, BLK])
            sinb = sin[RP, tcol].unsqueeze(1).to_broadcast([32, 2, BLK])
            S.op("dve", lambda: A_.tensor_tensor(out=t1[RP, :, :], in0=self.PS[6][RP, :].rearrange("p (h t) -> p h t", h=2), in1=cosb, op=ALU.mult), [self.BPS[6], Bcs], [Bt1])
            S.op("dve", lambda: A_.tensor_tensor(out=t2[RP, :, :], in0=self.PS[7][RP, :].rearrange("p (h t) -> p h t", h=2), in1=sinb, op=ALU.mult), [self.BPS[7], Bcs], [Bt2])
            S.op("dve", lambda: A_.tensor_tensor(out=qn_s[i][RP, 2 * hp:2 * hp + 2, :], in0=t1[RP, :, :], in1=t2[RP, :, :], op=ALU.add), [Bt1, Bt2], [Bqn[i]])
        S.dma("pool", QN[:, :, col:col + BLK], qn_s[i], reads=[Bqn[i]])
        S.dma("pool", KN[:, :, col:col + BLK], kn_s[i], reads=[Bkn[i]])
        Wkv = Wk.rearrange("p k (h c) -> p k h c", c=128)
        for tt in range(BLK // 128):
            vi = vti % 2
            vti += 1
            for hf in range(2):
                pb = 4 + hf
                for kc in range(2):
                    S.op("pe", lambda: nc.tensor.matmul(self.PS[pb][:, 0:512], lhsT=cn[:, 2 + kc, tt * 128:(tt + 1) * 128], rhs=Wkv[:, kc, hf * 8:(hf + 1) * 8, 64:128], start=(kc == 0), stop=(kc == 1)),
                         [BWk, Bcn], [self.BPS[pb]])
                S.op("act", lambda: nc.scalar.copy(out=vt_s[vi][:, hf * 512:(hf + 1) * 512], in_=self.PS[pb][:, 0:512]), [self.BPS[pb]], [Bvt[vi]])
            S.dma("pool", VT[col + tt * 128:col + (tt + 1) * 128, :], vt_s[vi], reads=[Bvt[vi]])
    st.close()
    st = Stage(self, "m2")
    NKT = T // 128
    Vall = st.sb("Vall", [128, NKT, D], BF16)
    KNh = [st.sb(f"KNh{i}", [96, T], BF16) for i in range(2)]
    QNh = [st.sb(f"QNh{i}", [96, T], BF16) for i in range(2)]
    VX = [st.sb(f"VX{i}", [128, NKT, 65], BF16) for i in range(2)]
    PT = [st.sb(f"PT{i}", [128, 512], BF16) for i in range(3)]
    rd = st.sb("rd", [65, 512]); rb = [st.sb(f"rb{i}", [64, 512]) for i in range(2)]
    ob = [st.sb(f"ob{i}", [64, 512], BF16) for i in range(2)]
    BVa, BKR, Brd = Buf(), Buf(), Buf()
    BKN, BQN, BQR, BVX, Brb, Bob = [[Buf(), Buf()] for _ in range(6)]
    BPT = [Buf() for _ in range(3)]
    for i in range(2):
        S.op("pool", lambda: nc.gpsimd.memset(VX[i], 1.0), [], [BVX[i]])
    qblocks = [(0, TC, 2)] + [(TC + qb * 512, 512, NKT) for qb in range(4)]
    pti = 0
    hn = 0
    for b in range(NB):
        c0 = b * T
        S.dma("sp", Vall, VT[c0:c0 + T, :].rearrange("(kt p) v -> p kt v", p=128), writes=[BVa])
        for h in range(NH):
            i = hn % 2
            hn += 1
            S.dma("sp", KNh[i], KN[:, h, c0:c0 + T], writes=[BKN[i]])
            S.dma("sp", QNh[i], QN[:, h, c0:c0 + T], writes=[BQN[i]])
            S.op("pool", lambda: nc.gpsimd.tensor_copy(out=VX[i][:, :, 0:64], in_=Vall[:, :, h * 64:(h + 1) * 64]), [BVa], [BVX[i]])
            for qi, (q0, nq, nkt) in enumerate(qblocks):
                po = 4 + (qi % 2)

                def score(kt):
                    ps = kt % 4
                    ks = slice(kt * 128, (kt + 1) * 128)
                    S.op("pe", lambda: nc.tensor.matmul(self.PS[ps][:, 0:nq], lhsT=KNh[i][:, ks], rhs=QNh[i][:, q0:q0 + nq], start=True, stop=True), [BKN[i], BQN[i]], [self.BPS[ps]])

                score(0)
                if nkt > 1:
                    score(1)
                for kt in range(nkt):
                    ps = kt % 4
                    p3 = pti % 3
                    pti += 1
                    if kt + 2 < nkt:
                        score(kt + 2)
                    S.op("act", lambda: nc.scalar.activation(out=PT[p3][:, 0:nq], in_=self.PS[ps][:, 0:nq], func=AF.Exp, scale=MLA_SCALE), [self.BPS[ps]], [BPT[p3]])
                    S.op("pe", lambda: nc.tensor.matmul(self.PS[po][0:65, 0:nq], lhsT=VX[i][:, kt, :], rhs=PT[p3][:, 0:nq], start=(kt == 0), stop=(kt == nkt - 1)), [BVX[i], BPT[p3]], [self.BPS[po]])
                r2 = qi % 2
                S.op("dve", lambda: A_.reciprocal(out=rd[64:65, 0:nq], in_=self.PS[po][64:65, 0:nq]), [self.BPS[po]], [Brd])
                S.op("pe", lambda: nc.tensor.matmul(self.PS[6 + r2][0:64, 0:nq], lhsT=self.onesf[64:65, 0:64], rhs=rd[64:65, 0:nq], start=True, stop=True), [Brd], [self.BPS[6 + r2]])
                S.op("act", lambda: nc.scalar.copy(out=rb[r2][:, 0:nq], in_=self.PS[6 + r2][0:64, 0:nq]), [self.BPS[6 + r2]], [Brb[r2]])
                S.op("dve", lambda: A_.tensor_tensor(out=ob[r2][:, 0:nq], in0=self.PS[po][0:64, 0:nq], in1=rb[r2][:, 0:nq], op=ALU.mult), [self.BPS[po], Brb[r2]], [Bob[r2]])
                S.dma("pool", og[h * 64:(h + 1) * 64, c0 + q0:c0 + q0 + nq], ob[r2][:, 0:nq], reads=[Bob[r2]])
    st.close()


Prog.mla_mixer = _mla_mixer


RW_ARR = ["r", "kt0", "kt1", "be0", "be1", "kap", "lw0", "lw1", "v", "g"]


def _rwkv_proj(self, l, xin, RWP, Vtm):
    nc, S = self.nc, self.S
    A_ = nc.vector
    st = Stage(self, "r1")
    Wrkv = st.sb("wrkv", [128, 8, 3 * D], BF16)
    BWrkv = []
    for i3 in range(3):
        v_ = self.W["rw_w_rkv"][0, i3].rearrange("(kc p) n -> p kc n", p=128)
        for hf in range(2):
            bb_ = Buf()
            S.dma("pool", Wrkv[:, :, i3 * D + hf * 512:i3 * D + (hf + 1) * 512], v_[:, :, hf * 512:(hf + 1) * 512], writes=[bb_])
            BWrkv.append(bb_)
    W1 = st.sb("w1", [128, 8, 2, 64], BF16); A1 = st.sb("a1", [128, 8, 2, 64], BF16); G1 = st.sb("g1", [128, 8, 160], BF16)
    W2 = st.sb("w2", [64, 2, D], BF16); A2 = st.sb("a2", [64, 2, D], BF16); G2a = st.sb("g2a", [128, D], BF16); G2b = st.sb("g2b", [32, D], BF16)
    Bsw = Buf()
    for d in range(2):
        S.dma("pool", W1[:, :, d, :], self.W["rw_w1"][0, d].rearrange("(kc p) n -> p kc n", p=128), writes=[Bsw])
        S.dma("pool", A1[:, :, d, :], self.W["rw_a1"][0, d].rearrange("(kc p) n -> p kc n", p=128), writes=[Bsw])
        S.dma("pool", W2[:, d, :], self.W["rw_w2"][0, d], writes=[Bsw])
        S.dma("pool", A2[:, d, :], self.W["rw_a2"][0, d], writes=[Bsw])
    S.dma("pool", G1, self.W["rw_g1"][0].rearrange("(kc p) n -> p kc n", p=128), writes=[Bsw])
    S.dma("pool", G2a, self.W["rw_g2"][0, 0:128, :], writes=[Bsw])
    S.dma("pool", G2b, self.W["rw_g2"][0, 128:160, :], writes=[Bsw])
    NH_ = BLK + 2
    xs = [st.sb(f"xs{i}", [128, 8, NH_]) for i in range(2)]
    hf_ = st.sb("hf", [128, 8, NH_])
    dx = st.sb("dx", [128, 8, BLK])
    xj = [st.sb(f"xj{j}", [128, 8, BLK], BF16) for j in range(6)]
    Bxs = [Buf(), Buf()]
    Bhf, Bdx = Buf(), Buf()
    Bxj = [Buf() for _ in range(6)]
    nt = self.norm_tiles(st)
    lt = st.sb("lt", [64, 5, BLK], BF16)
    gh = st.sb("gh", [128, BLK], BF16)
    Blt = Buf()
    stg = [st.sb(f"stg{i}", [128, 10, BLK]) for i in range(2)]
    Bstg = [Buf(), Buf()]
    tmp = [st.sb(f"tmp{i}", [128, BLK]) for i in range(6)]
    Btmp = [Buf() for _ in range(6)]
    sqb = st.sb("sqb", [128, BLK], BF16)
    Bsqb = Buf()
    vts = [st.sb(f"vts{i}", [128, D], BF16) for i in range(2)]
    Bvts = [Buf(), Buf()]
    for i in range(2):
        S.op("dve", lambda: A_.memset(xs[i], 0.0), [], [Bxs[i]])
    xiv = xin.rearrange("(c p) t -> p c t", p=128)
    blocks = self.blocks(False)
    blk64b = st.sb("blk64b", [128, 128], BF16)
    Bb64 = Buf()
    S.op("dve", lambda: A_.tensor_copy(out=blk64b, in_=self.blk64), [], [Bb64])

    def load(n):
        b, k = blocks[n]
        t0, lo, hi, _, _ = self.blk_range(k)
        S.dma("sp", xs[n % 2][:, :, lo - (t0 - 1):hi - (t0 - 1)], xiv[:, :, b * T + lo:b * T + hi], writes=[Bxs[n % 2]])

    load(0)
    si = 0
    vi_ = 0
    pbk = 0
    for n, (b, k) in enumerate(blocks):
        i = n % 2
        if n + 1 < len(blocks):
            load(n + 1)
        t0, lo, hi, first, last = self.blk_range(k)
        j = 2 if k == 0 else b
        A, sh, _ = self.mod_ab(l, 0, j)
        self.norm_block(nt, xs[i], Bxs[i], NH_, A, sh, hf_, Bhf, 6)
        if first:
            S.op("dve", lambda: A_.memset(hf_[:, :, 0:1], 0.0), [], [Bhf])
        if last:
            S.op("dve", lambda: A_.memset(hf_[:, :, NH_ - 1:NH_], 0.0), [], [Bhf])
        S.op("dve", lambda: A_.tensor_tensor(out=dx, in0=hf_[:, :, 0:BLK], in1=hf_[:, :, 2:2 + BLK], op=ALU.add), [Bhf], [Bdx])
        S.op("dve", lambda: A_.scalar_tensor_tensor(out=dx, in0=dx, scalar=0.5, in1=hf_[:, :, 1:1 + BLK], op0=ALU.mult, op1=ALU.subtract), [Bhf, Bdx], [Bdx])
        for jj in range(6):
            for c in range(8):
                S.op("dve", lambda: A_.scalar_tensor_tensor(out=xj[jj][:, c, :], in0=dx[:, c, :], scalar=self.pv(f"rw_mu{jj}", c), in1=hf_[:, c, 1:1 + BLK], op0=ALU.mult, op1=ALU.add),
                     [Bdx, Bhf], [Bxj[jj]])
        for d in range(2):
            for kc in range(8):
                S.op("pe", lambda: nc.tensor.matmul(self.PS[5][0:64, d * BLK:(d + 1) * BLK], lhsT=W1[:, kc, d, :], rhs=xj[1][:, kc, :], start=(kc == 0), stop=(kc == 7)), [Bsw, Bxj[1]], [self.BPS[5]])
        S.op("act", lambda: nc.scalar.activation(out=lt[:, 0:2, :], in_=self.PS[5][0:64, :].rearrange("p (d t) -> p d t", d=2), func=AF.Tanh), [self.BPS[5]], [Blt])
        for d in range(2):
            for kc in range(8):
                S.op("pe", lambda: nc.tensor.matmul(self.PS[5][0:64, d * BLK:(d + 1) * BLK], lhsT=A1[:, kc, d, :], rhs=xj[4][:, kc, :], start=(kc == 0), stop=(kc == 7)), [Bsw, Bxj[4]], [self.BPS[5]])
        S.op("act", lambda: nc.scalar.copy(out=lt[:, 2:4, :], in_=self.PS[5][0:64, :].rearrange("p (d t) -> p d t", d=2)), [self.BPS[5]], [Blt])
        for kc in range(8):
            S.op("pe", lambda: nc.tensor.matmul(self.PS[5][:, 0:BLK], lhsT=G1[:, kc, 0:128], rhs=xj[5][:, kc, :], start=(kc == 0), stop=(kc == 7)), [Bsw, Bxj[5]], [self.BPS[5]])
        for kc in range(8):
            S.op("pe", lambda: nc.tensor.matmul(self.PS[5][0:32, BLK:2 * BLK], lhsT=G1[:, kc, 128:160], rhs=xj[5][:, kc, :], start=(kc == 0), stop=(kc == 7)), [Bsw, Bxj[5]], [self.BPS[5]])
        S.op("act", lambda: nc.scalar.activation(out=gh, in_=self.PS[5][:, 0:BLK], func=AF.Sigmoid), [self.BPS[5]], [Blt])
        S.op("act", lambda: nc.scalar.activation(out=lt[0:32, 4, :], in_=self.PS[5][0:32, BLK:2 * BLK], func=AF.Sigmoid), [self.BPS[5]], [Blt])
        col = b * T + t0
        for c in range(8):
            s_ = si % 2
            si += 1
            sg_ = stg[s_]
            Bs = Bstg[s_]
            cs = slice(c * 128, (c + 1) * 128)

            def bank():
                nonlocal pbk
                pbk += 1
                return pbk % 5

            prk = []
            for which, xsrc in ((0, 0), (1, 2), (2, 3)):
                pb = bank()
                for kc in range(8):
                    S.op("pe", lambda: nc.tensor.matmul(self.PS[pb][:, 0:BLK], lhsT=Wrkv[:, kc, which * D + c * 128:which * D + (c + 1) * 128], rhs=xj[xsrc][:, kc, :], start=(kc == 0), stop=(kc == 7)),
                         [BWrkv[which * 2 + (c // 4)], Bxj[xsrc]], [self.BPS[pb]])
                prk.append(pb)
            S.op("act", lambda: nc.scalar.copy(out=sg_[:, 0, :], in_=self.PS[prk[0]][:, 0:BLK]), [self.BPS[prk[0]]], [Bs])
            S.op("act", lambda: nc.scalar.copy(out=sg_[:, 8, :], in_=self.PS[prk[2]][:, 0:BLK]), [self.BPS[prk[2]]], [Bs])
            kraw = tmp[0]
            S.op("act", lambda: nc.scalar.copy(out=kraw, in_=self.PS[prk[1]][:, 0:BLK]), [self.BPS[prk[1]]], [Btmp[0]])
            S.op("dve", lambda: A_.tensor_scalar(out=tmp[1], in0=kraw, scalar1=self.pv("rw_k_k", c), scalar2=None, op0=ALU.mult), [Btmp[0]], [Btmp[1]])
            S.op("act", lambda: nc.scalar.activation(out=sqb, in_=tmp[1], func=AF.Square), [Btmp[1]], [Bsqb])
            pb = bank()
            S.op("pe", lambda: nc.tensor.matmul(self.PS[pb][:, 0:BLK], lhsT=blk64b, rhs=sqb, start=True, stop=True), [Bsqb, Bb64], [self.BPS[pb]])
            S.op("act", lambda: nc.scalar.activation(out=tmp[2], in_=self.PS[pb][:, 0:BLK], func=AF.Sqrt), [self.BPS[pb]], [Btmp[2]])
            S.op("dve", lambda: A_.tensor_scalar(out=tmp[2], in0=tmp[2], scalar1=1e-12, scalar2=None, op0=ALU.max), [Btmp[2]], [Btmp[2]])
            S.op("dve", lambda: A_.reciprocal(out=tmp[2], in_=tmp[2]), [Btmp[2]], [Btmp[2]])
            S.op("dve", lambda: A_.tensor_tensor(out=sg_[:, 5, :], in0=tmp[1], in1=tmp[2], op=ALU.mult), [Btmp[1], Btmp[2]], [Bs])
            pb = bank()
            S.op("pe", lambda: nc.tensor.matmul(self.PS[pb][:, 0:BLK], lhsT=G2a[:, cs], rhs=gh, start=True, stop=False), [Bsw, Blt], [self.BPS[pb]])
            S.op("pe", lambda: nc.tensor.matmul(self.PS[pb][:, 0:BLK], lhsT=G2b[:, cs], rhs=lt[0:32, 4, :], start=False, stop=True), [Bsw, Blt], [self.BPS[pb]])
            S.op("act", lambda: nc.scalar.copy(out=sg_[:, 9, :], in_=self.PS[pb][:, 0:BLK]), [self.BPS[pb]], [Bs])
            for d in range(2):
                pb = bank()
                S.op("pe", lambda: nc.tensor.matmul(self.PS[pb][:, 0:BLK], lhsT=W2[:, d, cs], rhs=lt[:, d, :], start=True, stop=True), [Bsw, Blt], [self.BPS[pb]])
                S.op("act", lambda: nc.scalar.activation(out=tmp[3], in_=self.PS[pb][:, 0:BLK], func=AF.Sigmoid, bias=self.pv(f"rw_w0_{d}", c)), [self.BPS[pb]], [Btmp[3]])
                S.op("dve", lambda: A_.tensor_scalar(out=sg_[:, 6 + d, :], in0=tmp[3], scalar1=-float(np.exp(-0.5)), scalar2=None, op0=ALU.mult), [Btmp[3]], [Bs])
                pb = bank()
                S.op("pe", lambda: nc.tensor.matmul(self.PS[pb][:, 0:BLK], lhsT=A2[:, d, cs], rhs=lt[:, 2 + d, :], start=True, stop=True), [Bsw, Blt], [self.BPS[pb]])
                S.op("act", lambda: nc.scalar.activation(out=tmp[4], in_=self.PS[pb][:, 0:BLK], func=AF.Sigmoid, bias=self.pv(f"rw_a0_{d}", c)), [self.BPS[pb]], [Btmp[4]])
                S.op("dve", lambda: A_.tensor_tensor(out=sg_[:, 3 + d, :], in0=tmp[4], in1=sg_[:, 5, :], op=ALU.mult), [Btmp[4], Bs], [Bs])
                S.op("dve", lambda: A_.tensor_scalar(out=tmp[5], in0=tmp[4], scalar1=-1.0, scalar2=None, op0=ALU.add), [Btmp[4]], [Btmp[5]])
                S.op("dve", lambda: A_.tensor_scalar(out=tmp[5], in0=tmp[5], scalar1=self.pv("rw_k_a", c), scalar2=1.0, op0=ALU.mult, op1=ALU.add), [Btmp[5]], [Btmp[5]])
                S.op("dve", lambda: A_.tensor_tensor(out=sg_[:, 1 + d, :], in0=tmp[5], in1=kraw, op=ALU.mult), [Btmp[5], Btmp[0]], [Bs])
            S.dma("pool", RWP[:, c * 128:(c + 1) * 128, col:col + BLK].rearrange("a p t -> p a t"), sg_, reads=[Bs])
        for tt in range(BLK // 128):
            vi = vi_ % 2
            vi_ += 1
            for hfv in range(2):
                pb = 4 - hfv
                for kc in range(8):
                    S.op("pe", lambda: nc.tensor.matmul(self.PS[pb][:, 0:512], lhsT=xj[3][:, kc, tt * 128:(tt + 1) * 128], rhs=Wrkv[:, kc, 2 * D + hfv * 512:2 * D + (hfv + 1) * 512], start=(kc == 0), stop=(kc == 7)),
                         [BWrkv[4 + hfv], Bxj[3]], [self.BPS[pb]])
                S.op("act", lambda: nc.scalar.copy(out=vts[vi][:, hfv * 512:(hfv + 1) * 512], in_=self.PS[pb][:, 0:512]), [self.BPS[pb]], [Bvts[vi]])
            S.dma("pool", Vtm[col + tt * 128:col + (tt + 1) * 128, :], vts[vi], reads=[Bvts[vi]])
    st.close()


def _rwkv_mixer(self, l, xin, og):
    RWP = self.scr("rwP", [10, D, TT])
    Vtm = self.scr("rwV", [TT, D], BF16)
    self.rwkv_proj(l, xin, RWP, Vtm)
    if getattr(self, "rw_stop", 0) == 1:
        return
    self.rwkv_scan(RWP, Vtm, og)


Prog.rwkv_proj = _rwkv_proj
Prog.rwkv_mixer = _rwkv_mixer


def _rwkv_scan(self, RWP, Vtm, og):
    nc, S = self.nc, self.S
    A_ = nc.vector
    U32 = mybir.dt.uint32
    RWD = self.scr("rwD", [NB, 8, 2, 2, 128, NCH * 128], BF16)
    RWS = self.scr("rwS", [NB, 8, 2, 128, 3 * NCH])
    skipA = getattr(self, "rw_skipA", False)
    st = Stage(self, "r2a")
    smask = st.sb("smask", [128, T])
    Bsm = Buf()
    S.dma("sp", smask, self.cd["scanmask"], writes=[Bsm])
    lw = st.sb("lw", [128, T]); kap = st.sb("kap", [128, T]); rr = st.sb("r", [128, T]); kt = st.sb("kt", [128, T]); be = st.sb("be", [128, T])
    cw = st.sb("cw", [128, T]); cm = st.sb("cm", [128, T]); en = st.sb("en", [128, T]); ex = st.sb("ex", [128, T])
    ABt = [st.sb(f"AB{i}", [128, NCH, 2, CH], BF16) for i in range(2)]
    KBt_ = [st.sb(f"KB{i}", [128, NCH, 2, CH], BF16) for i in range(2)]
    SC = [st.sb(f"SC{i}", [128, 3, NCH]) for i in range(2)]
    Blw, Bkap, Br, Bkt, Bbe, Bcw, Bcm, Ben, Bex = [Buf() for _ in range(9)]
    BAB, BKB, BSC = [[Buf(), Buf()] for _ in range(3)]
    it = 0
    v3 = lambda t_: t_.rearrange("p (c s) -> p c s", s=CH)
    for b in range(0 if skipA else NB):
        cols = slice(b * T, (b + 1) * T)
        for p in range(8):
            rows = slice(p * 128, (p + 1) * 128)
            for d in range(2):
                i = it % 2
                it += 1
                S.dma("sp", lw, RWP[6 + d, rows, cols], writes=[Blw])
                S.dma("sp", kap, RWP[5, rows, cols], writes=[Bkap])
                S.dma("sp", rr, RWP[0, rows, cols], writes=[Br])
                S.dma("sp", kt, RWP[1 + d, rows, cols], writes=[Bkt])
                S.dma("sp", be, RWP[3 + d, rows, cols], writes=[Bbe])
                S.op("dve", lambda: A_.tensor_tensor_scan(out=cw, data0=smask, data1=lw, initial=0.0, op0=ALU.mult, op1=ALU.add), [Bsm, Blw], [Bcw])
                if d == 1:
                    S.op("dve", lambda: A_.tensor_tensor(out=cm, in0=lw, in1=cw, op=ALU.subtract), [Blw, Bcw], [Bcm])
                    S.op("dve", lambda: A_.tensor_tensor(out=v3(en), in0=v3(cm), in1=v3(cw)[:, :, CH - 1:CH].to_broadcast([128, NCH, CH]), op=ALU.add), [Bcm, Bcw], [Ben])
                    S.op("dve", lambda: A_.tensor_copy(out=cw, in_=en), [Ben], [Bcw])
                m_idx = 32 if d == 0 else 31
                e_idx = CH - 1 if d == 0 else 0
                c3 = v3(cw)
                S.op("act", lambda: nc.scalar.activation(out=SC[i][:, 0, :], in_=c3[:, :, m_idx], func=AF.Exp), [Bcw], [BSC[i]])
                S.op("act", lambda: nc.scalar.activation(out=SC[i][:, 1, :], in_=c3[:, :, e_idx], func=AF.Exp), [Bcw], [BSC[i]])
                S.op("dve", lambda: A_.tensor_tensor(out=SC[i][:, 2, :], in0=c3[:, :, e_idx], in1=c3[:, :, m_idx], op=ALU.subtract), [Bcw], [BSC[i]])
                S.op("act", lambda: nc.scalar.activation(out=SC[i][:, 2, :], in_=SC[i][:, 2, :], func=AF.Exp), [BSC[i]], [BSC[i]])
                S.dma("pool", RWS[b, p, d], SC[i].rearrange("p a c -> p (a c)"), reads=[BSC[i]])
                S.op("dve", lambda: A_.tensor_tensor(out=v3(cm), in0=c3, in1=c3[:, :, m_idx:m_idx + 1].to_broadcast([128, NCH, CH]), op=ALU.subtract), [Bcw], [Bcm])
                S.op("act", lambda: nc.scalar.activation(out=en, in_=cm, func=AF.Exp, scale=-1.0), [Bcm], [Ben])
                S.op("dve", lambda: A_.tensor_tensor(out=ex, in0=cm, in1=lw, op=ALU.subtract), [Bcm, Blw], [Bex])
                S.op("act", lambda: nc.scalar.activation(out=ex, in_=ex, func=AF.Exp), [Bex], [Bex])
                S.op("act", lambda: nc.scalar.activation(out=cm, in_=cm, func=AF.Exp), [Bcm], [Bcm])
                S.op("dve", lambda: A_.tensor_tensor(out=ABt[i][:, :, 0, :], in0=v3(kap), in1=v3(ex), op=ALU.mult), [Bkap, Bex], [BAB[i]])
                S.op("dve", lambda: A_.tensor_tensor(out=ABt[i][:, :, 1, :], in0=v3(rr), in1=v3(cm), op=ALU.mult), [Br, Bcm], [BAB[i]])
                S.op("dve", lambda: A_.tensor_tensor(out=KBt_[i][:, :, 0, :], in0=v3(kt), in1=v3(en), op=ALU.mult), [Bkt, Ben], [BKB[i]])
                S.op("dve", lambda: A_.tensor_tensor(out=KBt_[i][:, :, 1, :], in0=v3(be), in1=v3(en), op=ALU.mult), [Bbe, Ben], [BKB[i]])
                S.dma("pool", RWD[b, p, d, 0], ABt[i].rearrange("p c a s -> p (c a s)"), reads=[BAB[i]])
                S.dma("pool", RWD[b, p, d, 1], KBt_[i].rearrange("p c a s -> p (c a s)"), reads=[BKB[i]])
    st.close()
    if getattr(self, "rw_stop", 0) == 2:
        return
    st = Stage(self, "r2b")
    S.pe_selfwait = getattr(self, "rw_selfwait", False)
    S.pe_drain = getattr(self, "rw_drain", 2)
    epsLN = st.sb("epsLN", [128, 1])
    Bgl = Buf()
    S.op("dve", lambda: A_.memset(epsLN, RW_LN_EPS), [], [Bgl])
    AB = [st.sb(f"AB{d}", [128, NCH, 128], BF16) for d in range(2)]
    KB = [st.sb(f"KB{d}", [128, NCH, 128], BF16) for d in range(2)]
    SCs = [st.sb(f"SC{d}", [128, 3, NCH]) for d in range(2)]
    Vst = st.sb("Vst", [64, NCH, 128], BF16)
    BABl, BKBl, BSCl = [[Buf(), Buf()] for _ in range(3)]
    BVst = Buf()
    chains = [(hd, d) for hd in range(2) for d in range(2)]
    IDT = BF16 if getattr(self, "rw_inv_bf16", True) else F32
    VU, GGb, AN0, ANp, Xp, Wf, KBtr = {}, {}, {}, {}, {}, {}, {}
    BVU, BGG, BAN0, BANp, BXp, BWf, BKBtr, BST, BS0, BtS, By = [dict() for _ in range(11)]
    for ch in chains:
        nm = f"{ch[0]}{ch[1]}"
        VU[ch] = st.sb("VU" + nm, [128, NCH, CH], BF16)
        GGb[ch] = st.sb("GG" + nm, [128, 128], BF16)
        AN0[ch] = st.sb("AN0" + nm, [128, 128], IDT)
        ANp[ch] = [st.sb(f"ANp{q}" + nm, [128, 128], IDT) for q in range(2)]
        Xp[ch] = [st.sb(f"X{q}" + nm, [128, CH], IDT) for q in range(2)]
        Wf[ch] = st.sb("Wf" + nm, [128, CH], IDT)
        KBtr[ch] = st.sb("KBt" + nm, [128, CH], BF16)
        BVU[ch], BGG[ch], BAN0[ch], BWf[ch], BKBtr[ch], BST[ch], BS0[ch], BtS[ch], By[ch] = [Buf() for _ in range(9)]
        BANp[ch] = [Buf(), Buf()]
        BXp[ch] = [Buf(), Buf()]
        S.op("dve", lambda: A_.memset(GGb[ch], 0.0), [], [BGG[ch]])
        S.op("dve", lambda: A_.memset(AN0[ch], 0.0), [], [BAN0[ch]])
    ST = [st.sb(f"ST{d}", [128, CH]) for d in range(2)]
    S0m = [st.sb(f"S0m{d}", [128, CH], BF16) for d in range(2)]
    tS = [st.sb(f"tS{d}", [128, CH]) for d in range(2)]
    yacc = [st.sb(f"yacc{d}", [128, T]) for d in range(2)]
    rl = st.sb("rl", [128, T]); k0 = st.sb("k0", [128, T]); k1 = st.sb("k1", [128, T]); vf = st.sb("vf", [128, T]); gg = st.sb("gg", [128, T])
    t0_ = st.sb("t0", [128, T]); t1_ = st.sb("t1", [128, T])
    ogb = st.sb("ogb", [128, T], BF16)
    Brl, Bk0, Bk1, Bvf, Bgg, Bt0, Bt1, Bogb = [Buf() for _ in range(8)]
    MERGE = getattr(self, "rw_merge", True)
    if MERGE:
        mKB = [st.sb(f"mKBt{d}", [128, 2, CH], BF16) for d in range(2)]
        mGG = [st.sb(f"mGG{d}", [128, 2, 128], BF16) for d in range(2)]
        mAN0 = [st.sb(f"mAN0{d}", [128, 2, 128], IDT) for d in range(2)]
        mANp = [[st.sb(f"mANp{q}{d}", [128, 2, 128], IDT) for q in range(2)] for d in range(2)]
        mXp = [[st.sb(f"mX{q}{d}", [128, 2, CH], IDT) for q in range(2)] for d in range(2)]
        mWf = [st.sb(f"mWf{d}", [128, 2, CH], IDT) for d in range(2)]
        mVU = [st.sb(f"mVU{d}", [128, NCH, 2, CH], BF16) for d in range(2)]
        M4x2 = [st.sb(f"M4x2{d}", [128, 2, 128]) for d in range(2)]
        mAx2 = [st.sb(f"mAx2{d}", [128, 2, CH]) for d in range(2)]
        mNx2 = [st.sb(f"mNx2{d}", [128, 2, CH]) for d in range(2)]
        I2 = st.sb("I2", [128, 2, CH])
        Bmk = Buf()
        mBKB, mBGG, mBAN0, mBWf, mBVU, mBST, mBS0, mBtS, mBy = [[Buf(), Buf()] for _ in range(9)]
        mBANp = [[Buf(), Buf()], [Buf(), Buf()]]
        mBXp = [[Buf(), Buf()], [Buf(), Buf()]]
        mUB = [[Buf() for _ in range(4)] for d in range(2)]
        for d in range(2):
            S.op("dve", lambda: A_.memset(mGG[d], 0.0), [], [mBGG[d]])
            S.op("dve", lambda: A_.memset(mAN0[d], 0.0), [], [mBAN0[d]])
            for hd in range(2):
                S.op("dve", lambda: A_.tensor_copy(out=M4x2[d][:, hd, :], in_=(self.masks[:, 0:128] if d == 0 else self.masks[:, 128:256])), [], [Bmk])
                S.op("dve", lambda: A_.tensor_copy(out=mAx2[d][:, hd, :], in_=(self.masks[:, 0:64] if d == 0 else self.masks[:, 128:192])), [], [Bmk])
                S.op("dve", lambda: A_.tensor_copy(out=mNx2[d][:, hd, :], in_=(self.masks[:, 128:192] if d == 0 else self.masks[:, 0:64])), [], [Bmk])
        for hd in range(2):
            S.op("dve", lambda: A_.tensor_copy(out=I2[64:128, hd, :], in_=self.ident[64:128, 64:128]), [], [Bmk])
    R = {}
    BR = {}
    for ci, ch in enumerate(chains):
        b0, b1 = self.PS[2 * ci], self.PS[2 * ci + 1]
        R[ch] = dict(GA=b0[:, 0:128], LV=b0[:, 192:320], Wp=b0[:, 384:448],
                     XL=b1[:, 320:384], Up=b1[:, 448:512], Nn=b1[:, 128:192],
                     Yp=b1[:, 0:64], Sd=b1[:, 64:128], TR=b1.bitcast(BF16)[:, 512:576])
        u0, u1, u2, u3 = Buf(), Buf(), Buf(), Buf()
        ykp = [u2] if ch[0] == 0 else [u3]
        BR[ch] = dict(GAlo=[u0], GAup=[u1], GA=[u0, u1], LV=[u1], Wp=[u1], XL=[u3], Up=[u3], Nn=[u3], Yp=ykp, Sd=ykp, TR=[u2, u3], ALL=[u0, u1, u2, u3])
    up, lo = slice(64, 128), slice(0, 64)
    mU = lambda ap: ap.bitcast(U32)
    cf = list(range(NCH))
    cbk = list(range(TC // CH - 1, -1, -1)) + list(range(NCH - 1, TC // CH - 1, -1))
    order = [cf, cbk]
    dbgn = getattr(self, "rw_dbg", None)
    for b in range(NB):
        cols = slice(b * T, (b + 1) * T)
        for p in range(8):
            if dbgn is not None and (b * 8 + p) >= dbgn[0]:
                continue
            rows = slice(p * 128, (p + 1) * 128)
            for d in range(2):
                S.dma("sp", AB[d], RWD[b, p, d, 0].rearrange("k (c x) -> k c x", x=128), writes=[BABl[d]])
                S.dma("sp", KB[d], RWD[b, p, d, 1].rearrange("k (c x) -> k c x", x=128), writes=[BKBl[d]])
                S.dma("sp", SCs[d], RWS[b, p, d].rearrange("k (a c) -> k a c", a=3), writes=[BSCl[d]])
            S.dma("sp", Vst, Vtm[cols, rows].rearrange("(c s) v -> s c v", s=CH), writes=[BVst])
            S.dma("sp", rl, RWP[0, rows, cols], writes=[Brl])
            S.dma("sp", k0, RWP[1, rows, cols], writes=[Bk0])
            S.dma("sp", k1, RWP[2, rows, cols], writes=[Bk1])
            S.dma("sp", vf, RWP[8, rows, cols], writes=[Bvf])
            S.dma("sp", gg, RWP[9, rows, cols], writes=[Bgg])
            if MERGE:
                for d in range(2):
                    S.op("pool", lambda: nc.gpsimd.tensor_copy(out=mVU[d][lo, :, :, :], in_=Vst.rearrange("s c (h v) -> s c h v", h=2)), [BVst], [mBVU[d]])
                    S.op("dve", lambda: A_.memset(ST[d], 0.0), [], [mBST[d]])
                    S.op("dve", lambda: A_.memset(S0m[d], 0.0), [], [mBS0[d]])

                def dstep(d, step):
                    c = order[d][step]
                    cs = slice(c * CH, (c + 1) * CH)
                    bA, bB, bC, bD = [self.PS[4 * d + q] for q in range(4)]
                    uA, uB, uC, uD = mUB[d]
                    h2 = lambda ap: ap.rearrange("p (h x) -> p h x", h=2)
                    GA = h2(bA[:, 0:256]); LV = h2(bB[:, 0:256]); Wp = h2(bB[:, 256:384])
                    XL = h2(bC[:, 0:128]); Up_ = h2(bC[:, 128:256]); Nn = h2(bC[:, 256:384])
                    Yp = bD[:, 0:64]; Sd = bD[:, 64:128]; TR = h2(bD.bitcast(BF16)[:, 512:640])
                    KP = [slice(0, 64), slice(64, 128)]
                    for hd in range(2):
                        kp = KP[hd]
                        S.op("pe", lambda: nc.tensor.transpose(out=TR[:, hd, :], in_=KB[d][kp, c, :], identity=self.identb[kp, kp]), [BKBl[d]], [uD], pemode=("T", hd))
                        S.op("pe", lambda: nc.tensor.matmul(GA[lo, hd, :], lhsT=KB[d][kp, c, 0:64], rhs=AB[d][kp, c, :], start=True, stop=True), [BKBl[d], BABl[d]], [uA], pemode=("g", hd))
                        S.op("pe", lambda: nc.tensor.matmul(GA[up, hd, :], lhsT=KB[d][kp, c, 64:128], rhs=AB[d][kp, c, :], start=True, stop=True), [BKBl[d], BABl[d]], [uA], pemode=("g", hd))
                        S.op("pe", lambda: nc.tensor.matmul(Nn[up, hd, :], lhsT=AB[d][kp, c, 0:64], rhs=KB[d][kp, c, 64:128], start=True, stop=True), [BKBl[d], BABl[d]], [uC], pemode=("g", hd))
                    yield
                    S.op("act", lambda: nc.scalar.copy(out=mKB[d], in_=TR), [uD], [mBKB[d]])
                    S.op("dve", lambda: A_.copy_predicated(out=mGG[d], mask=mU(M4x2[d][:]), data=GA), [uA, Bmk], [mBGG[d]])
                    S.op("dve", lambda: A_.copy_predicated(out=mAN0[d][up, :, 0:64], mask=mU(mAx2[d][up, :, :]), data=GA[up, :, 0:64]), [uA, Bmk], [mBAN0[d]])
                    S.op("dve", lambda: A_.copy_predicated(out=mAN0[d][up, :, 64:128], mask=mU(mNx2[d][up, :, :]), data=Nn[up, :, :]), [uC, Bmk], [mBAN0[d]])
                    S.op("dve", lambda: A_.tensor_tensor(out=mXp[d][0][up, :, :], in0=I2[up, :, :], in1=mAN0[d][up, :, 0:64], op=ALU.subtract), [mBAN0[d], Bmk], [mBXp[d][0]])
                    yield
                    cur, Bcur = mAN0[d], mBAN0[d]
                    xq = 0
                    for lv in range(1, 7):
                        nx, Bnx = mANp[d][lv % 2], mBANp[d][lv % 2]
                        for hd in range(2):
                            if lv <= 5:
                                if lv < 5:
                                    S.op("pe", lambda: nc.tensor.matmul(LV[up, hd, 0:64], lhsT=cur[up, hd, 64:128], rhs=cur[up, hd, 0:64], start=True, stop=True), [Bcur], [uB], pemode=("f",))
                                S.op("pe", lambda: nc.tensor.matmul(LV[up, hd, 64:128], lhsT=cur[up, hd, 0:64], rhs=cur[up, hd, 64:128], start=True, stop=True), [Bcur], [uB], pemode=("f",))
                            if lv >= 2:
                                S.op("pe", lambda: nc.tensor.matmul(XL[up, hd, :], lhsT=cur[up, hd, 64:128], rhs=mXp[d][xq][up, hd, :], start=True, stop=True), [Bcur, mBXp[d][xq]], [uC], pemode=("f",))
                        yield
                        if lv <= 5:
                            if lv < 5:
                                S.op("act", lambda: nc.scalar.copy(out=nx[up, :, :], in_=LV[up, :, :]), [uB], [Bnx])
                            else:
                                S.op("act", lambda: nc.scalar.copy(out=nx[up, :, 64:128], in_=LV[up, :, 64:128]), [uB], [Bnx])
                        if lv >= 2:
                            S.op("dve", lambda: A_.tensor_tensor(out=mXp[d][1 - xq][up, :, :], in0=XL[up, :, :], in1=mXp[d][xq][up, :, :], op=ALU.add), [uC, mBXp[d][xq]], [mBXp[d][1 - xq]])
                            xq = 1 - xq
                        if lv <= 5:
                            cur, Bcur = nx, Bnx
                        yield
                    for hd in range(2):
                        kp = KP[hd]
                        S.op("pe", lambda: nc.tensor.matmul(Wp[up, hd, :], lhsT=AB[d][kp, c, 0:64], rhs=S0m[d][kp, :], start=True, stop=False), [BABl[d], mBS0[d]], [uB], pemode=("g", hd))
                        S.op("pe", lambda: nc.tensor.matmul(Wp[up, hd, :], lhsT=mGG[d][lo, hd, 0:64], rhs=mVU[d][lo, c, hd, :], start=False, stop=True), [mBGG[d], mBVU[d]], [uB], pemode=("w2",))
                    yield
                    S.op("act", lambda: nc.scalar.copy(out=mWf[d][up, :, :], in_=Wp[up, :, :]), [uB], [mBWf[d]])
                    yield
                    for hd in range(2):
                        S.op("pe", lambda: nc.tensor.matmul(Up_[up, hd, :], lhsT=mXp[d][xq][up, hd, :], rhs=mWf[d][up, hd, :], start=True, stop=True), [mBXp[d][xq], mBWf[d]], [uC], pemode=("f",))
                    yield
                    S.op("act", lambda: nc.scalar.activation(out=mVU[d][up, c, :, :], in_=Up_[up, :, :], func=AF.Copy, scale=-1.0), [uC], [mBVU[d]])
                    yield
                    for hd in range(2):
                        kp = KP[hd]
                        S.op("pe", lambda: nc.tensor.matmul(Yp[kp, :], lhsT=S0m[d][kp, :], rhs=AB[d][kp, c, 64:128], start=True, stop=False), [mBS0[d], BABl[d]], [uD], pemode=("g", hd))
                        S.op("pe", lambda: nc.tensor.matmul(Yp[kp, :], lhsT=mVU[d][:, c, hd, :], rhs=mGG[d][:, hd, 64:128], start=False, stop=True), [mBVU[d], mBGG[d]], [uD], pemode=("full",))
                    for hd in range(2):
                        kp = KP[hd]
                        S.op("pe", lambda: nc.tensor.matmul(Sd[kp, :], lhsT=mKB[d][:, hd, :], rhs=mVU[d][:, c, hd, :], start=True, stop=True), [mBKB[d], mBVU[d]], [uD], pemode=("full",))
                    yield
                    S.op("act", lambda: nc.scalar.copy(out=yacc[d][:, cs], in_=Yp), [uD], [mBy[d]])
                    S.op("act", lambda: nc.scalar.activation(out=tS[d], in_=Sd, func=AF.Identity, scale=SCs[d][:, 2, c:c + 1]), [uD, BSCl[d]], [mBtS[d]])
                    S.op("dve", lambda: A_.scalar_tensor_tensor(out=ST[d], in0=ST[d], scalar=SCs[d][:, 1, c:c + 1], in1=tS[d], op0=ALU.mult, op1=ALU.add), [mBST[d], mBtS[d], BSCl[d]], [mBST[d]])
                    if step + 1 < NCH:
                        cn = order[d][step + 1]
                        S.op("dve", lambda: A_.tensor_scalar(out=S0m[d], in0=ST[d], scalar1=SCs[d][:, 0, cn:cn + 1], scalar2=None, op0=ALU.mult), [mBST[d], BSCl[d]], [mBS0[d]])

                for step in range(NCH if dbgn is None else dbgn[1]):
                    gens = [dstep(d, step) for d in range(2)]
                    while gens:
                        for g_ in list(gens):
                            try:
                                next(g_)
                            except StopIteration:
                                gens.remove(g_)
            else:
                for ch in chains:
                    hd, d = ch
                    kp = slice(hd * 64, hd * 64 + 64)
                    S.op("pool", lambda: nc.gpsimd.tensor_copy(out=VU[ch][lo, :, :], in_=Vst[:, :, hd * 64:(hd + 1) * 64]), [BVst], [BVU[ch]])
                    S.op("dve", lambda: A_.memset(ST[d][kp, :], 0.0), [], [BST[ch]])
                    S.op("dve", lambda: A_.memset(S0m[d][kp, :], 0.0), [], [BS0[ch]])
                def chain_step(ch, step):
                    hd, d = ch
                    kp = slice(hd * 64, hd * 64 + 64)
                    c = order[d][step]
                    cs = slice(c * CH, (c + 1) * CH)
                    r_, br_ = R[ch], BR[ch]
                    M4 = self.masks[:, 0:128] if d == 0 else self.masks[:, 128:256]
                    mA = self.masks[up, 0:64] if d == 0 else self.masks[up, 128:192]
                    mN = self.masks[up, 128:192] if d == 0 else self.masks[up, 0:64]
                    S.op("pe", lambda: nc.tensor.transpose(out=r_["TR"], in_=KB[d][kp, c, :], identity=self.identb[kp, kp]), [BKBl[d]], br_["TR"], pemode=("T", hd))
                    S.op("act", lambda: nc.scalar.copy(out=KBtr[ch], in_=r_["TR"]), br_["TR"], [BKBtr[ch]])
                    S.op("pe", lambda: nc.tensor.matmul(r_["GA"][lo, :], lhsT=KB[d][kp, c, 0:64], rhs=AB[d][kp, c, :], start=True, stop=True), [BKBl[d], BABl[d]], br_["GAlo"], pemode=("g", hd))
                    S.op("pe", lambda: nc.tensor.matmul(r_["GA"][up, :], lhsT=KB[d][kp, c, 64:128], rhs=AB[d][kp, c, :], start=True, stop=True), [BKBl[d], BABl[d]], br_["GAup"], pemode=("g", hd))
                    S.op("pe", lambda: nc.tensor.matmul(r_["Nn"][up, :], lhsT=AB[d][kp, c, 0:64], rhs=KB[d][kp, c, 64:128], start=True, stop=True), [BKBl[d], BABl[d]], br_["Nn"], pemode=("g", hd))
                    yield
                    S.op("dve", lambda: A_.copy_predicated(out=GGb[ch], mask=mU(M4), data=r_["GA"]), br_["GA"], [BGG[ch]])
                    S.op("dve", lambda: A_.copy_predicated(out=AN0[ch][up, 0:64], mask=mU(mA), data=r_["GA"][up, 0:64]), br_["GAup"], [BAN0[ch]])
                    S.op("dve", lambda: A_.copy_predicated(out=AN0[ch][up, 64:128], mask=mU(mN), data=r_["Nn"][up, :]), br_["Nn"], [BAN0[ch]])
                    S.op("dve", lambda: A_.tensor_tensor(out=Xp[ch][0][up, :], in0=self.ident[up, up], in1=AN0[ch][up, 0:64], op=ALU.subtract), [BAN0[ch]], [BXp[ch][0]])
                    yield
                    cur, Bcur = AN0[ch], BAN0[ch]
                    xq = 0
                    for lv in range(1, 7):
                        nx, Bnx = ANp[ch][lv % 2], BANp[ch][lv % 2]
                        if lv <= 5:
                            if lv < 5:
                                S.op("pe", lambda: nc.tensor.matmul(r_["LV"][up, 0:64], lhsT=cur[up, 64:128], rhs=cur[up, 0:64], start=True, stop=True), [Bcur], br_["LV"], pemode=("f",))
                            S.op("pe", lambda: nc.tensor.matmul(r_["LV"][up, 64:128], lhsT=cur[up, 0:64], rhs=cur[up, 64:128], start=True, stop=True), [Bcur], br_["LV"], pemode=("f",))
                        if lv >= 2:
                            S.op("pe", lambda: nc.tensor.matmul(r_["XL"][up, :], lhsT=cur[up, 64:128], rhs=Xp[ch][xq][up, :], start=True, stop=True), [Bcur, BXp[ch][xq]], br_["XL"], pemode=("f",))
                        yield
                        if lv <= 5:
                            if lv < 5:
                                S.op("act", lambda: nc.scalar.copy(out=nx[up, :], in_=r_["LV"][up, :]), br_["LV"], [Bnx])
                            else:
                                S.op("act", lambda: nc.scalar.copy(out=nx[up, 64:128], in_=r_["LV"][up, 64:128]), br_["LV"], [Bnx])
                        if lv >= 2:
                            S.op("dve", lambda: A_.tensor_tensor(out=Xp[ch][1 - xq][up, :], in0=r_["XL"][up, :], in1=Xp[ch][xq][up, :], op=ALU.add), br_["XL"] + [BXp[ch][xq]], [BXp[ch][1 - xq]])
                            xq = 1 - xq
                        if lv <= 5:
                            cur, Bcur = nx, Bnx
                        if lv < 6:
                            yield
                    yield
                    S.op("pe", lambda: nc.tensor.matmul(r_["Wp"][up, :], lhsT=AB[d][kp, c, 0:64], rhs=S0m[d][kp, :], start=True, stop=False), [BABl[d], BS0[ch]], br_["Wp"], pemode=("g", hd))
                    S.op("pe", lambda: nc.tensor.matmul(r_["Wp"][up, :], lhsT=GGb[ch][lo, 0:64], rhs=VU[ch][lo, c, :], start=False, stop=True), [BGG[ch], BVU[ch]], br_["Wp"], pemode=("w2",))
                    yield
                    S.op("act", lambda: nc.scalar.copy(out=Wf[ch][up, :], in_=r_["Wp"][up, :]), br_["Wp"], [BWf[ch]])
                    yield
                    S.op("pe", lambda: nc.tensor.matmul(r_["Up"][up, :], lhsT=Xp[ch][xq][up, :], rhs=Wf[ch][up, :], start=True, stop=True), [BXp[ch][xq], BWf[ch]], br_["Up"], pemode=("f",))
                    yield
                    S.op("act", lambda: nc.scalar.activation(out=VU[ch][up, c, :], in_=r_["Up"][up, :], func=AF.Copy, scale=-1.0), br_["Up"], [BVU[ch]])
                    yield
                    S.op("pe", lambda: nc.tensor.matmul(r_["Yp"][kp, :], lhsT=S0m[d][kp, :], rhs=AB[d][kp, c, 64:128], start=True, stop=False), [BS0[ch], BABl[d]], br_["Yp"], pemode=("g", hd))
                    S.op("pe", lambda: nc.tensor.matmul(r_["Yp"][kp, :], lhsT=VU[ch][:, c, :], rhs=GGb[ch][:, 64:128], start=False, stop=True), [BVU[ch], BGG[ch]], br_["Yp"], pemode=("full",))
                    yield
                    S.op("act", lambda: nc.scalar.copy(out=yacc[d][kp, cs], in_=r_["Yp"][kp, :]), br_["Yp"], [By[ch]])
                    S.op("pe", lambda: nc.tensor.matmul(r_["Sd"][kp, :], lhsT=KBtr[ch], rhs=VU[ch][:, c, :], start=True, stop=True), [BKBtr[ch], BVU[ch]], br_["Sd"], pemode=("full",))
                    yield
                    S.op("act", lambda: nc.scalar.activation(out=tS[d][kp, :], in_=r_["Sd"][kp, :], func=AF.Identity, scale=SCs[d][kp, 2, c:c + 1]), br_["Sd"] + [BSCl[d]], [BtS[ch]])
                    S.op("dve", lambda: A_.scalar_tensor_tensor(out=ST[d][kp, :], in0=ST[d][kp, :], scalar=SCs[d][kp, 1, c:c + 1], in1=tS[d][kp, :], op0=ALU.mult, op1=ALU.add), [BST[ch], BtS[ch], BSCl[d]], [BST[ch]])
                    if step + 1 < NCH:
                        cn = order[d][step + 1]
                        S.op("dve", lambda: A_.tensor_scalar(out=S0m[d][kp, :], in0=ST[d][kp, :], scalar1=SCs[d][kp, 0, cn:cn + 1], scalar2=None, op0=ALU.mult), [BST[ch], BSCl[d]], [BS0[ch]])

                for step in range(NCH if dbgn is None else dbgn[1]):
                    gens = [chain_step(ch, step) for ch in chains]
                    if getattr(self, "rw_order", "phase") == "chain":
                        for g_ in gens:
                            for _ in g_:
                                pass
                        gens = []
                    while gens:
                        for g_ in list(gens):
                            try:
                                next(g_)
                            except StopIteration:
                                gens.remove(g_)
            if MERGE:
                RB = {0: [mUB[0][0]], 1: [mUB[0][1]]}
                By_all = [mBy[0], mBy[1]]
            else:
                RB = {0: [BR[chains[0]]["ALL"][0], BR[chains[0]]["ALL"][1]], 1: [BR[chains[0]]["ALL"][2], BR[chains[0]]["ALL"][3]]}
                By_all = [By[ch] for ch in chains]
            Byy = Buf()
            S.op("dve", lambda: A_.tensor_tensor(out=yacc[0], in0=yacc[0], in1=yacc[1], op=ALU.add), By_all, [Byy])
            NP_ = 6
            W_ = T // NP_
            for pc in range(NP_):
                sl_ = slice(pc * W_, (pc + 1) * W_)
                pb = pc % 2
                S.op("pe", lambda: nc.tensor.matmul(self.PS[pb][:, 0:W_], lhsT=self.blk64, rhs=yacc[0][:, sl_], start=True, stop=True), [Byy], RB[pb])
                S.op("dve", lambda: A_.scalar_tensor_tensor(out=t0_[:, sl_], in0=self.PS[pb][:, 0:W_], scalar=-1.0 / 64, in1=yacc[0][:, sl_], op0=ALU.mult, op1=ALU.add), RB[pb] + [Byy], [Bt0])
            S.op("act", lambda: nc.scalar.activation(out=t1_, in_=t0_, func=AF.Square), [Bt0], [Bt1])
            for pc in range(NP_):
                sl_ = slice(pc * W_, (pc + 1) * W_)
                pb = pc % 2
                S.op("pe", lambda: nc.tensor.matmul(self.PS[pb][:, 0:W_], lhsT=self.blk64, rhs=t1_[:, sl_], start=True, stop=True), [Bt1], RB[pb])
                S.op("act", lambda: nc.scalar.activation(out=yacc[1][:, sl_], in_=self.PS[pb][:, 0:W_], func=AF.Sqrt, scale=1.0 / 64, bias=epsLN), RB[pb] + [Bgl], [Byy])
            S.op("dve", lambda: A_.reciprocal(out=yacc[1], in_=yacc[1]), [Byy], [Byy])
            S.op("dve", lambda: A_.tensor_tensor(out=t0_, in0=t0_, in1=yacc[1], op=ALU.mult), [Bt0, Byy], [Bt0])
            S.op("act", lambda: nc.scalar.activation(out=t0_, in_=t0_, func=AF.Identity, scale=self.pv("rw_ln_w", p), bias=self.pv("rw_ln_b", p)), [Bt0], [Bt0])
            S.op("dve", lambda: A_.tensor_tensor(out=k0, in0=k0, in1=k1, op=ALU.add), [Bk0, Bk1], [Bk0])
            S.op("dve", lambda: A_.scalar_tensor_tensor(out=t1_, in0=rl, scalar=self.pv("rw_r_k", p), in1=k0, op0=ALU.mult, op1=ALU.mult), [Brl, Bk0, Bt1], [Bt1])
            for pc in range(NP_):
                sl_ = slice(pc * W_, (pc + 1) * W_)
                pb = pc % 2
                S.op("pe", lambda: nc.tensor.matmul(self.PS[pb][:, 0:W_], lhsT=self.blk64, rhs=t1_[:, sl_], start=True, stop=True), [Bt1], RB[pb])
                S.op("dve", lambda: A_.tensor_tensor(out=yacc[1][:, sl_], in0=self.PS[pb][:, 0:W_], in1=vf[:, sl_], op=ALU.mult), RB[pb] + [Bvf, Byy], [Byy])
            S.op("dve", lambda: A_.tensor_tensor(out=t0_, in0=t0_, in1=yacc[1], op=ALU.add), [Bt0, Byy], [Bt0])
            S.op("dve", lambda: A_.tensor_tensor(out=ogb, in0=t0_, in1=gg, op=ALU.mult), [Bt0, Bgg], [Bogb])
            S.dma("pool", og[rows, cols], ogb, reads=[Bogb])
            for b_ in By_all:
                b_.r.append(Byy.w)
    st.close()
    S.pe_selfwait = False
    S.pe_drain = 0


Prog.rwkv_scan = _rwkv_scan
```

```python
from contextlib import ExitStack
import numpy as np
import concourse.bass as bass
import concourse.mybir as mybir
from concourse.bass_utils import run_bass_kernel_spmd

F32 = mybir.dt.float32
BF16 = mybir.dt.bfloat16
AF = mybir.ActivationFunctionType
ALU = mybir.AluOpType

NCORES = 8
NB = 2
TC = 256
TL = 2048
T = TC + TL
TT = NB * T
D = 1024
DEPTH = 4
DFF = 2816
NFC = DFF // 128
BLK = 256
NBLK = T // BLK
EPS = 1e-6
CH = 64
NCH = T // CH
RW_LN_EPS = 64e-5
MLA_SCALE = 96 ** -0.5


class Buf:
    __slots__ = ("name", "w", "r")

    def __init__(self, name=""):
        self.name = name
        self.w = None
        self.r = []


class _Eng:
    def __init__(self, S, name, eng):
        self.S = S
        self.name = name
        self.eng = eng
        self.sem = None
        self.count = 0
        self.seen = {}
        self.nsem = 0
        self.ninst = 0
        self.own = set()

    def new_sem(self):
        self.sem = self.S.nc.alloc_semaphore(f"e_{self.name}_{self.nsem}")
        self.own.add(id(self.sem))
        self.nsem += 1
        self.count = 0

    def wait(self, ev):
        sem, val = ev
        k = id(sem)
        if self.name == "pe" and k in self.own and not self.S.pe_selfwait:
            return
        if self.seen.get(k, 0) >= val:
            return
        self.eng.wait_ge(sem, val)
        self.seen[k] = val


class Sched:
    EPOCH = 30000

    def __init__(self, nc, ndma_sems=48):
        self.nc = nc
        self.E = {}
        for name, eng in (("pe", nc.tensor), ("dve", nc.vector), ("act", nc.scalar),
                          ("pool", nc.gpsimd), ("sp", nc.sync)):
            e = _Eng(self, name, eng)
            e.new_sem()
            self.E[name] = e
        self.dsems = [[nc.alloc_semaphore(f"d{i}"), 0] for i in range(ndma_sems)]
        self.dnext = 0
        self._keep = []
        self.pe_selfwait = False
        self.pe_drain = 0
        self.last_pemode = None

    @staticmethod
    def _deps(reads, writes):
        deps = []
        for b in reads:
            if b.w is not None:
                deps.append(b.w)
        for b in writes:
            if b.w is not None:
                deps.append(b.w)
            deps.extend(b.r)
        return deps

    @staticmethod
    def _mark(ev, reads, writes):
        for b in writes:
            b.w = ev
            b.r = []
        for b in reads:
            if b not in writes:
                b.r.append(ev)
                if len(b.r) > 32:
                    b.r = b.r[-32:]

    def op(self, ename, fn, reads=(), writes=(), pemode=None):
        e = self.E[ename]
        for ev in self._deps(reads, writes):
            e.wait(ev)
        drain = False
        if ename == "pe":
            drain = self.pe_drain == 1 or (self.pe_drain == 2 and pemode != self.last_pemode)
            self.last_pemode = pemode
        if drain and e.count > 0:
            k = id(e.sem)
            if e.seen.get(k, 0) < e.count:
                e.eng.wait_ge(e.sem, e.count)
                e.seen[k] = e.count
        if e.count >= self.EPOCH:
            self._keep.append(e.sem)
            e.new_sem()
        inst = fn()
        e.count += 1
        e.ninst += 1
        inst.then_inc(e.sem, 1)
        ev = (e.sem, e.count)
        self._mark(ev, reads, writes)
        return ev

    def dma(self, qname, out, in_, reads=(), writes=(), **kw):
        q = self.E[qname]
        for ev in self._deps(reads, writes):
            q.wait(ev)
        slot = self.dsems[self.dnext % len(self.dsems)]
        self.dnext += 1
        if slot[1] >= self.EPOCH:
            self._keep.append(slot[0])
            slot[0] = self.nc.alloc_semaphore(f"dx{self.dnext}")
            slot[1] = 0
        if slot[1] > 0:
            q.wait((slot[0], slot[1]))
        q.eng.dma_start(out=out, in_=in_, **kw).then_inc(slot[0], 16)
        q.ninst += 1
        slot[1] += 16
        ev = (slot[0], slot[1])
        self._mark(ev, reads, writes)
        return ev

    def barrier(self):
        evs = [(e.sem, e.count) for e in self.E.values() if e.count > 0]
        evs += [(s[0], s[1]) for s in self.dsems if s[1] > 0]
        for e in self.E.values():
            for ev in evs:
                if ev[0] is e.sem:
                    continue
                e.wait(ev)


class PVec:
    def __init__(self):
        self.cols = []
        self.off = {}
        self.n = 0

    def add(self, name, vec):
        vec = np.asarray(vec, dtype=np.float32).reshape(-1)
        assert vec.size % 128 == 0
        nch = vec.size // 128
        self.off[name] = (self.n, nch)
        self.cols.append(np.ascontiguousarray(vec.reshape(nch, 128).T))
        self.n += nch

    def array(self):
        return np.ascontiguousarray(np.concatenate(self.cols, axis=1))


def pvec_layout(inputs):
    pv = PVec()
    for l in range(DEPTH):
        pv.add(f"b_mod{l}", inputs["b_mod"][l])
        pv.add(f"norm1_{l}", inputs["norm1"][l])
        pv.add(f"norm2_{l}", inputs["norm2"][l])
        for k in range(3):
            pv.add(f"conv{l}_{k}", inputs["ffn_conv"][l, k])
        pv.add(f"convb{l}", inputs["ffn_conv_b"][l])
    pv.add("norm_f", inputs["norm_f"])
    for d in range(2):
        for j in range(2):
            pv.add(f"hg_lb{d}_{j}", inputs["hg_lb"][d, j])
    for j in range(2):
        pv.add(f"hg_norm{j}", inputs["hg_norm"][j])
    for k in range(6):
        pv.add(f"rw_mu{k}", inputs["rw_mu"][0, k])
    for d in range(2):
        pv.add(f"rw_w0_{d}", inputs["rw_w0"][0, d])
        pv.add(f"rw_a0_{d}", inputs["rw_a0"][0, d])
    for nm in ("rw_k_k", "rw_k_a", "rw_r_k", "rw_ln_w", "rw_ln_b"):
        pv.add(nm, inputs[nm][0])
    pv.add("mla_q_norm", inputs["mla_q_norm"][0])
    pv.add("mla_kv_norm", inputs["mla_kv_norm"][0])
    return pv


def make_consts():
    c = {}
    c["ident"] = np.eye(128, dtype=np.float32)
    c["ones"] = np.ones((128, 128), dtype=np.float32)
    bo = np.zeros((128, 128), dtype=np.float32)
    bo[:64, :64] = 1.0
    bo[64:, 64:] = 1.0
    c["blk64"] = bo
    i = np.arange(64)[:, None]
    t = np.arange(64)[None, :]
    su = (i < t).astype(np.float32)
    iu = (i <= t).astype(np.float32)
    sl = (i > t).astype(np.float32)
    il = (i >= t).astype(np.float32)
    c["masks"] = np.concatenate([np.concatenate([su, iu, sl, il], axis=1)] * 2, axis=0)
    m = np.ones((128, T), dtype=np.float32)
    m[:, ::CH] = 0.0
    c["scanmask"] = m
    nq = 8
    inv_freq = (10000.0 ** (-np.arange(nq, dtype=np.float32) / nq)).astype(np.float32)
    pos = np.arange(TL)
    row = (pos // 64).astype(np.float32)
    col = (pos % 64).astype(np.float32)
    ang_r = row[:, None] * inv_freq
    ang_c = col[:, None] * inv_freq
    ang = np.concatenate([ang_r, ang_r, ang_c, ang_c], axis=-1).astype(np.float32)
    cos = np.ones((32, T), dtype=np.float32)
    sin = np.zeros((32, T), dtype=np.float32)
    cos[:, TC:] = np.cos(ang).T
    sin[:, TC:] = np.sin(ang).T
    c["rope_cos"] = cos
    c["rope_sin"] = sin
    return c


WEIGHT_NAMES = ["w_mod", "ffn_w_in", "ffn_w_out", "hg_w_in", "hg_w_o", "rw_w_rkv", "rw_w1", "rw_w2",
                "rw_a1", "rw_a2", "rw_g1", "rw_g2", "rw_w_o", "mla_w_dqkv", "mla_w_uq", "mla_w_ukv", "mla_w_o"]


class Stage:
    def __init__(self, P, name):
        self.P = P
        self.name = name
        self.es = ExitStack()
        P.nstage += 1
        self.k = 0

    def sb(self, name, shape, dt=F32):
        self.k += 1
        h = self.es.enter_context(self.P.nc.sbuf_tensor(f"{self.name}{self.P.nstage}_{name}_{self.k}", list(shape), dt))
        return h.ap()

    def close(self):
        self.P.S.barrier()
        self.es.close()


class Prog:
    def __init__(self, wshapes, pv_off, npv, dbg=(), xin_name=None):
        nc = bass.Bass("TRN2", target_bir_lowering=False)
        self.nc = nc
        self.dbg = set(dbg)
        self.pv_off = pv_off
        self.nstage = 0
        di = lambda n, s: nc.dram_tensor(n, list(s), F32, kind="ExternalInput").ap()
        self.x = di("x", [NB, TL, D])
        self.ctx = di("ctx", [NB, TC, D])
        self.cvec = di("cvec", [3, D])
        self.pvec_d = di("pvec", [128, npv])
        self.cd = {n: di("c_" + n, s) for n, s in (("ident", [128, 128]), ("ones", [128, 128]), ("blk64", [128, 128]),
                                                    ("masks", [128, 256]), ("scanmask", [128, T]),
                                                    ("rope_cos", [32, T]), ("rope_sin", [32, T]))}
        self.W = {n: di(n, wshapes[n]) for n in WEIGHT_NAMES}
        self.out = nc.dram_tensor("out", [NB, TL, D], F32, kind="ExternalOutput").ap()
        self.scratch = {}
        self.S = Sched(nc)
        S = self.S
        self.PS = [nc.alloc_psum_tensor(f"psb{i}", [128, 512], F32).ap() for i in range(8)]
        self.BPS = [Buf(f"ps{i}") for i in range(8)]
        g = lambda n, s, dt=F32: nc.alloc_sbuf_tensor("g_" + n, list(s), dt).ap()
        self.ident = g("ident", [128, 128])
        self.identb = g("identb", [128, 128], BF16)
        self.onesf = g("onesf", [128, 128])
        self.onesb = g("onesb", [128, 128], BF16)
        self.blk64 = g("blk64", [128, 128])
        self.masks = g("masks", [128, 256])
        self.pvec = g("pvec", [128, npv])
        self.MOD = g("MOD", [128, DEPTH, 48, 3])
        self.MA = g("MA", [128, DEPTH, 2, 8, 3])
        self.epsD = g("epsD", [128, 1])
        self.BC = Buf("consts")
        self.BMOD = Buf("mod")
        S.op("dve", lambda: nc.vector.memset(self.epsD, EPS), [], [self.BC])
        S.dma("sp", self.ident, self.cd["ident"], writes=[self.BC])
        b1, b2, b3, b4, b5, b6 = [Buf() for _ in range(6)]
        S.dma("sp", self.onesf, self.cd["ones"], writes=[b1])
        S.dma("sp", self.blk64, self.cd["blk64"], writes=[b2])
        S.dma("sp", self.masks, self.cd["masks"], writes=[b3])
        S.dma("sp", self.pvec, self.pvec_d, writes=[b4])
        S.dma("pool", self.identb, self.cd["ident"], writes=[b5])
        S.dma("pool", self.onesb, self.cd["ones"], writes=[b6])
        S.barrier()

    def scr(self, name, shape, dt=F32):
        if name not in self.scratch:
            kind = "ExternalOutput" if name in self.dbg else "Internal"
            self.scratch[name] = self.nc.dram_tensor("s_" + name, list(shape), dt, kind=kind).ap()
        return self.scratch[name]

    def pv(self, name, c=None):
        off, nch = self.pv_off[name]
        if c is None:
            return self.pvec[:, off:off + nch]
        return self.pvec[:, off + c:off + c + 1]

    def load_w(self, dst, src, bufs_cols, q="pool"):
        S = self.S
        n = dst.shape[2]
        v = src.rearrange("(kc p) n -> p kc n", p=128)
        bufs = []
        for n0 in range(0, n, 512):
            n1 = min(n, n0 + 512)
            b = Buf()
            S.dma(q, dst[:, :, n0:n1], v[:, :, n0:n1], writes=[b])
            bufs.append(b)
        return bufs

    def prologue_transpose(self, xT):
        nc, S = self.nc, self.S
        st = Stage(self, "pt")
        tin = [st.sb(f"tin{i}", [128, D]) for i in range(2)]
        tout = [st.sb(f"tout{i}", [128, 8, 128]) for i in range(2)]
        Bin = [Buf(), Buf()]
        Bout = [Buf(), Buf()]
        xTv = xT.rearrange("(c p) t -> p c t", p=128)
        tiles = []
        for b in range(NB):
            for k in range(T // 128):
                tiles.append((b, k))

        def src(b, k):
            t0 = k * 128
            if t0 < TC:
                return self.ctx[b, t0:t0 + 128, :]
            return self.x[b, t0 - TC:t0 - TC + 128, :]

        S.dma("sp", tin[0], src(*tiles[0]), writes=[Bin[0]])
        for n, (b, k) in enumerate(tiles):
            i = n % 2
            if n + 1 < len(tiles):
                S.dma("sp", tin[1 - i], src(*tiles[n + 1]), writes=[Bin[1 - i]])
            for hf in range(2):
                pb = 2 * (n % 2) + hf
                for c4 in range(4):
                    c = hf * 4 + c4
                    S.op("pe", lambda: nc.tensor.transpose(out=self.PS[pb][:, c4 * 128:(c4 + 1) * 128], in_=tin[i][:, c * 128:(c + 1) * 128], identity=self.ident),
                         [Bin[i]], [self.BPS[pb]])
                eng = "dve" if hf == 0 else "act"
                if hf == 0:
                    S.op("dve", lambda: nc.vector.tensor_copy(out=tout[i][:, 0:4, :], in_=self.PS[pb][:].rearrange("p (c t) -> p c t", c=4)), [self.BPS[pb]], [Bout[i]])
                else:
                    S.op("act", lambda: nc.scalar.copy(out=tout[i][:, 4:8, :], in_=self.PS[pb][:].rearrange("p (c t) -> p c t", c=4)), [self.BPS[pb]], [Bout[i]])
            col = b * T + k * 128
            S.dma("pool", xTv[:, :, col:col + 128], tout[i], reads=[Bout[i]])
        st.close()

    def prologue_mod(self):
        nc, S = self.nc, self.S
        st = Stage(self, "pm")
        cv = st.sb("cv", [3, D])
        sc = st.sb("sc", [3, D])
        scT = st.sb("scT", [128, 8, 3])
        Bcv, Bsc, BscT = Buf(), Buf(), Buf()
        S.dma("sp", cv, self.cvec, writes=[Bcv])
        S.op("act", lambda: nc.scalar.activation(out=sc, in_=cv, func=AF.Silu), [Bcv], [Bsc])
        for kc in range(8):
            S.op("pe", lambda: nc.tensor.transpose(out=self.PS[0][:, kc * 4:kc * 4 + 3], in_=sc[0:3, kc * 128:(kc + 1) * 128], identity=self.ident[0:3, 0:3]),
                 [Bsc], [self.BPS[0]])
        S.op("dve", lambda: nc.vector.tensor_copy(out=scT, in_=self.PS[0][:, 0:32].rearrange("p (k f) -> p k f", f=4)[:, :, 0:3]), [self.BPS[0]], [BscT])
        NWB = 4
        wt = [st.sb(f"wt{i}", [128, 8, 512]) for i in range(NWB)]
        Bwt = [Buf() for _ in range(NWB)]
        groups = [(l, g) for l in range(DEPTH) for g in range(12)]

        def wsrc(l, g):
            return self.W["w_mod"][l].rearrange("(kc p) n -> p kc n", p=128)[:, :, g * 512:(g + 1) * 512]

        def wload(n):
            S.dma("sp" if n % 2 == 0 else "act", wt[n % NWB], wsrc(*groups[n]), writes=[Bwt[n % NWB]])

        for n in range(NWB - 1):
            wload(n)
        for n, (l, g) in enumerate(groups):
            i = n % NWB
            if n + NWB - 1 < len(groups):
                wload(n + NWB - 1)
            pb = 1 + (n % 2)
            for oc in range(4):
                for kc in range(8):
                    S.op("pe", lambda: nc.tensor.matmul(self.PS[pb][:, oc * 4:oc * 4 + 3], lhsT=wt[i][:, kc, oc * 128:(oc + 1) * 128], rhs=scT[:, kc, :], start=(kc == 0), stop=(kc == 7)),
                         [Bwt[i], BscT], [self.BPS[pb]])
            boff, _ = self.pv_off[f"b_mod{l}"]
            bias = self.pvec[:, boff + g * 4:boff + g * 4 + 4].unsqueeze(2).to_broadcast([128, 4, 3])
            S.op("dve", lambda: nc.vector.tensor_tensor(out=self.MOD[:, l, g * 4:(g + 1) * 4, :], in0=self.PS[pb][:, 0:16].rearrange("p (o f) -> p o f", f=4)[:, :, 0:3], in1=bias, op=ALU.add),
                 [self.BPS[pb]], [self.BMOD])
        for l in range(DEPTH):
            for w in range(2):
                sc_idx = 8 if w == 0 else 32
                nrm = self.pv(f"norm{w + 1}_{l}").unsqueeze(2).to_broadcast([128, 8, 3])
                S.op("dve", lambda: nc.vector.scalar_tensor_tensor(out=self.MA[:, l, w, :, :], in0=self.MOD[:, l, sc_idx:sc_idx + 8, :], scalar=1.0, in1=nrm, op0=ALU.add, op1=ALU.mult),
                     [self.BMOD], [self.BMOD])
        st.close()

    def norm_tiles(self, st, n=BLK + 2):
        return dict(sq=st.sb("nsq", [128, 8, n], BF16), tmp=st.sb("ntmp", [128, 8, n]), r0=st.sb("nr0", [128, n]), r1=st.sb("nr1", [128, n]),
                    B=[Buf() for _ in range(4)])

    def norm_block(self, nt, xs, Bxs, n, A, Bsh, hb, Bhb, bank):
        nc, S = self.nc, self.S
        sq, tmp, r0, r1 = nt["sq"], nt["tmp"], nt["r0"], nt["r1"]
        Bsq, Btmp, Br0, Br1 = nt["B"]
        S.op("act", lambda: nc.scalar.activation(out=sq[:, :, :n], in_=xs, func=AF.Square), [Bxs], [Bsq])
        ps = self.PS[bank]
        for c in range(8):
            S.op("pe", lambda: nc.tensor.matmul(ps[:, :n], lhsT=self.onesb, rhs=sq[:, c, :n], start=(c == 0), stop=(c == 7)), [Bsq], [self.BPS[bank]])
        S.op("act", lambda: nc.scalar.activation(out=r0[:, :n], in_=ps[:, :n], func=AF.Sqrt, scale=1.0 / D, bias=self.epsD), [self.BPS[bank]], [Br0])
        S.op("dve", lambda: nc.vector.reciprocal(out=r1[:, :n], in_=r0[:, :n]), [Br0], [Br1])
        S.op("dve", lambda: nc.vector.tensor_tensor(out=tmp[:, :, :n], in0=xs, in1=r1[:, :n].unsqueeze(1).to_broadcast([128, 8, n]), op=ALU.mult), [Bxs, Br1], [Btmp])
        for c in range(8):
            S.op("act", lambda: nc.scalar.activation(out=hb[:, c, :n], in_=tmp[:, c, :n], func=AF.Identity, scale=A[:, c:c + 1], bias=(Bsh[:, c:c + 1] if Bsh is not None else 0.0)),
                 [Btmp, self.BMOD], [Bhb])

    def mod_ab(self, l, w, j):
        A = self.MA[:, l, w, :, j]
        sh = self.MOD[:, l, (0 if w == 0 else 24):(8 if w == 0 else 32), j]
        gt = self.MOD[:, l, (16 if w == 0 else 40):(24 if w == 0 else 48), j]
        return A, sh, gt

    @staticmethod
    def blocks(skip_ctx=False):
        out = []
        for b in range(NB):
            for k in range(NBLK):
                if skip_ctx and k == 0:
                    continue
                out.append((b, k))
        return out

    @staticmethod
    def blk_range(k):
        seq0, seq1 = (0, TC) if k == 0 else (TC, T)
        t0 = k * BLK
        lo = max(t0 - 1, seq0)
        hi = min(t0 + BLK + 1, seq1)
        return t0, lo, hi, (t0 == seq0), (t0 + BLK == seq1)

    def ffn_stage(self, l, xin, xout, skip_ctx):
        nc, S = self.nc, self.S
        st = Stage(self, "ffn")
        Win = st.sb("win", [128, 8, 2 * DFF], BF16)
        Wout = st.sb("wout", [128, NFC, D], BF16)
        BWin = self.load_w(Win, self.W["ffn_w_in"][l], None)
        BWout = []
        osrc = self.W["ffn_w_out"][l].rearrange("(fc p) n -> p fc n", p=128)
        for f0 in range(0, NFC, 2):
            b = Buf()
            S.dma("pool", Wout[:, f0:f0 + 2, :], osrc[:, f0:f0 + 2, :], writes=[b])
            BWout.append(b)
        NH = BLK + 2
        xs = [st.sb(f"xs{i}", [128, 8, NH]) for i in range(2)]
        hb = [st.sb(f"hb{i}", [128, 8, NH], BF16) for i in range(2)]
        gt_ = [st.sb(f"g{i}", [128, NFC, BLK], BF16) for i in range(2)]
        cv = [st.sb(f"cv{i}", [128, BLK]) for i in range(2)]
        sl = [st.sb(f"sl{i}", [128, BLK]) for i in range(2)]
        Bxs, Bhb, Bg, Bcv, Bsl = [[Buf(), Buf()] for _ in range(5)]
        nt = self.norm_tiles(st)
        for i in range(2):
            S.op("dve", lambda: nc.vector.memset(xs[i], 0.0), [], [Bxs[i]])
        xiv = xin.rearrange("(c p) t -> p c t", p=128)
        xov = xout.rearrange("(c p) t -> p c t", p=128)
        blocks = self.blocks(skip_ctx)

        def load(n):
            b, k = blocks[n]
            t0, lo, hi, _, _ = self.blk_range(k)
            S.dma("sp", xs[n % 2][:, :, lo - (t0 - 1):hi - (t0 - 1)], xiv[:, :, b * T + lo:b * T + hi], writes=[Bxs[n % 2]])

        load(0)
        for n, (b, k) in enumerate(blocks):
            i = n % 2
            if n + 1 < len(blocks):
                load(n + 1)
            t0, lo, hi, first, last = self.blk_range(k)
            j = 2 if k == 0 else b
            A, sh, gate = self.mod_ab(l, 1, j)
            self.norm_block(nt, xs[i], Bxs[i], NH, A, sh, hb[i], Bhb[i], 6)
            for fc in range(NFC):
                q = fc % 2
                pa, pvv = self.PS[q], self.PS[2 + q]
                ga = BWin[(fc * 128) // 512]
                gv = BWin[(DFF + fc * 128) // 512]
                for kc in range(8):
                    S.op("pe", lambda: nc.tensor.matmul(pa[:, :NH], lhsT=Win[:, kc, fc * 128:(fc + 1) * 128], rhs=hb[i][:, kc, :], start=(kc == 0), stop=(kc == 7)),
                         [ga, Bhb[i]], [self.BPS[q]])
                for kc in range(8):
                    S.op("pe", lambda: nc.tensor.matmul(pvv[:, :BLK], lhsT=Win[:, kc, DFF + fc * 128:DFF + (fc + 1) * 128], rhs=hb[i][:, kc, 1:1 + BLK], start=(kc == 0), stop=(kc == 7)),
                         [gv, Bhb[i]], [self.BPS[2 + q]])
                w0, w1, w2, cb = self.pv(f"conv{l}_0", fc), self.pv(f"conv{l}_1", fc), self.pv(f"conv{l}_2", fc), self.pv(f"convb{l}", fc)
                S.op("act", lambda: nc.scalar.activation(out=cv[q], in_=pa[:, 1:1 + BLK], func=AF.Identity, scale=w1, bias=cb), [self.BPS[q]], [Bcv[q]])
                c0 = 1 if first else 0
                S.op("dve", lambda: nc.vector.scalar_tensor_tensor(out=cv[q][:, c0:BLK], in0=pa[:, c0:BLK], scalar=w0, in1=cv[q][:, c0:BLK], op0=ALU.mult, op1=ALU.add),
                     [self.BPS[q], Bcv[q]], [Bcv[q]])
                c1 = BLK - 1 if last else BLK
                S.op("dve", lambda: nc.vector.scalar_tensor_tensor(out=cv[q][:, 0:c1], in0=pa[:, 2:2 + c1], scalar=w2, in1=cv[q][:, 0:c1], op0=ALU.mult, op1=ALU.add),
                     [self.BPS[q], Bcv[q]], [Bcv[q]])
                S.op("act", lambda: nc.scalar.activation(out=sl[q], in_=cv[q], func=AF.Silu), [Bcv[q]], [Bsl[q]])
                S.op("dve", lambda: nc.vector.tensor_tensor(out=gt_[i][:, fc, :], in0=sl[q], in1=pvv[:, :BLK], op=ALU.mult), [Bsl[q], self.BPS[2 + q]], [Bg[i]])
            for oc in range(8):
                q = 4 + oc % 2
                po = self.PS[q]
                for fc in range(NFC):
                    S.op("pe", lambda: nc.tensor.matmul(po[:, :BLK], lhsT=Wout[:, fc, oc * 128:(oc + 1) * 128], rhs=gt_[i][:, fc, :], start=(fc == 0), stop=(fc == NFC - 1)),
                         [BWout[fc // 2], Bg[i]], [self.BPS[q]])
                S.op("dve", lambda: nc.vector.scalar_tensor_tensor(out=xs[i][:, oc, 1:1 + BLK], in0=po[:, :BLK], scalar=gate[:, oc:oc + 1], in1=xs[i][:, oc, 1:1 + BLK], op0=ALU.mult, op1=ALU.add),
                     [self.BPS[q], Bxs[i], self.BMOD], [Bxs[i]])
            S.dma("pool", xov[:, :, b * T + t0:b * T + t0 + BLK], xs[i][:, :, 1:1 + BLK], reads=[Bxs[i]])
        st.close()

    def final_stage(self, xin):
        nc, S = self.nc, self.S
        st = Stage(self, "fin")
        xs = [st.sb(f"xs{i}", [128, 8, BLK]) for i in range(2)]
        hb = [st.sb(f"hb{i}", [128, 8, BLK]) for i in range(2)]
        ot = [st.sb(f"ot{i}", [128, D]) for i in range(2)]
        Bxs, Bhb, Bot = [[Buf(), Buf()] for _ in range(3)]
        nt = self.norm_tiles(st, BLK)
        xiv = xin.rearrange("(c p) t -> p c t", p=128)
        blocks = self.blocks(True)
        A = self.pv("norm_f")

        def load(n):
            b, k = blocks[n]
            S.dma("sp", xs[n % 2], xiv[:, :, b * T + k * BLK:b * T + (k + 1) * BLK], writes=[Bxs[n % 2]])

        load(0)
        nt_i = 0
        for n, (b, k) in enumerate(blocks):
            i = n % 2
            if n + 1 < len(blocks):
                load(n + 1)
            self.norm_block(nt, xs[i], Bxs[i], BLK, A, None, hb[i], Bhb[i], 6)
            for tt in range(2):
                o = nt_i % 2
                nt_i += 1
                for hf in range(2):
                    pb = 2 * o + hf
                    for c4 in range(4):
                        c = hf * 4 + c4
                        S.op("pe", lambda: nc.tensor.transpose(out=self.PS[pb][:, c4 * 128:(c4 + 1) * 128], in_=hb[i][:, c, tt * 128:(tt + 1) * 128], identity=self.ident),
                             [Bhb[i]], [self.BPS[pb]])
                    if hf == 0:
                        S.op("dve", lambda: nc.vector.tensor_copy(out=ot[o][:, 0:512], in_=self.PS[pb]), [self.BPS[pb]], [Bot[o]])
                    else:
                        S.op("act", lambda: nc.scalar.copy(out=ot[o][:, 512:1024], in_=self.PS[pb]), [self.BPS[pb]], [Bot[o]])
                tl = k * BLK - TC + tt * 128
                S.dma("pool", self.out[b, tl:tl + 128, :], ot[o], reads=[Bot[o]])
        st.close()


def build_program(wshapes, pv_off, npv, plan=None, dbg=()):
    P = Prog(wshapes, pv_off, npv, dbg=dbg)
    xa = P.scr("xA", [D, TT])
    xb = P.scr("xB", [D, TT])
    if plan is None:
        plan = ["tr", "mod"]
        for l in range(DEPTH):
            plan += [f"mix{l}", f"ffn{l}"]
        plan += ["final"]
    cur, nxt = xa, xb
    for step in plan:
        if step == "tr":
            P.prologue_transpose(cur)
        elif step == "mod":
            P.prologue_mod()
        elif step.startswith("mix"):
            l = int(step[3:])
            P.mixer(l, cur, nxt)
            cur, nxt = nxt, cur
        elif step.startswith("ffn"):
            l = int(step[3:])
            P.ffn_stage(l, cur, nxt, skip_ctx=(l == DEPTH - 1))
            cur, nxt = nxt, cur
        elif step == "final":
            P.final_stage(cur)
    P.S.barrier()
    return P


def prep_inputs(inputs, cores=range(NCORES)):
    pv = pvec_layout(inputs)
    pva = pv.array()
    consts = make_consts()
    shared = {"pvec": pva}
    for k, v in consts.items():
        shared["c_" + k] = v
    for n in WEIGHT_NAMES:
        shared[n] = np.ascontiguousarray(inputs[n], dtype=np.float32)
    in_maps = []
    for c in cores:
        m = dict(shared)
        m["x"] = np.ascontiguousarray(inputs["x"][NB * c:NB * (c + 1)], dtype=np.float32)
        m["ctx"] = np.ascontiguousarray(inputs["ctx"][NB * c:NB * (c + 1)], dtype=np.float32)
        m["cvec"] = np.ascontiguousarray(np.concatenate([inputs["c"][NB * c:NB * (c + 1)], inputs["c_ctx"][None, :]], axis=0), dtype=np.float32)
        in_maps.append(m)
    wshapes = {n: list(inputs[n].shape) for n in WEIGHT_NAMES}
    return in_maps, wshapes, pv.off, pva.shape[1]


def kernel(**inputs):
    inputs = {k: np.asarray(v) for k, v in inputs.items()}
    in_maps, wshapes, pv_off, npv = prep_inputs(inputs)
    P = build_program(wshapes, pv_off, npv)
    res = run_bass_kernel_spmd(P.nc, in_maps, core_ids=list(range(NCORES)))
    out = np.concatenate([np.asarray(r["out"]) for r in res.results], axis=0)
    return out.astype(np.float32)


def _inproj_stage(self, l, xin, Wd, N, dst_fm, tm_specs, f32_h=False):
    nc, S = self.nc, self.S
    st = Stage(self, "ip")
    Wt = st.sb("w", [128, 8, N], BF16)
    BW = self.load_w(Wt, Wd, None)
    xs = [st.sb(f"xs{i}", [128, 8, BLK]) for i in range(2)]
    hb = [st.sb(f"hb{i}", [128, 8, BLK], BF16) for i in range(2)]
    sg = [st.sb(f"sg{i}", [128, 8, BLK]) for i in range(2)]
    tmw = max([nc_ for (_, nc_, _) in tm_specs], default=0)
    tms = [st.sb(f"tm{i}", [128, max(tmw, 1)], BF16) for i in range(2)]
    Bxs, Bhb, Bsg, Btm = [[Buf(), Buf()] for _ in range(4)]
    nt = self.norm_tiles(st, BLK)
    xiv = xin.rearrange("(c p) t -> p c t", p=128)
    dv = dst_fm.rearrange("(c p) t -> p c t", p=128)
    blocks = self.blocks(False)

    def load(n):
        b, k = blocks[n]
        S.dma("sp", xs[n % 2], xiv[:, :, b * T + k * BLK:b * T + (k + 1) * BLK], writes=[Bxs[n % 2]])

    load(0)
    sgi = 0
    tmi = 0
    pbank = 0
    for n, (b, k) in enumerate(blocks):
        i = n % 2
        if n + 1 < len(blocks):
            load(n + 1)
        j = 2 if k == 0 else b
        A, sh, _ = self.mod_ab(l, 0, j)
        self.norm_block(nt, xs[i], Bxs[i], BLK, A, sh, hb[i], Bhb[i], 6)
        col = b * T + k * BLK
        for og in range(N // 1024):
            s_ = sgi % 2
            sgi += 1
            for o8 in range(8):
                oc = og * 8 + o8
                pb = pbank % 4
                pbank += 1
                for kc in range(8):
                    S.op("pe", lambda: nc.tensor.matmul(self.PS[pb][:, :BLK], lhsT=Wt[:, kc, oc * 128:(oc + 1) * 128], rhs=hb[i][:, kc, :], start=(kc == 0), stop=(kc == 7)),
                         [BW[(oc * 128) // 512], Bhb[i]], [self.BPS[pb]])
                if o8 % 2 == 0:
                    S.op("act", lambda: nc.scalar.copy(out=sg[s_][:, o8, :], in_=self.PS[pb][:, :BLK]), [self.BPS[pb]], [Bsg[s_]])
                else:
                    S.op("dve", lambda: nc.vector.tensor_copy(out=sg[s_][:, o8, :], in_=self.PS[pb][:, :BLK]), [self.BPS[pb]], [Bsg[s_]])
            S.dma("pool", dv[:, og * 8:(og + 1) * 8, col:col + BLK], sg[s_], reads=[Bsg[s_]])
        for (c0, ncols, dst_tm) in tm_specs:
            for tt in range(BLK // 128):
                s_ = tmi % 2
                tmi += 1
                for n0 in range(0, ncols, 512):
                    pb = 4 + (pbank % 2)
                    pbank += 1
                    for kc in range(8):
                        S.op("pe", lambda: nc.tensor.matmul(self.PS[pb][:, :512], lhsT=hb[i][:, kc, tt * 128:(tt + 1) * 128], rhs=Wt[:, kc, c0 + n0:c0 + n0 + 512], start=(kc == 0), stop=(kc == 7)),
                             [BW[(c0 + n0) // 512], Bhb[i]], [self.BPS[pb]])
                    S.op("act", lambda: nc.scalar.copy(out=tms[s_][:, n0:n0 + 512], in_=self.PS[pb][:, :512]), [self.BPS[pb]], [Btm[s_]])
                S.dma("pool", dst_tm[col + tt * 128:col + (tt + 1) * 128, :], tms[s_][:, :ncols], reads=[Btm[s_]])
    st.close()


def _outproj_stage(self, l, og, Wd, xin, xout, skip_ctx):
    nc, S = self.nc, self.S
    st = Stage(self, "op")
    Wt = st.sb("w", [128, 8, D], BF16)
    BW = self.load_w(Wt, Wd, None)
    xs = [st.sb(f"xs{i}", [128, 8, BLK]) for i in range(2)]
    ob = [st.sb(f"ob{i}", [128, 8, BLK], BF16) for i in range(2)]
    Bxs, Bob = [[Buf(), Buf()] for _ in range(2)]
    xiv = xin.rearrange("(c p) t -> p c t", p=128)
    xov = xout.rearrange("(c p) t -> p c t", p=128)
    ogv = og.rearrange("(c p) t -> p c t", p=128)
    blocks = self.blocks(skip_ctx)

    def load(n):
        b, k = blocks[n]
        col = b * T + k * BLK
        S.dma("sp", xs[n % 2], xiv[:, :, col:col + BLK], writes=[Bxs[n % 2]])
        S.dma("sp", ob[n % 2], ogv[:, :, col:col + BLK], writes=[Bob[n % 2]])

    load(0)
    for n, (b, k) in enumerate(blocks):
        i = n % 2
        if n + 1 < len(blocks):
            load(n + 1)
        j = 2 if k == 0 else b
        _, _, gate = self.mod_ab(l, 0, j)
        for oc in range(8):
            pb = oc % 4
            for kc in range(8):
                S.op("pe", lambda: nc.tensor.matmul(self.PS[pb][:, :BLK], lhsT=Wt[:, kc, oc * 128:(oc + 1) * 128], rhs=ob[i][:, kc, :], start=(kc == 0), stop=(kc == 7)),
                     [BW[(oc * 128) // 512], Bob[i]], [self.BPS[pb]])
            S.op("dve", lambda: nc.vector.scalar_tensor_tensor(out=xs[i][:, oc, :], in0=self.PS[pb][:, :BLK], scalar=gate[:, oc:oc + 1], in1=xs[i][:, oc, :], op0=ALU.mult, op1=ALU.add),
                 [self.BPS[pb], Bxs[i], self.BMOD], [Bxs[i]])
        col = b * T + k * BLK
        S.dma("pool", xov[:, :, col:col + BLK], xs[i], reads=[Bxs[i]])
    st.close()


def _hgrn2_scan(self, jh, Pfm, Itm, og):
    nc, S = self.nc, self.S
    st = Stage(self, "hs")
    A_ = nc.vector
    LB = st.sb("LB", [128, 2, 8])
    OML = st.sb("OML", [128, 2, 8])
    e0 = st.sb("e0", [128, 8]); e1 = st.sb("e1", [128, 8]); rr = st.sb("rr", [128, 8]); p0 = st.sb("p0", [128, 8]); p1 = st.sb("p1", [128, 8])
    BL = Buf()
    for d in range(2):
        S.op("act", lambda: nc.scalar.activation(out=e0, in_=self.pv(f"hg_lb{d}_0"), func=AF.Exp), [], [BL])
        S.op("act", lambda: nc.scalar.activation(out=e1, in_=self.pv(f"hg_lb{d}_1"), func=AF.Exp), [BL], [BL])
        S.op("dve", lambda: A_.tensor_tensor(out=rr, in0=e0, in1=e1, op=ALU.add), [BL], [BL])
        S.op("dve", lambda: A_.reciprocal(out=rr, in_=rr), [BL], [BL])
        S.op("dve", lambda: A_.tensor_tensor(out=p0, in0=e0, in1=rr, op=ALU.mult), [BL], [BL])
        S.op("dve", lambda: A_.tensor_tensor(out=p1, in0=e1, in1=rr, op=ALU.mult), [BL], [BL])
        if jh == 1:
            S.op("dve", lambda: A_.tensor_tensor(out=p1, in0=p0, in1=p1, op=ALU.add), [BL], [BL])
        else:
            S.op("dve", lambda: A_.tensor_copy(out=p1, in_=p0), [BL], [BL])
        S.op("dve", lambda: A_.tensor_tensor(out=LB[:, d, :], in0=p1, in1=p0, op=ALU.subtract), [BL], [BL])
        S.op("dve", lambda: A_.tensor_scalar(out=OML[:, d, :], in0=LB[:, d, :], scalar1=-1.0, scalar2=1.0, op0=ALU.mult, op1=ALU.add), [BL], [BL])
    smask = st.sb("smask", [128, T])
    Bsm = Buf()
    S.dma("sp", smask, self.cd["scanmask"], writes=[Bsm])
    f32t = lambda n: st.sb(n, [128, T])
    qs = f32t("qs"); graw = f32t("graw"); kk = f32t("kk"); ep = f32t("ep"); en = f32t("en")
    z = [f32t("z0"), f32t("z1")]; bb = [f32t("b0"), f32t("b1")]; of = [f32t("of0"), f32t("of1")]
    qt = [st.sb(f"qt{d}", [128, T], BF16) for d in range(2)]
    kh = [st.sb(f"kh{d}", [128, T], BF16) for d in range(2)]
    sqb = st.sb("sqb", [128, T], BF16)
    ogb = st.sb("ogb", [128, T], BF16)
    Vt = st.sb("Vt", [64, NCH, 128], BF16)
    emid = [st.sb(f"emid{d}", [128, NCH]) for d in range(2)]
    eend = [st.sb(f"eend{d}", [128, NCH]) for d in range(2)]
    eem = [st.sb(f"eem{d}", [128, NCH]) for d in range(2)]
    Sst = [st.sb(f"S{d}", [128, 128]) for d in range(2)]
    Sm = [st.sb(f"Sm{d}", [128, 128], BF16) for d in range(2)]
    tmpS = [st.sb(f"tS{d}", [128, 128]) for d in range(2)]
    khT = [st.sb(f"khT{d}", [64, 128], BF16) for d in range(2)]
    att = [st.sb(f"att{d}", [64, 64], BF16) for d in range(2)]
    Bqs, Bgr, Bkk, Bep, Ben, Bsq, Bog, BVt = [Buf() for _ in range(8)]
    Bz, Bbb, Bof, Bqt, Bkh, Bes, BS, BSm, BtS, BkT, Batt = [[Buf(), Buf()] for _ in range(11)]
    PSb = [self.PS[i].bitcast(BF16) for i in range(8)]
    for d in range(2):
        S.op("dve", lambda: A_.memset(att[d], 0.0), [], [Batt[d]])
    cf = list(range(NCH))
    cb = list(range(TC // CH - 1, -1, -1)) + list(range(NCH - 1, TC // CH - 1, -1))
    order = [cf, cb]
    for b in range(NB):
        for h in range(8):
            rows = slice(h * 128, (h + 1) * 128)
            cols = slice(b * T, (b + 1) * T)
            S.dma("sp", qs, Pfm[0 * D + h * 128:0 * D + (h + 1) * 128, cols], writes=[Bqs])
            S.dma("sp", z[0], Pfm[3 * D + h * 128:3 * D + (h + 1) * 128, cols], writes=[Bz[0]])
            S.dma("sp", z[1], Pfm[4 * D + h * 128:4 * D + (h + 1) * 128, cols], writes=[Bz[1]])
            S.dma("sp", graw, Pfm[2 * D + h * 128:2 * D + (h + 1) * 128, cols], writes=[Bgr])
            S.dma("sp", Vt, Itm[cols, rows].rearrange("(c s) v -> s c v", s=CH), writes=[BVt])
            S.op("act", lambda: nc.scalar.activation(out=qs, in_=qs, func=AF.Silu), [Bqs], [Bqs])
            for d in range(2):
                m_idx = 32 if d == 0 else 31
                zt = z[d]
                S.op("act", lambda: nc.scalar.activation(out=zt, in_=zt, func=AF.Sigmoid), [Bz[d]], [Bz[d]])
                S.op("act", lambda: nc.scalar.activation(out=zt, in_=zt, func=AF.Identity, scale=OML[:, d, h:h + 1], bias=LB[:, d, h:h + 1]), [Bz[d], BL], [Bz[d]])
                S.op("act", lambda: nc.scalar.activation(out=kk, in_=zt, func=AF.Identity, scale=-1.0, bias=self.onesf[:, 0:1]), [Bz[d]], [Bkk])
                S.op("act", lambda: nc.scalar.activation(out=zt, in_=zt, func=AF.Ln), [Bz[d]], [Bz[d]])
                S.op("dve", lambda: A_.tensor_tensor_scan(out=bb[d], data0=smask, data1=zt, initial=0.0, op0=ALU.mult, op1=ALU.add), [Bsm, Bz[d]], [Bbb[d]])
                b3 = bb[d].rearrange("p (c s) -> p c s", s=CH)
                if d == 1:
                    S.op("dve", lambda: A_.tensor_tensor(out=zt, in0=zt, in1=bb[d], op=ALU.subtract), [Bz[d], Bbb[d]], [Bz[d]])
                    S.op("dve", lambda: A_.tensor_tensor(out=ep.rearrange("p (c s) -> p c s", s=CH), in0=zt.rearrange("p (c s) -> p c s", s=CH),
                                                          in1=b3[:, :, CH - 1:CH].to_broadcast([128, NCH, CH]), op=ALU.add), [Bz[d], Bbb[d]], [Bep])
                    S.op("dve", lambda: A_.tensor_copy(out=bb[d], in_=ep), [Bep], [Bbb[d]])
                e_idx = CH - 1 if d == 0 else 0
                S.op("act", lambda: nc.scalar.activation(out=emid[d], in_=b3[:, :, m_idx], func=AF.Exp), [Bbb[d]], [Bes[d]])
                S.op("act", lambda: nc.scalar.activation(out=eend[d], in_=b3[:, :, e_idx], func=AF.Exp), [Bbb[d]], [Bes[d]])
                S.op("dve", lambda: A_.tensor_tensor(out=eem[d], in0=b3[:, :, e_idx], in1=b3[:, :, m_idx], op=ALU.subtract), [Bbb[d]], [Bes[d]])
                S.op("act", lambda: nc.scalar.activation(out=eem[d], in_=eem[d], func=AF.Exp), [Bes[d]], [Bes[d]])
                S.op("dve", lambda: A_.tensor_tensor(out=ep.rearrange("p (c s) -> p c s", s=CH), in0=b3, in1=b3[:, :, m_idx:m_idx + 1].to_broadcast([128, NCH, CH]), op=ALU.subtract),
                     [Bbb[d]], [Bep])
                S.op("act", lambda: nc.scalar.activation(out=en, in_=ep, func=AF.Exp, scale=-1.0), [Bep], [Ben])
                S.op("act", lambda: nc.scalar.activation(out=ep, in_=ep, func=AF.Exp), [Bep], [Bep])
                S.op("dve", lambda: A_.tensor_tensor(out=qt[d], in0=qs, in1=ep, op=ALU.mult), [Bqs, Bep], [Bqt[d]])
                S.op("dve", lambda: A_.tensor_tensor(out=kh[d], in0=kk, in1=en, op=ALU.mult), [Bkk, Ben], [Bkh[d]])
                S.op("dve", lambda: A_.memset(Sst[d], 0.0), [], [BS[d]])
                S.op("dve", lambda: A_.memset(Sm[d], 0.0), [], [BSm[d]])
            def hstep(d, step):
                c = order[d][step]
                cs = slice(c * CH, (c + 1) * CH)
                pb = d * 4
                mk = (self.masks[0:64, 64:128] if d == 0 else self.masks[0:64, 192:256]).bitcast(mybir.dt.uint32)
                S.op("pe", lambda: nc.tensor.transpose(out=PSb[pb][0:64, 0:128], in_=kh[d][:, cs], identity=self.identb), [Bkh[d]], [self.BPS[pb]])
                S.op("pe", lambda: nc.tensor.matmul(self.PS[pb + 1][0:64, 0:64], lhsT=kh[d][:, cs], rhs=qt[d][:, cs], start=True, stop=True), [Bkh[d], Bqt[d]], [self.BPS[pb + 1]])
                yield
                S.op("act", lambda: nc.scalar.copy(out=khT[d], in_=PSb[pb][0:64, 0:128]), [self.BPS[pb]], [BkT[d]])
                S.op("dve", lambda: A_.copy_predicated(out=att[d], mask=mk, data=self.PS[pb + 1][0:64, 0:64]), [self.BPS[pb + 1]], [Batt[d]])
                S.op("pe", lambda: nc.tensor.matmul(self.PS[pb + 2][:, 0:64], lhsT=Vt[:, c, :], rhs=att[d], start=True, stop=False), [BVt, Batt[d]], [self.BPS[pb + 2]])
                S.op("pe", lambda: nc.tensor.matmul(self.PS[pb + 2][:, 0:64], lhsT=Sm[d], rhs=qt[d][:, cs], start=False, stop=True), [BSm[d], Bqt[d]], [self.BPS[pb + 2]])
                S.op("pe", lambda: nc.tensor.matmul(self.PS[pb + 3][:, 0:128], lhsT=khT[d], rhs=Vt[:, c, :], start=True, stop=True), [BkT[d], BVt], [self.BPS[pb + 3]])
                yield
                S.op("act", lambda: nc.scalar.activation(out=tmpS[d], in_=self.PS[pb + 3][:, 0:128], func=AF.Identity, scale=eem[d][:, c:c + 1]), [self.BPS[pb + 3], Bes[d]], [BtS[d]])
                S.op("dve", lambda: A_.scalar_tensor_tensor(out=Sst[d], in0=Sst[d], scalar=eend[d][:, c:c + 1], in1=tmpS[d], op0=ALU.mult, op1=ALU.add), [BS[d], BtS[d], Bes[d]], [BS[d]])
                S.op("act", lambda: nc.scalar.copy(out=of[d][:, cs], in_=self.PS[pb + 2][:, 0:64]), [self.BPS[pb + 2]], [Bof[d]])
                if step + 1 < NCH:
                    cn = order[d][step + 1]
                    S.op("dve", lambda: A_.tensor_scalar(out=Sm[d], in0=Sst[d], scalar1=emid[d][:, cn:cn + 1], scalar2=None, op0=ALU.mult), [BS[d], Bes[d]], [BSm[d]])

            for step in range(NCH):
                gens = [hstep(d, step) for d in range(2)]
                while gens:
                    for g_ in list(gens):
                        try:
                            next(g_)
                        except StopIteration:
                            gens.remove(g_)
            S.op("dve", lambda: A_.tensor_tensor(out=of[0], in0=of[0], in1=of[1], op=ALU.add), [Bof[0], Bof[1]], [Bof[0]])
            S.op("act", lambda: nc.scalar.activation(out=sqb, in_=of[0], func=AF.Square), [Bof[0]], [Bsq])
            for pc in range(6):
                sl_ = slice(pc * 384, (pc + 1) * 384)
                pb = pc % 2
                S.op("pe", lambda: nc.tensor.matmul(self.PS[pb][:, 0:384], lhsT=self.onesb, rhs=sqb[:, sl_], start=True, stop=True), [Bsq], [self.BPS[pb]])
                S.op("act", lambda: nc.scalar.activation(out=ep[:, sl_], in_=self.PS[pb][:, 0:384], func=AF.Sqrt, scale=1.0 / 128, bias=self.epsD), [self.BPS[pb]], [Bep])
            S.op("dve", lambda: A_.reciprocal(out=ep, in_=ep), [Bep], [Bep])
            S.op("dve", lambda: A_.tensor_tensor(out=of[0], in0=of[0], in1=ep, op=ALU.mult), [Bof[0], Bep], [Bof[0]])
            S.op("act", lambda: nc.scalar.activation(out=graw, in_=graw, func=AF.Silu), [Bgr], [Bgr])
            S.op("dve", lambda: A_.scalar_tensor_tensor(out=ogb, in0=of[0], scalar=self.pv(f"hg_norm{jh}", 0), in1=graw, op0=ALU.mult, op1=ALU.mult), [Bof[0], Bgr], [Bog])
            S.dma("pool", og[rows, cols], ogb, reads=[Bog])
    st.close()


def _mixer(self, l, cur, nxt):
    kind, j = l % 3, l // 3
    last = (l == DEPTH - 1)
    og = self.scr("og", [D, TT], BF16)
    if kind == 0:
        Pfm = self.scr("hgP", [5 * D, TT])
        Itm = self.scr("hgI", [TT, D], BF16)
        self.inproj_stage(l, cur, self.W["hg_w_in"][j], 5 * D, Pfm, [(D, D, Itm)])
        self.hgrn2_scan(j, Pfm, Itm, og)
        self.outproj_stage(l, og, self.W["hg_w_o"][j], cur, nxt, last)
    elif kind == 1:
        self.rwkv_mixer(l, cur, og)
        self.outproj_stage(l, og, self.W["rw_w_o"][j], cur, nxt, last)
    else:
        self.mla_mixer(l, cur, og)
        self.outproj_stage(l, og, self.W["mla_w_o"][j], cur, nxt, last)


Prog.inproj_stage = _inproj_stage
Prog.outproj_stage = _outproj_stage
Prog.hgrn2_scan = _hgrn2_scan
Prog.mixer = _mixer


def _mla_mixer(self, l, xin, og):
    nc, S = self.nc, self.S
    A_ = nc.vector
    NH = 16
    QN = self.scr("mlaQN", [96, NH, TT], BF16)
    KN = self.scr("mlaKN", [96, NH, TT], BF16)
    VT = self.scr("mlaVT", [TT, D], BF16)
    st = Stage(self, "m1")
    Wd = st.sb("wd", [128, 8, 544], BF16)
    Wq = st.sb("wq", [128, 2, 1536], BF16)
    Wk = st.sb("wk", [128, 2, 2048], BF16)
    Wdr = st.sb("wdr", [128, 8, 32], BF16)
    Wqr = st.sb("wqr", [128, 2, NH, 32], BF16)
    BWd, BWq, BWk, BWr = Buf(), Buf(), Buf(), Buf()
    S.dma("pool", Wd, self.W["mla_w_dqkv"][0].rearrange("(kc p) n -> p kc n", p=128), writes=[BWd])
    wqv = self.W["mla_w_uq"][0].rearrange("(kc p) n -> p kc n", p=128)
    for i3 in range(3):
        S.dma("pool", Wq[:, :, i3 * 512:(i3 + 1) * 512], wqv[:, :, i3 * 512:(i3 + 1) * 512], writes=[BWq])
    wkv = self.W["mla_w_ukv"][0].rearrange("(kc p) n -> p kc n", p=128)
    for i4 in range(4):
        S.dma("pool", Wk[:, :, i4 * 512:(i4 + 1) * 512], wkv[:, :, i4 * 512:(i4 + 1) * 512], writes=[BWk])
    Wq4 = Wq.rearrange("p k (h c) -> p k h c", c=96)
    for seg in range(2):
        for half in range(2):
            sgn = -1.0 if half == 0 else 1.0
            so = 64 + seg * 16 + (1 - half) * 8
            do = seg * 16 + half * 8
            S.op("act", lambda: nc.scalar.activation(out=Wqr[:, :, :, do:do + 8], in_=Wq4[:, :, :, so:so + 8], func=AF.Copy, scale=sgn), [BWq], [BWr])
            so2 = 512 + seg * 16 + (1 - half) * 8
            S.op("act", lambda: nc.scalar.activation(out=Wdr[:, :, do:do + 8], in_=Wd[:, :, so2:so2 + 8], func=AF.Copy, scale=sgn), [BWd], [BWr])
    cos = st.sb("cos", [96, T]); sin = st.sb("sin", [96, T])
    Bcs = Buf()
    RP = slice(64, 96)
    S.dma("sp", cos[RP, :], self.cd["rope_cos"], writes=[Bcs])
    S.dma("sp", sin[RP, :], self.cd["rope_sin"], writes=[Bcs])
    xs = [st.sb(f"xs{i}", [128, 8, BLK]) for i in range(2)]
    hb = [st.sb(f"hb{i}", [128, 8, BLK], BF16) for i in range(2)]
    Bxs, Bhb = [[Buf(), Buf()] for _ in range(2)]
    nt = self.norm_tiles(st, BLK)
    cs_ = st.sb("cs", [128, 4, BLK]); csq = st.sb("csq", [128, 4, BLK], BF16); cn = st.sb("cn", [128, 4, BLK], BF16)
    rr0 = st.sb("rr0", [128, 2, BLK]); rr1 = st.sb("rr1", [128, 2, BLK]); ctmp = st.sb("ctmp", [128, 4, BLK])
    Bcs_, Bcsq, Bcn, Brr, Bct = [Buf() for _ in range(5)]
    qn_s = [st.sb(f"qns{i}", [96, NH, BLK], BF16) for i in range(2)]
    kn_s = [st.sb(f"kns{i}", [96, NH, BLK], BF16) for i in range(2)]
    vt_s = [st.sb(f"vts{i}", [128, D], BF16) for i in range(2)]
    t1 = st.sb("t1", [96, 2, BLK]); t2 = st.sb("t2", [96, 2, BLK])
    Bt1, Bt2 = Buf(), Buf()
    Bqn, Bkn, Bqr, Bkr, Bvt = [[Buf(), Buf()] for _ in range(5)]
    xiv = xin.rearrange("(c p) t -> p c t", p=128)
    blocks = self.blocks(False)

    def load(n):
        b, k = blocks[n]
        S.dma("sp", xs[n % 2], xiv[:, :, b * T + k * BLK:b * T + (k + 1) * BLK], writes=[Bxs[n % 2]])

    load(0)
    vti = 0
    for n, (b, k) in enumerate(blocks):
        i = n % 2
        if n + 1 < len(blocks):
            load(n + 1)
        j = 2 if k == 0 else b
        A, sh, _ = self.mod_ab(l, 0, j)
        self.norm_block(nt, xs[i], Bxs[i], BLK, A, sh, hb[i], Bhb[i], 6)
        col = b * T + k * BLK
        tcol = slice(k * BLK, (k + 1) * BLK)
        for c4 in range(4):
            pb = c4 // 2
            for kc in range(8):
                S.op("pe", lambda: nc.tensor.matmul(self.PS[pb][:, (c4 % 2) * BLK:(c4 % 2 + 1) * BLK], lhsT=Wd[:, kc, c4 * 128:(c4 + 1) * 128], rhs=hb[i][:, kc, :], start=(kc == 0), stop=(kc == 7)),
                     [BWd, Bhb[i]], [self.BPS[pb]])
        for kc in range(8):
            S.op("pe", lambda: nc.tensor.matmul(self.PS[2][RP, 0:BLK], lhsT=Wd[:, kc, 512:544], rhs=hb[i][:, kc, :], start=(kc == 0), stop=(kc == 7)), [BWd, Bhb[i]], [self.BPS[2]])
        for kc in range(8):
            S.op("pe", lambda: nc.tensor.matmul(self.PS[2][RP, BLK:2 * BLK], lhsT=Wdr[:, kc, :], rhs=hb[i][:, kc, :], start=(kc == 0), stop=(kc == 7)), [BWr, Bhb[i]], [self.BPS[2]])
        for pb in range(2):
            S.op("act", lambda: nc.scalar.copy(out=cs_[:, 2 * pb:2 * pb + 2, :], in_=self.PS[pb].rearrange("p (c t) -> p c t", c=2)), [self.BPS[pb]], [Bcs_])
            S.op("act", lambda: nc.scalar.activation(out=csq[:, 2 * pb:2 * pb + 2, :], in_=self.PS[pb].rearrange("p (c t) -> p c t", c=2), func=AF.Square), [self.BPS[pb]], [Bcsq])
        S.op("dve", lambda: A_.tensor_tensor(out=t1[RP, 0, :], in0=self.PS[2][RP, 0:BLK], in1=cos[RP, tcol], op=ALU.mult), [self.BPS[2], Bcs], [Bt1])
        S.op("dve", lambda: A_.tensor_tensor(out=t2[RP, 0, :], in0=self.PS[2][RP, BLK:2 * BLK], in1=sin[RP, tcol], op=ALU.mult), [self.BPS[2], Bcs], [Bt2])
        S.op("dve", lambda: A_.tensor_tensor(out=kn_s[i][RP, :, :], in0=t1[RP, 0:1, :].to_broadcast([32, NH, BLK]), in1=t2[RP, 0:1, :].to_broadcast([32, NH, BLK]), op=ALU.add), [Bt1, Bt2], [Bkn[i]])
        for w in range(2):
            for c in range(2):
                S.op("pe", lambda: nc.tensor.matmul(self.PS[3][:, w * BLK:(w + 1) * BLK], lhsT=self.onesb, rhs=csq[:, 2 * w + c, :], start=(c == 0), stop=(c == 1)), [Bcsq], [self.BPS[3]])
        S.op("act", lambda: nc.scalar.activation(out=rr0, in_=self.PS[3].rearrange("p (w t) -> p w t", w=2), func=AF.Sqrt, scale=1.0 / 256, bias=self.epsD), [self.BPS[3]], [Brr])
        S.op("dve", lambda: A_.reciprocal(out=rr1, in_=rr0), [Brr], [Brr])
        S.op("dve", lambda: A_.tensor_tensor(out=ctmp.rearrange("p (w c) t -> p w c t", w=2), in0=cs_.rearrange("p (w c) t -> p w c t", w=2),
                                              in1=rr1.unsqueeze(2).to_broadcast([128, 2, 2, BLK]), op=ALU.mult), [Bcs_, Brr], [Bct])
        for c4 in range(4):
            gname = "mla_q_norm" if c4 < 2 else "mla_kv_norm"
            S.op("act", lambda: nc.scalar.activation(out=cn[:, c4, :], in_=ctmp[:, c4, :], func=AF.Identity, scale=self.pv(gname, c4 % 2)), [Bct], [Bcn])
        for hp in range(8):
            for which in range(2):
                pb = 4 + (2 * hp + which) % 2
                Wt_, coff, hw, ci = (Wq, 0, 96, 0) if which == 0 else (Wk, 0, 128, 2)
                for hh in range(2):
                    h = 2 * hp + hh
                    for kc in range(2):
                        S.op("pe", lambda: nc.tensor.matmul(self.PS[pb][0:64, hh * BLK:(hh + 1) * BLK], lhsT=Wt_[:, kc, h * hw:h * hw + 64], rhs=cn[:, ci + kc, :], start=(kc == 0), stop=(kc == 1)),
                             [BWq if which == 0 else BWk, Bcn], [self.BPS[pb]])
                dst = qn_s[i] if which == 0 else kn_s[i]
                Bd = Bqn[i] if which == 0 else Bkn[i]
                if which == 0:
                    S.op("act", lambda: nc.scalar.copy(out=dst[0:64, 2 * hp:2 * hp + 2, :], in_=self.PS[pb][0:64, :].rearrange("p (h t) -> p h t", h=2)), [self.BPS[pb]], [Bd])
                else:
                    S.op("dve", lambda: A_.tensor_copy(out=dst[0:64, 2 * hp:2 * hp + 2, :], in_=self.PS[pb][0:64, :].rearrange("p (h t) -> p h t", h=2)), [self.BPS[pb]], [Bd])
            for hh in range(2):
                h = 2 * hp + hh
                for kc in range(2):
                    S.op("pe", lambda: nc.tensor.matmul(self.PS[6][RP, hh * BLK:(hh + 1) * BLK], lhsT=Wq[:, kc, h * 96 + 64:h * 96 + 96], rhs=cn[:, kc, :], start=(kc == 0), stop=(kc == 1)), [BWq, Bcn], [self.BPS[6]])
                for kc in range(2):
                    S.op("pe", lambda: nc.tensor.matmul(self.PS[7][RP, hh * BLK:(hh + 1) * BLK], lhsT=Wqr[:, kc, h, :], rhs=cn[:, kc, :], start=(kc == 0), stop=(kc == 1)), [BWr, Bcn], [self.BPS[7]])
            cosb = cos[RP, tcol].unsqueeze(1).to_broadcast([32, 2, BLK])
            sinb = sin[RP, tcol].unsqueeze(1).to_broadcast([32, 2, BLK])
            S.op("dve", lambda: A_.tensor_tensor(out=t1[RP, :, :], in0=self.PS[6][RP, :].rearrange("p (h t) -> p h t", h=2), in1=cosb, op=ALU.mult), [self.BPS[6], Bcs], [Bt1])
            S.op("dve", lambda: A_.tensor_tensor(out=t2[RP, :, :], in0=self.PS[7][RP, :].rearrange("p (h t) -> p h t", h=2), in1=sinb, op=ALU.mult), [self.BPS[7], Bcs], [Bt2])
            S.op("dve", lambda: A_.tensor_tensor(out=qn_s[i][RP, 2 * hp:2 * hp + 2, :], in0=t1[RP, :, :], in1=t2[RP, :, :], op=ALU.add), [Bt1, Bt2], [Bqn[i]])
        S.dma("pool", QN[:, :, col:col + BLK], qn_s[i], reads=[Bqn[i]])
        S.dma("pool", KN[:, :, col:col + BLK], kn_s[i], reads=[Bkn[i]])
        Wkv = Wk.rearrange("p k (h c) -> p k h c", c=128)
        for tt in range(BLK // 128):
            vi = vti % 2
            vti += 1
            for hf in range(2):
                pb = 4 + hf
                for kc in range(2):
                    S.op("pe", lambda: nc.tensor.matmul(self.PS[pb][:, 0:512], lhsT=cn[:, 2 + kc, tt * 128:(tt + 1) * 128], rhs=Wkv[:, kc, hf * 8:(hf + 1) * 8, 64:128], start=(kc == 0), stop=(kc == 1)),
                         [BWk, Bcn], [self.BPS[pb]])
                S.op("act", lambda: nc.scalar.copy(out=vt_s[vi][:, hf * 512:(hf + 1) * 512], in_=self.PS[pb][:, 0:512]), [self.BPS[pb]], [Bvt[vi]])
            S.dma("pool", VT[col + tt * 128:col + (tt + 1) * 128, :], vt_s[vi], reads=[Bvt[vi]])
    st.close()
    st = Stage(self, "m2")
    NKT = T // 128
    Vall = st.sb("Vall", [128, NKT, D], BF16)
    KNh = [st.sb(f"KNh{i}", [96, T], BF16) for i in range(2)]
    QNh = [st.sb(f"QNh{i}", [96, T], BF16) for i in range(2)]
    VX = [st.sb(f"VX{i}", [128, NKT, 65], BF16) for i in range(2)]
    PT = [st.sb(f"PT{i}", [128, 512], BF16) for i in range(3)]
    rd = st.sb("rd", [65, 512]); rb = [st.sb(f"rb{i}", [64, 512]) for i in range(2)]
    ob = [st.sb(f"ob{i}", [64, 512], BF16) for i in range(2)]
    BVa, BKR, Brd = Buf(), Buf(), Buf()
    BKN, BQN, BQR, BVX, Brb, Bob = [[Buf(), Buf()] for _ in range(6)]
    BPT = [Buf() for _ in range(3)]
    for i in range(2):
        S.op("pool", lambda: nc.gpsimd.memset(VX[i], 1.0), [], [BVX[i]])
    qblocks = [(0, TC, 2)] + [(TC + qb * 512, 512, NKT) for qb in range(4)]
    pti = 0
    hn = 0
    for b in range(NB):
        c0 = b * T
        S.dma("sp", Vall, VT[c0:c0 + T, :].rearrange("(kt p) v -> p kt v", p=128), writes=[BVa])
        for h in range(NH):
            i = hn % 2
            hn += 1
            S.dma("sp", KNh[i], KN[:, h, c0:c0 + T], writes=[BKN[i]])
            S.dma("sp", QNh[i], QN[:, h, c0:c0 + T], writes=[BQN[i]])
            S.op("pool", lambda: nc.gpsimd.tensor_copy(out=VX[i][:, :, 0:64], in_=Vall[:, :, h * 64:(h + 1) * 64]), [BVa], [BVX[i]])
            for qi, (q0, nq, nkt) in enumerate(qblocks):
                po = 4 + (qi % 2)

                def score(kt):
                    ps = kt % 4
                    ks = slice(kt * 128, (kt + 1) * 128)
                    S.op("pe", lambda: nc.tensor.matmul(self.PS[ps][:, 0:nq], lhsT=KNh[i][:, ks], rhs=QNh[i][:, q0:q0 + nq], start=True, stop=True), [BKN[i], BQN[i]], [self.BPS[ps]])

                score(0)
                if nkt > 1:
                    score(1)
                for kt in range(nkt):
                    ps = kt % 4
                    p3 = pti % 3
                    pti += 1
                    if kt + 2 < nkt:
                        score(kt + 2)
                    S.op("act", lambda: nc.scalar.activation(out=PT[p3][:, 0:nq], in_=self.PS[ps][:, 0:nq], func=AF.Exp, scale=MLA_SCALE), [self.BPS[ps]], [BPT[p3]])
                    S.op("pe", lambda: nc.tensor.matmul(self.PS[po][0:65, 0:nq], lhsT=VX[i][:, kt, :], rhs=PT[p3][:, 0:nq], start=(kt == 0), stop=(kt == nkt - 1)), [BVX[i], BPT[p3]], [self.BPS[po]])
                r2 = qi % 2
                S.op("dve", lambda: A_.reciprocal(out=rd[64:65, 0:nq], in_=self.PS[po][64:65, 0:nq]), [self.BPS[po]], [Brd])
                S.op("pe", lambda: nc.tensor.matmul(self.PS[6 + r2][0:64, 0:nq], lhsT=self.onesf[64:65, 0:64], rhs=rd[64:65, 0:nq], start=True, stop=True), [Brd], [self.BPS[6 + r2]])
                S.op("act", lambda: nc.scalar.copy(out=rb[r2][:, 0:nq], in_=self.PS[6 + r2][0:64, 0:nq]), [self.BPS[6 + r2]], [Brb[r2]])
                S.op("dve", lambda: A_.tensor_tensor(out=ob[r2][:, 0:nq], in0=self.PS[po][0:64, 0:nq], in1=rb[r2][:, 0:nq], op=ALU.mult), [self.BPS[po], Brb[r2]], [Bob[r2]])
                S.dma("pool", og[h * 64:(h + 1) * 64, c0 + q0:c0 + q0 + nq], ob[r2][:, 0:nq], reads=[Bob[r2]])
    st.close()


Prog.mla_mixer = _mla_mixer


RW_ARR = ["r", "kt0", "kt1", "be0", "be1", "kap", "lw0", "lw1", "v", "g"]


def _rwkv_proj(self, l, xin, RWP, Vtm):
    nc, S = self.nc, self.S
    A_ = nc.vector
    st = Stage(self, "r1")
    Wrkv = st.sb("wrkv", [128, 8, 3 * D], BF16)
    BWrkv = []
    for i3 in range(3):
        v_ = self.W["rw_w_rkv"][0, i3].rearrange("(kc p) n -> p kc n", p=128)
        for hf in range(2):
            bb_ = Buf()
            S.dma("pool", Wrkv[:, :, i3 * D + hf * 512:i3 * D + (hf + 1) * 512], v_[:, :, hf * 512:(hf + 1) * 512], writes=[bb_])
            BWrkv.append(bb_)
    W1 = st.sb("w1", [128, 8, 2, 64], BF16); A1 = st.sb("a1", [128, 8, 2, 64], BF16); G1 = st.sb("g1", [128, 8, 160], BF16)
    W2 = st.sb("w2", [64, 2, D], BF16); A2 = st.sb("a2", [64, 2, D], BF16); G2a = st.sb("g2a", [128, D], BF16); G2b = st.sb("g2b", [32, D], BF16)
    Bsw = Buf()
    for d in range(2):
        S.dma("pool", W1[:, :, d, :], self.W["rw_w1"][0, d].rearrange("(kc p) n -> p kc n", p=128), writes=[Bsw])
        S.dma("pool", A1[:, :, d, :], self.W["rw_a1"][0, d].rearrange("(kc p) n -> p kc n", p=128), writes=[Bsw])
        S.dma("pool", W2[:, d, :], self.W["rw_w2"][0, d], writes=[Bsw])
        S.dma("pool", A2[:, d, :], self.W["rw_a2"][0, d], writes=[Bsw])
    S.dma("pool", G1, self.W["rw_g1"][0].rearrange("(kc p) n -> p kc n", p=128), writes=[Bsw])
    S.dma("pool", G2a, self.W["rw_g2"][0, 0:128, :], writes=[Bsw])
    S.dma("pool", G2b, self.W["rw_g2"][0, 128:160, :], writes=[Bsw])
    NH_ = BLK + 2
    xs = [st.sb(f"xs{i}", [128, 8, NH_]) for i in range(2)]
    hf_ = st.sb("hf", [128, 8, NH_])
    dx = st.sb("dx", [128, 8, BLK])
    xj = [st.sb(f"xj{j}", [128, 8, BLK], BF16) for j in range(6)]
    Bxs = [Buf(), Buf()]
    Bhf, Bdx = Buf(), Buf()
    Bxj = [Buf() for _ in range(6)]
    nt = self.norm_tiles(st)
    lt = st.sb("lt", [64, 5, BLK], BF16)
    gh = st.sb("gh", [128, BLK], BF16)
    Blt = Buf()
    stg = [st.sb(f"stg{i}", [128, 10, BLK]) for i in range(2)]
    Bstg = [Buf(), Buf()]
    tmp = [st.sb(f"tmp{i}", [128, BLK]) for i in range(6)]
    Btmp = [Buf() for _ in range(6)]
    sqb = st.sb("sqb", [128, BLK], BF16)
    Bsqb = Buf()
    vts = [st.sb(f"vts{i}", [128, D], BF16) for i in range(2)]
    Bvts = [Buf(), Buf()]
    for i in range(2):
        S.op("dve", lambda: A_.memset(xs[i], 0.0), [], [Bxs[i]])
    xiv = xin.rearrange("(c p) t -> p c t", p=128)
    blocks = self.blocks(False)
    blk64b = st.sb("blk64b", [128, 128], BF16)
    Bb64 = Buf()
    S.op("dve", lambda: A_.tensor_copy(out=blk64b, in_=self.blk64), [], [Bb64])

    def load(n):
        b, k = blocks[n]
        t0, lo, hi, _, _ = self.blk_range(k)
        S.dma("sp", xs[n % 2][:, :, lo - (t0 - 1):hi - (t0 - 1)], xiv[:, :, b * T + lo:b * T + hi], writes=[Bxs[n % 2]])

    load(0)
    si = 0
    vi_ = 0
    pbk = 0
    for n, (b, k) in enumerate(blocks):
        i = n % 2
        if n + 1 < len(blocks):
            load(n + 1)
        t0, lo, hi, first, last = self.blk_range(k)
        j = 2 if k == 0 else b
        A, sh, _ = self.mod_ab(l, 0, j)
        self.norm_block(nt, xs[i], Bxs[i], NH_, A, sh, hf_, Bhf, 6)
        if first:
            S.op("dve", lambda: A_.memset(hf_[:, :, 0:1], 0.0), [], [Bhf])
        if last:
            S.op("dve", lambda: A_.memset(hf_[:, :, NH_ - 1:NH_], 0.0), [], [Bhf])
        S.op("dve", lambda: A_.tensor_tensor(out=dx, in0=hf_[:, :, 0:BLK], in1=hf_[:, :, 2:2 + BLK], op=ALU.add), [Bhf], [Bdx])
        S.op("dve", lambda: A_.scalar_tensor_tensor(out=dx, in0=dx, scalar=0.5, in1=hf_[:, :, 1:1 + BLK], op0=ALU.mult, op1=ALU.subtract), [Bhf, Bdx], [Bdx])
        for jj in range(6):
            for c in range(8):
                S.op("dve", lambda: A_.scalar_tensor_tensor(out=xj[jj][:, c, :], in0=dx[:, c, :], scalar=self.pv(f"rw_mu{jj}", c), in1=hf_[:, c, 1:1 + BLK], op0=ALU.mult, op1=ALU.add),
                     [Bdx, Bhf], [Bxj[jj]])
        for d in range(2):
            for kc in range(8):
                S.op("pe", lambda: nc.tensor.matmul(self.PS[5][0:64, d * BLK:(d + 1) * BLK], lhsT=W1[:, kc, d, :], rhs=xj[1][:, kc, :], start=(kc == 0), stop=(kc == 7)), [Bsw, Bxj[1]], [self.BPS[5]])
        S.op("act", lambda: nc.scalar.activation(out=lt[:, 0:2, :], in_=self.PS[5][0:64, :].rearrange("p (d t) -> p d t", d=2), func=AF.Tanh), [self.BPS[5]], [Blt])
        for d in range(2):
            for kc in range(8):
                S.op("pe", lambda: nc.tensor.matmul(self.PS[5][0:64, d * BLK:(d + 1) * BLK], lhsT=A1[:, kc, d, :], rhs=xj[4][:, kc, :], start=(kc == 0), stop=(kc == 7)), [Bsw, Bxj[4]], [self.BPS[5]])
        S.op("act", lambda: nc.scalar.copy(out=lt[:, 2:4, :], in_=self.PS[5][0:64, :].rearrange("p (d t) -> p d t", d=2)), [self.BPS[5]], [Blt])
        for kc in range(8):
            S.op("pe", lambda: nc.tensor.matmul(self.PS[5][:, 0:BLK], lhsT=G1[:, kc, 0:128], rhs=xj[5][:, kc, :], start=(kc == 0), stop=(kc == 7)), [Bsw, Bxj[5]], [self.BPS[5]])
        for kc in range(8):
            S.op("pe", lambda: nc.tensor.matmul(self.PS[5][0:32, BLK:2 * BLK], lhsT=G1[:, kc, 128:160], rhs=xj[5][:, kc, :], start=(kc == 0), stop=(kc == 7)), [Bsw, Bxj[5]], [self.BPS[5]])
        S.op("act", lambda: nc.scalar.activation(out=gh, in_=self.PS[5][:, 0:BLK], func=AF.Sigmoid), [self.BPS[5]], [Blt])
        S.op("act", lambda: nc.scalar.activation(out=lt[0:32, 4, :], in_=self.PS[5][0:32, BLK:2 * BLK], func=AF.Sigmoid), [self.BPS[5]], [Blt])
        col = b * T + t0
        for c in range(8):
            s_ = si % 2
            si += 1
            sg_ = stg[s_]
            Bs = Bstg[s_]
            cs = slice(c * 128, (c + 1) * 128)

            def bank():
                nonlocal pbk
                pbk += 1
                return pbk % 5

            prk = []
            for which, xsrc in ((0, 0), (1, 2), (2, 3)):
                pb = bank()
                for kc in range(8):
                    S.op("pe", lambda: nc.tensor.matmul(self.PS[pb][:, 0:BLK], lhsT=Wrkv[:, kc, which * D + c * 128:which * D + (c + 1) * 128], rhs=xj[xsrc][:, kc, :], start=(kc == 0), stop=(kc == 7)),
                         [BWrkv[which * 2 + (c // 4)], Bxj[xsrc]], [self.BPS[pb]])
                prk.append(pb)
            S.op("act", lambda: nc.scalar.copy(out=sg_[:, 0, :], in_=self.PS[prk[0]][:, 0:BLK]), [self.BPS[prk[0]]], [Bs])
            S.op("act", lambda: nc.scalar.copy(out=sg_[:, 8, :], in_=self.PS[prk[2]][:, 0:BLK]), [self.BPS[prk[2]]], [Bs])
            kraw = tmp[0]
            S.op("act", lambda: nc.scalar.copy(out=kraw, in_=self.PS[prk[1]][:, 0:BLK]), [self.BPS[prk[1]]], [Btmp[0]])
            S.op("dve", lambda: A_.tensor_scalar(out=tmp[1], in0=kraw, scalar1=self.pv("rw_k_k", c), scalar2=None, op0=ALU.mult), [Btmp[0]], [Btmp[1]])
            S.op("act", lambda: nc.scalar.activation(out=sqb, in_=tmp[1], func=AF.Square), [Btmp[1]], [Bsqb])
            pb = bank()
            S.op("pe", lambda: nc.tensor.matmul(self.PS[pb][:, 0:BLK], lhsT=blk64b, rhs=sqb, start=True, stop=True), [Bsqb, Bb64], [self.BPS[pb]])
            S.op("act", lambda: nc.scalar.activation(out=tmp[2], in_=self.PS[pb][:, 0:BLK], func=AF.Sqrt), [self.BPS[pb]], [Btmp[2]])
            S.op("dve", lambda: A_.tensor_scalar(out=tmp[2], in0=tmp[2], scalar1=1e-12, scalar2=None, op0=ALU.max), [Btmp[2]], [Btmp[2]])
            S.op("dve", lambda: A_.reciprocal(out=tmp[2], in_=tmp[2]), [Btmp[2]], [Btmp[2]])
            S.op("dve", lambda: A_.tensor_tensor(out=sg_[:, 5, :], in0=tmp[1], in1=tmp[2], op=ALU.mult), [Btmp[1], Btmp[2]], [Bs])
            pb = bank()
            S.op("pe", lambda: nc.tensor.matmul(self.PS[pb][:, 0:BLK], lhsT=G2a[:, cs], rhs=gh, start=True, stop=False), [Bsw, Blt], [self.BPS[pb]])
            S.op("pe", lambda: nc.tensor.matmul(self.PS[pb][:, 0:BLK], lhsT=G2b[:, cs], rhs=lt[0:32, 4, :], start=False, stop=True), [Bsw, Blt], [self.BPS[pb]])
            S.op("act", lambda: nc.scalar.copy(out=sg_[:, 9, :], in_=self.PS[pb][:, 0:BLK]), [self.BPS[pb]], [Bs])
            for d in range(2):
                pb = bank()
                S.op("pe", lambda: nc.tensor.matmul(self.PS[pb][:, 0:BLK], lhsT=W2[:, d, cs], rhs=lt[:, d, :], start=True, stop=True), [Bsw, Blt], [self.BPS[pb]])
                S.op("act", lambda: nc.scalar.activation(out=tmp[3], in_=self.PS[pb][:, 0:BLK], func=AF.Sigmoid, bias=self.pv(f"rw_w0_{d}", c)), [self.BPS[pb]], [Btmp[3]])
                S.op("dve", lambda: A_.tensor_scalar(out=sg_[:, 6 + d, :], in0=tmp[3], scalar1=-float(np.exp(-0.5)), scalar2=None, op0=ALU.mult), [Btmp[3]], [Bs])
                pb = bank()
                S.op("pe", lambda: nc.tensor.matmul(self.PS[pb][:, 0:BLK], lhsT=A2[:, d, cs], rhs=lt[:, 2 + d, :], start=True, stop=True), [Bsw, Blt], [self.BPS[pb]])
                S.op("act", lambda: nc.scalar.activation(out=tmp[4], in_=self.PS[pb][:, 0:BLK], func=AF.Sigmoid, bias=self.pv(f"rw_a0_{d}", c)), [self.BPS[pb]], [Btmp[4]])
                S.op("dve", lambda: A_.tensor_tensor(out=sg_[:, 3 + d, :], in0=tmp[4], in1=sg_[:, 5, :], op=ALU.mult), [Btmp[4], Bs], [Bs])
                S.op("dve", lambda: A_.tensor_scalar(out=tmp[5], in0=tmp[4], scalar1=-1.0, scalar2=None, op0=ALU.add), [Btmp[4]], [Btmp[5]])
                S.op("dve", lambda: A_.tensor_scalar(out=tmp[5], in0=tmp[5], scalar1=self.pv("rw_k_a", c), scalar2=1.0, op0=ALU.mult, op1=ALU.add), [Btmp[5]], [Btmp[5]])
                S.op("dve", lambda: A_.tensor_tensor(out=sg_[:, 1 + d, :], in0=tmp[5], in1=kraw, op=ALU.mult), [Btmp[5], Btmp[0]], [Bs])
            S.dma("pool", RWP[:, c * 128:(c + 1) * 128, col:col + BLK].rearrange("a p t -> p a t"), sg_, reads=[Bs])
        for tt in range(BLK // 128):
            vi = vi_ % 2
            vi_ += 1
            for hfv in range(2):
                pb = 4 - hfv
                for kc in range(8):
                    S.op("pe", lambda: nc.tensor.matmul(self.PS[pb][:, 0:512], lhsT=xj[3][:, kc, tt * 128:(tt + 1) * 128], rhs=Wrkv[:, kc, 2 * D + hfv * 512:2 * D + (hfv + 1) * 512], start=(kc == 0), stop=(kc == 7)),
                         [BWrkv[4 + hfv], Bxj[3]], [self.BPS[pb]])
                S.op("act", lambda: nc.scalar.copy(out=vts[vi][:, hfv * 512:(hfv + 1) * 512], in_=self.PS[pb][:, 0:512]), [self.BPS[pb]], [Bvts[vi]])
            S.dma("pool", Vtm[col + tt * 128:col + (tt + 1) * 128, :], vts[vi], reads=[Bvts[vi]])
    st.close()


def _rwkv_mixer(self, l, xin, og):
    RWP = self.scr("rwP", [10, D, TT])
    Vtm = self.scr("rwV", [TT, D], BF16)
    self.rwkv_proj(l, xin, RWP, Vtm)
    if getattr(self, "rw_stop", 0) == 1:
        return
    self.rwkv_scan(RWP, Vtm, og)


Prog.rwkv_proj = _rwkv_proj
Prog.rwkv_mixer = _rwkv_mixer


def _rwkv_scan(self, RWP, Vtm, og):
    nc, S = self.nc, self.S
    A_ = nc.vector
    U32 = mybir.dt.uint32
    RWD = self.scr("rwD", [NB, 8, 2, 2, 128, NCH * 128], BF16)
    RWS = self.scr("rwS", [NB, 8, 2, 128, 3 * NCH])
    skipA = getattr(self, "rw_skipA", False)
    st = Stage(self, "r2a")
    smask = st.sb("smask", [128, T])
    Bsm = Buf()
    S.dma("sp", smask, self.cd["scanmask"], writes=[Bsm])
    lw = st.sb("lw", [128, T]); kap = st.sb("kap", [128, T]); rr = st.sb("r", [128, T]); kt = st.sb("kt", [128, T]); be = st.sb("be", [128, T])
    cw = st.sb("cw", [128, T]); cm = st.sb("cm", [128, T]); en = st.sb("en", [128, T]); ex = st.sb("ex", [128, T])
    ABt = [st.sb(f"AB{i}", [128, NCH, 2, CH], BF16) for i in range(2)]
    KBt_ = [st.sb(f"KB{i}", [128, NCH, 2, CH], BF16) for i in range(2)]
    SC = [st.sb(f"SC{i}", [128, 3, NCH]) for i in range(2)]
    Blw, Bkap, Br, Bkt, Bbe, Bcw, Bcm, Ben, Bex = [Buf() for _ in range(9)]
    BAB, BKB, BSC = [[Buf(), Buf()] for _ in range(3)]
    it = 0
    v3 = lambda t_: t_.rearrange("p (c s) -> p c s", s=CH)
    for b in range(0 if skipA else NB):
        cols = slice(b * T, (b + 1) * T)
        for p in range(8):
            rows = slice(p * 128, (p + 1) * 128)
            for d in range(2):
                i = it % 2
                it += 1
                S.dma("sp", lw, RWP[6 + d, rows, cols], writes=[Blw])
                S.dma("sp", kap, RWP[5, rows, cols], writes=[Bkap])
                S.dma("sp", rr, RWP[0, rows, cols], writes=[Br])
                S.dma("sp", kt, RWP[1 + d, rows, cols], writes=[Bkt])
                S.dma("sp", be, RWP[3 + d, rows, cols], writes=[Bbe])
                S.op("dve", lambda: A_.tensor_tensor_scan(out=cw, data0=smask, data1=lw, initial=0.0, op0=ALU.mult, op1=ALU.add), [Bsm, Blw], [Bcw])
                if d == 1:
                    S.op("dve", lambda: A_.tensor_tensor(out=cm, in0=lw, in1=cw, op=ALU.subtract), [Blw, Bcw], [Bcm])
                    S.op("dve", lambda: A_.tensor_tensor(out=v3(en), in0=v3(cm), in1=v3(cw)[:, :, CH - 1:CH].to_broadcast([128, NCH, CH]), op=ALU.add), [Bcm, Bcw], [Ben])
                    S.op("dve", lambda: A_.tensor_copy(out=cw, in_=en), [Ben], [Bcw])
                m_idx = 32 if d == 0 else 31
                e_idx = CH - 1 if d == 0 else 0
                c3 = v3(cw)
                S.op("act", lambda: nc.scalar.activation(out=SC[i][:, 0, :], in_=c3[:, :, m_idx], func=AF.Exp), [Bcw], [BSC[i]])
                S.op("act", lambda: nc.scalar.activation(out=SC[i][:, 1, :], in_=c3[:, :, e_idx], func=AF.Exp), [Bcw], [BSC[i]])
                S.op("dve", lambda: A_.tensor_tensor(out=SC[i][:, 2, :], in0=c3[:, :, e_idx], in1=c3[:, :, m_idx], op=ALU.subtract), [Bcw], [BSC[i]])
                S.op("act", lambda: nc.scalar.activation(out=SC[i][:, 2, :], in_=SC[i][:, 2, :], func=AF.Exp), [BSC[i]], [BSC[i]])
                S.dma("pool", RWS[b, p, d], SC[i].rearrange("p a c -> p (a c)"), reads=[BSC[i]])
                S.op("dve", lambda: A_.tensor_tensor(out=v3(cm), in0=c3, in1=c3[:, :, m_idx:m_idx + 1].to_broadcast([128, NCH, CH]), op=ALU.subtract), [Bcw], [Bcm])
                S.op("act", lambda: nc.scalar.activation(out=en, in_=cm, func=AF.Exp, scale=-1.0), [Bcm], [Ben])
                S.op("dve", lambda: A_.tensor_tensor(out=ex, in0=cm, in1=lw, op=ALU.subtract), [Bcm, Blw], [Bex])
                S.op("act", lambda: nc.scalar.activation(out=ex, in_=ex, func=AF.Exp), [Bex], [Bex])
                S.op("act", lambda: nc.scalar.activation(out=cm, in_=cm, func=AF.Exp), [Bcm], [Bcm])
                S.op("dve", lambda: A_.tensor_tensor(out=ABt[i][:, :, 0, :], in0=v3(kap), in1=v3(ex), op=ALU.mult), [Bkap, Bex], [BAB[i]])
                S.op("dve", lambda: A_.tensor_tensor(out=ABt[i][:, :, 1, :], in0=v3(rr), in1=v3(cm), op=ALU.mult), [Br, Bcm], [BAB[i]])
                S.op("dve", lambda: A_.tensor_tensor(out=KBt_[i][:, :, 0, :], in0=v3(kt), in1=v3(en), op=ALU.mult), [Bkt, Ben], [BKB[i]])
                S.op("dve", lambda: A_.tensor_tensor(out=KBt_[i][:, :, 1, :], in0=v3(be), in1=v3(en), op=ALU.mult), [Bbe, Ben], [BKB[i]])
                S.dma("pool", RWD[b, p, d, 0], ABt[i].rearrange("p c a s -> p (c a s)"), reads=[BAB[i]])
                S.dma("pool", RWD[b, p, d, 1], KBt_[i].rearrange("p c a s -> p (c a s)"), reads=[BKB[i]])
    st.close()
    if getattr(self, "rw_stop", 0) == 2:
        return
    st = Stage(self, "r2b")
    S.pe_selfwait = getattr(self, "rw_selfwait", False)
    S.pe_drain = getattr(self, "rw_drain", 2)
    epsLN = st.sb("epsLN", [128, 1])
    Bgl = Buf()
    S.op("dve", lambda: A_.memset(epsLN, RW_LN_EPS), [], [Bgl])
    AB = [st.sb(f"AB{d}", [128, NCH, 128], BF16) for d in range(2)]
    KB = [st.sb(f"KB{d}", [128, NCH, 128], BF16) for d in range(2)]
    SCs = [st.sb(f"SC{d}", [128, 3, NCH]) for d in range(2)]
    Vst = st.sb("Vst", [64, NCH, 128], BF16)
    BABl, BKBl, BSCl = [[Buf(), Buf()] for _ in range(3)]
    BVst = Buf()
    chains = [(hd, d) for hd in range(2) for d in range(2)]
    IDT = BF16 if getattr(self, "rw_inv_bf16", True) else F32
    VU, GGb, AN0, ANp, Xp, Wf, KBtr = {}, {}, {}, {}, {}, {}, {}
    BVU, BGG, BAN0, BANp, BXp, BWf, BKBtr, BST, BS0, BtS, By = [dict() for _ in range(11)]
    for ch in chains:
        nm = f"{ch[0]}{ch[1]}"
        VU[ch] = st.sb("VU" + nm, [128, NCH, CH], BF16)
        GGb[ch] = st.sb("GG" + nm, [128, 128], BF16)
        AN0[ch] = st.sb("AN0" + nm, [128, 128], IDT)
        ANp[ch] = [st.sb(f"ANp{q}" + nm, [128, 128], IDT) for q in range(2)]
        Xp[ch] = [st.sb(f"X{q}" + nm, [128, CH], IDT) for q in range(2)]
        Wf[ch] = st.sb("Wf" + nm, [128, CH], IDT)
        KBtr[ch] = st.sb("KBt" + nm, [128, CH], BF16)
        BVU[ch], BGG[ch], BAN0[ch], BWf[ch], BKBtr[ch], BST[ch], BS0[ch], BtS[ch], By[ch] = [Buf() for _ in range(9)]
        BANp[ch] = [Buf(), Buf()]
        BXp[ch] = [Buf(), Buf()]
        S.op("dve", lambda: A_.memset(GGb[ch], 0.0), [], [BGG[ch]])
        S.op("dve", lambda: A_.memset(AN0[ch], 0.0), [], [BAN0[ch]])
    ST = [st.sb(f"ST{d}", [128, CH]) for d in range(2)]
    S0m = [st.sb(f"S0m{d}", [128, CH], BF16) for d in range(2)]
    tS = [st.sb(f"tS{d}", [128, CH]) for d in range(2)]
    yacc = [st.sb(f"yacc{d}", [128, T]) for d in range(2)]
    rl = st.sb("rl", [128, T]); k0 = st.sb("k0", [128, T]); k1 = st.sb("k1", [128, T]); vf = st.sb("vf", [128, T]); gg = st.sb("gg", [128, T])
    t0_ = st.sb("t0", [128, T]); t1_ = st.sb("t1", [128, T])
    ogb = st.sb("ogb", [128, T], BF16)
    Brl, Bk0, Bk1, Bvf, Bgg, Bt0, Bt1, Bogb = [Buf() for _ in range(8)]
    MERGE = getattr(self, "rw_merge", True)
    if MERGE:
        mKB = [st.sb(f"mKBt{d}", [128, 2, CH], BF16) for d in range(2)]
        mGG = [st.sb(f"mGG{d}", [128, 2, 128], BF16) for d in range(2)]
        mAN0 = [st.sb(f"mAN0{d}", [128, 2, 128], IDT) for d in range(2)]
        mANp = [[st.sb(f"mANp{q}{d}", [128, 2, 128], IDT) for q in range(2)] for d in range(2)]
        mXp = [[st.sb(f"mX{q}{d}", [128, 2, CH], IDT) for q in range(2)] for d in range(2)]
        mWf = [st.sb(f"mWf{d}", [128, 2, CH], IDT) for d in range(2)]
        mVU = [st.sb(f"mVU{d}", [128, NCH, 2, CH], BF16) for d in range(2)]
        M4x2 = [st.sb(f"M4x2{d}", [128, 2, 128]) for d in range(2)]
        mAx2 = [st.sb(f"mAx2{d}", [128, 2, CH]) for d in range(2)]
        mNx2 = [st.sb(f"mNx2{d}", [128, 2, CH]) for d in range(2)]
        I2 = st.sb("I2", [128, 2, CH])
        Bmk = Buf()
        mBKB, mBGG, mBAN0, mBWf, mBVU, mBST, mBS0, mBtS, mBy = [[Buf(), Buf()] for _ in range(9)]
        mBANp = [[Buf(), Buf()], [Buf(), Buf()]]
        mBXp = [[Buf(), Buf()], [Buf(), Buf()]]
        mUB = [[Buf() for _ in range(4)] for d in range(2)]
        for d in range(2):
            S.op("dve", lambda: A_.memset(mGG[d], 0.0), [], [mBGG[d]])
            S.op("dve", lambda: A_.memset(mAN0[d], 0.0), [], [mBAN0[d]])
            for hd in range(2):
                S.op("dve", lambda: A_.tensor_copy(out=M4x2[d][:, hd, :], in_=(self.masks[:, 0:128] if d == 0 else self.masks[:, 128:256])), [], [Bmk])
                S.op("dve", lambda: A_.tensor_copy(out=mAx2[d][:, hd, :], in_=(self.masks[:, 0:64] if d == 0 else self.masks[:, 128:192])), [], [Bmk])
                S.op("dve", lambda: A_.tensor_copy(out=mNx2[d][:, hd, :], in_=(self.masks[:, 128:192] if d == 0 else self.masks[:, 0:64])), [], [Bmk])
        for hd in range(2):
            S.op("dve", lambda: A_.tensor_copy(out=I2[64:128, hd, :], in_=self.ident[64:128, 64:128]), [], [Bmk])
    R = {}
    BR = {}
    for ci, ch in enumerate(chains):
        b0, b1 = self.PS[2 * ci], self.PS[2 * ci + 1]
        R[ch] = dict(GA=b0[:, 0:128], LV=b0[:, 192:320], Wp=b0[:, 384:448],
                     XL=b1[:, 320:384], Up=b1[:, 448:512], Nn=b1[:, 128:192],
                     Yp=b1[:, 0:64], Sd=b1[:, 64:128], TR=b1.bitcast(BF16)[:, 512:576])
        u0, u1, u2, u3 = Buf(), Buf(), Buf(), Buf()
        ykp = [u2] if ch[0] == 0 else [u3]
        BR[ch] = dict(GAlo=[u0], GAup=[u1], GA=[u0, u1], LV=[u1], Wp=[u1], XL=[u3], Up=[u3], Nn=[u3], Yp=ykp, Sd=ykp, TR=[u2, u3], ALL=[u0, u1, u2, u3])
    up, lo = slice(64, 128), slice(0, 64)
    mU = lambda ap: ap.bitcast(U32)
    cf = list(range(NCH))
    cbk = list(range(TC // CH - 1, -1, -1)) + list(range(NCH - 1, TC // CH - 1, -1))
    order = [cf, cbk]
    dbgn = getattr(self, "rw_dbg", None)
    for b in range(NB):
        cols = slice(b * T, (b + 1) * T)
        for p in range(8):
            if dbgn is not None and (b * 8 + p) >= dbgn[0]:
                continue
            rows = slice(p * 128, (p + 1) * 128)
            for d in range(2):
                S.dma("sp", AB[d], RWD[b, p, d, 0].rearrange("k (c x) -> k c x", x=128), writes=[BABl[d]])
                S.dma("sp", KB[d], RWD[b, p, d, 1].rearrange("k (c x) -> k c x", x=128), writes=[BKBl[d]])
                S.dma("sp", SCs[d], RWS[b, p, d].rearrange("k (a c) -> k a c", a=3), writes=[BSCl[d]])
            S.dma("sp", Vst, Vtm[cols, rows].rearrange("(c s) v -> s c v", s=CH), writes=[BVst])
            S.dma("sp", rl, RWP[0, rows, cols], writes=[Brl])
            S.dma("sp", k0, RWP[1, rows, cols], writes=[Bk0])
            S.dma("sp", k1, RWP[2, rows, cols], writes=[Bk1])
            S.dma("sp", vf, RWP[8, rows, cols], writes=[Bvf])
            S.dma("sp", gg, RWP[9, rows, cols], writes=[Bgg])
            if MERGE:
                for d in range(2):
                    S.op("pool", lambda: nc.gpsimd.tensor_copy(out=mVU[d][lo, :, :, :], in_=Vst.rearrange("s c (h v) -> s c h v", h=2)), [BVst], [mBVU[d]])
                    S.op("dve", lambda: A_.memset(ST[d], 0.0), [], [mBST[d]])
                    S.op("dve", lambda: A_.memset(S0m[d], 0.0), [], [mBS0[d]])

                def dstep(d, step):
                    c = order[d][step]
                    cs = slice(c * CH, (c + 1) * CH)
                    bA, bB, bC, bD = [self.PS[4 * d + q] for q in range(4)]
                    uA, uB, uC, uD = mUB[d]
                    h2 = lambda ap: ap.rearrange("p (h x) -> p h x", h=2)
                    GA = h2(bA[:, 0:256]); LV = h2(bB[:, 0:256]); Wp = h2(bB[:, 256:384])
                    XL = h2(bC[:, 0:128]); Up_ = h2(bC[:, 128:256]); Nn = h2(bC[:, 256:384])
                    Yp = bD[:, 0:64]; Sd = bD[:, 64:128]; TR = h2(bD.bitcast(BF16)[:, 512:640])
                    KP = [slice(0, 64), slice(64, 128)]
                    for hd in range(2):
                        kp = KP[hd]
                        S.op("pe", lambda: nc.tensor.transpose(out=TR[:, hd, :], in_=KB[d][kp, c, :], identity=self.identb[kp, kp]), [BKBl[d]], [uD], pemode=("T", hd))
                        S.op("pe", lambda: nc.tensor.matmul(GA[lo, hd, :], lhsT=KB[d][kp, c, 0:64], rhs=AB[d][kp, c, :], start=True, stop=True), [BKBl[d], BABl[d]], [uA], pemode=("g", hd))
                        S.op("pe", lambda: nc.tensor.matmul(GA[up, hd, :], lhsT=KB[d][kp, c, 64:128], rhs=AB[d][kp, c, :], start=True, stop=True), [BKBl[d], BABl[d]], [uA], pemode=("g", hd))
                        S.op("pe", lambda: nc.tensor.matmul(Nn[up, hd, :], lhsT=AB[d][kp, c, 0:64], rhs=KB[d][kp, c, 64:128], start=True, stop=True), [BKBl[d], BABl[d]], [uC], pemode=("g", hd))
                    yield
                    S.op("act", lambda: nc.scalar.copy(out=mKB[d], in_=TR), [uD], [mBKB[d]])
                    S.op("dve", lambda: A_.copy_predicated(out=mGG[d], mask=mU(M4x2[d][:]), data=GA), [uA, Bmk], [mBGG[d]])
                    S.op("dve", lambda: A_.copy_predicated(out=mAN0[d][up, :, 0:64], mask=mU(mAx2[d][up, :, :]), data=GA[up, :, 0:64]), [uA, Bmk], [mBAN0[d]])
                    S.op("dve", lambda: A_.copy_predicated(out=mAN0[d][up, :, 64:128], mask=mU(mNx2[d][up, :, :]), data=Nn[up, :, :]), [uC, Bmk], [mBAN0[d]])
                    S.op("dve", lambda: A_.tensor_tensor(out=mXp[d][0][up, :, :], in0=I2[up, :, :], in1=mAN0[d][up, :, 0:64], op=ALU.subtract), [mBAN0[d], Bmk], [mBXp[d][0]])
                    yield
                    cur, Bcur = mAN0[d], mBAN0[d]
                    xq = 0
                    for lv in range(1, 7):
                        nx, Bnx = mANp[d][lv % 2], mBANp[d][lv % 2]
                        for hd in range(2):
                            if lv <= 5:
                                if lv < 5:
                                    S.op("pe", lambda: nc.tensor.matmul(LV[up, hd, 0:64], lhsT=cur[up, hd, 64:128], rhs=cur[up, hd, 0:64], start=True, stop=True), [Bcur], [uB], pemode=("f",))
                                S.op("pe", lambda: nc.tensor.matmul(LV[up, hd, 64:128], lhsT=cur[up, hd, 0:64], rhs=cur[up, hd, 64:128], start=True, stop=True), [Bcur], [uB], pemode=("f",))
                            if lv >= 2:
                                S.op("pe", lambda: nc.tensor.matmul(XL[up, hd, :], lhsT=cur[up, hd, 64:128], rhs=mXp[d][xq][up, hd, :], start=True, stop=True), [Bcur, mBXp[d][xq]], [uC], pemode=("f",))
                        yield
                        if lv <= 5:
                            if lv < 5:
                                S.op("act", lambda: nc.scalar.copy(out=nx[up, :, :], in_=LV[up, :, :]), [uB], [Bnx])
                            else:
                                S.op("act", lambda: nc.scalar.copy(out=nx[up, :, 64:128], in_=LV[up, :, 64:128]), [uB], [Bnx])
                        if lv >= 2:
                            S.op("dve", lambda: A_.tensor_tensor(out=mXp[d][1 - xq][up, :, :], in0=XL[up, :, :], in1=mXp[d][xq][up, :, :], op=ALU.add), [uC, mBXp[d][xq]], [mBXp[d][1 - xq]])
                            xq = 1 - xq
                        if lv <= 5:
                            cur, Bcur = nx, Bnx
                        yield
                    for hd in range(2):
                        kp = KP[hd]
                        S.op("pe", lambda: nc.tensor.matmul(Wp[up, hd, :], lhsT=AB[d][kp, c, 0:64], rhs=S0m[d][kp, :], start=True, stop=False), [BABl[d], mBS0[d]], [uB], pemode=("g", hd))
                        S.op("pe", lambda: nc.tensor.matmul(Wp[up, hd, :], lhsT=mGG[d][lo, hd, 0:64], rhs=mVU[d][lo, c, hd, :], start=False, stop=True), [mBGG[d], mBVU[d]], [uB], pemode=("w2",))
                    yield
                    S.op("act", lambda: nc.scalar.copy(out=mWf[d][up, :, :], in_=Wp[up, :, :]), [uB], [mBWf[d]])
                    yield
                    for hd in range(2):
                        S.op("pe", lambda: nc.tensor.matmul(Up_[up, hd, :], lhsT=mXp[d][xq][up, hd, :], rhs=mWf[d][up, hd, :], start=True, stop=True), [mBXp[d][xq], mBWf[d]], [uC], pemode=("f",))
                    yield
                    S.op("act", lambda: nc.scalar.activation(out=mVU[d][up, c, :, :], in_=Up_[up, :, :], func=AF.Copy, scale=-1.0), [uC], [mBVU[d]])
                    yield
                    for hd in range(2):
                        kp = KP[hd]
                        S.op("pe", lambda: nc.tensor.matmul(Yp[kp, :], lhsT=S0m[d][kp, :], rhs=AB[d][kp, c, 64:128], start=True, stop=False), [mBS0[d], BABl[d]], [uD], pemode=("g", hd))
                        S.op("pe", lambda: nc.tensor.matmul(Yp[kp, :], lhsT=mVU[d][:, c, hd, :], rhs=mGG[d][:, hd, 64:128], start=False, stop=True), [mBVU[d], mBGG[d]], [uD], pemode=("full",))
                    for hd in range(2):
                        kp = KP[hd]
                        S.op("pe", lambda: nc.tensor.matmul(Sd[kp, :], lhsT=mKB[d][:, hd, :], rhs=mVU[d][:, c, hd, :], start=True, stop=True), [mBKB[d], mBVU[d]], [uD], pemode=("full",))
                    yield
                    S.op("act", lambda: nc.scalar.copy(out=yacc[d][:, cs], in_=Yp), [uD], [mBy[d]])
                    S.op("act", lambda: nc.scalar.activation(out=tS[d], in_=Sd, func=AF.Identity, scale=SCs[d][:, 2, c:c + 1]), [uD, BSCl[d]], [mBtS[d]])
                    S.op("dve", lambda: A_.scalar_tensor_tensor(out=ST[d], in0=ST[d], scalar=SCs[d][:, 1, c:c + 1], in1=tS[d], op0=ALU.mult, op1=ALU.add), [mBST[d], mBtS[d], BSCl[d]], [mBST[d]])
                    if step + 1 < NCH:
                        cn = order[d][step + 1]
                        S.op("dve", lambda: A_.tensor_scalar(out=S0m[d], in0=ST[d], scalar1=SCs[d][:, 0, cn:cn + 1], scalar2=None, op0=ALU.mult), [mBST[d], BSCl[d]], [mBS0[d]])

                for step in range(NCH if dbgn is None else dbgn[1]):
                    gens = [dstep(d, step) for d in range(2)]
                    while gens:
                        for g_ in list(gens):
                            try:
                                next(g_)
                            except StopIteration:
                                gens.remove(g_)
            else:
                for ch in chains:
                    hd, d = ch
                    kp = slice(hd * 64, hd * 64 + 64)
                    S.op("pool", lambda: nc.gpsimd.tensor_copy(out=VU[ch][lo, :, :], in_=Vst[:, :, hd * 64:(hd + 1) * 64]), [BVst], [BVU[ch]])
                    S.op("dve", lambda: A_.memset(ST[d][kp, :], 0.0), [], [BST[ch]])
                    S.op("dve", lambda: A_.memset(S0m[d][kp, :], 0.0), [], [BS0[ch]])
                def chain_step(ch, step):
                    hd, d = ch
                    kp = slice(hd * 64, hd * 64 + 64)
                    c = order[d][step]
                    cs = slice(c * CH, (c + 1) * CH)
                    r_, br_ = R[ch], BR[ch]
                    M4 = self.masks[:, 0:128] if d == 0 else self.masks[:, 128:256]
                    mA = self.masks[up, 0:64] if d == 0 else self.masks[up, 128:192]
                    mN = self.masks[up, 128:192] if d == 0 else self.masks[up, 0:64]
                    S.op("pe", lambda: nc.tensor.transpose(out=r_["TR"], in_=KB[d][kp, c, :], identity=self.identb[kp, kp]), [BKBl[d]], br_["TR"], pemode=("T", hd))
                    S.op("act", lambda: nc.scalar.copy(out=KBtr[ch], in_=r_["TR"]), br_["TR"], [BKBtr[ch]])
                    S.op("pe", lambda: nc.tensor.matmul(r_["GA"][lo, :], lhsT=KB[d][kp, c, 0:64], rhs=AB[d][kp, c, :], start=True, stop=True), [BKBl[d], BABl[d]], br_["GAlo"], pemode=("g", hd))
                    S.op("pe", lambda: nc.tensor.matmul(r_["GA"][up, :], lhsT=KB[d][kp, c, 64:128], rhs=AB[d][kp, c, :], start=True, stop=True), [BKBl[d], BABl[d]], br_["GAup"], pemode=("g", hd))
                    S.op("pe", lambda: nc.tensor.matmul(r_["Nn"][up, :], lhsT=AB[d][kp, c, 0:64], rhs=KB[d][kp, c, 64:128], start=True, stop=True), [BKBl[d], BABl[d]], br_["Nn"], pemode=("g", hd))
                    yield
                    S.op("dve", lambda: A_.copy_predicated(out=GGb[ch], mask=mU(M4), data=r_["GA"]), br_["GA"], [BGG[ch]])
                    S.op("dve", lambda: A_.copy_predicated(out=AN0[ch][up, 0:64], mask=mU(mA), data=r_["GA"][up, 0:64]), br_["GAup"], [BAN0[ch]])
                    S.op("dve", lambda: A_.copy_predicated(out=AN0[ch][up, 64:128], mask=mU(mN), data=r_["Nn"][up, :]), br_["Nn"], [BAN0[ch]])
                    S.op("dve", lambda: A_.tensor_tensor(out=Xp[ch][0][up, :], in0=self.ident[up, up], in1=AN0[ch][up, 0:64], op=ALU.subtract), [BAN0[ch]], [BXp[ch][0]])
                    yield
                    cur, Bcur = AN0[ch], BAN0[ch]
                    xq = 0
                    for lv in range(1, 7):
                        nx, Bnx = ANp[ch][lv % 2], BANp[ch][lv % 2]
                        if lv <= 5:
                            if lv < 5:
                                S.op("pe", lambda: nc.tensor.matmul(r_["LV"][up, 0:64], lhsT=cur[up, 64:128], rhs=cur[up, 0:64], start=True, stop=True), [Bcur], br_["LV"], pemode=("f",))
                            S.op("pe", lambda: nc.tensor.matmul(r_["LV"][up, 64:128], lhsT=cur[up, 0:64], rhs=cur[up, 64:128], start=True, stop=True), [Bcur], br_["LV"], pemode=("f",))
                        if lv >= 2:
                            S.op("pe", lambda: nc.tensor.matmul(r_["XL"][up, :], lhsT=cur[up, 64:128], rhs=Xp[ch][xq][up, :], start=True, stop=True), [Bcur, BXp[ch][xq]], br_["XL"], pemode=("f",))
                        yield
                        if lv <= 5:
                            if lv < 5:
                                S.op("act", lambda: nc.scalar.copy(out=nx[up, :], in_=r_["LV"][up, :]), br_["LV"], [Bnx])
                            else:
                                S.op("act", lambda: nc.scalar.copy(out=nx[up, 64:128], in_=r_["LV"][up, 64:128]), br_["LV"], [Bnx])
                        if lv >= 2:
                            S.op("dve", lambda: A_.tensor_tensor(out=Xp[ch][1 - xq][up, :], in0=r_["XL"][up, :], in1=Xp[ch][xq][up, :], op=ALU.add), br_["XL"] + [BXp[ch][xq]], [BXp[ch][1 - xq]])
                            xq = 1 - xq
                        if lv <= 5:
                            cur, Bcur = nx, Bnx
                        if lv < 6:
                            yield
                    yield
                    S.op("pe", lambda: nc.tensor.matmul(r_["Wp"][up, :], lhsT=AB[d][kp, c, 0:64], rhs=S0m[d][kp, :], start=True, stop=False), [BABl[d], BS0[ch]], br_["Wp"], pemode=("g", hd))
                    S.op("pe", lambda: nc.tensor.matmul(r_["Wp"][up, :], lhsT=GGb[ch][lo, 0:64], rhs=VU[ch][lo, c, :], start=False, stop=True), [BGG[ch], BVU[ch]], br_["Wp"], pemode=("w2",))
                    yield
                    S.op("act", lambda: nc.scalar.copy(out=Wf[ch][up, :], in_=r_["Wp"][up, :]), br_["Wp"], [BWf[ch]])
                    yield
                    S.op("pe", lambda: nc.tensor.matmul(r_["Up"][up, :], lhsT=Xp[ch][xq][up, :], rhs=Wf[ch][up, :], start=True, stop=True), [BXp[ch][xq], BWf[ch]], br_["Up"], pemode=("f",))
                    yield
                    S.op("act", lambda: nc.scalar.activation(out=VU[ch][up, c, :], in_=r_["Up"][up, :], func=AF.Copy, scale=-1.0), br_["Up"], [BVU[ch]])
                    yield
                    S.op("pe", lambda: nc.tensor.matmul(r_["Yp"][kp, :], lhsT=S0m[d][kp, :], rhs=AB[d][kp, c, 64:128], start=True, stop=False), [BS0[ch], BABl[d]], br_["Yp"], pemode=("g", hd))
                    S.op("pe", lambda: nc.tensor.matmul(r_["Yp"][kp, :], lhsT=VU[ch][:, c, :], rhs=GGb[ch][:, 64:128], start=False, stop=True), [BVU[ch], BGG[ch]], br_["Yp"], pemode=("full",))
                    yield
                    S.op("act", lambda: nc.scalar.copy(out=yacc[d][kp, cs], in_=r_["Yp"][kp, :]), br_["Yp"], [By[ch]])
                    S.op("pe", lambda: nc.tensor.matmul(r_["Sd"][kp, :], lhsT=KBtr[ch], rhs=VU[ch][:, c, :], start=True, stop=True), [BKBtr[ch], BVU[ch]], br_["Sd"], pemode=("full",))
                    yield
                    S.op("act", lambda: nc.scalar.activation(out=tS[d][kp, :], in_=r_["Sd"][kp, :], func=AF.Identity, scale=SCs[d][kp, 2, c:c + 1]), br_["Sd"] + [BSCl[d]], [BtS[ch]])
                    S.op("dve", lambda: A_.scalar_tensor_tensor(out=ST[d][kp, :], in0=ST[d][kp, :], scalar=SCs[d][kp, 1, c:c + 1], in1=tS[d][kp, :], op0=ALU.mult, op1=ALU.add), [BST[ch], BtS[ch], BSCl[d]], [BST[ch]])
                    if step + 1 < NCH:
                        cn = order[d][step + 1]
                        S.op("dve", lambda: A_.tensor_scalar(out=S0m[d][kp, :], in0=ST[d][kp, :], scalar1=SCs[d][kp, 0, cn:cn + 1], scalar2=None, op0=ALU.mult), [BST[ch], BSCl[d]], [BS0[ch]])

                for step in range(NCH if dbgn is None else dbgn[1]):
                    gens = [chain_step(ch, step) for ch in chains]
                    if getattr(self, "rw_order", "phase") == "chain":
                        for g_ in gens:
                            for _ in g_:
                                pass
                        gens = []
                    while gens:
                        for g_ in list(gens):
                            try:
                                next(g_)
                            except StopIteration:
                                gens.remove(g_)
            if MERGE:
                RB = {0: [mUB[0][0]], 1: [mUB[0][1]]}
                By_all = [mBy[0], mBy[1]]
            else:
                RB = {0: [BR[chains[0]]["ALL"][0], BR[chains[0]]["ALL"][1]], 1: [BR[chains[0]]["ALL"][2], BR[chains[0]]["ALL"][3]]}
                By_all = [By[ch] for ch in chains]
            Byy = Buf()
            S.op("dve", lambda: A_.tensor_tensor(out=yacc[0], in0=yacc[0], in1=yacc[1], op=ALU.add), By_all, [Byy])
            NP_ = 6
            W_ = T // NP_
            for pc in range(NP_):
                sl_ = slice(pc * W_, (pc + 1) * W_)
                pb = pc % 2
                S.op("pe", lambda: nc.tensor.matmul(self.PS[pb][:, 0:W_], lhsT=self.blk64, rhs=yacc[0][:, sl_], start=True, stop=True), [Byy], RB[pb])
                S.op("dve", lambda: A_.scalar_tensor_tensor(out=t0_[:, sl_], in0=self.PS[pb][:, 0:W_], scalar=-1.0 / 64, in1=yacc[0][:, sl_], op0=ALU.mult, op1=ALU.add), RB[pb] + [Byy], [Bt0])
            S.op("act", lambda: nc.scalar.activation(out=t1_, in_=t0_, func=AF.Square), [Bt0], [Bt1])
            for pc in range(NP_):
                sl_ = slice(pc * W_, (pc + 1) * W_)
                pb = pc % 2
                S.op("pe", lambda: nc.tensor.matmul(self.PS[pb][:, 0:W_], lhsT=self.blk64, rhs=t1_[:, sl_], start=True, stop=True), [Bt1], RB[pb])
                S.op("act", lambda: nc.scalar.activation(out=yacc[1][:, sl_], in_=self.PS[pb][:, 0:W_], func=AF.Sqrt, scale=1.0 / 64, bias=epsLN), RB[pb] + [Bgl], [Byy])
            S.op("dve", lambda: A_.reciprocal(out=yacc[1], in_=yacc[1]), [Byy], [Byy])
            S.op("dve", lambda: A_.tensor_tensor(out=t0_, in0=t0_, in1=yacc[1], op=ALU.mult), [Bt0, Byy], [Bt0])
            S.op("act", lambda: nc.scalar.activation(out=t0_, in_=t0_, func=AF.Identity, scale=self.pv("rw_ln_w", p), bias=self.pv("rw_ln_b", p)), [Bt0], [Bt0])
            S.op("dve", lambda: A_.tensor_tensor(out=k0, in0=k0, in1=k1, op=ALU.add), [Bk0, Bk1], [Bk0])
            S.op("dve", lambda: A_.scalar_tensor_tensor(out=t1_, in0=rl, scalar=self.pv("rw_r_k", p), in1=k0, op0=ALU.mult, op1=ALU.mult), [Brl, Bk0, Bt1], [Bt1])
            for pc in range(NP_):
                sl_ = slice(pc * W_, (pc + 1) * W_)
                pb = pc % 2
                S.op("pe", lambda: nc.tensor.matmul(self.PS[pb][:, 0:W_], lhsT=self.blk64, rhs=t1_[:, sl_], start=True, stop=True), [Bt1], RB[pb])
                S.op("dve", lambda: A_.tensor_tensor(out=yacc[1][:, sl_], in0=self.PS[pb][:, 0:W_], in1=vf[:, sl_], op=ALU.mult), RB[pb] + [Bvf, Byy], [Byy])
            S.op("dve", lambda: A_.tensor_tensor(out=t0_, in0=t0_, in1=yacc[1], op=ALU.add), [Bt0, Byy], [Bt0])
            S.op("dve", lambda: A_.tensor_tensor(out=ogb, in0=t0_, in1=gg, op=ALU.mult), [Bt0, Bgg], [Bogb])
            S.dma("pool", og[rows, cols], ogb, reads=[Bogb])
            for b_ in By_all:
                b_.r.append(Byy.w)
    st.close()
    S.pe_selfwait = False
    S.pe_drain = 0


Prog.rwkv_scan = _rwkv_scan
```

```python
from contextlib import ExitStack
import numpy as np
import concourse.bass as bass
import concourse.mybir as mybir
from concourse.bass_utils import run_bass_kernel_spmd

F32 = mybir.dt.float32
BF16 = mybir.dt.bfloat16
AF = mybir.ActivationFunctionType
ALU = mybir.AluOpType

NCORES = 8
NB = 2
TC = 256
TL = 2048
T = TC + TL
TT = NB * T
D = 1024
DEPTH = 4
DFF = 2816
NFC = DFF // 128
BLK = 256
NBLK = T // BLK
EPS = 1e-6
CH = 64
NCH = T // CH
RW_LN_EPS = 64e-5
MLA_SCALE = 96 ** -0.5


class Buf:
    __slots__ = ("name", "w", "r")

    def __init__(self, name=""):
        self.name = name
        self.w = None
        self.r = []


class _Eng:
    def __init__(self, S, name, eng):
        self.S = S
        self.name = name
        self.eng = eng
        self.sem = None
        self.count = 0
        self.seen = {}
        self.nsem = 0
        self.ninst = 0
        self.own = set()

    def new_sem(self):
        self.sem = self.S.nc.alloc_semaphore(f"e_{self.name}_{self.nsem}")
        self.own.add(id(self.sem))
        self.nsem += 1
        self.count = 0

    def wait(self, ev):
        sem, val = ev
        k = id(sem)
        if self.name == "pe" and k in self.own and not self.S.pe_selfwait:
            return
        if self.seen.get(k, 0) >= val:
            return
        self.eng.wait_ge(sem, val)
        self.seen[k] = val


class Sched:
    EPOCH = 30000

    def __init__(self, nc, ndma_sems=48):
        self.nc = nc
        self.E = {}
        for name, eng in (("pe", nc.tensor), ("dve", nc.vector), ("act", nc.scalar),
                          ("pool", nc.gpsimd), ("sp", nc.sync)):
            e = _Eng(self, name, eng)
            e.new_sem()
            self.E[name] = e
        self.dsems = [[nc.alloc_semaphore(f"d{i}"), 0] for i in range(ndma_sems)]
        self.dnext = 0
        self._keep = []
        self.pe_selfwait = False
        self.pe_drain = 0
        self.last_pemode = None

    @staticmethod
    def _deps(reads, writes):
        deps = []
        for b in reads:
            if b.w is not None:
                deps.append(b.w)
        for b in writes:
            if b.w is not None:
                deps.append(b.w)
            deps.extend(b.r)
        return deps

    @staticmethod
    def _mark(ev, reads, writes):
        for b in writes:
            b.w = ev
            b.r = []
        for b in reads:
            if b not in writes:
                b.r.append(ev)
                if len(b.r) > 32:
                    b.r = b.r[-32:]

    def op(self, ename, fn, reads=(), writes=(), pemode=None):
        e = self.E[ename]
        for ev in self._deps(reads, writes):
            e.wait(ev)
        drain = False
        if ename == "pe":
            drain = self.pe_drain == 1 or (self.pe_drain == 2 and pemode != self.last_pemode)
            self.last_pemode = pemode
        if drain and e.count > 0:
            k = id(e.sem)
            if e.seen.get(k, 0) < e.count:
                e.eng.wait_ge(e.sem, e.count)
                e.seen[k] = e.count
        if e.count >= self.EPOCH:
            self._keep.append(e.sem)
            e.new_sem()
        inst = fn()
        e.count += 1
        e.ninst += 1
        inst.then_inc(e.sem, 1)
        ev = (e.sem, e.count)
        self._mark(ev, reads, writes)
        return ev

    def dma(self, qname, out, in_, reads=(), writes=(), **kw):
        q = self.E[qname]
        for ev in self._deps(reads, writes):
            q.wait(ev)
        slot = self.dsems[self.dnext % len(self.dsems)]
        self.dnext += 1
        if slot[1] >= self.EPOCH:
            self._keep.append(slot[0])
            slot[0] = self.nc.alloc_semaphore(f"dx{self.dnext}")
            slot[1] = 0
        if slot[1] > 0:
            q.wait((slot[0], slot[1]))
        q.eng.dma_start(out=out, in_=in_, **kw).then_inc(slot[0], 16)
        q.ninst += 1
        slot[1] += 16
        ev = (slot[0], slot[1])
        self._mark(ev, reads, writes)
        return ev

    def barrier(self):
        evs = [(e.sem, e.count) for e in self.E.values() if e.count > 0]
        evs += [(s[0], s[1]) for s in self.dsems if s[1] > 0]
        for e in self.E.values():
            for ev in evs:
                if ev[0] is e.sem:
                    continue
                e.wait(ev)


class PVec:
    def __init__(self):
        self.cols = []
        self.off = {}
        self.n = 0

    def add(self, name, vec):
        vec = np.asarray(vec, dtype=np.float32).reshape(-1)
        assert vec.size % 128 == 0
        nch = vec.size // 128
        self.off[name] = (self.n, nch)
        self.cols.append(np.ascontiguousarray(vec.reshape(nch, 128).T))
        self.n += nch

    def array(self):
        return np.ascontiguousarray(np.concatenate(self.cols, axis=1))


def pvec_layout(inputs):
    pv = PVec()
    for l in range(DEPTH):
        pv.add(f"b_mod{l}", inputs["b_mod"][l])
        pv.add(f"norm1_{l}", inputs["norm1"][l])
        pv.add(f"norm2_{l}", inputs["norm2"][l])
        for k in range(3):
            pv.add(f"conv{l}_{k}", inputs["ffn_conv"][l, k])
        pv.add(f"convb{l}", inputs["ffn_conv_b"][l])
    pv.add("norm_f", inputs["norm_f"])
    for d in range(2):
        for j in range(2):
            pv.add(f"hg_lb{d}_{j}", inputs["hg_lb"][d, j])
    for j in range(2):
        pv.add(f"hg_norm{j}", inputs["hg_norm"][j])
    for k in range(6):
        pv.add(f"rw_mu{k}", inputs["rw_mu"][0, k])
    for d in range(2):
        pv.add(f"rw_w0_{d}", inputs["rw_w0"][0, d])
        pv.add(f"rw_a0_{d}", inputs["rw_a0"][0, d])
    for nm in ("rw_k_k", "rw_k_a", "rw_r_k", "rw_ln_w", "rw_ln_b"):
        pv.add(nm, inputs[nm][0])
    pv.add("mla_q_norm", inputs["mla_q_norm"][0])
    pv.add("mla_kv_norm", inputs["mla_kv_norm"][0])
    return pv


def make_consts():
    c = {}
    c["ident"] = np.eye(128, dtype=np.float32)
    c["ones"] = np.ones((128, 128), dtype=np.float32)
    bo = np.zeros((128, 128), dtype=np.float32)
    bo[:64, :64] = 1.0
    bo[64:, 64:] = 1.0
    c["blk64"] = bo
    i = np.arange(64)[:, None]
    t = np.arange(64)[None, :]
    su = (i < t).astype(np.float32)
    iu = (i <= t).astype(np.float32)
    sl = (i > t).astype(np.float32)
    il = (i >= t).astype(np.float32)
    c["masks"] = np.concatenate([np.concatenate([su, iu, sl, il], axis=1)] * 2, axis=0)
    m = np.ones((128, T), dtype=np.float32)
    m[:, ::CH] = 0.0
    c["scanmask"] = m
    nq = 8
    inv_freq = (10000.0 ** (-np.arange(nq, dtype=np.float32) / nq)).astype(np.float32)
    pos = np.arange(TL)
    row = (pos // 64).astype(np.float32)
    col = (pos % 64).astype(np.float32)
    ang_r = row[:, None] * inv_freq
    ang_c = col[:, None] * inv_freq
    ang = np.concatenate([ang_r, ang_r, ang_c, ang_c], axis=-1).astype(np.float32)
    cos = np.ones((32, T), dtype=np.float32)
    sin = np.zeros((32, T), dtype=np.float32)
    cos[:, TC:] = np.cos(ang).T
    sin[:, TC:] = np.sin(ang).T
    c["rope_cos"] = cos
    c["rope_sin"] = sin
    return c


WEIGHT_NAMES = ["w_mod", "ffn_w_in", "ffn_w_out", "hg_w_in", "hg_w_o", "rw_w_rkv", "rw_w1", "rw_w2",
                "rw_a1", "rw_a2", "rw_g1", "rw_g2", "rw_w_o", "mla_w_dqkv", "mla_w_uq", "mla_w_ukv", "mla_w_o"]


class Stage:
    def __init__(self, P, name):
        self.P = P
        self.name = name
        self.es = ExitStack()
        P.nstage += 1
        self.k = 0

    def sb(self, name, shape, dt=F32):
        self.k += 1
        h = self.es.enter_context(self.P.nc.sbuf_tensor(f"{self.name}{self.P.nstage}_{name}_{self.k}", list(shape), dt))
        return h.ap()

    def close(self):
        self.P.S.barrier()
        self.es.close()


class Prog:
    def __init__(self, wshapes, pv_off, npv, dbg=(), xin_name=None):
        nc = bass.Bass("TRN2", target_bir_lowering=False)
        self.nc = nc
        self.dbg = set(dbg)
        self.pv_off = pv_off
        self.nstage = 0
        di = lambda n, s: nc.dram_tensor(n, list(s), F32, kind="ExternalInput").ap()
        self.x = di("x", [NB, TL, D])
        self.ctx = di("ctx", [NB, TC, D])
        self.cvec = di("cvec", [3, D])
        self.pvec_d = di("pvec", [128, npv])
        self.cd = {n: di("c_" + n, s) for n, s in (("ident", [128, 128]), ("ones", [128, 128]), ("blk64", [128, 128]),
                                                    ("masks", [128, 256]), ("scanmask", [128, T]),
                                                    ("rope_cos", [32, T]), ("rope_sin", [32, T]))}
        self.W = {n: di(n, wshapes[n]) for n in WEIGHT_NAMES}
        self.out = nc.dram_tensor("out", [NB, TL, D], F32, kind="ExternalOutput").ap()
        self.scratch = {}
        self.S = Sched(nc)
        S = self.S
        self.PS = [nc.alloc_psum_tensor(f"psb{i}", [128, 512], F32).ap() for i in range(8)]
        self.BPS = [Buf(f"ps{i}") for i in range(8)]
        g = lambda n, s, dt=F32: nc.alloc_sbuf_tensor("g_" + n, list(s), dt).ap()
        self.ident = g("ident", [128, 128])
        self.identb = g("identb", [128, 128], BF16)
        self.onesf = g("onesf", [128, 128])
        self.onesb = g("onesb", [128, 128], BF16)
        self.blk64 = g("blk64", [128, 128])
        self.masks = g("masks", [128, 256])
        self.pvec = g("pvec", [128, npv])
        self.MOD = g("MOD", [128, DEPTH, 48, 3])
        self.MA = g("MA", [128, DEPTH, 2, 8, 3])
        self.epsD = g("epsD", [128, 1])
        self.BC = Buf("consts")
        self.BMOD = Buf("mod")
        S.op("dve", lambda: nc.vector.memset(self.epsD, EPS), [], [self.BC])
        S.dma("sp", self.ident, self.cd["ident"], writes=[self.BC])
        b1, b2, b3, b4, b5, b6 = [Buf() for _ in range(6)]
        S.dma("sp", self.onesf, self.cd["ones"], writes=[b1])
        S.dma("sp", self.blk64, self.cd["blk64"], writes=[b2])
        S.dma("sp", self.masks, self.cd["masks"], writes=[b3])
        S.dma("sp", self.pvec, self.pvec_d, writes=[b4])
        S.dma("pool", self.identb, self.cd["ident"], writes=[b5])
        S.dma("pool", self.onesb, self.cd["ones"], writes=[b6])
        S.barrier()

    def scr(self, name, shape, dt=F32):
        if name not in self.scratch:
            kind = "ExternalOutput" if name in self.dbg else "Internal"
            self.scratch[name] = self.nc.dram_tensor("s_" + name, list(shape), dt, kind=kind).ap()
        return self.scratch[name]

    def pv(self, name, c=None):
        off, nch = self.pv_off[name]
        if c is None:
            return self.pvec[:, off:off + nch]
        return self.pvec[:, off + c:off + c + 1]

    def load_w(self, dst, src, bufs_cols, q="pool"):
        S = self.S
        n = dst.shape[2]
        v = src.rearrange("(kc p) n -> p kc n", p=128)
        bufs = []
        for n0 in range(0, n, 512):
            n1 = min(n, n0 + 512)
            b = Buf()
            S.dma(q, dst[:, :, n0:n1], v[:, :, n0:n1], writes=[b])
            bufs.append(b)
        return bufs

    def prologue_transpose(self, xT):
        nc, S = self.nc, self.S
        st = Stage(self, "pt")
        tin = [st.sb(f"tin{i}", [128, D]) for i in range(2)]
        tout = [st.sb(f"tout{i}", [128, 8, 128]) for i in range(2)]
        Bin = [Buf(), Buf()]
        Bout = [Buf(), Buf()]
        xTv = xT.rearrange("(c p) t -> p c t", p=128)
        tiles = []
        for b in range(NB):
            for k in range(T // 128):
                tiles.append((b, k))

        def src(b, k):
            t0 = k * 128
            if t0 < TC:
                return self.ctx[b, t0:t0 + 128, :]
            return self.x[b, t0 - TC:t0 - TC + 128, :]

        S.dma("sp", tin[0], src(*tiles[0]), writes=[Bin[0]])
        for n, (b, k) in enumerate(tiles):
            i = n % 2
            if n + 1 < len(tiles):
                S.dma("sp", tin[1 - i], src(*tiles[n + 1]), writes=[Bin[1 - i]])
            for hf in range(2):
                pb = 2 * (n % 2) + hf
                for c4 in range(4):
                    c = hf * 4 + c4
                    S.op("pe", lambda: nc.tensor.transpose(out=self.PS[pb][:, c4 * 128:(c4 + 1) * 128], in_=tin[i][:, c * 128:(c + 1) * 128], identity=self.ident),
                         [Bin[i]], [self.BPS[pb]])
                eng = "dve" if hf == 0 else "act"
                if hf == 0:
                    S.op("dve", lambda: nc.vector.tensor_copy(out=tout[i][:, 0:4, :], in_=self.PS[pb][:].rearrange("p (c t) -> p c t", c=4)), [self.BPS[pb]], [Bout[i]])
                else:
                    S.op("act", lambda: nc.scalar.copy(out=tout[i][:, 4:8, :], in_=self.PS[pb][:].rearrange("p (c t) -> p c t", c=4)), [self.BPS[pb]], [Bout[i]])
            col = b * T + k * 128
            S.dma("pool", xTv[:, :, col:col + 128], tout[i], reads=[Bout[i]])
        st.close()

    def prologue_mod(self):
        nc, S = self.nc, self.S
        st = Stage(self, "pm")
        cv = st.sb("cv", [3, D])
        sc = st.sb("sc", [3, D])
        scT = st.sb("scT", [128, 8, 3])
        Bcv, Bsc, BscT = Buf(), Buf(), Buf()
        S.dma("sp", cv, self.cvec, writes=[Bcv])
        S.op("act", lambda: nc.scalar.activation(out=sc, in_=cv, func=AF.Silu), [Bcv], [Bsc])
        for kc in range(8):
            S.op("pe", lambda: nc.tensor.transpose(out=self.PS[0][:, kc * 4:kc * 4 + 3], in_=sc[0:3, kc * 128:(kc + 1) * 128], identity=self.ident[0:3, 0:3]),
                 [Bsc], [self.BPS[0]])
        S.op("dve", lambda: nc.vector.tensor_copy(out=scT, in_=self.PS[0][:, 0:32].rearrange("p (k f) -> p k f", f=4)[:, :, 0:3]), [self.BPS[0]], [BscT])
        NWB = 4
        wt = [st.sb(f"wt{i}", [128, 8, 512]) for i in range(NWB)]
        Bwt = [Buf() for _ in range(NWB)]
        groups = [(l, g) for l in range(DEPTH) for g in range(12)]

        def wsrc(l, g):
            return self.W["w_mod"][l].rearrange("(kc p) n -> p kc n", p=128)[:, :, g * 512:(g + 1) * 512]

        def wload(n):
            S.dma("sp" if n % 2 == 0 else "act", wt[n % NWB], wsrc(*groups[n]), writes=[Bwt[n % NWB]])

        for n in range(NWB - 1):
            wload(n)
        for n, (l, g) in enumerate(groups):
            i = n % NWB
            if n + NWB - 1 < len(groups):
                wload(n + NWB - 1)
            pb = 1 + (n % 2)
            for oc in range(4):
                for kc in range(8):
                    S.op("pe", lambda: nc.tensor.matmul(self.PS[pb][:, oc * 4:oc * 4 + 3], lhsT=wt[i][:, kc, oc * 128:(oc + 1) * 128], rhs=scT[:, kc, :], start=(kc == 0), stop=(kc == 7)),
                         [Bwt[i], BscT], [self.BPS[pb]])
            boff, _ = self.pv_off[f"b_mod{l}"]
            bias = self.pvec[:, boff + g * 4:boff + g * 4 + 4].unsqueeze(2).to_broadcast([128, 4, 3])
            S.op("dve", lambda: nc.vector.tensor_tensor(out=self.MOD[:, l, g * 4:(g + 1) * 4, :], in0=self.PS[pb][:, 0:16].rearrange("p (o f) -> p o f", f=4)[:, :, 0:3], in1=bias, op=ALU.add),
                 [self.BPS[pb]], [self.BMOD])
        for l in range(DEPTH):
            for w in range(2):
                sc_idx = 8 if w == 0 else 32
                nrm = self.pv(f"norm{w + 1}_{l}").unsqueeze(2).to_broadcast([128, 8, 3])
                S.op("dve", lambda: nc.vector.scalar_tensor_tensor(out=self.MA[:, l, w, :, :], in0=self.MOD[:, l, sc_idx:sc_idx + 8, :], scalar=1.0, in1=nrm, op0=ALU.add, op1=ALU.mult),
                     [self.BMOD], [self.BMOD])
        st.close()

    def norm_tiles(self, st, n=BLK + 2):
        return dict(sq=st.sb("nsq", [128, 8, n], BF16), tmp=st.sb("ntmp", [128, 8, n]), r0=st.sb("nr0", [128, n]), r1=st.sb("nr1", [128, n]),
                    B=[Buf() for _ in range(4)])

    def norm_block(self, nt, xs, Bxs, n, A, Bsh, hb, Bhb, bank):
        nc, S = self.nc, self.S
        sq, tmp, r0, r1 = nt["sq"], nt["tmp"], nt["r0"], nt["r1"]
        Bsq, Btmp, Br0, Br1 = nt["B"]
        S.op("act", lambda: nc.scalar.activation(out=sq[:, :, :n], in_=xs, func=AF.Square), [Bxs], [Bsq])
        ps = self.PS[bank]
        for c in range(8):
            S.op("pe", lambda: nc.tensor.matmul(ps[:, :n], lhsT=self.onesb, rhs=sq[:, c, :n], start=(c == 0), stop=(c == 7)), [Bsq], [self.BPS[bank]])
        S.op("act", lambda: nc.scalar.activation(out=r0[:, :n], in_=ps[:, :n], func=AF.Sqrt, scale=1.0 / D, bias=self.epsD), [self.BPS[bank]], [Br0])
        S.op("dve", lambda: nc.vector.reciprocal(out=r1[:, :n], in_=r0[:, :n]), [Br0], [Br1])
        S.op("dve", lambda: nc.vector.tensor_tensor(out=tmp[:, :, :n], in0=xs, in1=r1[:, :n].unsqueeze(1).to_broadcast([128, 8, n]), op=ALU.mult), [Bxs, Br1], [Btmp])
        for c in range(8):
            S.op("act", lambda: nc.scalar.activation(out=hb[:, c, :n], in_=tmp[:, c, :n], func=AF.Identity, scale=A[:, c:c + 1], bias=(Bsh[:, c:c + 1] if Bsh is not None else 0.0)),
                 [Btmp, self.BMOD], [Bhb])

    def mod_ab(self, l, w, j):
        A = self.MA[:, l, w, :, j]
        sh = self.MOD[:, l, (0 if w == 0 else 24):(8 if w == 0 else 32), j]
        gt = self.MOD[:, l, (16 if w == 0 else 40):(24 if w == 0 else 48), j]
        return A, sh, gt

    @staticmethod
    def blocks(skip_ctx=False):
        out = []
        for b in range(NB):
            for k in range(NBLK):
                if skip_ctx and k == 0:
                    continue
                out.append((b, k))
        return out

    @staticmethod
    def blk_range(k):
        seq0, seq1 = (0, TC) if k == 0 else (TC, T)
        t0 = k * BLK
        lo = max(t0 - 1, seq0)
        hi = min(t0 + BLK + 1, seq1)
        return t0, lo, hi, (t0 == seq0), (t0 + BLK == seq1)

    def ffn_stage(self, l, xin, xout, skip_ctx):
        nc, S = self.nc, self.S
        st = Stage(self, "ffn")
        Win = st.sb("win", [128, 8, 2 * DFF], BF16)
        Wout = st.sb("wout", [128, NFC, D], BF16)
        BWin = self.load_w(Win, self.W["ffn_w_in"][l], None)
        BWout = []
        osrc = self.W["ffn_w_out"][l].rearrange("(fc p) n -> p fc n", p=128)
        for f0 in range(0, NFC, 2):
            b = Buf()
            S.dma("pool", Wout[:, f0:f0 + 2, :], osrc[:, f0:f0 + 2, :], writes=[b])
            BWout.append(b)
        NH = BLK + 2
        xs = [st.sb(f"xs{i}", [128, 8, NH]) for i in range(2)]
        hb = [st.sb(f"hb{i}", [128, 8, NH], BF16) for i in range(2)]
        gt_ = [st.sb(f"g{i}", [128, NFC, BLK], BF16) for i in range(2)]
        cv = [st.sb(f"cv{i}", [128, BLK]) for i in range(2)]
        sl = [st.sb(f"sl{i}", [128, BLK]) for i in range(2)]
        Bxs, Bhb, Bg, Bcv, Bsl = [[Buf(), Buf()] for _ in range(5)]
        nt = self.norm_tiles(st)
        for i in range(2):
            S.op("dve", lambda: nc.vector.memset(xs[i], 0.0), [], [Bxs[i]])
        xiv = xin.rearrange("(c p) t -> p c t", p=128)
        xov = xout.rearrange("(c p) t -> p c t", p=128)
        blocks = self.blocks(skip_ctx)

        def load(n):
            b, k = blocks[n]
            t0, lo, hi, _, _ = self.blk_range(k)
            S.dma("sp", xs[n % 2][:, :, lo - (t0 - 1):hi - (t0 - 1)], xiv[:, :, b * T + lo:b * T + hi], writes=[Bxs[n % 2]])

        load(0)
        for n, (b, k) in enumerate(blocks):
            i = n % 2
            if n + 1 < len(blocks):
                load(n + 1)
            t0, lo, hi, first, last = self.blk_range(k)
            j = 2 if k == 0 else b
            A, sh, gate = self.mod_ab(l, 1, j)
            self.norm_block(nt, xs[i], Bxs[i], NH, A, sh, hb[i], Bhb[i], 6)
            for fc in range(NFC):
                q = fc % 2
                pa, pvv = self.PS[q], self.PS[2 + q]
                ga = BWin[(fc * 128) // 512]
                gv = BWin[(DFF + fc * 128) // 512]
                for kc in range(8):
                    S.op("pe", lambda: nc.tensor.matmul(pa[:, :NH], lhsT=Win[:, kc, fc * 128:(fc + 1) * 128], rhs=hb[i][:, kc, :], start=(kc == 0), stop=(kc == 7)),
                         [ga, Bhb[i]], [self.BPS[q]])
                for kc in range(8):
                    S.op("pe", lambda: nc.tensor.matmul(pvv[:, :BLK], lhsT=Win[:, kc, DFF + fc * 128:DFF + (fc + 1) * 128], rhs=hb[i][:, kc, 1:1 + BLK], start=(kc == 0), stop=(kc == 7)),
                         [gv, Bhb[i]], [self.BPS[2 + q]])
                w0, w1, w2, cb = self.pv(f"conv{l}_0", fc), self.pv(f"conv{l}_1", fc), self.pv(f"conv{l}_2", fc), self.pv(f"convb{l}", fc)
                S.op("act", lambda: nc.scalar.activation(out=cv[q], in_=pa[:, 1:1 + BLK], func=AF.Identity, scale=w1, bias=cb), [self.BPS[q]], [Bcv[q]])
                c0 = 1 if first else 0
                S.op("dve", lambda: nc.vector.scalar_tensor_tensor(out=cv[q][:, c0:BLK], in0=pa[:, c0:BLK], scalar=w0, in1=cv[q][:, c0:BLK], op0=ALU.mult, op1=ALU.add),
                     [self.BPS[q], Bcv[q]], [Bcv[q]])
                c1 = BLK - 1 if last else BLK
                S.op("dve", lambda: nc.vector.scalar_tensor_tensor(out=cv[q][:, 0:c1], in0=pa[:, 2:2 + c1], scalar=w2, in1=cv[q][:, 0:c1], op0=ALU.mult, op1=ALU.add),
                     [self.BPS[q], Bcv[q]], [Bcv[q]])
                S.op("act", lambda: nc.scalar.activation(out=sl[q], in_=cv[q], func=AF.Silu), [Bcv[q]], [Bsl[q]])
                S.op("dve", lambda: nc.vector.tensor_tensor(out=gt_[i][:, fc, :], in0=sl[q], in1=pvv[:, :BLK], op=ALU.mult), [Bsl[q], self.BPS[2 + q]], [Bg[i]])
            for oc in range(8):
                q = 4 + oc % 2
                po = self.PS[q]
                for fc in range(NFC):
                    S.op("pe", lambda: nc.tensor.matmul(po[:, :BLK], lhsT=Wout[:, fc, oc * 128:(oc + 1) * 128], rhs=gt_[i][:, fc, :], start=(fc == 0), stop=(fc == NFC - 1)),
                         [BWout[fc // 2], Bg[i]], [self.BPS[q]])
                S.op("dve", lambda: nc.vector.scalar_tensor_tensor(out=xs[i][:, oc, 1:1 + BLK], in0=po[:, :BLK], scalar=gate[:, oc:oc + 1], in1=xs[i][:, oc, 1:1 + BLK], op0=ALU.mult, op1=ALU.add),
                     [self.BPS[q], Bxs[i], self.BMOD], [Bxs[i]])
            S.dma("pool", xov[:, :, b * T + t0:b * T + t0 + BLK], xs[i][:, :, 1:1 + BLK], reads=[Bxs[i]])
        st.close()

    def final_stage(self, xin):
        nc, S = self.nc, self.S
        st = Stage(self, "fin")
        xs = [st.sb(f"xs{i}", [128, 8, BLK]) for i in range(2)]
        hb = [st.sb(f"hb{i}", [128, 8, BLK]) for i in range(2)]
        ot = [st.sb(f"ot{i}", [128, D]) for i in range(2)]
        Bxs, Bhb, Bot = [[Buf(), Buf()] for _ in range(3)]
        nt = self.norm_tiles(st, BLK)
        xiv = xin.rearrange("(c p) t -> p c t", p=128)
        blocks = self.blocks(True)
        A = self.pv("norm_f")

        def load(n):
            b, k = blocks[n]
            S.dma("sp", xs[n % 2], xiv[:, :, b * T + k * BLK:b * T + (k + 1) * BLK], writes=[Bxs[n % 2]])

        load(0)
        nt_i = 0
        for n, (b, k) in enumerate(blocks):
            i = n % 2
            if n + 1 < len(blocks):
                load(n + 1)
            self.norm_block(nt, xs[i], Bxs[i], BLK, A, None, hb[i], Bhb[i], 6)
            for tt in range(2):
                o = nt_i % 2
                nt_i += 1
                for hf in range(2):
                    pb = 2 * o + hf
                    for c4 in range(4):
                        c = hf * 4 + c4
                        S.op("pe", lambda: nc.tensor.transpose(out=self.PS[pb][:, c4 * 128:(c4 + 1) * 128], in_=hb[i][:, c, tt * 128:(tt + 1) * 128], identity=self.ident),
                             [Bhb[i]], [self.BPS[pb]])
                    if hf == 0:
                        S.op("dve", lambda: nc.vector.tensor_copy(out=ot[o][:, 0:512], in_=self.PS[pb]), [self.BPS[pb]], [Bot[o]])
                    else:
                        S.op("act", lambda: nc.scalar.copy(out=ot[o][:, 512:1024], in_=self.PS[pb]), [self.BPS[pb]], [Bot[o]])
                tl = k * BLK - TC + tt * 128
                S.dma("pool", self.out[b, tl:tl + 128, :], ot[o], reads=[Bot[o]])
        st.close()


def build_program(wshapes, pv_off, npv, plan=None, dbg=()):
    P = Prog(wshapes, pv_off, npv, dbg=dbg)
    xa = P.scr("xA", [D, TT])
    xb = P.scr("xB", [D, TT])
    if plan is None:
        plan = ["tr", "mod"]
        for l in range(DEPTH):
            plan += [f"mix{l}", f"ffn{l}"]
        plan += ["final"]
    cur, nxt = xa, xb
    for step in plan:
        if step == "tr":
            P.prologue_transpose(cur)
        elif step == "mod":
            P.prologue_mod()
        elif step.startswith("mix"):
            l = int(step[3:])
            P.mixer(l, cur, nxt)
            cur, nxt = nxt, cur
        elif step.startswith("ffn"):
            l = int(step[3:])
            P.ffn_stage(l, cur, nxt, skip_ctx=(l == DEPTH - 1))
            cur, nxt = nxt, cur
        elif step == "final":
            P.final_stage(cur)
    P.S.barrier()
    return P


def prep_inputs(inputs, cores=range(NCORES)):
    pv = pvec_layout(inputs)
    pva = pv.array()
    consts = make_consts()
    shared = {"pvec": pva}
    for k, v in consts.items():
        shared["c_" + k] = v
    for n in WEIGHT_NAMES:
        shared[n] = np.ascontiguousarray(inputs[n], dtype=np.float32)
    in_maps = []
    for c in cores:
        m = dict(shared)
        m["x"] = np.ascontiguousarray(inputs["x"][NB * c:NB * (c + 1)], dtype=np.float32)
        m["ctx"] = np.ascontiguousarray(inputs["ctx"][NB * c:NB * (c + 1)], dtype=np.float32)
        m["cvec"] = np.ascontiguousarray(np.concatenate([inputs["c"][NB * c:NB * (c + 1)], inputs["c_ctx"][None, :]], axis=0), dtype=np.float32)
        in_maps.append(m)
    wshapes = {n: list(inputs[n].shape) for n in WEIGHT_NAMES}
    return in_maps, wshapes, pv.off, pva.shape[1]


def kernel(**inputs):
    inputs = {k: np.asarray(v) for k, v in inputs.items()}
    in_maps, wshapes, pv_off, npv = prep_inputs(inputs)
    P = build_program(wshapes, pv_off, npv)
    res = run_bass_kernel_spmd(P.nc, in_maps, core_ids=list(range(NCORES)))
    out = np.concatenate([np.asarray(r["out"]) for r in res.results], axis=0)
    return out.astype(np.float32)


def _inproj_stage(self, l, xin, Wd, N, dst_fm, tm_specs, f32_h=False):
    nc, S = self.nc, self.S
    st = Stage(self, "ip")
    Wt = st.sb("w", [128, 8, N], BF16)
    BW = self.load_w(Wt, Wd, None)
    xs = [st.sb(f"xs{i}", [128, 8, BLK]) for i in range(2)]
    hb = [st.sb(f"hb{i}", [128, 8, BLK], BF16) for i in range(2)]
    sg = [st.sb(f"sg{i}", [128, 8, BLK]) for i in range(2)]
    tmw = max([nc_ for (_, nc_, _) in tm_specs], default=0)
    tms = [st.sb(f"tm{i}", [128, max(tmw, 1)], BF16) for i in range(2)]
    Bxs, Bhb, Bsg, Btm = [[Buf(), Buf()] for _ in range(4)]
    nt = self.norm_tiles(st, BLK)
    xiv = xin.rearrange("(c p) t -> p c t", p=128)
    dv = dst_fm.rearrange("(c p) t -> p c t", p=128)
    blocks = self.blocks(False)

    def load(n):
        b, k = blocks[n]
        S.dma("sp", xs[n % 2], xiv[:, :, b * T + k * BLK:b * T + (k + 1) * BLK], writes=[Bxs[n % 2]])

    load(0)
    sgi = 0
    tmi = 0
    pbank = 0
    for n, (b, k) in enumerate(blocks):
        i = n % 2
        if n + 1 < len(blocks):
            load(n + 1)
        j = 2 if k == 0 else b
        A, sh, _ = self.mod_ab(l, 0, j)
        self.norm_block(nt, xs[i], Bxs[i], BLK, A, sh, hb[i], Bhb[i], 6)
        col = b * T + k * BLK
        for og in range(N // 1024):
            s_ = sgi % 2
            sgi += 1
            for o8 in range(8):
                oc = og * 8 + o8
                pb = pbank % 4
                pbank += 1
                for kc in range(8):
                    S.op("pe", lambda: nc.tensor.matmul(self.PS[pb][:, :BLK], lhsT=Wt[:, kc, oc * 128:(oc + 1) * 128], rhs=hb[i][:, kc, :], start=(kc == 0), stop=(kc == 7)),
                         [BW[(oc * 128) // 512], Bhb[i]], [self.BPS[pb]])
                if o8 % 2 == 0:
                    S.op("act", lambda: nc.scalar.copy(out=sg[s_][:, o8, :], in_=self.PS[pb][:, :BLK]), [self.BPS[pb]], [Bsg[s_]])
                else:
                    S.op("dve", lambda: nc.vector.tensor_copy(out=sg[s_][:, o8, :], in_=self.PS[pb][:, :BLK]), [self.BPS[pb]], [Bsg[s_]])
            S.dma("pool", dv[:, og * 8:(og + 1) * 8, col:col + BLK], sg[s_], reads=[Bsg[s_]])
        for (c0, ncols, dst_tm) in tm_specs:
            for tt in range(BLK // 128):
                s_ = tmi % 2
                tmi += 1
                for n0 in range(0, ncols, 512):
                    pb = 4 + (pbank % 2)
                    pbank += 1
                    for kc in range(8):
                        S.op("pe", lambda: nc.tensor.matmul(self.PS[pb][:, :512], lhsT=hb[i][:, kc, tt * 128:(tt + 1) * 128], rhs=Wt[:, kc, c0 + n0:c0 + n0 + 512], start=(kc == 0), stop=(kc == 7)),
                             [BW[(c0 + n0) // 512], Bhb[i]], [self.BPS[pb]])
                    S.op("act", lambda: nc.scalar.copy(out=tms[s_][:, n0:n0 + 512], in_=self.PS[pb][:, :512]), [self.BPS[pb]], [Btm[s_]])
                S.dma("pool", dst_tm[col + tt * 128:col + (tt + 1) * 128, :], tms[s_][:, :ncols], reads=[Btm[s_]])
    st.close()


def _outproj_stage(self, l, og, Wd, xin, xout, skip_ctx):
    nc, S = self.nc, self.S
    st = Stage(self, "op")
    Wt = st.sb("w", [128, 8, D], BF16)
    BW = self.load_w(Wt, Wd, None)
    xs = [st.sb(f"xs{i}", [128, 8, BLK]) for i in range(2)]
    ob = [st.sb(f"ob{i}", [128, 8, BLK], BF16) for i in range(2)]
    Bxs, Bob = [[Buf(), Buf()] for _ in range(2)]
    xiv = xin.rearrange("(c p) t -> p c t", p=128)
    xov = xout.rearrange("(c p) t -> p c t", p=128)
    ogv = og.rearrange("(c p) t -> p c t", p=128)
    blocks = self.blocks(skip_ctx)

    def load(n):
        b, k = blocks[n]
        col = b * T + k * BLK
        S.dma("sp", xs[n % 2], xiv[:, :, col:col + BLK], writes=[Bxs[n % 2]])
        S.dma("sp", ob[n % 2], ogv[:, :, col:col + BLK], writes=[Bob[n % 2]])

    load(0)
    for n, (b, k) in enumerate(blocks):
        i = n % 2
        if n + 1 < len(blocks):
            load(n + 1)
        j = 2 if k == 0 else b
        _, _, gate = self.mod_ab(l, 0, j)
        for oc in range(8):
            pb = oc % 4
            for kc in range(8):
                S.op("pe", lambda: nc.tensor.matmul(self.PS[pb][:, :BLK], lhsT=Wt[:, kc, oc * 128:(oc + 1) * 128], rhs=ob[i][:, kc, :], start=(kc == 0), stop=(kc == 7)),
                     [BW[(oc * 128) // 512], Bob[i]], [self.BPS[pb]])
            S.op("dve", lambda: nc.vector.scalar_tensor_tensor(out=xs[i][:, oc, :], in0=self.PS[pb][:, :BLK], scalar=gate[:, oc:oc + 1], in1=xs[i][:, oc, :], op0=ALU.mult, op1=ALU.add),
                 [self.BPS[pb], Bxs[i], self.BMOD], [Bxs[i]])
        col = b * T + k * BLK
        S.dma("pool", xov[:, :, col:col + BLK], xs[i], reads=[Bxs[i]])
    st.close()


def _hgrn2_scan(self, jh, Pfm, Itm, og):
    nc, S = self.nc, self.S
    st = Stage(self, "hs")
    A_ = nc.vector
    LB = st.sb("LB", [128, 2, 8])
    OML = st.sb("OML", [128, 2, 8])
    e0 = st.sb("e0", [128, 8]); e1 = st.sb("e1", [128, 8]); rr = st.sb("rr", [128, 8]); p0 = st.sb("p0", [128, 8]); p1 = st.sb("p1", [128, 8])
    BL = Buf()
    for d in range(2):
        S.op("act", lambda: nc.scalar.activation(out=e0, in_=self.pv(f"hg_lb{d}_0"), func=AF.Exp), [], [BL])
        S.op("act", lambda: nc.scalar.activation(out=e1, in_=self.pv(f"hg_lb{d}_1"), func=AF.Exp), [BL], [BL])
        S.op("dve", lambda: A_.tensor_tensor(out=rr, in0=e0, in1=e1, op=ALU.add), [BL], [BL])
        S.op("dve", lambda: A_.reciprocal(out=rr, in_=rr), [BL], [BL])
        S.op("dve", lambda: A_.tensor_tensor(out=p0, in0=e0, in1=rr, op=ALU.mult), [BL], [BL])
        S.op("dve", lambda: A_.tensor_tensor(out=p1, in0=e1, in1=rr, op=ALU.mult), [BL], [BL])
        if jh == 1:
            S.op("dve", lambda: A_.tensor_tensor(out=p1, in0=p0, in1=p1, op=ALU.add), [BL], [BL])
        else:
            S.op("dve", lambda: A_.tensor_copy(out=p1, in_=p0), [BL], [BL])
        S.op("dve", lambda: A_.tensor_tensor(out=LB[:, d, :], in0=p1, in1=p0, op=ALU.subtract), [BL], [BL])
        S.op("dve", lambda: A_.tensor_scalar(out=OML[:, d, :], in0=LB[:, d, :], scalar1=-1.0, scalar2=1.0, op0=ALU.mult, op1=ALU.add), [BL], [BL])
    smask = st.sb("smask", [128, T])
    Bsm = Buf()
    S.dma("sp", smask, self.cd["scanmask"], writes=[Bsm])
    f32t = lambda n: st.sb(n, [128, T])
    qs = f32t("qs"); graw = f32t("graw"); kk = f32t("kk"); ep = f32t("ep"); en = f32t("en")
    z = [f32t("z0"), f32t("z1")]; bb = [f32t("b0"), f32t("b1")]; of = [f32t("of0"), f32t("of1")]
    qt = [st.sb(f"qt{d}", [128, T], BF16) for d in range(2)]
    kh = [st.sb(f"kh{d}", [128, T], BF16) for d in range(2)]
    sqb = st.sb("sqb", [128, T], BF16)
    ogb = st.sb("ogb", [128, T], BF16)
    Vt = st.sb("Vt", [64, NCH, 128], BF16)
    emid = [st.sb(f"emid{d}", [128, NCH]) for d in range(2)]
    eend = [st.sb(f"eend{d}", [128, NCH]) for d in range(2)]
    eem = [st.sb(f"eem{d}", [128, NCH]) for d in range(2)]
    Sst = [st.sb(f"S{d}", [128, 128]) for d in range(2)]
    Sm = [st.sb(f"Sm{d}", [128, 128], BF16) for d in range(2)]
    tmpS = [st.sb(f"tS{d}", [128, 128]) for d in range(2)]
    khT = [st.sb(f"khT{d}", [64, 128], BF16) for d in range(2)]
    att = [st.sb(f"att{d}", [64, 64], BF16) for d in range(2)]
    Bqs, Bgr, Bkk, Bep, Ben, Bsq, Bog, BVt = [Buf() for _ in range(8)]
    Bz, Bbb, Bof, Bqt, Bkh, Bes, BS, BSm, BtS, BkT, Batt = [[Buf(), Buf()] for _ in range(11)]
    PSb = [self.PS[i].bitcast(BF16) for i in range(8)]
    for d in range(2):
        S.op("dve", lambda: A_.memset(att[d], 0.0), [], [Batt[d]])
    cf = list(range(NCH))
    cb = list(range(TC // CH - 1, -1, -1)) + list(range(NCH - 1, TC // CH - 1, -1))
    order = [cf, cb]
    for b in range(NB):
        for h in range(8):
            rows = slice(h * 128, (h + 1) * 128)
            cols = slice(b * T, (b + 1) * T)
            S.dma("sp", qs, Pfm[0 * D + h * 128:0 * D + (h + 1) * 128, cols], writes=[Bqs])
            S.dma("sp", z[0], Pfm[3 * D + h * 128:3 * D + (h + 1) * 128, cols], writes=[Bz[0]])
            S.dma("sp", z[1], Pfm[4 * D + h * 128:4 * D + (h + 1) * 128, cols], writes=[Bz[1]])
            S.dma("sp", graw, Pfm[2 * D + h * 128:2 * D + (h + 1) * 128, cols], writes=[Bgr])
            S.dma("sp", Vt, Itm[cols, rows].rearrange("(c s) v -> s c v", s=CH), writes=[BVt])
            S.op("act", lambda: nc.scalar.activation(out=qs, in_=qs, func=AF.Silu), [Bqs], [Bqs])
            for d in range(2):
                m_idx = 32 if d == 0 else 31
                zt = z[d]
                S.op("act", lambda: nc.scalar.activation(out=zt, in_=zt, func=AF.Sigmoid), [Bz[d]], [Bz[d]])
                S.op("act", lambda: nc.scalar.activation(out=zt, in_=zt, func=AF.Identity, scale=OML[:, d, h:h + 1], bias=LB[:, d, h:h + 1]), [Bz[d], BL], [Bz[d]])
                S.op("act", lambda: nc.scalar.activation(out=kk, in_=zt, func=AF.Identity, scale=-1.0, bias=self.onesf[:, 0:1]), [Bz[d]], [Bkk])
                S.op("act", lambda: nc.scalar.activation(out=zt, in_=zt, func=AF.Ln), [Bz[d]], [Bz[d]])
                S.op("dve", lambda: A_.tensor_tensor_scan(out=bb[d], data0=smask, data1=zt, initial=0.0, op0=ALU.mult, op1=ALU.add), [Bsm, Bz[d]], [Bbb[d]])
                b3 = bb[d].rearrange("p (c s) -> p c s", s=CH)
                if d == 1:
                    S.op("dve", lambda: A_.tensor_tensor(out=zt, in0=zt, in1=bb[d], op=ALU.subtract), [Bz[d], Bbb[d]], [Bz[d]])
                    S.op("dve", lambda: A_.tensor_tensor(out=ep.rearrange("p (c s) -> p c s", s=CH), in0=zt.rearrange("p (c s) -> p c s", s=CH),
                                                          in1=b3[:, :, CH - 1:CH].to_broadcast([128, NCH, CH]), op=ALU.add), [Bz[d], Bbb[d]], [Bep])
                    S.op("dve", lambda: A_.tensor_copy(out=bb[d], in_=ep), [Bep], [Bbb[d]])
                e_idx = CH - 1 if d == 0 else 0
                S.op("act", lambda: nc.scalar.activation(out=emid[d], in_=b3[:, :, m_idx], func=AF.Exp), [Bbb[d]], [Bes[d]])
                S.op("act", lambda: nc.scalar.activation(out=eend[d], in_=b3[:, :, e_idx], func=AF.Exp), [Bbb[d]], [Bes[d]])
                S.op("dve", lambda: A_.tensor_tensor(out=eem[d], in0=b3[:, :, e_idx], in1=b3[:, :, m_idx], op=ALU.subtract), [Bbb[d]], [Bes[d]])
                S.op("act", lambda: nc.scalar.activation(out=eem[d], in_=eem[d], func=AF.Exp), [Bes[d]], [Bes[d]])
                S.op("dve", lambda: A_.tensor_tensor(out=ep.rearrange("p (c s) -> p c s", s=CH), in0=b3, in1=b3[:, :, m_idx:m_idx + 1].to_broadcast([128, NCH, CH]), op=ALU.subtract),
                     [Bbb[d]], [Bep])
                S.op("act", lambda: nc.scalar.activation(out=en, in_=ep, func=AF.Exp, scale=-1.0), [Bep], [Ben])
                S.op("act", lambda: nc.scalar.activation(out=ep, in_=ep, func=AF.Exp), [Bep], [Bep])
                S.op("dve", lambda: A_.tensor_tensor(out=qt[d], in0=qs, in1=ep, op=ALU.mult), [Bqs, Bep], [Bqt[d]])
                S.op("dve", lambda: A_.tensor_tensor(out=kh[d], in0=kk, in1=en, op=ALU.mult), [Bkk, Ben], [Bkh[d]])
                S.op("dve", lambda: A_.memset(Sst[d], 0.0), [], [BS[d]])
                S.op("dve", lambda: A_.memset(Sm[d], 0.0), [], [BSm[d]])
            def hstep(d, step):
                c = order[d][step]
                cs = slice(c * CH, (c + 1) * CH)
                pb = d * 4
                mk = (self.masks[0:64, 64:128] if d == 0 else self.masks[0:64, 192:256]).bitcast(mybir.dt.uint32)
                S.op("pe", lambda: nc.tensor.transpose(out=PSb[pb][0:64, 0:128], in_=kh[d][:, cs], identity=self.identb), [Bkh[d]], [self.BPS[pb]])
                S.op("pe", lambda: nc.tensor.matmul(self.PS[pb + 1][0:64, 0:64], lhsT=kh[d][:, cs], rhs=qt[d][:, cs], start=True, stop=True), [Bkh[d], Bqt[d]], [self.BPS[pb + 1]])
                yield
                S.op("act", lambda: nc.scalar.copy(out=khT[d], in_=PSb[pb][0:64, 0:128]), [self.BPS[pb]], [BkT[d]])
                S.op("dve", lambda: A_.copy_predicated(out=att[d], mask=mk, data=self.PS[pb + 1][0:64, 0:64]), [self.BPS[pb + 1]], [Batt[d]])
                S.op("pe", lambda: nc.tensor.matmul(self.PS[pb + 2][:, 0:64], lhsT=Vt[:, c, :], rhs=att[d], start=True, stop=False), [BVt, Batt[d]], [self.BPS[pb + 2]])
                S.op("pe", lambda: nc.tensor.matmul(self.PS[pb + 2][:, 0:64], lhsT=Sm[d], rhs=qt[d][:, cs], start=False, stop=True), [BSm[d], Bqt[d]], [self.BPS[pb + 2]])
                S.op("pe", lambda: nc.tensor.matmul(self.PS[pb + 3][:, 0:128], lhsT=khT[d], rhs=Vt[:, c, :], start=True, stop=True), [BkT[d], BVt], [self.BPS[pb + 3]])
                yield
                S.op("act", lambda: nc.scalar.activation(out=tmpS[d], in_=self.PS[pb + 3][:, 0:128], func=AF.Identity, scale=eem[d][:, c:c + 1]), [self.BPS[pb + 3], Bes[d]], [BtS[d]])
                S.op("dve", lambda: A_.scalar_tensor_tensor(out=Sst[d], in0=Sst[d], scalar=eend[d][:, c:c + 1], in1=tmpS[d], op0=ALU.mult, op1=ALU.add), [BS[d], BtS[d], Bes[d]], [BS[d]])
                S.op("act", lambda: nc.scalar.copy(out=of[d][:, cs], in_=self.PS[pb + 2][:, 0:64]), [self.BPS[pb + 2]], [Bof[d]])
                if step + 1 < NCH:
                    cn = order[d][step + 1]
                    S.op("dve", lambda: A_.tensor_scalar(out=Sm[d], in0=Sst[d], scalar1=emid[d][:, cn:cn + 1], scalar2=None, op0=ALU.mult), [BS[d], Bes[d]], [BSm[d]])

            for step in range(NCH):
                gens = [hstep(d, step) for d in range(2)]
                while gens:
                    for g_ in list(gens):
                        try:
                            next(g_)
                        except StopIteration:
                            gens.remove(g_)
            S.op("dve", lambda: A_.tensor_tensor(out=of[0], in0=of[0], in1=of[1], op=ALU.add), [Bof[0], Bof[1]], [Bof[0]])
            S.op("act", lambda: nc.scalar.activation(out=sqb, in_=of[0], func=AF.Square), [Bof[0]], [Bsq])
            for pc in range(6):
                sl_ = slice(pc * 384, (pc + 1) * 384)
                pb = pc % 2
                S.op("pe", lambda: nc.tensor.matmul(self.PS[pb][:, 0:384], lhsT=self.onesb, rhs=sqb[:, sl_], start=True, stop=True), [Bsq], [self.BPS[pb]])
                S.op("act", lambda: nc.scalar.activation(out=ep[:, sl_], in_=self.PS[pb][:, 0:384], func=AF.Sqrt, scale=1.0 / 128, bias=self.epsD), [self.BPS[pb]], [Bep])
            S.op("dve", lambda: A_.reciprocal(out=ep, in_=ep), [Bep], [Bep])
            S.op("dve", lambda: A_.tensor_tensor(out=of[0], in0=of[0], in1=ep, op=ALU.mult), [Bof[0], Bep], [Bof[0]])
            S.op("act", lambda: nc.scalar.activation(out=graw, in_=graw, func=AF.Silu), [Bgr], [Bgr])
            S.op("dve", lambda: A_.scalar_tensor_tensor(out=ogb, in0=of[0], scalar=self.pv(f"hg_norm{jh}", 0), in1=graw, op0=ALU.mult, op1=ALU.mult), [Bof[0], Bgr], [Bog])
            S.dma("pool", og[rows, cols], ogb, reads=[Bog])
    st.close()


def _mixer(self, l, cur, nxt):
    kind, j = l % 3, l // 3
    last = (l == DEPTH - 1)
    og = self.scr("og", [D, TT], BF16)
    if kind == 0:
        Pfm = self.scr("hgP", [5 * D, TT])
        Itm = self.scr("hgI", [TT, D], BF16)
        self.inproj_stage(l, cur, self.W["hg_w_in"][j], 5 * D, Pfm, [(D, D, Itm)])
        self.hgrn2_scan(j, Pfm, Itm, og)
        self.outproj_stage(l, og, self.W["hg_w_o"][j], cur, nxt, last)
    elif kind == 1:
        self.rwkv_mixer(l, cur, og)
        self.outproj_stage(l, og, self.W["rw_w_o"][j], cur, nxt, last)
    else:
        self.mla_mixer(l, cur, og)
        self.outproj_stage(l, og, self.W["mla_w_o"][j], cur, nxt, last)


Prog.inproj_stage = _inproj_stage
Prog.outproj_stage = _outproj_stage
Prog.hgrn2_scan = _hgrn2_scan
Prog.mixer = _mixer


def _mla_mixer(self, l, xin, og):
    nc, S = self.nc, self.S
    A_ = nc.vector
    NH = 16
    QN = self.scr("mlaQN", [96, NH, TT], BF16)
    KN = self.scr("mlaKN", [96, NH, TT], BF16)
    VT = self.scr("mlaVT", [TT, D], BF16)
    st = Stage(self, "m1")
    Wd = st.sb("wd", [128, 8, 544], BF16)
    Wq = st.sb("wq", [128, 2, 1536], BF16)
    Wk = st.sb("wk", [128, 2, 2048], BF16)
    Wdr = st.sb("wdr", [128, 8, 32], BF16)
    Wqr = st.sb("wqr", [128, 2, NH, 32], BF16)
    BWd, BWq, BWk, BWr = Buf(), Buf(), Buf(), Buf()
    S.dma("pool", Wd, self.W["mla_w_dqkv"][0].rearrange("(kc p) n -> p kc n", p=128), writes=[BWd])
    wqv = self.W["mla_w_uq"][0].rearrange("(kc p) n -> p kc n", p=128)
    for i3 in range(3):
        S.dma("pool", Wq[:, :, i3 * 512:(i3 + 1) * 512], wqv[:, :, i3 * 512:(i3 + 1) * 512], writes=[BWq])
    wkv = self.W["mla_w_ukv"][0].rearrange("(kc p) n -> p kc n", p=128)
    for i4 in range(4):
        S.dma("pool", Wk[:, :, i4 * 512:(i4 + 1) * 512], wkv[:, :, i4 * 512:(i4 + 1) * 512], writes=[BWk])
    Wq4 = Wq.rearrange("p k (h c) -> p k h c", c=96)
    for seg in range(2):
        for half in range(2):
            sgn = -1.0 if half == 0 else 1.0
            so = 64 + seg * 16 + (1 - half) * 8
            do = seg * 16 + half * 8
            S.op("act", lambda: nc.scalar.activation(out=Wqr[:, :, :, do:do + 8], in_=Wq4[:, :, :, so:so + 8], func=AF.Copy, scale=sgn), [BWq], [BWr])
            so2 = 512 + seg * 16 + (1 - half) * 8
            S.op("act", lambda: nc.scalar.activation(out=Wdr[:, :, do:do + 8], in_=Wd[:, :, so2:so2 + 8], func=AF.Copy, scale=sgn), [BWd], [BWr])
    cos = st.sb("cos", [96, T]); sin = st.sb("sin", [96, T])
    Bcs = Buf()
    RP = slice(64, 96)
    S.dma("sp", cos[RP, :], self.cd["rope_cos"], writes=[Bcs])
    S.dma("sp", sin[RP, :], self.cd["rope_sin"], writes=[Bcs])
    xs = [st.sb(f"xs{i}", [128, 8, BLK]) for i in range(2)]
    hb = [st.sb(f"hb{i}", [128, 8, BLK], BF16) for i in range(2)]
    Bxs, Bhb = [[Buf(), Buf()] for _ in range(2)]
    nt = self.norm_tiles(st, BLK)
    cs_ = st.sb("cs", [128, 4, BLK]); csq = st.sb("csq", [128, 4, BLK], BF16); cn = st.sb("cn", [128, 4, BLK], BF16)
    rr0 = st.sb("rr0", [128, 2, BLK]); rr1 = st.sb("rr1", [128, 2, BLK]); ctmp = st.sb("ctmp", [128, 4, BLK])
    Bcs_, Bcsq, Bcn, Brr, Bct = [Buf() for _ in range(5)]
    qn_s = [st.sb(f"qns{i}", [96, NH, BLK], BF16) for i in range(2)]
    kn_s = [st.sb(f"kns{i}", [96, NH, BLK], BF16) for i in range(2)]
    vt_s = [st.sb(f"vts{i}", [128, D], BF16) for i in range(2)]
    t1 = st.sb("t1", [96, 2, BLK]); t2 = st.sb("t2", [96, 2, BLK])
    Bt1, Bt2 = Buf(), Buf()
    Bqn, Bkn, Bqr, Bkr, Bvt = [[Buf(), Buf()] for _ in range(5)]
    xiv = xin.rearrange("(c p) t -> p c t", p=128)
    blocks = self.blocks(False)

    def load(n):
        b, k = blocks[n]
        S.dma("sp", xs[n % 2], xiv[:, :, b * T + k * BLK:b * T + (k + 1) * BLK], writes=[Bxs[n % 2]])

    load(0)
    vti = 0
    for n, (b, k) in enumerate(blocks):
        i = n % 2
        if n + 1 < len(blocks):
            load(n + 1)
        j = 2 if k == 0 else b
        A, sh, _ = self.mod_ab(l, 0, j)
        self.norm_block(nt, xs[i], Bxs[i], BLK, A, sh, hb[i], Bhb[i], 6)
        col = b * T + k * BLK
        tcol = slice(k * BLK, (k + 1) * BLK)
        for c4 in range(4):
            pb = c4 // 2
            for kc in range(8):
                S.op("pe", lambda: nc.tensor.matmul(self.PS[pb][:, (c4 % 2) * BLK:(c4 % 2 + 1) * BLK], lhsT=Wd[:, kc, c4 * 128:(c4 + 1) * 128], rhs=hb[i][:, kc, :], start=(kc == 0), stop=(kc == 7)),
                     [BWd, Bhb[i]], [self.BPS[pb]])
        for kc in range(8):
            S.op("pe", lambda: nc.tensor.matmul(self.PS[2][RP, 0:BLK], lhsT=Wd[:, kc, 512:544], rhs=hb[i][:, kc, :], start=(kc == 0), stop=(kc == 7)), [BWd, Bhb[i]], [self.BPS[2]])
        for kc in range(8):
            S.op("pe", lambda: nc.tensor.matmul(self.PS[2][RP, BLK:2 * BLK], lhsT=Wdr[:, kc, :], rhs=hb[i][:, kc, :], start=(kc == 0), stop=(kc == 7)), [BWr, Bhb[i]], [self.BPS[2]])
        for pb in range(2):
            S.op("act", lambda: nc.scalar.copy(out=cs_[:, 2 * pb:2 * pb + 2, :], in_=self.PS[pb].rearrange("p (c t) -> p c t", c=2)), [self.BPS[pb]], [Bcs_])
            S.op("act", lambda: nc.scalar.activation(out=csq[:, 2 * pb:2 * pb + 2, :], in_=self.PS[pb].rearrange("p (c t) -> p c t", c=2), func=AF.Square), [self.BPS[pb]], [Bcsq])
        S.op("dve", lambda: A_.tensor_tensor(out=t1[RP, 0, :], in0=self.PS[2][RP, 0:BLK], in1=cos[RP, tcol], op=ALU.mult), [self.BPS[2], Bcs], [Bt1])
        S.op("dve", lambda: A_.tensor_tensor(out=t2[RP, 0, :], in0=self.PS[2][RP, BLK:2 * BLK], in1=sin[RP, tcol], op=ALU.mult), [self.BPS[2], Bcs], [Bt2])
        S.op("dve", lambda: A_.tensor_tensor(out=kn_s[i][RP, :, :], in0=t1[RP, 0:1, :].to_broadcast([32, NH, BLK]), in1=t2[RP, 0:1, :].to_broadcast([32, NH, BLK]), op=ALU.add), [Bt1, Bt2], [Bkn[i]])
        for w in range(2):
            for c in range(2):
                S.op("pe", lambda: nc.tensor.matmul(self.PS[3][:, w * BLK:(w + 1) * BLK], lhsT=self.onesb, rhs=csq[:, 2 * w + c, :], start=(c == 0), stop=(c == 1)), [Bcsq], [self.BPS[3]])
        S.op("act", lambda: nc.scalar.activation(out=rr0, in_=self.PS[3].rearrange("p (w t) -> p w t", w=2), func=AF.Sqrt, scale=1.0 / 256, bias=self.epsD), [self.BPS[3]], [Brr])
        S.op("dve", lambda: A_.reciprocal(out=rr1, in_=rr0), [Brr], [Brr])
        S.op("dve", lambda: A_.tensor_tensor(out=ctmp.rearrange("p (w c) t -> p w c t", w=2), in0=cs_.rearrange("p (w c) t -> p w c t", w=2),
                                              in1=rr1.unsqueeze(2).to_broadcast([128, 2, 2, BLK]), op=ALU.mult), [Bcs_, Brr], [Bct])
        for c4 in range(4):
            gname = "mla_q_norm" if c4 < 2 else "mla_kv_norm"
            S.op("act", lambda: nc.scalar.activation(out=cn[:, c4, :], in_=ctmp[:, c4, :], func=AF.Identity, scale=self.pv(gname, c4 % 2)), [Bct], [Bcn])
        for hp in range(8):
            for which in range(2):
                pb = 4 + (2 * hp + which) % 2
                Wt_, coff, hw, ci = (Wq, 0, 96, 0) if which == 0 else (Wk, 0, 128, 2)
                for hh in range(2):
                    h = 2 * hp + hh
                    for kc in range(2):
                        S.op("pe", lambda: nc.tensor.matmul(self.PS[pb][0:64, hh * BLK:(hh + 1) * BLK], lhsT=Wt_[:, kc, h * hw:h * hw + 64], rhs=cn[:, ci + kc, :], start=(kc == 0), stop=(kc == 1)),
                             [BWq if which == 0 else BWk, Bcn], [self.BPS[pb]])
                dst = qn_s[i] if which == 0 else kn_s[i]
                Bd = Bqn[i] if which == 0 else Bkn[i]
                if which == 0:
                    S.op("act", lambda: nc.scalar.copy(out=dst[0:64, 2 * hp:2 * hp + 2, :], in_=self.PS[pb][0:64, :].rearrange("p (h t) -> p h t", h=2)), [self.BPS[pb]], [Bd])
                else:
                    S.op("dve", lambda: A_.tensor_copy(out=dst[0:64, 2 * hp:2 * hp + 2, :], in_=self.PS[pb][0:64, :].rearrange("p (h t) -> p h t", h=2)), [self.BPS[pb]], [Bd])
            for hh in range(2):
                h = 2 * hp + hh
                for kc in range(2):
                    S.op("pe", lambda: nc.tensor.matmul(self.PS[6][RP, hh * BLK:(hh + 1) * BLK], lhsT=Wq[:, kc, h * 96 + 64:h * 96 + 96], rhs=cn[:, kc, :], start=(kc == 0), stop=(kc == 1)), [BWq, Bcn], [self.BPS[6]])
                for kc in range(2):
                    S.op("pe", lambda: nc.tensor.matmul(self.PS[7][RP, hh * BLK:(hh + 1) * BLK], lhsT=Wqr[:, kc, h, :], rhs=cn[:, kc, :], start=(kc == 0), stop=(kc == 1)), [BWr, Bcn], [self.BPS[7]])
            cosb = cos[RP, tcol].unsqueeze(1).to_broadcast([32, 2, BLK])
            sinb = sin[RP, tcol].unsqueeze(1).to_broadcast([32, 2, BLK])
            S.op("dve", lambda: A_.tensor_tensor(out=t1[RP, :, :], in0=self.PS[6][RP, :].rearrange("p (h t) -> p h t", h=2), in1=cosb, op=ALU.mult), [self.BPS[6], Bcs], [Bt1])
            S.op("dve", lambda: A_.tensor_tensor(out=t2[RP, :, :], in0=self.PS[7][RP, :].rearrange("p (h t) -> p h t", h=2), in1=sinb, op=ALU.mult), [self.BPS[7], Bcs], [Bt2])
            S.op("dve", lambda: A_.tensor_tensor(out=qn_s[i][RP, 2 * hp:2 * hp + 2, :], in0=t1[RP, :, :], in1=t2[RP, :, :], op=ALU.add), [Bt1, Bt2], [Bqn[i]])
        S.dma("pool", QN[:, :, col:col + BLK], qn_s[i], reads=[Bqn[i]])
        S.dma("pool", KN[:, :, col:col + BLK], kn_s[i], reads=[Bkn[i]])
        Wkv = Wk.rearrange("p k (h c) -> p k h c", c=128)
        for tt in range(BLK // 128):
            vi = vti % 2
            vti += 1
            for hf in range(2):
                pb = 4 + hf
                for kc in range(2):
                    S.op("pe", lambda: nc.tensor.matmul(self.PS[pb][:, 0:512], lhsT=cn[:, 2 + kc, tt * 128:(tt + 1) * 128], rhs=Wkv[:, kc, hf * 8:(hf + 1) * 8, 64:128], start=(kc == 0), stop=(kc == 1)),
                         [BWk, Bcn], [self.BPS[pb]])
                S.op("act", lambda: nc.scalar.copy(out=vt_s[vi][:, hf * 512:(hf + 1) * 512], in_=self.PS[pb][:, 0:512]), [self.BPS[pb]], [Bvt[vi]])
            S.dma("pool", VT[col + tt * 128:col + (tt + 1) * 128, :], vt_s[vi], reads=[Bvt[vi]])
    st.close()
    st = Stage(self, "m2")
    NKT = T // 128
    Vall = st.sb("Vall", [128, NKT, D], BF16)
    KNh = [st.sb(f"KNh{i}", [96, T], BF16) for i in range(2)]
    QNh = [st.sb(f"QNh{i}", [96, T], BF16) for i in range(2)]
    VX = [st.sb(f"VX{i}", [128, NKT, 65], BF16) for i in range(2)]
    PT = [st.sb(f"PT{i}", [128, 512], BF16) for i in range(3)]
    rd = st.sb("rd", [65, 512]); rb = [st.sb(f"rb{i}", [64, 512]) for i in range(2)]
    ob = [st.sb(f"ob{i}", [64, 512], BF16) for i in range(2)]
    BVa, BKR, Brd = Buf(), Buf(), Buf()
    BKN, BQN, BQR, BVX, Brb, Bob = [[Buf(), Buf()] for _ in range(6)]
    BPT = [Buf() for _ in range(3)]
    for i in range(2):
        S.op("pool", lambda: nc.gpsimd.memset(VX[i], 1.0), [], [BVX[i]])
    qblocks = [(0, TC, 2)] + [(TC + qb * 512, 512, NKT) for qb in range(4)]
    pti = 0
    hn = 0
    for b in range(NB):
        c0 = b * T
        S.dma("sp", Vall, VT[c0:c0 + T, :].rearrange("(kt p) v -> p kt v", p=128), writes=[BVa])
        for h in range(NH):
            i = hn % 2
            hn += 1
            S.dma("sp", KNh[i], KN[:, h, c0:c0 + T], writes=[BKN[i]])
            S.dma("sp", QNh[i], QN[:, h, c0:c0 + T], writes=[BQN[i]])
            S.op("pool", lambda: nc.gpsimd.tensor_copy(out=VX[i][:, :, 0:64], in_=Vall[:, :, h * 64:(h + 1) * 64]), [BVa], [BVX[i]])
            for qi, (q0, nq, nkt) in enumerate(qblocks):
                po = 4 + (qi % 2)

                def score(kt):
                    ps = kt % 4
                    ks = slice(kt * 128, (kt + 1) * 128)
                    S.op("pe", lambda: nc.tensor.matmul(self.PS[ps][:, 0:nq], lhsT=KNh[i][:, ks], rhs=QNh[i][:, q0:q0 + nq], start=True, stop=True), [BKN[i], BQN[i]], [self.BPS[ps]])

                score(0)
                if nkt > 1:
                    score(1)
                for kt in range(nkt):
                    ps = kt % 4
                    p3 = pti % 3
                    pti += 1
                    if kt + 2 < nkt:
                        score(kt + 2)
                    S.op("act", lambda: nc.scalar.activation(out=PT[p3][:, 0:nq], in_=self.PS[ps][:, 0:nq], func=AF.Exp, scale=MLA_SCALE), [self.BPS[ps]], [BPT[p3]])
                    S.op("pe", lambda: nc.tensor.matmul(self.PS[po][0:65, 0:nq], lhsT=VX[i][:, kt, :], rhs=PT[p3][:, 0:nq], start=(kt == 0), stop=(kt == nkt - 1)), [BVX[i], BPT[p3]], [self.BPS[po]])
                r2 = qi % 2
                S.op("dve", lambda: A_.reciprocal(out=rd[64:65, 0:nq], in_=self.PS[po][64:65, 0:nq]), [self.BPS[po]], [Brd])
                S.op("pe", lambda: nc.tensor.matmul(self.PS[6 + r2][0:64, 0:nq], lhsT=self.onesf[64:65, 0:64], rhs=rd[64:65, 0:nq], start=True, stop=True), [Brd], [self.BPS[6 + r2]])
                S.op("act", lambda: nc.scalar.copy(out=rb[r2][:, 0:nq], in_=self.PS[6 + r2][0:64, 0:nq]), [self.BPS[6 + r2]], [Brb[r2]])
                S.op("dve", lambda: A_.tensor_tensor(out=ob[r2][:, 0:nq], in0=self.PS[po][0:64, 0:nq], in1=rb[r2][:, 0:nq], op=ALU.mult), [self.BPS[po], Brb[r2]], [Bob[r2]])
                S.dma("pool", og[h * 64:(h + 1) * 64, c0 + q0:c0 + q0 + nq], ob[r2][:, 0:nq], reads=[Bob[r2]])
    st.close()


Prog.mla_mixer = _mla_mixer


RW_ARR = ["r", "kt0", "kt1", "be0", "be1", "kap", "lw0", "lw1", "v", "g"]


def _rwkv_proj(self, l, xin, RWP, Vtm):
    nc, S = self.nc, self.S
    A_ = nc.vector
    st = Stage(self, "r1")
    Wrkv = st.sb("wrkv", [128, 8, 3 * D], BF16)
    BWrkv = []
    for i3 in range(3):
        v_ = self.W["rw_w_rkv"][0, i3].rearrange("(kc p) n -> p kc n", p=128)
        for hf in range(2):
            bb_ = Buf()
            S.dma("pool", Wrkv[:, :, i3 * D + hf * 512:i3 * D + (hf + 1) * 512], v_[:, :, hf * 512:(hf + 1) * 512], writes=[bb_])
            BWrkv.append(bb_)
    W1 = st.sb("w1", [128, 8, 2, 64], BF16); A1 = st.sb("a1", [128, 8, 2, 64], BF16); G1 = st.sb("g1", [128, 8, 160], BF16)
    W2 = st.sb("w2", [64, 2, D], BF16); A2 = st.sb("a2", [64, 2, D], BF16); G2a = st.sb("g2a", [128, D], BF16); G2b = st.sb("g2b", [32, D], BF16)
    Bsw = Buf()
    for d in range(2):
        S.dma("pool", W1[:, :, d, :], self.W["rw_w1"][0, d].rearrange("(kc p) n -> p kc n", p=128), writes=[Bsw])
        S.dma("pool", A1[:, :, d, :], self.W["rw_a1"][0, d].rearrange("(kc p) n -> p kc n", p=128), writes=[Bsw])
        S.dma("pool", W2[:, d, :], self.W["rw_w2"][0, d], writes=[Bsw])
        S.dma("pool", A2[:, d, :], self.W["rw_a2"][0, d], writes=[Bsw])
    S.dma("pool", G1, self.W["rw_g1"][0].rearrange("(kc p) n -> p kc n", p=128), writes=[Bsw])
    S.dma("pool", G2a, self.W["rw_g2"][0, 0:128, :], writes=[Bsw])
    S.dma("pool", G2b, self.W["rw_g2"][0, 128:160, :], writes=[Bsw])
    NH_ = BLK + 2
    xs = [st.sb(f"xs{i}", [128, 8, NH_]) for i in range(2)]
    hf_ = st.sb("hf", [128, 8, NH_])
    dx = st.sb("dx", [128, 8, BLK])
    xj = [st.sb(f"xj{j}", [128, 8, BLK], BF16) for j in range(6)]
    Bxs = [Buf(), Buf()]
    Bhf, Bdx = Buf(), Buf()
    Bxj = [Buf() for _ in range(6)]
    nt = self.norm_tiles(st)
    lt = st.sb("lt", [64, 5, BLK], BF16)
    gh = st.sb("gh", [128, BLK], BF16)
    Blt = Buf()
    stg = [st.sb(f"stg{i}", [128, 10, BLK]) for i in range(2)]
    Bstg = [Buf(), Buf()]
    tmp = [st.sb(f"tmp{i}", [128, BLK]) for i in range(6)]
    Btmp = [Buf() for _ in range(6)]
    sqb = st.sb("sqb", [128, BLK], BF16)
    Bsqb = Buf()
    vts = [st.sb(f"vts{i}", [128, D], BF16) for i in range(2)]
    Bvts = [Buf(), Buf()]
    for i in range(2):
        S.op("dve", lambda: A_.memset(xs[i], 0.0), [], [Bxs[i]])
    xiv = xin.rearrange("(c p) t -> p c t", p=128)
    blocks = self.blocks(False)
    blk64b = st.sb("blk64b", [128, 128], BF16)
    Bb64 = Buf()
    S.op("dve", lambda: A_.tensor_copy(out=blk64b, in_=self.blk64), [], [Bb64])

    def load(n):
        b, k = blocks[n]
        t0, lo, hi, _, _ = self.blk_range(k)
        S.dma("sp", xs[n % 2][:, :, lo - (t0 - 1):hi - (t0 - 1)], xiv[:, :, b * T + lo:b * T + hi], writes=[Bxs[n % 2]])

    load(0)
    si = 0
    vi_ = 0
    pbk = 0
    for n, (b, k) in enumerate(blocks):
        i = n % 2
        if n + 1 < len(blocks):
            load(n + 1)
        t0, lo, hi, first, last = self.blk_range(k)
        j = 2 if k == 0 else b
        A, sh, _ = self.mod_ab(l, 0, j)
        self.norm_block(nt, xs[i], Bxs[i], NH_, A, sh, hf_, Bhf, 6)
        if first:
            S.op("dve", lambda: A_.memset(hf_[:, :, 0:1], 0.0), [], [Bhf])
        if last:
            S.op("dve", lambda: A_.memset(hf_[:, :, NH_ - 1:NH_], 0.0), [], [Bhf])
        S.op("dve", lambda: A_.tensor_tensor(out=dx, in0=hf_[:, :, 0:BLK], in1=hf_[:, :, 2:2 + BLK], op=ALU.add), [Bhf], [Bdx])
        S.op("dve", lambda: A_.scalar_tensor_tensor(out=dx, in0=dx, scalar=0.5, in1=hf_[:, :, 1:1 + BLK], op0=ALU.mult, op1=ALU.subtract), [Bhf, Bdx], [Bdx])
        for jj in range(6):
            for c in range(8):
                S.op("dve", lambda: A_.scalar_tensor_tensor(out=xj[jj][:, c, :], in0=dx[:, c, :], scalar=self.pv(f"rw_mu{jj}", c), in1=hf_[:, c, 1:1 + BLK], op0=ALU.mult, op1=ALU.add),
                     [Bdx, Bhf], [Bxj[jj]])
        for d in range(2):
            for kc in range(8):
                S.op("pe", lambda: nc.tensor.matmul(self.PS[5][0:64, d * BLK:(d + 1) * BLK], lhsT=W1[:, kc, d, :], rhs=xj[1][:, kc, :], start=(kc == 0), stop=(kc == 7)), [Bsw, Bxj[1]], [self.BPS[5]])
        S.op("act", lambda: nc.scalar.activation(out=lt[:, 0:2, :], in_=self.PS[5][0:64, :].rearrange("p (d t) -> p d t", d=2), func=AF.Tanh), [self.BPS[5]], [Blt])
        for d in range(2):
            for kc in range(8):
                S.op("pe", lambda: nc.tensor.matmul(self.PS[5][0:64, d * BLK:(d + 1) * BLK], lhsT=A1[:, kc, d, :], rhs=xj[4][:, kc, :], start=(kc == 0), stop=(kc == 7)), [Bsw, Bxj[4]], [self.BPS[5]])
        S.op("act", lambda: nc.scalar.copy(out=lt[:, 2:4, :], in_=self.PS[5][0:64, :].rearrange("p (d t) -> p d t", d=2)), [self.BPS[5]], [Blt])
        for kc in range(8):
            S.op("pe", lambda: nc.tensor.matmul(self.PS[5][:, 0:BLK], lhsT=G1[:, kc, 0:128], rhs=xj[5][:, kc, :], start=(kc == 0), stop=(kc == 7)), [Bsw, Bxj[5]], [self.BPS[5]])
        for kc in range(8):
            S.op("pe", lambda: nc.tensor.matmul(self.PS[5][0:32, BLK:2 * BLK], lhsT=G1[:, kc, 128:160], rhs=xj[5][:, kc, :], start=(kc == 0), stop=(kc == 7)), [Bsw, Bxj[5]], [self.BPS[5]])
        S.op("act", lambda: nc.scalar.activation(out=gh, in_=self.PS[5][:, 0:BLK], func=AF.Sigmoid), [self.BPS[5]], [Blt])
        S.op("act", lambda: nc.scalar.activation(out=lt[0:32, 4, :], in_=self.PS[5][0:32, BLK:2 * BLK], func=AF.Sigmoid), [self.BPS[5]], [Blt])
        col = b * T + t0
        for c in range(8):
            s_ = si % 2
            si += 1
            sg_ = stg[s_]
            Bs = Bstg[s_]
            cs = slice(c * 128, (c + 1) * 128)

            def bank():
                nonlocal pbk
                pbk += 1
                return pbk % 5

            prk = []
            for which, xsrc in ((0, 0), (1, 2), (2, 3)):
                pb = bank()
                for kc in range(8):
                    S.op("pe", lambda: nc.tensor.matmul(self.PS[pb][:, 0:BLK], lhsT=Wrkv[:, kc, which * D + c * 128:which * D + (c + 1) * 128], rhs=xj[xsrc][:, kc, :], start=(kc == 0), stop=(kc == 7)),
                         [BWrkv[which * 2 + (c // 4)], Bxj[xsrc]], [self.BPS[pb]])
                prk.append(pb)
            S.op("act", lambda: nc.scalar.copy(out=sg_[:, 0, :], in_=self.PS[prk[0]][:, 0:BLK]), [self.BPS[prk[0]]], [Bs])
            S.op("act", lambda: nc.scalar.copy(out=sg_[:, 8, :], in_=self.PS[prk[2]][:, 0:BLK]), [self.BPS[prk[2]]], [Bs])
            kraw = tmp[0]
            S.op("act", lambda: nc.scalar.copy(out=kraw, in_=self.PS[prk[1]][:, 0:BLK]), [self.BPS[prk[1]]], [Btmp[0]])
            S.op("dve", lambda: A_.tensor_scalar(out=tmp[1], in0=kraw, scalar1=self.pv("rw_k_k", c), scalar2=None, op0=ALU.mult), [Btmp[0]], [Btmp[1]])
            S.op("act", lambda: nc.scalar.activation(out=sqb, in_=tmp[1], func=AF.Square), [Btmp[1]], [Bsqb])
            pb = bank()
            S.op("pe", lambda: nc.tensor.matmul(self.PS[pb][:, 0:BLK], lhsT=blk64b, rhs=sqb, start=True, stop=True), [Bsqb, Bb64], [self.BPS[pb]])
            S.op("act", lambda: nc.scalar.activation(out=tmp[2], in_=self.PS[pb][:, 0:BLK], func=AF.Sqrt), [self.BPS[pb]], [Btmp[2]])
            S.op("dve", lambda: A_.tensor_scalar(out=tmp[2], in0=tmp[2], scalar1=1e-12, scalar2=None, op0=ALU.max), [Btmp[2]], [Btmp[2]])
            S.op("dve", lambda: A_.reciprocal(out=tmp[2], in_=tmp[2]), [Btmp[2]], [Btmp[2]])
            S.op("dve", lambda: A_.tensor_tensor(out=sg_[:, 5, :], in0=tmp[1], in1=tmp[2], op=ALU.mult), [Btmp[1], Btmp[2]], [Bs])
            pb = bank()
            S.op("pe", lambda: nc.tensor.matmul(self.PS[pb][:, 0:BLK], lhsT=G2a[:, cs], rhs=gh, start=True, stop=False), [Bsw, Blt], [self.BPS[pb]])
            S.op("pe", lambda: nc.tensor.matmul(self.PS[pb][:, 0:BLK], lhsT=G2b[:, cs], rhs=lt[0:32, 4, :], start=False, stop=True), [Bsw, Blt], [self.BPS[pb]])
            S.op("act", lambda: nc.scalar.copy(out=sg_[:, 9, :], in_=self.PS[pb][:, 0:BLK]), [self.BPS[pb]], [Bs])
            for d in range(2):
                pb = bank()
                S.op("pe", lambda: nc.tensor.matmul(self.PS[pb][:, 0:BLK], lhsT=W2[:, d, cs], rhs=lt[:, d, :], start=True, stop=True), [Bsw, Blt], [self.BPS[pb]])
                S.op("act", lambda: nc.scalar.activation(out=tmp[3], in_=self.PS[pb][:, 0:BLK], func=AF.Sigmoid, bias=self.pv(f"rw_w0_{d}", c)), [self.BPS[pb]], [Btmp[3]])
                S.op("dve", lambda: A_.tensor_scalar(out=sg_[:, 6 + d, :], in0=tmp[3], scalar1=-float(np.exp(-0.5)), scalar2=None, op0=ALU.mult), [Btmp[3]], [Bs])
                pb = bank()
                S.op("pe", lambda: nc.tensor.matmul(self.PS[pb][:, 0:BLK], lhsT=A2[:, d, cs], rhs=lt[:, 2 + d, :], start=True, stop=True), [Bsw, Blt], [self.BPS[pb]])
                S.op("act", lambda: nc.scalar.activation(out=tmp[4], in_=self.PS[pb][:, 0:BLK], func=AF.Sigmoid, bias=self.pv(f"rw_a0_{d}", c)), [self.BPS[pb]], [Btmp[4]])
                S.op("dve", lambda: A_.tensor_tensor(out=sg_[:, 3 + d, :], in0=tmp[4], in1=sg_[:, 5, :], op=ALU.mult), [Btmp[4], Bs], [Bs])
                S.op("dve", lambda: A_.tensor_scalar(out=tmp[5], in0=tmp[4], scalar1=-1.0, scalar2=None, op0=ALU.add), [Btmp[4]], [Btmp[5]])
                S.op("dve", lambda: A_.tensor_scalar(out=tmp[5], in0=tmp[5], scalar1=self.pv("rw_k_a", c), scalar2=1.0, op0=ALU.mult, op1=ALU.add), [Btmp[5]], [Btmp[5]])
                S.op("dve", lambda: A_.tensor_tensor(out=sg_[:, 1 + d, :], in0=tmp[5], in1=kraw, op=ALU.mult), [Btmp[5], Btmp[0]], [Bs])
            S.dma("pool", RWP[:, c * 128:(c + 1) * 128, col:col + BLK].rearrange("a p t -> p a t"), sg_, reads=[Bs])
        for tt in range(BLK // 128):
            vi = vi_ % 2
            vi_ += 1
            for hfv in range(2):
                pb = 4 - hfv
                for kc in range(8):
                    S.op("pe", lambda: nc.tensor.matmul(self.PS[pb][:, 0:512], lhsT=xj[3][:, kc, tt * 128:(tt + 1) * 128], rhs=Wrkv[:, kc, 2 * D + hfv * 512:2 * D + (hfv + 1) * 512], start=(kc == 0), stop=(kc == 7)),
                         [BWrkv[4 + hfv], Bxj[3]], [self.BPS[pb]])
                S.op("act", lambda: nc.scalar.copy(out=vts[vi][:, hfv * 512:(hfv + 1) * 512], in_=self.PS[pb][:, 0:512]), [self.BPS[pb]], [Bvts[vi]])
            S.dma("pool", Vtm[col + tt * 128:col + (tt + 1) * 128, :], vts[vi], reads=[Bvts[vi]])
    st.close()


def _rwkv_mixer(self, l, xin, og):
    RWP = self.scr("rwP", [10, D, TT])
    Vtm = self.scr("rwV", [TT, D], BF16)
    self.rwkv_proj(l, xin, RWP, Vtm)
    if getattr(self, "rw_stop", 0) == 1:
        return
    self.rwkv_scan(RWP, Vtm, og)


Prog.rwkv_proj = _rwkv_proj
Prog.rwkv_mixer = _rwkv_mixer


def _rwkv_scan(self, RWP, Vtm, og):
    nc, S = self.nc, self.S
    A_ = nc.vector
    U32 = mybir.dt.uint32
    RWD = self.scr("rwD", [NB, 8, 2, 2, 128, NCH * 128], BF16)
    RWS = self.scr("rwS", [NB, 8, 2, 128, 3 * NCH])
    skipA = getattr(self, "rw_skipA", False)
    st = Stage(self, "r2a")
    smask = st.sb("smask", [128, T])
    Bsm = Buf()
    S.dma("sp", smask, self.cd["scanmask"], writes=[Bsm])
    lw = st.sb("lw", [128, T]); kap = st.sb("kap", [128, T]); rr = st.sb("r", [128, T]); kt = st.sb("kt", [128, T]); be = st.sb("be", [128, T])
    cw = st.sb("cw", [128, T]); cm = st.sb("cm", [128, T]); en = st.sb("en", [128, T]); ex = st.sb("ex", [128, T])
    ABt = [st.sb(f"AB{i}", [128, NCH, 2, CH], BF16) for i in range(2)]
    KBt_ = [st.sb(f"KB{i}", [128, NCH, 2, CH], BF16) for i in range(2)]
    SC = [st.sb(f"SC{i}", [128, 3, NCH]) for i in range(2)]
    Blw, Bkap, Br, Bkt, Bbe, Bcw, Bcm, Ben, Bex = [Buf() for _ in range(9)]
    BAB, BKB, BSC = [[Buf(), Buf()] for _ in range(3)]
    it = 0
    v3 = lambda t_: t_.rearrange("p (c s) -> p c s", s=CH)
    for b in range(0 if skipA else NB):
        cols = slice(b * T, (b + 1) * T)
        for p in range(8):
            rows = slice(p * 128, (p + 1) * 128)
            for d in range(2):
                i = it % 2
                it += 1
                S.dma("sp", lw, RWP[6 + d, rows, cols], writes=[Blw])
                S.dma("sp", kap, RWP[5, rows, cols], writes=[Bkap])
                S.dma("sp", rr, RWP[0, rows, cols], writes=[Br])
                S.dma("sp", kt, RWP[1 + d, rows, cols], writes=[Bkt])
                S.dma("sp", be, RWP[3 + d, rows, cols], writes=[Bbe])
                S.op("dve", lambda: A_.tensor_tensor_scan(out=cw, data0=smask, data1=lw, initial=0.0, op0=ALU.mult, op1=ALU.add), [Bsm, Blw], [Bcw])
                if d == 1:
                    S.op("dve", lambda: A_.tensor_tensor(out=cm, in0=lw, in1=cw, op=ALU.subtract), [Blw, Bcw], [Bcm])
                    S.op("dve", lambda: A_.tensor_tensor(out=v3(en), in0=v3(cm), in1=v3(cw)[:, :, CH - 1:CH].to_broadcast([128, NCH, CH]), op=ALU.add), [Bcm, Bcw], [Ben])
                    S.op("dve", lambda: A_.tensor_copy(out=cw, in_=en), [Ben], [Bcw])
                m_idx = 32 if d == 0 else 31
                e_idx = CH - 1 if d == 0 else 0
                c3 = v3(cw)
                S.op("act", lambda: nc.scalar.activation(out=SC[i][:, 0, :], in_=c3[:, :, m_idx], func=AF.Exp), [Bcw], [BSC[i]])
                S.op("act", lambda: nc.scalar.activation(out=SC[i][:, 1, :], in_=c3[:, :, e_idx], func=AF.Exp), [Bcw], [BSC[i]])
                S.op("dve", lambda: A_.tensor_tensor(out=SC[i][:, 2, :], in0=c3[:, :, e_idx], in1=c3[:, :, m_idx], op=ALU.subtract), [Bcw], [BSC[i]])
                S.op("act", lambda: nc.scalar.activation(out=SC[i][:, 2, :], in_=SC[i][:, 2, :], func=AF.Exp), [BSC[i]], [BSC[i]])
                S.dma("pool", RWS[b, p, d], SC[i].rearrange("p a c -> p (a c)"), reads=[BSC[i]])
                S.op("dve", lambda: A_.tensor_tensor(out=v3(cm), in0=c3, in1=c3[:, :, m_idx:m_idx + 1].to_broadcast([128, NCH, CH]), op=ALU.subtract), [Bcw], [Bcm])
                S.op("act", lambda: nc.scalar.activation(out=en, in_=cm, func=AF.Exp, scale=-1.0), [Bcm], [Ben])
                S.op("dve", lambda: A_.tensor_tensor(out=ex, in0=cm, in1=lw, op=ALU.subtract), [Bcm, Blw], [Bex])
                S.op("act", lambda: nc.scalar.activation(out=ex, in_=ex, func=AF.Exp), [Bex], [Bex])
                S.op("act", lambda: nc.scalar.activation(out=cm, in_=cm, func=AF.Exp), [Bcm], [Bcm])
                S.op("dve", lambda: A_.tensor_tensor(out=ABt[i][:, :, 0, :], in0=v3(kap), in1=v3(ex), op=ALU.mult), [Bkap, Bex], [BAB[i]])
                S.op("dve", lambda: A_.tensor_tensor(out=ABt[i][:, :, 1, :], in0=v3(rr), in1=v3(cm), op=ALU.mult), [Br, Bcm], [BAB[i]])
                S.op("dve", lambda: A_.tensor_tensor(out=KBt_[i][:, :, 0, :], in0=v3(kt), in1=v3(en), op=ALU.mult), [Bkt, Ben], [BKB[i]])
                S.op("dve", lambda: A_.tensor_tensor(out=KBt_[i][:, :, 1, :], in0=v3(be), in1=v3(en), op=ALU.mult), [Bbe, Ben], [BKB[i]])
                S.dma("pool", RWD[b, p, d, 0], ABt[i].rearrange("p c a s -> p (c a s)"), reads=[BAB[i]])
                S.dma("pool", RWD[b, p, d, 1], KBt_[i].rearrange("p c a s -> p (c a s)"), reads=[BKB[i]])
    st.close()
    if getattr(self, "rw_stop", 0) == 2:
        return
    st = Stage(self, "r2b")
    S.pe_selfwait = getattr(self, "rw_selfwait", False)
    S.pe_drain = getattr(self, "rw_drain", 2)
    epsLN = st.sb("epsLN", [128, 1])
    Bgl = Buf()
    S.op("dve", lambda: A_.memset(epsLN, RW_LN_EPS), [], [Bgl])
    AB = [st.sb(f"AB{d}", [128, NCH, 128], BF16) for d in range(2)]
    KB = [st.sb(f"KB{d}", [128, NCH, 128], BF16) for d in range(2)]
    SCs = [st.sb(f"SC{d}", [128, 3, NCH]) for d in range(2)]
    Vst = st.sb("Vst", [64, NCH, 128], BF16)
    BABl, BKBl, BSCl = [[Buf(), Buf()] for _ in range(3)]
    BVst = Buf()
    chains = [(hd, d) for hd in range(2) for d in range(2)]
    IDT = BF16 if getattr(self, "rw_inv_bf16", True) else F32
    VU, GGb, AN0, ANp, Xp, Wf, KBtr = {}, {}, {}, {}, {}, {}, {}
    BVU, BGG, BAN0, BANp, BXp, BWf, BKBtr, BST, BS0, BtS, By = [dict() for _ in range(11)]
    for ch in chains:
        nm = f"{ch[0]}{ch[1]}"
        VU[ch] = st.sb("VU" + nm, [128, NCH, CH], BF16)
        GGb[ch] = st.sb("GG" + nm, [128, 128], BF16)
        AN0[ch] = st.sb("AN0" + nm, [128, 128], IDT)
        ANp[ch] = [st.sb(f"ANp{q}" + nm, [128, 128], IDT) for q in range(2)]
        Xp[ch] = [st.sb(f"X{q}" + nm, [128, CH], IDT) for q in range(2)]
        Wf[ch] = st.sb("Wf" + nm, [128, CH], IDT)
        KBtr[ch] = st.sb("KBt" + nm, [128, CH], BF16)
        BVU[ch], BGG[ch], BAN0[ch], BWf[ch], BKBtr[ch], BST[ch], BS0[ch], BtS[ch], By[ch] = [Buf() for _ in range(9)]
        BANp[ch] = [Buf(), Buf()]
        BXp[ch] = [Buf(), Buf()]
        S.op("dve", lambda: A_.memset(GGb[ch], 0.0), [], [BGG[ch]])
        S.op("dve", lambda: A_.memset(AN0[ch], 0.0), [], [BAN0[ch]])
    ST = [st.sb(f"ST{d}", [128, CH]) for d in range(2)]
    S0m = [st.sb(f"S0m{d}", [128, CH], BF16) for d in range(2)]
    tS = [st.sb(f"tS{d}", [128, CH]) for d in range(2)]
    yacc = [st.sb(f"yacc{d}", [128, T]) for d in range(2)]
    rl = st.sb("rl", [128, T]); k0 = st.sb("k0", [128, T]); k1 = st.sb("k1", [128, T]); vf = st.sb("vf", [128, T]); gg = st.sb("gg", [128, T])
    t0_ = st.sb("t0", [128, T]); t1_ = st.sb("t1", [128, T])
    ogb = st.sb("ogb", [128, T], BF16)
    Brl, Bk0, Bk1, Bvf, Bgg, Bt0, Bt1, Bogb = [Buf() for _ in range(8)]
    MERGE = getattr(self, "rw_merge", True)
    if MERGE:
        mKB = [st.sb(f"mKBt{d}", [128, 2, CH], BF16) for d in range(2)]
        mGG = [st.sb(f"mGG{d}", [128, 2, 128], BF16) for d in range(2)]
        mAN0 = [st.sb(f"mAN0{d}", [128, 2, 128], IDT) for d in range(2)]
        mANp = [[st.sb(f"mANp{q}{d}", [128, 2, 128], IDT) for q in range(2)] for d in range(2)]
        mXp = [[st.sb(f"mX{q}{d}", [128, 2, CH], IDT) for q in range(2)] for d in range(2)]
        mWf = [st.sb(f"mWf{d}", [128, 2, CH], IDT) for d in range(2)]
        mVU = [st.sb(f"mVU{d}", [128, NCH, 2, CH], BF16) for d in range(2)]
        M4x2 = [st.sb(f"M4x2{d}", [128, 2, 128]) for d in range(2)]
        mAx2 = [st.sb(f"mAx2{d}", [128, 2, CH]) for d in range(2)]
        mNx2 = [st.sb(f"mNx2{d}", [128, 2, CH]) for d in range(2)]
        I2 = st.sb("I2", [128, 2, CH])
        Bmk = Buf()
        mBKB, mBGG, mBAN0, mBWf, mBVU, mBST, mBS0, mBtS, mBy = [[Buf(), Buf()] for _ in range(9)]
        mBANp = [[Buf(), Buf()], [Buf(), Buf()]]
        mBXp = [[Buf(), Buf()], [Buf(), Buf()]]
        mUB = [[Buf() for _ in range(4)] for d in range(2)]
        for d in range(2):
            S.op("dve", lambda: A_.memset(mGG[d], 0.0), [], [mBGG[d]])
            S.op("dve", lambda: A_.memset(mAN0[d], 0.0), [], [mBAN0[d]])
            for hd in range(2):
                S.op("dve", lambda: A_.tensor_copy(out=M4x2[d][:, hd, :], in_=(self.masks[:, 0:128] if d == 0 else self.masks[:, 128:256])), [], [Bmk])
                S.op("dve", lambda: A_.tensor_copy(out=mAx2[d][:, hd, :], in_=(self.masks[:, 0:64] if d == 0 else self.masks[:, 128:192])), [], [Bmk])
                S.op("dve", lambda: A_.tensor_copy(out=mNx2[d][:, hd, :], in_=(self.masks[:, 128:192] if d == 0 else self.masks[:, 0:64])), [], [Bmk])
        for hd in range(2):
            S.op("dve", lambda: A_.tensor_copy(out=I2[64:128, hd, :], in_=self.ident[64:128, 64:128]), [], [Bmk])
    R = {}
    BR = {}
    for ci, ch in enumerate(chains):
        b0, b1 = self.PS[2 * ci], self.PS[2 * ci + 1]
        R[ch] = dict(GA=b0[:, 0:128], LV=b0[:, 192:320], Wp=b0[:, 384:448],
                     XL=b1[:, 320:384], Up=b1[:, 448:512], Nn=b1[:, 128:192],
                     Yp=b1[:, 0:64], Sd=b1[:, 64:128], TR=b1.bitcast(BF16)[:, 512:576])
        u0, u1, u2, u3 = Buf(), Buf(), Buf(), Buf()
        ykp = [u2] if ch[0] == 0 else [u3]
        BR[ch] = dict(GAlo=[u0], GAup=[u1], GA=[u0, u1], LV=[u1], Wp=[u1], XL=[u3], Up=[u3], Nn=[u3], Yp=ykp, Sd=ykp, TR=[u2, u3], ALL=[u0, u1, u2, u3])
    up, lo = slice(64, 128), slice(0, 64)
    mU = lambda ap: ap.bitcast(U32)
    cf = list(range(NCH))
    cbk = list(range(TC // CH - 1, -1, -1)) + list(range(NCH - 1, TC // CH - 1, -1))
    order = [cf, cbk]
    dbgn = getattr(self, "rw_dbg", None)
    for b in range(NB):
        cols = slice(b * T, (b + 1) * T)
        for p in range(8):
            if dbgn is not None and (b * 8 + p) >= dbgn[0]:
                continue
            rows = slice(p * 128, (p + 1) * 128)
            for d in range(2):
                S.dma("sp", AB[d], RWD[b, p, d, 0].rearrange("k (c x) -> k c x", x=128), writes=[BABl[d]])
                S.dma("sp", KB[d], RWD[b, p, d, 1].rearrange("k (c x) -> k c x", x=128), writes=[BKBl[d]])
                S.dma("sp", SCs[d], RWS[b, p, d].rearrange("k (a c) -> k a c", a=3), writes=[BSCl[d]])
            S.dma("sp", Vst, Vtm[cols, rows].rearrange("(c s) v -> s c v", s=CH), writes=[BVst])
            S.dma("sp", rl, RWP[0, rows, cols], writes=[Brl])
            S.dma("sp", k0, RWP[1, rows, cols], writes=[Bk0])
            S.dma("sp", k1, RWP[2, rows, cols], writes=[Bk1])
            S.dma("sp", vf, RWP[8, rows, cols], writes=[Bvf])
            S.dma("sp", gg, RWP[9, rows, cols], writes=[Bgg])
            if MERGE:
                for d in range(2):
                    S.op("pool", lambda: nc.gpsimd.tensor_copy(out=mVU[d][lo, :, :, :], in_=Vst.rearrange("s c (h v) -> s c h v", h=2)), [BVst], [mBVU[d]])
                    S.op("dve", lambda: A_.memset(ST[d], 0.0), [], [mBST[d]])
                    S.op("dve", lambda: A_.memset(S0m[d], 0.0), [], [mBS0[d]])

                def dstep(d, step):
                    c = order[d][step]
                    cs = slice(c * CH, (c + 1) * CH)
                    bA, bB, bC, bD = [self.PS[4 * d + q] for q in range(4)]
                    uA, uB, uC, uD = mUB[d]
                    h2 = lambda ap: ap.rearrange("p (h x) -> p h x", h=2)
                    GA = h2(bA[:, 0:256]); LV = h2(bB[:, 0:256]); Wp = h2(bB[:, 256:384])
                    XL = h2(bC[:, 0:128]); Up_ = h2(bC[:, 128:256]); Nn = h2(bC[:, 256:384])
                    Yp = bD[:, 0:64]; Sd = bD[:, 64:128]; TR = h2(bD.bitcast(BF16)[:, 512:640])
                    KP = [slice(0, 64), slice(64, 128)]
                    for hd in range(2):
                        kp = KP[hd]
                        S.op("pe", lambda: nc.tensor.transpose(out=TR[:, hd, :], in_=KB[d][kp, c, :], identity=self.identb[kp, kp]), [BKBl[d]], [uD], pemode=("T", hd))
                        S.op("pe", lambda: nc.tensor.matmul(GA[lo, hd, :], lhsT=KB[d][kp, c, 0:64], rhs=AB[d][kp, c, :], start=True, stop=True), [BKBl[d], BABl[d]], [uA], pemode=("g", hd))
                        S.op("pe", lambda: nc.tensor.matmul(GA[up, hd, :], lhsT=KB[d][kp, c, 64:128], rhs=AB[d][kp, c, :], start=True, stop=True), [BKBl[d], BABl[d]], [uA], pemode=("g", hd))
                        S.op("pe", lambda: nc.tensor.matmul(Nn[up, hd, :], lhsT=AB[d][kp, c, 0:64], rhs=KB[d][kp, c, 64:128], start=True, stop=True), [BKBl[d], BABl[d]], [uC], pemode=("g", hd))
                    yield
                    S.op("act", lambda: nc.scalar.copy(out=mKB[d], in_=TR), [uD], [mBKB[d]])
                    S.op("dve", lambda: A_.copy_predicated(out=mGG[d], mask=mU(M4x2[d][:]), data=GA), [uA, Bmk], [mBGG[d]])
                    S.op("dve", lambda: A_.copy_predicated(out=mAN0[d][up, :, 0:64], mask=mU(mAx2[d][up, :, :]), data=GA[up, :, 0:64]), [uA, Bmk], [mBAN0[d]])
                    S.op("dve", lambda: A_.copy_predicated(out=mAN0[d][up, :, 64:128], mask=mU(mNx2[d][up, :, :]), data=Nn[up, :, :]), [uC, Bmk], [mBAN0[d]])
                    S.op("dve", lambda: A_.tensor_tensor(out=mXp[d][0][up, :, :], in0=I2[up, :, :], in1=mAN0[d][up, :, 0:64], op=ALU.subtract), [mBAN0[d], Bmk], [mBXp[d][0]])
                    yield
                    cur, Bcur = mAN0[d], mBAN0[d]
                    xq = 0
                    for lv in range(1, 7):
                        nx, Bnx = mANp[d][lv % 2], mBANp[d][lv % 2]
                        for hd in range(2):
                            if lv <= 5:
                                if lv < 5:
                                    S.op("pe", lambda: nc.tensor.matmul(LV[up, hd, 0:64], lhsT=cur[up, hd, 64:128], rhs=cur[up, hd, 0:64], start=True, stop=True), [Bcur], [uB], pemode=("g", 1))
                                S.op("pe", lambda: nc.tensor.matmul(LV[up, hd, 64:128], lhsT=cur[up, hd, 0:64], rhs=cur[up, hd, 64:128], start=True, stop=True), [Bcur], [uB], pemode=("g", 1))
                            if lv >= 2:
                                S.op("pe", lambda: nc.tensor.matmul(XL[up, hd, :], lhsT=cur[up, hd, 64:128], rhs=mXp[d][xq][up, hd, :], start=True, stop=True), [Bcur, mBXp[d][xq]], [uC], pemode=("g", 1))
                        yield
                        if lv <= 5:
                            if lv < 5:
                                S.op("act", lambda: nc.scalar.copy(out=nx[up, :, :], in_=LV[up, :, :]), [uB], [Bnx])
                            else:
                                S.op("act", lambda: nc.scalar.copy(out=nx[up, :, 64:128], in_=LV[up, :, 64:128]), [uB], [Bnx])
                        if lv >= 2:
                            S.op("dve", lambda: A_.tensor_tensor(out=mXp[d][1 - xq][up, :, :], in0=XL[up, :, :], in1=mXp[d][xq][up, :, :], op=ALU.add), [uC, mBXp[d][xq]], [mBXp[d][1 - xq]])
                            xq = 1 - xq
                        if lv <= 5:
                            cur, Bcur = nx, Bnx
                        yield
                    for hd in range(2):
                        kp = KP[hd]
                        S.op("pe", lambda: nc.tensor.matmul(Wp[up, hd, :], lhsT=AB[d][kp, c, 0:64], rhs=S0m[d][kp, :], start=True, stop=False), [BABl[d], mBS0[d]], [uB], pemode=("g", hd))
                        S.op("pe", lambda: nc.tensor.matmul(Wp[up, hd, :], lhsT=mGG[d][lo, hd, 0:64], rhs=mVU[d][lo, c, hd, :], start=False, stop=True), [mBGG[d], mBVU[d]], [uB], pemode=("g", 0))
                    yield
                    S.op("act", lambda: nc.scalar.copy(out=mWf[d][up, :, :], in_=Wp[up, :, :]), [uB], [mBWf[d]])
                    yield
                    for hd in range(2):
                        S.op("pe", lambda: nc.tensor.matmul(Up_[up, hd, :], lhsT=mXp[d][xq][up, hd, :], rhs=mWf[d][up, hd, :], start=True, stop=True), [mBXp[d][xq], mBWf[d]], [uC], pemode=("g", 1))
                    yield
                    S.op("act", lambda: nc.scalar.activation(out=mVU[d][up, c, :, :], in_=Up_[up, :, :], func=AF.Copy, scale=-1.0), [uC], [mBVU[d]])
                    yield
                    for hd in range(2):
                        kp = KP[hd]
                        S.op("pe", lambda: nc.tensor.matmul(Yp[kp, :], lhsT=S0m[d][kp, :], rhs=AB[d][kp, c, 64:128], start=True, stop=False), [mBS0[d], BABl[d]], [uD], pemode=("g", hd))
                        S.op("pe", lambda: nc.tensor.matmul(Yp[kp, :], lhsT=mVU[d][:, c, hd, :], rhs=mGG[d][:, hd, 64:128], start=False, stop=True), [mBVU[d], mBGG[d]], [uD], pemode=("full",))
                    for hd in range(2):
                        kp = KP[hd]
                        S.op("pe", lambda: nc.tensor.matmul(Sd[kp, :], lhsT=mKB[d][:, hd, :], rhs=mVU[d][:, c, hd, :], start=True, stop=True), [mBKB[d], mBVU[d]], [uD], pemode=("full",))
                    yield
                    S.op("act", lambda: nc.scalar.copy(out=yacc[d][:, cs], in_=Yp), [uD], [mBy[d]])
                    S.op("act", lambda: nc.scalar.activation(out=tS[d], in_=Sd, func=AF.Identity, scale=SCs[d][:, 2, c:c + 1]), [uD, BSCl[d]], [mBtS[d]])
                    S.op("dve", lambda: A_.scalar_tensor_tensor(out=ST[d], in0=ST[d], scalar=SCs[d][:, 1, c:c + 1], in1=tS[d], op0=ALU.mult, op1=ALU.add), [mBST[d], mBtS[d], BSCl[d]], [mBST[d]])
                    if step + 1 < NCH:
                        cn = order[d][step + 1]
                        S.op("dve", lambda: A_.tensor_scalar(out=S0m[d], in0=ST[d], scalar1=SCs[d][:, 0, cn:cn + 1], scalar2=None, op0=ALU.mult), [mBST[d], BSCl[d]], [mBS0[d]])

                for step in range(NCH if dbgn is None else dbgn[1]):
                    gens = [dstep(d, step) for d in range(2)]
                    while gens:
                        for g_ in list(gens):
                            try:
                                next(g_)
                            except StopIteration:
                                gens.remove(g_)
            else:
                for ch in chains:
                    hd, d = ch
                    kp = slice(hd * 64, hd * 64 + 64)
                    S.op("pool", lambda: nc.gpsimd.tensor_copy(out=VU[ch][lo, :, :], in_=Vst[:, :, hd * 64:(hd + 1) * 64]), [BVst], [BVU[ch]])
                    S.op("dve", lambda: A_.memset(ST[d][kp, :], 0.0), [], [BST[ch]])
                    S.op("dve", lambda: A_.memset(S0m[d][kp, :], 0.0), [], [BS0[ch]])
                def chain_step(ch, step):
                    hd, d = ch
                    kp = slice(hd * 64, hd * 64 + 64)
                    c = order[d][step]
                    cs = slice(c * CH, (c + 1) * CH)
                    r_, br_ = R[ch], BR[ch]
                    M4 = self.masks[:, 0:128] if d == 0 else self.masks[:, 128:256]
                    mA = self.masks[up, 0:64] if d == 0 else self.masks[up, 128:192]
                    mN = self.masks[up, 128:192] if d == 0 else self.masks[up, 0:64]
                    S.op("pe", lambda: nc.tensor.transpose(out=r_["TR"], in_=KB[d][kp, c, :], identity=self.identb[kp, kp]), [BKBl[d]], br_["TR"], pemode=("T", hd))
                    S.op("act", lambda: nc.scalar.copy(out=KBtr[ch], in_=r_["TR"]), br_["TR"], [BKBtr[ch]])
                    S.op("pe", lambda: nc.tensor.matmul(r_["GA"][lo, :], lhsT=KB[d][kp, c, 0:64], rhs=AB[d][kp, c, :], start=True, stop=True), [BKBl[d], BABl[d]], br_["GAlo"], pemode=("g", hd))
                    S.op("pe", lambda: nc.tensor.matmul(r_["GA"][up, :], lhsT=KB[d][kp, c, 64:128], rhs=AB[d][kp, c, :], start=True, stop=True), [BKBl[d], BABl[d]], br_["GAup"], pemode=("g", hd))
                    S.op("pe", lambda: nc.tensor.matmul(r_["Nn"][up, :], lhsT=AB[d][kp, c, 0:64], rhs=KB[d][kp, c, 64:128], start=True, stop=True), [BKBl[d], BABl[d]], br_["Nn"], pemode=("g", hd))
                    yield
                    S.op("dve", lambda: A_.copy_predicated(out=GGb[ch], mask=mU(M4), data=r_["GA"]), br_["GA"], [BGG[ch]])
                    S.op("dve", lambda: A_.copy_predicated(out=AN0[ch][up, 0:64], mask=mU(mA), data=r_["GA"][up, 0:64]), br_["GAup"], [BAN0[ch]])
                    S.op("dve", lambda: A_.copy_predicated(out=AN0[ch][up, 64:128], mask=mU(mN), data=r_["Nn"][up, :]), br_["Nn"], [BAN0[ch]])
                    S.op("dve", lambda: A_.tensor_tensor(out=Xp[ch][0][up, :], in0=self.ident[up, up], in1=AN0[ch][up, 0:64], op=ALU.subtract), [BAN0[ch]], [BXp[ch][0]])
                    yield
                    cur, Bcur = AN0[ch], BAN0[ch]
                    xq = 0
                    for lv in range(1, 7):
                        nx, Bnx = ANp[ch][lv % 2], BANp[ch][lv % 2]
                        if lv <= 5:
                            if lv < 5:
                                S.op("pe", lambda: nc.tensor.matmul(r_["LV"][up, 0:64], lhsT=cur[up, 64:128], rhs=cur[up, 0:64], start=True, stop=True), [Bcur], br_["LV"], pemode=("f",))
                            S.op("pe", lambda: nc.tensor.matmul(r_["LV"][up, 64:128], lhsT=cur[up, 0:64], rhs=cur[up, 64:128], start=True, stop=True), [Bcur], br_["LV"], pemode=("f",))
                        if lv >= 2:
                            S.op("pe", lambda: nc.tensor.matmul(r_["XL"][up, :], lhsT=cur[up, 64:128], rhs=Xp[ch][xq][up, :], start=True, stop=True), [Bcur, BXp[ch][xq]], br_["XL"], pemode=("f",))
                        yield
                        if lv <= 5:
                            if lv < 5:
                                S.op("act", lambda: nc.scalar.copy(out=nx[up, :], in_=r_["LV"][up, :]), br_["LV"], [Bnx])
                            else:
                                S.op("act", lambda: nc.scalar.copy(out=nx[up, 64:128], in_=r_["LV"][up, 64:128]), br_["LV"], [Bnx])
                        if lv >= 2:
                            S.op("dve", lambda: A_.tensor_tensor(out=Xp[ch][1 - xq][up, :], in0=r_["XL"][up, :], in1=Xp[ch][xq][up, :], op=ALU.add), br_["XL"] + [BXp[ch][xq]], [BXp[ch][1 - xq]])
                            xq = 1 - xq
                        if lv <= 5:
                            cur, Bcur = nx, Bnx
                        if lv < 6:
                            yield
                    yield
                    S.op("pe", lambda: nc.tensor.matmul(r_["Wp"][up, :], lhsT=AB[d][kp, c, 0:64], rhs=S0m[d][kp, :], start=True, stop=False), [BABl[d], BS0[ch]], br_["Wp"], pemode=("g", hd))
                    S.op("pe", lambda: nc.tensor.matmul(r_["Wp"][up, :], lhsT=GGb[ch][lo, 0:64], rhs=VU[ch][lo, c, :], start=False, stop=True), [BGG[ch], BVU[ch]], br_["Wp"], pemode=("w2",))
                    yield
                    S.op("act", lambda: nc.scalar.copy(out=Wf[ch][up, :], in_=r_["Wp"][up, :]), br_["Wp"], [BWf[ch]])
                    yield
                    S.op("pe", lambda: nc.tensor.matmul(r_["Up"][up, :], lhsT=Xp[ch][xq][up, :], rhs=Wf[ch][up, :], start=True, stop=True), [BXp[ch][xq], BWf[ch]], br_["Up"], pemode=("f",))
                    yield
                    S.op("act", lambda: nc.scalar.activation(out=VU[ch][up, c, :], in_=r_["Up"][up, :], func=AF.Copy, scale=-1.0), br_["Up"], [BVU[ch]])
                    yield
                    S.op("pe", lambda: nc.tensor.matmul(r_["Yp"][kp, :], lhsT=S0m[d][kp, :], rhs=AB[d][kp, c, 64:128], start=True, stop=False), [BS0[ch], BABl[d]], br_["Yp"], pemode=("g", hd))
                    S.op("pe", lambda: nc.tensor.matmul(r_["Yp"][kp, :], lhsT=VU[ch][:, c, :], rhs=GGb[ch][:, 64:128], start=False, stop=True), [BVU[ch], BGG[ch]], br_["Yp"], pemode=("full",))
                    yield
                    S.op("act", lambda: nc.scalar.copy(out=yacc[d][kp, cs], in_=r_["Yp"][kp, :]), br_["Yp"], [By[ch]])
                    S.op("pe", lambda: nc.tensor.matmul(r_["Sd"][kp, :], lhsT=KBtr[ch], rhs=VU[ch][:, c, :], start=True, stop=True), [BKBtr[ch], BVU[ch]], br_["Sd"], pemode=("full",))
                    yield
                    S.op("act", lambda: nc.scalar.activation(out=tS[d][kp, :], in_=r_["Sd"][kp, :], func=AF.Identity, scale=SCs[d][kp, 2, c:c + 1]), br_["Sd"] + [BSCl[d]], [BtS[ch]])
                    S.op("dve", lambda: A_.scalar_tensor_tensor(out=ST[d][kp, :], in0=ST[d][kp, :], scalar=SCs[d][kp, 1, c:c + 1], in1=tS[d][kp, :], op0=ALU.mult, op1=ALU.add), [BST[ch], BtS[ch], BSCl[d]], [BST[ch]])
                    if step + 1 < NCH:
                        cn = order[d][step + 1]
                        S.op("dve", lambda: A_.tensor_scalar(out=S0m[d][kp, :], in0=ST[d][kp, :], scalar1=SCs[d][kp, 0, cn:cn + 1], scalar2=None, op0=ALU.mult), [BST[ch], BSCl[d]], [BS0[ch]])

                for step in range(NCH if dbgn is None else dbgn[1]):
                    gens = [chain_step(ch, step) for ch in chains]
                    if getattr(self, "rw_order", "phase") == "chain":
                        for g_ in gens:
                            for _ in g_:
                                pass
                        gens = []
                    while gens:
                        for g_ in list(gens):
                            try:
                                next(g_)
                            except StopIteration:
                                gens.remove(g_)
            if MERGE:
                RB = {0: [mUB[0][0]], 1: [mUB[0][1]]}
                By_all = [mBy[0], mBy[1]]
            else:
                RB = {0: [BR[chains[0]]["ALL"][0], BR[chains[0]]["ALL"][1]], 1: [BR[chains[0]]["ALL"][2], BR[chains[0]]["ALL"][3]]}
                By_all = [By[ch] for ch in chains]
            Byy = Buf()
            S.op("dve", lambda: A_.tensor_tensor(out=yacc[0], in0=yacc[0], in1=yacc[1], op=ALU.add), By_all, [Byy])
            NP_ = 6
            W_ = T // NP_
            for pc in range(NP_):
                sl_ = slice(pc * W_, (pc + 1) * W_)
                pb = pc % 2
                S.op("pe", lambda: nc.tensor.matmul(self.PS[pb][:, 0:W_], lhsT=self.blk64, rhs=yacc[0][:, sl_], start=True, stop=True), [Byy], RB[pb])
                S.op("dve", lambda: A_.scalar_tensor_tensor(out=t0_[:, sl_], in0=self.PS[pb][:, 0:W_], scalar=-1.0 / 64, in1=yacc[0][:, sl_], op0=ALU.mult, op1=ALU.add), RB[pb] + [Byy], [Bt0])
            S.op("act", lambda: nc.scalar.activation(out=t1_, in_=t0_, func=AF.Square), [Bt0], [Bt1])
            for pc in range(NP_):
                sl_ = slice(pc * W_, (pc + 1) * W_)
                pb = pc % 2
                S.op("pe", lambda: nc.tensor.matmul(self.PS[pb][:, 0:W_], lhsT=self.blk64, rhs=t1_[:, sl_], start=True, stop=True), [Bt1], RB[pb])
                S.op("act", lambda: nc.scalar.activation(out=yacc[1][:, sl_], in_=self.PS[pb][:, 0:W_], func=AF.Sqrt, scale=1.0 / 64, bias=epsLN), RB[pb] + [Bgl], [Byy])
            S.op("dve", lambda: A_.reciprocal(out=yacc[1], in_=yacc[1]), [Byy], [Byy])
            S.op("dve", lambda: A_.tensor_tensor(out=t0_, in0=t0_, in1=yacc[1], op=ALU.mult), [Bt0, Byy], [Bt0])
            S.op("act", lambda: nc.scalar.activation(out=t0_, in_=t0_, func=AF.Identity, scale=self.pv("rw_ln_w", p), bias=self.pv("rw_ln_b", p)), [Bt0], [Bt0])
            S.op("dve", lambda: A_.tensor_tensor(out=k0, in0=k0, in1=k1, op=ALU.add), [Bk0, Bk1], [Bk0])
            S.op("dve", lambda: A_.scalar_tensor_tensor(out=t1_, in0=rl, scalar=self.pv("rw_r_k", p), in1=k0, op0=ALU.mult, op1=ALU.mult), [Brl, Bk0, Bt1], [Bt1])
            for pc in range(NP_):
                sl_ = slice(pc * W_, (pc + 1) * W_)
                pb = pc % 2
                S.op("pe", lambda: nc.tensor.matmul(self.PS[pb][:, 0:W_], lhsT=self.blk64, rhs=t1_[:, sl_], start=True, stop=True), [Bt1], RB[pb])
                S.op("dve", lambda: A_.tensor_tensor(out=yacc[1][:, sl_], in0=self.PS[pb][:, 0:W_], in1=vf[:, sl_], op=ALU.mult), RB[pb] + [Bvf, Byy], [Byy])
            S.op("dve", lambda: A_.tensor_tensor(out=t0_, in0=t0_, in1=yacc[1], op=ALU.add), [Bt0, Byy], [Bt0])
            S.op("dve", lambda: A_.tensor_tensor(out=ogb, in0=t0_, in1=gg, op=ALU.mult), [Bt0, Bgg], [Bogb])
            S.dma("pool", og[rows, cols], ogb, reads=[Bogb])
            for b_ in By_all:
                b_.r.append(Byy.w)
    st.close()
    S.pe_selfwait = False
    S.pe_drain = 0


Prog.rwkv_scan = _rwkv_scan
```

```python
from contextlib import ExitStack
import numpy as np
import concourse.bass as bass
import concourse.mybir as mybir
from concourse.bass_utils import run_bass_kernel_spmd

F32 = mybir.dt.float32
BF16 = mybir.dt.bfloat16
AF = mybir.ActivationFunctionType
ALU = mybir.AluOpType

NCORES = 8
NB = 2
TC = 256
TL = 2048
T = TC + TL
TT = NB * T
D = 1024
DEPTH = 4
DFF = 2816
NFC = DFF // 128
BLK = 256
NBLK = T // BLK
EPS = 1e-6
CH = 64
NCH = T // CH
RW_LN_EPS = 64e-5
MLA_SCALE = 96 ** -0.5


class Buf:
    __slots__ = ("name", "w", "r")

    def __init__(self, name=""):
        self.name = name
        self.w = None
        self.r = []


class _Eng:
    def __init__(self, S, name, eng):
        self.S = S
        self.name = name
        self.eng = eng
        self.sem = None
        self.count = 0
        self.seen = {}
        self.nsem = 0
        self.ninst = 0
        self.own = set()

    def new_sem(self):
        self.sem = self.S.nc.alloc_semaphore(f"e_{self.name}_{self.nsem}")
        self.own.add(id(self.sem))
        self.nsem += 1
        self.count = 0

    def wait(self, ev):
        sem, val = ev
        k = id(sem)
        if self.name == "pe" and k in self.own and not self.S.pe_selfwait:
            return
        if self.seen.get(k, 0) >= val:
            return
        self.eng.wait_ge(sem, val)
        self.seen[k] = val


class Sched:
    EPOCH = 30000

    def __init__(self, nc, ndma_sems=48):
        self.nc = nc
        self.E = {}
        for name, eng in (("pe", nc.tensor), ("dve", nc.vector), ("act", nc.scalar),
                          ("pool", nc.gpsimd), ("sp", nc.sync)):
            e = _Eng(self, name, eng)
            e.new_sem()
            self.E[name] = e
        self.dsems = [[nc.alloc_semaphore(f"d{i}"), 0] for i in range(ndma_sems)]
        self.dnext = 0
        self._keep = []
        self.pe_selfwait = False
        self.pe_drain = 0
        self.last_pemode = None

    @staticmethod
    def _deps(reads, writes):
        deps = []
        for b in reads:
            if b.w is not None:
                deps.append(b.w)
        for b in writes:
            if b.w is not None:
                deps.append(b.w)
            deps.extend(b.r)
        return deps

    @staticmethod
    def _mark(ev, reads, writes):
        for b in writes:
            b.w = ev
            b.r = []
        for b in reads:
            if b not in writes:
                b.r.append(ev)
                if len(b.r) > 32:
                    b.r = b.r[-32:]

    def op(self, ename, fn, reads=(), writes=(), pemode=None):
        e = self.E[ename]
        for ev in self._deps(reads, writes):
            e.wait(ev)
        drain = False
        if ename == "pe":
            drain = self.pe_drain == 1 or (self.pe_drain == 2 and pemode != self.last_pemode)
            self.last_pemode = pemode
        if drain and e.count > 0:
            k = id(e.sem)
            if e.seen.get(k, 0) < e.count:
                e.eng.wait_ge(e.sem, e.count)
                e.seen[k] = e.count
        if e.count >= self.EPOCH:
            self._keep.append(e.sem)
            e.new_sem()
        inst = fn()
        e.count += 1
        e.ninst += 1
        inst.then_inc(e.sem, 1)
        ev = (e.sem, e.count)
        self._mark(ev, reads, writes)
        return ev

    def dma(self, qname, out, in_, reads=(), writes=(), **kw):
        q = self.E[qname]
        for ev in self._deps(reads, writes):
            q.wait(ev)
        slot = self.dsems[self.dnext % len(self.dsems)]
        self.dnext += 1
        if slot[1] >= self.EPOCH:
            self._keep.append(slot[0])
            slot[0] = self.nc.alloc_semaphore(f"dx{self.dnext}")
            slot[1] = 0
        if slot[1] > 0:
            q.wait((slot[0], slot[1]))
        q.eng.dma_start(out=out, in_=in_, **kw).then_inc(slot[0], 16)
        q.ninst += 1
        slot[1] += 16
        ev = (slot[0], slot[1])
        self._mark(ev, reads, writes)
        return ev

    def barrier(self):
        evs = [(e.sem, e.count) for e in self.E.values() if e.count > 0]
        evs += [(s[0], s[1]) for s in self.dsems if s[1] > 0]
        for e in self.E.values():
            for ev in evs:
                if ev[0] is e.sem:
                    continue
                e.wait(ev)


class PVec:
    def __init__(self):
        self.cols = []
        self.off = {}
        self.n = 0

    def add(self, name, vec):
        vec = np.asarray(vec, dtype=np.float32).reshape(-1)
        assert vec.size % 128 == 0
        nch = vec.size // 128
        self.off[name] = (self.n, nch)
        self.cols.append(np.ascontiguousarray(vec.reshape(nch, 128).T))
        self.n += nch

    def array(self):
        return np.ascontiguousarray(np.concatenate(self.cols, axis=1))


def pvec_layout(inputs):
    pv = PVec()
    for l in range(DEPTH):
        pv.add(f"b_mod{l}", inputs["b_mod"][l])
        pv.add(f"norm1_{l}", inputs["norm1"][l])
        pv.add(f"norm2_{l}", inputs["norm2"][l])
        for k in range(3):
            pv.add(f"conv{l}_{k}", inputs["ffn_conv"][l, k])
        pv.add(f"convb{l}", inputs["ffn_conv_b"][l])
    pv.add("norm_f", inputs["norm_f"])
    for d in range(2):
        for j in range(2):
            pv.add(f"hg_lb{d}_{j}", inputs["hg_lb"][d, j])
    for j in range(2):
        pv.add(f"hg_norm{j}", inputs["hg_norm"][j])
    for k in range(6):
        pv.add(f"rw_mu{k}", inputs["rw_mu"][0, k])
    for d in range(2):
        pv.add(f"rw_w0_{d}", inputs["rw_w0"][0, d])
        pv.add(f"rw_a0_{d}", inputs["rw_a0"][0, d])
    for nm in ("rw_k_k", "rw_k_a", "rw_r_k", "rw_ln_w", "rw_ln_b"):
        pv.add(nm, inputs[nm][0])
    pv.add("mla_q_norm", inputs["mla_q_norm"][0])
    pv.add("mla_kv_norm", inputs["mla_kv_norm"][0])
    return pv


def make_consts():
    c = {}
    c["ident"] = np.eye(128, dtype=np.float32)
    c["ones"] = np.ones((128, 128), dtype=np.float32)
    bo = np.zeros((128, 128), dtype=np.float32)
    bo[:64, :64] = 1.0
    bo[64:, 64:] = 1.0
    c["blk64"] = bo
    i = np.arange(64)[:, None]
    t = np.arange(64)[None, :]
    su = (i < t).astype(np.float32)
    iu = (i <= t).astype(np.float32)
    sl = (i > t).astype(np.float32)
    il = (i >= t).astype(np.float32)
    c["masks"] = np.concatenate([np.concatenate([su, iu, sl, il], axis=1)] * 2, axis=0)
    m = np.ones((128, T), dtype=np.float32)
    m[:, ::CH] = 0.0
    c["scanmask"] = m
    nq = 8
    inv_freq = (10000.0 ** (-np.arange(nq, dtype=np.float32) / nq)).astype(np.float32)
    pos = np.arange(TL)
    row = (pos // 64).astype(np.float32)
    col = (pos % 64).astype(np.float32)
    ang_r = row[:, None] * inv_freq
    ang_c = col[:, None] * inv_freq
    ang = np.concatenate([ang_r, ang_r, ang_c, ang_c], axis=-1).astype(np.float32)
    cos = np.ones((32, T), dtype=np.float32)
    sin = np.zeros((32, T), dtype=np.float32)
    cos[:, TC:] = np.cos(ang).T
    sin[:, TC:] = np.sin(ang).T
    c["rope_cos"] = cos
    c["rope_sin"] = sin
    return c


WEIGHT_NAMES = ["w_mod", "ffn_w_in", "ffn_w_out", "hg_w_in", "hg_w_o", "rw_w_rkv", "rw_w1", "rw_w2",
                "rw_a1", "rw_a2", "rw_g1", "rw_g2", "rw_w_o", "mla_w_dqkv", "mla_w_uq", "mla_w_ukv", "mla_w_o"]


class Stage:
    def __init__(self, P, name):
        self.P = P
        self.name = name
        self.es = ExitStack()
        P.nstage += 1
        self.k = 0

    def sb(self, name, shape, dt=F32):
        self.k += 1
        h = self.es.enter_context(self.P.nc.sbuf_tensor(f"{self.name}{self.P.nstage}_{name}_{self.k}", list(shape), dt))
        return h.ap()

    def close(self):
        self.P.S.barrier()
        self.es.close()


class Prog:
    def __init__(self, wshapes, pv_off, npv, dbg=(), xin_name=None):
        nc = bass.Bass("TRN2", target_bir_lowering=False)
        self.nc = nc
        self.dbg = set(dbg)
        self.pv_off = pv_off
        self.nstage = 0
        di = lambda n, s: nc.dram_tensor(n, list(s), F32, kind="ExternalInput").ap()
        self.x = di("x", [NB, TL, D])
        self.ctx = di("ctx", [NB, TC, D])
        self.cvec = di("cvec", [3, D])
        self.pvec_d = di("pvec", [128, npv])
        self.cd = {n: di("c_" + n, s) for n, s in (("ident", [128, 128]), ("ones", [128, 128]), ("blk64", [128, 128]),
                                                    ("masks", [128, 256]), ("scanmask", [128, T]),
                                                    ("rope_cos", [32, T]), ("rope_sin", [32, T]))}
        self.W = {n: di(n, wshapes[n]) for n in WEIGHT_NAMES}
        self.out = nc.dram_tensor("out", [NB, TL, D], F32, kind="ExternalOutput").ap()
        self.scratch = {}
        self.S = Sched(nc)
        S = self.S
        self.PS = [nc.alloc_psum_tensor(f"psb{i}", [128, 512], F32).ap() for i in range(8)]
        self.BPS = [Buf(f"ps{i}") for i in range(8)]
        g = lambda n, s, dt=F32: nc.alloc_sbuf_tensor("g_" + n, list(s), dt).ap()
        self.ident = g("ident", [128, 128])
        self.identb = g("identb", [128, 128], BF16)
        self.onesf = g("onesf", [128, 128])
        self.onesb = g("onesb", [128, 128], BF16)
        self.blk64 = g("blk64", [128, 128])
        self.masks = g("masks", [128, 256])
        self.pvec = g("pvec", [128, npv])
        self.MOD = g("MOD", [128, DEPTH, 48, 3])
        self.MA = g("MA", [128, DEPTH, 2, 8, 3])
        self.epsD = g("epsD", [128, 1])
        self.BC = Buf("consts")
        self.BMOD = Buf("mod")
        S.op("dve", lambda: nc.vector.memset(self.epsD, EPS), [], [self.BC])
        S.dma("sp", self.ident, self.cd["ident"], writes=[self.BC])
        b1, b2, b3, b4, b5, b6 = [Buf() for _ in range(6)]
        S.dma("sp", self.onesf, self.cd["ones"], writes=[b1])
        S.dma("sp", self.blk64, self.cd["blk64"], writes=[b2])
        S.dma("sp", self.masks, self.cd["masks"], writes=[b3])
        S.dma("sp", self.pvec, self.pvec_d, writes=[b4])
        S.dma("pool", self.identb, self.cd["ident"], writes=[b5])
        S.dma("pool", self.onesb, self.cd["ones"], writes=[b6])
        S.barrier()

    def scr(self, name, shape, dt=F32):
        if name not in self.scratch:
            kind = "ExternalOutput" if name in self.dbg else "Internal"
            self.scratch[name] = self.nc.dram_tensor("s_" + name, list(shape), dt, kind=kind).ap()
        return self.scratch[name]

    def pv(self, name, c=None):
        off, nch = self.pv_off[name]
        if c is None:
            return self.pvec[:, off:off + nch]
        return self.pvec[:, off + c:off + c + 1]

    def load_w(self, dst, src, bufs_cols, q="pool"):
        S = self.S
        n = dst.shape[2]
        v = src.rearrange("(kc p) n -> p kc n", p=128)
        bufs = []
        for n0 in range(0, n, 512):
            n1 = min(n, n0 + 512)
            b = Buf()
            S.dma(q, dst[:, :, n0:n1], v[:, :, n0:n1], writes=[b])
            bufs.append(b)
        return bufs

    def prologue_transpose(self, xT):
        nc, S = self.nc, self.S
        st = Stage(self, "pt")
        tin = [st.sb(f"tin{i}", [128, D]) for i in range(2)]
        tout = [st.sb(f"tout{i}", [128, 8, 128]) for i in range(2)]
        Bin = [Buf(), Buf()]
        Bout = [Buf(), Buf()]
        xTv = xT.rearrange("(c p) t -> p c t", p=128)
        tiles = []
        for b in range(NB):
            for k in range(T // 128):
                tiles.append((b, k))

        def src(b, k):
            t0 = k * 128
            if t0 < TC:
                return self.ctx[b, t0:t0 + 128, :]
            return self.x[b, t0 - TC:t0 - TC + 128, :]

        S.dma("sp", tin[0], src(*tiles[0]), writes=[Bin[0]])
        for n, (b, k) in enumerate(tiles):
            i = n % 2
            if n + 1 < len(tiles):
                S.dma("sp", tin[1 - i], src(*tiles[n + 1]), writes=[Bin[1 - i]])
            for hf in range(2):
                pb = 2 * (n % 2) + hf
                for c4 in range(4):
                    c = hf * 4 + c4
                    S.op("pe", lambda: nc.tensor.transpose(out=self.PS[pb][:, c4 * 128:(c4 + 1) * 128], in_=tin[i][:, c * 128:(c + 1) * 128], identity=self.ident),
                         [Bin[i]], [self.BPS[pb]])
                eng = "dve" if hf == 0 else "act"
                if hf == 0:
                    S.op("dve", lambda: nc.vector.tensor_copy(out=tout[i][:, 0:4, :], in_=self.PS[pb][:].rearrange("p (c t) -> p c t", c=4)), [self.BPS[pb]], [Bout[i]])
                else:
                    S.op("act", lambda: nc.scalar.copy(out=tout[i][:, 4:8, :], in_=self.PS[pb][:].rearrange("p (c t) -> p c t", c=4)), [self.BPS[pb]], [Bout[i]])
            col = b * T + k * 128
            S.dma("pool", xTv[:, :, col:col + 128], tout[i], reads=[Bout[i]])
        st.close()

    def prologue_mod(self):
        nc, S = self.nc, self.S
        st = Stage(self, "pm")
        cv = st.sb("cv", [3, D])
        sc = st.sb("sc", [3, D])
        scT = st.sb("scT", [128, 8, 3])
        Bcv, Bsc, BscT = Buf(), Buf(), Buf()
        S.dma("sp", cv, self.cvec, writes=[Bcv])
        S.op("act", lambda: nc.scalar.activation(out=sc, in_=cv, func=AF.Silu), [Bcv], [Bsc])
        for kc in range(8):
            S.op("pe", lambda: nc.tensor.transpose(out=self.PS[0][:, kc * 4:kc * 4 + 3], in_=sc[0:3, kc * 128:(kc + 1) * 128], identity=self.ident[0:3, 0:3]),
                 [Bsc], [self.BPS[0]])
        S.op("dve", lambda: nc.vector.tensor_copy(out=scT, in_=self.PS[0][:, 0:32].rearrange("p (k f) -> p k f", f=4)[:, :, 0:3]), [self.BPS[0]], [BscT])
        NWB = 4
        wt = [st.sb(f"wt{i}", [128, 8, 512]) for i in range(NWB)]
        Bwt = [Buf() for _ in range(NWB)]
        groups = [(l, g) for l in range(DEPTH) for g in range(12)]

        def wsrc(l, g):
            return self.W["w_mod"][l].rearrange("(kc p) n -> p kc n", p=128)[:, :, g * 512:(g + 1) * 512]

        def wload(n):
            S.dma("sp" if n % 2 == 0 else "act", wt[n % NWB], wsrc(*groups[n]), writes=[Bwt[n % NWB]])

        for n in range(NWB - 1):
            wload(n)
        for n, (l, g) in enumerate(groups):
            i = n % NWB
            if n + NWB - 1 < len(groups):
                wload(n + NWB - 1)
            pb = 1 + (n % 2)
            for oc in range(4):
                for kc in range(8):
                    S.op("pe", lambda: nc.tensor.matmul(self.PS[pb][:, oc * 4:oc * 4 + 3], lhsT=wt[i][:, kc, oc * 128:(oc + 1) * 128], rhs=scT[:, kc, :], start=(kc == 0), stop=(kc == 7)),
                         [Bwt[i], BscT], [self.BPS[pb]])
            boff, _ = self.pv_off[f"b_mod{l}"]
            bias = self.pvec[:, boff + g * 4:boff + g * 4 + 4].unsqueeze(2).to_broadcast([128, 4, 3])
            S.op("dve", lambda: nc.vector.tensor_tensor(out=self.MOD[:, l, g * 4:(g + 1) * 4, :], in0=self.PS[pb][:, 0:16].rearrange("p (o f) -> p o f", f=4)[:, :, 0:3], in1=bias, op=ALU.add),
                 [self.BPS[pb]], [self.BMOD])
        for l in range(DEPTH):
            for w in range(2):
                sc_idx = 8 if w == 0 else 32
                nrm = self.pv(f"norm{w + 1}_{l}").unsqueeze(2).to_broadcast([128, 8, 3])
                S.op("dve", lambda: nc.vector.scalar_tensor_tensor(out=self.MA[:, l, w, :, :], in0=self.MOD[:, l, sc_idx:sc_idx + 8, :], scalar=1.0, in1=nrm, op0=ALU.add, op1=ALU.mult),
                     [self.BMOD], [self.BMOD])
        st.close()

    def norm_tiles(self, st, n=BLK + 2):
        return dict(sq=st.sb("nsq", [128, 8, n], BF16), tmp=st.sb("ntmp", [128, 8, n]), r0=st.sb("nr0", [128, n]), r1=st.sb("nr1", [128, n]),
                    B=[Buf() for _ in range(4)])

    def norm_block(self, nt, xs, Bxs, n, A, Bsh, hb, Bhb, bank):
        nc, S = self.nc, self.S
        sq, tmp, r0, r1 = nt["sq"], nt["tmp"], nt["r0"], nt["r1"]
        Bsq, Btmp, Br0, Br1 = nt["B"]
        S.op("act", lambda: nc.scalar.activation(out=sq[:, :, :n], in_=xs, func=AF.Square), [Bxs], [Bsq])
        ps = self.PS[bank]
        for c in range(8):
            S.op("pe", lambda: nc.tensor.matmul(ps[:, :n], lhsT=self.onesb, rhs=sq[:, c, :n], start=(c == 0), stop=(c == 7)), [Bsq], [self.BPS[bank]])
        S.op("act", lambda: nc.scalar.activation(out=r0[:, :n], in_=ps[:, :n], func=AF.Sqrt, scale=1.0 / D, bias=self.epsD), [self.BPS[bank]], [Br0])
        S.op("dve", lambda: nc.vector.reciprocal(out=r1[:, :n], in_=r0[:, :n]), [Br0], [Br1])
        S.op("dve", lambda: nc.vector.tensor_tensor(out=tmp[:, :, :n], in0=xs, in1=r1[:, :n].unsqueeze(1).to_broadcast([128, 8, n]), op=ALU.mult), [Bxs, Br1], [Btmp])
        for c in range(8):
            S.op("act", lambda: nc.scalar.activation(out=hb[:, c, :n], in_=tmp[:, c, :n], func=AF.Identity, scale=A[:, c:c + 1], bias=(Bsh[:, c:c + 1] if Bsh is not None else 0.0)),
                 [Btmp, self.BMOD], [Bhb])

    def mod_ab(self, l, w, j):
        A = self.MA[:, l, w, :, j]
        sh = self.MOD[:, l, (0 if w == 0 else 24):(8 if w == 0 else 32), j]
        gt = self.MOD[:, l, (16 if w == 0 else 40):(24 if w == 0 else 48), j]
        return A, sh, gt

    @staticmethod
    def blocks(skip_ctx=False):
        out = []
        for b in range(NB):
            for k in range(NBLK):
                if skip_ctx and k == 0:
                    continue
                out.append((b, k))
        return out

    @staticmethod
    def blk_range(k):
        seq0, seq1 = (0, TC) if k == 0 else (TC, T)
        t0 = k * BLK
        lo = max(t0 - 1, seq0)
        hi = min(t0 + BLK + 1, seq1)
        return t0, lo, hi, (t0 == seq0), (t0 + BLK == seq1)

    def ffn_stage(self, l, xin, xout, skip_ctx):
        nc, S = self.nc, self.S
        st = Stage(self, "ffn")
        Win = st.sb("win", [128, 8, 2 * DFF], BF16)
        Wout = st.sb("wout", [128, NFC, D], BF16)
        BWin = self.load_w(Win, self.W["ffn_w_in"][l], None)
        BWout = []
        osrc = self.W["ffn_w_out"][l].rearrange("(fc p) n -> p fc n", p=128)
        for f0 in range(0, NFC, 2):
            b = Buf()
            S.dma("pool", Wout[:, f0:f0 + 2, :], osrc[:, f0:f0 + 2, :], writes=[b])
            BWout.append(b)
        NH = BLK + 2
        xs = [st.sb(f"xs{i}", [128, 8, NH]) for i in range(2)]
        hb = [st.sb(f"hb{i}", [128, 8, NH], BF16) for i in range(2)]
        gt_ = [st.sb(f"g{i}", [128, NFC, BLK], BF16) for i in range(2)]
        cv = [st.sb(f"cv{i}", [128, BLK]) for i in range(2)]
        sl = [st.sb(f"sl{i}", [128, BLK]) for i in range(2)]
        Bxs, Bhb, Bg, Bcv, Bsl = [[Buf(), Buf()] for _ in range(5)]
        nt = self.norm_tiles(st)
        for i in range(2):
            S.op("dve", lambda: nc.vector.memset(xs[i], 0.0), [], [Bxs[i]])
        xiv = xin.rearrange("(c p) t -> p c t", p=128)
        xov = xout.rearrange("(c p) t -> p c t", p=128)
        blocks = self.blocks(skip_ctx)

        def load(n):
            b, k = blocks[n]
            t0, lo, hi, _, _ = self.blk_range(k)
            S.dma("sp", xs[n % 2][:, :, lo - (t0 - 1):hi - (t0 - 1)], xiv[:, :, b * T + lo:b * T + hi], writes=[Bxs[n % 2]])

        load(0)
        for n, (b, k) in enumerate(blocks):
            i = n % 2
            if n + 1 < len(blocks):
                load(n + 1)
            t0, lo, hi, first, last = self.blk_range(k)
            j = 2 if k == 0 else b
            A, sh, gate = self.mod_ab(l, 1, j)
            self.norm_block(nt, xs[i], Bxs[i], NH, A, sh, hb[i], Bhb[i], 6)
            for fc in range(NFC):
                q = fc % 2
                pa, pvv = self.PS[q], self.PS[2 + q]
                ga = BWin[(fc * 128) // 512]
                gv = BWin[(DFF + fc * 128) // 512]
                for kc in range(8):
                    S.op("pe", lambda: nc.tensor.matmul(pa[:, :NH], lhsT=Win[:, kc, fc * 128:(fc + 1) * 128], rhs=hb[i][:, kc, :], start=(kc == 0), stop=(kc == 7)),
                         [ga, Bhb[i]], [self.BPS[q]])
                for kc in range(8):
                    S.op("pe", lambda: nc.tensor.matmul(pvv[:, :BLK], lhsT=Win[:, kc, DFF + fc * 128:DFF + (fc + 1) * 128], rhs=hb[i][:, kc, 1:1 + BLK], start=(kc == 0), stop=(kc == 7)),
                         [gv, Bhb[i]], [self.BPS[2 + q]])
                w0, w1, w2, cb = self.pv(f"conv{l}_0", fc), self.pv(f"conv{l}_1", fc), self.pv(f"conv{l}_2", fc), self.pv(f"convb{l}", fc)
                S.op("act", lambda: nc.scalar.activation(out=cv[q], in_=pa[:, 1:1 + BLK], func=AF.Identity, scale=w1, bias=cb), [self.BPS[q]], [Bcv[q]])
                c0 = 1 if first else 0
                S.op("dve", lambda: nc.vector.scalar_tensor_tensor(out=cv[q][:, c0:BLK], in0=pa[:, c0:BLK], scalar=w0, in1=cv[q][:, c0:BLK], op0=ALU.mult, op1=ALU.add),
                     [self.BPS[q], Bcv[q]], [Bcv[q]])
                c1 = BLK - 1 if last else BLK
                S.op("dve", lambda: nc.vector.scalar_tensor_tensor(out=cv[q][:, 0:c1], in0=pa[:, 2:2 + c1], scalar=w2, in1=cv[q][:, 0:c1], op0=ALU.mult, op1=ALU.add),
                     [self.BPS[q], Bcv[q]], [Bcv[q]])
                S.op("act", lambda: nc.scalar.activation(out=sl[q], in_=cv[q], func=AF.Silu), [Bcv[q]], [Bsl[q]])
                S.op("dve", lambda: nc.vector.tensor_tensor(out=gt_[i][:, fc, :], in0=sl[q], in1=pvv[:, :BLK], op=ALU.mult), [Bsl[q], self.BPS[2 + q]], [Bg[i]])
            for oc in range(8):
                q = 4 + oc % 2
                po = self.PS[q]
                for fc in range(NFC):
                    S.op("pe", lambda: nc.tensor.matmul(po[:, :BLK], lhsT=Wout[:, fc, oc * 128:(oc + 1) * 128], rhs=gt_[i][:, fc, :], start=(fc == 0), stop=(fc == NFC - 1)),
                         [BWout[fc // 2], Bg[i]], [self.BPS[q]])
                S.op("dve", lambda: nc.vector.scalar_tensor_tensor(out=xs[i][:, oc, 1:1 + BLK], in0=po[:, :BLK], scalar=gate[:, oc:oc + 1], in1=xs[i][:, oc, 1:1 + BLK], op0=ALU.mult, op1=ALU.add),
                     [self.BPS[q], Bxs[i], self.BMOD], [Bxs[i]])
            S.dma("pool", xov[:, :, b * T + t0:b * T + t0 + BLK], xs[i][:, :, 1:1 + BLK], reads=[Bxs[i]])
        st.close()

    def final_stage(self, xin):
        nc, S = self.nc, self.S
        st = Stage(self, "fin")
        xs = [st.sb(f"xs{i}", [128, 8, BLK]) for i in range(2)]
        hb = [st.sb(f"hb{i}", [128, 8, BLK]) for i in range(2)]
        ot = [st.sb(f"ot{i}", [128, D]) for i in range(2)]
        Bxs, Bhb, Bot = [[Buf(), Buf()] for _ in range(3)]
        nt = self.norm_tiles(st, BLK)
        xiv = xin.rearrange("(c p) t -> p c t", p=128)
        blocks = self.blocks(True)
        A = self.pv("norm_f")

        def load(n):
            b, k = blocks[n]
            S.dma("sp", xs[n % 2], xiv[:, :, b * T + k * BLK:b * T + (k + 1) * BLK], writes=[Bxs[n % 2]])

        load(0)
        nt_i = 0
        for n, (b, k) in enumerate(blocks):
            i = n % 2
            if n + 1 < len(blocks):
                load(n + 1)
            self.norm_block(nt, xs[i], Bxs[i], BLK, A, None, hb[i], Bhb[i], 6)
            for tt in range(2):
                o = nt_i % 2
                nt_i += 1
                for hf in range(2):
                    pb = 2 * o + hf
                    for c4 in range(4):
                        c = hf * 4 + c4
                        S.op("pe", lambda: nc.tensor.transpose(out=self.PS[pb][:, c4 * 128:(c4 + 1) * 128], in_=hb[i][:, c, tt * 128:(tt + 1) * 128], identity=self.ident),
                             [Bhb[i]], [self.BPS[pb]])
                    if hf == 0:
                        S.op("dve", lambda: nc.vector.tensor_copy(out=ot[o][:, 0:512], in_=self.PS[pb]), [self.BPS[pb]], [Bot[o]])
                    else:
                        S.op("act", lambda: nc.scalar.copy(out=ot[o][:, 512:1024], in_=self.PS[pb]), [self.BPS[pb]], [Bot[o]])
                tl = k * BLK - TC + tt * 128
                S.dma("pool", self.out[b, tl:tl + 128, :], ot[o], reads=[Bot[o]])
        st.close()


def build_program(wshapes, pv_off, npv, plan=None, dbg=()):
    P = Prog(wshapes, pv_off, npv, dbg=dbg)
    xa = P.scr("xA", [D, TT])
    xb = P.scr("xB", [D, TT])
    if plan is None:
        plan = ["tr", "mod"]
        for l in range(DEPTH):
            plan += [f"mix{l}", f"ffn{l}"]
        plan += ["final"]
    cur, nxt = xa, xb
    for step in plan:
        if step == "tr":
            P.prologue_transpose(cur)
        elif step == "mod":
            P.prologue_mod()
        elif step.startswith("mix"):
            l = int(step[3:])
            P.mixer(l, cur, nxt)
            cur, nxt = nxt, cur
        elif step.startswith("ffn"):
            l = int(step[3:])
            P.ffn_stage(l, cur, nxt, skip_ctx=(l == DEPTH - 1))
            cur, nxt = nxt, cur
        elif step == "final":
            P.final_stage(cur)
    P.S.barrier()
    return P


def prep_inputs(inputs, cores=range(NCORES)):
    pv = pvec_layout(inputs)
    pva = pv.array()
    consts = make_consts()
    shared = {"pvec": pva}
    for k, v in consts.items():
        shared["c_" + k] = v
    for n in WEIGHT_NAMES:
        shared[n] = np.ascontiguousarray(inputs[n], dtype=np.float32)
    in_maps = []
    for c in cores:
        m = dict(shared)
        m["x"] = np.ascontiguousarray(inputs["x"][NB * c:NB * (c + 1)], dtype=np.float32)
        m["ctx"] = np.ascontiguousarray(inputs["ctx"][NB * c:NB * (c + 1)], dtype=np.float32)
        m["cvec"] = np.ascontiguousarray(np.concatenate([inputs["c"][NB * c:NB * (c + 1)], inputs["c_ctx"][None, :]], axis=0), dtype=np.float32)
        in_maps.append(m)
    wshapes = {n: list(inputs[n].shape) for n in WEIGHT_NAMES}
    return in_maps, wshapes, pv.off, pva.shape[1]


def kernel(**inputs):
    inputs = {k: np.asarray(v) for k, v in inputs.items()}
    in_maps, wshapes, pv_off, npv = prep_inputs(inputs)
    P = build_program(wshapes, pv_off, npv)
    res = run_bass_kernel_spmd(P.nc, in_maps, core_ids=list(range(NCORES)))
    out = np.concatenate([np.asarray(r["out"]) for r in res.results], axis=0)
    return out.astype(np.float32)


def _inproj_stage(self, l, xin, Wd, N, dst_fm, tm_specs, f32_h=False):
    nc, S = self.nc, self.S
    st = Stage(self, "ip")
    Wt = st.sb("w", [128, 8, N], BF16)
    BW = self.load_w(Wt, Wd, None)
    xs = [st.sb(f"xs{i}", [128, 8, BLK]) for i in range(2)]
    hb = [st.sb(f"hb{i}", [128, 8, BLK], BF16) for i in range(2)]
    sg = [st.sb(f"sg{i}", [128, 8, BLK]) for i in range(2)]
    tmw = max([nc_ for (_, nc_, _) in tm_specs], default=0)
    tms = [st.sb(f"tm{i}", [128, max(tmw, 1)], BF16) for i in range(2)]
    Bxs, Bhb, Bsg, Btm = [[Buf(), Buf()] for _ in range(4)]
    nt = self.norm_tiles(st, BLK)
    xiv = xin.rearrange("(c p) t -> p c t", p=128)
    dv = dst_fm.rearrange("(c p) t -> p c t", p=128)
    blocks = self.blocks(False)

    def load(n):
        b, k = blocks[n]
        S.dma("sp", xs[n % 2], xiv[:, :, b * T + k * BLK:b * T + (k + 1) * BLK], writes=[Bxs[n % 2]])

    load(0)
    sgi = 0
    tmi = 0
    pbank = 0
    for n, (b, k) in enumerate(blocks):
        i = n % 2
        if n + 1 < len(blocks):
            load(n + 1)
        j = 2 if k == 0 else b
        A, sh, _ = self.mod_ab(l, 0, j)
        self.norm_block(nt, xs[i], Bxs[i], BLK, A, sh, hb[i], Bhb[i], 6)
        col = b * T + k * BLK
        for og in range(N // 1024):
            s_ = sgi % 2
            sgi += 1
            for o8 in range(8):
                oc = og * 8 + o8
                pb = pbank % 4
                pbank += 1
                for kc in range(8):
                    S.op("pe", lambda: nc.tensor.matmul(self.PS[pb][:, :BLK], lhsT=Wt[:, kc, oc * 128:(oc + 1) * 128], rhs=hb[i][:, kc, :], start=(kc == 0), stop=(kc == 7)),
                         [BW[(oc * 128) // 512], Bhb[i]], [self.BPS[pb]])
                if o8 % 2 == 0:
                    S.op("act", lambda: nc.scalar.copy(out=sg[s_][:, o8, :], in_=self.PS[pb][:, :BLK]), [self.BPS[pb]], [Bsg[s_]])
                else:
                    S.op("dve", lambda: nc.vector.tensor_copy(out=sg[s_][:, o8, :], in_=self.PS[pb][:, :BLK]), [self.BPS[pb]], [Bsg[s_]])
            S.dma("pool", dv[:, og * 8:(og + 1) * 8, col:col + BLK], sg[s_], reads=[Bsg[s_]])
        for (c0, ncols, dst_tm) in tm_specs:
            for tt in range(BLK // 128):
                s_ = tmi % 2
                tmi += 1
                for n0 in range(0, ncols, 512):
                    pb = 4 + (pbank % 2)
                    pbank += 1
                    for kc in range(8):
                        S.op("pe", lambda: nc.tensor.matmul(self.PS[pb][:, :512], lhsT=hb[i][:, kc, tt * 128:(tt + 1) * 128], rhs=Wt[:, kc, c0 + n0:c0 + n0 + 512], start=(kc == 0), stop=(kc == 7)),
                             [BW[(c0 + n0) // 512], Bhb[i]], [self.BPS[pb]])
                    S.op("act", lambda: nc.scalar.copy(out=tms[s_][:, n0:n0 + 512], in_=self.PS[pb][:, :512]), [self.BPS[pb]], [Btm[s_]])
                S.dma("pool", dst_tm[col + tt * 128:col + (tt + 1) * 128, :], tms[s_][:, :ncols], reads=[Btm[s_]])
    st.close()


def _outproj_stage(self, l, og, Wd, xin, xout, skip_ctx):
    nc, S = self.nc, self.S
    st = Stage(self, "op")
    Wt = st.sb("w", [128, 8, D], BF16)
    BW = self.load_w(Wt, Wd, None)
    xs = [st.sb(f"xs{i}", [128, 8, BLK]) for i in range(2)]
    ob = [st.sb(f"ob{i}", [128, 8, BLK], BF16) for i in range(2)]
    Bxs, Bob = [[Buf(), Buf()] for _ in range(2)]
    xiv = xin.rearrange("(c p) t -> p c t", p=128)
    xov = xout.rearrange("(c p) t -> p c t", p=128)
    ogv = og.rearrange("(c p) t -> p c t", p=128)
    blocks = self.blocks(skip_ctx)

    def load(n):
        b, k = blocks[n]
        col = b * T + k * BLK
        S.dma("sp", xs[n % 2], xiv[:, :, col:col + BLK], writes=[Bxs[n % 2]])
        S.dma("sp", ob[n % 2], ogv[:, :, col:col + BLK], writes=[Bob[n % 2]])

    load(0)
    for n, (b, k) in enumerate(blocks):
        i = n % 2
        if n + 1 < len(blocks):
            load(n + 1)
        j = 2 if k == 0 else b
        _, _, gate = self.mod_ab(l, 0, j)
        for oc in range(8):
            pb = oc % 4
            for kc in range(8):
                S.op("pe", lambda: nc.tensor.matmul(self.PS[pb][:, :BLK], lhsT=Wt[:, kc, oc * 128:(oc + 1) * 128], rhs=ob[i][:, kc, :], start=(kc == 0), stop=(kc == 7)),
                     [BW[(oc * 128) // 512], Bob[i]], [self.BPS[pb]])
            S.op("dve", lambda: nc.vector.scalar_tensor_tensor(out=xs[i][:, oc, :], in0=self.PS[pb][:, :BLK], scalar=gate[:, oc:oc + 1], in1=xs[i][:, oc, :], op0=ALU.mult, op1=ALU.add),
                 [self.BPS[pb], Bxs[i], self.BMOD], [Bxs[i]])
        col = b * T + k * BLK
        S.dma("pool", xov[:, :, col:col + BLK], xs[i], reads=[Bxs[i]])
    st.close()


def _hgrn2_scan(self, jh, Pfm, Itm, og):
    nc, S = self.nc, self.S
    st = Stage(self, "hs")
    A_ = nc.vector
    LB = st.sb("LB", [128, 2, 8])
    OML = st.sb("OML", [128, 2, 8])
    e0 = st.sb("e0", [128, 8]); e1 = st.sb("e1", [128, 8]); rr = st.sb("rr", [128, 8]); p0 = st.sb("p0", [128, 8]); p1 = st.sb("p1", [128, 8])
    BL = Buf()
    for d in range(2):
        S.op("act", lambda: nc.scalar.activation(out=e0, in_=self.pv(f"hg_lb{d}_0"), func=AF.Exp), [], [BL])
        S.op("act", lambda: nc.scalar.activation(out=e1, in_=self.pv(f"hg_lb{d}_1"), func=AF.Exp), [BL], [BL])
        S.op("dve", lambda: A_.tensor_tensor(out=rr, in0=e0, in1=e1, op=ALU.add), [BL], [BL])
        S.op("dve", lambda: A_.reciprocal(out=rr, in_=rr), [BL], [BL])
        S.op("dve", lambda: A_.tensor_tensor(out=p0, in0=e0, in1=rr, op=ALU.mult), [BL], [BL])
        S.op("dve", lambda: A_.tensor_tensor(out=p1, in0=e1, in1=rr, op=ALU.mult), [BL], [BL])
        if jh == 1:
            S.op("dve", lambda: A_.tensor_tensor(out=p1, in0=p0, in1=p1, op=ALU.add), [BL], [BL])
        else:
            S.op("dve", lambda: A_.tensor_copy(out=p1, in_=p0), [BL], [BL])
        S.op("dve", lambda: A_.tensor_tensor(out=LB[:, d, :], in0=p1, in1=p0, op=ALU.subtract), [BL], [BL])
        S.op("dve", lambda: A_.tensor_scalar(out=OML[:, d, :], in0=LB[:, d, :], scalar1=-1.0, scalar2=1.0, op0=ALU.mult, op1=ALU.add), [BL], [BL])
    smask = st.sb("smask", [128, T])
    Bsm = Buf()
    S.dma("sp", smask, self.cd["scanmask"], writes=[Bsm])
    f32t = lambda n: st.sb(n, [128, T])
    qs = f32t("qs"); graw = f32t("graw"); kk = f32t("kk"); ep = f32t("ep"); en = f32t("en")
    z = [f32t("z0"), f32t("z1")]; bb = [f32t("b0"), f32t("b1")]; of = [f32t("of0"), f32t("of1")]
    qt = [st.sb(f"qt{d}", [128, T], BF16) for d in range(2)]
    kh = [st.sb(f"kh{d}", [128, T], BF16) for d in range(2)]
    sqb = st.sb("sqb", [128, T], BF16)
    ogb = st.sb("ogb", [128, T], BF16)
    Vt = st.sb("Vt", [64, NCH, 128], BF16)
    emid = [st.sb(f"emid{d}", [128, NCH]) for d in range(2)]
    eend = [st.sb(f"eend{d}", [128, NCH]) for d in range(2)]
    eem = [st.sb(f"eem{d}", [128, NCH]) for d in range(2)]
    Sst = [st.sb(f"S{d}", [128, 128]) for d in range(2)]
    Sm = [st.sb(f"Sm{d}", [128, 128], BF16) for d in range(2)]
    tmpS = [st.sb(f"tS{d}", [128, 128]) for d in range(2)]
    khT = [st.sb(f"khT{d}", [64, 128], BF16) for d in range(2)]
    att = [st.sb(f"att{d}", [64, 64], BF16) for d in range(2)]
    Bqs, Bgr, Bkk, Bep, Ben, Bsq, Bog, BVt = [Buf() for _ in range(8)]
    Bz, Bbb, Bof, Bqt, Bkh, Bes, BS, BSm, BtS, BkT, Batt = [[Buf(), Buf()] for _ in range(11)]
    PSb = [self.PS[i].bitcast(BF16) for i in range(8)]
    for d in range(2):
        S.op("dve", lambda: A_.memset(att[d], 0.0), [], [Batt[d]])
    cf = list(range(NCH))
    cb = list(range(TC // CH - 1, -1, -1)) + list(range(NCH - 1, TC // CH - 1, -1))
    order = [cf, cb]
    for b in range(NB):
        for h in range(8):
            rows = slice(h * 128, (h + 1) * 128)
            cols = slice(b * T, (b + 1) * T)
            S.dma("sp", qs, Pfm[0 * D + h * 128:0 * D + (h + 1) * 128, cols], writes=[Bqs])
            S.dma("sp", z[0], Pfm[3 * D + h * 128:3 * D + (h + 1) * 128, cols], writes=[Bz[0]])
            S.dma("sp", z[1], Pfm[4 * D + h * 128:4 * D + (h + 1) * 128, cols], writes=[Bz[1]])
            S.dma("sp", graw, Pfm[2 * D + h * 128:2 * D + (h + 1) * 128, cols], writes=[Bgr])
            S.dma("sp", Vt, Itm[cols, rows].rearrange("(c s) v -> s c v", s=CH), writes=[BVt])
            S.op("act", lambda: nc.scalar.activation(out=qs, in_=qs, func=AF.Silu), [Bqs], [Bqs])
            for d in range(2):
                m_idx = 32 if d == 0 else 31
                zt = z[d]
                S.op("act", lambda: nc.scalar.activation(out=zt, in_=zt, func=AF.Sigmoid), [Bz[d]], [Bz[d]])
                S.op("act", lambda: nc.scalar.activation(out=zt, in_=zt, func=AF.Identity, scale=OML[:, d, h:h + 1], bias=LB[:, d, h:h + 1]), [Bz[d], BL], [Bz[d]])
                S.op("act", lambda: nc.scalar.activation(out=kk, in_=zt, func=AF.Identity, scale=-1.0, bias=self.onesf[:, 0:1]), [Bz[d]], [Bkk])
                S.op("act", lambda: nc.scalar.activation(out=zt, in_=zt, func=AF.Ln), [Bz[d]], [Bz[d]])
                S.op("dve", lambda: A_.tensor_tensor_scan(out=bb[d], data0=smask, data1=zt, initial=0.0, op0=ALU.mult, op1=ALU.add), [Bsm, Bz[d]], [Bbb[d]])
                b3 = bb[d].rearrange("p (c s) -> p c s", s=CH)
                if d == 1:
                    S.op("dve", lambda: A_.tensor_tensor(out=zt, in0=zt, in1=bb[d], op=ALU.subtract), [Bz[d], Bbb[d]], [Bz[d]])
                    S.op("dve", lambda: A_.tensor_tensor(out=ep.rearrange("p (c s) -> p c s", s=CH), in0=zt.rearrange("p (c s) -> p c s", s=CH),
                                                          in1=b3[:, :, CH - 1:CH].to_broadcast([128, NCH, CH]), op=ALU.add), [Bz[d], Bbb[d]], [Bep])
                    S.op("dve", lambda: A_.tensor_copy(out=bb[d], in_=ep), [Bep], [Bbb[d]])
                e_idx = CH - 1 if d == 0 else 0
                S.op("act", lambda: nc.scalar.activation(out=emid[d], in_=b3[:, :, m_idx], func=AF.Exp), [Bbb[d]], [Bes[d]])
                S.op("act", lambda: nc.scalar.activation(out=eend[d], in_=b3[:, :, e_idx], func=AF.Exp), [Bbb[d]], [Bes[d]])
                S.op("dve", lambda: A_.tensor_tensor(out=eem[d], in0=b3[:, :, e_idx], in1=b3[:, :, m_idx], op=ALU.subtract), [Bbb[d]], [Bes[d]])
                S.op("act", lambda: nc.scalar.activation(out=eem[d], in_=eem[d], func=AF.Exp), [Bes[d]], [Bes[d]])
                S.op("dve", lambda: A_.tensor_tensor(out=ep.rearrange("p (c s) -> p c s", s=CH), in0=b3, in1=b3[:, :, m_idx:m_idx + 1].to_broadcast([128, NCH, CH]), op=ALU.subtract),
                     [Bbb[d]], [Bep])
                S.op("act", lambda: nc.scalar.activation(out=en, in_=ep, func=AF.Exp, scale=-1.0), [Bep], [Ben])
                S.op("act", lambda: nc.scalar.activation(out=ep, in_=ep, func=AF.Exp), [Bep], [Bep])
                S.op("dve", lambda: A_.tensor_tensor(out=qt[d], in0=qs, in1=ep, op=ALU.mult), [Bqs, Bep], [Bqt[d]])
                S.op("dve", lambda: A_.tensor_tensor(out=kh[d], in0=kk, in1=en, op=ALU.mult), [Bkk, Ben], [Bkh[d]])
                S.op("dve", lambda: A_.memset(Sst[d], 0.0), [], [BS[d]])
                S.op("dve", lambda: A_.memset(Sm[d], 0.0), [], [BSm[d]])
            def hstep(d, step):
                c = order[d][step]
                cs = slice(c * CH, (c + 1) * CH)
                pb = d * 4
                mk = (self.masks[0:64, 64:128] if d == 0 else self.masks[0:64, 192:256]).bitcast(mybir.dt.uint32)
                S.op("pe", lambda: nc.tensor.transpose(out=PSb[pb][0:64, 0:128], in_=kh[d][:, cs], identity=self.identb), [Bkh[d]], [self.BPS[pb]])
                S.op("pe", lambda: nc.tensor.matmul(self.PS[pb + 1][0:64, 0:64], lhsT=kh[d][:, cs], rhs=qt[d][:, cs], start=True, stop=True), [Bkh[d], Bqt[d]], [self.BPS[pb + 1]])
                yield
                S.op("act", lambda: nc.scalar.copy(out=khT[d], in_=PSb[pb][0:64, 0:128]), [self.BPS[pb]], [BkT[d]])
                S.op("dve", lambda: A_.copy_predicated(out=att[d], mask=mk, data=self.PS[pb + 1][0:64, 0:64]), [self.BPS[pb + 1]], [Batt[d]])
                S.op("pe", lambda: nc.tensor.matmul(self.PS[pb + 2][:, 0:64], lhsT=Vt[:, c, :], rhs=att[d], start=True, stop=False), [BVt, Batt[d]], [self.BPS[pb + 2]])
                S.op("pe", lambda: nc.tensor.matmul(self.PS[pb + 2][:, 0:64], lhsT=Sm[d], rhs=qt[d][:, cs], start=False, stop=True), [BSm[d], Bqt[d]], [self.BPS[pb + 2]])
                S.op("pe", lambda: nc.tensor.matmul(self.PS[pb + 3][:, 0:128], lhsT=khT[d], rhs=Vt[:, c, :], start=True, stop=True), [BkT[d], BVt], [self.BPS[pb + 3]])
                yield
                S.op("act", lambda: nc.scalar.activation(out=tmpS[d], in_=self.PS[pb + 3][:, 0:128], func=AF.Identity, scale=eem[d][:, c:c + 1]), [self.BPS[pb + 3], Bes[d]], [BtS[d]])
                S.op("dve", lambda: A_.scalar_tensor_tensor(out=Sst[d], in0=Sst[d], scalar=eend[d][:, c:c + 1], in1=tmpS[d], op0=ALU.mult, op1=ALU.add), [BS[d], BtS[d], Bes[d]], [BS[d]])
                S.op("act", lambda: nc.scalar.copy(out=of[d][:, cs], in_=self.PS[pb + 2][:, 0:64]), [self.BPS[pb + 2]], [Bof[d]])
                if step + 1 < NCH:
                    cn = order[d][step + 1]
                    S.op("dve", lambda: A_.tensor_scalar(out=Sm[d], in0=Sst[d], scalar1=emid[d][:, cn:cn + 1], scalar2=None, op0=ALU.mult), [BS[d], Bes[d]], [BSm[d]])

            for step in range(NCH):
                gens = [hstep(d, step) for d in range(2)]
                while gens:
                    for g_ in list(gens):
                        try:
                            next(g_)
                        except StopIteration:
                            gens.remove(g_)
            S.op("dve", lambda: A_.tensor_tensor(out=of[0], in0=of[0], in1=of[1], op=ALU.add), [Bof[0], Bof[1]], [Bof[0]])
            S.op("act", lambda: nc.scalar.activation(out=sqb, in_=of[0], func=AF.Square), [Bof[0]], [Bsq])
            for pc in range(6):
                sl_ = slice(pc * 384, (pc + 1) * 384)
                pb = pc % 2
                S.op("pe", lambda: nc.tensor.matmul(self.PS[pb][:, 0:384], lhsT=self.onesb, rhs=sqb[:, sl_], start=True, stop=True), [Bsq], [self.BPS[pb]])
                S.op("act", lambda: nc.scalar.activation(out=ep[:, sl_], in_=self.PS[pb][:, 0:384], func=AF.Sqrt, scale=1.0 / 128, bias=self.epsD), [self.BPS[pb]], [Bep])
            S.op("dve", lambda: A_.reciprocal(out=ep, in_=ep), [Bep], [Bep])
            S.op("dve", lambda: A_.tensor_tensor(out=of[0], in0=of[0], in1=ep, op=ALU.mult), [Bof[0], Bep], [Bof[0]])
            S.op("act", lambda: nc.scalar.activation(out=graw, in_=graw, func=AF.Silu), [Bgr], [Bgr])
            S.op("dve", lambda: A_.scalar_tensor_tensor(out=ogb, in0=of[0], scalar=self.pv(f"hg_norm{jh}", 0), in1=graw, op0=ALU.mult, op1=ALU.mult), [Bof[0], Bgr], [Bog])
            S.dma("pool", og[rows, cols], ogb, reads=[Bog])
    st.close()


def _mixer(self, l, cur, nxt):
    kind, j = l % 3, l // 3
    last = (l == DEPTH - 1)
    og = self.scr("og", [D, TT], BF16)
    if kind == 0:
        Pfm = self.scr("hgP", [5 * D, TT])
        Itm = self.scr("hgI", [TT, D], BF16)
        self.inproj_stage(l, cur, self.W["hg_w_in"][j], 5 * D, Pfm, [(D, D, Itm)])
        self.hgrn2_scan(j, Pfm, Itm, og)
        self.outproj_stage(l, og, self.W["hg_w_o"][j], cur, nxt, last)
    elif kind == 1:
        self.rwkv_mixer(l, cur, og)
        self.outproj_stage(l, og, self.W["rw_w_o"][j], cur, nxt, last)
    else:
        self.mla_mixer(l, cur, og)
        self.outproj_stage(l, og, self.W["mla_w_o"][j], cur, nxt, last)


Prog.inproj_stage = _inproj_stage
Prog.outproj_stage = _outproj_stage
Prog.hgrn2_scan = _hgrn2_scan
Prog.mixer = _mixer


def _mla_mixer(self, l, xin, og):
    nc, S = self.nc, self.S
    A_ = nc.vector
    NH = 16
    QN = self.scr("mlaQN", [96, NH, TT], BF16)
    KN = self.scr("mlaKN", [96, NH, TT], BF16)
    VT = self.scr("mlaVT", [TT, D], BF16)
    st = Stage(self, "m1")
    Wd = st.sb("wd", [128, 8, 544], BF16)
    Wq = st.sb("wq", [128, 2, 1536], BF16)
    Wk = st.sb("wk", [128, 2, 2048], BF16)
    Wdr = st.sb("wdr", [128, 8, 32], BF16)
    Wqr = st.sb("wqr", [128, 2, NH, 32], BF16)
    BWd, BWq, BWk, BWr = Buf(), Buf(), Buf(), Buf()
    S.dma("pool", Wd, self.W["mla_w_dqkv"][0].rearrange("(kc p) n -> p kc n", p=128), writes=[BWd])
    wqv = self.W["mla_w_uq"][0].rearrange("(kc p) n -> p kc n", p=128)
    for i3 in range(3):
        S.dma("pool", Wq[:, :, i3 * 512:(i3 + 1) * 512], wqv[:, :, i3 * 512:(i3 + 1) * 512], writes=[BWq])
    wkv = self.W["mla_w_ukv"][0].rearrange("(kc p) n -> p kc n", p=128)
    for i4 in range(4):
        S.dma("pool", Wk[:, :, i4 * 512:(i4 + 1) * 512], wkv[:, :, i4 * 512:(i4 + 1) * 512], writes=[BWk])
    Wq4 = Wq.rearrange("p k (h c) -> p k h c", c=96)
    for seg in range(2):
        for half in range(2):
            sgn = -1.0 if half == 0 else 1.0
            so = 64 + seg * 16 + (1 - half) * 8
            do = seg * 16 + half * 8
            S.op("act", lambda: nc.scalar.activation(out=Wqr[:, :, :, do:do + 8], in_=Wq4[:, :, :, so:so + 8], func=AF.Copy, scale=sgn), [BWq], [BWr])
            so2 = 512 + seg * 16 + (1 - half) * 8
            S.op("act", lambda: nc.scalar.activation(out=Wdr[:, :, do:do + 8], in_=Wd[:, :, so2:so2 + 8], func=AF.Copy, scale=sgn), [BWd], [BWr])
    cos = st.sb("cos", [96, T]); sin = st.sb("sin", [96, T])
    Bcs = Buf()
    RP = slice(64, 96)
    S.dma("sp", cos[RP, :], self.cd["rope_cos"], writes=[Bcs])
    S.dma("sp", sin[RP, :], self.cd["rope_sin"], writes=[Bcs])
    xs = [st.sb(f"xs{i}", [128, 8, BLK]) for i in range(2)]
    hb = [st.sb(f"hb{i}", [128, 8, BLK], BF16) for i in range(2)]
    Bxs, Bhb = [[Buf(), Buf()] for _ in range(2)]
    nt = self.norm_tiles(st, BLK)
    cs_ = st.sb("cs", [128, 4, BLK]); csq = st.sb("csq", [128, 4, BLK], BF16); cn = st.sb("cn", [128, 4, BLK], BF16)
    rr0 = st.sb("rr0", [128, 2, BLK]); rr1 = st.sb("rr1", [128, 2, BLK]); ctmp = st.sb("ctmp", [128, 4, BLK])
    Bcs_, Bcsq, Bcn, Brr, Bct = [Buf() for _ in range(5)]
    qn_s = [st.sb(f"qns{i}", [96, NH, BLK], BF16) for i in range(2)]
    kn_s = [st.sb(f"kns{i}", [96, NH, BLK], BF16) for i in range(2)]
    vt_s = [st.sb(f"vts{i}", [128, D], BF16) for i in range(2)]
    t1 = st.sb("t1", [96, 2, BLK]); t2 = st.sb("t2", [96, 2, BLK])
    Bt1, Bt2 = Buf(), Buf()
    Bqn, Bkn, Bqr, Bkr, Bvt = [[Buf(), Buf()] for _ in range(5)]
    xiv = xin.rearrange("(c p) t -> p c t", p=128)
    blocks = self.blocks(False)

    def load(n):
        b, k = blocks[n]
        S.dma("sp", xs[n % 2], xiv[:, :, b * T + k * BLK:b * T + (k + 1) * BLK], writes=[Bxs[n % 2]])

    load(0)
    vti = 0
    for n, (b, k) in enumerate(blocks):
        i = n % 2
        if n + 1 < len(blocks):
            load(n + 1)
        j = 2 if k == 0 else b
        A, sh, _ = self.mod_ab(l, 0, j)
        self.norm_block(nt, xs[i], Bxs[i], BLK, A, sh, hb[i], Bhb[i], 6)
        col = b * T + k * BLK
        tcol = slice(k * BLK, (k + 1) * BLK)
        for c4 in range(4):
            pb = c4 // 2
            for kc in range(8):
                S.op("pe", lambda: nc.tensor.matmul(self.PS[pb][:, (c4 % 2) * BLK:(c4 % 2 + 1) * BLK], lhsT=Wd[:, kc, c4 * 128:(c4 + 1) * 128], rhs=hb[i][:, kc, :], start=(kc == 0), stop=(kc == 7)),
                     [BWd, Bhb[i]], [self.BPS[pb]])
        for kc in range(8):
            S.op("pe", lambda: nc.tensor.matmul(self.PS[2][RP, 0:BLK], lhsT=Wd[:, kc, 512:544], rhs=hb[i][:, kc, :], start=(kc == 0), stop=(kc == 7)), [BWd, Bhb[i]], [self.BPS[2]])
        for kc in range(8):
            S.op("pe", lambda: nc.tensor.matmul(self.PS[2][RP, BLK:2 * BLK], lhsT=Wdr[:, kc, :], rhs=hb[i][:, kc, :], start=(kc == 0), stop=(kc == 7)), [BWr, Bhb[i]], [self.BPS[2]])
        for pb in range(2):
            S.op("act", lambda: nc.scalar.copy(out=cs_[:, 2 * pb:2 * pb + 2, :], in_=self.PS[pb].rearrange("p (c t) -> p c t", c=2)), [self.BPS[pb]], [Bcs_])
            S.op("act", lambda: nc.scalar.activation(out=csq[:, 2 * pb:2 * pb + 2, :], in_=self.PS[pb].rearrange("p (c t) -> p c t", c=2), func=AF.Square), [self.BPS[pb]], [Bcsq])
        S.op("dve", lambda: A_.tensor_tensor(out=t1[RP, 0, :], in0=self.PS[2][RP, 0:BLK], in1=cos[RP, tcol], op=ALU.mult), [self.BPS[2], Bcs], [Bt1])
        S.op("dve", lambda: A_.tensor_tensor(out=t2[RP, 0, :], in0=self.PS[2][RP, BLK:2 * BLK], in1=sin[RP, tcol], op=ALU.mult), [self.BPS[2], Bcs], [Bt2])
        S.op("dve", lambda: A_.tensor_tensor(out=kn_s[i][RP, :, :], in0=t1[RP, 0:1, :].to_broadcast([32, NH, BLK]), in1=t2[RP, 0:1, :].to_broadcast([32, NH, BLK]), op=ALU.add), [Bt1, Bt2], [Bkn[i]])
        for w in range(2):
            for c in range(2):
                S.op("pe", lambda: nc.tensor.matmul(self.PS[3][:, w * BLK:(w + 1) * BLK], lhsT=self.onesb, rhs=csq[:, 2 * w + c, :], start=(c == 0), stop=(c == 1)), [Bcsq], [self.BPS[3]])
        S.op("act", lambda: nc.scalar.activation(out=rr0, in_=self.PS[3].rearrange("p (w t) -> p w t", w=2), func=AF.Sqrt, scale=1.0 / 256, bias=self.epsD), [self.BPS[3]], [Brr])
        S.op("dve", lambda: A_.reciprocal(out=rr1, in_=rr0), [Brr], [Brr])
        S.op("dve", lambda: A_.tensor_tensor(out=ctmp.rearrange("p (w c) t -> p w c t", w=2), in0=cs_.rearrange("p (w c) t -> p w c t", w=2),
                                              in1=rr1.unsqueeze(2).to_broadcast([128, 2, 2, BLK]), op=ALU.mult), [Bcs_, Brr], [Bct])
        for c4 in range(4):
            gname = "mla_q_norm" if c4 < 2 else "mla_kv_norm"
            S.op("act", lambda: nc.scalar.activation(out=cn[:, c4, :], in_=ctmp[:, c4, :], func=AF.Identity, scale=self.pv(gname, c4 % 2)), [Bct], [Bcn])
        for hp in range(8):
            for which in range(2):
                pb = 4 + (2 * hp + which) % 2
                Wt_, coff, hw, ci = (Wq, 0, 96, 0) if which == 0 else (Wk, 0, 128, 2)
                for hh in range(2):
                    h = 2 * hp + hh
                    for kc in range(2):
                        S.op("pe", lambda: nc.tensor.matmul(self.PS[pb][0:64, hh * BLK:(hh + 1) * BLK], lhsT=Wt_[:, kc, h * hw:h * hw + 64], rhs=cn[:, ci + kc, :], start=(kc == 0), stop=(kc == 1)),
                             [BWq if which == 0 else BWk, Bcn], [self.BPS[pb]])
                dst = qn_s[i] if which == 0 else kn_s[i]
                Bd = Bqn[i] if which == 0 else Bkn[i]
                if which == 0:
                    S.op("act", lambda: nc.scalar.copy(out=dst[0:64, 2 * hp:2 * hp + 2, :], in_=self.PS[pb][0:64, :].rearrange("p (h t) -> p h t", h=2)), [self.BPS[pb]], [Bd])
                else:
                    S.op("dve", lambda: A_.tensor_copy(out=dst[0:64, 2 * hp:2 * hp + 2, :], in_=self.PS[pb][0:64, :].rearrange("p (h t) -> p h t", h=2)), [self.BPS[pb]], [Bd])
            for hh in range(2):
                h = 2 * hp + hh
                for kc in range(2):
                    S.op("pe", lambda: nc.tensor.matmul(self.PS[6][RP, hh * BLK:(hh + 1) * BLK], lhsT=Wq[:, kc, h * 96 + 64:h * 96 + 96], rhs=cn[:, kc, :], start=(kc == 0), stop=(kc == 1)), [BWq, Bcn], [self.BPS[6]])
                for kc in range(2):
                    S.op("pe", lambda: nc.tensor.matmul(self.PS[7][RP, hh * BLK:(hh + 1) * BLK], lhsT=Wqr[:, kc, h, :], rhs=cn[:, kc, :], start=(kc == 0), stop=(kc == 1)), [BWr, Bcn], [self.BPS[7]])
            cosb = cos[RP, tcol].unsqueeze(1).to_broadcast([32, 2, BLK])
            sinb = sin[RP, tcol].unsqueeze(1).to_broadcast([32, 2, BLK])
            S.op("dve", lambda: A_.tensor_tensor(out=t1[RP, :, :], in0=self.PS[6][RP, :].rearrange("p (h t) -> p h t", h=2), in1=cosb, op=ALU.mult), [self.BPS[6], Bcs], [Bt1])
            S.op("dve", lambda: A_.tensor_tensor(out=t2[RP, :, :], in0=self.PS[7][RP, :].rearrange("p (h t) -> p h t", h=2), in1=sinb, op=ALU.mult), [self.BPS[7], Bcs], [Bt2])
            S.op("dve", lambda: A_.tensor_tensor(out=qn_s[i][RP, 2 * hp:2 * hp + 2, :], in0=t1[RP, :, :], in1=t2[RP, :, :], op=ALU.add), [Bt1, Bt2], [Bqn[i]])
        S.dma("pool", QN[:, :, col:col + BLK], qn_s[i], reads=[Bqn[i]])
        S.dma("pool", KN[:, :, col:col + BLK], kn_s[i], reads=[Bkn[i]])
        Wkv = Wk.rearrange("p k (h c) -> p k h c", c=128)
        for tt in range(BLK // 128):
            vi = vti % 2
            vti += 1
            for hf in range(2):
                pb = 4 + hf
                for kc in range(2):
                    S.op("pe", lambda: nc.tensor.matmul(self.PS[pb][:, 0:512], lhsT=cn[:, 2 + kc, tt * 128:(tt + 1) * 128], rhs=Wkv[:, kc, hf * 8:(hf + 1) * 8, 64:128], start=(kc == 0), stop=(kc == 1)),
                         [BWk, Bcn], [self.BPS[pb]])
                S.op("act", lambda: nc.scalar.copy(out=vt_s[vi][:, hf * 512:(hf + 1) * 512], in_=self.PS[pb][:, 0:512]), [self.BPS[pb]], [Bvt[vi]])
            S.dma("pool", VT[col + tt * 128:col + (tt + 1) * 128, :], vt_s[vi], reads=[Bvt[vi]])
    st.close()
    st = Stage(self, "m2")
    NKT = T // 128
    Vall = st.sb("Vall", [128, NKT, D], BF16)
    KNh = [st.sb(f"KNh{i}", [96, T], BF16) for i in range(2)]
    QNh = [st.sb(f"QNh{i}", [96, T], BF16) for i in range(2)]
    VX = [st.sb(f"VX{i}", [128, NKT, 65], BF16) for i in range(2)]
    PT = [st.sb(f"PT{i}", [128, 512], BF16) for i in range(3)]
    rd = st.sb("rd", [65, 512]); rb = [st.sb(f"rb{i}", [64, 512]) for i in range(2)]
    ob = [st.sb(f"ob{i}", [64, 512], BF16) for i in range(2)]
    BVa, BKR, Brd = Buf(), Buf(), Buf()
    BKN, BQN, BQR, BVX, Brb, Bob = [[Buf(), Buf()] for _ in range(6)]
    BPT = [Buf() for _ in range(3)]
    for i in range(2):
        S.op("pool", lambda: nc.gpsimd.memset(VX[i], 1.0), [], [BVX[i]])
    qblocks = [(0, TC, 2)] + [(TC + qb * 512, 512, NKT) for qb in range(4)]
    pti = 0
    hn = 0
    for b in range(NB):
        c0 = b * T
        S.dma("sp", Vall, VT[c0:c0 + T, :].rearrange("(kt p) v -> p kt v", p=128), writes=[BVa])
        for h in range(NH):
            i = hn % 2
            hn += 1
            S.dma("sp", KNh[i], KN[:, h, c0:c0 + T], writes=[BKN[i]])
            S.dma("sp", QNh[i], QN[:, h, c0:c0 + T], writes=[BQN[i]])
            S.op("pool", lambda: nc.gpsimd.tensor_copy(out=VX[i][:, :, 0:64], in_=Vall[:, :, h * 64:(h + 1) * 64]), [BVa], [BVX[i]])
            for qi, (q0, nq, nkt) in enumerate(qblocks):
                po = 4 + (qi % 2)

                def score(kt):
                    ps = kt % 4
                    ks = slice(kt * 128, (kt + 1) * 128)
                    S.op("pe", lambda: nc.tensor.matmul(self.PS[ps][:, 0:nq], lhsT=KNh[i][:, ks], rhs=QNh[i][:, q0:q0 + nq], start=True, stop=True), [BKN[i], BQN[i]], [self.BPS[ps]])

                score(0)
                if nkt > 1:
                    score(1)
                for kt in range(nkt):
                    ps = kt % 4
                    p3 = pti % 3
                    pti += 1
                    if kt + 2 < nkt:
                        score(kt + 2)
                    S.op("act", lambda: nc.scalar.activation(out=PT[p3][:, 0:nq], in_=self.PS[ps][:, 0:nq], func=AF.Exp, scale=MLA_SCALE), [self.BPS[ps]], [BPT[p3]])
                    S.op("pe", lambda: nc.tensor.matmul(self.PS[po][0:65, 0:nq], lhsT=VX[i][:, kt, :], rhs=PT[p3][:, 0:nq], start=(kt == 0), stop=(kt == nkt - 1)), [BVX[i], BPT[p3]], [self.BPS[po]])
                r2 = qi % 2
                S.op("dve", lambda: A_.reciprocal(out=rd[64:65, 0:nq], in_=self.PS[po][64:65, 0:nq]), [self.BPS[po]], [Brd])
                S.op("pe", lambda: nc.tensor.matmul(self.PS[6 + r2][0:64, 0:nq], lhsT=self.onesf[64:65, 0:64], rhs=rd[64:65, 0:nq], start=True, stop=True), [Brd], [self.BPS[6 + r2]])
                S.op("act", lambda: nc.scalar.copy(out=rb[r2][:, 0:nq], in_=self.PS[6 + r2][0:64, 0:nq]), [self.BPS[6 + r2]], [Brb[r2]])
                S.op("dve", lambda: A_.tensor_tensor(out=ob[r2][:, 0:nq], in0=self.PS[po][0:64, 0:nq], in1=rb[r2][:, 0:nq], op=ALU.mult), [self.BPS[po], Brb[r2]], [Bob[r2]])
                S.dma("pool", og[h * 64:(h + 1) * 64, c0 + q0:c0 + q0 + nq], ob[r2][:, 0:nq], reads=[Bob[r2]])
    st.close()


Prog.mla_mixer = _mla_mixer


RW_ARR = ["r", "kt0", "kt1", "be0", "be1", "kap", "lw0", "lw1", "v", "g"]


def _rwkv_proj(self, l, xin, RWP, Vtm):
    nc, S = self.nc, self.S
    A_ = nc.vector
    st = Stage(self, "r1")
    Wrkv = st.sb("wrkv", [128, 8, 3 * D], BF16)
    BWrkv = []
    for i3 in range(3):
        v_ = self.W["rw_w_rkv"][0, i3].rearrange("(kc p) n -> p kc n", p=128)
        for hf in range(2):
            bb_ = Buf()
            S.dma("pool", Wrkv[:, :, i3 * D + hf * 512:i3 * D + (hf + 1) * 512], v_[:, :, hf * 512:(hf + 1) * 512], writes=[bb_])
            BWrkv.append(bb_)
    W1 = st.sb("w1", [128, 8, 2, 64], BF16); A1 = st.sb("a1", [128, 8, 2, 64], BF16); G1 = st.sb("g1", [128, 8, 160], BF16)
    W2 = st.sb("w2", [64, 2, D], BF16); A2 = st.sb("a2", [64, 2, D], BF16); G2a = st.sb("g2a", [128, D], BF16); G2b = st.sb("g2b", [32, D], BF16)
    Bsw = Buf()
    for d in range(2):
        S.dma("pool", W1[:, :, d, :], self.W["rw_w1"][0, d].rearrange("(kc p) n -> p kc n", p=128), writes=[Bsw])
        S.dma("pool", A1[:, :, d, :], self.W["rw_a1"][0, d].rearrange("(kc p) n -> p kc n", p=128), writes=[Bsw])
        S.dma("pool", W2[:, d, :], self.W["rw_w2"][0, d], writes=[Bsw])
        S.dma("pool", A2[:, d, :], self.W["rw_a2"][0, d], writes=[Bsw])
    S.dma("pool", G1, self.W["rw_g1"][0].rearrange("(kc p) n -> p kc n", p=128), writes=[Bsw])
    S.dma("pool", G2a, self.W["rw_g2"][0, 0:128, :], writes=[Bsw])
    S.dma("pool", G2b, self.W["rw_g2"][0, 128:160, :], writes=[Bsw])
    NH_ = BLK + 2
    xs = [st.sb(f"xs{i}", [128, 8, NH_]) for i in range(2)]
    hf_ = st.sb("hf", [128, 8, NH_])
    dx = st.sb("dx", [128, 8, BLK])
    xj = [st.sb(f"xj{j}", [128, 8, BLK], BF16) for j in range(6)]
    Bxs = [Buf(), Buf()]
    Bhf, Bdx = Buf(), Buf()
    Bxj = [Buf() for _ in range(6)]
    nt = self.norm_tiles(st)
    lt = st.sb("lt", [64, 5, BLK], BF16)
    gh = st.sb("gh", [128, BLK], BF16)
    Blt = Buf()
    stg = [st.sb(f"stg{i}", [128, 10, BLK]) for i in range(2)]
    Bstg = [Buf(), Buf()]
    tmp = [st.sb(f"tmp{i}", [128, BLK]) for i in range(6)]
    Btmp = [Buf() for _ in range(6)]
    sqb = st.sb("sqb", [128, BLK], BF16)
    Bsqb = Buf()
    vts = [st.sb(f"vts{i}", [128, D], BF16) for i in range(2)]
    Bvts = [Buf(), Buf()]
    for i in range(2):
        S.op("dve", lambda: A_.memset(xs[i], 0.0), [], [Bxs[i]])
    xiv = xin.rearrange("(c p) t -> p c t", p=128)
    blocks = self.blocks(False)
    blk64b = st.sb("blk64b", [128, 128], BF16)
    Bb64 = Buf()
    S.op("dve", lambda: A_.tensor_copy(out=blk64b, in_=self.blk64), [], [Bb64])

    def load(n):
        b, k = blocks[n]
        t0, lo, hi, _, _ = self.blk_range(k)
        S.dma("sp", xs[n % 2][:, :, lo - (t0 - 1):hi - (t0 - 1)], xiv[:, :, b * T + lo:b * T + hi], writes=[Bxs[n % 2]])

    load(0)
    si = 0
    vi_ = 0
    pbk = 0
    for n, (b, k) in enumerate(blocks):
        i = n % 2
        if n + 1 < len(blocks):
            load(n + 1)
        t0, lo, hi, first, last = self.blk_range(k)
        j = 2 if k == 0 else b
        A, sh, _ = self.mod_ab(l, 0, j)
        self.norm_block(nt, xs[i], Bxs[i], NH_, A, sh, hf_, Bhf, 6)
        if first:
            S.op("dve", lambda: A_.memset(hf_[:, :, 0:1], 0.0), [], [Bhf])
        if last:
            S.op("dve", lambda: A_.memset(hf_[:, :, NH_ - 1:NH_], 0.0), [], [Bhf])
        S.op("dve", lambda: A_.tensor_tensor(out=dx, in0=hf_[:, :, 0:BLK], in1=hf_[:, :, 2:2 + BLK], op=ALU.add), [Bhf], [Bdx])
        S.op("dve", lambda: A_.scalar_tensor_tensor(out=dx, in0=dx, scalar=0.5, in1=hf_[:, :, 1:1 + BLK], op0=ALU.mult, op1=ALU.subtract), [Bhf, Bdx], [Bdx])
        for jj in range(6):
            for c in range(8):
                S.op("dve", lambda: A_.scalar_tensor_tensor(out=xj[jj][:, c, :], in0=dx[:, c, :], scalar=self.pv(f"rw_mu{jj}", c), in1=hf_[:, c, 1:1 + BLK], op0=ALU.mult, op1=ALU.add),
                     [Bdx, Bhf], [Bxj[jj]])
        for d in range(2):
            for kc in range(8):
                S.op("pe", lambda: nc.tensor.matmul(self.PS[5][0:64, d * BLK:(d + 1) * BLK], lhsT=W1[:, kc, d, :], rhs=xj[1][:, kc, :], start=(kc == 0), stop=(kc == 7)), [Bsw, Bxj[1]], [self.BPS[5]])
        S.op("act", lambda: nc.scalar.activation(out=lt[:, 0:2, :], in_=self.PS[5][0:64, :].rearrange("p (d t) -> p d t", d=2), func=AF.Tanh), [self.BPS[5]], [Blt])
        for d in range(2):
            for kc in range(8):
                S.op("pe", lambda: nc.tensor.matmul(self.PS[5][0:64, d * BLK:(d + 1) * BLK], lhsT=A1[:, kc, d, :], rhs=xj[4][:, kc, :], start=(kc == 0), stop=(kc == 7)), [Bsw, Bxj[4]], [self.BPS[5]])
        S.op("act", lambda: nc.scalar.copy(out=lt[:, 2:4, :], in_=self.PS[5][0:64, :].rearrange("p (d t) -> p d t", d=2)), [self.BPS[5]], [Blt])
        for kc in range(8):
            S.op("pe", lambda: nc.tensor.matmul(self.PS[5][:, 0:BLK], lhsT=G1[:, kc, 0:128], rhs=xj[5][:, kc, :], start=(kc == 0), stop=(kc == 7)), [Bsw, Bxj[5]], [self.BPS[5]])
        for kc in range(8):
            S.op("pe", lambda: nc.tensor.matmul(self.PS[5][0:32, BLK:2 * BLK], lhsT=G1[:, kc, 128:160], rhs=xj[5][:, kc, :], start=(kc == 0), stop=(kc == 7)), [Bsw, Bxj[5]], [self.BPS[5]])
        S.op("act", lambda: nc.scalar.activation(out=gh, in_=self.PS[5][:, 0:BLK], func=AF.Sigmoid), [self.BPS[5]], [Blt])
        S.op("act", lambda: nc.scalar.activation(out=lt[0:32, 4, :], in_=self.PS[5][0:32, BLK:2 * BLK], func=AF.Sigmoid), [self.BPS[5]], [Blt])
        col = b * T + t0
        for c in range(8):
            s_ = si % 2
            si += 1
            sg_ = stg[s_]
            Bs = Bstg[s_]
            cs = slice(c * 128, (c + 1) * 128)

            def bank():
                nonlocal pbk
                pbk += 1
                return pbk % 5

            prk = []
            for which, xsrc in ((0, 0), (1, 2), (2, 3)):
                pb = bank()
                for kc in range(8):
                    S.op("pe", lambda: nc.tensor.matmul(self.PS[pb][:, 0:BLK], lhsT=Wrkv[:, kc, which * D + c * 128:which * D + (c + 1) * 128], rhs=xj[xsrc][:, kc, :], start=(kc == 0), stop=(kc == 7)),
                         [BWrkv[which * 2 + (c // 4)], Bxj[xsrc]], [self.BPS[pb]])
                prk.append(pb)
            S.op("act", lambda: nc.scalar.copy(out=sg_[:, 0, :], in_=self.PS[prk[0]][:, 0:BLK]), [self.BPS[prk[0]]], [Bs])
            S.op("act", lambda: nc.scalar.copy(out=sg_[:, 8, :], in_=self.PS[prk[2]][:, 0:BLK]), [self.BPS[prk[2]]], [Bs])
            kraw = tmp[0]
            S.op("act", lambda: nc.scalar.copy(out=kraw, in_=self.PS[prk[1]][:, 0:BLK]), [self.BPS[prk[1]]], [Btmp[0]])
            S.op("dve", lambda: A_.tensor_scalar(out=tmp[1], in0=kraw, scalar1=self.pv("rw_k_k", c), scalar2=None, op0=ALU.mult), [Btmp[0]], [Btmp[1]])
            S.op("act", lambda: nc.scalar.activation(out=sqb, in_=tmp[1], func=AF.Square), [Btmp[1]], [Bsqb])
            pb = bank()
            S.op("pe", lambda: nc.tensor.matmul(self.PS[pb][:, 0:BLK], lhsT=blk64b, rhs=sqb, start=True, stop=True), [Bsqb, Bb64], [self.BPS[pb]])
            S.op("act", lambda: nc.scalar.activation(out=tmp[2], in_=self.PS[pb][:, 0:BLK], func=AF.Sqrt), [self.BPS[pb]], [Btmp[2]])
            S.op("dve", lambda: A_.tensor_scalar(out=tmp[2], in0=tmp[2], scalar1=1e-12, scalar2=None, op0=ALU.max), [Btmp[2]], [Btmp[2]])
            S.op("dve", lambda: A_.reciprocal(out=tmp[2], in_=tmp[2]), [Btmp[2]], [Btmp[2]])
            S.op("dve", lambda: A_.tensor_tensor(out=sg_[:, 5, :], in0=tmp[1], in1=tmp[2], op=ALU.mult), [Btmp[1], Btmp[2]], [Bs])
            pb = bank()
            S.op("pe", lambda: nc.tensor.matmul(self.PS[pb][:, 0:BLK], lhsT=G2a[:, cs], rhs=gh, start=True, stop=False), [Bsw, Blt], [self.BPS[pb]])
            S.op("pe", lambda: nc.tensor.matmul(self.PS[pb][:, 0:BLK], lhsT=G2b[:, cs], rhs=lt[0:32, 4, :], start=False, stop=True), [Bsw, Blt], [self.BPS[pb]])
            S.op("act", lambda: nc.scalar.copy(out=sg_[:, 9, :], in_=self.PS[pb][:, 0:BLK]), [self.BPS[pb]], [Bs])
            for d in range(2):
                pb = bank()
                S.op("pe", lambda: nc.tensor.matmul(self.PS[pb][:, 0:BLK], lhsT=W2[:, d, cs], rhs=lt[:, d, :], start=True, stop=True), [Bsw, Blt], [self.BPS[pb]])
                S.op("act", lambda: nc.scalar.activation(out=tmp[3], in_=self.PS[pb][:, 0:BLK], func=AF.Sigmoid, bias=self.pv(f"rw_w0_{d}", c)), [self.BPS[pb]], [Btmp[3]])
                S.op("dve", lambda: A_.tensor_scalar(out=sg_[:, 6 + d, :], in0=tmp[3], scalar1=-float(np.exp(-0.5)), scalar2=None, op0=ALU.mult), [Btmp[3]], [Bs])
                pb = bank()
                S.op("pe", lambda: nc.tensor.matmul(self.PS[pb][:, 0:BLK], lhsT=A2[:, d, cs], rhs=lt[:, 2 + d, :], start=True, stop=True), [Bsw, Blt], [self.BPS[pb]])
                S.op("act", lambda: nc.scalar.activation(out=tmp[4], in_=self.PS[pb][:, 0:BLK], func=AF.Sigmoid, bias=self.pv(f"rw_a0_{d}", c)), [self.BPS[pb]], [Btmp[4]])
                S.op("dve", lambda: A_.tensor_tensor(out=sg_[:, 3 + d, :], in0=tmp[4], in1=sg_[:, 5, :], op=ALU.mult), [Btmp[4], Bs], [Bs])
                S.op("dve", lambda: A_.tensor_scalar(out=tmp[5], in0=tmp[4], scalar1=-1.0, scalar2=None, op0=ALU.add), [Btmp[4]], [Btmp[5]])
                S.op("dve", lambda: A_.tensor_scalar(out=tmp[5], in0=tmp[5], scalar1=self.pv("rw_k_a", c), scalar2=1.0, op0=ALU.mult, op1=ALU.add), [Btmp[5]], [Btmp[5]])
                S.op("dve", lambda: A_.tensor_tensor(out=sg_[:, 1 + d, :], in0=tmp[5], in1=kraw, op=ALU.mult), [Btmp[5], Btmp[0]], [Bs])
            S.dma("pool", RWP[:, c * 128:(c + 1) * 128, col:col + BLK].rearrange("a p t -> p a t"), sg_, reads=[Bs])
        for tt in range(BLK // 128):
            vi = vi_ % 2
            vi_ += 1
            for hfv in range(2):
                pb = 4 - hfv
                for kc in range(8):
                    S.op("pe", lambda: nc.tensor.matmul(self.PS[pb][:, 0:512], lhsT=xj[3][:, kc, tt * 128:(tt + 1) * 128], rhs=Wrkv[:, kc, 2 * D + hfv * 512:2 * D + (hfv + 1) * 512], start=(kc == 0), stop=(kc == 7)),
                         [BWrkv[4 + hfv], Bxj[3]], [self.BPS[pb]])
                S.op("act", lambda: nc.scalar.copy(out=vts[vi][:, hfv * 512:(hfv + 1) * 512], in_=self.PS[pb][:, 0:512]), [self.BPS[pb]], [Bvts[vi]])
            S.dma("pool", Vtm[col + tt * 128:col + (tt + 1) * 128, :], vts[vi], reads=[Bvts[vi]])
    st.close()


def _rwkv_mixer(self, l, xin, og):
    RWP = self.scr("rwP", [10, D, TT])
    Vtm = self.scr("rwV", [TT, D], BF16)
    self.rwkv_proj(l, xin, RWP, Vtm)
    if getattr(self, "rw_stop", 0) == 1:
        return
    self.rwkv_scan(RWP, Vtm, og)


Prog.rwkv_proj = _rwkv_proj
Prog.rwkv_mixer = _rwkv_mixer


def _rwkv_scan(self, RWP, Vtm, og):
    nc, S = self.nc, self.S
    A_ = nc.vector
    U32 = mybir.dt.uint32
    RWD = self.scr("rwD", [NB, 8, 2, 2, 128, NCH * 128], BF16)
    RWS = self.scr("rwS", [NB, 8, 2, 128, 3 * NCH])
    skipA = getattr(self, "rw_skipA", False)
    st = Stage(self, "r2a")
    smask = st.sb("smask", [128, T])
    Bsm = Buf()
    S.dma("sp", smask, self.cd["scanmask"], writes=[Bsm])
    lw = st.sb("lw", [128, T]); kap = st.sb("kap", [128, T]); rr = st.sb("r", [128, T]); kt = st.sb("kt", [128, T]); be = st.sb("be", [128, T])
    cw = st.sb("cw", [128, T]); cm = st.sb("cm", [128, T]); en = st.sb("en", [128, T]); ex = st.sb("ex", [128, T])
    ABt = [st.sb(f"AB{i}", [128, NCH, 2, CH], BF16) for i in range(2)]
    KBt_ = [st.sb(f"KB{i}", [128, NCH, 2, CH], BF16) for i in range(2)]
    SC = [st.sb(f"SC{i}", [128, 3, NCH]) for i in range(2)]
    Blw, Bkap, Br, Bkt, Bbe, Bcw, Bcm, Ben, Bex = [Buf() for _ in range(9)]
    BAB, BKB, BSC = [[Buf(), Buf()] for _ in range(3)]
    it = 0
    v3 = lambda t_: t_.rearrange("p (c s) -> p c s", s=CH)
    for b in range(0 if skipA else NB):
        cols = slice(b * T, (b + 1) * T)
        for p in range(8):
            rows = slice(p * 128, (p + 1) * 128)
            for d in range(2):
                i = it % 2
                it += 1
                S.dma("sp", lw, RWP[6 + d, rows, cols], writes=[Blw])
                S.dma("sp", kap, RWP[5, rows, cols], writes=[Bkap])
                S.dma("sp", rr, RWP[0, rows, cols], writes=[Br])
                S.dma("sp", kt, RWP[1 + d, rows, cols], writes=[Bkt])
                S.dma("sp", be, RWP[3 + d, rows, cols], writes=[Bbe])
                S.op("dve", lambda: A_.tensor_tensor_scan(out=cw, data0=smask, data1=lw, initial=0.0, op0=ALU.mult, op1=ALU.add), [Bsm, Blw], [Bcw])
                if d == 1:
                    S.op("dve", lambda: A_.tensor_tensor(out=cm, in0=lw, in1=cw, op=ALU.subtract), [Blw, Bcw], [Bcm])
                    S.op("dve", lambda: A_.tensor_tensor(out=v3(en), in0=v3(cm), in1=v3(cw)[:, :, CH - 1:CH].to_broadcast([128, NCH, CH]), op=ALU.add), [Bcm, Bcw], [Ben])
                    S.op("dve", lambda: A_.tensor_copy(out=cw, in_=en), [Ben], [Bcw])
                m_idx = 32 if d == 0 else 31
                e_idx = CH - 1 if d == 0 else 0
                c3 = v3(cw)
                S.op("act", lambda: nc.scalar.activation(out=SC[i][:, 0, :], in_=c3[:, :, m_idx], func=AF.Exp), [Bcw], [BSC[i]])
                S.op("act", lambda: nc.scalar.activation(out=SC[i][:, 1, :], in_=c3[:, :, e_idx], func=AF.Exp), [Bcw], [BSC[i]])
                S.op("dve", lambda: A_.tensor_tensor(out=SC[i][:, 2, :], in0=c3[:, :, e_idx], in1=c3[:, :, m_idx], op=ALU.subtract), [Bcw], [BSC[i]])
                S.op("act", lambda: nc.scalar.activation(out=SC[i][:, 2, :], in_=SC[i][:, 2, :], func=AF.Exp), [BSC[i]], [BSC[i]])
                S.dma("pool", RWS[b, p, d], SC[i].rearrange("p a c -> p (a c)"), reads=[BSC[i]])
                S.op("dve", lambda: A_.tensor_tensor(out=v3(cm), in0=c3, in1=c3[:, :, m_idx:m_idx + 1].to_broadcast([128, NCH, CH]), op=ALU.subtract), [Bcw], [Bcm])
                S.op("act", lambda: nc.scalar.activation(out=en, in_=cm, func=AF.Exp, scale=-1.0), [Bcm], [Ben])
                S.op("dve", lambda: A_.tensor_tensor(out=ex, in0=cm, in1=lw, op=ALU.subtract), [Bcm, Blw], [Bex])
                S.op("act", lambda: nc.scalar.activation(out=ex, in_=ex, func=AF.Exp), [Bex], [Bex])
                S.op("act", lambda: nc.scalar.activation(out=cm, in_=cm, func=AF.Exp), [Bcm], [Bcm])
                S.op("dve", lambda: A_.tensor_tensor(out=ABt[i][:, :, 0, :], in0=v3(kap), in1=v3(ex), op=ALU.mult), [Bkap, Bex], [BAB[i]])
                S.op("dve", lambda: A_.tensor_tensor(out=ABt[i][:, :, 1, :], in0=v3(rr), in1=v3(cm), op=ALU.mult), [Br, Bcm], [BAB[i]])
                S.op("dve", lambda: A_.tensor_tensor(out=KBt_[i][:, :, 0, :], in0=v3(kt), in1=v3(en), op=ALU.mult), [Bkt, Ben], [BKB[i]])
                S.op("dve", lambda: A_.tensor_tensor(out=KBt_[i][:, :, 1, :], in0=v3(be), in1=v3(en), op=ALU.mult), [Bbe, Ben], [BKB[i]])
                S.dma("pool", RWD[b, p, d, 0], ABt[i].rearrange("p c a s -> p (c a s)"), reads=[BAB[i]])
                S.dma("pool", RWD[b, p, d, 1], KBt_[i].rearrange("p c a s -> p (c a s)"), reads=[BKB[i]])
    st.close()
    if getattr(self, "rw_stop", 0) == 2:
        return
    st = Stage(self, "r2b")
    S.pe_selfwait = getattr(self, "rw_selfwait", False)
    S.pe_drain = getattr(self, "rw_drain", 2)
    epsLN = st.sb("epsLN", [128, 1])
    Bgl = Buf()
    S.op("dve", lambda: A_.memset(epsLN, RW_LN_EPS), [], [Bgl])
    AB = [st.sb(f"AB{d}", [128, NCH, 128], BF16) for d in range(2)]
    KB = [st.sb(f"KB{d}", [128, NCH, 128], BF16) for d in range(2)]
    SCs = [st.sb(f"SC{d}", [128, 3, NCH]) for d in range(2)]
    Vst = st.sb("Vst", [64, NCH, 128], BF16)
    BABl, BKBl, BSCl = [[Buf(), Buf()] for _ in range(3)]
    BVst = Buf()
    chains = [(hd, d) for hd in range(2) for d in range(2)]
    IDT = BF16 if getattr(self, "rw_inv_bf16", True) else F32
    VU, GGb, AN0, ANp, Xp, Wf, KBtr = {}, {}, {}, {}, {}, {}, {}
    BVU, BGG, BAN0, BANp, BXp, BWf, BKBtr, BST, BS0, BtS, By = [dict() for _ in range(11)]
    for ch in chains:
        nm = f"{ch[0]}{ch[1]}"
        VU[ch] = st.sb("VU" + nm, [128, NCH, CH], BF16)
        GGb[ch] = st.sb("GG" + nm, [128, 128], BF16)
        AN0[ch] = st.sb("AN0" + nm, [128, 128], IDT)
        ANp[ch] = [st.sb(f"ANp{q}" + nm, [128, 128], IDT) for q in range(2)]
        Xp[ch] = [st.sb(f"X{q}" + nm, [128, CH], IDT) for q in range(2)]
        Wf[ch] = st.sb("Wf" + nm, [128, CH], IDT)
        KBtr[ch] = st.sb("KBt" + nm, [128, CH], BF16)
        BVU[ch], BGG[ch], BAN0[ch], BWf[ch], BKBtr[ch], BST[ch], BS0[ch], BtS[ch], By[ch] = [Buf() for _ in range(9)]
        BANp[ch] = [Buf(), Buf()]
        BXp[ch] = [Buf(), Buf()]
        S.op("dve", lambda: A_.memset(GGb[ch], 0.0), [], [BGG[ch]])
        S.op("dve", lambda: A_.memset(AN0[ch], 0.0), [], [BAN0[ch]])
    ST = [st.sb(f"ST{d}", [128, CH]) for d in range(2)]
    S0m = [st.sb(f"S0m{d}", [128, CH], BF16) for d in range(2)]
    tS = [st.sb(f"tS{d}", [128, CH]) for d in range(2)]
    yacc = [st.sb(f"yacc{d}", [128, T]) for d in range(2)]
    rl = st.sb("rl", [128, T]); k0 = st.sb("k0", [128, T]); k1 = st.sb("k1", [128, T]); vf = st.sb("vf", [128, T]); gg = st.sb("gg", [128, T])
    t0_ = st.sb("t0", [128, T]); t1_ = st.sb("t1", [128, T])
    ogb = st.sb("ogb", [128, T], BF16)
    Brl, Bk0, Bk1, Bvf, Bgg, Bt0, Bt1, Bogb = [Buf() for _ in range(8)]
    MERGE = getattr(self, "rw_merge", True)
    if MERGE:
        mKB = [st.sb(f"mKBt{d}", [128, 2, CH], BF16) for d in range(2)]
        mGG = [st.sb(f"mGG{d}", [128, 2, 128], BF16) for d in range(2)]
        mAN0 = [st.sb(f"mAN0{d}", [128, 2, 128], IDT) for d in range(2)]
        mANp = [[st.sb(f"mANp{q}{d}", [128, 2, 128], IDT) for q in range(2)] for d in range(2)]
        mXp = [[st.sb(f"mX{q}{d}", [128, 2, CH], IDT) for q in range(2)] for d in range(2)]
        mWf = [st.sb(f"mWf{d}", [128, 2, CH], IDT) for d in range(2)]
        mVU = [st.sb(f"mVU{d}", [128, NCH, 2, CH], BF16) for d in range(2)]
        M4x2 = [st.sb(f"M4x2{d}", [128, 2, 128]) for d in range(2)]
        mAx2 = [st.sb(f"mAx2{d}", [128, 2, CH]) for d in range(2)]
        mNx2 = [st.sb(f"mNx2{d}", [128, 2, CH]) for d in range(2)]
        I2 = st.sb("I2", [128, 2, CH])
        Bmk = Buf()
        mBKB, mBGG, mBAN0, mBWf, mBVU, mBST, mBS0, mBtS, mBy = [[Buf(), Buf()] for _ in range(9)]
        mBANp = [[Buf(), Buf()], [Buf(), Buf()]]
        mBXp = [[Buf(), Buf()], [Buf(), Buf()]]
        mUB = [[Buf() for _ in range(4)] for d in range(2)]
        for d in range(2):
            S.op("dve", lambda: A_.memset(mGG[d], 0.0), [], [mBGG[d]])
            S.op("dve", lambda: A_.memset(mAN0[d], 0.0), [], [mBAN0[d]])
            for hd in range(2):
                S.op("dve", lambda: A_.tensor_copy(out=M4x2[d][:, hd, :], in_=(self.masks[:, 0:128] if d == 0 else self.masks[:, 128:256])), [], [Bmk])
                S.op("dve", lambda: A_.tensor_copy(out=mAx2[d][:, hd, :], in_=(self.masks[:, 0:64] if d == 0 else self.masks[:, 128:192])), [], [Bmk])
                S.op("dve", lambda: A_.tensor_copy(out=mNx2[d][:, hd, :], in_=(self.masks[:, 128:192] if d == 0 else self.masks[:, 0:64])), [], [Bmk])
        for hd in range(2):
            S.op("dve", lambda: A_.tensor_copy(out=I2[64:128, hd, :], in_=self.ident[64:128, 64:128]), [], [Bmk])
    R = {}
    BR = {}
    for ci, ch in enumerate(chains):
        b0, b1 = self.PS[2 * ci], self.PS[2 * ci + 1]
        R[ch] = dict(GA=b0[:, 0:128], LV=b0[:, 192:320], Wp=b0[:, 384:448],
                     XL=b1[:, 320:384], Up=b1[:, 448:512], Nn=b1[:, 128:192],
                     Yp=b1[:, 0:64], Sd=b1[:, 64:128], TR=b1.bitcast(BF16)[:, 512:576])
        u0, u1, u2, u3 = Buf(), Buf(), Buf(), Buf()
        ykp = [u2] if ch[0] == 0 else [u3]
        BR[ch] = dict(GAlo=[u0], GAup=[u1], GA=[u0, u1], LV=[u1], Wp=[u1], XL=[u3], Up=[u3], Nn=[u3], Yp=ykp, Sd=ykp, TR=[u2, u3], ALL=[u0, u1, u2, u3])
    up, lo = slice(64, 128), slice(0, 64)
    mU = lambda ap: ap.bitcast(U32)
    cf = list(range(NCH))
    cbk = list(range(TC // CH - 1, -1, -1)) + list(range(NCH - 1, TC // CH - 1, -1))
    order = [cf, cbk]
    dbgn = getattr(self, "rw_dbg", None)
    for b in range(NB):
        cols = slice(b * T, (b + 1) * T)
        for p in range(8):
            if dbgn is not None and (b * 8 + p) >= dbgn[0]:
                continue
            rows = slice(p * 128, (p + 1) * 128)
            for d in range(2):
                S.dma("sp", AB[d], RWD[b, p, d, 0].rearrange("k (c x) -> k c x", x=128), writes=[BABl[d]])
                S.dma("sp", KB[d], RWD[b, p, d, 1].rearrange("k (c x) -> k c x", x=128), writes=[BKBl[d]])
                S.dma("sp", SCs[d], RWS[b, p, d].rearrange("k (a c) -> k a c", a=3), writes=[BSCl[d]])
            S.dma("sp", Vst, Vtm[cols, rows].rearrange("(c s) v -> s c v", s=CH), writes=[BVst])
            S.dma("sp", rl, RWP[0, rows, cols], writes=[Brl])
            S.dma("sp", k0, RWP[1, rows, cols], writes=[Bk0])
            S.dma("sp", k1, RWP[2, rows, cols], writes=[Bk1])
            S.dma("sp", vf, RWP[8, rows, cols], writes=[Bvf])
            S.dma("sp", gg, RWP[9, rows, cols], writes=[Bgg])
            if MERGE:
                for d in range(2):
                    S.op("pool", lambda: nc.gpsimd.tensor_copy(out=mVU[d][lo, :, :, :], in_=Vst.rearrange("s c (h v) -> s c h v", h=2)), [BVst], [mBVU[d]])
                    S.op("dve", lambda: A_.memset(ST[d], 0.0), [], [mBST[d]])
                    S.op("dve", lambda: A_.memset(S0m[d], 0.0), [], [mBS0[d]])

                def dstep(d, step):
                    c = order[d][step]
                    cs = slice(c * CH, (c + 1) * CH)
                    bA, bB, bC, bD = [self.PS[4 * d + q] for q in range(4)]
                    uA, uB, uC, uD = mUB[d]
                    h2 = lambda ap: ap.rearrange("p (h x) -> p h x", h=2)
                    GA = h2(bA[:, 0:256]); LV = h2(bB[:, 0:256]); Wp = h2(bB[:, 256:384])
                    XL = h2(bC[:, 0:128]); Up_ = h2(bC[:, 128:256]); Nn = h2(bC[:, 256:384])
                    Yp = bD[:, 0:64]; Sd = bD[:, 64:128]; TR = h2(bD.bitcast(BF16)[:, 512:640])
                    KP = [slice(0, 64), slice(64, 128)]
                    for hd in range(2):
                        kp = KP[hd]
                        S.op("pe", lambda: nc.tensor.transpose(out=TR[:, hd, :], in_=KB[d][kp, c, :], identity=self.identb[kp, kp]), [BKBl[d]], [uD], pemode=("g", hd))
                        S.op("pe", lambda: nc.tensor.matmul(GA[lo, hd, :], lhsT=KB[d][kp, c, 0:64], rhs=AB[d][kp, c, :], start=True, stop=True), [BKBl[d], BABl[d]], [uA], pemode=("g", hd))
                        S.op("pe", lambda: nc.tensor.matmul(GA[up, hd, :], lhsT=KB[d][kp, c, 64:128], rhs=AB[d][kp, c, :], start=True, stop=True), [BKBl[d], BABl[d]], [uA], pemode=("g", hd))
                        S.op("pe", lambda: nc.tensor.matmul(Nn[up, hd, :], lhsT=AB[d][kp, c, 0:64], rhs=KB[d][kp, c, 64:128], start=True, stop=True), [BKBl[d], BABl[d]], [uC], pemode=("g", hd))
                    yield
                    S.op("act", lambda: nc.scalar.copy(out=mKB[d], in_=TR), [uD], [mBKB[d]])
                    S.op("dve", lambda: A_.copy_predicated(out=mGG[d], mask=mU(M4x2[d][:]), data=GA), [uA, Bmk], [mBGG[d]])
                    S.op("dve", lambda: A_.copy_predicated(out=mAN0[d][up, :, 0:64], mask=mU(mAx2[d][up, :, :]), data=GA[up, :, 0:64]), [uA, Bmk], [mBAN0[d]])
                    S.op("dve", lambda: A_.copy_predicated(out=mAN0[d][up, :, 64:128], mask=mU(mNx2[d][up, :, :]), data=Nn[up, :, :]), [uC, Bmk], [mBAN0[d]])
                    S.op("dve", lambda: A_.tensor_tensor(out=mXp[d][0][up, :, :], in0=I2[up, :, :], in1=mAN0[d][up, :, 0:64], op=ALU.subtract), [mBAN0[d], Bmk], [mBXp[d][0]])
                    yield
                    cur, Bcur = mAN0[d], mBAN0[d]
                    xq = 0
                    for lv in range(1, 7):
                        nx, Bnx = mANp[d][lv % 2], mBANp[d][lv % 2]
                        for hd in range(2):
                            if lv <= 5:
                                if lv < 5:
                                    S.op("pe", lambda: nc.tensor.matmul(LV[up, hd, 0:64], lhsT=cur[up, hd, 64:128], rhs=cur[up, hd, 0:64], start=True, stop=True), [Bcur], [uB], pemode=("g", 1))
                                S.op("pe", lambda: nc.tensor.matmul(LV[up, hd, 64:128], lhsT=cur[up, hd, 0:64], rhs=cur[up, hd, 64:128], start=True, stop=True), [Bcur], [uB], pemode=("g", 1))
                            if lv >= 2:
                                S.op("pe", lambda: nc.tensor.matmul(XL[up, hd, :], lhsT=cur[up, hd, 64:128], rhs=mXp[d][xq][up, hd, :], start=True, stop=True), [Bcur, mBXp[d][xq]], [uC], pemode=("g", 1))
                        yield
                        if lv <= 5:
                            if lv < 5:
                                S.op("act", lambda: nc.scalar.copy(out=nx[up, :, :], in_=LV[up, :, :]), [uB], [Bnx])
                            else:
                                S.op("act", lambda: nc.scalar.copy(out=nx[up, :, 64:128], in_=LV[up, :, 64:128]), [uB], [Bnx])
                        if lv >= 2:
                            S.op("dve", lambda: A_.tensor_tensor(out=mXp[d][1 - xq][up, :, :], in0=XL[up, :, :], in1=mXp[d][xq][up, :, :], op=ALU.add), [uC, mBXp[d][xq]], [mBXp[d][1 - xq]])
                            xq = 1 - xq
                        if lv <= 5:
                            cur, Bcur = nx, Bnx
                        yield
                    for hd in (1, 0):
                        kp = KP[hd]
                        S.op("pe", lambda: nc.tensor.matmul(Wp[up, hd, :], lhsT=AB[d][kp, c, 0:64], rhs=S0m[d][kp, :], start=True, stop=False), [BABl[d], mBS0[d]], [uB], pemode=("g", hd))
                        S.op("pe", lambda: nc.tensor.matmul(Wp[up, hd, :], lhsT=mGG[d][lo, hd, 0:64], rhs=mVU[d][lo, c, hd, :], start=False, stop=True), [mBGG[d], mBVU[d]], [uB], pemode=("g", 0))
                    yield
                    S.op("act", lambda: nc.scalar.copy(out=mWf[d][up, :, :], in_=Wp[up, :, :]), [uB], [mBWf[d]])
                    yield
                    for hd in range(2):
                        S.op("pe", lambda: nc.tensor.matmul(Up_[up, hd, :], lhsT=mXp[d][xq][up, hd, :], rhs=mWf[d][up, hd, :], start=True, stop=True), [mBXp[d][xq], mBWf[d]], [uC], pemode=("g", 1))
                    yield
                    S.op("act", lambda: nc.scalar.activation(out=mVU[d][up, c, :, :], in_=Up_[up, :, :], func=AF.Copy, scale=-1.0), [uC], [mBVU[d]])
                    yield
                    for hd in (1, 0):
                        kp = KP[hd]
                        S.op("pe", lambda: nc.tensor.matmul(Yp[kp, :], lhsT=S0m[d][kp, :], rhs=AB[d][kp, c, 64:128], start=True, stop=False), [mBS0[d], BABl[d]], [uD], pemode=("g", hd))
                        S.op("pe", lambda: nc.tensor.matmul(Yp[kp, :], lhsT=mVU[d][:, c, hd, :], rhs=mGG[d][:, hd, 64:128], start=False, stop=True), [mBVU[d], mBGG[d]], [uD], pemode=("full",))
                    for hd in range(2):
                        kp = KP[hd]
                        S.op("pe", lambda: nc.tensor.matmul(Sd[kp, :], lhsT=mKB[d][:, hd, :], rhs=mVU[d][:, c, hd, :], start=True, stop=True), [mBKB[d], mBVU[d]], [uD], pemode=("full",))
                    yield
                    S.op("act", lambda: nc.scalar.copy(out=yacc[d][:, cs], in_=Yp), [uD], [mBy[d]])
                    S.op("act", lambda: nc.scalar.activation(out=tS[d], in_=Sd, func=AF.Identity, scale=SCs[d][:, 2, c:c + 1]), [uD, BSCl[d]], [mBtS[d]])
                    S.op("dve", lambda: A_.scalar_tensor_tensor(out=ST[d], in0=ST[d], scalar=SCs[d][:, 1, c:c + 1], in1=tS[d], op0=ALU.mult, op1=ALU.add), [mBST[d], mBtS[d], BSCl[d]], [mBST[d]])
                    if step + 1 < NCH:
                        cn = order[d][step + 1]
                        S.op("dve", lambda: A_.tensor_scalar(out=S0m[d], in0=ST[d], scalar1=SCs[d][:, 0, cn:cn + 1], scalar2=None, op0=ALU.mult), [mBST[d], BSCl[d]], [mBS0[d]])

                for step in range(NCH if dbgn is None else dbgn[1]):
                    gens = [dstep(d, step) for d in range(2)]
                    while gens:
                        for g_ in list(gens):
                            try:
                                next(g_)
                            except StopIteration:
                                gens.remove(g_)
            else:
                for ch in chains:
                    hd, d = ch
                    kp = slice(hd * 64, hd * 64 + 64)
                    S.op("pool", lambda: nc.gpsimd.tensor_copy(out=VU[ch][lo, :, :], in_=Vst[:, :, hd * 64:(hd + 1) * 64]), [BVst], [BVU[ch]])
                    S.op("dve", lambda: A_.memset(ST[d][kp, :], 0.0), [], [BST[ch]])
                    S.op("dve", lambda: A_.memset(S0m[d][kp, :], 0.0), [], [BS0[ch]])
                def chain_step(ch, step):
                    hd, d = ch
                    kp = slice(hd * 64, hd * 64 + 64)
                    c = order[d][step]
                    cs = slice(c * CH, (c + 1) * CH)
                    r_, br_ = R[ch], BR[ch]
                    M4 = self.masks[:, 0:128] if d == 0 else self.masks[:, 128:256]
                    mA = self.masks[up, 0:64] if d == 0 else self.masks[up, 128:192]
                    mN = self.masks[up, 128:192] if d == 0 else self.masks[up, 0:64]
                    S.op("pe", lambda: nc.tensor.transpose(out=r_["TR"], in_=KB[d][kp, c, :], identity=self.identb[kp, kp]), [BKBl[d]], br_["TR"], pemode=("T", hd))
                    S.op("act", lambda: nc.scalar.copy(out=KBtr[ch], in_=r_["TR"]), br_["TR"], [BKBtr[ch]])
                    S.op("pe", lambda: nc.tensor.matmul(r_["GA"][lo, :], lhsT=KB[d][kp, c, 0:64], rhs=AB[d][kp, c, :], start=True, stop=True), [BKBl[d], BABl[d]], br_["GAlo"], pemode=("g", hd))
                    S.op("pe", lambda: nc.tensor.matmul(r_["GA"][up, :], lhsT=KB[d][kp, c, 64:128], rhs=AB[d][kp, c, :], start=True, stop=True), [BKBl[d], BABl[d]], br_["GAup"], pemode=("g", hd))
                    S.op("pe", lambda: nc.tensor.matmul(r_["Nn"][up, :], lhsT=AB[d][kp, c, 0:64], rhs=KB[d][kp, c, 64:128], start=True, stop=True), [BKBl[d], BABl[d]], br_["Nn"], pemode=("g", hd))
                    yield
                    S.op("dve", lambda: A_.copy_predicated(out=GGb[ch], mask=mU(M4), data=r_["GA"]), br_["GA"], [BGG[ch]])
                    S.op("dve", lambda: A_.copy_predicated(out=AN0[ch][up, 0:64], mask=mU(mA), data=r_["GA"][up, 0:64]), br_["GAup"], [BAN0[ch]])
                    S.op("dve", lambda: A_.copy_predicated(out=AN0[ch][up, 64:128], mask=mU(mN), data=r_["Nn"][up, :]), br_["Nn"], [BAN0[ch]])
                    S.op("dve", lambda: A_.tensor_tensor(out=Xp[ch][0][up, :], in0=self.ident[up, up], in1=AN0[ch][up, 0:64], op=ALU.subtract), [BAN0[ch]], [BXp[ch][0]])
                    yield
                    cur, Bcur = AN0[ch], BAN0[ch]
                    xq = 0
                    for lv in range(1, 7):
                        nx, Bnx = ANp[ch][lv % 2], BANp[ch][lv % 2]
                        if lv <= 5:
                            if lv < 5:
                                S.op("pe", lambda: nc.tensor.matmul(r_["LV"][up, 0:64], lhsT=cur[up, 64:128], rhs=cur[up, 0:64], start=True, stop=True), [Bcur], br_["LV"], pemode=("f",))
                            S.op("pe", lambda: nc.tensor.matmul(r_["LV"][up, 64:128], lhsT=cur[up, 0:64], rhs=cur[up, 64:128], start=True, stop=True), [Bcur], br_["LV"], pemode=("f",))
                        if lv >= 2:
                            S.op("pe", lambda: nc.tensor.matmul(r_["XL"][up, :], lhsT=cur[up, 64:128], rhs=Xp[ch][xq][up, :], start=True, stop=True), [Bcur, BXp[ch][xq]], br_["XL"], pemode=("f",))
                        yield
                        if lv <= 5:
                            if lv < 5:
                                S.op("act", lambda: nc.scalar.copy(out=nx[up, :], in_=r_["LV"][up, :]), br_["LV"], [Bnx])
                            else:
                                S.op("act", lambda: nc.scalar.copy(out=nx[up, 64:128], in_=r_["LV"][up, 64:128]), br_["LV"], [Bnx])
                        if lv >= 2:
                            S.op("dve", lambda: A_.tensor_tensor(out=Xp[ch][1 - xq][up, :], in0=r_["XL"][up, :], in1=Xp[ch][xq][up, :], op=ALU.add), br_["XL"] + [BXp[ch][xq]], [BXp[ch][1 - xq]])
                            xq = 1 - xq
                        if lv <= 5:
                            cur, Bcur = nx, Bnx
                        if lv < 6:
                            yield
                    yield
                    S.op("pe", lambda: nc.tensor.matmul(r_["Wp"][up, :], lhsT=AB[d][kp, c, 0:64], rhs=S0m[d][kp, :], start=True, stop=False), [BABl[d], BS0[ch]], br_["Wp"], pemode=("g", hd))
                    S.op("pe", lambda: nc.tensor.matmul(r_["Wp"][up, :], lhsT=GGb[ch][lo, 0:64], rhs=VU[ch][lo, c, :], start=False, stop=True), [BGG[ch], BVU[ch]], br_["Wp"], pemode=("w2",))
                    yield
                    S.op("act", lambda: nc.scalar.copy(out=Wf[ch][up, :], in_=r_["Wp"][up, :]), br_["Wp"], [BWf[ch]])
                    yield
                    S.op("pe", lambda: nc.tensor.matmul(r_["Up"][up, :], lhsT=Xp[ch][xq][up, :], rhs=Wf[ch][up, :], start=True, stop=True), [BXp[ch][xq], BWf[ch]], br_["Up"], pemode=("f",))
                    yield
                    S.op("act", lambda: nc.scalar.activation(out=VU[ch][up, c, :], in_=r_["Up"][up, :], func=AF.Copy, scale=-1.0), br_["Up"], [BVU[ch]])
                    yield
                    S.op("pe", lambda: nc.tensor.matmul(r_["Yp"][kp, :], lhsT=S0m[d][kp, :], rhs=AB[d][kp, c, 64:128], start=True, stop=False), [BS0[ch], BABl[d]], br_["Yp"], pemode=("g", hd))
                    S.op("pe", lambda: nc.tensor.matmul(r_["Yp"][kp, :], lhsT=VU[ch][:, c, :], rhs=GGb[ch][:, 64:128], start=False, stop=True), [BVU[ch], BGG[ch]], br_["Yp"], pemode=("full",))
                    yield
                    S.op("act", lambda: nc.scalar.copy(out=yacc[d][kp, cs], in_=r_["Yp"][kp, :]), br_["Yp"], [By[ch]])
                    S.op("pe", lambda: nc.tensor.matmul(r_["Sd"][kp, :], lhsT=KBtr[ch], rhs=VU[ch][:, c, :], start=True, stop=True), [BKBtr[ch], BVU[ch]], br_["Sd"], pemode=("full",))
                    yield
                    S.op("act", lambda: nc.scalar.activation(out=tS[d][kp, :], in_=r_["Sd"][kp, :], func=AF.Identity, scale=SCs[d][kp, 2, c:c + 1]), br_["Sd"] + [BSCl[d]], [BtS[ch]])
                    S.op("dve", lambda: A_.scalar_tensor_tensor(out=ST[d][kp, :], in0=ST[d][kp, :], scalar=SCs[d][kp, 1, c:c + 1], in1=tS[d][kp, :], op0=ALU.mult, op1=ALU.add), [BST[ch], BtS[ch], BSCl[d]], [BST[ch]])
                    if step + 1 < NCH:
                        cn = order[d][step + 1]
                        S.op("dve", lambda: A_.tensor_scalar(out=S0m[d][kp, :], in0=ST[d][kp, :], scalar1=SCs[d][kp, 0, cn:cn + 1], scalar2=None, op0=ALU.mult), [BST[ch], BSCl[d]], [BS0[ch]])

                for step in range(NCH if dbgn is None else dbgn[1]):
                    gens = [chain_step(ch, step) for ch in chains]
                    if getattr(self, "rw_order", "phase") == "chain":
                        for g_ in gens:
                            for _ in g_:
                                pass
                        gens = []
                    while gens:
                        for g_ in list(gens):
                            try:
                                next(g_)
                            except StopIteration:
                                gens.remove(g_)
            if MERGE:
                RB = {0: [mUB[0][0]], 1: [mUB[0][1]]}
                By_all = [mBy[0], mBy[1]]
            else:
                RB = {0: [BR[chains[0]]["ALL"][0], BR[chains[0]]["ALL"][1]], 1: [BR[chains[0]]["ALL"][2], BR[chains[0]]["ALL"][3]]}
                By_all = [By[ch] for ch in chains]
            Byy = Buf()
            S.op("dve", lambda: A_.tensor_tensor(out=yacc[0], in0=yacc[0], in1=yacc[1], op=ALU.add), By_all, [Byy])
            NP_ = 6
            W_ = T // NP_
            for pc in range(NP_):
                sl_ = slice(pc * W_, (pc + 1) * W_)
                pb = pc % 2
                S.op("pe", lambda: nc.tensor.matmul(self.PS[pb][:, 0:W_], lhsT=self.blk64, rhs=yacc[0][:, sl_], start=True, stop=True), [Byy], RB[pb])
                S.op("dve", lambda: A_.scalar_tensor_tensor(out=t0_[:, sl_], in0=self.PS[pb][:, 0:W_], scalar=-1.0 / 64, in1=yacc[0][:, sl_], op0=ALU.mult, op1=ALU.add), RB[pb] + [Byy], [Bt0])
            S.op("act", lambda: nc.scalar.activation(out=t1_, in_=t0_, func=AF.Square), [Bt0], [Bt1])
            for pc in range(NP_):
                sl_ = slice(pc * W_, (pc + 1) * W_)
                pb = pc % 2
                S.op("pe", lambda: nc.tensor.matmul(self.PS[pb][:, 0:W_], lhsT=self.blk64, rhs=t1_[:, sl_], start=True, stop=True), [Bt1], RB[pb])
                S.op("act", lambda: nc.scalar.activation(out=yacc[1][:, sl_], in_=self.PS[pb][:, 0:W_], func=AF.Sqrt, scale=1.0 / 64, bias=epsLN), RB[pb] + [Bgl], [Byy])
            S.op("dve", lambda: A_.reciprocal(out=yacc[1], in_=yacc[1]), [Byy], [Byy])
            S.op("dve", lambda: A_.tensor_tensor(out=t0_, in0=t0_, in1=yacc[1], op=ALU.mult), [Bt0, Byy], [Bt0])
            S.op("act", lambda: nc.scalar.activation(out=t0_, in_=t0_, func=AF.Identity, scale=self.pv("rw_ln_w", p), bias=self.pv("rw_ln_b", p)), [Bt0], [Bt0])
            S.op("dve", lambda: A_.tensor_tensor(out=k0, in0=k0, in1=k1, op=ALU.add), [Bk0, Bk1], [Bk0])
            S.op("dve", lambda: A_.scalar_tensor_tensor(out=t1_, in0=rl, scalar=self.pv("rw_r_k", p), in1=k0, op0=ALU.mult, op1=ALU.mult), [Brl, Bk0, Bt1], [Bt1])
            for pc in range(NP_):
                sl_ = slice(pc * W_, (pc + 1) * W_)
                pb = pc % 2
                S.op("pe", lambda: nc.tensor.matmul(self.PS[pb][:, 0:W_], lhsT=self.blk64, rhs=t1_[:, sl_], start=True, stop=True), [Bt1], RB[pb])
                S.op("dve", lambda: A_.tensor_tensor(out=yacc[1][:, sl_], in0=self.PS[pb][:, 0:W_], in1=vf[:, sl_], op=ALU.mult), RB[pb] + [Bvf, Byy], [Byy])
            S.op("dve", lambda: A_.tensor_tensor(out=t0_, in0=t0_, in1=yacc[1], op=ALU.add), [Bt0, Byy], [Bt0])
            S.op("dve", lambda: A_.tensor_tensor(out=ogb, in0=t0_, in1=gg, op=ALU.mult), [Bt0, Bgg], [Bogb])
            S.dma("pool", og[rows, cols], ogb, reads=[Bogb])
            for b_ in By_all:
                b_.r.append(Byy.w)
    st.close()
    S.pe_selfwait = False
    S.pe_drain = 0


Prog.rwkv_scan = _rwkv_scan
```

```python
from contextlib import ExitStack
import numpy as np
import concourse.bass as bass
import concourse.mybir as mybir
from concourse.bass_utils import run_bass_kernel_spmd

F32 = mybir.dt.float32
BF16 = mybir.dt.bfloat16
AF = mybir.ActivationFunctionType
ALU = mybir.AluOpType

NCORES = 8
NB = 2
TC = 256
TL = 2048
T = TC + TL
TT = NB * T
D = 1024
DEPTH = 4
DFF = 2816
NFC = DFF // 128
BLK = 256
NBLK = T // BLK
EPS = 1e-6
CH = 64
NCH = T // CH
RW_LN_EPS = 64e-5
MLA_SCALE = 96 ** -0.5


class Buf:
    __slots__ = ("name", "w", "r")

    def __init__(self, name=""):
        self.name = name
        self.w = None
        self.r = []


class _Eng:
    def __init__(self, S, name, eng):
        self.S = S
        self.name = name
        self.eng = eng
        self.sem = None
        self.count = 0
        self.seen = {}
        self.nsem = 0
        self.ninst = 0
        self.own = set()

    def new_sem(self):
        self.sem = self.S.nc.alloc_semaphore(f"e_{self.name}_{self.nsem}")
        self.own.add(id(self.sem))
        self.nsem += 1
        self.count = 0

    def wait(self, ev):
        sem, val = ev
        k = id(sem)
        if self.name == "pe" and k in self.own and not self.S.pe_selfwait:
            return
        if self.seen.get(k, 0) >= val:
            return
        self.eng.wait_ge(sem, val)
        self.seen[k] = val


class Sched:
    EPOCH = 30000

    def __init__(self, nc, ndma_sems=48):
        self.nc = nc
        self.E = {}
        for name, eng in (("pe", nc.tensor), ("dve", nc.vector), ("act", nc.scalar),
                          ("pool", nc.gpsimd), ("sp", nc.sync)):
            e = _Eng(self, name, eng)
            e.new_sem()
            self.E[name] = e
        self.dsems = [[nc.alloc_semaphore(f"d{i}"), 0] for i in range(ndma_sems)]
        self.dnext = 0
        self._keep = []
        self.pe_selfwait = False
        self.pe_drain = 0
        self.last_pemode = None

    @staticmethod
    def _deps(reads, writes):
        deps = []
        for b in reads:
            if b.w is not None:
                deps.append(b.w)
        for b in writes:
            if b.w is not None:
                deps.append(b.w)
            deps.extend(b.r)
        return deps

    @staticmethod
    def _mark(ev, reads, writes):
        for b in writes:
            b.w = ev
            b.r = []
        for b in reads:
            if b not in writes:
                b.r.append(ev)
                if len(b.r) > 32:
                    b.r = b.r[-32:]

    def op(self, ename, fn, reads=(), writes=(), pemode=None):
        e = self.E[ename]
        for ev in self._deps(reads, writes):
            e.wait(ev)
        drain = False
        if ename == "pe":
            drain = self.pe_drain == 1 or (self.pe_drain == 2 and pemode != self.last_pemode)
            self.last_pemode = pemode
        if drain and e.count > 0:
            k = id(e.sem)
            if e.seen.get(k, 0) < e.count:
                e.eng.wait_ge(e.sem, e.count)
                e.seen[k] = e.count
        if e.count >= self.EPOCH:
            self._keep.append(e.sem)
            e.new_sem()
        inst = fn()
        e.count += 1
        e.ninst += 1
        inst.then_inc(e.sem, 1)
        ev = (e.sem, e.count)
        self._mark(ev, reads, writes)
        return ev

    def dma(self, qname, out, in_, reads=(), writes=(), **kw):
        q = self.E[qname]
        for ev in self._deps(reads, writes):
            q.wait(ev)
        slot = self.dsems[self.dnext % len(self.dsems)]
        self.dnext += 1
        if slot[1] >= self.EPOCH:
            self._keep.append(slot[0])
            slot[0] = self.nc.alloc_semaphore(f"dx{self.dnext}")
            slot[1] = 0
        if slot[1] > 0:
            q.wait((slot[0], slot[1]))
        q.eng.dma_start(out=out, in_=in_, **kw).then_inc(slot[0], 16)
        q.ninst += 1
        slot[1] += 16
        ev = (slot[0], slot[1])
        self._mark(ev, reads, writes)
        return ev

    def barrier(self):
        evs = [(e.sem, e.count) for e in self.E.values() if e.count > 0]
        evs += [(s[0], s[1]) for s in self.dsems if s[1] > 0]
        for e in self.E.values():
            for ev in evs:
                if ev[0] is e.sem:
                    continue
                e.wait(ev)


class PVec:
    def __init__(self):
        self.cols = []
        self.off = {}
        self.n = 0

    def add(self, name, vec):
        vec = np.asarray(vec, dtype=np.float32).reshape(-1)
        assert vec.size % 128 == 0
        nch = vec.size // 128
        self.off[name] = (self.n, nch)
        self.cols.append(np.ascontiguousarray(vec.reshape(nch, 128).T))
        self.n += nch

    def array(self):
        return np.ascontiguousarray(np.concatenate(self.cols, axis=1))


def pvec_layout(inputs):
    pv = PVec()
    for l in range(DEPTH):
        pv.add(f"b_mod{l}", inputs["b_mod"][l])
        pv.add(f"norm1_{l}", inputs["norm1"][l])
        pv.add(f"norm2_{l}", inputs["norm2"][l])
        for k in range(3):
            pv.add(f"conv{l}_{k}", inputs["ffn_conv"][l, k])
        pv.add(f"convb{l}", inputs["ffn_conv_b"][l])
    pv.add("norm_f", inputs["norm_f"])
    for d in range(2):
        for j in range(2):
            pv.add(f"hg_lb{d}_{j}", inputs["hg_lb"][d, j])
    for j in range(2):
        pv.add(f"hg_norm{j}", inputs["hg_norm"][j])
    for k in range(6):
        pv.add(f"rw_mu{k}", inputs["rw_mu"][0, k])
    for d in range(2):
        pv.add(f"rw_w0_{d}", inputs["rw_w0"][0, d])
        pv.add(f"rw_a0_{d}", inputs["rw_a0"][0, d])
    for nm in ("rw_k_k", "rw_k_a", "rw_r_k", "rw_ln_w", "rw_ln_b"):
        pv.add(nm, inputs[nm][0])
    pv.add("mla_q_norm", inputs["mla_q_norm"][0])
    pv.add("mla_kv_norm", inputs["mla_kv_norm"][0])
    return pv


def make_consts():
    c = {}
    c["ident"] = np.eye(128, dtype=np.float32)
    c["ones"] = np.ones((128, 128), dtype=np.float32)
    bo = np.zeros((128, 128), dtype=np.float32)
    bo[:64, :64] = 1.0
    bo[64:, 64:] = 1.0
    c["blk64"] = bo
    i = np.arange(64)[:, None]
    t = np.arange(64)[None, :]
    su = (i < t).astype(np.float32)
    iu = (i <= t).astype(np.float32)
    sl = (i > t).astype(np.float32)
    il = (i >= t).astype(np.float32)
    c["masks"] = np.concatenate([np.concatenate([su, iu, sl, il], axis=1)] * 2, axis=0)
    m = np.ones((128, T), dtype=np.float32)
    m[:, ::CH] = 0.0
    c["scanmask"] = m
    nq = 8
    inv_freq = (10000.0 ** (-np.arange(nq, dtype=np.float32) / nq)).astype(np.float32)
    pos = np.arange(TL)
    row = (pos // 64).astype(np.float32)
    col = (pos % 64).astype(np.float32)
    ang_r = row[:, None] * inv_freq
    ang_c = col[:, None] * inv_freq
    ang = np.concatenate([ang_r, ang_r, ang_c, ang_c], axis=-1).astype(np.float32)
    cos = np.ones((32, T), dtype=np.float32)
    sin = np.zeros((32, T), dtype=np.float32)
    cos[:, TC:] = np.cos(ang).T
    sin[:, TC:] = np.sin(ang).T
    c["rope_cos"] = cos
    c["rope_sin"] = sin
    return c


WEIGHT_NAMES = ["w_mod", "ffn_w_in", "ffn_w_out", "hg_w_in", "hg_w_o", "rw_w_rkv", "rw_w1", "rw_w2",
                "rw_a1", "rw_a2", "rw_g1", "rw_g2", "rw_w_o", "mla_w_dqkv", "mla_w_uq", "mla_w_ukv", "mla_w_o"]


class Stage:
    def __init__(self, P, name):
        self.P = P
        self.name = name
        self.es = ExitStack()
        P.nstage += 1
        self.k = 0

    def sb(self, name, shape, dt=F32):
        self.k += 1
        h = self.es.enter_context(self.P.nc.sbuf_tensor(f"{self.name}{self.P.nstage}_{name}_{self.k}", list(shape), dt))
        return h.ap()

    def close(self):
        self.P.S.barrier()
        self.es.close()


class Prog:
    def __init__(self, wshapes, pv_off, npv, dbg=(), xin_name=None):
        nc = bass.Bass("TRN2", target_bir_lowering=False)
        self.nc = nc
        self.dbg = set(dbg)
        self.pv_off = pv_off
        self.nstage = 0
        di = lambda n, s: nc.dram_tensor(n, list(s), F32, kind="ExternalInput").ap()
        self.x = di("x", [NB, TL, D])
        self.ctx = di("ctx", [NB, TC, D])
        self.cvec = di("cvec", [3, D])
        self.pvec_d = di("pvec", [128, npv])
        self.cd = {n: di("c_" + n, s) for n, s in (("ident", [128, 128]), ("ones", [128, 128]), ("blk64", [128, 128]),
                                                    ("masks", [128, 256]), ("scanmask", [128, T]),
                                                    ("rope_cos", [32, T]), ("rope_sin", [32, T]))}
        self.W = {n: di(n, wshapes[n]) for n in WEIGHT_NAMES}
        self.out = nc.dram_tensor("out", [NB, TL, D], F32, kind="ExternalOutput").ap()
        self.scratch = {}
        self.S = Sched(nc)
        S = self.S
        self.PS = [nc.alloc_psum_tensor(f"psb{i}", [128, 512], F32).ap() for i in range(8)]
        self.BPS = [Buf(f"ps{i}") for i in range(8)]
        g = lambda n, s, dt=F32: nc.alloc_sbuf_tensor("g_" + n, list(s), dt).ap()
        self.ident = g("ident", [128, 128])
        self.identb = g("identb", [128, 128], BF16)
        self.onesf = g("onesf", [128, 128])
        self.onesb = g("onesb", [128, 128], BF16)
        self.blk64 = g("blk64", [128, 128])
        self.masks = g("masks", [128, 256])
        self.pvec = g("pvec", [128, npv])
        self.MOD = g("MOD", [128, DEPTH, 48, 3])
        self.MA = g("MA", [128, DEPTH, 2, 8, 3])
        self.epsD = g("epsD", [128, 1])
        self.BC = Buf("consts")
        self.BMOD = Buf("mod")
        S.op("dve", lambda: nc.vector.memset(self.epsD, EPS), [], [self.BC])
        S.dma("sp", self.ident, self.cd["ident"], writes=[self.BC])
        b1, b2, b3, b4, b5, b6 = [Buf() for _ in range(6)]
        S.dma("sp", self.onesf, self.cd["ones"], writes=[b1])
        S.dma("sp", self.blk64, self.cd["blk64"], writes=[b2])
        S.dma("sp", self.masks, self.cd["masks"], writes=[b3])
        S.dma("sp", self.pvec, self.pvec_d, writes=[b4])
        S.dma("pool", self.identb, self.cd["ident"], writes=[b5])
        S.dma("pool", self.onesb, self.cd["ones"], writes=[b6])
        S.barrier()

    def scr(self, name, shape, dt=F32):
        if name not in self.scratch:
            kind = "ExternalOutput" if name in self.dbg else "Internal"
            self.scratch[name] = self.nc.dram_tensor("s_" + name, list(shape), dt, kind=kind).ap()
        return self.scratch[name]

    def pv(self, name, c=None):
        off, nch = self.pv_off[name]
        if c is None:
            return self.pvec[:, off:off + nch]
        return self.pvec[:, off + c:off + c + 1]

    def load_w(self, dst, src, bufs_cols, q="pool"):
        S = self.S
        n = dst.shape[2]
        v = src.rearrange("(kc p) n -> p kc n", p=128)
        bufs = []
        for n0 in range(0, n, 512):
            n1 = min(n, n0 + 512)
            b = Buf()
            S.dma(q, dst[:, :, n0:n1], v[:, :, n0:n1], writes=[b])
            bufs.append(b)
        return bufs

    def prologue_transpose(self, xT):
        nc, S = self.nc, self.S
        st = Stage(self, "pt")
        tin = [st.sb(f"tin{i}", [128, D]) for i in range(2)]
        tout = [st.sb(f"tout{i}", [128, 8, 128]) for i in range(2)]
        Bin = [Buf(), Buf()]
        Bout = [Buf(), Buf()]
        xTv = xT.rearrange("(c p) t -> p c t", p=128)
        tiles = []
        for b in range(NB):
            for k in range(T // 128):
                tiles.append((b, k))

        def src(b, k):
            t0 = k * 128
            if t0 < TC:
                return self.ctx[b, t0:t0 + 128, :]
            return self.x[b, t0 - TC:t0 - TC + 128, :]

        S.dma("sp", tin[0], src(*tiles[0]), writes=[Bin[0]])
        for n, (b, k) in enumerate(tiles):
            i = n % 2
            if n + 1 < len(tiles):
                S.dma("sp", tin[1 - i], src(*tiles[n + 1]), writes=[Bin[1 - i]])
            for hf in range(2):
                pb = 2 * (n % 2) + hf
                for c4 in range(4):
                    c = hf * 4 + c4
                    S.op("pe", lambda: nc.tensor.transpose(out=self.PS[pb][:, c4 * 128:(c4 + 1) * 128], in_=tin[i][:, c * 128:(c + 1) * 128], identity=self.ident),
                         [Bin[i]], [self.BPS[pb]])
                eng = "dve" if hf == 0 else "act"
                if hf == 0:
                    S.op("dve", lambda: nc.vector.tensor_copy(out=tout[i][:, 0:4, :], in_=self.PS[pb][:].rearrange("p (c t) -> p c t", c=4)), [self.BPS[pb]], [Bout[i]])
                else:
                    S.op("act", lambda: nc.scalar.copy(out=tout[i][:, 4:8, :], in_=self.PS[pb][:].rearrange("p (c t) -> p c t", c=4)), [self.BPS[pb]], [Bout[i]])
            col = b * T + k * 128
            S.dma("pool", xTv[:, :, col:col + 128], tout[i], reads=[Bout[i]])
        st.close()

    def prologue_mod(self):
        nc, S = self.nc, self.S
        st = Stage(self, "pm")
        cv = st.sb("cv", [3, D])
        sc = st.sb("sc", [3, D])
        scT = st.sb("scT", [128, 8, 3])
        Bcv, Bsc, BscT = Buf(), Buf(), Buf()
        S.dma("sp", cv, self.cvec, writes=[Bcv])
        S.op("act", lambda: nc.scalar.activation(out=sc, in_=cv, func=AF.Silu), [Bcv], [Bsc])
        for kc in range(8):
            S.op("pe", lambda: nc.tensor.transpose(out=self.PS[0][:, kc * 4:kc * 4 + 3], in_=sc[0:3, kc * 128:(kc + 1) * 128], identity=self.ident[0:3, 0:3]),
                 [Bsc], [self.BPS[0]])
        S.op("dve", lambda: nc.vector.tensor_copy(out=scT, in_=self.PS[0][:, 0:32].rearrange("p (k f) -> p k f", f=4)[:, :, 0:3]), [self.BPS[0]], [BscT])
        NWB = 4
        wt = [st.sb(f"wt{i}", [128, 8, 512]) for i in range(NWB)]
        Bwt = [Buf() for _ in range(NWB)]
        groups = [(l, g) for l in range(DEPTH) for g in range(12)]

        def wsrc(l, g):
            return self.W["w_mod"][l].rearrange("(kc p) n -> p kc n", p=128)[:, :, g * 512:(g + 1) * 512]

        def wload(n):
            S.dma("sp" if n % 2 == 0 else "act", wt[n % NWB], wsrc(*groups[n]), writes=[Bwt[n % NWB]])

        for n in range(NWB - 1):
            wload(n)
        for n, (l, g) in enumerate(groups):
            i = n % NWB
            if n + NWB - 1 < len(groups):
                wload(n + NWB - 1)
            pb = 1 + (n % 2)
            for oc in range(4):
                for kc in range(8):
                    S.op("pe", lambda: nc.tensor.matmul(self.PS[pb][:, oc * 4:oc * 4 + 3], lhsT=wt[i][:, kc, oc * 128:(oc + 1) * 128], rhs=scT[:, kc, :], start=(kc == 0), stop=(kc == 7)),
                         [Bwt[i], BscT], [self.BPS[pb]])
            boff, _ = self.pv_off[f"b_mod{l}"]
            bias = self.pvec[:, boff + g * 4:boff + g * 4 + 4].unsqueeze(2).to_broadcast([128, 4, 3])
            S.op("dve", lambda: nc.vector.tensor_tensor(out=self.MOD[:, l, g * 4:(g + 1) * 4, :], in0=self.PS[pb][:, 0:16].rearrange("p (o f) -> p o f", f=4)[:, :, 0:3], in1=bias, op=ALU.add),
                 [self.BPS[pb]], [self.BMOD])
        for l in range(DEPTH):
            for w in range(2):
                sc_idx = 8 if w == 0 else 32
                nrm = self.pv(f"norm{w + 1}_{l}").unsqueeze(2).to_broadcast([128, 8, 3])
                S.op("dve", lambda: nc.vector.scalar_tensor_tensor(out=self.MA[:, l, w, :, :], in0=self.MOD[:, l, sc_idx:sc_idx + 8, :], scalar=1.0, in1=nrm, op0=ALU.add, op1=ALU.mult),
                     [self.BMOD], [self.BMOD])
        st.close()

    def norm_tiles(self, st, n=BLK + 2):
        return dict(sq=st.sb("nsq", [128, 8, n], BF16), tmp=st.sb("ntmp", [128, 8, n]), r0=st.sb("nr0", [128, n]), r1=st.sb("nr1", [128, n]),
                    B=[Buf() for _ in range(4)])

    def norm_block(self, nt, xs, Bxs, n, A, Bsh, hb, Bhb, bank):
        nc, S = self.nc, self.S
        sq, tmp, r0, r1 = nt["sq"], nt["tmp"], nt["r0"], nt["r1"]
        Bsq, Btmp, Br0, Br1 = nt["B"]
        S.op("act", lambda: nc.scalar.activation(out=sq[:, :, :n], in_=xs, func=AF.Square), [Bxs], [Bsq])
        ps = self.PS[bank]
        for c in range(8):
            S.op("pe", lambda: nc.tensor.matmul(ps[:, :n], lhsT=self.onesb, rhs=sq[:, c, :n], start=(c == 0), stop=(c == 7)), [Bsq], [self.BPS[bank]])
        S.op("act", lambda: nc.scalar.activation(out=r0[:, :n], in_=ps[:, :n], func=AF.Sqrt, scale=1.0 / D, bias=self.epsD), [self.BPS[bank]], [Br0])
        S.op("dve", lambda: nc.vector.reciprocal(out=r1[:, :n], in_=r0[:, :n]), [Br0], [Br1])
        S.op("dve", lambda: nc.vector.tensor_tensor(out=tmp[:, :, :n], in0=xs, in1=r1[:, :n].unsqueeze(1).to_broadcast([128, 8, n]), op=ALU.mult), [Bxs, Br1], [Btmp])
        for c in range(8):
            S.op("act", lambda: nc.scalar.activation(out=hb[:, c, :n], in_=tmp[:, c, :n], func=AF.Identity, scale=A[:, c:c + 1], bias=(Bsh[:, c:c + 1] if Bsh is not None else 0.0)),
                 [Btmp, self.BMOD], [Bhb])

    def mod_ab(self, l, w, j):
        A = self.MA[:, l, w, :, j]
        sh = self.MOD[:, l, (0 if w == 0 else 24):(8 if w == 0 else 32), j]
        gt = self.MOD[:, l, (16 if w == 0 else 40):(24 if w == 0 else 48), j]
        return A, sh, gt

    @staticmethod
    def blocks(skip_ctx=False):
        out = []
        for b in range(NB):
            for k in range(NBLK):
                if skip_ctx and k == 0:
                    continue
                out.append((b, k))
        return out

    @staticmethod
    def blk_range(k):
        seq0, seq1 = (0, TC) if k == 0 else (TC, T)
        t0 = k * BLK
        lo = max(t0 - 1, seq0)
        hi = min(t0 + BLK + 1, seq1)
        return t0, lo, hi, (t0 == seq0), (t0 + BLK == seq1)

    def ffn_stage(self, l, xin, xout, skip_ctx):
        nc, S = self.nc, self.S
        st = Stage(self, "ffn")
        Win = st.sb("win", [128, 8, 2 * DFF], BF16)
        Wout = st.sb("wout", [128, NFC, D], BF16)
        BWin = self.load_w(Win, self.W["ffn_w_in"][l], None)
        BWout = []
        osrc = self.W["ffn_w_out"][l].rearrange("(fc p) n -> p fc n", p=128)
        for f0 in range(0, NFC, 2):
            b = Buf()
            S.dma("pool", Wout[:, f0:f0 + 2, :], osrc[:, f0:f0 + 2, :], writes=[b])
            BWout.append(b)
        NH = BLK + 2
        xs = [st.sb(f"xs{i}", [128, 8, NH]) for i in range(2)]
        hb = [st.sb(f"hb{i}", [128, 8, NH], BF16) for i in range(2)]
        gt_ = [st.sb(f"g{i}", [128, NFC, BLK], BF16) for i in range(2)]
        cv = [st.sb(f"cv{i}", [128, BLK]) for i in range(2)]
        sl = [st.sb(f"sl{i}", [128, BLK]) for i in range(2)]
        Bxs, Bhb, Bg, Bcv, Bsl = [[Buf(), Buf()] for _ in range(5)]
        nt = self.norm_tiles(st)
        for i in range(2):
            S.op("dve", lambda: nc.vector.memset(xs[i], 0.0), [], [Bxs[i]])
        xiv = xin.rearrange("(c p) t -> p c t", p=128)
        xov = xout.rearrange("(c p) t -> p c t", p=128)
        blocks = self.blocks(skip_ctx)

        def load(n):
            b, k = blocks[n]
            t0, lo, hi, _, _ = self.blk_range(k)
            S.dma("sp", xs[n % 2][:, :, lo - (t0 - 1):hi - (t0 - 1)], xiv[:, :, b * T + lo:b * T + hi], writes=[Bxs[n % 2]])

        load(0)
        for n, (b, k) in enumerate(blocks):
            i = n % 2
            if n + 1 < len(blocks):
                load(n + 1)
            t0, lo, hi, first, last = self.blk_range(k)
            j = 2 if k == 0 else b
            A, sh, gate = self.mod_ab(l, 1, j)
            self.norm_block(nt, xs[i], Bxs[i], NH, A, sh, hb[i], Bhb[i], 6)
            for fc in range(NFC):
                q = fc % 2
                pa, pvv = self.PS[q], self.PS[2 + q]
                ga = BWin[(fc * 128) // 512]
                gv = BWin[(DFF + fc * 128) // 512]
                for kc in range(8):
                    S.op("pe", lambda: nc.tensor.matmul(pa[:, :NH], lhsT=Win[:, kc, fc * 128:(fc + 1) * 128], rhs=hb[i][:, kc, :], start=(kc == 0), stop=(kc == 7)),
                         [ga, Bhb[i]], [self.BPS[q]])
                for kc in range(8):
                    S.op("pe", lambda: nc.tensor.matmul(pvv[:, :BLK], lhsT=Win[:, kc, DFF + fc * 128:DFF + (fc + 1) * 128], rhs=hb[i][:, kc, 1:1 + BLK], start=(kc == 0), stop=(kc == 7)),
                         [gv, Bhb[i]], [self.BPS[2 + q]])
                w0, w1, w2, cb = self.pv(f"conv{l}_0", fc), self.pv(f"conv{l}_1", fc), self.pv(f"conv{l}_2", fc), self.pv(f"convb{l}", fc)
                S.op("act", lambda: nc.scalar.activation(out=cv[q], in_=pa[:, 1:1 + BLK], func=AF.Identity, scale=w1, bias=cb), [self.BPS[q]], [Bcv[q]])
                c0 = 1 if first else 0
                S.op("dve", lambda: nc.vector.scalar_tensor_tensor(out=cv[q][:, c0:BLK], in0=pa[:, c0:BLK], scalar=w0, in1=cv[q][:, c0:BLK], op0=ALU.mult, op1=ALU.add),
                     [self.BPS[q], Bcv[q]], [Bcv[q]])
                c1 = BLK - 1 if last else BLK
                S.op("dve", lambda: nc.vector.scalar_tensor_tensor(out=cv[q][:, 0:c1], in0=pa[:, 2:2 + c1], scalar=w2, in1=cv[q][:, 0:c1], op0=ALU.mult, op1=ALU.add),
                     [self.BPS[q], Bcv[q]], [Bcv[q]])
                S.op("act", lambda: nc.scalar.activation(out=sl[q], in_=cv[q], func=AF.Silu), [Bcv[q]], [Bsl[q]])
                S.op("dve", lambda: nc.vector.tensor_tensor(out=gt_[i][:, fc, :], in0=sl[q], in1=pvv[:, :BLK], op=ALU.mult), [Bsl[q], self.BPS[2 + q]], [Bg[i]])
            for oc in range(8):
                q = 4 + oc % 2
                po = self.PS[q]
                for fc in range(NFC):
                    S.op("pe", lambda: nc.tensor.matmul(po[:, :BLK], lhsT=Wout[:, fc, oc * 128:(oc + 1) * 128], rhs=gt_[i][:, fc, :], start=(fc == 0), stop=(fc == NFC - 1)),
                         [BWout[fc // 2], Bg[i]], [self.BPS[q]])
                S.op("dve", lambda: nc.vector.scalar_tensor_tensor(out=xs[i][:, oc, 1:1 + BLK], in0=po[:, :BLK], scalar=gate[:, oc:oc + 1], in1=xs[i][:, oc, 1:1 + BLK], op0=ALU.mult, op1=ALU.add),
                     [self.BPS[q], Bxs[i], self.BMOD], [Bxs[i]])
            S.dma("pool", xov[:, :, b * T + t0:b * T + t0 + BLK], xs[i][:, :, 1:1 + BLK], reads=[Bxs[i]])
        st.close()

    def final_stage(self, xin):
        nc, S = self.nc, self.S
        st = Stage(self, "fin")
        xs = [st.sb(f"xs{i}", [128, 8, BLK]) for i in range(2)]
        hb = [st.sb(f"hb{i}", [128, 8, BLK]) for i in range(2)]
        ot = [st.sb(f"ot{i}", [128, D]) for i in range(2)]
        Bxs, Bhb, Bot = [[Buf(), Buf()] for _ in range(3)]
        nt = self.norm_tiles(st, BLK)
        xiv = xin.rearrange("(c p) t -> p c t", p=128)
        blocks = self.blocks(True)
        A = self.pv("norm_f")

        def load(n):
            b, k = blocks[n]
            S.dma("sp", xs[n % 2], xiv[:, :, b * T + k * BLK:b * T + (k + 1) * BLK], writes=[Bxs[n % 2]])

        load(0)
        nt_i = 0
        for n, (b, k) in enumerate(blocks):
            i = n % 2
            if n + 1 < len(blocks):
                load(n + 1)
            self.norm_block(nt, xs[i], Bxs[i], BLK, A, None, hb[i], Bhb[i], 6)
            for tt in range(2):
                o = nt_i % 2
                nt_i += 1
                for hf in range(2):
                    pb = 2 * o + hf
                    for c4 in range(4):
                        c = hf * 4 + c4
                        S.op("pe", lambda: nc.tensor.transpose(out=self.PS[pb][:, c4 * 128:(c4 + 1) * 128], in_=hb[i][:, c, tt * 128:(tt + 1) * 128], identity=self.ident),
                             [Bhb[i]], [self.BPS[pb]])
                    if hf == 0:
                        S.op("dve", lambda: nc.vector.tensor_copy(out=ot[o][:, 0:512], in_=self.PS[pb]), [self.BPS[pb]], [Bot[o]])
                    else:
                        S.op("act", lambda: nc.scalar.copy(out=ot[o][:, 512:1024], in_=self.PS[pb]), [self.BPS[pb]], [Bot[o]])
                tl = k * BLK - TC + tt * 128
                S.dma("pool", self.out[b, tl:tl + 128, :], ot[o], reads=[Bot[o]])
        st.close()


def build_program(wshapes, pv_off, npv, plan=None, dbg=()):
    P = Prog(wshapes, pv_off, npv, dbg=dbg)
    xa = P.scr("xA", [D, TT])
    xb = P.scr("xB", [D, TT])
    if plan is None:
        plan = ["tr", "mod"]
        for l in range(DEPTH):
            plan += [f"mix{l}", f"ffn{l}"]
        plan += ["final"]
    cur, nxt = xa, xb
    for step in plan:
        if step == "tr":
            P.prologue_transpose(cur)
        elif step == "mod":
            P.prologue_mod()
        elif step.startswith("mix"):
            l = int(step[3:])
            P.mixer(l, cur, nxt)
            cur, nxt = nxt, cur
        elif step.startswith("ffn"):
            l = int(step[3:])
            P.ffn_stage(l, cur, nxt, skip_ctx=(l == DEPTH - 1))
            cur, nxt = nxt, cur
        elif step == "final":
            P.final_stage(cur)
    P.S.barrier()
    return P


def prep_inputs(inputs, cores=range(NCORES)):
    pv = pvec_layout(inputs)
    pva = pv.array()
    consts = make_consts()
    shared = {"pvec": pva}
    for k, v in consts.items():
        shared["c_" + k] = v
    for n in WEIGHT_NAMES:
        shared[n] = np.ascontiguousarray(inputs[n], dtype=np.float32)
    in_maps = []
    for c in cores:
        m = dict(shared)
        m["x"] = np.ascontiguousarray(inputs["x"][NB * c:NB * (c + 1)], dtype=np.float32)
        m["ctx"] = np.ascontiguousarray(inputs["ctx"][NB * c:NB * (c + 1)], dtype=np.float32)
        m["cvec"] = np.ascontiguousarray(np.concatenate([inputs["c"][NB * c:NB * (c + 1)], inputs["c_ctx"][None, :]], axis=0), dtype=np.float32)
        in_maps.append(m)
    wshapes = {n: list(inputs[n].shape) for n in WEIGHT_NAMES}
    return in_maps, wshapes, pv.off, pva.shape[1]


def kernel(**inputs):
    inputs = {k: np.asarray(v) for k, v in inputs.items()}
    in_maps, wshapes, pv_off, npv = prep_inputs(inputs)
    P = build_program(wshapes, pv_off, npv)
    res = run_bass_kernel_spmd(P.nc, in_maps, core_ids=list(range(NCORES)))
    out = np.concatenate([np.asarray(r["out"]) for r in res.results], axis=0)
    return out.astype(np.float32)


def _inproj_stage(self, l, xin, Wd, N, dst_fm, tm_specs, f32_h=False):
    nc, S = self.nc, self.S
    st = Stage(self, "ip")
    Wt = st.sb("w", [128, 8, N], BF16)
    BW = self.load_w(Wt, Wd, None)
    xs = [st.sb(f"xs{i}", [128, 8, BLK]) for i in range(2)]
    hb = [st.sb(f"hb{i}", [128, 8, BLK], BF16) for i in range(2)]
    sg = [st.sb(f"sg{i}", [128, 8, BLK]) for i in range(2)]
    tmw = max([nc_ for (_, nc_, _) in tm_specs], default=0)
    tms = [st.sb(f"tm{i}", [128, max(tmw, 1)], BF16) for i in range(2)]
    Bxs, Bhb, Bsg, Btm = [[Buf(), Buf()] for _ in range(4)]
    nt = self.norm_tiles(st, BLK)
    xiv = xin.rearrange("(c p) t -> p c t", p=128)
    dv = dst_fm.rearrange("(c p) t -> p c t", p=128)
    blocks = self.blocks(False)

    def load(n):
        b, k = blocks[n]
        S.dma("sp", xs[n % 2], xiv[:, :, b * T + k * BLK:b * T + (k + 1) * BLK], writes=[Bxs[n % 2]])

    load(0)
    sgi = 0
    tmi = 0
    pbank = 0
    for n, (b, k) in enumerate(blocks):
        i = n % 2
        if n + 1 < len(blocks):
            load(n + 1)
        j = 2 if k == 0 else b
        A, sh, _ = self.mod_ab(l, 0, j)
        self.norm_block(nt, xs[i], Bxs[i], BLK, A, sh, hb[i], Bhb[i], 6)
        col = b * T + k * BLK
        for og in range(N // 1024):
            s_ = sgi % 2
            sgi += 1
            for o8 in range(8):
                oc = og * 8 + o8
                pb = pbank % 4
                pbank += 1
                for kc in range(8):
                    S.op("pe", lambda: nc.tensor.matmul(self.PS[pb][:, :BLK], lhsT=Wt[:, kc, oc * 128:(oc + 1) * 128], rhs=hb[i][:, kc, :], start=(kc == 0), stop=(kc == 7)),
                         [BW[(oc * 128) // 512], Bhb[i]], [self.BPS[pb]])
                if o8 % 2 == 0:
                    S.op("act", lambda: nc.scalar.copy(out=sg[s_][:, o8, :], in_=self.PS[pb][:, :BLK]), [self.BPS[pb]], [Bsg[s_]])
                else:
                    S.op("dve", lambda: nc.vector.tensor_copy(out=sg[s_][:, o8, :], in_=self.PS[pb][:, :BLK]), [self.BPS[pb]], [Bsg[s_]])
            S.dma("pool", dv[:, og * 8:(og + 1) * 8, col:col + BLK], sg[s_], reads=[Bsg[s_]])
        for (c0, ncols, dst_tm) in tm_specs:
            for tt in range(BLK // 128):
                s_ = tmi % 2
                tmi += 1
                for n0 in range(0, ncols, 512):
                    pb = 4 + (pbank % 2)
                    pbank += 1
                    for kc in range(8):
                        S.op("pe", lambda: nc.tensor.matmul(self.PS[pb][:, :512], lhsT=hb[i][:, kc, tt * 128:(tt + 1) * 128], rhs=Wt[:, kc, c0 + n0:c0 + n0 + 512], start=(kc == 0), stop=(kc == 7)),
                             [BW[(c0 + n0) // 512], Bhb[i]], [self.BPS[pb]])
                    S.op("act", lambda: nc.scalar.copy(out=tms[s_][:, n0:n0 + 512], in_=self.PS[pb][:, :512]), [self.BPS[pb]], [Btm[s_]])
                S.dma("pool", dst_tm[col + tt * 128:col + (tt + 1) * 128, :], tms[s_][:, :ncols], reads=[Btm[s_]])
    st.close()


def _outproj_stage(self, l, og, Wd, xin, xout, skip_ctx):
    nc, S = self.nc, self.S
    st = Stage(self, "op")
    Wt = st.sb("w", [128, 8, D], BF16)
    BW = self.load_w(Wt, Wd, None)
    xs = [st.sb(f"xs{i}", [128, 8, BLK]) for i in range(2)]
    ob = [st.sb(f"ob{i}", [128, 8, BLK], BF16) for i in range(2)]
    Bxs, Bob = [[Buf(), Buf()] for _ in range(2)]
    xiv = xin.rearrange("(c p) t -> p c t", p=128)
    xov = xout.rearrange("(c p) t -> p c t", p=128)
    ogv = og.rearrange("(c p) t -> p c t", p=128)
    blocks = self.blocks(skip_ctx)

    def load(n):
        b, k = blocks[n]
        col = b * T + k * BLK
        S.dma("sp", xs[n % 2], xiv[:, :, col:col + BLK], writes=[Bxs[n % 2]])
        S.dma("sp", ob[n % 2], ogv[:, :, col:col + BLK], writes=[Bob[n % 2]])

    load(0)
    for n, (b, k) in enumerate(blocks):
        i = n % 2
        if n + 1 < len(blocks):
            load(n + 1)
        j = 2 if k == 0 else b
        _, _, gate = self.mod_ab(l, 0, j)
        for oc in range(8):
            pb = oc % 4
            for kc in range(8):
                S.op("pe", lambda: nc.tensor.matmul(self.PS[pb][:, :BLK], lhsT=Wt[:, kc, oc * 128:(oc + 1) * 128], rhs=ob[i][:, kc, :], start=(kc == 0), stop=(kc == 7)),
                     [BW[(oc * 128) // 512], Bob[i]], [self.BPS[pb]])
            S.op("dve", lambda: nc.vector.scalar_tensor_tensor(out=xs[i][:, oc, :], in0=self.PS[pb][:, :BLK], scalar=gate[:, oc:oc + 1], in1=xs[i][:, oc, :], op0=ALU.mult, op1=ALU.add),
                 [self.BPS[pb], Bxs[i], self.BMOD], [Bxs[i]])
        col = b * T + k * BLK
        S.dma("pool", xov[:, :, col:col + BLK], xs[i], reads=[Bxs[i]])
    st.close()


def _hgrn2_scan(self, jh, Pfm, Itm, og):
    nc, S = self.nc, self.S
    st = Stage(self, "hs")
    A_ = nc.vector
    LB = st.sb("LB", [128, 2, 8])
    OML = st.sb("OML", [128, 2, 8])
    e0 = st.sb("e0", [128, 8]); e1 = st.sb("e1", [128, 8]); rr = st.sb("rr", [128, 8]); p0 = st.sb("p0", [128, 8]); p1 = st.sb("p1", [128, 8])
    BL = Buf()
    for d in range(2):
        S.op("act", lambda: nc.scalar.activation(out=e0, in_=self.pv(f"hg_lb{d}_0"), func=AF.Exp), [], [BL])
        S.op("act", lambda: nc.scalar.activation(out=e1, in_=self.pv(f"hg_lb{d}_1"), func=AF.Exp), [BL], [BL])
        S.op("dve", lambda: A_.tensor_tensor(out=rr, in0=e0, in1=e1, op=ALU.add), [BL], [BL])
        S.op("dve", lambda: A_.reciprocal(out=rr, in_=rr), [BL], [BL])
        S.op("dve", lambda: A_.tensor_tensor(out=p0, in0=e0, in1=rr, op=ALU.mult), [BL], [BL])
        S.op("dve", lambda: A_.tensor_tensor(out=p1, in0=e1, in1=rr, op=ALU.mult), [BL], [BL])
        if jh == 1:
            S.op("dve", lambda: A_.tensor_tensor(out=p1, in0=p0, in1=p1, op=ALU.add), [BL], [BL])
        else:
            S.op("dve", lambda: A_.tensor_copy(out=p1, in_=p0), [BL], [BL])
        S.op("dve", lambda: A_.tensor_tensor(out=LB[:, d, :], in0=p1, in1=p0, op=ALU.subtract), [BL], [BL])
        S.op("dve", lambda: A_.tensor_scalar(out=OML[:, d, :], in0=LB[:, d, :], scalar1=-1.0, scalar2=1.0, op0=ALU.mult, op1=ALU.add), [BL], [BL])
    smask = st.sb("smask", [128, T])
    Bsm = Buf()
    S.dma("sp", smask, self.cd["scanmask"], writes=[Bsm])
    f32t = lambda n: st.sb(n, [128, T])
    qs = f32t("qs"); graw = f32t("graw"); kk = f32t("kk"); ep = f32t("ep"); en = f32t("en")
    z = [f32t("z0"), f32t("z1")]; bb = [f32t("b0"), f32t("b1")]; of = [f32t("of0"), f32t("of1")]
    qt = [st.sb(f"qt{d}", [128, T], BF16) for d in range(2)]
    kh = [st.sb(f"kh{d}", [128, T], BF16) for d in range(2)]
    sqb = st.sb("sqb", [128, T], BF16)
    ogb = st.sb("ogb", [128, T], BF16)
    Vt = st.sb("Vt", [64, NCH, 128], BF16)
    emid = [st.sb(f"emid{d}", [128, NCH]) for d in range(2)]
    eend = [st.sb(f"eend{d}", [128, NCH]) for d in range(2)]
    eem = [st.sb(f"eem{d}", [128, NCH]) for d in range(2)]
    Sst = [st.sb(f"S{d}", [128, 128]) for d in range(2)]
    Sm = [st.sb(f"Sm{d}", [128, 128], BF16) for d in range(2)]
    tmpS = [st.sb(f"tS{d}", [128, 128]) for d in range(2)]
    khT = [st.sb(f"khT{d}", [64, 128], BF16) for d in range(2)]
    att = [st.sb(f"att{d}", [64, 64], BF16) for d in range(2)]
    Bqs, Bgr, Bkk, Bep, Ben, Bsq, Bog, BVt = [Buf() for _ in range(8)]
    Bz, Bbb, Bof, Bqt, Bkh, Bes, BS, BSm, BtS, BkT, Batt = [[Buf(), Buf()] for _ in range(11)]
    PSb = [self.PS[i].bitcast(BF16) for i in range(8)]
    for d in range(2):
        S.op("dve", lambda: A_.memset(att[d], 0.0), [], [Batt[d]])
    cf = list(range(NCH))
    cb = list(range(TC // CH - 1, -1, -1)) + list(range(NCH - 1, TC // CH - 1, -1))
    order = [cf, cb]
    for b in range(NB):
        for h in range(8):
            rows = slice(h * 128, (h + 1) * 128)
            cols = slice(b * T, (b + 1) * T)
            S.dma("sp", qs, Pfm[0 * D + h * 128:0 * D + (h + 1) * 128, cols], writes=[Bqs])
            S.dma("sp", z[0], Pfm[3 * D + h * 128:3 * D + (h + 1) * 128, cols], writes=[Bz[0]])
            S.dma("sp", z[1], Pfm[4 * D + h * 128:4 * D + (h + 1) * 128, cols], writes=[Bz[1]])
            S.dma("sp", graw, Pfm[2 * D + h * 128:2 * D + (h + 1) * 128, cols], writes=[Bgr])
            S.dma("sp", Vt, Itm[cols, rows].rearrange("(c s) v -> s c v", s=CH), writes=[BVt])
            S.op("act", lambda: nc.scalar.activation(out=qs, in_=qs, func=AF.Silu), [Bqs], [Bqs])
            for d in range(2):
                m_idx = 32 if d == 0 else 31
                zt = z[d]
                S.op("act", lambda: nc.scalar.activation(out=zt, in_=zt, func=AF.Sigmoid), [Bz[d]], [Bz[d]])
                S.op("act", lambda: nc.scalar.activation(out=zt, in_=zt, func=AF.Identity, scale=OML[:, d, h:h + 1], bias=LB[:, d, h:h + 1]), [Bz[d], BL], [Bz[d]])
                S.op("act", lambda: nc.scalar.activation(out=kk, in_=zt, func=AF.Identity, scale=-1.0, bias=self.onesf[:, 0:1]), [Bz[d]], [Bkk])
                S.op("act", lambda: nc.scalar.activation(out=zt, in_=zt, func=AF.Ln), [Bz[d]], [Bz[d]])
                S.op("dve", lambda: A_.tensor_tensor_scan(out=bb[d], data0=smask, data1=zt, initial=0.0, op0=ALU.mult, op1=ALU.add), [Bsm, Bz[d]], [Bbb[d]])
                b3 = bb[d].rearrange("p (c s) -> p c s", s=CH)
                if d == 1:
                    S.op("dve", lambda: A_.tensor_tensor(out=zt, in0=zt, in1=bb[d], op=ALU.subtract), [Bz[d], Bbb[d]], [Bz[d]])
                    S.op("dve", lambda: A_.tensor_tensor(out=ep.rearrange("p (c s) -> p c s", s=CH), in0=zt.rearrange("p (c s) -> p c s", s=CH),
                                                          in1=b3[:, :, CH - 1:CH].to_broadcast([128, NCH, CH]), op=ALU.add), [Bz[d], Bbb[d]], [Bep])
                    S.op("act", lambda: nc.scalar.copy(out=bb[d], in_=ep), [Bep], [Bbb[d]])
                e_idx = CH - 1 if d == 0 else 0
                S.op("act", lambda: nc.scalar.activation(out=emid[d], in_=b3[:, :, m_idx], func=AF.Exp), [Bbb[d]], [Bes[d]])
                S.op("act", lambda: nc.scalar.activation(out=eend[d], in_=b3[:, :, e_idx], func=AF.Exp), [Bbb[d]], [Bes[d]])
                S.op("dve", lambda: A_.tensor_tensor(out=eem[d], in0=b3[:, :, e_idx], in1=b3[:, :, m_idx], op=ALU.subtract), [Bbb[d]], [Bes[d]])
                S.op("act", lambda: nc.scalar.activation(out=eem[d], in_=eem[d], func=AF.Exp), [Bes[d]], [Bes[d]])
                S.op("dve", lambda: A_.tensor_tensor(out=ep.rearrange("p (c s) -> p c s", s=CH), in0=b3, in1=b3[:, :, m_idx:m_idx + 1].to_broadcast([128, NCH, CH]), op=ALU.subtract),
                     [Bbb[d]], [Bep])
                S.op("act", lambda: nc.scalar.activation(out=en, in_=ep, func=AF.Exp, scale=-1.0), [Bep], [Ben])
                S.op("act", lambda: nc.scalar.activation(out=ep, in_=ep, func=AF.Exp), [Bep], [Bep])
                S.op("dve", lambda: A_.tensor_tensor(out=qt[d], in0=qs, in1=ep, op=ALU.mult), [Bqs, Bep], [Bqt[d]])
                S.op("dve", lambda: A_.tensor_tensor(out=kh[d], in0=kk, in1=en, op=ALU.mult), [Bkk, Ben], [Bkh[d]])
                S.op("dve", lambda: A_.memset(Sst[d], 0.0), [], [BS[d]])
                S.op("dve", lambda: A_.memset(Sm[d], 0.0), [], [BSm[d]])
            def hstep(d, step):
                c = order[d][step]
                cs = slice(c * CH, (c + 1) * CH)
                pb = d * 4
                mk = (self.masks[0:64, 64:128] if d == 0 else self.masks[0:64, 192:256]).bitcast(mybir.dt.uint32)
                S.op("pe", lambda: nc.tensor.transpose(out=PSb[pb][0:64, 0:128], in_=kh[d][:, cs], identity=self.identb), [Bkh[d]], [self.BPS[pb]])
                S.op("pe", lambda: nc.tensor.matmul(self.PS[pb + 1][0:64, 0:64], lhsT=kh[d][:, cs], rhs=qt[d][:, cs], start=True, stop=True), [Bkh[d], Bqt[d]], [self.BPS[pb + 1]])
                yield
                S.op("act", lambda: nc.scalar.copy(out=khT[d], in_=PSb[pb][0:64, 0:128]), [self.BPS[pb]], [BkT[d]])
                S.op("dve", lambda: A_.copy_predicated(out=att[d], mask=mk, data=self.PS[pb + 1][0:64, 0:64]), [self.BPS[pb + 1]], [Batt[d]])
                S.op("pe", lambda: nc.tensor.matmul(self.PS[pb + 2][:, 0:64], lhsT=Vt[:, c, :], rhs=att[d], start=True, stop=False), [BVt, Batt[d]], [self.BPS[pb + 2]])
                S.op("pe", lambda: nc.tensor.matmul(self.PS[pb + 2][:, 0:64], lhsT=Sm[d], rhs=qt[d][:, cs], start=False, stop=True), [BSm[d], Bqt[d]], [self.BPS[pb + 2]])
                S.op("pe", lambda: nc.tensor.matmul(self.PS[pb + 3][:, 0:128], lhsT=khT[d], rhs=Vt[:, c, :], start=True, stop=True), [BkT[d], BVt], [self.BPS[pb + 3]])
                yield
                S.op("act", lambda: nc.scalar.activation(out=tmpS[d], in_=self.PS[pb + 3][:, 0:128], func=AF.Identity, scale=eem[d][:, c:c + 1]), [self.BPS[pb + 3], Bes[d]], [BtS[d]])
                S.op("dve", lambda: A_.scalar_tensor_tensor(out=Sst[d], in0=Sst[d], scalar=eend[d][:, c:c + 1], in1=tmpS[d], op0=ALU.mult, op1=ALU.add), [BS[d], BtS[d], Bes[d]], [BS[d]])
                S.op("act", lambda: nc.scalar.copy(out=of[d][:, cs], in_=self.PS[pb + 2][:, 0:64]), [self.BPS[pb + 2]], [Bof[d]])
                if step + 1 < NCH:
                    cn = order[d][step + 1]
                    S.op("dve", lambda: A_.tensor_scalar(out=Sm[d], in0=Sst[d], scalar1=emid[d][:, cn:cn + 1], scalar2=None, op0=ALU.mult), [BS[d], Bes[d]], [BSm[d]])

            for step in range(NCH):
                gens = [hstep(d, step) for d in range(2)]
                while gens:
                    for g_ in list(gens):
                        try:
                            next(g_)
                        except StopIteration:
                            gens.remove(g_)
            S.op("dve", lambda: A_.tensor_tensor(out=of[0], in0=of[0], in1=of[1], op=ALU.add), [Bof[0], Bof[1]], [Bof[0]])
            S.op("act", lambda: nc.scalar.activation(out=sqb, in_=of[0], func=AF.Square), [Bof[0]], [Bsq])
            for pc in range(6):
                sl_ = slice(pc * 384, (pc + 1) * 384)
                pb = pc % 2
                S.op("pe", lambda: nc.tensor.matmul(self.PS[pb][:, 0:384], lhsT=self.onesb, rhs=sqb[:, sl_], start=True, stop=True), [Bsq], [self.BPS[pb]])
                S.op("act", lambda: nc.scalar.activation(out=ep[:, sl_], in_=self.PS[pb][:, 0:384], func=AF.Sqrt, scale=1.0 / 128, bias=self.epsD), [self.BPS[pb]], [Bep])
            S.op("dve", lambda: A_.reciprocal(out=ep, in_=ep), [Bep], [Bep])
            S.op("dve", lambda: A_.tensor_tensor(out=of[0], in0=of[0], in1=ep, op=ALU.mult), [Bof[0], Bep], [Bof[0]])
            S.op("act", lambda: nc.scalar.activation(out=graw, in_=graw, func=AF.Silu), [Bgr], [Bgr])
            S.op("dve", lambda: A_.scalar_tensor_tensor(out=ogb, in0=of[0], scalar=self.pv(f"hg_norm{jh}", 0), in1=graw, op0=ALU.mult, op1=ALU.mult), [Bof[0], Bgr], [Bog])
            S.dma("pool", og[rows, cols], ogb, reads=[Bog])
    st.close()


def _mixer(self, l, cur, nxt):
    kind, j = l % 3, l // 3
    last = (l == DEPTH - 1)
    og = self.scr("og", [D, TT], BF16)
    if kind == 0:
        Pfm = self.scr("hgP", [5 * D, TT])
        Itm = self.scr("hgI", [TT, D], BF16)
        self.inproj_stage(l, cur, self.W["hg_w_in"][j], 5 * D, Pfm, [(D, D, Itm)])
        self.hgrn2_scan(j, Pfm, Itm, og)
        self.outproj_stage(l, og, self.W["hg_w_o"][j], cur, nxt, last)
    elif kind == 1:
        self.rwkv_mixer(l, cur, og)
        self.outproj_stage(l, og, self.W["rw_w_o"][j], cur, nxt, last)
    else:
        self.mla_mixer(l, cur, og)
        self.outproj_stage(l, og, self.W["mla_w_o"][j], cur, nxt, last)


Prog.inproj_stage = _inproj_stage
Prog.outproj_stage = _outproj_stage
Prog.hgrn2_scan = _hgrn2_scan
Prog.mixer = _mixer


def _mla_mixer(self, l, xin, og):
    nc, S = self.nc, self.S
    A_ = nc.vector
    NH = 16
    QN = self.scr("mlaQN", [96, NH, TT], BF16)
    KN = self.scr("mlaKN", [96, NH, TT], BF16)
    VT = self.scr("mlaVT", [TT, D], BF16)
    st = Stage(self, "m1")
    Wd = st.sb("wd", [128, 8, 544], BF16)
    Wq = st.sb("wq", [128, 2, 1536], BF16)
    Wk = st.sb("wk", [128, 2, 2048], BF16)
    Wdr = st.sb("wdr", [128, 8, 32], BF16)
    Wqr = st.sb("wqr", [128, 2, NH, 32], BF16)
    BWd, BWq, BWk, BWr = Buf(), Buf(), Buf(), Buf()
    S.dma("pool", Wd, self.W["mla_w_dqkv"][0].rearrange("(kc p) n -> p kc n", p=128), writes=[BWd])
    wqv = self.W["mla_w_uq"][0].rearrange("(kc p) n -> p kc n", p=128)
    for i3 in range(3):
        S.dma("pool", Wq[:, :, i3 * 512:(i3 + 1) * 512], wqv[:, :, i3 * 512:(i3 + 1) * 512], writes=[BWq])
    wkv = self.W["mla_w_ukv"][0].rearrange("(kc p) n -> p kc n", p=128)
    for i4 in range(4):
        S.dma("pool", Wk[:, :, i4 * 512:(i4 + 1) * 512], wkv[:, :, i4 * 512:(i4 + 1) * 512], writes=[BWk])
    Wq4 = Wq.rearrange("p k (h c) -> p k h c", c=96)
    for seg in range(2):
        for half in range(2):
            sgn = -1.0 if half == 0 else 1.0
            so = 64 + seg * 16 + (1 - half) * 8
            do = seg * 16 + half * 8
            S.op("act", lambda: nc.scalar.activation(out=Wqr[:, :, :, do:do + 8], in_=Wq4[:, :, :, so:so + 8], func=AF.Copy, scale=sgn), [BWq], [BWr])
            so2 = 512 + seg * 16 + (1 - half) * 8
            S.op("act", lambda: nc.scalar.activation(out=Wdr[:, :, do:do + 8], in_=Wd[:, :, so2:so2 + 8], func=AF.Copy, scale=sgn), [BWd], [BWr])
    cos = st.sb("cos", [96, T]); sin = st.sb("sin", [96, T])
    Bcs = Buf()
    RP = slice(64, 96)
    S.dma("sp", cos[RP, :], self.cd["rope_cos"], writes=[Bcs])
    S.dma("sp", sin[RP, :], self.cd["rope_sin"], writes=[Bcs])
    xs = [st.sb(f"xs{i}", [128, 8, BLK]) for i in range(2)]
    hb = [st.sb(f"hb{i}", [128, 8, BLK], BF16) for i in range(2)]
    Bxs, Bhb = [[Buf(), Buf()] for _ in range(2)]
    nt = self.norm_tiles(st, BLK)
    cs_ = st.sb("cs", [128, 4, BLK]); csq = st.sb("csq", [128, 4, BLK], BF16); cn = st.sb("cn", [128, 4, BLK], BF16)
    rr0 = st.sb("rr0", [128, 2, BLK]); rr1 = st.sb("rr1", [128, 2, BLK]); ctmp = st.sb("ctmp", [128, 4, BLK])
    Bcs_, Bcsq, Bcn, Brr, Bct = [Buf() for _ in range(5)]
    qn_s = [st.sb(f"qns{i}", [96, NH, BLK], BF16) for i in range(2)]
    kn_s = [st.sb(f"kns{i}", [96, NH, BLK], BF16) for i in range(2)]
    vt_s = [st.sb(f"vts{i}", [128, D], BF16) for i in range(2)]
    t1 = st.sb("t1", [96, 2, BLK]); t2 = st.sb("t2", [96, 2, BLK])
    Bt1, Bt2 = Buf(), Buf()
    Bqn, Bkn, Bqr, Bkr, Bvt = [[Buf(), Buf()] for _ in range(5)]
    xiv = xin.rearrange("(c p) t -> p c t", p=128)
    blocks = self.blocks(False)

    def load(n):
        b, k = blocks[n]
        S.dma("sp", xs[n % 2], xiv[:, :, b * T + k * BLK:b * T + (k + 1) * BLK], writes=[Bxs[n % 2]])

    load(0)
    vti = 0
    for n, (b, k) in enumerate(blocks):
        i = n % 2
        if n + 1 < len(blocks):
            load(n + 1)
        j = 2 if k == 0 else b
        A, sh, _ = self.mod_ab(l, 0, j)
        self.norm_block(nt, xs[i], Bxs[i], BLK, A, sh, hb[i], Bhb[i], 6)
        col = b * T + k * BLK
        tcol = slice(k * BLK, (k + 1) * BLK)
        for c4 in range(4):
            pb = c4 // 2
            for kc in range(8):
                S.op("pe", lambda: nc.tensor.matmul(self.PS[pb][:, (c4 % 2) * BLK:(c4 % 2 + 1) * BLK], lhsT=Wd[:, kc, c4 * 128:(c4 + 1) * 128], rhs=hb[i][:, kc, :], start=(kc == 0), stop=(kc == 7)),
                     [BWd, Bhb[i]], [self.BPS[pb]])
        for kc in range(8):
            S.op("pe", lambda: nc.tensor.matmul(self.PS[2][RP, 0:BLK], lhsT=Wd[:, kc, 512:544], rhs=hb[i][:, kc, :], start=(kc == 0), stop=(kc == 7)), [BWd, Bhb[i]], [self.BPS[2]])
        for kc in range(8):
            S.op("pe", lambda: nc.tensor.matmul(self.PS[2][RP, BLK:2 * BLK], lhsT=Wdr[:, kc, :], rhs=hb[i][:, kc, :], start=(kc == 0), stop=(kc == 7)), [BWr, Bhb[i]], [self.BPS[2]])
        for pb in range(2):
            S.op("act", lambda: nc.scalar.copy(out=cs_[:, 2 * pb:2 * pb + 2, :], in_=self.PS[pb].rearrange("p (c t) -> p c t", c=2)), [self.BPS[pb]], [Bcs_])
            S.op("act", lambda: nc.scalar.activation(out=csq[:, 2 * pb:2 * pb + 2, :], in_=self.PS[pb].rearrange("p (c t) -> p c t", c=2), func=AF.Square), [self.BPS[pb]], [Bcsq])
        S.op("dve", lambda: A_.tensor_tensor(out=t1[RP, 0, :], in0=self.PS[2][RP, 0:BLK], in1=cos[RP, tcol], op=ALU.mult), [self.BPS[2], Bcs], [Bt1])
        S.op("dve", lambda: A_.tensor_tensor(out=t2[RP, 0, :], in0=self.PS[2][RP, BLK:2 * BLK], in1=sin[RP, tcol], op=ALU.mult), [self.BPS[2], Bcs], [Bt2])
        S.op("dve", lambda: A_.tensor_tensor(out=kn_s[i][RP, :, :], in0=t1[RP, 0:1, :].to_broadcast([32, NH, BLK]), in1=t2[RP, 0:1, :].to_broadcast([32, NH, BLK]), op=ALU.add), [Bt1, Bt2], [Bkn[i]])
        for w in range(2):
            for c in range(2):
                S.op("pe", lambda: nc.tensor.matmul(self.PS[3][:, w * BLK:(w + 1) * BLK], lhsT=self.onesb, rhs=csq[:, 2 * w + c, :], start=(c == 0), stop=(c == 1)), [Bcsq], [self.BPS[3]])
        S.op("act", lambda: nc.scalar.activation(out=rr0, in_=self.PS[3].rearrange("p (w t) -> p w t", w=2), func=AF.Sqrt, scale=1.0 / 256, bias=self.epsD), [self.BPS[3]], [Brr])
        S.op("dve", lambda: A_.reciprocal(out=rr1, in_=rr0), [Brr], [Brr])
        S.op("dve", lambda: A_.tensor_tensor(out=ctmp.rearrange("p (w c) t -> p w c t", w=2), in0=cs_.rearrange("p (w c) t -> p w c t", w=2),
                                              in1=rr1.unsqueeze(2).to_broadcast([128, 2, 2, BLK]), op=ALU.mult), [Bcs_, Brr], [Bct])
        for c4 in range(4):
            gname = "mla_q_norm" if c4 < 2 else "mla_kv_norm"
            S.op("act", lambda: nc.scalar.activation(out=cn[:, c4, :], in_=ctmp[:, c4, :], func=AF.Identity, scale=self.pv(gname, c4 % 2)), [Bct], [Bcn])
        for hp in range(8):
            for which in range(2):
                pb = 4 + (2 * hp + which) % 2
                Wt_, coff, hw, ci = (Wq, 0, 96, 0) if which == 0 else (Wk, 0, 128, 2)
                for hh in range(2):
                    h = 2 * hp + hh
                    for kc in range(2):
                        S.op("pe", lambda: nc.tensor.matmul(self.PS[pb][0:64, hh * BLK:(hh + 1) * BLK], lhsT=Wt_[:, kc, h * hw:h * hw + 64], rhs=cn[:, ci + kc, :], start=(kc == 0), stop=(kc == 1)),
                             [BWq if which == 0 else BWk, Bcn], [self.BPS[pb]])
                dst = qn_s[i] if which == 0 else kn_s[i]
                Bd = Bqn[i] if which == 0 else Bkn[i]
                if which == 0:
                    S.op("act", lambda: nc.scalar.copy(out=dst[0:64, 2 * hp:2 * hp + 2, :], in_=self.PS[pb][0:64, :].rearrange("p (h t) -> p h t", h=2)), [self.BPS[pb]], [Bd])
                else:
                    S.op("dve", lambda: A_.tensor_copy(out=dst[0:64, 2 * hp:2 * hp + 2, :], in_=self.PS[pb][0:64, :].rearrange("p (h t) -> p h t", h=2)), [self.BPS[pb]], [Bd])
            for hh in range(2):
                h = 2 * hp + hh
                for kc in range(2):
                    S.op("pe", lambda: nc.tensor.matmul(self.PS[6][RP, hh * BLK:(hh + 1) * BLK], lhsT=Wq[:, kc, h * 96 + 64:h * 96 + 96], rhs=cn[:, kc, :], start=(kc == 0), stop=(kc == 1)), [BWq, Bcn], [self.BPS[6]])
                for kc in range(2):
                    S.op("pe", lambda: nc.tensor.matmul(self.PS[7][RP, hh * BLK:(hh + 1) * BLK], lhsT=Wqr[:, kc, h, :], rhs=cn[:, kc, :], start=(kc == 0), stop=(kc == 1)), [BWr, Bcn], [self.BPS[7]])
            cosb = cos[RP, tcol].unsqueeze(1).to_broadcast([32, 2, BLK])
            sinb = sin[RP, tcol].unsqueeze(1).to_broadcast([32, 2, BLK])
            S.op("dve", lambda: A_.tensor_tensor(out=t1[RP, :, :], in0=self.PS[6][RP, :].rearrange("p (h t) -> p h t", h=2), in1=cosb, op=ALU.mult), [self.BPS[6], Bcs], [Bt1])
            S.op("dve", lambda: A_.tensor_tensor(out=t2[RP, :, :], in0=self.PS[7][RP, :].rearrange("p (h t) -> p h t", h=2), in1=sinb, op=ALU.mult), [self.BPS[7], Bcs], [Bt2])
            S.op("dve", lambda: A_.tensor_tensor(out=qn_s[i][RP, 2 * hp:2 * hp + 2, :], in0=t1[RP, :, :], in1=t2[RP, :, :], op=ALU.add), [Bt1, Bt2], [Bqn[i]])
        S.dma("pool", QN[:, :, col:col + BLK], qn_s[i], reads=[Bqn[i]])
        S.dma("pool", KN[:, :, col:col + BLK], kn_s[i], reads=[Bkn[i]])
        Wkv = Wk.rearrange("p k (h c) -> p k h c", c=128)
        for tt in range(BLK // 128):
            vi = vti % 2
            vti += 1
            for hf in range(2):
                pb = 4 + hf
                for kc in range(2):
                    S.op("pe", lambda: nc.tensor.matmul(self.PS[pb][:, 0:512], lhsT=cn[:, 2 + kc, tt * 128:(tt + 1) * 128], rhs=Wkv[:, kc, hf * 8:(hf + 1) * 8, 64:128], start=(kc == 0), stop=(kc == 1)),
                         [BWk, Bcn], [self.BPS[pb]])
                S.op("act", lambda: nc.scalar.copy(out=vt_s[vi][:, hf * 512:(hf + 1) * 512], in_=self.PS[pb][:, 0:512]), [self.BPS[pb]], [Bvt[vi]])
            S.dma("pool", VT[col + tt * 128:col + (tt + 1) * 128, :], vt_s[vi], reads=[Bvt[vi]])
    st.close()
    st = Stage(self, "m2")
    NKT = T // 128
    Vall = st.sb("Vall", [128, NKT, D], BF16)
    KNh = [st.sb(f"KNh{i}", [96, T], BF16) for i in range(2)]
    QNh = [st.sb(f"QNh{i}", [96, T], BF16) for i in range(2)]
    VX = [st.sb(f"VX{i}", [128, NKT, 65], BF16) for i in range(2)]
    PT = [st.sb(f"PT{i}", [128, 512], BF16) for i in range(3)]
    rd = st.sb("rd", [65, 512]); rb = [st.sb(f"rb{i}", [64, 512]) for i in range(2)]
    ob = [st.sb(f"ob{i}", [64, 512], BF16) for i in range(2)]
    BVa, BKR, Brd = Buf(), Buf(), Buf()
    BKN, BQN, BQR, BVX, Brb, Bob = [[Buf(), Buf()] for _ in range(6)]
    BPT = [Buf() for _ in range(3)]
    for i in range(2):
        S.op("pool", lambda: nc.gpsimd.memset(VX[i], 1.0), [], [BVX[i]])
    qblocks = [(0, TC, 2)] + [(TC + qb * 512, 512, NKT) for qb in range(4)]
    pti = 0
    hn = 0
    for b in range(NB):
        c0 = b * T
        S.dma("sp", Vall, VT[c0:c0 + T, :].rearrange("(kt p) v -> p kt v", p=128), writes=[BVa])
        for h in range(NH):
            i = hn % 2
            hn += 1
            S.dma("sp", KNh[i], KN[:, h, c0:c0 + T], writes=[BKN[i]])
            S.dma("sp", QNh[i], QN[:, h, c0:c0 + T], writes=[BQN[i]])
            S.op("pool", lambda: nc.gpsimd.tensor_copy(out=VX[i][:, :, 0:64], in_=Vall[:, :, h * 64:(h + 1) * 64]), [BVa], [BVX[i]])
            for qi, (q0, nq, nkt) in enumerate(qblocks):
                po = 4 + (qi % 2)

                def score(kt):
                    ps = kt % 4
                    ks = slice(kt * 128, (kt + 1) * 128)
                    S.op("pe", lambda: nc.tensor.matmul(self.PS[ps][:, 0:nq], lhsT=KNh[i][:, ks], rhs=QNh[i][:, q0:q0 + nq], start=True, stop=True), [BKN[i], BQN[i]], [self.BPS[ps]])

                score(0)
                if nkt > 1:
                    score(1)
                for kt in range(nkt):
                    ps = kt % 4
                    p3 = pti % 3
                    pti += 1
                    if kt + 2 < nkt:
                        score(kt + 2)
                    S.op("act", lambda: nc.scalar.activation(out=PT[p3][:, 0:nq], in_=self.PS[ps][:, 0:nq], func=AF.Exp, scale=MLA_SCALE), [self.BPS[ps]], [BPT[p3]])
                    S.op("pe", lambda: nc.tensor.matmul(self.PS[po][0:65, 0:nq], lhsT=VX[i][:, kt, :], rhs=PT[p3][:, 0:nq], start=(kt == 0), stop=(kt == nkt - 1)), [BVX[i], BPT[p3]], [self.BPS[po]])
                r2 = qi % 2
                S.op("dve", lambda: A_.reciprocal(out=rd[64:65, 0:nq], in_=self.PS[po][64:65, 0:nq]), [self.BPS[po]], [Brd])
                S.op("pe", lambda: nc.tensor.matmul(self.PS[6 + r2][0:64, 0:nq], lhsT=self.onesf[64:65, 0:64], rhs=rd[64:65, 0:nq], start=True, stop=True), [Brd], [self.BPS[6 + r2]])
                S.op("dve", lambda: A_.tensor_copy(out=rb[r2][:, 0:nq], in_=self.PS[6 + r2][0:64, 0:nq]), [self.BPS[6 + r2]], [Brb[r2]])
                S.op("dve", lambda: A_.tensor_tensor(out=ob[r2][:, 0:nq], in0=self.PS[po][0:64, 0:nq], in1=rb[r2][:, 0:nq], op=ALU.mult), [self.BPS[po], Brb[r2]], [Bob[r2]])
                S.dma("pool", og[h * 64:(h + 1) * 64, c0 + q0:c0 + q0 + nq], ob[r2][:, 0:nq], reads=[Bob[r2]])
    st.close()


Prog.mla_mixer = _mla_mixer


RW_ARR = ["r", "kt0", "kt1", "be0", "be1", "kap", "lw0", "lw1", "v", "g"]


def _rwkv_proj(self, l, xin, RWP, Vtm):
    nc, S = self.nc, self.S
    A_ = nc.vector
    st = Stage(self, "r1")
    Wrkv = st.sb("wrkv", [128, 8, 3 * D], BF16)
    BWrkv = []
    for i3 in range(3):
        v_ = self.W["rw_w_rkv"][0, i3].rearrange("(kc p) n -> p kc n", p=128)
        for hf in range(2):
            bb_ = Buf()
            S.dma("pool", Wrkv[:, :, i3 * D + hf * 512:i3 * D + (hf + 1) * 512], v_[:, :, hf * 512:(hf + 1) * 512], writes=[bb_])
            BWrkv.append(bb_)
    W1 = st.sb("w1", [128, 8, 2, 64], BF16); A1 = st.sb("a1", [128, 8, 2, 64], BF16); G1 = st.sb("g1", [128, 8, 160], BF16)
    W2 = st.sb("w2", [64, 2, D], BF16); A2 = st.sb("a2", [64, 2, D], BF16); G2a = st.sb("g2a", [128, D], BF16); G2b = st.sb("g2b", [32, D], BF16)
    Bsw = Buf()
    for d in range(2):
        S.dma("pool", W1[:, :, d, :], self.W["rw_w1"][0, d].rearrange("(kc p) n -> p kc n", p=128), writes=[Bsw])
        S.dma("pool", A1[:, :, d, :], self.W["rw_a1"][0, d].rearrange("(kc p) n -> p kc n", p=128), writes=[Bsw])
        S.dma("pool", W2[:, d, :], self.W["rw_w2"][0, d], writes=[Bsw])
        S.dma("pool", A2[:, d, :], self.W["rw_a2"][0, d], writes=[Bsw])
    S.dma("pool", G1, self.W["rw_g1"][0].rearrange("(kc p) n -> p kc n", p=128), writes=[Bsw])
    S.dma("pool", G2a, self.W["rw_g2"][0, 0:128, :], writes=[Bsw])
    S.dma("pool", G2b, self.W["rw_g2"][0, 128:160, :], writes=[Bsw])
    NH_ = BLK + 2
    xs = [st.sb(f"xs{i}", [128, 8, NH_]) for i in range(2)]
    hf_ = st.sb("hf", [128, 8, NH_])
    dx = st.sb("dx", [128, 8, BLK])
    xj = [st.sb(f"xj{j}", [128, 8, BLK], BF16) for j in range(6)]
    Bxs = [Buf(), Buf()]
    Bhf, Bdx = Buf(), Buf()
    Bxj = [Buf() for _ in range(6)]
    nt = self.norm_tiles(st)
    lt = st.sb("lt", [64, 5, BLK], BF16)
    gh = st.sb("gh", [128, BLK], BF16)
    Blt = Buf()
    stg = [st.sb(f"stg{i}", [128, 10, BLK]) for i in range(2)]
    Bstg = [Buf(), Buf()]
    tmp = [st.sb(f"tmp{i}", [128, BLK]) for i in range(6)]
    Btmp = [Buf() for _ in range(6)]
    sqb = st.sb("sqb", [128, BLK], BF16)
    Bsqb = Buf()
    vts = [st.sb(f"vts{i}", [128, D], BF16) for i in range(2)]
    Bvts = [Buf(), Buf()]
    for i in range(2):
        S.op("dve", lambda: A_.memset(xs[i], 0.0), [], [Bxs[i]])
    xiv = xin.rearrange("(c p) t -> p c t", p=128)
    blocks = self.blocks(False)
    blk64b = st.sb("blk64b", [128, 128], BF16)
    Bb64 = Buf()
    S.op("dve", lambda: A_.tensor_copy(out=blk64b, in_=self.blk64), [], [Bb64])

    def load(n):
        b, k = blocks[n]
        t0, lo, hi, _, _ = self.blk_range(k)
        S.dma("sp", xs[n % 2][:, :, lo - (t0 - 1):hi - (t0 - 1)], xiv[:, :, b * T + lo:b * T + hi], writes=[Bxs[n % 2]])

    load(0)
    si = 0
    vi_ = 0
    pbk = 0
    for n, (b, k) in enumerate(blocks):
        i = n % 2
        if n + 1 < len(blocks):
            load(n + 1)
        t0, lo, hi, first, last = self.blk_range(k)
        j = 2 if k == 0 else b
        A, sh, _ = self.mod_ab(l, 0, j)
        self.norm_block(nt, xs[i], Bxs[i], NH_, A, sh, hf_, Bhf, 6)
        if first:
            S.op("dve", lambda: A_.memset(hf_[:, :, 0:1], 0.0), [], [Bhf])
        if last:
            S.op("dve", lambda: A_.memset(hf_[:, :, NH_ - 1:NH_], 0.0), [], [Bhf])
        S.op("dve", lambda: A_.tensor_tensor(out=dx, in0=hf_[:, :, 0:BLK], in1=hf_[:, :, 2:2 + BLK], op=ALU.add), [Bhf], [Bdx])
        S.op("dve", lambda: A_.scalar_tensor_tensor(out=dx, in0=dx, scalar=0.5, in1=hf_[:, :, 1:1 + BLK], op0=ALU.mult, op1=ALU.subtract), [Bhf, Bdx], [Bdx])
        for jj in range(6):
            for c in range(8):
                S.op("dve", lambda: A_.scalar_tensor_tensor(out=xj[jj][:, c, :], in0=dx[:, c, :], scalar=self.pv(f"rw_mu{jj}", c), in1=hf_[:, c, 1:1 + BLK], op0=ALU.mult, op1=ALU.add),
                     [Bdx, Bhf], [Bxj[jj]])
        for d in range(2):
            for kc in range(8):
                S.op("pe", lambda: nc.tensor.matmul(self.PS[5][0:64, d * BLK:(d + 1) * BLK], lhsT=W1[:, kc, d, :], rhs=xj[1][:, kc, :], start=(kc == 0), stop=(kc == 7)), [Bsw, Bxj[1]], [self.BPS[5]])
        S.op("act", lambda: nc.scalar.activation(out=lt[:, 0:2, :], in_=self.PS[5][0:64, :].rearrange("p (d t) -> p d t", d=2), func=AF.Tanh), [self.BPS[5]], [Blt])
        for d in range(2):
            for kc in range(8):
                S.op("pe", lambda: nc.tensor.matmul(self.PS[5][0:64, d * BLK:(d + 1) * BLK], lhsT=A1[:, kc, d, :], rhs=xj[4][:, kc, :], start=(kc == 0), stop=(kc == 7)), [Bsw, Bxj[4]], [self.BPS[5]])
        S.op("act", lambda: nc.scalar.copy(out=lt[:, 2:4, :], in_=self.PS[5][0:64, :].rearrange("p (d t) -> p d t", d=2)), [self.BPS[5]], [Blt])
        for kc in range(8):
            S.op("pe", lambda: nc.tensor.matmul(self.PS[5][:, 0:BLK], lhsT=G1[:, kc, 0:128], rhs=xj[5][:, kc, :], start=(kc == 0), stop=(kc == 7)), [Bsw, Bxj[5]], [self.BPS[5]])
        for kc in range(8):
            S.op("pe", lambda: nc.tensor.matmul(self.PS[5][0:32, BLK:2 * BLK], lhsT=G1[:, kc, 128:160], rhs=xj[5][:, kc, :], start=(kc == 0), stop=(kc == 7)), [Bsw, Bxj[5]], [self.BPS[5]])
        S.op("act", lambda: nc.scalar.activation(out=gh, in_=self.PS[5][:, 0:BLK], func=AF.Sigmoid), [self.BPS[5]], [Blt])
        S.op("act", lambda: nc.scalar.activation(out=lt[0:32, 4, :], in_=self.PS[5][0:32, BLK:2 * BLK], func=AF.Sigmoid), [self.BPS[5]], [Blt])
        col = b * T + t0
        for c in range(8):
            s_ = si % 2
            si += 1
            sg_ = stg[s_]
            Bs = Bstg[s_]
            cs = slice(c * 128, (c + 1) * 128)

            def bank():
                nonlocal pbk
                pbk += 1
                return pbk % 5

            prk = []
            for which, xsrc in ((0, 0), (1, 2), (2, 3)):
                pb = bank()
                for kc in range(8):
                    S.op("pe", lambda: nc.tensor.matmul(self.PS[pb][:, 0:BLK], lhsT=Wrkv[:, kc, which * D + c * 128:which * D + (c + 1) * 128], rhs=xj[xsrc][:, kc, :], start=(kc == 0), stop=(kc == 7)),
                         [BWrkv[which * 2 + (c // 4)], Bxj[xsrc]], [self.BPS[pb]])
                prk.append(pb)
            S.op("act", lambda: nc.scalar.copy(out=sg_[:, 0, :], in_=self.PS[prk[0]][:, 0:BLK]), [self.BPS[prk[0]]], [Bs])
            S.op("act", lambda: nc.scalar.copy(out=sg_[:, 8, :], in_=self.PS[prk[2]][:, 0:BLK]), [self.BPS[prk[2]]], [Bs])
            kraw = tmp[0]
            S.op("act", lambda: nc.scalar.copy(out=kraw, in_=self.PS[prk[1]][:, 0:BLK]), [self.BPS[prk[1]]], [Btmp[0]])
            S.op("dve", lambda: A_.tensor_scalar(out=tmp[1], in0=kraw, scalar1=self.pv("rw_k_k", c), scalar2=None, op0=ALU.mult), [Btmp[0]], [Btmp[1]])
            S.op("act", lambda: nc.scalar.activation(out=sqb, in_=tmp[1], func=AF.Square), [Btmp[1]], [Bsqb])
            pb = bank()
            S.op("pe", lambda: nc.tensor.matmul(self.PS[pb][:, 0:BLK], lhsT=blk64b, rhs=sqb, start=True, stop=True), [Bsqb, Bb64], [self.BPS[pb]])
            S.op("act", lambda: nc.scalar.activation(out=tmp[2], in_=self.PS[pb][:, 0:BLK], func=AF.Sqrt), [self.BPS[pb]], [Btmp[2]])
            S.op("dve", lambda: A_.tensor_scalar(out=tmp[2], in0=tmp[2], scalar1=1e-12, scalar2=None, op0=ALU.max), [Btmp[2]], [Btmp[2]])
            S.op("dve", lambda: A_.reciprocal(out=tmp[2], in_=tmp[2]), [Btmp[2]], [Btmp[2]])
            S.op("dve", lambda: A_.tensor_tensor(out=sg_[:, 5, :], in0=tmp[1], in1=tmp[2], op=ALU.mult), [Btmp[1], Btmp[2]], [Bs])
            pb = bank()
            S.op("pe", lambda: nc.tensor.matmul(self.PS[pb][:, 0:BLK], lhsT=G2a[:, cs], rhs=gh, start=True, stop=False), [Bsw, Blt], [self.BPS[pb]])
            S.op("pe", lambda: nc.tensor.matmul(self.PS[pb][:, 0:BLK], lhsT=G2b[:, cs], rhs=lt[0:32, 4, :], start=False, stop=True), [Bsw, Blt], [self.BPS[pb]])
            S.op("act", lambda: nc.scalar.copy(out=sg_[:, 9, :], in_=self.PS[pb][:, 0:BLK]), [self.BPS[pb]], [Bs])
            for d in range(2):
                pb = bank()
                S.op("pe", lambda: nc.tensor.matmul(self.PS[pb][:, 0:BLK], lhsT=W2[:, d, cs], rhs=lt[:, d, :], start=True, stop=True), [Bsw, Blt], [self.BPS[pb]])
                S.op("act", lambda: nc.scalar.activation(out=tmp[3], in_=self.PS[pb][:, 0:BLK], func=AF.Sigmoid, bias=self.pv(f"rw_w0_{d}", c)), [self.BPS[pb]], [Btmp[3]])
                S.op("dve", lambda: A_.tensor_scalar(out=sg_[:, 6 + d, :], in0=tmp[3], scalar1=-float(np.exp(-0.5)), scalar2=None, op0=ALU.mult), [Btmp[3]], [Bs])
                pb = bank()
                S.op("pe", lambda: nc.tensor.matmul(self.PS[pb][:, 0:BLK], lhsT=A2[:, d, cs], rhs=lt[:, 2 + d, :], start=True, stop=True), [Bsw, Blt], [self.BPS[pb]])
                S.op("act", lambda: nc.scalar.activation(out=tmp[4], in_=self.PS[pb][:, 0:BLK], func=AF.Sigmoid, bias=self.pv(f"rw_a0_{d}", c)), [self.BPS[pb]], [Btmp[4]])
                S.op("dve", lambda: A_.tensor_tensor(out=sg_[:, 3 + d, :], in0=tmp[4], in1=sg_[:, 5, :], op=ALU.mult), [Btmp[4], Bs], [Bs])
                S.op("dve", lambda: A_.tensor_scalar(out=tmp[5], in0=tmp[4], scalar1=-1.0, scalar2=None, op0=ALU.add), [Btmp[4]], [Btmp[5]])
                S.op("dve", lambda: A_.tensor_scalar(out=tmp[5], in0=tmp[5], scalar1=self.pv("rw_k_a", c), scalar2=1.0, op0=ALU.mult, op1=ALU.add), [Btmp[5]], [Btmp[5]])
                S.op("dve", lambda: A_.tensor_tensor(out=sg_[:, 1 + d, :], in0=tmp[5], in1=kraw, op=ALU.mult), [Btmp[5], Btmp[0]], [Bs])
            S.dma("pool", RWP[:, c * 128:(c + 1) * 128, col:col + BLK].rearrange("a p t -> p a t"), sg_, reads=[Bs])
        for tt in range(BLK // 128):
            vi = vi_ % 2
            vi_ += 1
            for hfv in range(2):
                pb = 4 - hfv
                for kc in range(8):
                    S.op("pe", lambda: nc.tensor.matmul(self.PS[pb][:, 0:512], lhsT=xj[3][:, kc, tt * 128:(tt + 1) * 128], rhs=Wrkv[:, kc, 2 * D + hfv * 512:2 * D + (hfv + 1) * 512], start=(kc == 0), stop=(kc == 7)),
                         [BWrkv[4 + hfv], Bxj[3]], [self.BPS[pb]])
                S.op("act", lambda: nc.scalar.copy(out=vts[vi][:, hfv * 512:(hfv + 1) * 512], in_=self.PS[pb][:, 0:512]), [self.BPS[pb]], [Bvts[vi]])
            S.dma("pool", Vtm[col + tt * 128:col + (tt + 1) * 128, :], vts[vi], reads=[Bvts[vi]])
    st.close()


def _rwkv_mixer(self, l, xin, og):
    RWP = self.scr("rwP", [10, D, TT])
    Vtm = self.scr("rwV", [TT, D], BF16)
    self.rwkv_proj(l, xin, RWP, Vtm)
    if getattr(self, "rw_stop", 0) == 1:
        return
    self.rwkv_scan(RWP, Vtm, og)


Prog.rwkv_proj = _rwkv_proj
Prog.rwkv_mixer = _rwkv_mixer


def _rwkv_scan(self, RWP, Vtm, og):
    nc, S = self.nc, self.S
    A_ = nc.vector
    U32 = mybir.dt.uint32
    RWD = self.scr("rwD", [NB, 8, 2, 2, 128, NCH * 128], BF16)
    RWS = self.scr("rwS", [NB, 8, 2, 128, 3 * NCH])
    skipA = getattr(self, "rw_skipA", False)
    st = Stage(self, "r2a")
    smask = st.sb("smask", [128, T])
    Bsm = Buf()
    S.dma("sp", smask, self.cd["scanmask"], writes=[Bsm])
    lw = st.sb("lw", [128, T]); kap = st.sb("kap", [128, T]); rr = st.sb("r", [128, T]); kt = st.sb("kt", [128, T]); be = st.sb("be", [128, T])
    cw = st.sb("cw", [128, T]); cm = st.sb("cm", [128, T]); en = st.sb("en", [128, T]); ex = st.sb("ex", [128, T])
    ABt = [st.sb(f"AB{i}", [128, NCH, 2, CH], BF16) for i in range(2)]
    KBt_ = [st.sb(f"KB{i}", [128, NCH, 2, CH], BF16) for i in range(2)]
    SC = [st.sb(f"SC{i}", [128, 3, NCH]) for i in range(2)]
    Blw, Bkap, Br, Bkt, Bbe, Bcw, Bcm, Ben, Bex = [Buf() for _ in range(9)]
    BAB, BKB, BSC = [[Buf(), Buf()] for _ in range(3)]
    it = 0
    v3 = lambda t_: t_.rearrange("p (c s) -> p c s", s=CH)
    for b in range(0 if skipA else NB):
        cols = slice(b * T, (b + 1) * T)
        for p in range(8):
            rows = slice(p * 128, (p + 1) * 128)
            for d in range(2):
                i = it % 2
                it += 1
                S.dma("sp", lw, RWP[6 + d, rows, cols], writes=[Blw])
                S.dma("sp", kap, RWP[5, rows, cols], writes=[Bkap])
                S.dma("sp", rr, RWP[0, rows, cols], writes=[Br])
                S.dma("sp", kt, RWP[1 + d, rows, cols], writes=[Bkt])
                S.dma("sp", be, RWP[3 + d, rows, cols], writes=[Bbe])
                S.op("dve", lambda: A_.tensor_tensor_scan(out=cw, data0=smask, data1=lw, initial=0.0, op0=ALU.mult, op1=ALU.add), [Bsm, Blw], [Bcw])
                if d == 1:
                    S.op("dve", lambda: A_.tensor_tensor(out=cm, in0=lw, in1=cw, op=ALU.subtract), [Blw, Bcw], [Bcm])
                    S.op("dve", lambda: A_.tensor_tensor(out=v3(en), in0=v3(cm), in1=v3(cw)[:, :, CH - 1:CH].to_broadcast([128, NCH, CH]), op=ALU.add), [Bcm, Bcw], [Ben])
                    S.op("act", lambda: nc.scalar.copy(out=cw, in_=en), [Ben], [Bcw])
                m_idx = 32 if d == 0 else 31
                e_idx = CH - 1 if d == 0 else 0
                c3 = v3(cw)
                S.op("act", lambda: nc.scalar.activation(out=SC[i][:, 0, :], in_=c3[:, :, m_idx], func=AF.Exp), [Bcw], [BSC[i]])
                S.op("act", lambda: nc.scalar.activation(out=SC[i][:, 1, :], in_=c3[:, :, e_idx], func=AF.Exp), [Bcw], [BSC[i]])
                S.op("dve", lambda: A_.tensor_tensor(out=SC[i][:, 2, :], in0=c3[:, :, e_idx], in1=c3[:, :, m_idx], op=ALU.subtract), [Bcw], [BSC[i]])
                S.op("act", lambda: nc.scalar.activation(out=SC[i][:, 2, :], in_=SC[i][:, 2, :], func=AF.Exp), [BSC[i]], [BSC[i]])
                S.dma("pool", RWS[b, p, d], SC[i].rearrange("p a c -> p (a c)"), reads=[BSC[i]])
                S.op("dve", lambda: A_.tensor_tensor(out=v3(cm), in0=c3, in1=c3[:, :, m_idx:m_idx + 1].to_broadcast([128, NCH, CH]), op=ALU.subtract), [Bcw], [Bcm])
                S.op("act", lambda: nc.scalar.activation(out=en, in_=cm, func=AF.Exp, scale=-1.0), [Bcm], [Ben])
                S.op("dve", lambda: A_.tensor_tensor(out=ex, in0=cm, in1=lw, op=ALU.subtract), [Bcm, Blw], [Bex])
                S.op("act", lambda: nc.scalar.activation(out=ex, in_=ex, func=AF.Exp), [Bex], [Bex])
                S.op("act", lambda: nc.scalar.activation(out=cm, in_=cm, func=AF.Exp), [Bcm], [Bcm])
                S.op("dve", lambda: A_.tensor_tensor(out=ABt[i][:, :, 0, :], in0=v3(kap), in1=v3(ex), op=ALU.mult), [Bkap, Bex], [BAB[i]])
                S.op("dve", lambda: A_.tensor_tensor(out=ABt[i][:, :, 1, :], in0=v3(rr), in1=v3(cm), op=ALU.mult), [Br, Bcm], [BAB[i]])
                S.op("dve", lambda: A_.tensor_tensor(out=KBt_[i][:, :, 0, :], in0=v3(kt), in1=v3(en), op=ALU.mult), [Bkt, Ben], [BKB[i]])
                S.op("dve", lambda: A_.tensor_tensor(out=KBt_[i][:, :, 1, :], in0=v3(be), in1=v3(en), op=ALU.mult), [Bbe, Ben], [BKB[i]])
                S.dma("pool", RWD[b, p, d, 0], ABt[i].rearrange("p c a s -> p (c a s)"), reads=[BAB[i]])
                S.dma("pool", RWD[b, p, d, 1], KBt_[i].rearrange("p c a s -> p (c a s)"), reads=[BKB[i]])
    st.close()
    if getattr(self, "rw_stop", 0) == 2:
        return
    st = Stage(self, "r2b")
    S.pe_selfwait = getattr(self, "rw_selfwait", False)
    S.pe_drain = getattr(self, "rw_drain", 2)
    epsLN = st.sb("epsLN", [128, 1])
    Bgl = Buf()
    S.op("dve", lambda: A_.memset(epsLN, RW_LN_EPS), [], [Bgl])
    AB = [st.sb(f"AB{d}", [128, NCH, 128], BF16) for d in range(2)]
    KB = [st.sb(f"KB{d}", [128, NCH, 128], BF16) for d in range(2)]
    SCs = [st.sb(f"SC{d}", [128, 3, NCH]) for d in range(2)]
    Vst = st.sb("Vst", [64, NCH, 128], BF16)
    BABl, BKBl, BSCl = [[Buf(), Buf()] for _ in range(3)]
    BVst = Buf()
    chains = [(hd, d) for hd in range(2) for d in range(2)]
    IDT = BF16 if getattr(self, "rw_inv_bf16", True) else F32
    VU, GGb, AN0, ANp, Xp, Wf, KBtr = {}, {}, {}, {}, {}, {}, {}
    BVU, BGG, BAN0, BANp, BXp, BWf, BKBtr, BST, BS0, BtS, By = [dict() for _ in range(11)]
    for ch in chains:
        nm = f"{ch[0]}{ch[1]}"
        VU[ch] = st.sb("VU" + nm, [128, NCH, CH], BF16)
        GGb[ch] = st.sb("GG" + nm, [128, 128], BF16)
        AN0[ch] = st.sb("AN0" + nm, [128, 128], IDT)
        ANp[ch] = [st.sb(f"ANp{q}" + nm, [128, 128], IDT) for q in range(2)]
        Xp[ch] = [st.sb(f"X{q}" + nm, [128, CH], IDT) for q in range(2)]
        Wf[ch] = st.sb("Wf" + nm, [128, CH], IDT)
        KBtr[ch] = st.sb("KBt" + nm, [128, CH], BF16)
        BVU[ch], BGG[ch], BAN0[ch], BWf[ch], BKBtr[ch], BST[ch], BS0[ch], BtS[ch], By[ch] = [Buf() for _ in range(9)]
        BANp[ch] = [Buf(), Buf()]
        BXp[ch] = [Buf(), Buf()]
        S.op("dve", lambda: A_.memset(GGb[ch], 0.0), [], [BGG[ch]])
        S.op("dve", lambda: A_.memset(AN0[ch], 0.0), [], [BAN0[ch]])
    ST = [st.sb(f"ST{d}", [128, CH]) for d in range(2)]
    S0m = [st.sb(f"S0m{d}", [128, CH], BF16) for d in range(2)]
    tS = [st.sb(f"tS{d}", [128, CH]) for d in range(2)]
    yacc = [st.sb(f"yacc{d}", [128, T]) for d in range(2)]
    rl = st.sb("rl", [128, T]); k0 = st.sb("k0", [128, T]); k1 = st.sb("k1", [128, T]); vf = st.sb("vf", [128, T]); gg = st.sb("gg", [128, T])
    t0_ = st.sb("t0", [128, T]); t1_ = st.sb("t1", [128, T])
    ogb = st.sb("ogb", [128, T], BF16)
    Brl, Bk0, Bk1, Bvf, Bgg, Bt0, Bt1, Bogb = [Buf() for _ in range(8)]
    MERGE = getattr(self, "rw_merge", True)
    if MERGE:
        mKB = [st.sb(f"mKBt{d}", [128, 2, CH], BF16) for d in range(2)]
        mGG = [st.sb(f"mGG{d}", [128, 2, 128], BF16) for d in range(2)]
        mAN0 = [st.sb(f"mAN0{d}", [128, 2, 128], IDT) for d in range(2)]
        mANp = [[st.sb(f"mANp{q}{d}", [128, 2, 128], IDT) for q in range(2)] for d in range(2)]
        mXp = [[st.sb(f"mX{q}{d}", [128, 2, CH], IDT) for q in range(2)] for d in range(2)]
        mWf = [st.sb(f"mWf{d}", [128, 2, CH], IDT) for d in range(2)]
        mVU = [st.sb(f"mVU{d}", [128, NCH, 2, CH], BF16) for d in range(2)]
        M4x2 = [st.sb(f"M4x2{d}", [128, 2, 128]) for d in range(2)]
        mAx2 = [st.sb(f"mAx2{d}", [128, 2, CH]) for d in range(2)]
        mNx2 = [st.sb(f"mNx2{d}", [128, 2, CH]) for d in range(2)]
        I2 = st.sb("I2", [128, 2, CH])
        Bmk = Buf()
        mBKB, mBGG, mBAN0, mBWf, mBVU, mBST, mBS0, mBtS, mBy = [[Buf(), Buf()] for _ in range(9)]
        mBANp = [[Buf(), Buf()], [Buf(), Buf()]]
        mBXp = [[Buf(), Buf()], [Buf(), Buf()]]
        mUB = [[Buf() for _ in range(4)] for d in range(2)]
        for d in range(2):
            S.op("dve", lambda: A_.memset(mGG[d], 0.0), [], [mBGG[d]])
            S.op("dve", lambda: A_.memset(mAN0[d], 0.0), [], [mBAN0[d]])
            for hd in range(2):
                S.op("dve", lambda: A_.tensor_copy(out=M4x2[d][:, hd, :], in_=(self.masks[:, 0:128] if d == 0 else self.masks[:, 128:256])), [], [Bmk])
                S.op("dve", lambda: A_.tensor_copy(out=mAx2[d][:, hd, :], in_=(self.masks[:, 0:64] if d == 0 else self.masks[:, 128:192])), [], [Bmk])
                S.op("dve", lambda: A_.tensor_copy(out=mNx2[d][:, hd, :], in_=(self.masks[:, 128:192] if d == 0 else self.masks[:, 0:64])), [], [Bmk])
        for hd in range(2):
            S.op("dve", lambda: A_.tensor_copy(out=I2[64:128, hd, :], in_=self.ident[64:128, 64:128]), [], [Bmk])
    R = {}
    BR = {}
    for ci, ch in enumerate(chains):
        b0, b1 = self.PS[2 * ci], self.PS[2 * ci + 1]
        R[ch] = dict(GA=b0[:, 0:128], LV=b0[:, 192:320], Wp=b0[:, 384:448],
                     XL=b1[:, 320:384], Up=b1[:, 448:512], Nn=b1[:, 128:192],
                     Yp=b1[:, 0:64], Sd=b1[:, 64:128], TR=b1.bitcast(BF16)[:, 512:576])
        u0, u1, u2, u3 = Buf(), Buf(), Buf(), Buf()
        ykp = [u2] if ch[0] == 0 else [u3]
        BR[ch] = dict(GAlo=[u0], GAup=[u1], GA=[u0, u1], LV=[u1], Wp=[u1], XL=[u3], Up=[u3], Nn=[u3], Yp=ykp, Sd=ykp, TR=[u2, u3], ALL=[u0, u1, u2, u3])
    up, lo = slice(64, 128), slice(0, 64)
    mU = lambda ap: ap.bitcast(U32)
    cf = list(range(NCH))
    cbk = list(range(TC // CH - 1, -1, -1)) + list(range(NCH - 1, TC // CH - 1, -1))
    order = [cf, cbk]
    dbgn = getattr(self, "rw_dbg", None)
    for b in range(NB):
        cols = slice(b * T, (b + 1) * T)
        for p in range(8):
            if dbgn is not None and (b * 8 + p) >= dbgn[0]:
                continue
            rows = slice(p * 128, (p + 1) * 128)
            for d in range(2):
                S.dma("sp", AB[d], RWD[b, p, d, 0].rearrange("k (c x) -> k c x", x=128), writes=[BABl[d]])
                S.dma("sp", KB[d], RWD[b, p, d, 1].rearrange("k (c x) -> k c x", x=128), writes=[BKBl[d]])
                S.dma("sp", SCs[d], RWS[b, p, d].rearrange("k (a c) -> k a c", a=3), writes=[BSCl[d]])
            S.dma("sp", Vst, Vtm[cols, rows].rearrange("(c s) v -> s c v", s=CH), writes=[BVst])
            S.dma("sp", rl, RWP[0, rows, cols], writes=[Brl])
            S.dma("sp", k0, RWP[1, rows, cols], writes=[Bk0])
            S.dma("sp", k1, RWP[2, rows, cols], writes=[Bk1])
            S.dma("sp", vf, RWP[8, rows, cols], writes=[Bvf])
            S.dma("sp", gg, RWP[9, rows, cols], writes=[Bgg])
            if MERGE:
                for d in range(2):
                    S.op("pool", lambda: nc.gpsimd.tensor_copy(out=mVU[d][lo, :, :, :], in_=Vst.rearrange("s c (h v) -> s c h v", h=2)), [BVst], [mBVU[d]])
                    S.op("dve", lambda: A_.memset(ST[d], 0.0), [], [mBST[d]])
                    S.op("dve", lambda: A_.memset(S0m[d], 0.0), [], [mBS0[d]])

                def dstep(d, step):
                    c = order[d][step]
                    cs = slice(c * CH, (c + 1) * CH)
                    bA, bB, bC, bD = [self.PS[4 * d + q] for q in range(4)]
                    uA, uB, uC, uD = mUB[d]
                    h2 = lambda ap: ap.rearrange("p (h x) -> p h x", h=2)
                    GA = h2(bA[:, 0:256]); LV = h2(bB[:, 0:256]); Wp = h2(bB[:, 256:384])
                    XL = h2(bC[:, 0:128]); Up_ = h2(bC[:, 128:256]); Nn = h2(bC[:, 256:384])
                    Yp = bD[:, 0:64]; Sd = bD[:, 64:128]; TR = h2(bD.bitcast(BF16)[:, 512:640])
                    KP = [slice(0, 64), slice(64, 128)]
                    for hd in range(2):
                        kp = KP[hd]
                        S.op("pe", lambda: nc.tensor.transpose(out=TR[:, hd, :], in_=KB[d][kp, c, :], identity=self.identb[kp, kp]), [BKBl[d]], [uD], pemode=("g", hd))
                        S.op("pe", lambda: nc.tensor.matmul(GA[lo, hd, :], lhsT=KB[d][kp, c, 0:64], rhs=AB[d][kp, c, :], start=True, stop=True), [BKBl[d], BABl[d]], [uA], pemode=("g", hd))
                        S.op("pe", lambda: nc.tensor.matmul(GA[up, hd, :], lhsT=KB[d][kp, c, 64:128], rhs=AB[d][kp, c, :], start=True, stop=True), [BKBl[d], BABl[d]], [uA], pemode=("g", hd))
                        S.op("pe", lambda: nc.tensor.matmul(Nn[up, hd, :], lhsT=AB[d][kp, c, 0:64], rhs=KB[d][kp, c, 64:128], start=True, stop=True), [BKBl[d], BABl[d]], [uC], pemode=("g", hd))
                    yield
                    S.op("act", lambda: nc.scalar.copy(out=mKB[d], in_=TR), [uD], [mBKB[d]])
                    S.op("dve", lambda: A_.copy_predicated(out=mGG[d], mask=mU(M4x2[d][:]), data=GA), [uA, Bmk], [mBGG[d]])
                    S.op("dve", lambda: A_.copy_predicated(out=mAN0[d][up, :, 0:64], mask=mU(mAx2[d][up, :, :]), data=GA[up, :, 0:64]), [uA, Bmk], [mBAN0[d]])
                    S.op("dve", lambda: A_.copy_predicated(out=mAN0[d][up, :, 64:128], mask=mU(mNx2[d][up, :, :]), data=Nn[up, :, :]), [uC, Bmk], [mBAN0[d]])
                    S.op("dve", lambda: A_.tensor_tensor(out=mXp[d][0][up, :, :], in0=I2[up, :, :], in1=mAN0[d][up, :, 0:64], op=ALU.subtract), [mBAN0[d], Bmk], [mBXp[d][0]])
                    yield
                    cur, Bcur = mAN0[d], mBAN0[d]
                    xq = 0
                    for lv in range(1, 7):
                        nx, Bnx = mANp[d][lv % 2], mBANp[d][lv % 2]
                        for hd in range(2):
                            if lv <= 5:
                                if lv < 5:
                                    S.op("pe", lambda: nc.tensor.matmul(LV[up, hd, 0:64], lhsT=cur[up, hd, 64:128], rhs=cur[up, hd, 0:64], start=True, stop=True), [Bcur], [uB], pemode=("g", 1))
                                S.op("pe", lambda: nc.tensor.matmul(LV[up, hd, 64:128], lhsT=cur[up, hd, 0:64], rhs=cur[up, hd, 64:128], start=True, stop=True), [Bcur], [uB], pemode=("g", 1))
                            if lv >= 2:
                                S.op("pe", lambda: nc.tensor.matmul(XL[up, hd, :], lhsT=cur[up, hd, 64:128], rhs=mXp[d][xq][up, hd, :], start=True, stop=True), [Bcur, mBXp[d][xq]], [uC], pemode=("g", 1))
                        yield
                        if lv <= 5:
                            if lv < 5:
                                S.op("act", lambda: nc.scalar.copy(out=nx[up, :, :], in_=LV[up, :, :]), [uB], [Bnx])
                            else:
                                S.op("act", lambda: nc.scalar.copy(out=nx[up, :, 64:128], in_=LV[up, :, 64:128]), [uB], [Bnx])
                        if lv >= 2:
                            S.op("dve", lambda: A_.tensor_tensor(out=mXp[d][1 - xq][up, :, :], in0=XL[up, :, :], in1=mXp[d][xq][up, :, :], op=ALU.add), [uC, mBXp[d][xq]], [mBXp[d][1 - xq]])
                            xq = 1 - xq
                        if lv <= 5:
                            cur, Bcur = nx, Bnx
                        yield
                    for hd in (1, 0):
                        kp = KP[hd]
                        S.op("pe", lambda: nc.tensor.matmul(Wp[up, hd, :], lhsT=AB[d][kp, c, 0:64], rhs=S0m[d][kp, :], start=True, stop=False), [BABl[d], mBS0[d]], [uB], pemode=("g", hd))
                        S.op("pe", lambda: nc.tensor.matmul(Wp[up, hd, :], lhsT=mGG[d][lo, hd, 0:64], rhs=mVU[d][lo, c, hd, :], start=False, stop=True), [mBGG[d], mBVU[d]], [uB], pemode=("g", 0))
                    yield
                    S.op("act", lambda: nc.scalar.copy(out=mWf[d][up, :, :], in_=Wp[up, :, :]), [uB], [mBWf[d]])
                    yield
                    for hd in range(2):
                        S.op("pe", lambda: nc.tensor.matmul(Up_[up, hd, :], lhsT=mXp[d][xq][up, hd, :], rhs=mWf[d][up, hd, :], start=True, stop=True), [mBXp[d][xq], mBWf[d]], [uC], pemode=("g", 1))
                    yield
                    S.op("act", lambda: nc.scalar.activation(out=mVU[d][up, c, :, :], in_=Up_[up, :, :], func=AF.Copy, scale=-1.0), [uC], [mBVU[d]])
                    yield
                    for hd in (1, 0):
                        kp = KP[hd]
                        S.op("pe", lambda: nc.tensor.matmul(Yp[kp, :], lhsT=S0m[d][kp, :], rhs=AB[d][kp, c, 64:128], start=True, stop=False), [mBS0[d], BABl[d]], [uD], pemode=("g", hd))
                        S.op("pe", lambda: nc.tensor.matmul(Yp[kp, :], lhsT=mVU[d][:, c, hd, :], rhs=mGG[d][:, hd, 64:128], start=False, stop=True), [mBVU[d], mBGG[d]], [uD], pemode=("full",))
                    for hd in range(2):
                        kp = KP[hd]
                        S.op("pe", lambda: nc.tensor.matmul(Sd[kp, :], lhsT=mKB[d][:, hd, :], rhs=mVU[d][:, c, hd, :], start=True, stop=True), [mBKB[d], mBVU[d]], [uD], pemode=("full",))
                    yield
                    S.op("act", lambda: nc.scalar.copy(out=yacc[d][:, cs], in_=Yp), [uD], [mBy[d]])
                    S.op("act", lambda: nc.scalar.activation(out=tS[d], in_=Sd, func=AF.Identity, scale=SCs[d][:, 2, c:c + 1]), [uD, BSCl[d]], [mBtS[d]])
                    S.op("dve", lambda: A_.scalar_tensor_tensor(out=ST[d], in0=ST[d], scalar=SCs[d][:, 1, c:c + 1], in1=tS[d], op0=ALU.mult, op1=ALU.add), [mBST[d], mBtS[d], BSCl[d]], [mBST[d]])
                    if step + 1 < NCH:
                        cn = order[d][step + 1]
                        S.op("dve", lambda: A_.tensor_scalar(out=S0m[d], in0=ST[d], scalar1=SCs[d][:, 0, cn:cn + 1], scalar2=None, op0=ALU.mult), [mBST[d], BSCl[d]], [mBS0[d]])

                for step in range(NCH if dbgn is None else dbgn[1]):
                    gens = [dstep(d, step) for d in range(2)]
                    while gens:
                        for g_ in list(gens):
                            try:
                                next(g_)
                            except StopIteration:
                                gens.remove(g_)
            else:
                for ch in chains:
                    hd, d = ch
                    kp = slice(hd * 64, hd * 64 + 64)
                    S.op("pool", lambda: nc.gpsimd.tensor_copy(out=VU[ch][lo, :, :], in_=Vst[:, :, hd * 64:(hd + 1) * 64]), [BVst], [BVU[ch]])
                    S.op("dve", lambda: A_.memset(ST[d][kp, :], 0.0), [], [BST[ch]])
                    S.op("dve", lambda: A_.memset(S0m[d][kp, :], 0.0), [], [BS0[ch]])
                def chain_step(ch, step):
                    hd, d = ch
                    kp = slice(hd * 64, hd * 64 + 64)
                    c = order[d][step]
                    cs = slice(c * CH, (c + 1) * CH)
                    r_, br_ = R[ch], BR[ch]
                    M4 = self.masks[:, 0:128] if d == 0 else self.masks[:, 128:256]
                    mA = self.masks[up, 0:64] if d == 0 else self.masks[up, 128:192]
                    mN = self.masks[up, 128:192] if d == 0 else self.masks[up, 0:64]
                    S.op("pe", lambda: nc.tensor.transpose(out=r_["TR"], in_=KB[d][kp, c, :], identity=self.identb[kp, kp]), [BKBl[d]], br_["TR"], pemode=("T", hd))
                    S.op("act", lambda: nc.scalar.copy(out=KBtr[ch], in_=r_["TR"]), br_["TR"], [BKBtr[ch]])
                    S.op("pe", lambda: nc.tensor.matmul(r_["GA"][lo, :], lhsT=KB[d][kp, c, 0:64], rhs=AB[d][kp, c, :], start=True, stop=True), [BKBl[d], BABl[d]], br_["GAlo"], pemode=("g", hd))
                    S.op("pe", lambda: nc.tensor.matmul(r_["GA"][up, :], lhsT=KB[d][kp, c, 64:128], rhs=AB[d][kp, c, :], start=True, stop=True), [BKBl[d], BABl[d]], br_["GAup"], pemode=("g", hd))
                    S.op("pe", lambda: nc.tensor.matmul(r_["Nn"][up, :], lhsT=AB[d][kp, c, 0:64], rhs=KB[d][kp, c, 64:128], start=True, stop=True), [BKBl[d], BABl[d]], br_["Nn"], pemode=("g", hd))
                    yield
                    S.op("dve", lambda: A_.copy_predicated(out=GGb[ch], mask=mU(M4), data=r_["GA"]), br_["GA"], [BGG[ch]])
                    S.op("dve", lambda: A_.copy_predicated(out=AN0[ch][up, 0:64], mask=mU(mA), data=r_["GA"][up, 0:64]), br_["GAup"], [BAN0[ch]])
                    S.op("dve", lambda: A_.copy_predicated(out=AN0[ch][up, 64:128], mask=mU(mN), data=r_["Nn"][up, :]), br_["Nn"], [BAN0[ch]])
                    S.op("dve", lambda: A_.tensor_tensor(out=Xp[ch][0][up, :], in0=self.ident[up, up], in1=AN0[ch][up, 0:64], op=ALU.subtract), [BAN0[ch]], [BXp[ch][0]])
                    yield
                    cur, Bcur = AN0[ch], BAN0[ch]
                    xq = 0
                    for lv in range(1, 7):
                        nx, Bnx = ANp[ch][lv % 2], BANp[ch][lv % 2]
                        if lv <= 5:
                            if lv < 5:
                                S.op("pe", lambda: nc.tensor.matmul(r_["LV"][up, 0:64], lhsT=cur[up, 64:128], rhs=cur[up, 0:64], start=True, stop=True), [Bcur], br_["LV"], pemode=("f",))
                            S.op("pe", lambda: nc.tensor.matmul(r_["LV"][up, 64:128], lhsT=cur[up, 0:64], rhs=cur[up, 64:128], start=True, stop=True), [Bcur], br_["LV"], pemode=("f",))
                        if lv >= 2:
                            S.op("pe", lambda: nc.tensor.matmul(r_["XL"][up, :], lhsT=cur[up, 64:128], rhs=Xp[ch][xq][up, :], start=True, stop=True), [Bcur, BXp[ch][xq]], br_["XL"], pemode=("f",))
                        yield
                        if lv <= 5:
                            if lv < 5:
                                S.op("act", lambda: nc.scalar.copy(out=nx[up, :], in_=r_["LV"][up, :]), br_["LV"], [Bnx])
                            else:
                                S.op("act", lambda: nc.scalar.copy(out=nx[up, 64:128], in_=r_["LV"][up, 64:128]), br_["LV"], [Bnx])
                        if lv >= 2:
                            S.op("dve", lambda: A_.tensor_tensor(out=Xp[ch][1 - xq][up, :], in0=r_["XL"][up, :], in1=Xp[ch][xq][up, :], op=ALU.add), br_["XL"] + [BXp[ch][xq]], [BXp[ch][1 - xq]])
                            xq = 1 - xq
                        if lv <= 5:
                            cur, Bcur = nx, Bnx
                        if lv < 6:
                            yield
                    yield
                    S.op("pe", lambda: nc.tensor.matmul(r_["Wp"][up, :], lhsT=AB[d][kp, c, 0:64], rhs=S0m[d][kp, :], start=True, stop=False), [BABl[d], BS0[ch]], br_["Wp"], pemode=("g", hd))
                    S.op("pe", lambda: nc.tensor.matmul(r_["Wp"][up, :], lhsT=GGb[ch][lo, 0:64], rhs=VU[ch][lo, c, :], start=False, stop=True), [BGG[ch], BVU[ch]], br_["Wp"], pemode=("w2",))
                    yield
                    S.op("act", lambda: nc.scalar.copy(out=Wf[ch][up, :], in_=r_["Wp"][up, :]), br_["Wp"], [BWf[ch]])
                    yield
                    S.op("pe", lambda: nc.tensor.matmul(r_["Up"][up, :], lhsT=Xp[ch][xq][up, :], rhs=Wf[ch][up, :], start=True, stop=True), [BXp[ch][xq], BWf[ch]], br_["Up"], pemode=("f",))
                    yield
                    S.op("act", lambda: nc.scalar.activation(out=VU[ch][up, c, :], in_=r_["Up"][up, :], func=AF.Copy, scale=-1.0), br_["Up"], [BVU[ch]])
                    yield
                    S.op("pe", lambda: nc.tensor.matmul(r_["Yp"][kp, :], lhsT=S0m[d][kp, :], rhs=AB[d][kp, c, 64:128], start=True, stop=False), [BS0[ch], BABl[d]], br_["Yp"], pemode=("g", hd))
                    S.op("pe", lambda: nc.tensor.matmul(r_["Yp"][kp, :], lhsT=VU[ch][:, c, :], rhs=GGb[ch][:, 64:128], start=False, stop=True), [BVU[ch], BGG[ch]], br_["Yp"], pemode=("full",))
                    yield
                    S.op("act", lambda: nc.scalar.copy(out=yacc[d][kp, cs], in_=r_["Yp"][kp, :]), br_["Yp"], [By[ch]])
                    S.op("pe", lambda: nc.tensor.matmul(r_["Sd"][kp, :], lhsT=KBtr[ch], rhs=VU[ch][:, c, :], start=True, stop=True), [BKBtr[ch], BVU[ch]], br_["Sd"], pemode=("full",))
                    yield
                    S.op("act", lambda: nc.scalar.activation(out=tS[d][kp, :], in_=r_["Sd"][kp, :], func=AF.Identity, scale=SCs[d][kp, 2, c:c + 1]), br_["Sd"] + [BSCl[d]], [BtS[ch]])
                    S.op("dve", lambda: A_.scalar_tensor_tensor(out=ST[d][kp, :], in0=ST[d][kp, :], scalar=SCs[d][kp, 1, c:c + 1], in1=tS[d][kp, :], op0=ALU.mult, op1=ALU.add), [BST[ch], BtS[ch], BSCl[d]], [BST[ch]])
                    if step + 1 < NCH:
                        cn = order[d][step + 1]
                        S.op("dve", lambda: A_.tensor_scalar(out=S0m[d][kp, :], in0=ST[d][kp, :], scalar1=SCs[d][kp, 0, cn:cn + 1], scalar2=None, op0=ALU.mult), [BST[ch], BSCl[d]], [BS0[ch]])

                for step in range(NCH if dbgn is None else dbgn[1]):
                    gens = [chain_step(ch, step) for ch in chains]
                    if getattr(self, "rw_order", "phase") == "chain":
                        for g_ in gens:
                            for _ in g_:
                                pass
                        gens = []
                    while gens:
                        for g_ in list(gens):
                            try:
                                next(g_)
                            except StopIteration:
                                gens.remove(g_)
            if MERGE:
                RB = {0: [mUB[0][0]], 1: [mUB[0][1]]}
                By_all = [mBy[0], mBy[1]]
            else:
                RB = {0: [BR[chains[0]]["ALL"][0], BR[chains[0]]["ALL"][1]], 1: [BR[chains[0]]["ALL"][2], BR[chains[0]]["ALL"][3]]}
                By_all = [By[ch] for ch in chains]
            Byy = Buf()
            S.op("dve", lambda: A_.tensor_tensor(out=yacc[0], in0=yacc[0], in1=yacc[1], op=ALU.add), By_all, [Byy])
            NP_ = 6
            W_ = T // NP_
            for pc in range(NP_):
                sl_ = slice(pc * W_, (pc + 1) * W_)
                pb = pc % 2
                S.op("pe", lambda: nc.tensor.matmul(self.PS[pb][:, 0:W_], lhsT=self.blk64, rhs=yacc[0][:, sl_], start=True, stop=True), [Byy], RB[pb])
                S.op("dve", lambda: A_.scalar_tensor_tensor(out=t0_[:, sl_], in0=self.PS[pb][:, 0:W_], scalar=-1.0 / 64, in1=yacc[0][:, sl_], op0=ALU.mult, op1=ALU.add), RB[pb] + [Byy], [Bt0])
            S.op("act", lambda: nc.scalar.activation(out=t1_, in_=t0_, func=AF.Square), [Bt0], [Bt1])
            for pc in range(NP_):
                sl_ = slice(pc * W_, (pc + 1) * W_)
                pb = pc % 2
                S.op("pe", lambda: nc.tensor.matmul(self.PS[pb][:, 0:W_], lhsT=self.blk64, rhs=t1_[:, sl_], start=True, stop=True), [Bt1], RB[pb])
                S.op("act", lambda: nc.scalar.activation(out=yacc[1][:, sl_], in_=self.PS[pb][:, 0:W_], func=AF.Sqrt, scale=1.0 / 64, bias=epsLN), RB[pb] + [Bgl], [Byy])
            S.op("dve", lambda: A_.reciprocal(out=yacc[1], in_=yacc[1]), [Byy], [Byy])
            S.op("dve", lambda: A_.tensor_tensor(out=t0_, in0=t0_, in1=yacc[1], op=ALU.mult), [Bt0, Byy], [Bt0])
            S.op("act", lambda: nc.scalar.activation(out=t0_, in_=t0_, func=AF.Identity, scale=self.pv("rw_ln_w", p), bias=self.pv("rw_ln_b", p)), [Bt0], [Bt0])
            S.op("dve", lambda: A_.tensor_tensor(out=k0, in0=k0, in1=k1, op=ALU.add), [Bk0, Bk1], [Bk0])
            S.op("dve", lambda: A_.scalar_tensor_tensor(out=t1_, in0=rl, scalar=self.pv("rw_r_k", p), in1=k0, op0=ALU.mult, op1=ALU.mult), [Brl, Bk0, Bt1], [Bt1])
            for pc in range(NP_):
                sl_ = slice(pc * W_, (pc + 1) * W_)
                pb = pc % 2
                S.op("pe", lambda: nc.tensor.matmul(self.PS[pb][:, 0:W_], lhsT=self.blk64, rhs=t1_[:, sl_], start=True, stop=True), [Bt1], RB[pb])
                S.op("dve", lambda: A_.tensor_tensor(out=yacc[1][:, sl_], in0=self.PS[pb][:, 0:W_], in1=vf[:, sl_], op=ALU.mult), RB[pb] + [Bvf, Byy], [Byy])
            S.op("dve", lambda: A_.tensor_tensor(out=t0_, in0=t0_, in1=yacc[1], op=ALU.add), [Bt0, Byy], [Bt0])
            S.op("dve", lambda: A_.tensor_tensor(out=ogb, in0=t0_, in1=gg, op=ALU.mult), [Bt0, Bgg], [Bogb])
            S.dma("pool", og[rows, cols], ogb, reads=[Bogb])
            for b_ in By_all:
                b_.r.append(Byy.w)
    st.close()
    S.pe_selfwait = False
    S.pe_drain = 0


Prog.rwkv_scan = _rwkv_scan
```
